# Optimizing a Trainium2 kernel written in Bass

```python
import jax, jax.numpy as jnp
from jax import lax
import numpy as np

D_MODEL = 1024
BATCH = 8
SEQ = 2048
DEPTH = 1
DEC_BATCH = 128
DEC_SEQ = 4
PAST_LEN = 16384
PAGE_SIZE = 128

D_MIX = D_MODEL
D_RWKV = D_MIX // 2
D_POOL = D_MIX - D_RWKV
HEAD_SIZE = 64
N_HEADS = D_RWKV // HEAD_SIZE
D_DECAY_LORA = 64
D_AAA_LORA = 64
POOL_WINDOWS = (2, 4, 8, 16)
N_POOL_GROUPS = len(POOL_WINDOWS)
POOL_GROUP = D_POOL // N_POOL_GROUPS
POOL_BUF = max(POOL_WINDOWS) - 1
D_SHIFT = 3 * D_RWKV + D_DECAY_LORA + D_AAA_LORA
D_IN = D_SHIFT + D_RWKV + 2 * D_POOL
NORM_EPS = 1e-6
GN_EPS = 64e-5
L2_EPS = 1e-12

kernel_name = "hymba_rwkv7_pool_decode_step"


def _rmsnorm(x, w):
    xf = x.astype(jnp.float32)
    return xf * lax.rsqrt(jnp.mean(xf * xf, axis=-1, keepdims=True) + NORM_EPS) * w.astype(jnp.float32)


def _wkv_step(S, inp):
    r, w, k, v, kk, a = inp
    sa = jnp.einsum('bhvk,bhk->bhv', S, -kk)
    S = S * w[:, :, None, :] + sa[..., None] * (kk * a)[:, :, None, :] + v[..., None] * k[:, :, None, :]
    y = jnp.einsum('bhvk,bhk->bhv', S, r)
    return S, y


def _layer(x, shift_prev, wkv_prev, pool_prev, t0, norm_w, w_in, mu_shift, w_decay_b, w0,
           w_aaa_b, a0, k_k, k_a, r_k, gn_w, gn_b, pool_w, pool_scale, w_out):
    f32 = jnp.float32
    B, T, _ = x.shape
    h = _rmsnorm(x, norm_w)
    proj = jnp.einsum('btd,de->bte', h, w_in.astype(f32))
    p_rwkv, g_rwkv, u_pool, g_pool = jnp.split(
        proj, [D_SHIFT, D_SHIFT + D_RWKV, D_SHIFT + D_RWKV + D_POOL], axis=-1)

    prev = jnp.concatenate([shift_prev.astype(f32)[:, None, :], p_rwkv[:, :-1]], axis=1)
    ps = p_rwkv + mu_shift.astype(f32) * (prev - p_rwkv)
    r, k, v, xw, xa = jnp.split(
        ps, [D_RWKV, 2 * D_RWKV, 3 * D_RWKV, 3 * D_RWKV + D_DECAY_LORA], axis=-1)
    w_raw = -jax.nn.softplus(-(w0.astype(f32) + jnp.tanh(xw) @ w_decay_b.astype(f32))) - 0.5
    decay = jnp.exp(-jnp.exp(w_raw))
    a = jax.nn.sigmoid(a0.astype(f32) + xa @ w_aaa_b.astype(f32))
    heads = lambda z: z.reshape(B, T, N_HEADS, HEAD_SIZE)
    kk = heads(k * k_k.astype(f32))
    kk = kk * lax.rsqrt(jnp.sum(kk * kk, axis=-1, keepdims=True) + L2_EPS)
    k = k * (1.0 + (a - 1.0) * k_a.astype(f32))
    r, k, v, decay, a = heads(r), heads(k), heads(v), heads(decay), heads(a)
    xs = tuple(jnp.moveaxis(z, 1, 0) for z in (r, decay, k, v, kk, a))
    S_fin, ys = lax.scan(_wkv_step, wkv_prev.astype(f32), xs)
    y = jnp.moveaxis(ys, 0, 1)
    mu = jnp.mean(y, axis=-1, keepdims=True)
    var = jnp.mean(jnp.square(y - mu), axis=-1, keepdims=True)
    y = ((y - mu) * lax.rsqrt(var + GN_EPS)).reshape(B, T, D_RWKV) * gn_w.astype(f32) + gn_b.astype(f32)
    bonus = jnp.sum(r * k * r_k.astype(f32), axis=-1, keepdims=True) * v
    o_rwkv = (y + bonus.reshape(B, T, D_RWKV)) * jax.nn.silu(g_rwkv)

    u_ext = jnp.concatenate([pool_prev.astype(f32), u_pool], axis=1)
    cs = jnp.concatenate([jnp.zeros((B, 1, D_POOL), f32), jnp.cumsum(u_ext, axis=1)], axis=1)
    pos = t0 + jnp.arange(T, dtype=jnp.int32)
    diffs = []
    for gi, win in enumerate(POOL_WINDOWS):
        sl = slice(gi * POOL_GROUP, (gi + 1) * POOL_GROUP)
        total = cs[:, POOL_BUF + 1:, sl] - cs[:, POOL_BUF + 1 - win:POOL_BUF + 1 - win + T, sl]
        cnt = jnp.minimum(pos + 1, win).astype(f32)[None, :, None]
        diffs.append(total / cnt - u_pool[..., sl])
    d = jnp.stack(diffs, axis=2)
    o_pool = jnp.einsum('btgc,gce->btge', d, pool_w.astype(f32)).reshape(B, T, D_POOL)
    o_pool = o_pool * pool_scale.astype(f32) * jax.nn.silu(g_pool)

    out = jnp.concatenate([o_rwkv, o_pool], axis=-1) @ w_out.astype(f32)
    return x.astype(f32) + out, p_rwkv[:, -1], S_fin, u_ext[:, -POOL_BUF:]


def setup_inputs(seed: int = 0) -> dict:
    key = jax.random.key(seed)
    ks = jax.random.split(key, 24)
    n = jax.random.normal
    L = DEPTH
    return {
        "x_prompt": n(ks[0], (BATCH, SEQ, D_MODEL), jnp.float32),
        "x_sample": n(ks[1], (DEC_BATCH, DEC_SEQ, D_MODEL), jnp.float32),
        "state_shift": n(ks[2], (L, DEC_BATCH, D_SHIFT), jnp.float32),
        "state_wkv": 0.3 * n(ks[3], (L, DEC_BATCH, N_HEADS, HEAD_SIZE, HEAD_SIZE), jnp.float32),
        "state_pool": n(ks[4], (L, DEC_BATCH, POOL_BUF, D_POOL), jnp.float32),
        "norm_w": 1.0 + 0.05 * n(ks[5], (L, D_MODEL), jnp.float32),
        "w_in": n(ks[6], (L, D_MODEL, D_IN), jnp.float32) * D_MODEL ** -0.5,
        "mu_shift": jax.random.uniform(ks[7], (L, D_SHIFT), jnp.float32),
        "w_decay_b": 0.1 * n(ks[8], (L, D_DECAY_LORA, D_RWKV), jnp.float32) * D_DECAY_LORA ** -0.5,
        "w0": n(ks[9], (L, D_RWKV), jnp.float32),
        "w_aaa_b": 0.5 * n(ks[10], (L, D_AAA_LORA, D_RWKV), jnp.float32) * D_AAA_LORA ** -0.5,
        "a0": 0.1 * n(ks[11], (L, D_RWKV), jnp.float32),
        "k_k": 0.85 + 0.1 * n(ks[12], (L, D_RWKV), jnp.float32),
        "k_a": 1.0 + 0.1 * n(ks[13], (L, D_RWKV), jnp.float32),
        "r_k": 0.1 * n(ks[14], (L, N_HEADS, HEAD_SIZE), jnp.float32),
        "gn_w": 1.0 + 0.05 * n(ks[15], (L, D_RWKV), jnp.float32),
        "gn_b": 0.02 * n(ks[16], (L, D_RWKV), jnp.float32),
        "pool_w": n(ks[17], (L, N_POOL_GROUPS, POOL_GROUP, POOL_GROUP), jnp.float32) * POOL_GROUP ** -0.5,
        "pool_scale": 0.5 + 0.1 * n(ks[18], (L, D_POOL), jnp.float32),
        "w_out": n(ks[19], (L, D_MIX, D_MODEL), jnp.float32) * D_MIX ** -0.5,
        "norm_f": 1.0 + 0.05 * n(ks[20], (D_MODEL,), jnp.float32),
    }


def reference(x_prompt, x_sample, state_shift, state_wkv, state_pool, norm_w, w_in, mu_shift,
              w_decay_b, w0, w_aaa_b, a0, k_k, k_a, r_k, gn_w, gn_b, pool_w, pool_scale,
              w_out, norm_f):
    f32 = jnp.float32
    Bp = x_prompt.shape[0]
    hp = x_prompt.astype(f32)
    hs = x_sample.astype(f32)
    sh_p, wkv_p, pool_p, sh_s, wkv_s, pool_s = [], [], [], [], [], []
    for l in range(DEPTH):
        params = (norm_w[l], w_in[l], mu_shift[l], w_decay_b[l], w0[l], w_aaa_b[l], a0[l],
                  k_k[l], k_a[l], r_k[l], gn_w[l], gn_b[l], pool_w[l], pool_scale[l], w_out[l])
        hp, s1, s2, s3 = _layer(hp, jnp.zeros((Bp, D_SHIFT), f32),
                                jnp.zeros((Bp, N_HEADS, HEAD_SIZE, HEAD_SIZE), f32),
                                jnp.zeros((Bp, POOL_BUF, D_POOL), f32), 0, *params)
        sh_p.append(s1); wkv_p.append(s2); pool_p.append(s3)
        hs, s1, s2, s3 = _layer(hs, state_shift[l], state_wkv[l], state_pool[l], PAST_LEN, *params)
        sh_s.append(s1); wkv_s.append(s2); pool_s.append(s3)
    y_prompt = _rmsnorm(hp, norm_f).astype(x_prompt.dtype)
    y_sample = _rmsnorm(hs, norm_f).astype(x_sample.dtype)
    new_shift_prompt = jnp.stack(sh_p, axis=0)
    new_wkv_prompt = jnp.stack(wkv_p, axis=0)
    new_pool_prompt = jnp.stack(pool_p, axis=0)
    new_shift_sample = jnp.stack(sh_s, axis=0)
    new_wkv_sample = jnp.stack(wkv_s, axis=0)
    new_pool_sample = jnp.stack(pool_s, axis=0)
    return (y_prompt, y_sample, new_shift_prompt, new_wkv_prompt, new_pool_prompt,
            new_shift_sample, new_wkv_sample, new_pool_sample)
```

```python
import numpy as np
from contextlib import ExitStack
import concourse.bass as bass
import concourse.mybir as mybir
from concourse.bass_utils import run_bass_kernel_spmd

F32 = mybir.dt.float32
BF16 = mybir.dt.bfloat16
AF = mybir.ActivationFunctionType
ALU = mybir.AluOpType
AX = mybir.AxisListType

D = 1024
SEQ = 2048
NCORE = 8
DB = 16
DT = 4
NS = DB * DT
D_SHIFT = 1664
D_IN = 3200
C0 = float(np.exp(-0.5))
NORM_EPS = 1e-6
GN_EPS = 64e-5
L2_EPS = 1e-12
TB = 128
NTB = SEQ // TB
CH = 64
NCH = TB // CH
WINS = (2, 4, 8, 16)

C_ID, C_ONES, C_MUS, C_MUI, C_MLS, C_RST, C_ICNT, C_END = 0, 128, 256, 320, 384, 448, 960, 1024
PV_NW, PV_MU, PV_W0, PV_A0, PV_KK, PV_KA, PV_RK, PV_GW, PV_GB, PV_PS, PV_END = 0, 8, 21, 25, 29, 33, 37, 41, 45, 49, 53
BRB_W0, BRB_A0, BRB_KK, BRB_KA, BRB_RK, BRB_GW, BRB_GB = 0, 512, 1024, 1536, 2048, 2560, 3072


class Buf:
    __slots__ = ("name", "w", "r")

    def __init__(self, name):
        self.name = name
        self.w = None
        self.r = []


class T:
    def __init__(self, t, name, buf=None):
        self.t = t
        self.b = buf if buf is not None else Buf(name)

    def __getitem__(self, k):
        return self.t[k]


class Sched:
    def __init__(self, nc, n_dma_sems=32):
        self.nc = nc
        self.eng = {}
        for name in ["tensor", "vector", "scalar", "gpsimd", "sync"]:
            h = getattr(nc, name)
            sem = nc.alloc_semaphore(name="prog_" + name)
            self.eng[name] = dict(h=h, sem=sem, cnt=0, waited={})
        self.dma_sems = [dict(sem=nc.alloc_semaphore(name=f"dma{i}"), cnt=0) for i in range(n_dma_sems)]
        self.dma_rr = 0
        self.ninstr = 0
        self.rec = None
        self.tag = "-"
        self.tags = {n: [] for n in self.eng}

    def _wait(self, engname, tok):
        sem, val, src = tok
        e = self.eng[engname]
        key = id(sem)
        if e["waited"].get(key, 0) >= val:
            return
        e["h"].wait_ge(sem, val)
        e["waited"][key] = val
        self.ninstr += 1

    def _deps(self, engname, reads, writes):
        toks = []
        for b in reads:
            if b.w is not None:
                toks.append(b.w)
        for b in writes:
            if b.w is not None:
                toks.append(b.w)
            toks.extend(b.r)
        for tok in toks:
            if tok[2] == engname and engname == "tensor":
                continue
            self._wait(engname, tok)

    @staticmethod
    def _bufs(xs):
        return [x.b if isinstance(x, T) else x for x in xs]

    def _record(self, tok, reads, writes):
        for b in reads:
            b.r.append(tok)
            if len(b.r) > 64:
                b.r = b.r[-64:] if False else b.r
        for b in writes:
            b.w = tok
            b.r = []

    def op(self, engname, fn, reads=(), writes=(), cost=0.3):
        reads = self._bufs(reads)
        writes = self._bufs(writes)
        if self.rec is not None:
            self.rec.append(("op", engname, fn, reads, writes, cost, None))
            return None
        e = self.eng[engname]
        self._deps(engname, reads, writes)
        ins = fn(e["h"])
        e["cnt"] += 1
        self.tags[engname].append(self.tag)
        ins.then_inc(e["sem"], 1)
        e["waited"][id(e["sem"])] = max(e["waited"].get(id(e["sem"]), 0), 0)
        tok = (e["sem"], e["cnt"], engname)
        self._record(tok, reads, writes)
        self.ninstr += 1
        return tok

    def dma(self, qname, out, in_, reads=(), writes=(), **kw):
        reads = self._bufs(reads)
        writes = self._bufs(writes)
        if self.rec is not None:
            self.rec.append(("dma", qname, (out, in_), reads, writes, 2.5, kw))
            return None
        e = self.eng[qname]
        self._deps(qname, reads, writes)
        d = self.dma_sems[self.dma_rr]
        self.dma_rr = (self.dma_rr + 1) % len(self.dma_sems)
        if d["cnt"] > 0:
            self._wait(qname, (d["sem"], 16 * d["cnt"], "dma"))
        ins = e["h"].dma_start(out=out, in_=in_, **kw)
        d["cnt"] += 1
        ins.then_inc(d["sem"], 16)
        tok = (d["sem"], 16 * d["cnt"], "dma")
        self._record(tok, reads, writes)
        self.ninstr += 1
        return tok

    def emit(self, r):
        kind, eng, fn, reads, writes, cost, kw = r
        if kind == "op":
            self.op(eng, fn, reads=reads, writes=writes)
        else:
            self.dma(eng, fn[0], fn[1], reads=reads, writes=writes, **kw)

    def merge_emit(self, streams, ok):
        eng_free = {}
        ready = {}
        acc = {}

        def est(r):
            kind, eng, fn, reads, writes, cost, kw = r
            t = eng_free.get(eng, 0.0)
            for b in reads:
                rt_, re_ = ready.get(id(b), (0.0, eng))
                t = max(t, rt_ + (0.15 if re_ != eng else 0.0))
            for b in writes:
                rt_, re_ = ready.get(id(b), (0.0, eng))
                t = max(t, rt_ + (0.15 if re_ != eng else 0.0), acc.get(id(b), 0.0) + 0.1)
            return t

        def commit(r, t):
            kind, eng, fn, reads, writes, cost, kw = r
            if kind == "dma":
                eng_free[eng] = t + 0.1
                end = t + cost
            else:
                end = t + cost
                eng_free[eng] = end
            for b in reads:
                acc[id(b)] = max(acc.get(id(b), 0.0), end)
            for b in writes:
                ready[id(b)] = (end, eng)
                acc[id(b)] = max(acc.get(id(b), 0.0), end)

        names = list(streams)
        bi = {n: 0 for n in names}
        ui = {n: 0 for n in names}
        last_end = {n: 0.0 for n in names}
        while any(bi[n] < len(streams[n]) for n in names):
            cands = []
            for n in names:
                if bi[n] >= len(streams[n]):
                    continue
                if ui[n] == 0 and not ok(n, bi[n], bi):
                    continue
                u = streams[n][bi[n]][ui[n]]
                cands.append((est(u[0]), n, u))
            assert cands, (bi, ui)
            tmin = min(c[0] for c in cands)
            elig = [c for c in cands if c[0] <= tmin + 1.0]
            _, n, u = min(elig, key=lambda c: last_end[c[1]])
            self.tag = f"{n}{bi[n]}:{ui[n]}"
            for r in u:
                t_ = est(r)
                commit(r, t_)
                last_end[n] = max(last_end[n], t_ + r[5])
                self.emit(r)
            self.tag = "-"
            ui[n] += 1
            if ui[n] == len(streams[n][bi[n]]):
                bi[n] += 1
                ui[n] = 0

    def barrier(self):
        toks = [(e["sem"], e["cnt"], n) for n, e in self.eng.items() if e["cnt"] > 0]
        toks += [(d["sem"], 16 * d["cnt"], "dma") for d in self.dma_sems if d["cnt"] > 0]
        for n in self.eng:
            for tok in toks:
                if tok[2] == n:
                    continue
                self._wait(n, tok)

    def finish(self, tiles, engname="sync"):
        for b in self._bufs(tiles):
            if b.w is not None:
                self._wait(engname, b.w)


class _Stop(Exception):
    pass


def build_program(stop=None):
    nc = bass.Bass("TRN2", target_bir_lowering=False)
    S = Sched(nc)
    try:
        _build_body(nc, S, stop)
    except _Stop:
        S.barrier()
    return nc, S


def _build_body(nc, S, stop):
    def chk(label):
        if stop == label:
            raise _Stop()


    def din(name, shape):
        return nc.dram_tensor(name, list(shape), F32, kind="ExternalInput").ap()

    def dout(name, shape):
        return T(nc.dram_tensor(name, list(shape), F32, kind="ExternalOutput").ap(), name)

    xp = din("xp", [SEQ, D])
    xs = din("xs", [NS, D])
    sshift = din("sshift", [DB, D_SHIFT])
    swkv = din("swkv", [128, 4096])
    spool = din("spool", [DB * 15, 512])
    w_in = din("w_in", [D, D_IN])
    w_out = din("w_out", [D, D])
    wdec = din("wdec", [64, 512])
    waaa = din("waaa", [64, 512])
    poolw = din("poolw", [4, 128, 128])
    pvec_d = din("pvec", [128, PV_END])
    browA_d = din("browA", [1, D_SHIFT])
    browB_d = din("browB", [1, 3584])
    normf_d = din("normf", [1, D])
    cst_d = din("cst", [128, C_END])

    yp = dout("yp", [SEQ, D])
    ys = dout("ys", [NS, D])
    nsp = dout("nsp", [13, 128])
    nwp = dout("nwp", [8, 64, 64])
    npp = dout("npp", [15, 512])
    nss = dout("nss", [DB, D_SHIFT])
    nws = dout("nws", [128, 4096])
    nps = dout("nps", [DB, 15, 512])
    scr1 = T(nc.dram_tensor("scr1", [6, DT, DB, 8, 64], F32, kind="Internal").ap(), "scr1")
    scr2 = T(nc.dram_tensor("scr2", [DB, 8, DT, 64], F32, kind="Internal").ap(), "scr2")

    es_top = ExitStack()

    def sb(es, name, shape, dt=F32):
        return T(es.enter_context(nc.sbuf_tensor("s_" + name, list(shape), dt)), name)

    def pst(name, shape, dt=F32):
        return T(nc.alloc_psum_tensor("p_" + name, list(shape), dt), name)

    def nel(ap):
        n = 1
        for s_ in ap.shape[1:]:
            n *= s_
        return n

    def mm(out, lhsT, rhs, start, stop, reads, writes):
        passes = 4 if lhsT.dtype == F32 else 1
        c_ = max(0.055, nel(rhs) * passes / 2000.0 + 0.03)
        S.op("tensor", lambda e: e.matmul(out, lhsT=lhsT, rhs=rhs, start=start, stop=stop), reads=reads, writes=writes, cost=c_)

    def tr(out, in_, ident, reads, writes):
        S.op("tensor", lambda e: e.transpose(out, in_, ident), reads=reads, writes=writes, cost=0.13)

    def act(out, in_, func, reads, writes, bias=None, scale=None, eng="scalar"):
        kw = {}
        if bias is not None:
            kw["bias"] = bias
        if scale is not None:
            kw["scale"] = scale
        S.op("scalar", lambda e: e.activation(out=out, in_=in_, func=func, **kw), reads=reads, writes=writes,
             cost=0.1 + 0.1 * len(kw) + nel(in_) * 0.00095)

    def ecost(eng, n):
        return 0.08 + n * (0.00105 if eng == "vector" else 0.0025)

    def vtt(out, in0, in1, op, reads, writes, eng="vector"):
        S.op(eng, lambda e: e.tensor_tensor(out=out, in0=in0, in1=in1, op=op), reads=reads, writes=writes, cost=ecost(eng, nel(out)))

    def vts(out, in0, s1, s2, op0, op1, reads, writes, eng="vector"):
        if op1 is None:
            S.op(eng, lambda e: e.tensor_scalar(out=out, in0=in0, scalar1=s1, scalar2=None, op0=op0), reads=reads, writes=writes,
                 cost=ecost(eng, nel(out)))
        else:
            S.op(eng, lambda e: e.tensor_scalar(out=out, in0=in0, scalar1=s1, scalar2=s2, op0=op0, op1=op1), reads=reads, writes=writes,
                 cost=ecost(eng, nel(out)))

    def vstt(out, in0, scalar, in1, op0, op1, reads, writes):
        S.op("vector", lambda e: e.scalar_tensor_tensor(out=out, in0=in0, scalar=scalar, in1=in1, op0=op0, op1=op1), reads=reads, writes=writes,
             cost=ecost("vector", nel(out)))

    def vcopy(out, in_, reads, writes, eng="vector"):
        S.op(eng, lambda e: e.tensor_copy(out=out, in_=in_), reads=reads, writes=writes, cost=ecost(eng, nel(out)))

    def vred(out, in_, reads, writes):
        S.op("vector", lambda e: e.tensor_reduce(out=out, in_=in_, axis=AX.X, op=ALU.add), reads=reads, writes=writes,
             cost=ecost("vector", nel(in_)))

    def vrecip(out, in_, reads, writes):
        S.op("vector", lambda e: e.reciprocal(out=out, in_=in_), reads=reads, writes=writes, cost=0.08 + nel(out) * 0.0084)

    def memset(ap, val, writes, eng="gpsimd"):
        S.op(eng, lambda e: e.memset(ap, val), writes=writes)

    def rsqrt_small(out, in_, tmp, scale, eps, reads, writes):
        act(tmp, in_, AF.Sqrt, reads=reads, writes=writes, bias=None, scale=None) if False else None
        vts(tmp, in_, scale, eps, ALU.mult, ALU.add, reads=reads, writes=writes)
        act(tmp, tmp, AF.Sqrt, reads=writes, writes=writes)
        vrecip(out, tmp, reads=writes, writes=writes)

    pg = [pst(f"pg{i}", [128, 512]) for i in range(2)]
    pT = pst("pT", [128, 1024], BF16)
    pM = pst("pM", [128, 512])
    pA = pst("pA", [128, 512])
    pB = pst("pB", [128, 512])
    pC = pst("pC", [128, 512])
    pD = pst("pD", [128, 512])

    cst = sb(es_top, "cst", [128, C_END])
    pvec = sb(es_top, "pvec", [128, PV_END])
    omu = sb(es_top, "omu", [128, 13])
    omka = sb(es_top, "omka", [128, 4])
    identb = sb(es_top, "identb", [128, 128], BF16)
    winb = sb(es_top, "winb", [128, 8, D_IN], BF16)
    woutb = sb(es_top, "woutb", [128, 8, D], BF16)
    wlo = sb(es_top, "wlo", [128, 512])
    wd = T(wlo[0:64, :], "wd", buf=wlo.b)
    wa = wlo
    pw = sb(es_top, "pw", [128, 4, 128], BF16)
    normf = sb(es_top, "normf", [128, D])

    ident = cst[:, C_ID:C_ID + 128]
    onesblk = cst[:, C_ONES:C_ONES + 128]

    S.dma("sync", cst[:], cst_d, writes=[cst])
    S.dma("sync", pvec[:], pvec_d, writes=[pvec])
    S.dma("sync", wd[:], wdec, writes=[wd])
    S.dma("sync", wa[64:128, :], waaa, writes=[wa])
    S.dma("sync", normf[:], normf_d.partition_broadcast(128), writes=[normf])
    vcopy(identb[:], ident, reads=[cst], writes=[identb])
    vts(omka[:], pvec[:, PV_KA:PV_KA + 4], -1.0, 1.0, ALU.mult, ALU.add, reads=[pvec], writes=[omka])

    with ExitStack() as es:
        stg = [sb(es, f"stg{i}", [128, D_IN]) for i in range(2)]
        S.dma("sync", stg[1][:, 0:512], poolw.rearrange("g c e -> c g e"), writes=[stg[1]])
        vcopy(pw[:].rearrange("p g e -> p (g e)"), stg[1][:, 0:512], reads=[stg[1]], writes=[pw])
        for dc in range(8):
            st = stg[dc % 2]
            S.dma("sync", st[:], w_in[dc * 128:(dc + 1) * 128, :], writes=[st])
            h = D_IN // 2
            vts(winb[:, dc, 0:h], st[:, 0:h], pvec[:, PV_NW + dc:PV_NW + dc + 1], None, ALU.mult, None, reads=[st, pvec], writes=[winb])
            act(winb[:, dc, h:], st[:, h:], AF.Copy, reads=[st, pvec], writes=[winb], scale=pvec[:, PV_NW + dc:PV_NW + dc + 1])
        for fc in range(8):
            st = stg[fc % 2]
            S.dma("sync", st[:, 0:D], w_out[fc * 128:(fc + 1) * 128, :], writes=[st])
            vcopy(woutb[:, fc, 0:512], st[:, 0:512], reads=[st], writes=[woutb])
            act(woutb[:, fc, 512:], st[:, 512:D], AF.Copy, reads=[st], writes=[woutb])
        S.barrier()
        chk("W")

    def final_tile(es_tiles, n, x_t, oT_list, out_dram_ap, out_T):
        res, sq, ssum, tmp1, rstd, yo = es_tiles
        for half in range(2):
            bank = pD if half == 0 else pC
            for fc in range(8):
                mm(bank[0:n, :], oT_list[fc], woutb[:, fc, half * 512:(half + 1) * 512], fc == 0, fc == 7,
                   reads=[oT_list_T, woutb], writes=[bank])
            vtt(res[0:n, half * 512:(half + 1) * 512], bank[0:n, :], x_t[0:n, half * 512:(half + 1) * 512], ALU.add,
                reads=[bank, x_t], writes=[res])
        act(sq[0:n, :], res[0:n, :], AF.Square, reads=[res], writes=[sq])
        vred(ssum[0:n, :], sq[0:n, :], reads=[sq], writes=[ssum])
        rsqrt_small(rstd[0:n, :], ssum[0:n, :], tmp1[0:n, :], 1.0 / D, NORM_EPS, reads=[ssum], writes=[tmp1, rstd])
        vstt(yo[0:n, :], res[0:n, :], rstd[0:n, 0:1], normf[0:n, :], ALU.mult, ALU.mult, reads=[res, rstd, normf], writes=[yo])
        S.dma("sync", out_dram_ap, yo[0:n, :], reads=[yo], writes=[out_T])

    oT_list_T = None

    with ExitStack() as es:
        browB = sb(es, "browB", [NS, 3584])
        S.dma("sync", browB[:], browB_d.partition_broadcast(NS), writes=[browB])
        x_s = sb(es, "x_s", [NS, D])
        S.dma("sync", x_s[:], xs, writes=[x_s])
        hTs = sb(es, "hTs", [128, 8, DB + NS], BF16)
        graw_s = sb(es, "graw_s", [NS, 512])
        u_s = sb(es, "u_s", [NS, 512])
        gp_s = sb(es, "gp_s", [NS, 512])
        bonus_s = sb(es, "bonus_s", [NS, 512])
        st8 = sb(es, "st8", [NS, 8])
        st8b = sb(es, "st8b", [NS, 8])
        st8c = sb(es, "st8c", [NS, 8])

        def v3(ap):
            return ap.rearrange("p (h k) -> p h k", k=64)

        def bc8(ap8):
            return ap8.unsqueeze(2).to_broadcast([NS, 8, 64])

        with ExitStack() as e1:
            browA = sb(e1, "browA", [NS, D_SHIFT])
            S.dma("sync", browA[:], browA_d.partition_broadcast(NS), writes=[browA])
            omka_b = sb(e1, "omka_b", [NS, 512])
            vts(omka_b[:], browB[:, BRB_KA:BRB_KA + 512], -1.0, 1.0, ALU.mult, ALU.add, reads=[browB], writes=[omka_b])
            sq_s = sb(e1, "sq_s", [NS, D])
            ss_s = sb(e1, "ss_s", [NS, 1])
            t1_s = sb(e1, "t1_s", [NS, 1])
            rstd_s = sb(e1, "rstd_s", [NS, 1])
            xn_s = sb(e1, "xn_s", [NS, D], BF16)
            act(sq_s[:], x_s[:], AF.Square, reads=[x_s], writes=[sq_s])
            vred(ss_s[:], sq_s[:], reads=[sq_s], writes=[ss_s])
            rsqrt_small(rstd_s[:], ss_s[:], t1_s[:], 1.0 / D, NORM_EPS, reads=[ss_s], writes=[t1_s, rstd_s])
            vts(xn_s[:], x_s[:], rstd_s[:, 0:1], None, ALU.mult, None, reads=[x_s, rstd_s], writes=[xn_s])
            memset(hTs[:, :, 0:DB], 0.0, writes=[hTs])
            for dc in range(8):
                tr(pT[:, dc * 128:dc * 128 + NS], xn_s[:, dc * 128:(dc + 1) * 128], identb[0:NS, 0:NS], reads=[xn_s, identb], writes=[pT])
            vcopy(hTs[:, :, DB:DB + NS], pT[:].rearrange("p (c t) -> p c t", t=128)[:, :, 0:NS], reads=[pT], writes=[hTs])

            p_s = sb(e1, "p_s", [NS, D_SHIFT])
            prev_s = sb(e1, "prev_s", [NS, D_SHIFT])
            col_chunks = [(0, 512), (512, 512), (1024, 512), (1536, 128)]
            kk_ = 0
            for (c0, n) in col_chunks:
                bank = pg[kk_ % 2]; kk_ += 1
                for dc in range(8):
                    mm(bank[0:NS, 0:n], hTs[:, dc, DB:DB + NS], winb[:, dc, c0:c0 + n], dc == 0, dc == 7, reads=[hTs, winb], writes=[bank])
                act(p_s[:, c0:c0 + n], bank[0:NS, 0:n], AF.Copy, reads=[bank], writes=[p_s])
                bank = pg[kk_ % 2]; kk_ += 1
                for dc in range(8):
                    mm(bank[0:NS, 0:n], hTs[:, dc, 0:NS], winb[:, dc, c0:c0 + n], dc == 0, dc == 7, reads=[hTs, winb], writes=[bank])
                vcopy(prev_s[:, c0:c0 + n], bank[0:NS, 0:n], reads=[bank], writes=[prev_s])
            for (c0, dst, fn) in [(1664, graw_s, AF.Silu), (2176, u_s, AF.Copy), (2688, gp_s, AF.Silu)]:
                bank = pg[kk_ % 2]; kk_ += 1
                for dc in range(8):
                    mm(bank[0:NS, :], hTs[:, dc, DB:DB + NS], winb[:, dc, c0:c0 + 512], dc == 0, dc == 7, reads=[hTs, winb], writes=[bank])
                act(dst[:], bank[0:NS, :], fn, reads=[bank], writes=[dst])
            S.dma("sync", prev_s[0:DB, :], sshift, writes=[prev_s])
            S.dma("sync", nss[:], p_s[NS - DB:NS, :], reads=[p_s], writes=[nss])
            S.dma("sync", nps[:, 0:11, :], spool.rearrange("(b j) c -> b j c", j=15)[:, 4:15, :], writes=[nps])
            for t in range(DT):
                S.dma("sync", nps[:, 11 + t, :], u_s[t * DB:(t + 1) * DB, :], reads=[u_s], writes=[nps])

            vtt(prev_s[:], prev_s[:], p_s[:], ALU.subtract, reads=[prev_s, p_s], writes=[prev_s])
            vtt(prev_s[:], prev_s[:], browA[:], ALU.mult, reads=[prev_s, browA], writes=[prev_s])
            vtt(prev_s[:], prev_s[:], p_s[:], ALU.add, reads=[prev_s, p_s], writes=[prev_s])
            ps_s = prev_s
            r_s = ps_s[:, 0:512]
            k_s = ps_s[:, 512:1024]
            v_s = ps_s[:, 1024:1536]

            lT = sb(e1, "lT", [128, NS])
            tr(pM[:, 0:NS], ps_s[:, 1536:1664], ident[0:NS, 0:NS], reads=[ps_s, cst], writes=[pM])
            act(lT[0:64, :], pM[0:64, 0:NS], AF.Tanh, reads=[pM], writes=[lT])
            act(lT[64:128, :], pM[64:128, 0:NS], AF.Copy, reads=[pM], writes=[lT])
            sg_s = sb(e1, "sg_s", [NS, 512])
            a_s = sb(e1, "a_s", [NS, 512])
            mm(pA[0:NS, :], lT[0:64, :], wd[:, :], True, True, reads=[lT, wd], writes=[pA])
            vtt(sg_s[:], pA[0:NS, :], browB[:, BRB_W0:BRB_W0 + 512], ALU.add, reads=[pA, browB], writes=[sg_s])
            act(sg_s[:], sg_s[:], AF.Sigmoid, reads=[sg_s], writes=[sg_s])
            mm(pB[0:NS, :], lT[64:128, :], wa[64:128, :], True, True, reads=[lT, wa], writes=[pB])
            vtt(a_s[:], pB[0:NS, :], browB[:, BRB_A0:BRB_A0 + 512], ALU.add, reads=[pB, browB], writes=[a_s])
            act(a_s[:], a_s[:], AF.Sigmoid, reads=[a_s], writes=[a_s])

            pk = sb(e1, "pk", [NS, 4, 512])
            PQ = {1: 0, 2: 1, 4: 2, 5: 3}
            tmpA = sb(e1, "tmpA", [NS, 512])
            tmpB = sb(e1, "tmpB", [NS, 512])
            act(pk[:, PQ[1], :], sg_s[:], AF.Exp, reads=[sg_s], writes=[pk], scale=-C0)
            vtt(tmpA[:], k_s, browB[:, BRB_KK:BRB_KK + 512], ALU.mult, reads=[ps_s, browB], writes=[tmpA])
            vtt(tmpB[:], tmpA[:], tmpA[:], ALU.mult, reads=[tmpA], writes=[tmpB])
            vred(st8[:], v3(tmpB[:]), reads=[tmpB], writes=[st8])
            rsqrt_small(st8b[:], st8[:], st8c[:], 1.0, L2_EPS, reads=[st8], writes=[st8c, st8b])
            vtt(v3(tmpA[:]), v3(tmpA[:]), bc8(st8b[:]), ALU.mult, reads=[tmpA, st8b], writes=[tmpA])
            vts(pk[:, PQ[4], :], tmpA[:], -1.0, None, ALU.mult, None, reads=[tmpA], writes=[pk])
            vtt(pk[:, PQ[5], :], tmpA[:], a_s[:], ALU.mult, reads=[tmpA, a_s], writes=[pk])
            vtt(tmpB[:], a_s[:], browB[:, BRB_KA:BRB_KA + 512], ALU.mult, reads=[a_s, browB], writes=[tmpB])
            vtt(tmpB[:], tmpB[:], omka_b[:], ALU.add, reads=[tmpB, omka_b], writes=[tmpB])
            vtt(pk[:, PQ[2], :], k_s, tmpB[:], ALU.mult, reads=[ps_s, tmpB], writes=[pk])
            vtt(tmpB[:], r_s, browB[:, BRB_RK:BRB_RK + 512], ALU.mult, reads=[ps_s, browB], writes=[tmpB])
            vtt(tmpB[:], tmpB[:], pk[:, PQ[2], :], ALU.mult, reads=[tmpB, pk], writes=[tmpB])
            vred(st8[:], v3(tmpB[:]), reads=[tmpB], writes=[st8])
            vtt(v3(bonus_s[:]), v3(v_s), bc8(st8[:]), ALU.mult, reads=[ps_s, st8], writes=[bonus_s])
            sview = scr1[:].rearrange("q t b h k -> q (t b) (h k)")
            S.dma("sync", sview[0], r_s, reads=[ps_s], writes=[scr1])
            S.dma("sync", sview[3], v_s, reads=[ps_s], writes=[scr1])
            for qq, slot in PQ.items():
                S.dma("sync", sview[qq], pk[:, slot, :], reads=[pk], writes=[scr1])
            S.finish([scr1], engname="sync")
            S.barrier()
            chk("S1")

        with ExitStack() as e2:
            sIn = sb(e2, "sIn", [128, 6, DT, 64])
            S.dma("sync", sIn[:], scr1[:].rearrange("q t b h k -> (b h) q t k"), reads=[scr1], writes=[sIn])
            St = sb(e2, "St", [128, 64, 64])
            S.dma("sync", St[:].rearrange("p v k -> p (v k)"), swkv, writes=[St])
            tmpS = sb(e2, "tmpS", [128, 64, 64])
            sa = sb(e2, "sa", [128, 64])
            yS = sb(e2, "yS", [128, DT, 64])

            def bv(ap):
                return ap.unsqueeze(1).to_broadcast([128, 64, 64])

            def bk(ap):
                return ap.unsqueeze(2).to_broadcast([128, 64, 64])

            for t in range(DT):
                q = lambda i: sIn[:, i, t, :]
                vtt(tmpS[:], St[:], bv(q(4)), ALU.mult, reads=[St, sIn], writes=[tmpS])
                vred(sa[:], tmpS[:], reads=[tmpS], writes=[sa])
                vtt(St[:], St[:], bv(q(1)), ALU.mult, reads=[St, sIn], writes=[St])
                vtt(tmpS[:], bk(sa[:]), bv(q(5)), ALU.mult, reads=[sa, sIn], writes=[tmpS])
                vtt(St[:], St[:], tmpS[:], ALU.add, reads=[St, tmpS], writes=[St])
                vtt(tmpS[:], bk(q(3)), bv(q(2)), ALU.mult, reads=[sIn], writes=[tmpS])
                vtt(St[:], St[:], tmpS[:], ALU.add, reads=[St, tmpS], writes=[St])
                vtt(tmpS[:], St[:], bv(q(0)), ALU.mult, reads=[St, sIn], writes=[tmpS])
                vred(yS[:, t, :], tmpS[:], reads=[tmpS], writes=[yS])
            S.dma("sync", nws[:], St[:].rearrange("p v k -> p (v k)"), reads=[St], writes=[nws])
            S.dma("sync", scr2[:].rearrange("b h t v -> (b h) t v"), yS[:], reads=[yS], writes=[scr2])
            S.finish([scr2, nws], engname="sync")
            S.barrier()
            chk("S2")

        with ExitStack() as e3:
            yT = sb(e3, "yT", [NS, 512])
            tmpA = sb(e3, "tmpA3", [NS, 512])
            for t in range(DT):
                S.dma("sync", yT[t * DB:(t + 1) * DB, :].rearrange("b (h v) -> b h v", v=64), scr2[:][:, :, t, :], reads=[scr2], writes=[yT])
            vred(st8[:], v3(yT[:]), reads=[yT], writes=[st8])
            vts(st8[:], st8[:], 1.0 / 64, None, ALU.mult, None, reads=[st8], writes=[st8])
            vtt(v3(yT[:]), v3(yT[:]), bc8(st8[:]), ALU.subtract, reads=[yT, st8], writes=[yT])
            vtt(tmpA[:], yT[:], yT[:], ALU.mult, reads=[yT], writes=[tmpA])
            vred(st8[:], v3(tmpA[:]), reads=[tmpA], writes=[st8])
            rsqrt_small(st8b[:], st8[:], st8c[:], 1.0 / 64, GN_EPS, reads=[st8], writes=[st8c, st8b])
            vtt(v3(yT[:]), v3(yT[:]), bc8(st8b[:]), ALU.mult, reads=[yT, st8b], writes=[yT])
            vtt(yT[:], yT[:], browB[:, BRB_GW:BRB_GW + 512], ALU.mult, reads=[yT, browB], writes=[yT])
            vtt(yT[:], yT[:], browB[:, BRB_GB:BRB_GB + 512], ALU.add, reads=[yT, browB], writes=[yT])
            vtt(yT[:], yT[:], bonus_s[:], ALU.add, reads=[yT, bonus_s], writes=[yT])
            vtt(yT[:], yT[:], graw_s[:], ALU.mult, reads=[yT, graw_s], writes=[yT])
            oTs = sb(e3, "oTs", [128, 8, NS], BF16)
            for fb in range(4):
                tr(pA[:, fb * 64:fb * 64 + NS], yT[:, fb * 128:(fb + 1) * 128], ident[0:NS, 0:NS], reads=[yT, cst], writes=[pA])
            vcopy(oTs[:, 0:4, :], pA[:, 0:4 * NS].rearrange("p (f t) -> p f t", t=NS), reads=[pA], writes=[oTs])

            uext = sb(e3, "uext_s", [128, 4, DB, 19])
            sp0 = sb(e3, "sp0", [120, 512])
            sp1 = sb(e3, "sp1", [120, 512])
            S.dma("sync", sp0[:], spool[0:120, :], writes=[sp0])
            S.dma("sync", sp1[:], spool[120:240, :], writes=[sp1])
            for g in range(4):
                tr(pB[:, 0:120], sp0[:, g * 128:(g + 1) * 128], ident[0:120, 0:120], reads=[sp0, cst], writes=[pB])
                tr(pB[:, 128:248], sp1[:, g * 128:(g + 1) * 128], ident[0:120, 0:120], reads=[sp1, cst], writes=[pB])
                vcopy(uext[:, g, 0:8, 0:15], pB[:, 0:120].rearrange("p (b j) -> p b j", j=15), reads=[pB], writes=[uext])
                vcopy(uext[:, g, 8:16, 0:15], pB[:, 128:248].rearrange("p (b j) -> p b j", j=15), reads=[pB], writes=[uext])
                tr(pM[:, 0:NS], u_s[:, g * 128:(g + 1) * 128], ident[0:NS, 0:NS], reads=[u_s, cst], writes=[pM])
                vcopy(uext[:, g, :, 15:19], pM[:, 0:NS].rearrange("p (t b) -> p b t", b=DB), reads=[pM], writes=[uext])
            s2 = sb(e3, "s2_s", [128, 4, DB, 19])
            s4 = sb(e3, "s4_s", [128, 3, DB, 19])
            s8 = sb(e3, "s8_s", [128, 2, DB, 19])
            s16 = sb(e3, "s16_s", [128, 1, DB, 19])
            d_s = sb(e3, "d_s", [128, 4, DT, DB], BF16)
            vtt(s2[:, :, :, 1:19], uext[:, :, :, 1:19], uext[:, :, :, 0:18], ALU.add, reads=[uext], writes=[s2])
            vtt(s4[:, :, :, 3:19], s2[:, 1:4, :, 3:19], s2[:, 1:4, :, 1:17], ALU.add, reads=[s2], writes=[s4])
            vtt(s8[:, :, :, 7:19], s4[:, 1:3, :, 7:19], s4[:, 1:3, :, 3:15], ALU.add, reads=[s4], writes=[s8])
            vtt(s16[:, :, :, 15:19], s8[:, 1:2, :, 15:19], s8[:, 1:2, :, 7:11], ALU.add, reads=[s8], writes=[s16])
            tots = [(s2, 0), (s4, 1), (s8, 2), (s16, 3)]
            for g in range(4):
                tt, off = tots[g]
                vstt(d_s[:, g, :, :].rearrange("p t b -> p b t"), tt[:, g - off, :, 15:19], 1.0 / WINS[g], uext[:, g, :, 15:19],
                     ALU.mult, ALU.subtract, reads=[tt, uext], writes=[d_s])
            gpT = sb(e3, "gpT", [128, 4, NS])
            for g in range(4):
                tr(pM[:, 64 + g * 64:64 + g * 64 + NS], gp_s[:, g * 128:(g + 1) * 128], ident[0:NS, 0:NS], reads=[gp_s, cst], writes=[pM])
            vcopy(gpT[:], pM[:, 64:64 + 4 * NS].rearrange("p (g t) -> p g t", t=NS), reads=[pM], writes=[gpT])
            for g in range(4):
                mm(pA[:, g * 64:g * 64 + NS], pw[:, g, :], d_s[:, g, :, :].rearrange("p t b -> p (t b)"), True, True, reads=[pw, d_s], writes=[pA])
            for g in range(4):
                vstt(oTs[:, 4 + g, :], pA[:, g * 64:g * 64 + NS], pvec[:, PV_PS + g:PV_PS + g + 1], gpT[:, g, :], ALU.mult, ALU.mult,
                     reads=[pA, pvec, gpT], writes=[oTs])

            sq = sb(e3, "sq2_s", [NS, D]); ssum = sb(e3, "ssum_s", [NS, 1])
            tmp1 = sb(e3, "tmp1_s", [NS, 1]); rstd = sb(e3, "rstd2_s", [NS, 1]); yo = sb(e3, "yo_s", [NS, D])
            oT_list_T = oTs
            final_tile((x_s, sq, ssum, tmp1, rstd, yo), NS, x_s, [oTs[:, fc, :] for fc in range(8)], ys[:], ys)
            S.finish([ys, nss, nps], engname="sync")
            S.barrier()
            chk("S3")

    with ExitStack() as es:
        def sbl(name, shape, dt=F32, n=2):
            return [sb(es, f"{name}_{i}", shape, dt) for i in range(n)]

        xt = sbl("xt", [128, D])
        sqx = sb(es, "sqx", [128, D], BF16)
        yo = sb(es, "yo", [128, D])
        ssx = sb(es, "ssx", [128, 1]); t1x = sb(es, "t1x", [128, 1]); rsx = sb(es, "rsx", [128, 1])
        xnb = sb(es, "xnb", [128, D], BF16)
        hT = sb(es, "hT", [128, 8, TB], BF16)
        praw = sbl("praw", [128, 4, TB + 1])
        qsc = sbl("qsc", [128, 4, TB])
        halo = sb(es, "halo", [128, 13])
        omu = sb(es, "omu2", [128, 13])
        psr = sb(es, "psr", [128, 4, TB]); psk = sb(es, "psk", [128, 4, TB]); psv = sb(es, "psv", [128, 4, TB])
        ps12 = sb(es, "ps12", [128, TB])
        psx = [T(g_[:, i, :], f"psx{gi_}_{i}", buf=g_.b) for gi_, g_ in enumerate([psr, psk, psv]) for i in range(4)] + [ps12]
        sg = sb(es, "sg", [128, 4, TB]); av = sb(es, "av", [128, 4, TB])
        gsil = sbl("gsil", [128, 4, TB], BF16)
        gpsil = sb(es, "gpsil", [128, 4, TB], BF16)
        uext = sb(es, "uext", [128, 4, 15 + TB])
        th = sb(es, "th", [64, TB])
        slab = [sb(es, f"slab{i}", [128, 4, TB]) for i in range(7)]
        w1, w2, wAB, g1, g2, g3, g4 = slab
        srot = [sb(es, f"srot{i}", [128, 15 + TB]) for i in range(4)]

        def bfv(i):
            return slab[i][0:64, :, :].rearrange("p f t -> p (f t)").bitcast(BF16)

        XQ = [[T(bfv(2 * c + k).rearrange("p (h s t) -> p h s t", s=2, t=64), f"XQ{c}{k}", buf=slab[2 * c + k].b) for k in range(2)]
              for c in range(NCH)]
        XTt = [[T(bfv(4 + c)[:, k * 512:(k + 1) * 512], f"XT{c}{k}", buf=slab[4 + c].b) for k in range(2)] for c in range(NCH)]
        kkn = sb(es, "kkn", [128, 4, TB]); kmod = sb(es, "kmod", [128, 4, TB]); bv_ = sb(es, "bv_", [128, 4, TB])
        cum = sb(es, "cum", [128, 4, TB])
        dpl = sb(es, "dplb", [128, TB], BF16)
        at = sbl("at", [64, 4, 2, TB], BF16)
        rt = sbl("rt", [64, 4, 2, TB], BF16)
        bt = sb(es, "bt", [64, 4, 2, TB], BF16)
        kt = sb(es, "kt", [64, 4, 2, TB], BF16)
        bh = sb(es, "bh", [128, 4, TB], BF16); kh = sb(es, "kh", [128, 4, TB], BF16); vb = sb(es, "vb", [128, 4, TB], BF16)
        bon = sbl("bon", [128, 4, TB])
        gC = sbl("gC", [64, 4, 2, NCH])
        VT = [[sb(es, f"VT{p}{c}", [64, 512], BF16) for c in range(NCH)] for p in range(2)]
        BKT = [[sb(es, f"BKT{p}{c}", [64, 1024], BF16) for c in range(NCH)] for p in range(2)]
        Aak = [[sb(es, f"Aak{p}{c}", [64, 512], BF16) for c in range(NCH)] for p in range(2)]
        Arb = [[sb(es, f"Arb{p}{c}", [64, 512], BF16) for c in range(NCH)] for p in range(2)]
        Ark = [[sb(es, f"Ark{p}{c}", [64, 512], BF16) for c in range(NCH)] for p in range(2)]
        Minv = [[sb(es, f"Minv{p}{c}", [64, 512], BF16) for c in range(NCH)] for p in range(2)]
        ST = sb(es, "ST", [64, 8, 64]); STb = sb(es, "STb", [64, 8, 64], BF16)
        Wsb = sb(es, "Wsb", [64, 512], BF16); Usb = sb(es, "Usb", [64, 512], BF16)
        yc = sb(es, "yc", [64, 512]); ysq = sb(es, "ysq", [64, 512])
        STt = T(ysq[:].rearrange("p (h v) -> p h v", v=64), "STt", buf=ysq.b)
        m8 = sb(es, "m8", [64, 8]); v8 = sb(es, "v8", [64, 8]); r8 = sb(es, "r8", [64, 8]); t8 = sb(es, "t8", [64, 8])
        o1 = sb(es, "o1", [128, 4, 64])
        oT = sbl("oT", [128, 8, TB], BF16)
        ssum = sb(es, "ssum", [128, 1]); tmp1 = sb(es, "tmp1", [128, 1]); rstd = sb(es, "rstd", [128, 1])
        ppT = T(ysq[0:16, :], "ppT", buf=ysq.b); m13 = sb(es, "m13", [13, 128])
        SvT = T(yc[:].rearrange("p (h k) -> p h k", k=64), "SvT", buf=yc.b)

        memset(halo[:], 0.0, writes=[halo])
        memset(uext[:, :, 0:15], 0.0, writes=[uext])
        memset(ST[:], 0.0, writes=[ST])
        memset(STb[:], 0.0, writes=[STb])
        vts(omu[:], pvec[:, PV_MU:PV_MU + 13], -1.0, 1.0, ALU.mult, ALU.add, reads=[pvec], writes=[omu])

        def b8(ap):
            return ap.unsqueeze(1).to_broadcast([64, 8, 64])

        def h3(ap):
            return ap.rearrange("p (h v) -> p h v", v=64)

        def hc(h):
            return slice(h * 64, (h + 1) * 64)

        maskUs = b8(cst[0:64, C_MUS:C_MUS + 64])
        maskUi = b8(cst[0:64, C_MUI:C_MUI + 64])
        maskLs = b8(cst[0:64, C_MLS:C_MLS + 64])
        ident8 = b8(cst[0:64, C_ID:C_ID + 64])
        rstm = cst[:, C_RST:C_RST + 512]
        st = dict(gk=0, ak=0)
        pT32 = T(pT[:].bitcast(F32), "pT32", buf=pT.b)
        abanks = [pA, pB, pM, pg[1]]

        def nextbank():
            b = abanks[st["ak"] % len(abanks)]
            st["ak"] += 1
            return b

        def pb4(col):
            return pvec[:, col:col + 4].unsqueeze(2).to_broadcast([128, 4, TB])

        def fl(t_):
            return t_[:].rearrange("p f t -> p (f t)")

        def f1a(tb):
            pb = tb % 2
            t0 = tb * TB
            x_t = xt[pb]
            S.dma("sync", x_t[:], xp[t0:t0 + TB, :], writes=[x_t])
            act(sqx[:], x_t[:], AF.Square, reads=[x_t], writes=[sqx])
            vred(ssx[:], sqx[:], reads=[sqx], writes=[ssx])
            rsqrt_small(rsx[:], ssx[:], t1x[:], 1.0 / D, NORM_EPS, reads=[ssx], writes=[t1x, rsx])
            act(xnb[:], x_t[:], AF.Copy, reads=[x_t, rsx], writes=[xnb], scale=rsx[:, 0:1])
            yield
            for dc in range(8):
                tr(pT[:, dc * 128:(dc + 1) * 128], xnb[:, dc * 128:(dc + 1) * 128], identb[:], reads=[xnb, identb], writes=[pT])
            vcopy(hT[:].rearrange("p c t -> p (c t)"), pT[:], reads=[pT], writes=[hT])
            yield

            def gemm_group(ebs):
                bank = pg[0]
                for i, eb in enumerate(ebs):
                    for dc in range(8):
                        mm(bank[:, i * TB:(i + 1) * TB], winb[:, dc, eb * 128:(eb + 1) * 128], hT[:, dc, :], dc == 0, dc == 7,
                           reads=[winb, hT], writes=[bank])
                return bank

            for gi, ebs in enumerate([[0, 1, 2, 3], [4, 5, 6, 7], [8, 9, 10, 11], [12]]):
                bank = gemm_group(ebs)
                yield
                n = len(ebs)
                pr = praw[gi % 2]
                qs = qsc[gi % 2]
                e0 = ebs[0]
                vcopy(pr[:, 0:n, 0:1], halo[:, e0:e0 + n].unsqueeze(2), reads=[halo], writes=[pr], eng="gpsimd")
                act(pr[:, 0:n, 1:TB + 1], bank[:, 0:n * TB].rearrange("p (e t) -> p e t", t=TB), AF.Copy, reads=[bank], writes=[pr])
                for i, eb in enumerate(ebs):
                    act(qs[:, i, :], bank[:, i * TB:(i + 1) * TB], AF.Copy, reads=[bank, omu], writes=[qs], scale=omu[:, eb:eb + 1])
                for i, eb in enumerate(ebs):
                    vstt(psx[eb][:], pr[:, i, 0:TB], pvec[:, PV_MU + eb:PV_MU + eb + 1], qs[:, i, :], ALU.mult, ALU.add,
                         reads=[pr, pvec, qs], writes=[psx[eb]])
                vcopy(halo[:, e0:e0 + n].unsqueeze(2), pr[:, 0:n, TB:TB + 1], reads=[pr], writes=[halo], eng="gpsimd")
                yield
            bank = gemm_group([13, 14, 15, 16])
            act(gsil[pb][:].rearrange("p f t -> p (f t)"), bank[:, :], AF.Silu, reads=[bank], writes=[gsil[pb]])
            yield
            bank = gemm_group([17, 18, 19, 20])
            act(uext[:, :, 15:15 + TB], bank[:, :].rearrange("p (g t) -> p g t", t=TB), AF.Copy, reads=[bank], writes=[uext])
            yield
            bank = gemm_group([21, 22, 23, 24])
            act(gpsil[:].rearrange("p g t -> p (g t)"), bank[:, :], AF.Silu, reads=[bank], writes=[gpsil])
            yield

            act(th[:], psx[12][0:64, :], AF.Tanh, reads=[psx[12]], writes=[th])
            for fb in range(4):
                mm(pg[0][:, fb * TB:(fb + 1) * TB], wd[:, fb * 128:(fb + 1) * 128], th[:], True, True, reads=[wd, th], writes=[pg[0]])
            for fb in range(4):
                mm(pT32[:, fb * TB:(fb + 1) * TB], wa[64:128, fb * 128:(fb + 1) * 128], psx[12][64:128, :], True, True, reads=[wa, psx[12]], writes=[pT32])
            for fb in range(4):
                act(sg[:, fb, :], pg[0][:, fb * TB:(fb + 1) * TB], AF.Sigmoid, reads=[pg[0], pvec], writes=[sg], bias=pvec[:, PV_W0 + fb:PV_W0 + fb + 1])
                act(av[:, fb, :], pT32[:, fb * TB:(fb + 1) * TB], AF.Sigmoid, reads=[pT32, pvec], writes=[av], bias=pvec[:, PV_A0 + fb:PV_A0 + fb + 1])
            yield

            L = 15 + TB
            for g in range(4):
                vtt(srot[0][:, 1:], uext[:, g, 1:], uext[:, g, 0:L - 1], ALU.add, reads=[uext], writes=[srot[0]], eng="gpsimd")
                tot = srot[0]
                if g >= 1:
                    vtt(srot[1][:, 3:], srot[0][:, 3:], srot[0][:, 1:L - 2], ALU.add, reads=[srot[0]], writes=[srot[1]], eng="gpsimd")
                    tot = srot[1]
                if g >= 2:
                    vtt(srot[2][:, 7:], srot[1][:, 7:], srot[1][:, 3:L - 4], ALU.add, reads=[srot[1]], writes=[srot[2]], eng="gpsimd")
                    tot = srot[2]
                if g >= 3:
                    vtt(srot[3][:, 15:], srot[2][:, 15:], srot[2][:, 7:L - 8], ALU.add, reads=[srot[2]], writes=[srot[3]], eng="gpsimd")
                    tot = srot[3]
                vstt(dpl[:], tot[:, 15:], 1.0 / WINS[g], uext[:, g, 15:], ALU.mult, ALU.subtract, reads=[tot, uext], writes=[dpl])
                if tb == 0:
                    vtt(dpl[:, 0:16], tot[:, 15:31], cst[:, C_ICNT + g * 16:C_ICNT + (g + 1) * 16], ALU.mult, reads=[tot, cst], writes=[dpl])
                    vtt(dpl[:, 0:16], dpl[:, 0:16], uext[:, g, 15:31], ALU.subtract, reads=[dpl, uext], writes=[dpl])
                mm(pM[:, 0:TB], pw[:, g, :], dpl[:], True, True, reads=[pw, dpl], writes=[pM])
                vstt(oT[pb][:, 4 + g, :], pM[:, 0:TB], pvec[:, PV_PS + g:PV_PS + g + 1], gpsil[:, g, :], ALU.mult, ALU.mult,
                     reads=[pM, pvec, gpsil], writes=[oT[pb]])
                yield
            if tb == NTB - 1:
                for g in range(4):
                    tr(pA[0:16, g * 128:(g + 1) * 128], uext[:, g, TB - 1:TB + 15], ident, reads=[uext, cst], writes=[pA])
                vcopy(ppT[:], pA[0:16, :], reads=[pA], writes=[ppT])
                S.dma("sync", npp[:], ppT[1:16, :], reads=[ppT], writes=[npp])
                tr(pB[0:13, 0:128], halo[:, 0:13], ident, reads=[halo, cst], writes=[pB])
                vcopy(m13[:], pB[0:13, 0:128], reads=[pB], writes=[m13])
                S.dma("sync", nsp[:], m13[:], reads=[m13], writes=[nsp])
            vcopy(uext[:, :, 0:15], uext[:, :, TB:TB + 15], reads=[uext], writes=[uext], eng="gpsimd")
            yield

        def f1b(tb):
            pb = tb % 2

            bomk = omka[:, 0:4].unsqueeze(2).to_broadcast([128, 4, TB])
            c3 = cum[:].rearrange("p f (c t) -> p (f c) t", t=CH)
            vtt(w1[:], psk[:], pb4(PV_KK), ALU.mult, reads=[psk, pvec], writes=[w1])
            S.op("vector", lambda e: e.tensor_tensor_scan(out=fl(cum), data0=rstm, data1=fl(sg), initial=0.0, op0=ALU.mult, op1=ALU.add),
                 reads=[cst, sg], writes=[cum], cost=1.2)
            vtt(wAB[:], av[:], pb4(PV_KA), ALU.mult, reads=[av, pvec], writes=[wAB], eng="gpsimd")
            vtt(w2[:], w1[:], w1[:], ALU.mult, reads=[w1], writes=[w2], eng="gpsimd")
            vtt(wAB[:], wAB[:], bomk, ALU.add, reads=[wAB, omka], writes=[wAB], eng="gpsimd")
            mm(pM[:, :], onesblk, fl(w2), True, True, reads=[cst, w2], writes=[pM])
            act(g1[:], cum[:], AF.Exp, reads=[cum], writes=[g1], scale=-C0)
            act(g2[:], cum[:], AF.Exp, reads=[cum], writes=[g2], scale=C0)
            vtt(g3[:], cum[:], sg[:], ALU.subtract, reads=[cum, sg], writes=[g3], eng="gpsimd")
            vtt(kmod[:], psk[:], wAB[:], ALU.mult, reads=[psk, wAB], writes=[kmod])
            vts(fl(w2), pM[:, :], L2_EPS, None, ALU.add, None, reads=[pM], writes=[w2])
            yield
            act(g3[:], g3[:], AF.Exp, reads=[g3], writes=[g3], scale=-C0)
            vtt(g4[:].rearrange("p f (c t) -> p (f c) t", t=CH), c3[:, :, CH - 1:CH].to_broadcast([128, 4 * NCH, CH]), c3, ALU.subtract,
                reads=[cum], writes=[g4], eng="gpsimd")
            act(w2[:], w2[:], AF.Ln, reads=[w2], writes=[w2])
            act(w2[:], w2[:], AF.Exp, reads=[w2], writes=[w2], scale=-0.5)
            vtt(wAB[:], psr[:], pb4(PV_RK), ALU.mult, reads=[psr, pvec], writes=[wAB], eng="gpsimd")
            act(g4[:], g4[:], AF.Exp, reads=[g4], writes=[g4], scale=-C0)
            vtt(wAB[:], wAB[:], kmod[:], ALU.mult, reads=[wAB, kmod], writes=[wAB])
            vcopy(vb[:], psv[:], reads=[psv], writes=[vb], eng="gpsimd")
            mm(pA[:, :], onesblk, fl(wAB), True, True, reads=[cst, wAB], writes=[pA])
            vstt(kkn[:], w1[:], -1.0, w2[:], ALU.mult, ALU.mult, reads=[w1, w2], writes=[kkn])
            for j in range(2):
                pp = slice(64 * j, 64 * j + 64)
                act(gC[pb][:, :, j, :], cum[pp, :, :].rearrange("p f (c t) -> p f c t", t=CH)[:, :, :, CH - 1], AF.Exp,
                    reads=[cum], writes=[gC[pb]], scale=-C0)
            yield
            vtt(fl(bon[pb]), pA[:, :], fl(psv), ALU.mult, reads=[pA, psv], writes=[bon[pb]])
            vstt(bv_[:], kkn[:], -1.0, av[:], ALU.mult, ALU.mult, reads=[kkn, av], writes=[bv_])
            p0, p1 = slice(0, 64), slice(64, 128)
            vtt(rt[pb][:, :, 1, :], psr[p1, :, :], g1[p1, :, :], ALU.mult, reads=[psr, g1], writes=[rt[pb]], eng="gpsimd")
            vtt(rt[pb][:, :, 0, :], psr[p0, :, :], g1[p0, :, :], ALU.mult, reads=[psr, g1], writes=[rt[pb]])
            vtt(kt[:, :, 0, :], kmod[p0, :, :], g2[p0, :, :], ALU.mult, reads=[kmod, g2], writes=[kt])
            vtt(kt[:, :, 1, :], kmod[p1, :, :], g2[p1, :, :], ALU.mult, reads=[kmod, g2], writes=[kt], eng="gpsimd")
            vtt(at[pb][:, :, 0, :], kkn[p0, :, :], g3[p0, :, :], ALU.mult, reads=[kkn, g3], writes=[at[pb]])
            vtt(at[pb][:, :, 1, :], kkn[p1, :, :], g3[p1, :, :], ALU.mult, reads=[kkn, g3], writes=[at[pb]])
            vtt(kh[:], kmod[:], g4[:], ALU.mult, reads=[kmod, g4], writes=[kh], eng="gpsimd")
            vtt(bt[:, :, 0, :], bv_[p0, :, :], g2[p0, :, :], ALU.mult, reads=[bv_, g2], writes=[bt])
            vtt(bt[:, :, 1, :], bv_[p1, :, :], g2[p1, :, :], ALU.mult, reads=[bv_, g2], writes=[bt])
            vtt(bh[:], bv_[:], g4[:], ALU.mult, reads=[bv_, g4], writes=[bh])
            yield

        def f2s(tb):
            pb = tb % 2
            css = [slice(c * CH, (c + 1) * CH) for c in range(NCH)]
            for c in range(NCH):
                for qi, srcl in enumerate([bh, kh]):
                    for fb in range(4):
                        tr(pT[0:64, qi * 512 + fb * 128:qi * 512 + (fb + 1) * 128], srcl[:, fb, css[c]], identb[:], reads=[srcl, identb], writes=[pT])
                vcopy(BKT[pb][c][:], pT[0:64, :], reads=[pT], writes=[BKT[pb][c]])
                for fb in range(4):
                    tr(pT[0:64, fb * 128:(fb + 1) * 128], vb[:, fb, css[c]], identb[:], reads=[vb, identb], writes=[pT])
                act(VT[pb][c][:], pT[0:64, 0:512], AF.Copy, reads=[pT], writes=[VT[pb][c]])
                yield

            def hsl(tl, h, c):
                fb, j = divmod(h, 2)
                return tl[:, fb, j, css[c]]

            def amat(Lt, Rt, mask, outs):
                banks = []
                for c in range(NCH):
                    bank = nextbank()
                    banks.append(bank)
                    for h in range(8):
                        mm(bank[0:64, hc(h)], hsl(Lt, h, c), hsl(Rt, h, c), True, True, reads=[Lt, Rt], writes=[bank])
                for c in range(NCH):
                    o_ap, o_t = outs[c]
                    vtt(o_ap, h3(banks[c][0:64, :]), mask, ALU.mult, reads=[banks[c], cst], writes=[o_t])

            amat(bt, at[pb], maskUs, [(XQ[c][0][:, :, 0, :], XQ[c][0]) for c in range(NCH)])
            amat(at[pb], bt, maskLs, [(h3(XTt[c][0][:]), XTt[c][0]) for c in range(NCH)])
            yield
            amat(kt, at[pb], maskUs, [(h3(Aak[pb][c][:]), Aak[pb][c]) for c in range(NCH)])
            for c in range(NCH):
                vtt(XQ[c][0][:, :, 1, :], XQ[c][0][:, :, 0, :], ident8, ALU.add, reads=[XQ[c][0], cst], writes=[XQ[c][0]], eng="gpsimd")
            yield
            amat(bt, rt[pb], maskUi, [(h3(Arb[pb][c][:]), Arb[pb][c]) for c in range(NCH)])
            amat(kt, rt[pb], maskUi, [(h3(Ark[pb][c][:]), Ark[pb][c]) for c in range(NCH)])
            yield
            for lvl in range(6):
                k = lvl % 2
                for c in range(NCH):
                    XQc, XQn, XTc, XTn = XQ[c][k], XQ[c][1 - k], XTt[c][k], XTt[c][1 - k]
                    if lvl == 0:
                        b1 = nextbank()
                        for h in range(8):
                            mm(b1[0:64, hc(h)], XTc[:, hc(h)], XQc[:, h, 0, :], True, True, reads=[XTc, XQc], writes=[b1])
                        b2 = nextbank()
                        for h in range(8):
                            mm(b2[0:64, hc(h)], XQc[:, h, 0, :], XTc[:, hc(h)], True, True, reads=[XTc, XQc], writes=[b2])
                        act(XQn[:, :, 0, :], h3(b1[0:64, :]), AF.Copy, reads=[b1], writes=[XQn])
                        act(h3(XTn[:]), h3(b2[0:64, :]), AF.Copy, reads=[b2], writes=[XTn])
                        vcopy(XQn[:, :, 1, :], XQc[:, :, 1, :], reads=[XQc], writes=[XQn], eng="gpsimd")
                    elif lvl < 5:
                        bks = [nextbank(), nextbank()]
                        for h in range(8):
                            bk = bks[h // 4]
                            hh = h % 4
                            mm(bk[0:64, hh * 128:(hh + 1) * 128], XTc[:, hc(h)], XQc[:, h, :, :].rearrange("p s t -> p (s t)"), True, True,
                               reads=[XTc, XQc], writes=[bk])
                        b2 = nextbank()
                        for h in range(8):
                            mm(b2[0:64, hc(h)], XQc[:, h, 0, :], XTc[:, hc(h)], True, True, reads=[XTc, XQc], writes=[b2])
                        act(h3(XTn[:]), h3(b2[0:64, :]), AF.Copy, reads=[b2], writes=[XTn])
                        for g_ in range(2):
                            pv_ = bks[g_][0:64, :].rearrange("p (h s t) -> p h s t", s=2, t=64)
                            hsl_ = slice(4 * g_, 4 * g_ + 4)
                            act(XQn[:, hsl_, 0, :], pv_[:, :, 0, :], AF.Copy, reads=[bks[g_]], writes=[XQn])
                            vtt(XQn[:, hsl_, 1, :], pv_[:, :, 1, :], XQc[:, hsl_, 1, :], ALU.add, reads=[bks[g_], XQc], writes=[XQn])
                    else:
                        b1 = nextbank()
                        for h in range(8):
                            mm(b1[0:64, hc(h)], XTc[:, hc(h)], XQc[:, h, 1, :], True, True, reads=[XTc, XQc], writes=[b1])
                        vtt(h3(Minv[pb][c][:]), h3(b1[0:64, :]), XQc[:, :, 1, :], ALU.add, reads=[b1, XQc], writes=[Minv[pb][c]])
                    yield

        def chain(tb):
            pb = tb % 2
            t0 = tb * TB
            for c in range(NCH):
                cs = slice(c * CH, (c + 1) * CH)
                aT, rT = at[pb], rt[pb]
                VTc, BKTc, Aakc, Arbc, Arkc, Minvc = VT[pb][c], BKT[pb][c], Aak[pb][c], Arb[pb][c], Ark[pb][c], Minv[pb][c]
                for h in range(8):
                    fb, j = divmod(h, 2)
                    mm(pC[0:64, hc(h)], aT[:, fb, j, cs], STb[:, h, :], True, False, reads=[aT, STb], writes=[pC])
                    mm(pC[0:64, hc(h)], Aakc[:, hc(h)], VTc[:, hc(h)], False, True, reads=[Aakc, VTc], writes=[pC])
                act(Wsb[:], pC[0:64, :], AF.Copy, reads=[pC], writes=[Wsb])
                yield
                for h in range(8):
                    mm(pC[0:64, hc(h)], Minvc[:, hc(h)], Wsb[:, hc(h)], True, True, reads=[Minvc, Wsb], writes=[pC])
                act(Usb[:], pC[0:64, :], AF.Copy, reads=[pC], writes=[Usb])
                yield
                for h in range(8):
                    mm(pC[0:64, hc(h)], BKTc[:, hc(h)], Usb[:, hc(h)], True, False, reads=[BKTc, Usb], writes=[pC])
                    mm(pC[0:64, hc(h)], BKTc[:, 512 + h * 64:512 + (h + 1) * 64], VTc[:, hc(h)], False, True, reads=[BKTc, VTc], writes=[pC])
                for h in range(8):
                    fb, j = divmod(h, 2)
                    mm(pD[0:64, hc(h)], rT[:, fb, j, cs], STb[:, h, :], True, False, reads=[rT, STb], writes=[pD])
                    mm(pD[0:64, hc(h)], Arbc[:, hc(h)], Usb[:, hc(h)], False, False, reads=[Arbc, Usb], writes=[pD])
                    mm(pD[0:64, hc(h)], Arkc[:, hc(h)], VTc[:, hc(h)], False, True, reads=[Arkc, VTc], writes=[pD])
                vtt(STt[:], ST[:], gC[pb][:].rearrange("p f j c -> p (f j) c")[:, :, c:c + 1].to_broadcast([64, 8, 64]), ALU.mult,
                    reads=[ST, gC[pb]], writes=[STt])
                vtt(ST[:], STt[:], h3(pC[0:64, :]), ALU.add, reads=[STt, pC], writes=[ST])
                act(STb[:], ST[:], AF.Copy, reads=[ST], writes=[STb])
                yield
                y3 = h3(pD[0:64, :])
                vred(m8[:], y3, reads=[pD], writes=[m8])
                vts(m8[:], m8[:], 1.0 / 64, None, ALU.mult, None, reads=[m8], writes=[m8])
                vtt(h3(yc[:]), y3, m8[:].unsqueeze(2).to_broadcast([64, 8, 64]), ALU.subtract, reads=[pD, m8], writes=[yc])
                act(ysq[:], yc[:], AF.Square, reads=[yc], writes=[ysq])
                vred(v8[:], h3(ysq[:]), reads=[ysq], writes=[v8])
                rsqrt_small(r8[:], v8[:], t8[:], 1.0 / 64, GN_EPS, reads=[v8], writes=[t8, r8])
                vtt(h3(yc[:]), h3(yc[:]), r8[:].unsqueeze(2).to_broadcast([64, 8, 64]), ALU.mult, reads=[yc, r8], writes=[yc], eng="gpsimd")
                yield
                for fb in range(4):
                    tr(pD[:, fb * 64:(fb + 1) * 64], yc[:, fb * 128:(fb + 1) * 128], ident[0:64, 0:64], reads=[yc, cst], writes=[pD])
                for fb in range(4):
                    vts(o1[:, fb, :], pD[:, fb * 64:(fb + 1) * 64], pvec[:, PV_GW + fb:PV_GW + fb + 1], pvec[:, PV_GB + fb:PV_GB + fb + 1],
                        ALU.mult, ALU.add, reads=[pD, pvec], writes=[o1])
                vtt(o1[:], o1[:], bon[pb][:, :, cs], ALU.add, reads=[o1, bon[pb]], writes=[o1], eng="gpsimd")
                vtt(oT[pb][:, 0:4, cs], o1[:], gsil[pb][:, :, cs], ALU.mult, reads=[o1, gsil[pb]], writes=[oT[pb]])
                yield
            x_t = xt[pb]
            for half in range(2):
                bank = pD if half == 0 else pC
                for fc in range(8):
                    mm(bank[:, :], oT[pb][:, fc, :], woutb[:, fc, half * 512:(half + 1) * 512], fc == 0, fc == 7, reads=[oT[pb], woutb], writes=[bank])
                vtt(x_t[:, half * 512:(half + 1) * 512], bank[:, :], x_t[:, half * 512:(half + 1) * 512], ALU.add, reads=[bank, x_t], writes=[x_t])
                yield
            act(yo[:], x_t[:], AF.Square, reads=[x_t], writes=[yo])
            vred(ssum[:], yo[:], reads=[yo], writes=[ssum])
            rsqrt_small(rstd[:], ssum[:], tmp1[:], 1.0 / D, NORM_EPS, reads=[ssum], writes=[tmp1, rstd])
            vstt(yo[:], x_t[:], rstd[:, 0:1], normf[:], ALU.mult, ALU.mult, reads=[x_t, rstd, normf], writes=[yo])
            S.dma("sync", yp[t0:t0 + TB, :], yo[:], reads=[yo], writes=[yp])
            yield

        def run_all(g):
            n = 0
            for _ in g:
                n += 1
            return n

        def interleave(ga, na, gb, nb):
            ia = ib = 0
            da = db = False
            while not (da and db):
                pick_a = (not da) and (db or (ia * nb <= ib * na))
                if pick_a:
                    try:
                        next(ga); ia += 1
                    except StopIteration:
                        da = True
                else:
                    try:
                        next(gb); ib += 1
                    except StopIteration:
                        db = True
            return ia, ib

        def record_units(g):
            units = []
            S.rec = []
            for _ in g:
                if S.rec:
                    units.append(S.rec)
                S.rec = []
            if S.rec:
                units.append(S.rec)
            S.rec = None
            return units

        Z = [record_units(f1a(t)) for t in range(NTB)]
        Y = []
        for t in range(NTB):
            Y.append(record_units(f1b(t)))
            Y.append(record_units(f2s(t)))
        X = [record_units(chain(t)) for t in range(NTB)]

        def ok(name, i, done):
            if name == "Z":
                return done["Y"] >= 2 * (i - 1) + 1 and done["X"] >= i - 1
            if name == "Y":
                t, part = divmod(i, 2)
                if part == 0:
                    return done["Z"] >= t + 1 and done["X"] >= t - 1
                return True
            return done["Y"] >= 2 * i + 2

        S.merge_emit({"X": X, "Y": Y, "Z": Z}, ok)
        for h in range(8):
            tr(pA[0:64, h * 64:(h + 1) * 64], ST[:, h, :], ident[0:64, 0:64], reads=[ST, cst], writes=[pA])
        vcopy(SvT[:].rearrange("p h k -> p (h k)"), pA[0:64, :], reads=[pA], writes=[SvT])
        S.dma("sync", nwp[:].rearrange("h v k -> v h k"), SvT[:], reads=[SvT], writes=[nwp])
        S.finish([yp, ys, nsp, nwp, npp, nss, nws, nps], engname="sync")
        S.barrier()
    es_top.close()
    return nc, S


_CACHE = {}


def _consts():
    cst = np.zeros((128, C_END), np.float32)
    cst[:, C_ID:C_ID + 128] = np.eye(128, dtype=np.float32)
    ob = np.zeros((128, 128), np.float32)
    ob[0:64, 0:64] = 1.0
    ob[64:128, 64:128] = 1.0
    cst[:, C_ONES:C_ONES + 128] = ob
    s = np.arange(64)[:, None]
    t = np.arange(64)[None, :]
    mus = (s < t).astype(np.float32)
    mui = (s <= t).astype(np.float32)
    mls = (s > t).astype(np.float32)
    i64 = np.eye(64, dtype=np.float32)
    cst[0:64, C_MUS:C_MUS + 64] = mus
    cst[0:64, C_MUI:C_MUI + 64] = mui
    cst[0:64, C_MLS:C_MLS + 64] = mls
    rst = np.ones((512,), np.float32)
    rst[::CH] = 0.0
    cst[:, C_RST:C_RST + 512] = rst[None, :]
    for g, w in enumerate(WINS):
        pos = np.arange(16)
        cst[:, C_ICNT + g * 16:C_ICNT + (g + 1) * 16] = (1.0 / np.minimum(pos + 1, w)).astype(np.float32)[None, :]
    return cst


def kernel(x_prompt, x_sample, state_shift, state_wkv, state_pool, norm_w, w_in, mu_shift,
           w_decay_b, w0, w_aaa_b, a0, k_k, k_a, r_k, gn_w, gn_b, pool_w, pool_scale, w_out, norm_f):
    f = lambda a: np.ascontiguousarray(np.asarray(a, dtype=np.float32))
    x_prompt, x_sample, state_shift, state_wkv, state_pool = map(f, (x_prompt, x_sample, state_shift, state_wkv, state_pool))
    if "nc" not in _CACHE:
        _CACHE["nc"] = build_program()
    nc, S = _CACHE["nc"]

    def colmajor(v, n):
        return f(v).reshape(n, 128).T

    pvec = np.concatenate([
        colmajor(norm_w[0], 8), colmajor(mu_shift[0], 13), colmajor(w0[0], 4), colmajor(a0[0], 4), colmajor(k_k[0], 4),
        colmajor(k_a[0], 4), colmajor(f(r_k[0]).reshape(-1), 4), colmajor(gn_w[0], 4), colmajor(gn_b[0], 4), colmajor(pool_scale[0], 4)], axis=1)
    pvec = f(pvec)
    browA = f(f(mu_shift[0])[None, :])
    browB = f(np.concatenate([f(w0[0]), f(a0[0]), f(k_k[0]), f(k_a[0]), f(r_k[0]).reshape(-1), f(gn_w[0]), f(gn_b[0])])[None, :])
    cst = _consts()
    shared = {
        "w_in": f(w_in[0]), "w_out": f(w_out[0]), "wdec": f(w_decay_b[0]), "waaa": f(w_aaa_b[0]), "poolw": f(pool_w[0]),
        "pvec": pvec, "browA": browA, "browB": browB, "normf": f(norm_f)[None, :], "cst": cst,
    }
    in_maps = []
    for c in range(NCORE):
        bs = slice(c * DB, (c + 1) * DB)
        m = dict(shared)
        m["xp"] = x_prompt[c]
        m["xs"] = f(x_sample[bs].transpose(1, 0, 2).reshape(NS, D))
        m["sshift"] = state_shift[0, bs]
        m["swkv"] = f(state_wkv[0, bs].reshape(128, 4096))
        m["spool"] = f(state_pool[0, bs].reshape(DB * 15, 512))
        in_maps.append(m)
    res = run_bass_kernel_spmd(nc, in_maps, core_ids=list(range(NCORE)))
    R = res.results
    y_prompt = np.stack([R[c]["yp"] for c in range(NCORE)], axis=0)
    y_sample = np.concatenate([R[c]["ys"].reshape(DT, DB, D).transpose(1, 0, 2) for c in range(NCORE)], axis=0)
    nsp = np.stack([R[c]["nsp"].reshape(D_SHIFT) for c in range(NCORE)], axis=0)[None]
    nwp = np.stack([R[c]["nwp"] for c in range(NCORE)], axis=0)[None]
    npp = np.stack([R[c]["npp"] for c in range(NCORE)], axis=0)[None]
    nss = np.concatenate([R[c]["nss"] for c in range(NCORE)], axis=0)[None]
    nws = np.concatenate([R[c]["nws"].reshape(DB, 8, 64, 64) for c in range(NCORE)], axis=0)[None]
    nps = np.concatenate([R[c]["nps"] for c in range(NCORE)], axis=0)[None]
    out = (y_prompt, y_sample, nsp, nwp, npp, nss, nws, nps)
    return tuple(np.ascontiguousarray(o.astype(np.float32)) for o in out)
```

```python
import numpy as np
from contextlib import ExitStack
import concourse.bass as bass
import concourse.mybir as mybir
from concourse.bass_utils import run_bass_kernel_spmd

F32 = mybir.dt.float32
BF16 = mybir.dt.bfloat16
AF = mybir.ActivationFunctionType
ALU = mybir.AluOpType
AX = mybir.AxisListType

D = 1024
SEQ = 2048
NCORE = 8
DB = 16
DT = 4
NS = DB * DT
D_SHIFT = 1664
D_IN = 3200
C0 = float(np.exp(-0.5))
NORM_EPS = 1e-6
GN_EPS = 64e-5
L2_EPS = 1e-12
TB = 128
NTB = SEQ // TB
CH = 64
NCH = TB // CH
WINS = (2, 4, 8, 16)

C_ID, C_ONES, C_MUS, C_MUI, C_MLS, C_RST, C_ICNT, C_END = 0, 128, 256, 320, 384, 448, 960, 1024
PV_NW, PV_MU, PV_W0, PV_A0, PV_KK, PV_KA, PV_RK, PV_GW, PV_GB, PV_PS, PV_END = 0, 8, 21, 25, 29, 33, 37, 41, 45, 49, 53
BRB_W0, BRB_A0, BRB_KK, BRB_KA, BRB_RK, BRB_GW, BRB_GB = 0, 512, 1024, 1536, 2048, 2560, 3072


class Buf:
    __slots__ = ("name", "w", "r")

    def __init__(self, name):
        self.name = name
        self.w = None
        self.r = []


class T:
    def __init__(self, t, name, buf=None):
        self.t = t
        self.b = buf if buf is not None else Buf(name)

    def __getitem__(self, k):
        return self.t[k]


class Sched:
    def __init__(self, nc, n_dma_sems=32):
        self.nc = nc
        self.eng = {}
        for name in ["tensor", "vector", "scalar", "gpsimd", "sync"]:
            h = getattr(nc, name)
            sem = nc.alloc_semaphore(name="prog_" + name)
            self.eng[name] = dict(h=h, sem=sem, cnt=0, waited={})
        self.dma_sems = [dict(sem=nc.alloc_semaphore(name=f"dma{i}"), cnt=0) for i in range(n_dma_sems)]
        self.dma_rr = 0
        self.ninstr = 0
        self.rec = None

    def _wait(self, engname, tok):
        sem, val, src = tok
        e = self.eng[engname]
        key = id(sem)
        if e["waited"].get(key, 0) >= val:
            return
        e["h"].wait_ge(sem, val)
        e["waited"][key] = val
        self.ninstr += 1

    def _deps(self, engname, reads, writes):
        toks = []
        for b in reads:
            if b.w is not None:
                toks.append(b.w)
        for b in writes:
            if b.w is not None:
                toks.append(b.w)
            toks.extend(b.r)
        for tok in toks:
            if tok[2] == engname and engname == "tensor":
                continue
            self._wait(engname, tok)

    @staticmethod
    def _bufs(xs):
        return [x.b if isinstance(x, T) else x for x in xs]

    def _record(self, tok, reads, writes):
        for b in reads:
            b.r.append(tok)
            if len(b.r) > 64:
                b.r = b.r[-64:] if False else b.r
        for b in writes:
            b.w = tok
            b.r = []

    def op(self, engname, fn, reads=(), writes=(), cost=0.3):
        reads = self._bufs(reads)
        writes = self._bufs(writes)
        if self.rec is not None:
            self.rec.append(("op", engname, fn, reads, writes, cost, None))
            return None
        e = self.eng[engname]
        self._deps(engname, reads, writes)
        ins = fn(e["h"])
        e["cnt"] += 1
        ins.then_inc(e["sem"], 1)
        e["waited"][id(e["sem"])] = max(e["waited"].get(id(e["sem"]), 0), 0)
        tok = (e["sem"], e["cnt"], engname)
        self._record(tok, reads, writes)
        self.ninstr += 1
        return tok

    def dma(self, qname, out, in_, reads=(), writes=(), **kw):
        reads = self._bufs(reads)
        writes = self._bufs(writes)
        if self.rec is not None:
            self.rec.append(("dma", qname, (out, in_), reads, writes, 2.5, kw))
            return None
        e = self.eng[qname]
        self._deps(qname, reads, writes)
        d = self.dma_sems[self.dma_rr]
        self.dma_rr = (self.dma_rr + 1) % len(self.dma_sems)
        if d["cnt"] > 0:
            self._wait(qname, (d["sem"], 16 * d["cnt"], "dma"))
        ins = e["h"].dma_start(out=out, in_=in_, **kw)
        d["cnt"] += 1
        ins.then_inc(d["sem"], 16)
        tok = (d["sem"], 16 * d["cnt"], "dma")
        self._record(tok, reads, writes)
        self.ninstr += 1
        return tok

    def emit(self, r):
        kind, eng, fn, reads, writes, cost, kw = r
        if kind == "op":
            self.op(eng, fn, reads=reads, writes=writes)
        else:
            self.dma(eng, fn[0], fn[1], reads=reads, writes=writes, **kw)

    def merge_emit(self, A, B, a_ok, b_ok):
        eng_free = {}
        ready = {}
        acc = {}

        def est(r):
            kind, eng, fn, reads, writes, cost, kw = r
            t = eng_free.get(eng, 0.0)
            for b in reads:
                rt_, re_ = ready.get(id(b), (0.0, eng))
                t = max(t, rt_ + (0.15 if re_ != eng else 0.0))
            for b in writes:
                rt_, re_ = ready.get(id(b), (0.0, eng))
                t = max(t, rt_ + (0.15 if re_ != eng else 0.0), acc.get(id(b), 0.0) + 0.1)
            return t

        def commit(r, t):
            kind, eng, fn, reads, writes, cost, kw = r
            if kind == "dma":
                eng_free[eng] = t + 0.1
                end = t + cost
            else:
                end = t + cost
                eng_free[eng] = end
            for b in reads:
                acc[id(b)] = max(acc.get(id(b), 0.0), end)
            for b in writes:
                ready[id(b)] = (end, eng)
                acc[id(b)] = max(acc.get(id(b), 0.0), end)

        last_end = [0.0, 0.0]

        def run_unit(u):
            e_ = 0.0
            for r in u:
                t_ = est(r)
                commit(r, t_)
                e_ = max(e_, t_ + r[5])
                self.emit(r)
            return e_

        ia = ib = 0
        ja = jb = 0
        while ia < len(A) or ib < len(B):
            ca = None
            cb = None
            if ia < len(A) and (ja > 0 or a_ok(ia, ib)):
                ca = A[ia][ja]
            if ib < len(B) and (jb > 0 or b_ok(ib, ia)):
                cb = B[ib][jb]
            assert ca is not None or cb is not None, (ia, ib, ja, jb)
            ta = est(ca[0]) if ca is not None else None
            tb_ = est(cb[0]) if cb is not None else None
            if cb is None:
                pick_a = True
            elif ca is None:
                pick_a = False
            elif abs(ta - tb_) <= 1.0:
                pick_a = last_end[0] <= last_end[1]
            else:
                pick_a = ta < tb_
            if pick_a:
                last_end[0] = run_unit(ca); ja += 1
                if ja == len(A[ia]):
                    ia += 1; ja = 0
            else:
                last_end[1] = run_unit(cb); jb += 1
                if jb == len(B[ib]):
                    ib += 1; jb = 0

    def barrier(self):
        toks = [(e["sem"], e["cnt"], n) for n, e in self.eng.items() if e["cnt"] > 0]
        toks += [(d["sem"], 16 * d["cnt"], "dma") for d in self.dma_sems if d["cnt"] > 0]
        for n in self.eng:
            for tok in toks:
                if tok[2] == n:
                    continue
                self._wait(n, tok)

    def finish(self, tiles, engname="sync"):
        for b in self._bufs(tiles):
            if b.w is not None:
                self._wait(engname, b.w)


class _Stop(Exception):
    pass


def build_program(stop=None):
    nc = bass.Bass("TRN2", target_bir_lowering=False)
    S = Sched(nc)
    try:
        _build_body(nc, S, stop)
    except _Stop:
        S.barrier()
    return nc, S


def _build_body(nc, S, stop):
    def chk(label):
        if stop == label:
            raise _Stop()


    def din(name, shape):
        return nc.dram_tensor(name, list(shape), F32, kind="ExternalInput").ap()

    def dout(name, shape):
        return T(nc.dram_tensor(name, list(shape), F32, kind="ExternalOutput").ap(), name)

    xp = din("xp", [SEQ, D])
    xs = din("xs", [NS, D])
    sshift = din("sshift", [DB, D_SHIFT])
    swkv = din("swkv", [128, 4096])
    spool = din("spool", [DB * 15, 512])
    w_in = din("w_in", [D, D_IN])
    w_out = din("w_out", [D, D])
    wdec = din("wdec", [64, 512])
    waaa = din("waaa", [64, 512])
    poolw = din("poolw", [4, 128, 128])
    pvec_d = din("pvec", [128, PV_END])
    browA_d = din("browA", [1, D_SHIFT])
    browB_d = din("browB", [1, 3584])
    normf_d = din("normf", [1, D])
    cst_d = din("cst", [128, C_END])

    yp = dout("yp", [SEQ, D])
    ys = dout("ys", [NS, D])
    nsp = dout("nsp", [13, 128])
    nwp = dout("nwp", [8, 64, 64])
    npp = dout("npp", [15, 512])
    nss = dout("nss", [DB, D_SHIFT])
    nws = dout("nws", [128, 4096])
    nps = dout("nps", [DB, 15, 512])
    scr1 = T(nc.dram_tensor("scr1", [6, DT, DB, 8, 64], F32, kind="Internal").ap(), "scr1")
    scr2 = T(nc.dram_tensor("scr2", [DB, 8, DT, 64], F32, kind="Internal").ap(), "scr2")

    es_top = ExitStack()

    def sb(es, name, shape, dt=F32):
        return T(es.enter_context(nc.sbuf_tensor("s_" + name, list(shape), dt)), name)

    def pst(name, shape, dt=F32):
        return T(nc.alloc_psum_tensor("p_" + name, list(shape), dt), name)

    def nel(ap):
        n = 1
        for s_ in ap.shape[1:]:
            n *= s_
        return n

    def mm(out, lhsT, rhs, start, stop, reads, writes):
        passes = 4 if lhsT.dtype == F32 else 1
        c_ = max(0.055, nel(rhs) * passes / 2000.0 + 0.03)
        S.op("tensor", lambda e: e.matmul(out, lhsT=lhsT, rhs=rhs, start=start, stop=stop), reads=reads, writes=writes, cost=c_)

    def tr(out, in_, ident, reads, writes):
        S.op("tensor", lambda e: e.transpose(out, in_, ident), reads=reads, writes=writes, cost=0.13)

    def act(out, in_, func, reads, writes, bias=None, scale=None, eng="scalar"):
        kw = {}
        if bias is not None:
            kw["bias"] = bias
        if scale is not None:
            kw["scale"] = scale
        S.op("scalar", lambda e: e.activation(out=out, in_=in_, func=func, **kw), reads=reads, writes=writes,
             cost=0.1 + 0.1 * len(kw) + nel(in_) * 0.00095)

    def ecost(eng, n):
        return 0.08 + n * (0.00105 if eng == "vector" else 0.0025)

    def vtt(out, in0, in1, op, reads, writes, eng="vector"):
        S.op(eng, lambda e: e.tensor_tensor(out=out, in0=in0, in1=in1, op=op), reads=reads, writes=writes, cost=ecost(eng, nel(out)))

    def vts(out, in0, s1, s2, op0, op1, reads, writes, eng="vector"):
        if op1 is None:
            S.op(eng, lambda e: e.tensor_scalar(out=out, in0=in0, scalar1=s1, scalar2=None, op0=op0), reads=reads, writes=writes,
                 cost=ecost(eng, nel(out)))
        else:
            S.op(eng, lambda e: e.tensor_scalar(out=out, in0=in0, scalar1=s1, scalar2=s2, op0=op0, op1=op1), reads=reads, writes=writes,
                 cost=ecost(eng, nel(out)))

    def vstt(out, in0, scalar, in1, op0, op1, reads, writes):
        S.op("vector", lambda e: e.scalar_tensor_tensor(out=out, in0=in0, scalar=scalar, in1=in1, op0=op0, op1=op1), reads=reads, writes=writes,
             cost=ecost("vector", nel(out)))

    def vcopy(out, in_, reads, writes, eng="vector"):
        S.op(eng, lambda e: e.tensor_copy(out=out, in_=in_), reads=reads, writes=writes, cost=ecost(eng, nel(out)))

    def vred(out, in_, reads, writes):
        S.op("vector", lambda e: e.tensor_reduce(out=out, in_=in_, axis=AX.X, op=ALU.add), reads=reads, writes=writes,
             cost=ecost("vector", nel(in_)))

    def vrecip(out, in_, reads, writes):
        S.op("vector", lambda e: e.reciprocal(out=out, in_=in_), reads=reads, writes=writes, cost=0.08 + nel(out) * 0.0084)

    def memset(ap, val, writes, eng="gpsimd"):
        S.op(eng, lambda e: e.memset(ap, val), writes=writes)

    def rsqrt_small(out, in_, tmp, scale, eps, reads, writes):
        act(tmp, in_, AF.Sqrt, reads=reads, writes=writes, bias=None, scale=None) if False else None
        vts(tmp, in_, scale, eps, ALU.mult, ALU.add, reads=reads, writes=writes)
        act(tmp, tmp, AF.Sqrt, reads=writes, writes=writes)
        vrecip(out, tmp, reads=writes, writes=writes)

    pg = [pst(f"pg{i}", [128, 512]) for i in range(2)]
    pT = pst("pT", [128, 1024], BF16)
    pM = pst("pM", [128, 512])
    pA = pst("pA", [128, 512])
    pB = pst("pB", [128, 512])
    pC = pst("pC", [128, 512])
    pD = pst("pD", [128, 512])

    cst = sb(es_top, "cst", [128, C_END])
    pvec = sb(es_top, "pvec", [128, PV_END])
    omu = sb(es_top, "omu", [128, 13])
    omka = sb(es_top, "omka", [128, 4])
    identb = sb(es_top, "identb", [128, 128], BF16)
    winb = sb(es_top, "winb", [128, 8, D_IN], BF16)
    woutb = sb(es_top, "woutb", [128, 8, D], BF16)
    wd = sb(es_top, "wd", [64, 512])
    wa = sb(es_top, "wa", [128, 512])
    pw = sb(es_top, "pw", [128, 4, 128])
    normf = sb(es_top, "normf", [128, D])

    ident = cst[:, C_ID:C_ID + 128]
    onesblk = cst[:, C_ONES:C_ONES + 128]

    S.dma("sync", cst[:], cst_d, writes=[cst])
    S.dma("sync", pvec[:], pvec_d, writes=[pvec])
    S.dma("sync", wd[:], wdec, writes=[wd])
    S.dma("sync", wa[64:128, :], waaa, writes=[wa])
    S.dma("sync", pw[:], poolw.rearrange("g c e -> c g e"), writes=[pw])
    S.dma("sync", normf[:], normf_d.partition_broadcast(128), writes=[normf])
    vcopy(identb[:], ident, reads=[cst], writes=[identb])
    vts(omka[:], pvec[:, PV_KA:PV_KA + 4], -1.0, 1.0, ALU.mult, ALU.add, reads=[pvec], writes=[omka])

    with ExitStack() as es:
        stg = [sb(es, f"stg{i}", [128, D_IN]) for i in range(2)]
        for dc in range(8):
            st = stg[dc % 2]
            S.dma("sync", st[:], w_in[dc * 128:(dc + 1) * 128, :], writes=[st])
            h = D_IN // 2
            vts(winb[:, dc, 0:h], st[:, 0:h], pvec[:, PV_NW + dc:PV_NW + dc + 1], None, ALU.mult, None, reads=[st, pvec], writes=[winb])
            act(winb[:, dc, h:], st[:, h:], AF.Copy, reads=[st, pvec], writes=[winb], scale=pvec[:, PV_NW + dc:PV_NW + dc + 1])
        for fc in range(8):
            st = stg[fc % 2]
            S.dma("sync", st[:, 0:D], w_out[fc * 128:(fc + 1) * 128, :], writes=[st])
            vcopy(woutb[:, fc, 0:512], st[:, 0:512], reads=[st], writes=[woutb])
            act(woutb[:, fc, 512:], st[:, 512:D], AF.Copy, reads=[st], writes=[woutb])
        S.barrier()
        chk("W")

    def final_tile(es_tiles, n, x_t, oT_list, out_dram_ap, out_T):
        res, sq, ssum, tmp1, rstd, yo = es_tiles
        for half in range(2):
            bank = pD if half == 0 else pC
            for fc in range(8):
                mm(bank[0:n, :], oT_list[fc], woutb[:, fc, half * 512:(half + 1) * 512], fc == 0, fc == 7,
                   reads=[oT_list_T, woutb], writes=[bank])
            vtt(res[0:n, half * 512:(half + 1) * 512], bank[0:n, :], x_t[0:n, half * 512:(half + 1) * 512], ALU.add,
                reads=[bank, x_t], writes=[res])
        act(sq[0:n, :], res[0:n, :], AF.Square, reads=[res], writes=[sq])
        vred(ssum[0:n, :], sq[0:n, :], reads=[sq], writes=[ssum])
        rsqrt_small(rstd[0:n, :], ssum[0:n, :], tmp1[0:n, :], 1.0 / D, NORM_EPS, reads=[ssum], writes=[tmp1, rstd])
        vstt(yo[0:n, :], res[0:n, :], rstd[0:n, 0:1], normf[0:n, :], ALU.mult, ALU.mult, reads=[res, rstd, normf], writes=[yo])
        S.dma("sync", out_dram_ap, yo[0:n, :], reads=[yo], writes=[out_T])

    oT_list_T = None

    with ExitStack() as es:
        browB = sb(es, "browB", [NS, 3584])
        S.dma("sync", browB[:], browB_d.partition_broadcast(NS), writes=[browB])
        x_s = sb(es, "x_s", [NS, D])
        S.dma("sync", x_s[:], xs, writes=[x_s])
        hTs = sb(es, "hTs", [128, 8, DB + NS], BF16)
        graw_s = sb(es, "graw_s", [NS, 512])
        u_s = sb(es, "u_s", [NS, 512])
        gp_s = sb(es, "gp_s", [NS, 512])
        bonus_s = sb(es, "bonus_s", [NS, 512])
        st8 = sb(es, "st8", [NS, 8])
        st8b = sb(es, "st8b", [NS, 8])
        st8c = sb(es, "st8c", [NS, 8])

        def v3(ap):
            return ap.rearrange("p (h k) -> p h k", k=64)

        def bc8(ap8):
            return ap8.unsqueeze(2).to_broadcast([NS, 8, 64])

        with ExitStack() as e1:
            browA = sb(e1, "browA", [NS, D_SHIFT])
            S.dma("sync", browA[:], browA_d.partition_broadcast(NS), writes=[browA])
            omka_b = sb(e1, "omka_b", [NS, 512])
            vts(omka_b[:], browB[:, BRB_KA:BRB_KA + 512], -1.0, 1.0, ALU.mult, ALU.add, reads=[browB], writes=[omka_b])
            sq_s = sb(e1, "sq_s", [NS, D])
            ss_s = sb(e1, "ss_s", [NS, 1])
            t1_s = sb(e1, "t1_s", [NS, 1])
            rstd_s = sb(e1, "rstd_s", [NS, 1])
            xn_s = sb(e1, "xn_s", [NS, D], BF16)
            act(sq_s[:], x_s[:], AF.Square, reads=[x_s], writes=[sq_s])
            vred(ss_s[:], sq_s[:], reads=[sq_s], writes=[ss_s])
            rsqrt_small(rstd_s[:], ss_s[:], t1_s[:], 1.0 / D, NORM_EPS, reads=[ss_s], writes=[t1_s, rstd_s])
            vts(xn_s[:], x_s[:], rstd_s[:, 0:1], None, ALU.mult, None, reads=[x_s, rstd_s], writes=[xn_s])
            memset(hTs[:, :, 0:DB], 0.0, writes=[hTs])
            for dc in range(8):
                tr(pT[:, dc * 128:dc * 128 + NS], xn_s[:, dc * 128:(dc + 1) * 128], identb[0:NS, 0:NS], reads=[xn_s, identb], writes=[pT])
            vcopy(hTs[:, :, DB:DB + NS], pT[:].rearrange("p (c t) -> p c t", t=128)[:, :, 0:NS], reads=[pT], writes=[hTs])

            p_s = sb(e1, "p_s", [NS, D_SHIFT])
            prev_s = sb(e1, "prev_s", [NS, D_SHIFT])
            col_chunks = [(0, 512), (512, 512), (1024, 512), (1536, 128)]
            kk_ = 0
            for (c0, n) in col_chunks:
                bank = pg[kk_ % 2]; kk_ += 1
                for dc in range(8):
                    mm(bank[0:NS, 0:n], hTs[:, dc, DB:DB + NS], winb[:, dc, c0:c0 + n], dc == 0, dc == 7, reads=[hTs, winb], writes=[bank])
                act(p_s[:, c0:c0 + n], bank[0:NS, 0:n], AF.Copy, reads=[bank], writes=[p_s])
                bank = pg[kk_ % 2]; kk_ += 1
                for dc in range(8):
                    mm(bank[0:NS, 0:n], hTs[:, dc, 0:NS], winb[:, dc, c0:c0 + n], dc == 0, dc == 7, reads=[hTs, winb], writes=[bank])
                vcopy(prev_s[:, c0:c0 + n], bank[0:NS, 0:n], reads=[bank], writes=[prev_s])
            for (c0, dst, fn) in [(1664, graw_s, AF.Silu), (2176, u_s, AF.Copy), (2688, gp_s, AF.Silu)]:
                bank = pg[kk_ % 2]; kk_ += 1
                for dc in range(8):
                    mm(bank[0:NS, :], hTs[:, dc, DB:DB + NS], winb[:, dc, c0:c0 + 512], dc == 0, dc == 7, reads=[hTs, winb], writes=[bank])
                act(dst[:], bank[0:NS, :], fn, reads=[bank], writes=[dst])
            S.dma("sync", prev_s[0:DB, :], sshift, writes=[prev_s])
            S.dma("sync", nss[:], p_s[NS - DB:NS, :], reads=[p_s], writes=[nss])
            S.dma("sync", nps[:, 0:11, :], spool.rearrange("(b j) c -> b j c", j=15)[:, 4:15, :], writes=[nps])
            for t in range(DT):
                S.dma("sync", nps[:, 11 + t, :], u_s[t * DB:(t + 1) * DB, :], reads=[u_s], writes=[nps])

            vtt(prev_s[:], prev_s[:], p_s[:], ALU.subtract, reads=[prev_s, p_s], writes=[prev_s])
            vtt(prev_s[:], prev_s[:], browA[:], ALU.mult, reads=[prev_s, browA], writes=[prev_s])
            vtt(prev_s[:], prev_s[:], p_s[:], ALU.add, reads=[prev_s, p_s], writes=[prev_s])
            ps_s = prev_s
            r_s = ps_s[:, 0:512]
            k_s = ps_s[:, 512:1024]
            v_s = ps_s[:, 1024:1536]

            lT = sb(e1, "lT", [128, NS])
            tr(pM[:, 0:NS], ps_s[:, 1536:1664], ident[0:NS, 0:NS], reads=[ps_s, cst], writes=[pM])
            act(lT[0:64, :], pM[0:64, 0:NS], AF.Tanh, reads=[pM], writes=[lT])
            act(lT[64:128, :], pM[64:128, 0:NS], AF.Copy, reads=[pM], writes=[lT])
            sg_s = sb(e1, "sg_s", [NS, 512])
            a_s = sb(e1, "a_s", [NS, 512])
            mm(pA[0:NS, :], lT[0:64, :], wd[:, :], True, True, reads=[lT, wd], writes=[pA])
            vtt(sg_s[:], pA[0:NS, :], browB[:, BRB_W0:BRB_W0 + 512], ALU.add, reads=[pA, browB], writes=[sg_s])
            act(sg_s[:], sg_s[:], AF.Sigmoid, reads=[sg_s], writes=[sg_s])
            mm(pB[0:NS, :], lT[64:128, :], wa[64:128, :], True, True, reads=[lT, wa], writes=[pB])
            vtt(a_s[:], pB[0:NS, :], browB[:, BRB_A0:BRB_A0 + 512], ALU.add, reads=[pB, browB], writes=[a_s])
            act(a_s[:], a_s[:], AF.Sigmoid, reads=[a_s], writes=[a_s])

            pk = sb(e1, "pk", [NS, 4, 512])
            PQ = {1: 0, 2: 1, 4: 2, 5: 3}
            tmpA = sb(e1, "tmpA", [NS, 512])
            tmpB = sb(e1, "tmpB", [NS, 512])
            act(pk[:, PQ[1], :], sg_s[:], AF.Exp, reads=[sg_s], writes=[pk], scale=-C0)
            vtt(tmpA[:], k_s, browB[:, BRB_KK:BRB_KK + 512], ALU.mult, reads=[ps_s, browB], writes=[tmpA])
            vtt(tmpB[:], tmpA[:], tmpA[:], ALU.mult, reads=[tmpA], writes=[tmpB])
            vred(st8[:], v3(tmpB[:]), reads=[tmpB], writes=[st8])
            rsqrt_small(st8b[:], st8[:], st8c[:], 1.0, L2_EPS, reads=[st8], writes=[st8c, st8b])
            vtt(v3(tmpA[:]), v3(tmpA[:]), bc8(st8b[:]), ALU.mult, reads=[tmpA, st8b], writes=[tmpA])
            vts(pk[:, PQ[4], :], tmpA[:], -1.0, None, ALU.mult, None, reads=[tmpA], writes=[pk])
            vtt(pk[:, PQ[5], :], tmpA[:], a_s[:], ALU.mult, reads=[tmpA, a_s], writes=[pk])
            vtt(tmpB[:], a_s[:], browB[:, BRB_KA:BRB_KA + 512], ALU.mult, reads=[a_s, browB], writes=[tmpB])
            vtt(tmpB[:], tmpB[:], omka_b[:], ALU.add, reads=[tmpB, omka_b], writes=[tmpB])
            vtt(pk[:, PQ[2], :], k_s, tmpB[:], ALU.mult, reads=[ps_s, tmpB], writes=[pk])
            vtt(tmpB[:], r_s, browB[:, BRB_RK:BRB_RK + 512], ALU.mult, reads=[ps_s, browB], writes=[tmpB])
            vtt(tmpB[:], tmpB[:], pk[:, PQ[2], :], ALU.mult, reads=[tmpB, pk], writes=[tmpB])
            vred(st8[:], v3(tmpB[:]), reads=[tmpB], writes=[st8])
            vtt(v3(bonus_s[:]), v3(v_s), bc8(st8[:]), ALU.mult, reads=[ps_s, st8], writes=[bonus_s])
            sview = scr1[:].rearrange("q t b h k -> q (t b) (h k)")
            S.dma("sync", sview[0], r_s, reads=[ps_s], writes=[scr1])
            S.dma("sync", sview[3], v_s, reads=[ps_s], writes=[scr1])
            for qq, slot in PQ.items():
                S.dma("sync", sview[qq], pk[:, slot, :], reads=[pk], writes=[scr1])
            S.finish([scr1], engname="sync")
            S.barrier()
            chk("S1")

        with ExitStack() as e2:
            sIn = sb(e2, "sIn", [128, 6, DT, 64])
            S.dma("sync", sIn[:], scr1[:].rearrange("q t b h k -> (b h) q t k"), reads=[scr1], writes=[sIn])
            St = sb(e2, "St", [128, 64, 64])
            S.dma("sync", St[:].rearrange("p v k -> p (v k)"), swkv, writes=[St])
            tmpS = sb(e2, "tmpS", [128, 64, 64])
            sa = sb(e2, "sa", [128, 64])
            yS = sb(e2, "yS", [128, DT, 64])

            def bv(ap):
                return ap.unsqueeze(1).to_broadcast([128, 64, 64])

            def bk(ap):
                return ap.unsqueeze(2).to_broadcast([128, 64, 64])

            for t in range(DT):
                q = lambda i: sIn[:, i, t, :]
                vtt(tmpS[:], St[:], bv(q(4)), ALU.mult, reads=[St, sIn], writes=[tmpS])
                vred(sa[:], tmpS[:], reads=[tmpS], writes=[sa])
                vtt(St[:], St[:], bv(q(1)), ALU.mult, reads=[St, sIn], writes=[St])
                vtt(tmpS[:], bk(sa[:]), bv(q(5)), ALU.mult, reads=[sa, sIn], writes=[tmpS])
                vtt(St[:], St[:], tmpS[:], ALU.add, reads=[St, tmpS], writes=[St])
                vtt(tmpS[:], bk(q(3)), bv(q(2)), ALU.mult, reads=[sIn], writes=[tmpS])
                vtt(St[:], St[:], tmpS[:], ALU.add, reads=[St, tmpS], writes=[St])
                vtt(tmpS[:], St[:], bv(q(0)), ALU.mult, reads=[St, sIn], writes=[tmpS])
                vred(yS[:, t, :], tmpS[:], reads=[tmpS], writes=[yS])
            S.dma("sync", nws[:], St[:].rearrange("p v k -> p (v k)"), reads=[St], writes=[nws])
            S.dma("sync", scr2[:].rearrange("b h t v -> (b h) t v"), yS[:], reads=[yS], writes=[scr2])
            S.finish([scr2, nws], engname="sync")
            S.barrier()
            chk("S2")

        with ExitStack() as e3:
            yT = sb(e3, "yT", [NS, 512])
            tmpA = sb(e3, "tmpA3", [NS, 512])
            for t in range(DT):
                S.dma("sync", yT[t * DB:(t + 1) * DB, :].rearrange("b (h v) -> b h v", v=64), scr2[:][:, :, t, :], reads=[scr2], writes=[yT])
            vred(st8[:], v3(yT[:]), reads=[yT], writes=[st8])
            vts(st8[:], st8[:], 1.0 / 64, None, ALU.mult, None, reads=[st8], writes=[st8])
            vtt(v3(yT[:]), v3(yT[:]), bc8(st8[:]), ALU.subtract, reads=[yT, st8], writes=[yT])
            vtt(tmpA[:], yT[:], yT[:], ALU.mult, reads=[yT], writes=[tmpA])
            vred(st8[:], v3(tmpA[:]), reads=[tmpA], writes=[st8])
            rsqrt_small(st8b[:], st8[:], st8c[:], 1.0 / 64, GN_EPS, reads=[st8], writes=[st8c, st8b])
            vtt(v3(yT[:]), v3(yT[:]), bc8(st8b[:]), ALU.mult, reads=[yT, st8b], writes=[yT])
            vtt(yT[:], yT[:], browB[:, BRB_GW:BRB_GW + 512], ALU.mult, reads=[yT, browB], writes=[yT])
            vtt(yT[:], yT[:], browB[:, BRB_GB:BRB_GB + 512], ALU.add, reads=[yT, browB], writes=[yT])
            vtt(yT[:], yT[:], bonus_s[:], ALU.add, reads=[yT, bonus_s], writes=[yT])
            vtt(yT[:], yT[:], graw_s[:], ALU.mult, reads=[yT, graw_s], writes=[yT])
            oTs = sb(e3, "oTs", [128, 8, NS], BF16)
            for fb in range(4):
                tr(pA[:, fb * 64:fb * 64 + NS], yT[:, fb * 128:(fb + 1) * 128], ident[0:NS, 0:NS], reads=[yT, cst], writes=[pA])
            vcopy(oTs[:, 0:4, :], pA[:, 0:4 * NS].rearrange("p (f t) -> p f t", t=NS), reads=[pA], writes=[oTs])

            uext = sb(e3, "uext_s", [128, 4, DB, 19])
            sp0 = sb(e3, "sp0", [120, 512])
            sp1 = sb(e3, "sp1", [120, 512])
            S.dma("sync", sp0[:], spool[0:120, :], writes=[sp0])
            S.dma("sync", sp1[:], spool[120:240, :], writes=[sp1])
            for g in range(4):
                tr(pB[:, 0:120], sp0[:, g * 128:(g + 1) * 128], ident[0:120, 0:120], reads=[sp0, cst], writes=[pB])
                tr(pB[:, 128:248], sp1[:, g * 128:(g + 1) * 128], ident[0:120, 0:120], reads=[sp1, cst], writes=[pB])
                vcopy(uext[:, g, 0:8, 0:15], pB[:, 0:120].rearrange("p (b j) -> p b j", j=15), reads=[pB], writes=[uext])
                vcopy(uext[:, g, 8:16, 0:15], pB[:, 128:248].rearrange("p (b j) -> p b j", j=15), reads=[pB], writes=[uext])
                tr(pM[:, 0:NS], u_s[:, g * 128:(g + 1) * 128], ident[0:NS, 0:NS], reads=[u_s, cst], writes=[pM])
                vcopy(uext[:, g, :, 15:19], pM[:, 0:NS].rearrange("p (t b) -> p b t", b=DB), reads=[pM], writes=[uext])
            s2 = sb(e3, "s2_s", [128, 4, DB, 19])
            s4 = sb(e3, "s4_s", [128, 3, DB, 19])
            s8 = sb(e3, "s8_s", [128, 2, DB, 19])
            s16 = sb(e3, "s16_s", [128, 1, DB, 19])
            d_s = sb(e3, "d_s", [128, 4, DT, DB])
            vtt(s2[:, :, :, 1:19], uext[:, :, :, 1:19], uext[:, :, :, 0:18], ALU.add, reads=[uext], writes=[s2])
            vtt(s4[:, :, :, 3:19], s2[:, 1:4, :, 3:19], s2[:, 1:4, :, 1:17], ALU.add, reads=[s2], writes=[s4])
            vtt(s8[:, :, :, 7:19], s4[:, 1:3, :, 7:19], s4[:, 1:3, :, 3:15], ALU.add, reads=[s4], writes=[s8])
            vtt(s16[:, :, :, 15:19], s8[:, 1:2, :, 15:19], s8[:, 1:2, :, 7:11], ALU.add, reads=[s8], writes=[s16])
            tots = [(s2, 0), (s4, 1), (s8, 2), (s16, 3)]
            for g in range(4):
                tt, off = tots[g]
                vstt(d_s[:, g, :, :].rearrange("p t b -> p b t"), tt[:, g - off, :, 15:19], 1.0 / WINS[g], uext[:, g, :, 15:19],
                     ALU.mult, ALU.subtract, reads=[tt, uext], writes=[d_s])
            gpT = sb(e3, "gpT", [128, 4, NS])
            for g in range(4):
                tr(pM[:, 64 + g * 64:64 + g * 64 + NS], gp_s[:, g * 128:(g + 1) * 128], ident[0:NS, 0:NS], reads=[gp_s, cst], writes=[pM])
            vcopy(gpT[:], pM[:, 64:64 + 4 * NS].rearrange("p (g t) -> p g t", t=NS), reads=[pM], writes=[gpT])
            for g in range(4):
                mm(pA[:, g * 64:g * 64 + NS], pw[:, g, :], d_s[:, g, :, :].rearrange("p t b -> p (t b)"), True, True, reads=[pw, d_s], writes=[pA])
            for g in range(4):
                vstt(oTs[:, 4 + g, :], pA[:, g * 64:g * 64 + NS], pvec[:, PV_PS + g:PV_PS + g + 1], gpT[:, g, :], ALU.mult, ALU.mult,
                     reads=[pA, pvec, gpT], writes=[oTs])

            sq = sb(e3, "sq2_s", [NS, D]); ssum = sb(e3, "ssum_s", [NS, 1])
            tmp1 = sb(e3, "tmp1_s", [NS, 1]); rstd = sb(e3, "rstd2_s", [NS, 1]); yo = sb(e3, "yo_s", [NS, D])
            oT_list_T = oTs
            final_tile((x_s, sq, ssum, tmp1, rstd, yo), NS, x_s, [oTs[:, fc, :] for fc in range(8)], ys[:], ys)
            S.finish([ys, nss, nps], engname="sync")
            S.barrier()
            chk("S3")

    with ExitStack() as es:
        def sbl(name, shape, dt=F32, n=2):
            return [sb(es, f"{name}_{i}", shape, dt) for i in range(n)]

        xt = sbl("xt", [128, D])
        sqx = sb(es, "sqx", [128, D], BF16)
        yo = sb(es, "yo", [128, D])
        ssx = sb(es, "ssx", [128, 1]); t1x = sb(es, "t1x", [128, 1]); rsx = sb(es, "rsx", [128, 1])
        xnb = sb(es, "xnb", [128, D], BF16)
        hT = sb(es, "hT", [128, 8, TB], BF16)
        praw = sbl("praw", [128, 4, TB + 1], n=1) * 2
        qsc = sbl("qsc", [128, 4, TB], n=1) * 2
        halo = sb(es, "halo", [128, 13])
        omu = sb(es, "omu2", [128, 13])
        psr = sb(es, "psr", [128, 4, TB]); psk = sb(es, "psk", [128, 4, TB]); psv = sb(es, "psv", [128, 4, TB])
        ps12 = sb(es, "ps12", [128, TB])
        psx = [T(g_[:, i, :], f"psx{gi_}_{i}", buf=g_.b) for gi_, g_ in enumerate([psr, psk, psv]) for i in range(4)] + [ps12]
        sg = sb(es, "sg", [128, 4, TB]); av = sb(es, "av", [128, 4, TB])
        gsil = sbl("gsil", [128, 4, TB], BF16)
        gpsil = sb(es, "gpsil", [128, 4, TB], BF16)
        uext = sb(es, "uext", [128, 4, 15 + TB])
        th = sb(es, "th", [64, TB])
        wbig = [sb(es, f"wbig{i}", [128, 4, TB]) for i in range(4)]
        w1, w2, w3, w4 = wbig
        srot = [T(wbig[i][:].rearrange("p f t -> p (f t)")[:, 0:15 + TB], f"srot{i}", buf=wbig[i].b) for i in range(4)]
        kkn = sb(es, "kkn", [128, 4, TB]); kmod = sb(es, "kmod", [128, 4, TB]); bv_ = sb(es, "bv_", [128, 4, TB])
        cum = sb(es, "cum", [128, 4, TB])
        dpl = T(kkn[:, 0, :], "dpl", buf=kkn.b)
        at = sbl("at", [64, 4, 2, TB], BF16)
        rt = sbl("rt", [64, 4, 2, TB], BF16)
        bt = sb(es, "bt", [64, 4, 2, TB], BF16)
        kt = sb(es, "kt", [64, 4, 2, TB], BF16)
        bh = sb(es, "bh", [128, 4, TB], BF16); kh = sb(es, "kh", [128, 4, TB], BF16); vb = sb(es, "vb", [128, 4, TB], BF16)
        bon = sbl("bon", [128, 4, TB])
        gC = sbl("gC", [64, 4, 2, NCH])
        VT = [[sb(es, f"VT{p}{c}", [64, 512], BF16) for c in range(NCH)] for p in range(2)]
        BKT = [[sb(es, f"BKT{p}{c}", [64, 1024], BF16) for c in range(NCH)] for p in range(2)]
        Aak = [[sb(es, f"Aak{p}{c}", [64, 512], BF16) for c in range(NCH)] for p in range(2)]
        Arb = [[sb(es, f"Arb{p}{c}", [64, 512], BF16) for c in range(NCH)] for p in range(2)]
        Ark = [[sb(es, f"Ark{p}{c}", [64, 512], BF16) for c in range(NCH)] for p in range(2)]
        Minv = [[sb(es, f"Minv{p}{c}", [64, 512], BF16) for c in range(NCH)] for p in range(2)]
        Nsb = [sb(es, f"Nsb{c}", [64, 512], BF16) for c in range(NCH)]
        NTsb = [sb(es, f"NTsb{c}", [64, 512], BF16) for c in range(NCH)]
        Xa0 = [sb(es, f"Xa0{c}", [64, 512], BF16) for c in range(NCH)]
        XTa0 = [sb(es, f"XTa0{c}", [64, 512], BF16) for c in range(NCH)]
        Qtmp = [sb(es, f"Qtmp{c}", [64, 512], BF16) for c in range(NCH)]
        ST = sb(es, "ST", [64, 8, 64]); STb = sb(es, "STb", [64, 8, 64], BF16)
        Wsb = sb(es, "Wsb", [64, 512], BF16); Usb = sb(es, "Usb", [64, 512], BF16)
        yc = sb(es, "yc", [64, 512]); ysq = sb(es, "ysq", [64, 512])
        STt = T(ysq[:].rearrange("p (h v) -> p h v", v=64), "STt", buf=ysq.b)
        m8 = sb(es, "m8", [64, 8]); v8 = sb(es, "v8", [64, 8]); r8 = sb(es, "r8", [64, 8]); t8 = sb(es, "t8", [64, 8])
        o1 = sb(es, "o1", [128, 4, 64])
        oT = sbl("oT", [128, 8, TB], BF16)
        ssum = sb(es, "ssum", [128, 1]); tmp1 = sb(es, "tmp1", [128, 1]); rstd = sb(es, "rstd", [128, 1])
        ppT = T(ysq[0:16, :], "ppT", buf=ysq.b); m13 = sb(es, "m13", [13, 128])
        SvT = T(yc[:].rearrange("p (h k) -> p h k", k=64), "SvT", buf=yc.b)

        memset(halo[:], 0.0, writes=[halo])
        memset(uext[:, :, 0:15], 0.0, writes=[uext])
        memset(ST[:], 0.0, writes=[ST])
        memset(STb[:], 0.0, writes=[STb])
        vts(omu[:], pvec[:, PV_MU:PV_MU + 13], -1.0, 1.0, ALU.mult, ALU.add, reads=[pvec], writes=[omu])

        def b8(ap):
            return ap.unsqueeze(1).to_broadcast([64, 8, 64])

        def h3(ap):
            return ap.rearrange("p (h v) -> p h v", v=64)

        def hc(h):
            return slice(h * 64, (h + 1) * 64)

        maskUs = b8(cst[0:64, C_MUS:C_MUS + 64])
        maskUi = b8(cst[0:64, C_MUI:C_MUI + 64])
        maskLs = b8(cst[0:64, C_MLS:C_MLS + 64])
        ident8 = b8(cst[0:64, C_ID:C_ID + 64])
        rstm = cst[:, C_RST:C_RST + 512]
        st = dict(gk=0, ak=0)
        pT32 = T(pT[:].bitcast(F32), "pT32", buf=pT.b)
        abanks = [pA, pB, pg[0], pg[1], pM, pT32]

        def nextbank():
            b = abanks[st["ak"] % len(abanks)]
            st["ak"] += 1
            return b

        def front(tb):
            pb = tb % 2
            t0 = tb * TB
            x_t = xt[pb]
            S.dma("sync", x_t[:], xp[t0:t0 + TB, :], writes=[x_t])
            act(sqx[:], x_t[:], AF.Square, reads=[x_t], writes=[sqx])
            vred(ssx[:], sqx[:], reads=[sqx], writes=[ssx])
            rsqrt_small(rsx[:], ssx[:], t1x[:], 1.0 / D, NORM_EPS, reads=[ssx], writes=[t1x, rsx])
            act(xnb[:], x_t[:], AF.Copy, reads=[x_t, rsx], writes=[xnb], scale=rsx[:, 0:1])
            yield
            for dc in range(8):
                tr(pT[:, dc * 128:(dc + 1) * 128], xnb[:, dc * 128:(dc + 1) * 128], identb[:], reads=[xnb, identb], writes=[pT])
            vcopy(hT[:].rearrange("p c t -> p (c t)"), pT[:], reads=[pT], writes=[hT])
            yield

            def gemm_group(ebs):
                bank = pg[st["gk"] % 2]
                st["gk"] += 1
                for i, eb in enumerate(ebs):
                    for dc in range(8):
                        mm(bank[:, i * TB:(i + 1) * TB], winb[:, dc, eb * 128:(eb + 1) * 128], hT[:, dc, :], dc == 0, dc == 7,
                           reads=[winb, hT], writes=[bank])
                return bank

            for gi, ebs in enumerate([[0, 1, 2, 3], [4, 5, 6, 7], [8, 9, 10, 11], [12]]):
                bank = gemm_group(ebs)
                yield
                n = len(ebs)
                pr = praw[gi % 2]
                qs = qsc[gi % 2]
                e0 = ebs[0]
                vcopy(pr[:, 0:n, 0:1], halo[:, e0:e0 + n].unsqueeze(2), reads=[halo], writes=[pr], eng="gpsimd")
                act(pr[:, 0:n, 1:TB + 1], bank[:, 0:n * TB].rearrange("p (e t) -> p e t", t=TB), AF.Copy, reads=[bank], writes=[pr])
                for i, eb in enumerate(ebs):
                    act(qs[:, i, :], bank[:, i * TB:(i + 1) * TB], AF.Copy, reads=[bank, omu], writes=[qs], scale=omu[:, eb:eb + 1])
                for i, eb in enumerate(ebs):
                    vstt(psx[eb][:], pr[:, i, 0:TB], pvec[:, PV_MU + eb:PV_MU + eb + 1], qs[:, i, :], ALU.mult, ALU.add,
                         reads=[pr, pvec, qs], writes=[psx[eb]])
                vcopy(halo[:, e0:e0 + n].unsqueeze(2), pr[:, 0:n, TB:TB + 1], reads=[pr], writes=[halo], eng="gpsimd")
                yield
            bank = gemm_group([13, 14, 15, 16])
            act(gsil[pb][:].rearrange("p f t -> p (f t)"), bank[:, :], AF.Silu, reads=[bank], writes=[gsil[pb]])
            yield
            bank = gemm_group([17, 18, 19, 20])
            act(uext[:, :, 15:15 + TB], bank[:, :].rearrange("p (g t) -> p g t", t=TB), AF.Copy, reads=[bank], writes=[uext])
            yield
            bank = gemm_group([21, 22, 23, 24])
            act(gpsil[:].rearrange("p g t -> p (g t)"), bank[:, :], AF.Silu, reads=[bank], writes=[gpsil])
            yield

            act(th[:], psx[12][0:64, :], AF.Tanh, reads=[psx[12]], writes=[th])
            for fb in range(4):
                mm(pA[:, fb * TB:(fb + 1) * TB], wd[:, fb * 128:(fb + 1) * 128], th[:], True, True, reads=[wd, th], writes=[pA])
            for fb in range(4):
                mm(pM[:, fb * TB:(fb + 1) * TB], wa[64:128, fb * 128:(fb + 1) * 128], psx[12][64:128, :], True, True, reads=[wa, psx[12]], writes=[pM])
            for fb in range(4):
                act(sg[:, fb, :], pA[:, fb * TB:(fb + 1) * TB], AF.Sigmoid, reads=[pA, pvec], writes=[sg], bias=pvec[:, PV_W0 + fb:PV_W0 + fb + 1])
                act(av[:, fb, :], pM[:, fb * TB:(fb + 1) * TB], AF.Sigmoid, reads=[pM, pvec], writes=[av], bias=pvec[:, PV_A0 + fb:PV_A0 + fb + 1])
            yield

            def pb4(col):
                return pvec[:, col:col + 4].unsqueeze(2).to_broadcast([128, 4, TB])

            def f2(t_):
                return t_[:].rearrange("p f t -> p (f t)")

            vcopy(vb[:], psv[:], reads=[psv], writes=[vb], eng="gpsimd")
            vtt(w1[:], psk[:], pb4(PV_KK), ALU.mult, reads=[psk, pvec], writes=[w1])
            vtt(w2[:], w1[:], w1[:], ALU.mult, reads=[w1], writes=[w2], eng="gpsimd")
            mm(pM[:, :], onesblk, f2(w2), True, True, reads=[cst, w2], writes=[pM])
            vts(f2(w2), pM[:, :], L2_EPS, None, ALU.add, None, reads=[pM], writes=[w2])
            act(w2[:], w2[:], AF.Ln, reads=[w2], writes=[w2])
            act(w2[:], w2[:], AF.Exp, reads=[w2], writes=[w2], scale=-0.5)
            vstt(kkn[:], w1[:], -1.0, w2[:], ALU.mult, ALU.mult, reads=[w1, w2], writes=[kkn])
            yield
            vtt(w1[:], av[:], pb4(PV_KA), ALU.mult, reads=[av, pvec], writes=[w1], eng="gpsimd")
            vtt(w1[:], w1[:], omka[:, 0:4].unsqueeze(2).to_broadcast([128, 4, TB]), ALU.add, reads=[w1, omka], writes=[w1], eng="gpsimd")
            vtt(kmod[:], psk[:], w1[:], ALU.mult, reads=[psk, w1], writes=[kmod])
            vstt(bv_[:], kkn[:], -1.0, av[:], ALU.mult, ALU.mult, reads=[kkn, av], writes=[bv_])
            vtt(w1[:], psr[:], pb4(PV_RK), ALU.mult, reads=[psr, pvec], writes=[w1], eng="gpsimd")
            vtt(w1[:], w1[:], kmod[:], ALU.mult, reads=[w1, kmod], writes=[w1])
            mm(pA[:, :], onesblk, f2(w1), True, True, reads=[cst, w1], writes=[pA])
            vtt(f2(bon[pb]), pA[:, :], f2(psv), ALU.mult, reads=[pA, psv], writes=[bon[pb]])
            yield
            S.op("vector", lambda e: e.tensor_tensor_scan(out=f2(cum), data0=rstm, data1=f2(sg), initial=0.0, op0=ALU.mult, op1=ALU.add),
                 reads=[cst, sg], writes=[cum], cost=1.2)
            c3 = cum[:].rearrange("p f (c t) -> p (f c) t", t=CH)
            vtt(w1[:], cum[:], sg[:], ALU.subtract, reads=[cum, sg], writes=[w1], eng="gpsimd")
            act(w2[:], cum[:], AF.Exp, reads=[cum], writes=[w2], scale=-C0)
            act(w3[:], cum[:], AF.Exp, reads=[cum], writes=[w3], scale=C0)
            act(w1[:], w1[:], AF.Exp, reads=[w1], writes=[w1], scale=-C0)
            vtt(w4[:].rearrange("p f (c t) -> p (f c) t", t=CH), c3[:, :, CH - 1:CH].to_broadcast([128, 4 * NCH, CH]), c3, ALU.subtract,
                reads=[cum], writes=[w4], eng="gpsimd")
            act(w4[:], w4[:], AF.Exp, reads=[w4], writes=[w4], scale=-C0)
            yield
            for j in range(2):
                pp = slice(64 * j, 64 * j + 64)
                e_ = "vector" if j == 0 else "gpsimd"
                vtt(rt[pb][:, :, j, :], psr[pp, :, :], w2[pp, :, :], ALU.mult, reads=[psr, w2], writes=[rt[pb]], eng=e_)
                vtt(bt[:, :, j, :], bv_[pp, :, :], w3[pp, :, :], ALU.mult, reads=[bv_, w3], writes=[bt], eng=e_)
                vtt(kt[:, :, j, :], kmod[pp, :, :], w3[pp, :, :], ALU.mult, reads=[kmod, w3], writes=[kt], eng=e_)
                vtt(at[pb][:, :, j, :], kkn[pp, :, :], w1[pp, :, :], ALU.mult, reads=[kkn, w1], writes=[at[pb]], eng=e_)
                act(gC[pb][:, :, j, :], cum[pp, :, :].rearrange("p f (c t) -> p f c t", t=CH)[:, :, :, CH - 1], AF.Exp,
                    reads=[cum], writes=[gC[pb]], scale=-C0)
            vtt(bh[:], bv_[:], w4[:], ALU.mult, reads=[bv_, w4], writes=[bh])
            vtt(kh[:], kmod[:], w4[:], ALU.mult, reads=[kmod, w4], writes=[kh], eng="gpsimd")
            yield

            L = 15 + TB
            for g in range(4):
                vtt(srot[0][:, 1:], uext[:, g, 1:], uext[:, g, 0:L - 1], ALU.add, reads=[uext], writes=[srot[0]], eng="gpsimd")
                tot = srot[0]
                if g >= 1:
                    vtt(srot[1][:, 3:], srot[0][:, 3:], srot[0][:, 1:L - 2], ALU.add, reads=[srot[0]], writes=[srot[1]], eng="gpsimd")
                    tot = srot[1]
                if g >= 2:
                    vtt(srot[2][:, 7:], srot[1][:, 7:], srot[1][:, 3:L - 4], ALU.add, reads=[srot[1]], writes=[srot[2]], eng="gpsimd")
                    tot = srot[2]
                if g >= 3:
                    vtt(srot[3][:, 15:], srot[2][:, 15:], srot[2][:, 7:L - 8], ALU.add, reads=[srot[2]], writes=[srot[3]], eng="gpsimd")
                    tot = srot[3]
                vstt(dpl[:], tot[:, 15:], 1.0 / WINS[g], uext[:, g, 15:], ALU.mult, ALU.subtract, reads=[tot, uext], writes=[dpl])
                if tb == 0:
                    vtt(dpl[:, 0:16], tot[:, 15:31], cst[:, C_ICNT + g * 16:C_ICNT + (g + 1) * 16], ALU.mult, reads=[tot, cst], writes=[dpl])
                    vtt(dpl[:, 0:16], dpl[:, 0:16], uext[:, g, 15:31], ALU.subtract, reads=[dpl, uext], writes=[dpl])
                mm(pM[:, 0:TB], pw[:, g, :], dpl[:], True, True, reads=[pw, dpl], writes=[pM])
                vstt(oT[pb][:, 4 + g, :], pM[:, 0:TB], pvec[:, PV_PS + g:PV_PS + g + 1], gpsil[:, g, :], ALU.mult, ALU.mult,
                     reads=[pM, pvec, gpsil], writes=[oT[pb]])
                yield
            if tb == NTB - 1:
                for g in range(4):
                    tr(pA[0:16, g * 128:(g + 1) * 128], uext[:, g, TB - 1:TB + 15], ident, reads=[uext, cst], writes=[pA])
                vcopy(ppT[:], pA[0:16, :], reads=[pA], writes=[ppT])
                S.dma("sync", npp[:], ppT[1:16, :], reads=[ppT], writes=[npp])
                tr(pB[0:13, 0:128], halo[:, 0:13], ident, reads=[halo, cst], writes=[pB])
                vcopy(m13[:], pB[0:13, 0:128], reads=[pB], writes=[m13])
                S.dma("sync", nsp[:], m13[:], reads=[m13], writes=[nsp])
            vcopy(uext[:, :, 0:15], uext[:, :, TB:TB + 15], reads=[uext], writes=[uext], eng="gpsimd")
            yield

            css = [slice(c * CH, (c + 1) * CH) for c in range(NCH)]
            for c in range(NCH):
                for qi, srcl in enumerate([bh, kh]):
                    for fb in range(4):
                        tr(pT[0:64, qi * 512 + fb * 128:qi * 512 + (fb + 1) * 128], srcl[:, fb, css[c]], identb[:], reads=[srcl, identb], writes=[pT])
                vcopy(BKT[pb][c][:], pT[0:64, :], reads=[pT], writes=[BKT[pb][c]])
                for fb in range(4):
                    tr(pT[0:64, fb * 128:(fb + 1) * 128], vb[:, fb, css[c]], identb[:], reads=[vb, identb], writes=[pT])
                act(VT[pb][c][:], pT[0:64, 0:512], AF.Copy, reads=[pT], writes=[VT[pb][c]])
                yield

            def hsl(tl, h, c):
                fb, j = divmod(h, 2)
                return tl[:, fb, j, css[c]]

            for (Lt, Rt, mask, dsts) in [(bt, at[pb], maskUs, Nsb), (at[pb], bt, maskLs, NTsb), (kt, at[pb], maskUs, Aak[pb]),
                                         (bt, rt[pb], maskUi, Arb[pb]), (kt, rt[pb], maskUi, Ark[pb])]:
                banks = []
                for c in range(NCH):
                    bank = nextbank()
                    banks.append(bank)
                    for h in range(8):
                        mm(bank[0:64, hc(h)], hsl(Lt, h, c), hsl(Rt, h, c), True, True, reads=[Lt, Rt], writes=[bank])
                for c in range(NCH):
                    vtt(h3(dsts[c][:]), h3(banks[c][0:64, :]), mask, ALU.mult, reads=[banks[c], cst], writes=[dsts[c]])
                yield
            X = list(Nsb); XT = list(NTsb)
            Q = [Qtmp[c] for c in range(NCH)]
            for c in range(NCH):
                vtt(h3(Q[c][:]), h3(Nsb[c][:]), ident8, ALU.add, reads=[Nsb[c], cst], writes=[Q[c]])
            for lvl in range(5):
                Xn = [(Xa0[c] if lvl % 2 == 0 else Nsb[c]) for c in range(NCH)]
                XTn = [(XTa0[c] if lvl % 2 == 0 else NTsb[c]) for c in range(NCH)]
                Qn = [(Minv[pb][c] if lvl % 2 == 0 else Qtmp[c]) for c in range(NCH)]
                banks = []
                for c in range(NCH):
                    bank = nextbank(); banks.append(bank)
                    for h in range(8):
                        mm(bank[0:64, hc(h)], X[c][:, hc(h)], XT[c][:, hc(h)], True, True, reads=[X[c], XT[c]], writes=[bank])
                for c in range(NCH):
                    act(XTn[c][:], banks[c][0:64, :], AF.Copy, reads=[banks[c]], writes=[XTn[c]])
                yield
                if lvl < 4:
                    banks = []
                    for c in range(NCH):
                        bank = nextbank(); banks.append(bank)
                        for h in range(8):
                            mm(bank[0:64, hc(h)], XT[c][:, hc(h)], X[c][:, hc(h)], True, True, reads=[X[c], XT[c]], writes=[bank])
                    for c in range(NCH):
                        act(Xn[c][:], banks[c][0:64, :], AF.Copy, reads=[banks[c]], writes=[Xn[c]])
                    yield
                banks = []
                for c in range(NCH):
                    bank = nextbank(); banks.append(bank)
                    for h in range(8):
                        mm(bank[0:64, hc(h)], XTn[c][:, hc(h)], Q[c][:, hc(h)], True, True, reads=[XTn[c], Q[c]], writes=[bank])
                for c in range(NCH):
                    vtt(Qn[c][:], banks[c][0:64, :], Q[c][:], ALU.add, reads=[banks[c], Q[c]], writes=[Qn[c]])
                X, XT, Q = Xn, XTn, Qn
                yield

        def chain(tb):
            pb = tb % 2
            t0 = tb * TB
            for c in range(NCH):
                cs = slice(c * CH, (c + 1) * CH)
                aT, rT = at[pb], rt[pb]
                VTc, BKTc, Aakc, Arbc, Arkc, Minvc = VT[pb][c], BKT[pb][c], Aak[pb][c], Arb[pb][c], Ark[pb][c], Minv[pb][c]
                for h in range(8):
                    fb, j = divmod(h, 2)
                    mm(pC[0:64, hc(h)], aT[:, fb, j, cs], STb[:, h, :], True, False, reads=[aT, STb], writes=[pC])
                    mm(pC[0:64, hc(h)], Aakc[:, hc(h)], VTc[:, hc(h)], False, True, reads=[Aakc, VTc], writes=[pC])
                act(Wsb[:], pC[0:64, :], AF.Copy, reads=[pC], writes=[Wsb])
                yield
                for h in range(8):
                    mm(pC[0:64, hc(h)], Minvc[:, hc(h)], Wsb[:, hc(h)], True, True, reads=[Minvc, Wsb], writes=[pC])
                act(Usb[:], pC[0:64, :], AF.Copy, reads=[pC], writes=[Usb])
                yield
                for h in range(8):
                    mm(pC[0:64, hc(h)], BKTc[:, hc(h)], Usb[:, hc(h)], True, False, reads=[BKTc, Usb], writes=[pC])
                    mm(pC[0:64, hc(h)], BKTc[:, 512 + h * 64:512 + (h + 1) * 64], VTc[:, hc(h)], False, True, reads=[BKTc, VTc], writes=[pC])
                for h in range(8):
                    fb, j = divmod(h, 2)
                    mm(pD[0:64, hc(h)], rT[:, fb, j, cs], STb[:, h, :], True, False, reads=[rT, STb], writes=[pD])
                    mm(pD[0:64, hc(h)], Arbc[:, hc(h)], Usb[:, hc(h)], False, False, reads=[Arbc, Usb], writes=[pD])
                    mm(pD[0:64, hc(h)], Arkc[:, hc(h)], VTc[:, hc(h)], False, True, reads=[Arkc, VTc], writes=[pD])
                vtt(STt[:], ST[:], gC[pb][:].rearrange("p f j c -> p (f j) c")[:, :, c:c + 1].to_broadcast([64, 8, 64]), ALU.mult,
                    reads=[ST, gC[pb]], writes=[STt])
                vtt(ST[:], STt[:], h3(pC[0:64, :]), ALU.add, reads=[STt, pC], writes=[ST])
                act(STb[:], ST[:], AF.Copy, reads=[ST], writes=[STb])
                yield
                y3 = h3(pD[0:64, :])
                vred(m8[:], y3, reads=[pD], writes=[m8])
                vts(m8[:], m8[:], 1.0 / 64, None, ALU.mult, None, reads=[m8], writes=[m8])
                vtt(h3(yc[:]), y3, m8[:].unsqueeze(2).to_broadcast([64, 8, 64]), ALU.subtract, reads=[pD, m8], writes=[yc])
                act(ysq[:], yc[:], AF.Square, reads=[yc], writes=[ysq])
                vred(v8[:], h3(ysq[:]), reads=[ysq], writes=[v8])
                rsqrt_small(r8[:], v8[:], t8[:], 1.0 / 64, GN_EPS, reads=[v8], writes=[t8, r8])
                vtt(h3(yc[:]), h3(yc[:]), r8[:].unsqueeze(2).to_broadcast([64, 8, 64]), ALU.mult, reads=[yc, r8], writes=[yc], eng="gpsimd")
                yield
                for fb in range(4):
                    tr(pD[:, fb * 64:(fb + 1) * 64], yc[:, fb * 128:(fb + 1) * 128], ident[0:64, 0:64], reads=[yc, cst], writes=[pD])
                for fb in range(4):
                    vts(o1[:, fb, :], pD[:, fb * 64:(fb + 1) * 64], pvec[:, PV_GW + fb:PV_GW + fb + 1], pvec[:, PV_GB + fb:PV_GB + fb + 1],
                        ALU.mult, ALU.add, reads=[pD, pvec], writes=[o1])
                vtt(o1[:], o1[:], bon[pb][:, :, cs], ALU.add, reads=[o1, bon[pb]], writes=[o1], eng="gpsimd")
                vtt(oT[pb][:, 0:4, cs], o1[:], gsil[pb][:, :, cs], ALU.mult, reads=[o1, gsil[pb]], writes=[oT[pb]])
                yield
            x_t = xt[pb]
            for half in range(2):
                bank = pD if half == 0 else pC
                for fc in range(8):
                    mm(bank[:, :], oT[pb][:, fc, :], woutb[:, fc, half * 512:(half + 1) * 512], fc == 0, fc == 7, reads=[oT[pb], woutb], writes=[bank])
                vtt(x_t[:, half * 512:(half + 1) * 512], bank[:, :], x_t[:, half * 512:(half + 1) * 512], ALU.add, reads=[bank, x_t], writes=[x_t])
                yield
            act(yo[:], x_t[:], AF.Square, reads=[x_t], writes=[yo])
            vred(ssum[:], yo[:], reads=[yo], writes=[ssum])
            rsqrt_small(rstd[:], ssum[:], tmp1[:], 1.0 / D, NORM_EPS, reads=[ssum], writes=[tmp1, rstd])
            vstt(yo[:], x_t[:], rstd[:, 0:1], normf[:], ALU.mult, ALU.mult, reads=[x_t, rstd, normf], writes=[yo])
            S.dma("sync", yp[t0:t0 + TB, :], yo[:], reads=[yo], writes=[yp])
            yield

        def run_all(g):
            n = 0
            for _ in g:
                n += 1
            return n

        def interleave(ga, na, gb, nb):
            ia = ib = 0
            da = db = False
            while not (da and db):
                pick_a = (not da) and (db or (ia * nb <= ib * na))
                if pick_a:
                    try:
                        next(ga); ia += 1
                    except StopIteration:
                        da = True
                else:
                    try:
                        next(gb); ib += 1
                    except StopIteration:
                        db = True
            return ia, ib

        run_all(front(0))

        def record_units(g):
            units = []
            S.rec = []
            for _ in g:
                if S.rec:
                    units.append(S.rec)
                S.rec = []
            if S.rec:
                units.append(S.rec)
            S.rec = None
            return units

        A, B = [], []
        for tb in range(NTB):
            A.append(record_units(chain(tb)))
            if tb + 1 < NTB:
                B.append(record_units(front(tb + 1)))
        S.merge_emit(A, B, a_ok=lambda ia, ib: ib >= ia, b_ok=lambda ib, ia: ia >= ib)
        for h in range(8):
            tr(pA[0:64, h * 64:(h + 1) * 64], ST[:, h, :], ident[0:64, 0:64], reads=[ST, cst], writes=[pA])
        vcopy(SvT[:].rearrange("p h k -> p (h k)"), pA[0:64, :], reads=[pA], writes=[SvT])
        S.dma("sync", nwp[:].rearrange("h v k -> v h k"), SvT[:], reads=[SvT], writes=[nwp])
        S.finish([yp, ys, nsp, nwp, npp, nss, nws, nps], engname="sync")
        S.barrier()
    es_top.close()
    return nc, S


_CACHE = {}


def _consts():
    cst = np.zeros((128, C_END), np.float32)
    cst[:, C_ID:C_ID + 128] = np.eye(128, dtype=np.float32)
    ob = np.zeros((128, 128), np.float32)
    ob[0:64, 0:64] = 1.0
    ob[64:128, 64:128] = 1.0
    cst[:, C_ONES:C_ONES + 128] = ob
    s = np.arange(64)[:, None]
    t = np.arange(64)[None, :]
    mus = (s < t).astype(np.float32)
    mui = (s <= t).astype(np.float32)
    mls = (s > t).astype(np.float32)
    i64 = np.eye(64, dtype=np.float32)
    cst[0:64, C_MUS:C_MUS + 64] = mus
    cst[0:64, C_MUI:C_MUI + 64] = mui
    cst[0:64, C_MLS:C_MLS + 64] = mls
    rst = np.ones((512,), np.float32)
    rst[::CH] = 0.0
    cst[:, C_RST:C_RST + 512] = rst[None, :]
    for g, w in enumerate(WINS):
        pos = np.arange(16)
        cst[:, C_ICNT + g * 16:C_ICNT + (g + 1) * 16] = (1.0 / np.minimum(pos + 1, w)).astype(np.float32)[None, :]
    return cst


def kernel(x_prompt, x_sample, state_shift, state_wkv, state_pool, norm_w, w_in, mu_shift,
           w_decay_b, w0, w_aaa_b, a0, k_k, k_a, r_k, gn_w, gn_b, pool_w, pool_scale, w_out, norm_f):
    f = lambda a: np.ascontiguousarray(np.asarray(a, dtype=np.float32))
    x_prompt, x_sample, state_shift, state_wkv, state_pool = map(f, (x_prompt, x_sample, state_shift, state_wkv, state_pool))
    if "nc" not in _CACHE:
        _CACHE["nc"] = build_program()
    nc, S = _CACHE["nc"]

    def colmajor(v, n):
        return f(v).reshape(n, 128).T

    pvec = np.concatenate([
        colmajor(norm_w[0], 8), colmajor(mu_shift[0], 13), colmajor(w0[0], 4), colmajor(a0[0], 4), colmajor(k_k[0], 4),
        colmajor(k_a[0], 4), colmajor(f(r_k[0]).reshape(-1), 4), colmajor(gn_w[0], 4), colmajor(gn_b[0], 4), colmajor(pool_scale[0], 4)], axis=1)
    pvec = f(pvec)
    browA = f(f(mu_shift[0])[None, :])
    browB = f(np.concatenate([f(w0[0]), f(a0[0]), f(k_k[0]), f(k_a[0]), f(r_k[0]).reshape(-1), f(gn_w[0]), f(gn_b[0])])[None, :])
    cst = _consts()
    shared = {
        "w_in": f(w_in[0]), "w_out": f(w_out[0]), "wdec": f(w_decay_b[0]), "waaa": f(w_aaa_b[0]), "poolw": f(pool_w[0]),
        "pvec": pvec, "browA": browA, "browB": browB, "normf": f(norm_f)[None, :], "cst": cst,
    }
    in_maps = []
    for c in range(NCORE):
        bs = slice(c * DB, (c + 1) * DB)
        m = dict(shared)
        m["xp"] = x_prompt[c]
        m["xs"] = f(x_sample[bs].transpose(1, 0, 2).reshape(NS, D))
        m["sshift"] = state_shift[0, bs]
        m["swkv"] = f(state_wkv[0, bs].reshape(128, 4096))
        m["spool"] = f(state_pool[0, bs].reshape(DB * 15, 512))
        in_maps.append(m)
    res = run_bass_kernel_spmd(nc, in_maps, core_ids=list(range(NCORE)))
    R = res.results
    y_prompt = np.stack([R[c]["yp"] for c in range(NCORE)], axis=0)
    y_sample = np.concatenate([R[c]["ys"].reshape(DT, DB, D).transpose(1, 0, 2) for c in range(NCORE)], axis=0)
    nsp = np.stack([R[c]["nsp"].reshape(D_SHIFT) for c in range(NCORE)], axis=0)[None]
    nwp = np.stack([R[c]["nwp"] for c in range(NCORE)], axis=0)[None]
    npp = np.stack([R[c]["npp"] for c in range(NCORE)], axis=0)[None]
    nss = np.concatenate([R[c]["nss"] for c in range(NCORE)], axis=0)[None]
    nws = np.concatenate([R[c]["nws"].reshape(DB, 8, 64, 64) for c in range(NCORE)], axis=0)[None]
    nps = np.concatenate([R[c]["nps"] for c in range(NCORE)], axis=0)[None]
    out = (y_prompt, y_sample, nsp, nwp, npp, nss, nws, nps)
    return tuple(np.ascontiguousarray(o.astype(np.float32)) for o in out)
```

```python
import numpy as np
from contextlib import ExitStack
import concourse.bass as bass
import concourse.mybir as mybir
from concourse.bass_utils import run_bass_kernel_spmd

F32 = mybir.dt.float32
BF16 = mybir.dt.bfloat16
AF = mybir.ActivationFunctionType
ALU = mybir.AluOpType
AX = mybir.AxisListType

D = 1024
SEQ = 2048
NCORE = 8
DB = 16
DT = 4
NS = DB * DT
D_SHIFT = 1664
D_IN = 3200
C0 = float(np.exp(-0.5))
NORM_EPS = 1e-6
GN_EPS = 64e-5
L2_EPS = 1e-12
TB = 128
NTB = SEQ // TB
CH = 64
NCH = TB // CH
WINS = (2, 4, 8, 16)

C_ID, C_ONES, C_MUS, C_MUI, C_MLS, C_RST, C_ICNT, C_END = 0, 128, 256, 320, 384, 448, 960, 1024
PV_NW, PV_MU, PV_W0, PV_A0, PV_KK, PV_KA, PV_RK, PV_GW, PV_GB, PV_PS, PV_END = 0, 8, 21, 25, 29, 33, 37, 41, 45, 49, 53
BRB_W0, BRB_A0, BRB_KK, BRB_KA, BRB_RK, BRB_GW, BRB_GB = 0, 512, 1024, 1536, 2048, 2560, 3072


class Buf:
    __slots__ = ("name", "w", "r")

    def __init__(self, name):
        self.name = name
        self.w = None
        self.r = []


class T:
    def __init__(self, t, name, buf=None):
        self.t = t
        self.b = buf if buf is not None else Buf(name)

    def __getitem__(self, k):
        return self.t[k]


class Sched:
    def __init__(self, nc, n_dma_sems=32):
        self.nc = nc
        self.eng = {}
        for name in ["tensor", "vector", "scalar", "gpsimd", "sync"]:
            h = getattr(nc, name)
            sem = nc.alloc_semaphore(name="prog_" + name)
            self.eng[name] = dict(h=h, sem=sem, cnt=0, waited={})
        self.dma_sems = [dict(sem=nc.alloc_semaphore(name=f"dma{i}"), cnt=0) for i in range(n_dma_sems)]
        self.dma_rr = 0
        self.ninstr = 0
        self.rec = None

    def _wait(self, engname, tok):
        sem, val, src = tok
        e = self.eng[engname]
        key = id(sem)
        if e["waited"].get(key, 0) >= val:
            return
        e["h"].wait_ge(sem, val)
        e["waited"][key] = val
        self.ninstr += 1

    def _deps(self, engname, reads, writes):
        toks = []
        for b in reads:
            if b.w is not None:
                toks.append(b.w)
        for b in writes:
            if b.w is not None:
                toks.append(b.w)
            toks.extend(b.r)
        for tok in toks:
            if tok[2] == engname and engname == "tensor":
                continue
            self._wait(engname, tok)

    @staticmethod
    def _bufs(xs):
        return [x.b if isinstance(x, T) else x for x in xs]

    def _record(self, tok, reads, writes):
        for b in reads:
            b.r.append(tok)
            if len(b.r) > 64:
                b.r = b.r[-64:] if False else b.r
        for b in writes:
            b.w = tok
            b.r = []

    def op(self, engname, fn, reads=(), writes=(), cost=0.3):
        reads = self._bufs(reads)
        writes = self._bufs(writes)
        if self.rec is not None:
            self.rec.append(("op", engname, fn, reads, writes, cost, None))
            return None
        e = self.eng[engname]
        self._deps(engname, reads, writes)
        ins = fn(e["h"])
        e["cnt"] += 1
        ins.then_inc(e["sem"], 1)
        e["waited"][id(e["sem"])] = max(e["waited"].get(id(e["sem"]), 0), 0)
        tok = (e["sem"], e["cnt"], engname)
        self._record(tok, reads, writes)
        self.ninstr += 1
        return tok

    def dma(self, qname, out, in_, reads=(), writes=(), **kw):
        reads = self._bufs(reads)
        writes = self._bufs(writes)
        if self.rec is not None:
            self.rec.append(("dma", qname, (out, in_), reads, writes, 2.5, kw))
            return None
        e = self.eng[qname]
        self._deps(qname, reads, writes)
        d = self.dma_sems[self.dma_rr]
        self.dma_rr = (self.dma_rr + 1) % len(self.dma_sems)
        if d["cnt"] > 0:
            self._wait(qname, (d["sem"], 16 * d["cnt"], "dma"))
        ins = e["h"].dma_start(out=out, in_=in_, **kw)
        d["cnt"] += 1
        ins.then_inc(d["sem"], 16)
        tok = (d["sem"], 16 * d["cnt"], "dma")
        self._record(tok, reads, writes)
        self.ninstr += 1
        return tok

    def emit(self, r):
        kind, eng, fn, reads, writes, cost, kw = r
        if kind == "op":
            self.op(eng, fn, reads=reads, writes=writes)
        else:
            self.dma(eng, fn[0], fn[1], reads=reads, writes=writes, **kw)

    def merge_emit(self, A, B, a_ok, b_ok):
        eng_free = {}
        ready = {}
        acc = {}

        def est(r):
            kind, eng, fn, reads, writes, cost, kw = r
            t = eng_free.get(eng, 0.0)
            for b in reads:
                rt_, re_ = ready.get(id(b), (0.0, eng))
                t = max(t, rt_ + (0.15 if re_ != eng else 0.0))
            for b in writes:
                rt_, re_ = ready.get(id(b), (0.0, eng))
                t = max(t, rt_ + (0.15 if re_ != eng else 0.0), acc.get(id(b), 0.0) + 0.1)
            return t

        def commit(r, t):
            kind, eng, fn, reads, writes, cost, kw = r
            if kind == "dma":
                eng_free[eng] = t + 0.1
                end = t + cost
            else:
                end = t + cost
                eng_free[eng] = end
            for b in reads:
                acc[id(b)] = max(acc.get(id(b), 0.0), end)
            for b in writes:
                ready[id(b)] = (end, eng)
                acc[id(b)] = max(acc.get(id(b), 0.0), end)

        def run_unit(u):
            for r in u:
                commit(r, est(r))
                self.emit(r)

        ia = ib = 0
        ja = jb = 0
        while ia < len(A) or ib < len(B):
            ca = None
            cb = None
            if ia < len(A) and (ja > 0 or a_ok(ia, ib)):
                ca = A[ia][ja]
            if ib < len(B) and (jb > 0 or b_ok(ib, ia)):
                cb = B[ib][jb]
            assert ca is not None or cb is not None, (ia, ib, ja, jb)
            ta = est(ca[0]) if ca is not None else None
            tb_ = est(cb[0]) if cb is not None else None
            if cb is None or (ca is not None and ta <= tb_):
                run_unit(ca); ja += 1
                if ja == len(A[ia]):
                    ia += 1; ja = 0
            else:
                run_unit(cb); jb += 1
                if jb == len(B[ib]):
                    ib += 1; jb = 0

    def barrier(self):
        toks = [(e["sem"], e["cnt"], n) for n, e in self.eng.items() if e["cnt"] > 0]
        toks += [(d["sem"], 16 * d["cnt"], "dma") for d in self.dma_sems if d["cnt"] > 0]
        for n in self.eng:
            for tok in toks:
                if tok[2] == n:
                    continue
                self._wait(n, tok)

    def finish(self, tiles, engname="sync"):
        for b in self._bufs(tiles):
            if b.w is not None:
                self._wait(engname, b.w)


class _Stop(Exception):
    pass


def build_program(stop=None):
    nc = bass.Bass("TRN2", target_bir_lowering=False)
    S = Sched(nc)
    try:
        _build_body(nc, S, stop)
    except _Stop:
        S.barrier()
    return nc, S


def _build_body(nc, S, stop):
    def chk(label):
        if stop == label:
            raise _Stop()


    def din(name, shape):
        return nc.dram_tensor(name, list(shape), F32, kind="ExternalInput").ap()

    def dout(name, shape):
        return T(nc.dram_tensor(name, list(shape), F32, kind="ExternalOutput").ap(), name)

    xp = din("xp", [SEQ, D])
    xs = din("xs", [NS, D])
    sshift = din("sshift", [DB, D_SHIFT])
    swkv = din("swkv", [128, 4096])
    spool = din("spool", [DB * 15, 512])
    w_in = din("w_in", [D, D_IN])
    w_out = din("w_out", [D, D])
    wdec = din("wdec", [64, 512])
    waaa = din("waaa", [64, 512])
    poolw = din("poolw", [4, 128, 128])
    pvec_d = din("pvec", [128, PV_END])
    browA_d = din("browA", [1, D_SHIFT])
    browB_d = din("browB", [1, 3584])
    normf_d = din("normf", [1, D])
    cst_d = din("cst", [128, C_END])

    yp = dout("yp", [SEQ, D])
    ys = dout("ys", [NS, D])
    nsp = dout("nsp", [13, 128])
    nwp = dout("nwp", [8, 64, 64])
    npp = dout("npp", [15, 512])
    nss = dout("nss", [DB, D_SHIFT])
    nws = dout("nws", [128, 4096])
    nps = dout("nps", [DB, 15, 512])
    scr1 = T(nc.dram_tensor("scr1", [6, DT, DB, 8, 64], F32, kind="Internal").ap(), "scr1")
    scr2 = T(nc.dram_tensor("scr2", [DB, 8, DT, 64], F32, kind="Internal").ap(), "scr2")

    es_top = ExitStack()

    def sb(es, name, shape, dt=F32):
        return T(es.enter_context(nc.sbuf_tensor("s_" + name, list(shape), dt)), name)

    def pst(name, shape, dt=F32):
        return T(nc.alloc_psum_tensor("p_" + name, list(shape), dt), name)

    def nel(ap):
        n = 1
        for s_ in ap.shape[1:]:
            n *= s_
        return n

    def mm(out, lhsT, rhs, start, stop, reads, writes):
        passes = 4 if lhsT.dtype == F32 else 1
        c_ = max(0.055, nel(rhs) * passes / 2000.0 + 0.03)
        S.op("tensor", lambda e: e.matmul(out, lhsT=lhsT, rhs=rhs, start=start, stop=stop), reads=reads, writes=writes, cost=c_)

    def tr(out, in_, ident, reads, writes):
        S.op("tensor", lambda e: e.transpose(out, in_, ident), reads=reads, writes=writes, cost=0.13)

    def act(out, in_, func, reads, writes, bias=None, scale=None, eng="scalar"):
        kw = {}
        if bias is not None:
            kw["bias"] = bias
        if scale is not None:
            kw["scale"] = scale
        S.op("scalar", lambda e: e.activation(out=out, in_=in_, func=func, **kw), reads=reads, writes=writes,
             cost=0.1 + 0.1 * len(kw) + nel(in_) * 0.00095)

    def ecost(eng, n):
        return 0.08 + n * (0.00105 if eng == "vector" else 0.0025)

    def vtt(out, in0, in1, op, reads, writes, eng="vector"):
        S.op(eng, lambda e: e.tensor_tensor(out=out, in0=in0, in1=in1, op=op), reads=reads, writes=writes, cost=ecost(eng, nel(out)))

    def vts(out, in0, s1, s2, op0, op1, reads, writes, eng="vector"):
        if op1 is None:
            S.op(eng, lambda e: e.tensor_scalar(out=out, in0=in0, scalar1=s1, scalar2=None, op0=op0), reads=reads, writes=writes,
                 cost=ecost(eng, nel(out)))
        else:
            S.op(eng, lambda e: e.tensor_scalar(out=out, in0=in0, scalar1=s1, scalar2=s2, op0=op0, op1=op1), reads=reads, writes=writes,
                 cost=ecost(eng, nel(out)))

    def vstt(out, in0, scalar, in1, op0, op1, reads, writes):
        S.op("vector", lambda e: e.scalar_tensor_tensor(out=out, in0=in0, scalar=scalar, in1=in1, op0=op0, op1=op1), reads=reads, writes=writes,
             cost=ecost("vector", nel(out)))

    def vcopy(out, in_, reads, writes, eng="vector"):
        S.op(eng, lambda e: e.tensor_copy(out=out, in_=in_), reads=reads, writes=writes, cost=ecost(eng, nel(out)))

    def vred(out, in_, reads, writes):
        S.op("vector", lambda e: e.tensor_reduce(out=out, in_=in_, axis=AX.X, op=ALU.add), reads=reads, writes=writes,
             cost=ecost("vector", nel(in_)))

    def vrecip(out, in_, reads, writes):
        S.op("vector", lambda e: e.reciprocal(out=out, in_=in_), reads=reads, writes=writes, cost=0.08 + nel(out) * 0.0084)

    def memset(ap, val, writes, eng="gpsimd"):
        S.op(eng, lambda e: e.memset(ap, val), writes=writes)

    def rsqrt_small(out, in_, tmp, scale, eps, reads, writes):
        act(tmp, in_, AF.Sqrt, reads=reads, writes=writes, bias=None, scale=None) if False else None
        vts(tmp, in_, scale, eps, ALU.mult, ALU.add, reads=reads, writes=writes)
        act(tmp, tmp, AF.Sqrt, reads=writes, writes=writes)
        vrecip(out, tmp, reads=writes, writes=writes)

    pg = [pst(f"pg{i}", [128, 512]) for i in range(2)]
    pT = pst("pT", [128, 1024], BF16)
    pM = pst("pM", [128, 512])
    pA = pst("pA", [128, 512])
    pB = pst("pB", [128, 512])
    pC = pst("pC", [128, 512])
    pD = pst("pD", [128, 512])

    cst = sb(es_top, "cst", [128, C_END])
    pvec = sb(es_top, "pvec", [128, PV_END])
    omu = sb(es_top, "omu", [128, 13])
    omka = sb(es_top, "omka", [128, 4])
    identb = sb(es_top, "identb", [128, 128], BF16)
    winb = sb(es_top, "winb", [128, 8, D_IN], BF16)
    woutb = sb(es_top, "woutb", [128, 8, D], BF16)
    wd = sb(es_top, "wd", [64, 512])
    wa = sb(es_top, "wa", [128, 512])
    pw = sb(es_top, "pw", [128, 4, 128])
    normf = sb(es_top, "normf", [128, D])

    ident = cst[:, C_ID:C_ID + 128]
    onesblk = cst[:, C_ONES:C_ONES + 128]

    S.dma("sync", cst[:], cst_d, writes=[cst])
    S.dma("sync", pvec[:], pvec_d, writes=[pvec])
    S.dma("sync", wd[:], wdec, writes=[wd])
    S.dma("sync", wa[64:128, :], waaa, writes=[wa])
    S.dma("sync", pw[:], poolw.rearrange("g c e -> c g e"), writes=[pw])
    S.dma("sync", normf[:], normf_d.partition_broadcast(128), writes=[normf])
    vcopy(identb[:], ident, reads=[cst], writes=[identb])
    vts(omka[:], pvec[:, PV_KA:PV_KA + 4], -1.0, 1.0, ALU.mult, ALU.add, reads=[pvec], writes=[omka])

    with ExitStack() as es:
        stg = [sb(es, f"stg{i}", [128, D_IN]) for i in range(3)]
        for dc in range(8):
            st = stg[dc % 3]
            S.dma("sync", st[:], w_in[dc * 128:(dc + 1) * 128, :], writes=[st])
            h = D_IN // 2
            vts(winb[:, dc, 0:h], st[:, 0:h], pvec[:, PV_NW + dc:PV_NW + dc + 1], None, ALU.mult, None, reads=[st, pvec], writes=[winb])
            act(winb[:, dc, h:], st[:, h:], AF.Copy, reads=[st, pvec], writes=[winb], scale=pvec[:, PV_NW + dc:PV_NW + dc + 1])
        S.barrier()
        chk("W")

    def final_tile(es_tiles, n, x_t, oT_list, out_dram_ap, out_T):
        res, sq, ssum, tmp1, rstd, yo = es_tiles
        for half in range(2):
            bank = pD if half == 0 else pC
            for fc in range(8):
                mm(bank[0:n, :], oT_list[fc], woutb[:, fc, half * 512:(half + 1) * 512], fc == 0, fc == 7,
                   reads=[oT_list_T, woutb], writes=[bank])
            vtt(res[0:n, half * 512:(half + 1) * 512], bank[0:n, :], x_t[0:n, half * 512:(half + 1) * 512], ALU.add,
                reads=[bank, x_t], writes=[res])
        act(sq[0:n, :], res[0:n, :], AF.Square, reads=[res], writes=[sq])
        vred(ssum[0:n, :], sq[0:n, :], reads=[sq], writes=[ssum])
        rsqrt_small(rstd[0:n, :], ssum[0:n, :], tmp1[0:n, :], 1.0 / D, NORM_EPS, reads=[ssum], writes=[tmp1, rstd])
        vstt(yo[0:n, :], res[0:n, :], rstd[0:n, 0:1], normf[0:n, :], ALU.mult, ALU.mult, reads=[res, rstd, normf], writes=[yo])
        S.dma("sync", out_dram_ap, yo[0:n, :], reads=[yo], writes=[out_T])

    oT_list_T = None

    with ExitStack() as es:
        browB = sb(es, "browB", [NS, 3584])
        S.dma("sync", browB[:], browB_d.partition_broadcast(NS), writes=[browB])
        x_s = sb(es, "x_s", [NS, D])
        S.dma("sync", x_s[:], xs, writes=[x_s])
        hTs = sb(es, "hTs", [128, 8, DB + NS], BF16)
        graw_s = sb(es, "graw_s", [NS, 512])
        u_s = sb(es, "u_s", [NS, 512])
        gp_s = sb(es, "gp_s", [NS, 512])
        bonus_s = sb(es, "bonus_s", [NS, 512])
        st8 = sb(es, "st8", [NS, 8])
        st8b = sb(es, "st8b", [NS, 8])
        st8c = sb(es, "st8c", [NS, 8])

        def v3(ap):
            return ap.rearrange("p (h k) -> p h k", k=64)

        def bc8(ap8):
            return ap8.unsqueeze(2).to_broadcast([NS, 8, 64])

        with ExitStack() as e1:
            browA = sb(e1, "browA", [NS, D_SHIFT])
            S.dma("sync", browA[:], browA_d.partition_broadcast(NS), writes=[browA])
            omka_b = sb(e1, "omka_b", [NS, 512])
            vts(omka_b[:], browB[:, BRB_KA:BRB_KA + 512], -1.0, 1.0, ALU.mult, ALU.add, reads=[browB], writes=[omka_b])
            sq_s = sb(e1, "sq_s", [NS, D])
            ss_s = sb(e1, "ss_s", [NS, 1])
            t1_s = sb(e1, "t1_s", [NS, 1])
            rstd_s = sb(e1, "rstd_s", [NS, 1])
            xn_s = sb(e1, "xn_s", [NS, D], BF16)
            act(sq_s[:], x_s[:], AF.Square, reads=[x_s], writes=[sq_s])
            vred(ss_s[:], sq_s[:], reads=[sq_s], writes=[ss_s])
            rsqrt_small(rstd_s[:], ss_s[:], t1_s[:], 1.0 / D, NORM_EPS, reads=[ss_s], writes=[t1_s, rstd_s])
            vts(xn_s[:], x_s[:], rstd_s[:, 0:1], None, ALU.mult, None, reads=[x_s, rstd_s], writes=[xn_s])
            memset(hTs[:, :, 0:DB], 0.0, writes=[hTs])
            for dc in range(8):
                tr(pT[:, dc * 128:dc * 128 + NS], xn_s[:, dc * 128:(dc + 1) * 128], identb[0:NS, 0:NS], reads=[xn_s, identb], writes=[pT])
            vcopy(hTs[:, :, DB:DB + NS], pT[:].rearrange("p (c t) -> p c t", t=128)[:, :, 0:NS], reads=[pT], writes=[hTs])

            p_s = sb(e1, "p_s", [NS, D_SHIFT])
            prev_s = sb(e1, "prev_s", [NS, D_SHIFT])
            col_chunks = [(0, 512), (512, 512), (1024, 512), (1536, 128)]
            kk_ = 0
            for (c0, n) in col_chunks:
                bank = pg[kk_ % 2]; kk_ += 1
                for dc in range(8):
                    mm(bank[0:NS, 0:n], hTs[:, dc, DB:DB + NS], winb[:, dc, c0:c0 + n], dc == 0, dc == 7, reads=[hTs, winb], writes=[bank])
                act(p_s[:, c0:c0 + n], bank[0:NS, 0:n], AF.Copy, reads=[bank], writes=[p_s])
                bank = pg[kk_ % 2]; kk_ += 1
                for dc in range(8):
                    mm(bank[0:NS, 0:n], hTs[:, dc, 0:NS], winb[:, dc, c0:c0 + n], dc == 0, dc == 7, reads=[hTs, winb], writes=[bank])
                vcopy(prev_s[:, c0:c0 + n], bank[0:NS, 0:n], reads=[bank], writes=[prev_s])
            for (c0, dst, fn) in [(1664, graw_s, AF.Silu), (2176, u_s, AF.Copy), (2688, gp_s, AF.Silu)]:
                bank = pg[kk_ % 2]; kk_ += 1
                for dc in range(8):
                    mm(bank[0:NS, :], hTs[:, dc, DB:DB + NS], winb[:, dc, c0:c0 + 512], dc == 0, dc == 7, reads=[hTs, winb], writes=[bank])
                act(dst[:], bank[0:NS, :], fn, reads=[bank], writes=[dst])
            S.dma("sync", prev_s[0:DB, :], sshift, writes=[prev_s])
            S.dma("sync", nss[:], p_s[NS - DB:NS, :], reads=[p_s], writes=[nss])
            S.dma("sync", nps[:, 0:11, :], spool.rearrange("(b j) c -> b j c", j=15)[:, 4:15, :], writes=[nps])
            for t in range(DT):
                S.dma("sync", nps[:, 11 + t, :], u_s[t * DB:(t + 1) * DB, :], reads=[u_s], writes=[nps])

            vtt(prev_s[:], prev_s[:], p_s[:], ALU.subtract, reads=[prev_s, p_s], writes=[prev_s])
            vtt(prev_s[:], prev_s[:], browA[:], ALU.mult, reads=[prev_s, browA], writes=[prev_s])
            vtt(prev_s[:], prev_s[:], p_s[:], ALU.add, reads=[prev_s, p_s], writes=[prev_s])
            ps_s = prev_s
            r_s = ps_s[:, 0:512]
            k_s = ps_s[:, 512:1024]
            v_s = ps_s[:, 1024:1536]

            lT = sb(e1, "lT", [128, NS])
            tr(pM[:, 0:NS], ps_s[:, 1536:1664], ident[0:NS, 0:NS], reads=[ps_s, cst], writes=[pM])
            act(lT[0:64, :], pM[0:64, 0:NS], AF.Tanh, reads=[pM], writes=[lT])
            act(lT[64:128, :], pM[64:128, 0:NS], AF.Copy, reads=[pM], writes=[lT])
            sg_s = sb(e1, "sg_s", [NS, 512])
            a_s = sb(e1, "a_s", [NS, 512])
            mm(pA[0:NS, :], lT[0:64, :], wd[:, :], True, True, reads=[lT, wd], writes=[pA])
            vtt(sg_s[:], pA[0:NS, :], browB[:, BRB_W0:BRB_W0 + 512], ALU.add, reads=[pA, browB], writes=[sg_s])
            act(sg_s[:], sg_s[:], AF.Sigmoid, reads=[sg_s], writes=[sg_s])
            mm(pB[0:NS, :], lT[64:128, :], wa[64:128, :], True, True, reads=[lT, wa], writes=[pB])
            vtt(a_s[:], pB[0:NS, :], browB[:, BRB_A0:BRB_A0 + 512], ALU.add, reads=[pB, browB], writes=[a_s])
            act(a_s[:], a_s[:], AF.Sigmoid, reads=[a_s], writes=[a_s])

            pk = sb(e1, "pk", [NS, 4, 512])
            PQ = {1: 0, 2: 1, 4: 2, 5: 3}
            tmpA = sb(e1, "tmpA", [NS, 512])
            tmpB = sb(e1, "tmpB", [NS, 512])
            act(pk[:, PQ[1], :], sg_s[:], AF.Exp, reads=[sg_s], writes=[pk], scale=-C0)
            vtt(tmpA[:], k_s, browB[:, BRB_KK:BRB_KK + 512], ALU.mult, reads=[ps_s, browB], writes=[tmpA])
            vtt(tmpB[:], tmpA[:], tmpA[:], ALU.mult, reads=[tmpA], writes=[tmpB])
            vred(st8[:], v3(tmpB[:]), reads=[tmpB], writes=[st8])
            rsqrt_small(st8b[:], st8[:], st8c[:], 1.0, L2_EPS, reads=[st8], writes=[st8c, st8b])
            vtt(v3(tmpA[:]), v3(tmpA[:]), bc8(st8b[:]), ALU.mult, reads=[tmpA, st8b], writes=[tmpA])
            vts(pk[:, PQ[4], :], tmpA[:], -1.0, None, ALU.mult, None, reads=[tmpA], writes=[pk])
            vtt(pk[:, PQ[5], :], tmpA[:], a_s[:], ALU.mult, reads=[tmpA, a_s], writes=[pk])
            vtt(tmpB[:], a_s[:], browB[:, BRB_KA:BRB_KA + 512], ALU.mult, reads=[a_s, browB], writes=[tmpB])
            vtt(tmpB[:], tmpB[:], omka_b[:], ALU.add, reads=[tmpB, omka_b], writes=[tmpB])
            vtt(pk[:, PQ[2], :], k_s, tmpB[:], ALU.mult, reads=[ps_s, tmpB], writes=[pk])
            vtt(tmpB[:], r_s, browB[:, BRB_RK:BRB_RK + 512], ALU.mult, reads=[ps_s, browB], writes=[tmpB])
            vtt(tmpB[:], tmpB[:], pk[:, PQ[2], :], ALU.mult, reads=[tmpB, pk], writes=[tmpB])
            vred(st8[:], v3(tmpB[:]), reads=[tmpB], writes=[st8])
            vtt(v3(bonus_s[:]), v3(v_s), bc8(st8[:]), ALU.mult, reads=[ps_s, st8], writes=[bonus_s])
            sview = scr1[:].rearrange("q t b h k -> q (t b) (h k)")
            S.dma("sync", sview[0], r_s, reads=[ps_s], writes=[scr1])
            S.dma("sync", sview[3], v_s, reads=[ps_s], writes=[scr1])
            for qq, slot in PQ.items():
                S.dma("sync", sview[qq], pk[:, slot, :], reads=[pk], writes=[scr1])
            S.finish([scr1], engname="sync")
            S.barrier()
            chk("S1")

        with ExitStack() as e2:
            sIn = sb(e2, "sIn", [128, 6, DT, 64])
            S.dma("sync", sIn[:], scr1[:].rearrange("q t b h k -> (b h) q t k"), reads=[scr1], writes=[sIn])
            St = sb(e2, "St", [128, 64, 64])
            S.dma("sync", St[:].rearrange("p v k -> p (v k)"), swkv, writes=[St])
            tmpS = sb(e2, "tmpS", [128, 64, 64])
            sa = sb(e2, "sa", [128, 64])
            yS = sb(e2, "yS", [128, DT, 64])
            stgo = [sb(e2, f"stgo{i}", [128, D]) for i in range(3)]
            for fc in range(8):
                so = stgo[fc % 3]
                S.dma("sync", so[:], w_out[fc * 128:(fc + 1) * 128, :], writes=[so])
                act(woutb[:, fc, :], so[:], AF.Copy, reads=[so], writes=[woutb])

            def bv(ap):
                return ap.unsqueeze(1).to_broadcast([128, 64, 64])

            def bk(ap):
                return ap.unsqueeze(2).to_broadcast([128, 64, 64])

            for t in range(DT):
                q = lambda i: sIn[:, i, t, :]
                vtt(tmpS[:], St[:], bv(q(4)), ALU.mult, reads=[St, sIn], writes=[tmpS])
                vred(sa[:], tmpS[:], reads=[tmpS], writes=[sa])
                vtt(St[:], St[:], bv(q(1)), ALU.mult, reads=[St, sIn], writes=[St])
                vtt(tmpS[:], bk(sa[:]), bv(q(5)), ALU.mult, reads=[sa, sIn], writes=[tmpS])
                vtt(St[:], St[:], tmpS[:], ALU.add, reads=[St, tmpS], writes=[St])
                vtt(tmpS[:], bk(q(3)), bv(q(2)), ALU.mult, reads=[sIn], writes=[tmpS])
                vtt(St[:], St[:], tmpS[:], ALU.add, reads=[St, tmpS], writes=[St])
                vtt(tmpS[:], St[:], bv(q(0)), ALU.mult, reads=[St, sIn], writes=[tmpS])
                vred(yS[:, t, :], tmpS[:], reads=[tmpS], writes=[yS])
            S.dma("sync", nws[:], St[:].rearrange("p v k -> p (v k)"), reads=[St], writes=[nws])
            S.dma("sync", scr2[:].rearrange("b h t v -> (b h) t v"), yS[:], reads=[yS], writes=[scr2])
            S.finish([scr2, nws], engname="sync")
            S.barrier()
            chk("S2")

        with ExitStack() as e3:
            yT = sb(e3, "yT", [NS, 512])
            tmpA = sb(e3, "tmpA3", [NS, 512])
            for t in range(DT):
                S.dma("sync", yT[t * DB:(t + 1) * DB, :].rearrange("b (h v) -> b h v", v=64), scr2[:][:, :, t, :], reads=[scr2], writes=[yT])
            vred(st8[:], v3(yT[:]), reads=[yT], writes=[st8])
            vts(st8[:], st8[:], 1.0 / 64, None, ALU.mult, None, reads=[st8], writes=[st8])
            vtt(v3(yT[:]), v3(yT[:]), bc8(st8[:]), ALU.subtract, reads=[yT, st8], writes=[yT])
            vtt(tmpA[:], yT[:], yT[:], ALU.mult, reads=[yT], writes=[tmpA])
            vred(st8[:], v3(tmpA[:]), reads=[tmpA], writes=[st8])
            rsqrt_small(st8b[:], st8[:], st8c[:], 1.0 / 64, GN_EPS, reads=[st8], writes=[st8c, st8b])
            vtt(v3(yT[:]), v3(yT[:]), bc8(st8b[:]), ALU.mult, reads=[yT, st8b], writes=[yT])
            vtt(yT[:], yT[:], browB[:, BRB_GW:BRB_GW + 512], ALU.mult, reads=[yT, browB], writes=[yT])
            vtt(yT[:], yT[:], browB[:, BRB_GB:BRB_GB + 512], ALU.add, reads=[yT, browB], writes=[yT])
            vtt(yT[:], yT[:], bonus_s[:], ALU.add, reads=[yT, bonus_s], writes=[yT])
            vtt(yT[:], yT[:], graw_s[:], ALU.mult, reads=[yT, graw_s], writes=[yT])
            oTs = sb(e3, "oTs", [128, 8, NS], BF16)
            for fb in range(4):
                tr(pA[:, fb * 64:fb * 64 + NS], yT[:, fb * 128:(fb + 1) * 128], ident[0:NS, 0:NS], reads=[yT, cst], writes=[pA])
            vcopy(oTs[:, 0:4, :], pA[:, 0:4 * NS].rearrange("p (f t) -> p f t", t=NS), reads=[pA], writes=[oTs])

            uext = sb(e3, "uext_s", [128, 4, DB, 19])
            sp0 = sb(e3, "sp0", [120, 512])
            sp1 = sb(e3, "sp1", [120, 512])
            S.dma("sync", sp0[:], spool[0:120, :], writes=[sp0])
            S.dma("sync", sp1[:], spool[120:240, :], writes=[sp1])
            for g in range(4):
                tr(pB[:, 0:120], sp0[:, g * 128:(g + 1) * 128], ident[0:120, 0:120], reads=[sp0, cst], writes=[pB])
                tr(pB[:, 128:248], sp1[:, g * 128:(g + 1) * 128], ident[0:120, 0:120], reads=[sp1, cst], writes=[pB])
                vcopy(uext[:, g, 0:8, 0:15], pB[:, 0:120].rearrange("p (b j) -> p b j", j=15), reads=[pB], writes=[uext])
                vcopy(uext[:, g, 8:16, 0:15], pB[:, 128:248].rearrange("p (b j) -> p b j", j=15), reads=[pB], writes=[uext])
                tr(pM[:, 0:NS], u_s[:, g * 128:(g + 1) * 128], ident[0:NS, 0:NS], reads=[u_s, cst], writes=[pM])
                vcopy(uext[:, g, :, 15:19], pM[:, 0:NS].rearrange("p (t b) -> p b t", b=DB), reads=[pM], writes=[uext])
            s2 = sb(e3, "s2_s", [128, 4, DB, 19])
            s4 = sb(e3, "s4_s", [128, 3, DB, 19])
            s8 = sb(e3, "s8_s", [128, 2, DB, 19])
            s16 = sb(e3, "s16_s", [128, 1, DB, 19])
            d_s = sb(e3, "d_s", [128, 4, DT, DB])
            vtt(s2[:, :, :, 1:19], uext[:, :, :, 1:19], uext[:, :, :, 0:18], ALU.add, reads=[uext], writes=[s2])
            vtt(s4[:, :, :, 3:19], s2[:, 1:4, :, 3:19], s2[:, 1:4, :, 1:17], ALU.add, reads=[s2], writes=[s4])
            vtt(s8[:, :, :, 7:19], s4[:, 1:3, :, 7:19], s4[:, 1:3, :, 3:15], ALU.add, reads=[s4], writes=[s8])
            vtt(s16[:, :, :, 15:19], s8[:, 1:2, :, 15:19], s8[:, 1:2, :, 7:11], ALU.add, reads=[s8], writes=[s16])
            tots = [(s2, 0), (s4, 1), (s8, 2), (s16, 3)]
            for g in range(4):
                tt, off = tots[g]
                vstt(d_s[:, g, :, :].rearrange("p t b -> p b t"), tt[:, g - off, :, 15:19], 1.0 / WINS[g], uext[:, g, :, 15:19],
                     ALU.mult, ALU.subtract, reads=[tt, uext], writes=[d_s])
            gpT = sb(e3, "gpT", [128, 4, NS])
            for g in range(4):
                tr(pM[:, 64 + g * 64:64 + g * 64 + NS], gp_s[:, g * 128:(g + 1) * 128], ident[0:NS, 0:NS], reads=[gp_s, cst], writes=[pM])
            vcopy(gpT[:], pM[:, 64:64 + 4 * NS].rearrange("p (g t) -> p g t", t=NS), reads=[pM], writes=[gpT])
            for g in range(4):
                mm(pA[:, g * 64:g * 64 + NS], pw[:, g, :], d_s[:, g, :, :].rearrange("p t b -> p (t b)"), True, True, reads=[pw, d_s], writes=[pA])
            for g in range(4):
                vstt(oTs[:, 4 + g, :], pA[:, g * 64:g * 64 + NS], pvec[:, PV_PS + g:PV_PS + g + 1], gpT[:, g, :], ALU.mult, ALU.mult,
                     reads=[pA, pvec, gpT], writes=[oTs])

            sq = sb(e3, "sq2_s", [NS, D]); ssum = sb(e3, "ssum_s", [NS, 1])
            tmp1 = sb(e3, "tmp1_s", [NS, 1]); rstd = sb(e3, "rstd2_s", [NS, 1]); yo = sb(e3, "yo_s", [NS, D])
            oT_list_T = oTs
            final_tile((x_s, sq, ssum, tmp1, rstd, yo), NS, x_s, [oTs[:, fc, :] for fc in range(8)], ys[:], ys)
            S.finish([ys, nss, nps], engname="sync")
            S.barrier()
            chk("S3")

    with ExitStack() as es:
        def sbl(name, shape, dt=F32, n=2):
            return [sb(es, f"{name}_{i}", shape, dt) for i in range(n)]

        xt = sbl("xt", [128, D])
        sqx = sb(es, "sqx", [128, D], BF16)
        yo = sb(es, "yo", [128, D])
        ssx = sb(es, "ssx", [128, 1]); t1x = sb(es, "t1x", [128, 1]); rsx = sb(es, "rsx", [128, 1])
        xnb = sb(es, "xnb", [128, D], BF16)
        hT = sb(es, "hT", [128, 8, TB], BF16)
        praw = sbl("praw", [128, 4, TB + 1], n=1) * 2
        qsc = sbl("qsc", [128, 4, TB], n=1) * 2
        halo = sb(es, "halo", [128, 13])
        omu = sb(es, "omu2", [128, 13])
        psr = sb(es, "psr", [128, 4, TB]); psk = sb(es, "psk", [128, 4, TB]); psv = sb(es, "psv", [128, 4, TB])
        ps12 = sb(es, "ps12", [128, TB])
        psx = [T(g_[:, i, :], f"psx{gi_}_{i}", buf=g_.b) for gi_, g_ in enumerate([psr, psk, psv]) for i in range(4)] + [ps12]
        sg = sb(es, "sg", [128, 4, TB]); av = sb(es, "av", [128, 4, TB])
        gsil = sbl("gsil", [128, 4, TB], BF16)
        gpsil = sb(es, "gpsil", [128, 4, TB], BF16)
        uext = sb(es, "uext", [128, 4, 15 + TB])
        th = sb(es, "th", [64, TB])
        wbig = [sb(es, f"wbig{i}", [128, 4, TB]) for i in range(4)]
        w1, w2, w3, w4 = wbig
        srot = [T(wbig[i][:].rearrange("p f t -> p (f t)")[:, 0:15 + TB], f"srot{i}", buf=wbig[i].b) for i in range(4)]
        kkn = sb(es, "kkn", [128, 4, TB]); kmod = sb(es, "kmod", [128, 4, TB]); bv_ = sb(es, "bv_", [128, 4, TB])
        cum = sb(es, "cum", [128, 4, TB])
        dpl = T(kkn[:, 0, :], "dpl", buf=kkn.b)
        at = sbl("at", [64, 4, 2, TB], BF16)
        rt = sbl("rt", [64, 4, 2, TB], BF16)
        bt = sb(es, "bt", [64, 4, 2, TB], BF16)
        kt = sb(es, "kt", [64, 4, 2, TB], BF16)
        bh = sb(es, "bh", [128, 4, TB], BF16); kh = sb(es, "kh", [128, 4, TB], BF16); vb = sb(es, "vb", [128, 4, TB], BF16)
        bon = sbl("bon", [128, 4, TB])
        gC = sbl("gC", [64, 4, 2, NCH])
        VT = [[sb(es, f"VT{p}{c}", [64, 512], BF16) for c in range(NCH)] for p in range(2)]
        BKT = [[sb(es, f"BKT{p}{c}", [64, 1024], BF16) for c in range(NCH)] for p in range(2)]
        Aak = [[sb(es, f"Aak{p}{c}", [64, 512], BF16) for c in range(NCH)] for p in range(2)]
        Arb = [[sb(es, f"Arb{p}{c}", [64, 512], BF16) for c in range(NCH)] for p in range(2)]
        Ark = [[sb(es, f"Ark{p}{c}", [64, 512], BF16) for c in range(NCH)] for p in range(2)]
        Minv = [[sb(es, f"Minv{p}{c}", [64, 512], BF16) for c in range(NCH)] for p in range(2)]
        Nsb = [sb(es, f"Nsb{c}", [64, 512], BF16) for c in range(NCH)]
        NTsb = [sb(es, f"NTsb{c}", [64, 512], BF16) for c in range(NCH)]
        Xa0 = [sb(es, f"Xa0{c}", [64, 512], BF16) for c in range(NCH)]
        XTa0 = [sb(es, f"XTa0{c}", [64, 512], BF16) for c in range(NCH)]
        Qtmp = [sb(es, f"Qtmp{c}", [64, 512], BF16) for c in range(NCH)]
        ST = sb(es, "ST", [64, 8, 64]); STb = sb(es, "STb", [64, 8, 64], BF16)
        Wsb = sb(es, "Wsb", [64, 512], BF16); Usb = sb(es, "Usb", [64, 512], BF16)
        yc = sb(es, "yc", [64, 512]); ysq = sb(es, "ysq", [64, 512])
        STt = T(ysq[:].rearrange("p (h v) -> p h v", v=64), "STt", buf=ysq.b)
        m8 = sb(es, "m8", [64, 8]); v8 = sb(es, "v8", [64, 8]); r8 = sb(es, "r8", [64, 8]); t8 = sb(es, "t8", [64, 8])
        o1 = sb(es, "o1", [128, 4, 64])
        oT = sbl("oT", [128, 8, TB], BF16)
        ssum = sb(es, "ssum", [128, 1]); tmp1 = sb(es, "tmp1", [128, 1]); rstd = sb(es, "rstd", [128, 1])
        ppT = T(ysq[0:16, :], "ppT", buf=ysq.b); m13 = sb(es, "m13", [13, 128])
        SvT = T(yc[:].rearrange("p (h k) -> p h k", k=64), "SvT", buf=yc.b)

        memset(halo[:], 0.0, writes=[halo])
        memset(uext[:, :, 0:15], 0.0, writes=[uext])
        memset(ST[:], 0.0, writes=[ST])
        memset(STb[:], 0.0, writes=[STb])
        vts(omu[:], pvec[:, PV_MU:PV_MU + 13], -1.0, 1.0, ALU.mult, ALU.add, reads=[pvec], writes=[omu])

        def b8(ap):
            return ap.unsqueeze(1).to_broadcast([64, 8, 64])

        def h3(ap):
            return ap.rearrange("p (h v) -> p h v", v=64)

        def hc(h):
            return slice(h * 64, (h + 1) * 64)

        maskUs = b8(cst[0:64, C_MUS:C_MUS + 64])
        maskUi = b8(cst[0:64, C_MUI:C_MUI + 64])
        maskLs = b8(cst[0:64, C_MLS:C_MLS + 64])
        ident8 = b8(cst[0:64, C_ID:C_ID + 64])
        rstm = cst[:, C_RST:C_RST + 512]
        st = dict(gk=0, ak=0)
        pT32 = T(pT[:].bitcast(F32), "pT32", buf=pT.b)
        abanks = [pA, pB, pg[0], pg[1], pM, pT32]

        def nextbank():
            b = abanks[st["ak"] % len(abanks)]
            st["ak"] += 1
            return b

        def front(tb):
            pb = tb % 2
            t0 = tb * TB
            x_t = xt[pb]
            S.dma("sync", x_t[:], xp[t0:t0 + TB, :], writes=[x_t])
            act(sqx[:], x_t[:], AF.Square, reads=[x_t], writes=[sqx])
            vred(ssx[:], sqx[:], reads=[sqx], writes=[ssx])
            rsqrt_small(rsx[:], ssx[:], t1x[:], 1.0 / D, NORM_EPS, reads=[ssx], writes=[t1x, rsx])
            act(xnb[:], x_t[:], AF.Copy, reads=[x_t, rsx], writes=[xnb], scale=rsx[:, 0:1])
            yield
            for dc in range(8):
                tr(pT[:, dc * 128:(dc + 1) * 128], xnb[:, dc * 128:(dc + 1) * 128], identb[:], reads=[xnb, identb], writes=[pT])
            vcopy(hT[:].rearrange("p c t -> p (c t)"), pT[:], reads=[pT], writes=[hT])
            yield

            def gemm_group(ebs):
                bank = pg[st["gk"] % 2]
                st["gk"] += 1
                for i, eb in enumerate(ebs):
                    for dc in range(8):
                        mm(bank[:, i * TB:(i + 1) * TB], winb[:, dc, eb * 128:(eb + 1) * 128], hT[:, dc, :], dc == 0, dc == 7,
                           reads=[winb, hT], writes=[bank])
                return bank

            for gi, ebs in enumerate([[0, 1, 2, 3], [4, 5, 6, 7], [8, 9, 10, 11], [12]]):
                bank = gemm_group(ebs)
                yield
                n = len(ebs)
                pr = praw[gi % 2]
                qs = qsc[gi % 2]
                e0 = ebs[0]
                vcopy(pr[:, 0:n, 0:1], halo[:, e0:e0 + n].unsqueeze(2), reads=[halo], writes=[pr], eng="gpsimd")
                act(pr[:, 0:n, 1:TB + 1], bank[:, 0:n * TB].rearrange("p (e t) -> p e t", t=TB), AF.Copy, reads=[bank], writes=[pr])
                for i, eb in enumerate(ebs):
                    act(qs[:, i, :], bank[:, i * TB:(i + 1) * TB], AF.Copy, reads=[bank, omu], writes=[qs], scale=omu[:, eb:eb + 1])
                for i, eb in enumerate(ebs):
                    vstt(psx[eb][:], pr[:, i, 0:TB], pvec[:, PV_MU + eb:PV_MU + eb + 1], qs[:, i, :], ALU.mult, ALU.add,
                         reads=[pr, pvec, qs], writes=[psx[eb]])
                vcopy(halo[:, e0:e0 + n].unsqueeze(2), pr[:, 0:n, TB:TB + 1], reads=[pr], writes=[halo], eng="gpsimd")
                yield
            bank = gemm_group([13, 14, 15, 16])
            act(gsil[pb][:].rearrange("p f t -> p (f t)"), bank[:, :], AF.Silu, reads=[bank], writes=[gsil[pb]])
            yield
            bank = gemm_group([17, 18, 19, 20])
            act(uext[:, :, 15:15 + TB], bank[:, :].rearrange("p (g t) -> p g t", t=TB), AF.Copy, reads=[bank], writes=[uext])
            yield
            bank = gemm_group([21, 22, 23, 24])
            act(gpsil[:].rearrange("p g t -> p (g t)"), bank[:, :], AF.Silu, reads=[bank], writes=[gpsil])
            yield

            act(th[:], psx[12][0:64, :], AF.Tanh, reads=[psx[12]], writes=[th])
            for fb in range(4):
                mm(pA[:, fb * TB:(fb + 1) * TB], wd[:, fb * 128:(fb + 1) * 128], th[:], True, True, reads=[wd, th], writes=[pA])
            for fb in range(4):
                mm(pM[:, fb * TB:(fb + 1) * TB], wa[64:128, fb * 128:(fb + 1) * 128], psx[12][64:128, :], True, True, reads=[wa, psx[12]], writes=[pM])
            for fb in range(4):
                act(sg[:, fb, :], pA[:, fb * TB:(fb + 1) * TB], AF.Sigmoid, reads=[pA, pvec], writes=[sg], bias=pvec[:, PV_W0 + fb:PV_W0 + fb + 1])
                act(av[:, fb, :], pM[:, fb * TB:(fb + 1) * TB], AF.Sigmoid, reads=[pM, pvec], writes=[av], bias=pvec[:, PV_A0 + fb:PV_A0 + fb + 1])
            yield

            def pb4(col):
                return pvec[:, col:col + 4].unsqueeze(2).to_broadcast([128, 4, TB])

            def f2(t_):
                return t_[:].rearrange("p f t -> p (f t)")

            vcopy(vb[:], psv[:], reads=[psv], writes=[vb], eng="gpsimd")
            vtt(w1[:], psk[:], pb4(PV_KK), ALU.mult, reads=[psk, pvec], writes=[w1])
            vtt(w2[:], w1[:], w1[:], ALU.mult, reads=[w1], writes=[w2], eng="gpsimd")
            mm(pM[:, :], onesblk, f2(w2), True, True, reads=[cst, w2], writes=[pM])
            vts(f2(w2), pM[:, :], L2_EPS, None, ALU.add, None, reads=[pM], writes=[w2])
            act(w2[:], w2[:], AF.Ln, reads=[w2], writes=[w2])
            act(w2[:], w2[:], AF.Exp, reads=[w2], writes=[w2], scale=-0.5)
            vstt(kkn[:], w1[:], -1.0, w2[:], ALU.mult, ALU.mult, reads=[w1, w2], writes=[kkn])
            yield
            vtt(w1[:], av[:], pb4(PV_KA), ALU.mult, reads=[av, pvec], writes=[w1], eng="gpsimd")
            vtt(w1[:], w1[:], omka[:, 0:4].unsqueeze(2).to_broadcast([128, 4, TB]), ALU.add, reads=[w1, omka], writes=[w1], eng="gpsimd")
            vtt(kmod[:], psk[:], w1[:], ALU.mult, reads=[psk, w1], writes=[kmod])
            vstt(bv_[:], kkn[:], -1.0, av[:], ALU.mult, ALU.mult, reads=[kkn, av], writes=[bv_])
            vtt(w1[:], psr[:], pb4(PV_RK), ALU.mult, reads=[psr, pvec], writes=[w1], eng="gpsimd")
            vtt(w1[:], w1[:], kmod[:], ALU.mult, reads=[w1, kmod], writes=[w1])
            mm(pA[:, :], onesblk, f2(w1), True, True, reads=[cst, w1], writes=[pA])
            vtt(f2(bon[pb]), pA[:, :], f2(psv), ALU.mult, reads=[pA, psv], writes=[bon[pb]])
            yield
            S.op("vector", lambda e: e.tensor_tensor_scan(out=f2(cum), data0=rstm, data1=f2(sg), initial=0.0, op0=ALU.mult, op1=ALU.add),
                 reads=[cst, sg], writes=[cum], cost=1.2)
            c3 = cum[:].rearrange("p f (c t) -> p (f c) t", t=CH)
            vtt(w1[:], cum[:], sg[:], ALU.subtract, reads=[cum, sg], writes=[w1], eng="gpsimd")
            act(w2[:], cum[:], AF.Exp, reads=[cum], writes=[w2], scale=-C0)
            act(w3[:], cum[:], AF.Exp, reads=[cum], writes=[w3], scale=C0)
            act(w1[:], w1[:], AF.Exp, reads=[w1], writes=[w1], scale=-C0)
            vtt(w4[:].rearrange("p f (c t) -> p (f c) t", t=CH), c3[:, :, CH - 1:CH].to_broadcast([128, 4 * NCH, CH]), c3, ALU.subtract,
                reads=[cum], writes=[w4], eng="gpsimd")
            act(w4[:], w4[:], AF.Exp, reads=[w4], writes=[w4], scale=-C0)
            yield
            for j in range(2):
                pp = slice(64 * j, 64 * j + 64)
                e_ = "vector" if j == 0 else "gpsimd"
                vtt(rt[pb][:, :, j, :], psr[pp, :, :], w2[pp, :, :], ALU.mult, reads=[psr, w2], writes=[rt[pb]], eng=e_)
                vtt(bt[:, :, j, :], bv_[pp, :, :], w3[pp, :, :], ALU.mult, reads=[bv_, w3], writes=[bt], eng=e_)
                vtt(kt[:, :, j, :], kmod[pp, :, :], w3[pp, :, :], ALU.mult, reads=[kmod, w3], writes=[kt], eng=e_)
                vtt(at[pb][:, :, j, :], kkn[pp, :, :], w1[pp, :, :], ALU.mult, reads=[kkn, w1], writes=[at[pb]], eng=e_)
                act(gC[pb][:, :, j, :], cum[pp, :, :].rearrange("p f (c t) -> p f c t", t=CH)[:, :, :, CH - 1], AF.Exp,
                    reads=[cum], writes=[gC[pb]], scale=-C0)
            vtt(bh[:], bv_[:], w4[:], ALU.mult, reads=[bv_, w4], writes=[bh])
            vtt(kh[:], kmod[:], w4[:], ALU.mult, reads=[kmod, w4], writes=[kh], eng="gpsimd")
            yield

            L = 15 + TB
            for g in range(4):
                vtt(srot[0][:, 1:], uext[:, g, 1:], uext[:, g, 0:L - 1], ALU.add, reads=[uext], writes=[srot[0]], eng="gpsimd")
                tot = srot[0]
                if g >= 1:
                    vtt(srot[1][:, 3:], srot[0][:, 3:], srot[0][:, 1:L - 2], ALU.add, reads=[srot[0]], writes=[srot[1]], eng="gpsimd")
                    tot = srot[1]
                if g >= 2:
                    vtt(srot[2][:, 7:], srot[1][:, 7:], srot[1][:, 3:L - 4], ALU.add, reads=[srot[1]], writes=[srot[2]], eng="gpsimd")
                    tot = srot[2]
                if g >= 3:
                    vtt(srot[3][:, 15:], srot[2][:, 15:], srot[2][:, 7:L - 8], ALU.add, reads=[srot[2]], writes=[srot[3]], eng="gpsimd")
                    tot = srot[3]
                vstt(dpl[:], tot[:, 15:], 1.0 / WINS[g], uext[:, g, 15:], ALU.mult, ALU.subtract, reads=[tot, uext], writes=[dpl])
                if tb == 0:
                    vtt(dpl[:, 0:16], tot[:, 15:31], cst[:, C_ICNT + g * 16:C_ICNT + (g + 1) * 16], ALU.mult, reads=[tot, cst], writes=[dpl])
                    vtt(dpl[:, 0:16], dpl[:, 0:16], uext[:, g, 15:31], ALU.subtract, reads=[dpl, uext], writes=[dpl])
                mm(pM[:, 0:TB], pw[:, g, :], dpl[:], True, True, reads=[pw, dpl], writes=[pM])
                vstt(oT[pb][:, 4 + g, :], pM[:, 0:TB], pvec[:, PV_PS + g:PV_PS + g + 1], gpsil[:, g, :], ALU.mult, ALU.mult,
                     reads=[pM, pvec, gpsil], writes=[oT[pb]])
                yield
            if tb == NTB - 1:
                for g in range(4):
                    tr(pA[0:16, g * 128:(g + 1) * 128], uext[:, g, TB - 1:TB + 15], ident, reads=[uext, cst], writes=[pA])
                vcopy(ppT[:], pA[0:16, :], reads=[pA], writes=[ppT])
                S.dma("sync", npp[:], ppT[1:16, :], reads=[ppT], writes=[npp])
                tr(pB[0:13, 0:128], halo[:, 0:13], ident, reads=[halo, cst], writes=[pB])
                vcopy(m13[:], pB[0:13, 0:128], reads=[pB], writes=[m13])
                S.dma("sync", nsp[:], m13[:], reads=[m13], writes=[nsp])
            vcopy(uext[:, :, 0:15], uext[:, :, TB:TB + 15], reads=[uext], writes=[uext], eng="gpsimd")
            yield

            css = [slice(c * CH, (c + 1) * CH) for c in range(NCH)]
            for c in range(NCH):
                for qi, srcl in enumerate([bh, kh]):
                    for fb in range(4):
                        tr(pT[0:64, qi * 512 + fb * 128:qi * 512 + (fb + 1) * 128], srcl[:, fb, css[c]], identb[:], reads=[srcl, identb], writes=[pT])
                vcopy(BKT[pb][c][:], pT[0:64, :], reads=[pT], writes=[BKT[pb][c]])
                for fb in range(4):
                    tr(pT[0:64, fb * 128:(fb + 1) * 128], vb[:, fb, css[c]], identb[:], reads=[vb, identb], writes=[pT])
                act(VT[pb][c][:], pT[0:64, 0:512], AF.Copy, reads=[pT], writes=[VT[pb][c]])
                yield

            def hsl(tl, h, c):
                fb, j = divmod(h, 2)
                return tl[:, fb, j, css[c]]

            for (Lt, Rt, mask, dsts) in [(bt, at[pb], maskUs, Nsb), (at[pb], bt, maskLs, NTsb), (kt, at[pb], maskUs, Aak[pb]),
                                         (bt, rt[pb], maskUi, Arb[pb]), (kt, rt[pb], maskUi, Ark[pb])]:
                banks = []
                for c in range(NCH):
                    bank = nextbank()
                    banks.append(bank)
                    for h in range(8):
                        mm(bank[0:64, hc(h)], hsl(Lt, h, c), hsl(Rt, h, c), True, True, reads=[Lt, Rt], writes=[bank])
                for c in range(NCH):
                    vtt(h3(dsts[c][:]), h3(banks[c][0:64, :]), mask, ALU.mult, reads=[banks[c], cst], writes=[dsts[c]])
                yield
            X = list(Nsb); XT = list(NTsb)
            Q = [Qtmp[c] for c in range(NCH)]
            for c in range(NCH):
                vtt(h3(Q[c][:]), h3(Nsb[c][:]), ident8, ALU.add, reads=[Nsb[c], cst], writes=[Q[c]])
            for lvl in range(5):
                Xn = [(Xa0[c] if lvl % 2 == 0 else Nsb[c]) for c in range(NCH)]
                XTn = [(XTa0[c] if lvl % 2 == 0 else NTsb[c]) for c in range(NCH)]
                Qn = [(Minv[pb][c] if lvl % 2 == 0 else Qtmp[c]) for c in range(NCH)]
                banks = []
                for c in range(NCH):
                    bank = nextbank(); banks.append(bank)
                    for h in range(8):
                        mm(bank[0:64, hc(h)], X[c][:, hc(h)], XT[c][:, hc(h)], True, True, reads=[X[c], XT[c]], writes=[bank])
                for c in range(NCH):
                    act(XTn[c][:], banks[c][0:64, :], AF.Copy, reads=[banks[c]], writes=[XTn[c]])
                yield
                if lvl < 4:
                    banks = []
                    for c in range(NCH):
                        bank = nextbank(); banks.append(bank)
                        for h in range(8):
                            mm(bank[0:64, hc(h)], XT[c][:, hc(h)], X[c][:, hc(h)], True, True, reads=[X[c], XT[c]], writes=[bank])
                    for c in range(NCH):
                        act(Xn[c][:], banks[c][0:64, :], AF.Copy, reads=[banks[c]], writes=[Xn[c]])
                    yield
                banks = []
                for c in range(NCH):
                    bank = nextbank(); banks.append(bank)
                    for h in range(8):
                        mm(bank[0:64, hc(h)], XTn[c][:, hc(h)], Q[c][:, hc(h)], True, True, reads=[XTn[c], Q[c]], writes=[bank])
                for c in range(NCH):
                    vtt(Qn[c][:], banks[c][0:64, :], Q[c][:], ALU.add, reads=[banks[c], Q[c]], writes=[Qn[c]])
                X, XT, Q = Xn, XTn, Qn
                yield

        def chain(tb):
            pb = tb % 2
            t0 = tb * TB
            for c in range(NCH):
                cs = slice(c * CH, (c + 1) * CH)
                aT, rT = at[pb], rt[pb]
                VTc, BKTc, Aakc, Arbc, Arkc, Minvc = VT[pb][c], BKT[pb][c], Aak[pb][c], Arb[pb][c], Ark[pb][c], Minv[pb][c]
                for h in range(8):
                    fb, j = divmod(h, 2)
                    mm(pC[0:64, hc(h)], aT[:, fb, j, cs], STb[:, h, :], True, False, reads=[aT, STb], writes=[pC])
                    mm(pC[0:64, hc(h)], Aakc[:, hc(h)], VTc[:, hc(h)], False, True, reads=[Aakc, VTc], writes=[pC])
                act(Wsb[:], pC[0:64, :], AF.Copy, reads=[pC], writes=[Wsb])
                yield
                for h in range(8):
                    mm(pC[0:64, hc(h)], Minvc[:, hc(h)], Wsb[:, hc(h)], True, True, reads=[Minvc, Wsb], writes=[pC])
                act(Usb[:], pC[0:64, :], AF.Copy, reads=[pC], writes=[Usb])
                yield
                for h in range(8):
                    mm(pC[0:64, hc(h)], BKTc[:, hc(h)], Usb[:, hc(h)], True, False, reads=[BKTc, Usb], writes=[pC])
                    mm(pC[0:64, hc(h)], BKTc[:, 512 + h * 64:512 + (h + 1) * 64], VTc[:, hc(h)], False, True, reads=[BKTc, VTc], writes=[pC])
                for h in range(8):
                    fb, j = divmod(h, 2)
                    mm(pD[0:64, hc(h)], rT[:, fb, j, cs], STb[:, h, :], True, False, reads=[rT, STb], writes=[pD])
                    mm(pD[0:64, hc(h)], Arbc[:, hc(h)], Usb[:, hc(h)], False, False, reads=[Arbc, Usb], writes=[pD])
                    mm(pD[0:64, hc(h)], Arkc[:, hc(h)], VTc[:, hc(h)], False, True, reads=[Arkc, VTc], writes=[pD])
                vtt(STt[:], ST[:], gC[pb][:].rearrange("p f j c -> p (f j) c")[:, :, c:c + 1].to_broadcast([64, 8, 64]), ALU.mult,
                    reads=[ST, gC[pb]], writes=[STt])
                vtt(ST[:], STt[:], h3(pC[0:64, :]), ALU.add, reads=[STt, pC], writes=[ST])
                act(STb[:], ST[:], AF.Copy, reads=[ST], writes=[STb])
                yield
                y3 = h3(pD[0:64, :])
                vred(m8[:], y3, reads=[pD], writes=[m8])
                vts(m8[:], m8[:], 1.0 / 64, None, ALU.mult, None, reads=[m8], writes=[m8])
                vtt(h3(yc[:]), y3, m8[:].unsqueeze(2).to_broadcast([64, 8, 64]), ALU.subtract, reads=[pD, m8], writes=[yc])
                act(ysq[:], yc[:], AF.Square, reads=[yc], writes=[ysq])
                vred(v8[:], h3(ysq[:]), reads=[ysq], writes=[v8])
                rsqrt_small(r8[:], v8[:], t8[:], 1.0 / 64, GN_EPS, reads=[v8], writes=[t8, r8])
                vtt(h3(yc[:]), h3(yc[:]), r8[:].unsqueeze(2).to_broadcast([64, 8, 64]), ALU.mult, reads=[yc, r8], writes=[yc], eng="gpsimd")
                yield
                for fb in range(4):
                    tr(pD[:, fb * 64:(fb + 1) * 64], yc[:, fb * 128:(fb + 1) * 128], ident[0:64, 0:64], reads=[yc, cst], writes=[pD])
                for fb in range(4):
                    vts(o1[:, fb, :], pD[:, fb * 64:(fb + 1) * 64], pvec[:, PV_GW + fb:PV_GW + fb + 1], pvec[:, PV_GB + fb:PV_GB + fb + 1],
                        ALU.mult, ALU.add, reads=[pD, pvec], writes=[o1])
                vtt(o1[:], o1[:], bon[pb][:, :, cs], ALU.add, reads=[o1, bon[pb]], writes=[o1], eng="gpsimd")
                vtt(oT[pb][:, 0:4, cs], o1[:], gsil[pb][:, :, cs], ALU.mult, reads=[o1, gsil[pb]], writes=[oT[pb]])
                yield
            x_t = xt[pb]
            for half in range(2):
                bank = pD if half == 0 else pC
                for fc in range(8):
                    mm(bank[:, :], oT[pb][:, fc, :], woutb[:, fc, half * 512:(half + 1) * 512], fc == 0, fc == 7, reads=[oT[pb], woutb], writes=[bank])
                vtt(x_t[:, half * 512:(half + 1) * 512], bank[:, :], x_t[:, half * 512:(half + 1) * 512], ALU.add, reads=[bank, x_t], writes=[x_t])
                yield
            act(yo[:], x_t[:], AF.Square, reads=[x_t], writes=[yo])
            vred(ssum[:], yo[:], reads=[yo], writes=[ssum])
            rsqrt_small(rstd[:], ssum[:], tmp1[:], 1.0 / D, NORM_EPS, reads=[ssum], writes=[tmp1, rstd])
            vstt(yo[:], x_t[:], rstd[:, 0:1], normf[:], ALU.mult, ALU.mult, reads=[x_t, rstd, normf], writes=[yo])
            S.dma("sync", yp[t0:t0 + TB, :], yo[:], reads=[yo], writes=[yp])
            yield

        def run_all(g):
            n = 0
            for _ in g:
                n += 1
            return n

        def interleave(ga, na, gb, nb):
            ia = ib = 0
            da = db = False
            while not (da and db):
                pick_a = (not da) and (db or (ia * nb <= ib * na))
                if pick_a:
                    try:
                        next(ga); ia += 1
                    except StopIteration:
                        da = True
                else:
                    try:
                        next(gb); ib += 1
                    except StopIteration:
                        db = True
            return ia, ib

        run_all(front(0))

        def record_units(g):
            units = []
            S.rec = []
            for _ in g:
                if S.rec:
                    units.append(S.rec)
                S.rec = []
            if S.rec:
                units.append(S.rec)
            S.rec = None
            return units

        A, B = [], []
        for tb in range(NTB):
            A.append(record_units(chain(tb)))
            if tb + 1 < NTB:
                B.append(record_units(front(tb + 1)))
        S.merge_emit(A, B, a_ok=lambda ia, ib: ib >= ia, b_ok=lambda ib, ia: ia >= ib)
        for h in range(8):
            tr(pA[0:64, h * 64:(h + 1) * 64], ST[:, h, :], ident[0:64, 0:64], reads=[ST, cst], writes=[pA])
        vcopy(SvT[:].rearrange("p h k -> p (h k)"), pA[0:64, :], reads=[pA], writes=[SvT])
        S.dma("sync", nwp[:].rearrange("h v k -> v h k"), SvT[:], reads=[SvT], writes=[nwp])
        S.finish([yp, ys, nsp, nwp, npp, nss, nws, nps], engname="sync")
        S.barrier()
    es_top.close()
    return nc, S


_CACHE = {}


def _consts():
    cst = np.zeros((128, C_END), np.float32)
    cst[:, C_ID:C_ID + 128] = np.eye(128, dtype=np.float32)
    ob = np.zeros((128, 128), np.float32)
    ob[0:64, 0:64] = 1.0
    ob[64:128, 64:128] = 1.0
    cst[:, C_ONES:C_ONES + 128] = ob
    s = np.arange(64)[:, None]
    t = np.arange(64)[None, :]
    mus = (s < t).astype(np.float32)
    mui = (s <= t).astype(np.float32)
    mls = (s > t).astype(np.float32)
    i64 = np.eye(64, dtype=np.float32)
    cst[0:64, C_MUS:C_MUS + 64] = mus
    cst[0:64, C_MUI:C_MUI + 64] = mui
    cst[0:64, C_MLS:C_MLS + 64] = mls
    rst = np.ones((512,), np.float32)
    rst[::CH] = 0.0
    cst[:, C_RST:C_RST + 512] = rst[None, :]
    for g, w in enumerate(WINS):
        pos = np.arange(16)
        cst[:, C_ICNT + g * 16:C_ICNT + (g + 1) * 16] = (1.0 / np.minimum(pos + 1, w)).astype(np.float32)[None, :]
    return cst


def kernel(x_prompt, x_sample, state_shift, state_wkv, state_pool, norm_w, w_in, mu_shift,
           w_decay_b, w0, w_aaa_b, a0, k_k, k_a, r_k, gn_w, gn_b, pool_w, pool_scale, w_out, norm_f):
    f = lambda a: np.ascontiguousarray(np.asarray(a, dtype=np.float32))
    x_prompt, x_sample, state_shift, state_wkv, state_pool = map(f, (x_prompt, x_sample, state_shift, state_wkv, state_pool))
    if "nc" not in _CACHE:
        _CACHE["nc"] = build_program()
    nc, S = _CACHE["nc"]

    def colmajor(v, n):
        return f(v).reshape(n, 128).T

    pvec = np.concatenate([
        colmajor(norm_w[0], 8), colmajor(mu_shift[0], 13), colmajor(w0[0], 4), colmajor(a0[0], 4), colmajor(k_k[0], 4),
        colmajor(k_a[0], 4), colmajor(f(r_k[0]).reshape(-1), 4), colmajor(gn_w[0], 4), colmajor(gn_b[0], 4), colmajor(pool_scale[0], 4)], axis=1)
    pvec = f(pvec)
    browA = f(f(mu_shift[0])[None, :])
    browB = f(np.concatenate([f(w0[0]), f(a0[0]), f(k_k[0]), f(k_a[0]), f(r_k[0]).reshape(-1), f(gn_w[0]), f(gn_b[0])])[None, :])
    cst = _consts()
    shared = {
        "w_in": f(w_in[0]), "w_out": f(w_out[0]), "wdec": f(w_decay_b[0]), "waaa": f(w_aaa_b[0]), "poolw": f(pool_w[0]),
        "pvec": pvec, "browA": browA, "browB": browB, "normf": f(norm_f)[None, :], "cst": cst,
    }
    in_maps = []
    for c in range(NCORE):
        bs = slice(c * DB, (c + 1) * DB)
        m = dict(shared)
        m["xp"] = x_prompt[c]
        m["xs"] = f(x_sample[bs].transpose(1, 0, 2).reshape(NS, D))
        m["sshift"] = state_shift[0, bs]
        m["swkv"] = f(state_wkv[0, bs].reshape(128, 4096))
        m["spool"] = f(state_pool[0, bs].reshape(DB * 15, 512))
        in_maps.append(m)
    res = run_bass_kernel_spmd(nc, in_maps, core_ids=list(range(NCORE)))
    R = res.results
    y_prompt = np.stack([R[c]["yp"] for c in range(NCORE)], axis=0)
    y_sample = np.concatenate([R[c]["ys"].reshape(DT, DB, D).transpose(1, 0, 2) for c in range(NCORE)], axis=0)
    nsp = np.stack([R[c]["nsp"].reshape(D_SHIFT) for c in range(NCORE)], axis=0)[None]
    nwp = np.stack([R[c]["nwp"] for c in range(NCORE)], axis=0)[None]
    npp = np.stack([R[c]["npp"] for c in range(NCORE)], axis=0)[None]
    nss = np.concatenate([R[c]["nss"] for c in range(NCORE)], axis=0)[None]
    nws = np.concatenate([R[c]["nws"].reshape(DB, 8, 64, 64) for c in range(NCORE)], axis=0)[None]
    nps = np.concatenate([R[c]["nps"] for c in range(NCORE)], axis=0)[None]
    out = (y_prompt, y_sample, nsp, nwp, npp, nss, nws, nps)
    return tuple(np.ascontiguousarray(o.astype(np.float32)) for o in out)
```

```python
import numpy as np
from contextlib import ExitStack
import concourse.bass as bass
import concourse.mybir as mybir
from concourse.bass_utils import run_bass_kernel_spmd

F32 = mybir.dt.float32
BF16 = mybir.dt.bfloat16
AF = mybir.ActivationFunctionType
ALU = mybir.AluOpType
AX = mybir.AxisListType

D = 1024
SEQ = 2048
NCORE = 8
DB = 16
DT = 4
NS = DB * DT
D_SHIFT = 1664
D_IN = 3200
C0 = float(np.exp(-0.5))
NORM_EPS = 1e-6
GN_EPS = 64e-5
L2_EPS = 1e-12
TB = 128
NTB = SEQ // TB
CH = 64
FBIAS = 0.0
NCH = TB // CH
WINS = (2, 4, 8, 16)

C_ID, C_ONES, C_MUS, C_MUI, C_MLS, C_RST, C_ICNT, C_END = 0, 128, 256, 320, 384, 448, 960, 1024
PV_NW, PV_MU, PV_W0, PV_A0, PV_KK, PV_KA, PV_RK, PV_GW, PV_GB, PV_PS, PV_END = 0, 8, 21, 25, 29, 33, 37, 41, 45, 49, 53
BRB_W0, BRB_A0, BRB_KK, BRB_KA, BRB_RK, BRB_GW, BRB_GB = 0, 512, 1024, 1536, 2048, 2560, 3072


class Buf:
    __slots__ = ("name", "w", "r")

    def __init__(self, name):
        self.name = name
        self.w = None
        self.r = []


class T:
    def __init__(self, t, name, buf=None):
        self.t = t
        self.b = buf if buf is not None else Buf(name)

    def __getitem__(self, k):
        return self.t[k]


class Sched:
    def __init__(self, nc, n_dma_sems=32):
        self.nc = nc
        self.eng = {}
        for name in ["tensor", "vector", "scalar", "gpsimd", "sync"]:
            h = getattr(nc, name)
            sem = nc.alloc_semaphore(name="prog_" + name)
            self.eng[name] = dict(h=h, sem=sem, cnt=0, waited={})
        self.dma_sems = [dict(sem=nc.alloc_semaphore(name=f"dma{i}"), cnt=0) for i in range(n_dma_sems)]
        self.dma_rr = 0
        self.ninstr = 0
        self.rec = None

    def _wait(self, engname, tok):
        sem, val, src = tok
        e = self.eng[engname]
        key = id(sem)
        if e["waited"].get(key, 0) >= val:
            return
        e["h"].wait_ge(sem, val)
        e["waited"][key] = val
        self.ninstr += 1

    def _deps(self, engname, reads, writes):
        toks = []
        for b in reads:
            if b.w is not None:
                toks.append(b.w)
        for b in writes:
            if b.w is not None:
                toks.append(b.w)
            toks.extend(b.r)
        for tok in toks:
            if tok[2] == engname and engname == "tensor":
                continue
            self._wait(engname, tok)

    @staticmethod
    def _bufs(xs):
        return [x.b if isinstance(x, T) else x for x in xs]

    def _record(self, tok, reads, writes):
        for b in reads:
            b.r.append(tok)
            if len(b.r) > 64:
                b.r = b.r[-64:] if False else b.r
        for b in writes:
            b.w = tok
            b.r = []

    def op(self, engname, fn, reads=(), writes=(), cost=0.3):
        reads = self._bufs(reads)
        writes = self._bufs(writes)
        if self.rec is not None:
            self.rec.append(("op", engname, fn, reads, writes, cost, None))
            return None
        e = self.eng[engname]
        self._deps(engname, reads, writes)
        ins = fn(e["h"])
        e["cnt"] += 1
        ins.then_inc(e["sem"], 1)
        e["waited"][id(e["sem"])] = max(e["waited"].get(id(e["sem"]), 0), 0)
        tok = (e["sem"], e["cnt"], engname)
        self._record(tok, reads, writes)
        self.ninstr += 1
        return tok

    def dma(self, qname, out, in_, reads=(), writes=(), **kw):
        reads = self._bufs(reads)
        writes = self._bufs(writes)
        if self.rec is not None:
            self.rec.append(("dma", qname, (out, in_), reads, writes, 2.5, kw))
            return None
        e = self.eng[qname]
        self._deps(qname, reads, writes)
        d = self.dma_sems[self.dma_rr]
        self.dma_rr = (self.dma_rr + 1) % len(self.dma_sems)
        if d["cnt"] > 0:
            self._wait(qname, (d["sem"], 16 * d["cnt"], "dma"))
        ins = e["h"].dma_start(out=out, in_=in_, **kw)
        d["cnt"] += 1
        ins.then_inc(d["sem"], 16)
        tok = (d["sem"], 16 * d["cnt"], "dma")
        self._record(tok, reads, writes)
        self.ninstr += 1
        return tok

    def emit(self, r):
        kind, eng, fn, reads, writes, cost, kw = r
        if kind == "op":
            self.op(eng, fn, reads=reads, writes=writes)
        else:
            self.dma(eng, fn[0], fn[1], reads=reads, writes=writes, **kw)

    def merge_emit(self, A, B, a_ok, b_ok):
        eng_free = {}
        ready = {}
        acc = {}

        def est(r):
            kind, eng, fn, reads, writes, cost, kw = r
            t = eng_free.get(eng, 0.0)
            for b in reads:
                rt_, re_ = ready.get(id(b), (0.0, eng))
                t = max(t, rt_ + (0.15 if re_ != eng else 0.0))
            for b in writes:
                rt_, re_ = ready.get(id(b), (0.0, eng))
                t = max(t, rt_ + (0.15 if re_ != eng else 0.0), acc.get(id(b), 0.0) + 0.1)
            return t

        def commit(r, t):
            kind, eng, fn, reads, writes, cost, kw = r
            if kind == "dma":
                eng_free[eng] = t + 0.1
                end = t + cost
            else:
                end = t + cost
                eng_free[eng] = end
            for b in reads:
                acc[id(b)] = max(acc.get(id(b), 0.0), end)
            for b in writes:
                ready[id(b)] = (end, eng)
                acc[id(b)] = max(acc.get(id(b), 0.0), end)

        def run_unit(u):
            for r in u:
                commit(r, est(r))
                self.emit(r)

        ia = ib = 0
        ja = jb = 0
        while ia < len(A) or ib < len(B):
            ca = None
            cb = None
            if ia < len(A) and (ja > 0 or a_ok(ia, ib)):
                ca = A[ia][ja]
            if ib < len(B) and (jb > 0 or b_ok(ib, ia)):
                cb = B[ib][jb]
            assert ca is not None or cb is not None, (ia, ib, ja, jb)
            ta = est(ca[0]) if ca is not None else None
            tb_ = est(cb[0]) if cb is not None else None
            if cb is None or (ca is not None and ta + FBIAS < tb_):
                run_unit(ca); ja += 1
                if ja == len(A[ia]):
                    ia += 1; ja = 0
            else:
                run_unit(cb); jb += 1
                if jb == len(B[ib]):
                    ib += 1; jb = 0

    def barrier(self):
        toks = [(e["sem"], e["cnt"], n) for n, e in self.eng.items() if e["cnt"] > 0]
        toks += [(d["sem"], 16 * d["cnt"], "dma") for d in self.dma_sems if d["cnt"] > 0]
        for n in self.eng:
            for tok in toks:
                if tok[2] == n:
                    continue
                self._wait(n, tok)

    def finish(self, tiles, engname="sync"):
        for b in self._bufs(tiles):
            if b.w is not None:
                self._wait(engname, b.w)


class _Stop(Exception):
    pass


def build_program(stop=None):
    nc = bass.Bass("TRN2", target_bir_lowering=False)
    S = Sched(nc)
    try:
        _build_body(nc, S, stop)
    except _Stop:
        S.barrier()
    return nc, S


def _build_body(nc, S, stop):
    def chk(label):
        if stop == label:
            raise _Stop()


    def din(name, shape):
        return nc.dram_tensor(name, list(shape), F32, kind="ExternalInput").ap()

    def dout(name, shape):
        return T(nc.dram_tensor(name, list(shape), F32, kind="ExternalOutput").ap(), name)

    xp = din("xp", [SEQ, D])
    xs = din("xs", [NS, D])
    sshift = din("sshift", [DB, D_SHIFT])
    swkv = din("swkv", [128, 4096])
    spool = din("spool", [DB * 15, 512])
    w_in = din("w_in", [D, D_IN])
    w_out = din("w_out", [D, D])
    wdec = din("wdec", [64, 512])
    waaa = din("waaa", [64, 512])
    poolw = din("poolw", [4, 128, 128])
    pvec_d = din("pvec", [128, PV_END])
    browA_d = din("browA", [1, D_SHIFT])
    browB_d = din("browB", [1, 3584])
    normf_d = din("normf", [1, D])
    cst_d = din("cst", [128, C_END])

    yp = dout("yp", [SEQ, D])
    ys = dout("ys", [NS, D])
    nsp = dout("nsp", [13, 128])
    nwp = dout("nwp", [8, 64, 64])
    npp = dout("npp", [15, 512])
    nss = dout("nss", [DB, D_SHIFT])
    nws = dout("nws", [128, 4096])
    nps = dout("nps", [DB, 15, 512])
    scr1 = T(nc.dram_tensor("scr1", [6, DT, DB, 8, 64], F32, kind="Internal").ap(), "scr1")
    scr2 = T(nc.dram_tensor("scr2", [DB, 8, DT, 64], F32, kind="Internal").ap(), "scr2")

    es_top = ExitStack()

    def sb(es, name, shape, dt=F32):
        return T(es.enter_context(nc.sbuf_tensor("s_" + name, list(shape), dt)), name)

    def pst(name, shape, dt=F32):
        return T(nc.alloc_psum_tensor("p_" + name, list(shape), dt), name)

    def nel(ap):
        n = 1
        for s_ in ap.shape[1:]:
            n *= s_
        return n

    def mm(out, lhsT, rhs, start, stop, reads, writes):
        passes = 4 if lhsT.dtype == F32 else 1
        c_ = max(0.055, nel(rhs) * passes / 2000.0 + 0.03)
        S.op("tensor", lambda e: e.matmul(out, lhsT=lhsT, rhs=rhs, start=start, stop=stop), reads=reads, writes=writes, cost=c_)

    def tr(out, in_, ident, reads, writes):
        S.op("tensor", lambda e: e.transpose(out, in_, ident), reads=reads, writes=writes, cost=0.13)

    def act(out, in_, func, reads, writes, bias=None, scale=None, eng="scalar"):
        kw = {}
        if bias is not None:
            kw["bias"] = bias
        if scale is not None:
            kw["scale"] = scale
        S.op("scalar", lambda e: e.activation(out=out, in_=in_, func=func, **kw), reads=reads, writes=writes,
             cost=0.1 + 0.1 * len(kw) + nel(in_) * 0.00095)

    def ecost(eng, n):
        return 0.08 + n * (0.00105 if eng == "vector" else 0.0025)

    def vtt(out, in0, in1, op, reads, writes, eng="vector"):
        S.op(eng, lambda e: e.tensor_tensor(out=out, in0=in0, in1=in1, op=op), reads=reads, writes=writes, cost=ecost(eng, nel(out)))

    def vts(out, in0, s1, s2, op0, op1, reads, writes, eng="vector"):
        if op1 is None:
            S.op(eng, lambda e: e.tensor_scalar(out=out, in0=in0, scalar1=s1, scalar2=None, op0=op0), reads=reads, writes=writes,
                 cost=ecost(eng, nel(out)))
        else:
            S.op(eng, lambda e: e.tensor_scalar(out=out, in0=in0, scalar1=s1, scalar2=s2, op0=op0, op1=op1), reads=reads, writes=writes,
                 cost=ecost(eng, nel(out)))

    def vstt(out, in0, scalar, in1, op0, op1, reads, writes):
        S.op("vector", lambda e: e.scalar_tensor_tensor(out=out, in0=in0, scalar=scalar, in1=in1, op0=op0, op1=op1), reads=reads, writes=writes,
             cost=ecost("vector", nel(out)))

    def vcopy(out, in_, reads, writes, eng="vector"):
        S.op(eng, lambda e: e.tensor_copy(out=out, in_=in_), reads=reads, writes=writes, cost=ecost(eng, nel(out)))

    def vred(out, in_, reads, writes):
        S.op("vector", lambda e: e.tensor_reduce(out=out, in_=in_, axis=AX.X, op=ALU.add), reads=reads, writes=writes,
             cost=ecost("vector", nel(in_)))

    def vrecip(out, in_, reads, writes):
        S.op("vector", lambda e: e.reciprocal(out=out, in_=in_), reads=reads, writes=writes, cost=0.08 + nel(out) * 0.0084)

    def memset(ap, val, writes, eng="gpsimd"):
        S.op(eng, lambda e: e.memset(ap, val), writes=writes)

    def rsqrt_small(out, in_, tmp, scale, eps, reads, writes):
        act(tmp, in_, AF.Sqrt, reads=reads, writes=writes, bias=None, scale=None) if False else None
        vts(tmp, in_, scale, eps, ALU.mult, ALU.add, reads=reads, writes=writes)
        act(tmp, tmp, AF.Sqrt, reads=writes, writes=writes)
        vrecip(out, tmp, reads=writes, writes=writes)

    pg = [pst(f"pg{i}", [128, 512]) for i in range(2)]
    pT = pst("pT", [128, 1024], BF16)
    pM = pst("pM", [128, 512])
    pA = pst("pA", [128, 512])
    pB = pst("pB", [128, 512])
    pC = pst("pC", [128, 512])
    pD = pst("pD", [128, 512])

    cst = sb(es_top, "cst", [128, C_END])
    pvec = sb(es_top, "pvec", [128, PV_END])
    omu = sb(es_top, "omu", [128, 13])
    omka = sb(es_top, "omka", [128, 4])
    identb = sb(es_top, "identb", [128, 128], BF16)
    winb = sb(es_top, "winb", [128, 8, D_IN], BF16)
    woutb = sb(es_top, "woutb", [128, 8, D], BF16)
    wd = sb(es_top, "wd", [64, 512])
    wa = sb(es_top, "wa", [128, 512])
    pw = sb(es_top, "pw", [128, 4, 128])
    normf = sb(es_top, "normf", [128, D])

    ident = cst[:, C_ID:C_ID + 128]
    onesblk = cst[:, C_ONES:C_ONES + 128]

    S.dma("sync", cst[:], cst_d, writes=[cst])
    S.dma("sync", pvec[:], pvec_d, writes=[pvec])
    S.dma("sync", wd[:], wdec, writes=[wd])
    S.dma("sync", wa[64:128, :], waaa, writes=[wa])
    S.dma("sync", pw[:], poolw.rearrange("g c e -> c g e"), writes=[pw])
    S.dma("sync", normf[:], normf_d.partition_broadcast(128), writes=[normf])
    vcopy(identb[:], ident, reads=[cst], writes=[identb])
    vts(omka[:], pvec[:, PV_KA:PV_KA + 4], -1.0, 1.0, ALU.mult, ALU.add, reads=[pvec], writes=[omka])

    with ExitStack() as es:
        stg = [sb(es, f"stg{i}", [128, D_IN]) for i in range(3)]
        for dc in range(8):
            st = stg[dc % 3]
            S.dma("sync", st[:], w_in[dc * 128:(dc + 1) * 128, :], writes=[st])
            h = D_IN // 2
            vts(winb[:, dc, 0:h], st[:, 0:h], pvec[:, PV_NW + dc:PV_NW + dc + 1], None, ALU.mult, None, reads=[st, pvec], writes=[winb])
            act(winb[:, dc, h:], st[:, h:], AF.Copy, reads=[st, pvec], writes=[winb], scale=pvec[:, PV_NW + dc:PV_NW + dc + 1])
        S.barrier()
        chk("W")

    def final_tile(es_tiles, n, x_t, oT_list, out_dram_ap, out_T):
        res, sq, ssum, tmp1, rstd, yo = es_tiles
        for half in range(2):
            bank = pD if half == 0 else pC
            for fc in range(8):
                mm(bank[0:n, :], oT_list[fc], woutb[:, fc, half * 512:(half + 1) * 512], fc == 0, fc == 7,
                   reads=[oT_list_T, woutb], writes=[bank])
            vtt(res[0:n, half * 512:(half + 1) * 512], bank[0:n, :], x_t[0:n, half * 512:(half + 1) * 512], ALU.add,
                reads=[bank, x_t], writes=[res])
        act(sq[0:n, :], res[0:n, :], AF.Square, reads=[res], writes=[sq])
        vred(ssum[0:n, :], sq[0:n, :], reads=[sq], writes=[ssum])
        rsqrt_small(rstd[0:n, :], ssum[0:n, :], tmp1[0:n, :], 1.0 / D, NORM_EPS, reads=[ssum], writes=[tmp1, rstd])
        vstt(yo[0:n, :], res[0:n, :], rstd[0:n, 0:1], normf[0:n, :], ALU.mult, ALU.mult, reads=[res, rstd, normf], writes=[yo])
        S.dma("sync", out_dram_ap, yo[0:n, :], reads=[yo], writes=[out_T])

    oT_list_T = None

    with ExitStack() as es:
        browB = sb(es, "browB", [NS, 3584])
        S.dma("sync", browB[:], browB_d.partition_broadcast(NS), writes=[browB])
        x_s = sb(es, "x_s", [NS, D])
        S.dma("sync", x_s[:], xs, writes=[x_s])
        hTs = sb(es, "hTs", [128, 8, DB + NS], BF16)
        graw_s = sb(es, "graw_s", [NS, 512])
        u_s = sb(es, "u_s", [NS, 512])
        gp_s = sb(es, "gp_s", [NS, 512])
        bonus_s = sb(es, "bonus_s", [NS, 512])
        st8 = sb(es, "st8", [NS, 8])
        st8b = sb(es, "st8b", [NS, 8])
        st8c = sb(es, "st8c", [NS, 8])

        def v3(ap):
            return ap.rearrange("p (h k) -> p h k", k=64)

        def bc8(ap8):
            return ap8.unsqueeze(2).to_broadcast([NS, 8, 64])

        with ExitStack() as e1:
            browA = sb(e1, "browA", [NS, D_SHIFT])
            S.dma("sync", browA[:], browA_d.partition_broadcast(NS), writes=[browA])
            omka_b = sb(e1, "omka_b", [NS, 512])
            vts(omka_b[:], browB[:, BRB_KA:BRB_KA + 512], -1.0, 1.0, ALU.mult, ALU.add, reads=[browB], writes=[omka_b])
            sq_s = sb(e1, "sq_s", [NS, D])
            ss_s = sb(e1, "ss_s", [NS, 1])
            t1_s = sb(e1, "t1_s", [NS, 1])
            rstd_s = sb(e1, "rstd_s", [NS, 1])
            xn_s = sb(e1, "xn_s", [NS, D], BF16)
            act(sq_s[:], x_s[:], AF.Square, reads=[x_s], writes=[sq_s])
            vred(ss_s[:], sq_s[:], reads=[sq_s], writes=[ss_s])
            rsqrt_small(rstd_s[:], ss_s[:], t1_s[:], 1.0 / D, NORM_EPS, reads=[ss_s], writes=[t1_s, rstd_s])
            vts(xn_s[:], x_s[:], rstd_s[:, 0:1], None, ALU.mult, None, reads=[x_s, rstd_s], writes=[xn_s])
            memset(hTs[:, :, 0:DB], 0.0, writes=[hTs])
            for dc in range(8):
                tr(pT[:, dc * 128:dc * 128 + NS], xn_s[:, dc * 128:(dc + 1) * 128], identb[0:NS, 0:NS], reads=[xn_s, identb], writes=[pT])
            vcopy(hTs[:, :, DB:DB + NS], pT[:].rearrange("p (c t) -> p c t", t=128)[:, :, 0:NS], reads=[pT], writes=[hTs])

            p_s = sb(e1, "p_s", [NS, D_SHIFT])
            prev_s = sb(e1, "prev_s", [NS, D_SHIFT])
            col_chunks = [(0, 512), (512, 512), (1024, 512), (1536, 128)]
            kk_ = 0
            for (c0, n) in col_chunks:
                bank = pg[kk_ % 2]; kk_ += 1
                for dc in range(8):
                    mm(bank[0:NS, 0:n], hTs[:, dc, DB:DB + NS], winb[:, dc, c0:c0 + n], dc == 0, dc == 7, reads=[hTs, winb], writes=[bank])
                act(p_s[:, c0:c0 + n], bank[0:NS, 0:n], AF.Copy, reads=[bank], writes=[p_s])
                bank = pg[kk_ % 2]; kk_ += 1
                for dc in range(8):
                    mm(bank[0:NS, 0:n], hTs[:, dc, 0:NS], winb[:, dc, c0:c0 + n], dc == 0, dc == 7, reads=[hTs, winb], writes=[bank])
                vcopy(prev_s[:, c0:c0 + n], bank[0:NS, 0:n], reads=[bank], writes=[prev_s])
            for (c0, dst, fn) in [(1664, graw_s, AF.Silu), (2176, u_s, AF.Copy), (2688, gp_s, AF.Silu)]:
                bank = pg[kk_ % 2]; kk_ += 1
                for dc in range(8):
                    mm(bank[0:NS, :], hTs[:, dc, DB:DB + NS], winb[:, dc, c0:c0 + 512], dc == 0, dc == 7, reads=[hTs, winb], writes=[bank])
                act(dst[:], bank[0:NS, :], fn, reads=[bank], writes=[dst])
            S.dma("sync", prev_s[0:DB, :], sshift, writes=[prev_s])
            S.dma("sync", nss[:], p_s[NS - DB:NS, :], reads=[p_s], writes=[nss])
            S.dma("sync", nps[:, 0:11, :], spool.rearrange("(b j) c -> b j c", j=15)[:, 4:15, :], writes=[nps])
            for t in range(DT):
                S.dma("sync", nps[:, 11 + t, :], u_s[t * DB:(t + 1) * DB, :], reads=[u_s], writes=[nps])

            vtt(prev_s[:], prev_s[:], p_s[:], ALU.subtract, reads=[prev_s, p_s], writes=[prev_s])
            vtt(prev_s[:], prev_s[:], browA[:], ALU.mult, reads=[prev_s, browA], writes=[prev_s])
            vtt(prev_s[:], prev_s[:], p_s[:], ALU.add, reads=[prev_s, p_s], writes=[prev_s])
            ps_s = prev_s
            r_s = ps_s[:, 0:512]
            k_s = ps_s[:, 512:1024]
            v_s = ps_s[:, 1024:1536]

            lT = sb(e1, "lT", [128, NS])
            tr(pM[:, 0:NS], ps_s[:, 1536:1664], ident[0:NS, 0:NS], reads=[ps_s, cst], writes=[pM])
            act(lT[0:64, :], pM[0:64, 0:NS], AF.Tanh, reads=[pM], writes=[lT])
            act(lT[64:128, :], pM[64:128, 0:NS], AF.Copy, reads=[pM], writes=[lT])
            sg_s = sb(e1, "sg_s", [NS, 512])
            a_s = sb(e1, "a_s", [NS, 512])
            mm(pA[0:NS, :], lT[0:64, :], wd[:, :], True, True, reads=[lT, wd], writes=[pA])
            vtt(sg_s[:], pA[0:NS, :], browB[:, BRB_W0:BRB_W0 + 512], ALU.add, reads=[pA, browB], writes=[sg_s])
            act(sg_s[:], sg_s[:], AF.Sigmoid, reads=[sg_s], writes=[sg_s])
            mm(pB[0:NS, :], lT[64:128, :], wa[64:128, :], True, True, reads=[lT, wa], writes=[pB])
            vtt(a_s[:], pB[0:NS, :], browB[:, BRB_A0:BRB_A0 + 512], ALU.add, reads=[pB, browB], writes=[a_s])
            act(a_s[:], a_s[:], AF.Sigmoid, reads=[a_s], writes=[a_s])

            pk = sb(e1, "pk", [NS, 4, 512])
            PQ = {1: 0, 2: 1, 4: 2, 5: 3}
            tmpA = sb(e1, "tmpA", [NS, 512])
            tmpB = sb(e1, "tmpB", [NS, 512])
            act(pk[:, PQ[1], :], sg_s[:], AF.Exp, reads=[sg_s], writes=[pk], scale=-C0)
            vtt(tmpA[:], k_s, browB[:, BRB_KK:BRB_KK + 512], ALU.mult, reads=[ps_s, browB], writes=[tmpA])
            vtt(tmpB[:], tmpA[:], tmpA[:], ALU.mult, reads=[tmpA], writes=[tmpB])
            vred(st8[:], v3(tmpB[:]), reads=[tmpB], writes=[st8])
            rsqrt_small(st8b[:], st8[:], st8c[:], 1.0, L2_EPS, reads=[st8], writes=[st8c, st8b])
            vtt(v3(tmpA[:]), v3(tmpA[:]), bc8(st8b[:]), ALU.mult, reads=[tmpA, st8b], writes=[tmpA])
            vts(pk[:, PQ[4], :], tmpA[:], -1.0, None, ALU.mult, None, reads=[tmpA], writes=[pk])
            vtt(pk[:, PQ[5], :], tmpA[:], a_s[:], ALU.mult, reads=[tmpA, a_s], writes=[pk])
            vtt(tmpB[:], a_s[:], browB[:, BRB_KA:BRB_KA + 512], ALU.mult, reads=[a_s, browB], writes=[tmpB])
            vtt(tmpB[:], tmpB[:], omka_b[:], ALU.add, reads=[tmpB, omka_b], writes=[tmpB])
            vtt(pk[:, PQ[2], :], k_s, tmpB[:], ALU.mult, reads=[ps_s, tmpB], writes=[pk])
            vtt(tmpB[:], r_s, browB[:, BRB_RK:BRB_RK + 512], ALU.mult, reads=[ps_s, browB], writes=[tmpB])
            vtt(tmpB[:], tmpB[:], pk[:, PQ[2], :], ALU.mult, reads=[tmpB, pk], writes=[tmpB])
            vred(st8[:], v3(tmpB[:]), reads=[tmpB], writes=[st8])
            vtt(v3(bonus_s[:]), v3(v_s), bc8(st8[:]), ALU.mult, reads=[ps_s, st8], writes=[bonus_s])
            sview = scr1[:].rearrange("q t b h k -> q (t b) (h k)")
            S.dma("sync", sview[0], r_s, reads=[ps_s], writes=[scr1])
            S.dma("sync", sview[3], v_s, reads=[ps_s], writes=[scr1])
            for qq, slot in PQ.items():
                S.dma("sync", sview[qq], pk[:, slot, :], reads=[pk], writes=[scr1])
            S.finish([scr1], engname="sync")
            S.barrier()
            chk("S1")

        with ExitStack() as e2:
            sIn = sb(e2, "sIn", [128, 6, DT, 64])
            S.dma("sync", sIn[:], scr1[:].rearrange("q t b h k -> (b h) q t k"), reads=[scr1], writes=[sIn])
            St = sb(e2, "St", [128, 64, 64])
            S.dma("sync", St[:].rearrange("p v k -> p (v k)"), swkv, writes=[St])
            tmpS = sb(e2, "tmpS", [128, 64, 64])
            sa = sb(e2, "sa", [128, 64])
            yS = sb(e2, "yS", [128, DT, 64])
            stgo = [sb(e2, f"stgo{i}", [128, D]) for i in range(3)]
            for fc in range(8):
                so = stgo[fc % 3]
                S.dma("sync", so[:], w_out[fc * 128:(fc + 1) * 128, :], writes=[so])
                act(woutb[:, fc, :], so[:], AF.Copy, reads=[so], writes=[woutb])

            def bv(ap):
                return ap.unsqueeze(1).to_broadcast([128, 64, 64])

            def bk(ap):
                return ap.unsqueeze(2).to_broadcast([128, 64, 64])

            for t in range(DT):
                q = lambda i: sIn[:, i, t, :]
                vtt(tmpS[:], St[:], bv(q(4)), ALU.mult, reads=[St, sIn], writes=[tmpS])
                vred(sa[:], tmpS[:], reads=[tmpS], writes=[sa])
                vtt(St[:], St[:], bv(q(1)), ALU.mult, reads=[St, sIn], writes=[St])
                vtt(tmpS[:], bk(sa[:]), bv(q(5)), ALU.mult, reads=[sa, sIn], writes=[tmpS])
                vtt(St[:], St[:], tmpS[:], ALU.add, reads=[St, tmpS], writes=[St])
                vtt(tmpS[:], bk(q(3)), bv(q(2)), ALU.mult, reads=[sIn], writes=[tmpS])
                vtt(St[:], St[:], tmpS[:], ALU.add, reads=[St, tmpS], writes=[St])
                vtt(tmpS[:], St[:], bv(q(0)), ALU.mult, reads=[St, sIn], writes=[tmpS])
                vred(yS[:, t, :], tmpS[:], reads=[tmpS], writes=[yS])
            S.dma("sync", nws[:], St[:].rearrange("p v k -> p (v k)"), reads=[St], writes=[nws])
            S.dma("sync", scr2[:].rearrange("b h t v -> (b h) t v"), yS[:], reads=[yS], writes=[scr2])
            S.finish([scr2, nws], engname="sync")
            S.barrier()
            chk("S2")

        with ExitStack() as e3:
            yT = sb(e3, "yT", [NS, 512])
            tmpA = sb(e3, "tmpA3", [NS, 512])
            for t in range(DT):
                S.dma("sync", yT[t * DB:(t + 1) * DB, :].rearrange("b (h v) -> b h v", v=64), scr2[:][:, :, t, :], reads=[scr2], writes=[yT])
            vred(st8[:], v3(yT[:]), reads=[yT], writes=[st8])
            vts(st8[:], st8[:], 1.0 / 64, None, ALU.mult, None, reads=[st8], writes=[st8])
            vtt(v3(yT[:]), v3(yT[:]), bc8(st8[:]), ALU.subtract, reads=[yT, st8], writes=[yT])
            vtt(tmpA[:], yT[:], yT[:], ALU.mult, reads=[yT], writes=[tmpA])
            vred(st8[:], v3(tmpA[:]), reads=[tmpA], writes=[st8])
            rsqrt_small(st8b[:], st8[:], st8c[:], 1.0 / 64, GN_EPS, reads=[st8], writes=[st8c, st8b])
            vtt(v3(yT[:]), v3(yT[:]), bc8(st8b[:]), ALU.mult, reads=[yT, st8b], writes=[yT])
            vtt(yT[:], yT[:], browB[:, BRB_GW:BRB_GW + 512], ALU.mult, reads=[yT, browB], writes=[yT])
            vtt(yT[:], yT[:], browB[:, BRB_GB:BRB_GB + 512], ALU.add, reads=[yT, browB], writes=[yT])
            vtt(yT[:], yT[:], bonus_s[:], ALU.add, reads=[yT, bonus_s], writes=[yT])
            vtt(yT[:], yT[:], graw_s[:], ALU.mult, reads=[yT, graw_s], writes=[yT])
            oTs = sb(e3, "oTs", [128, 8, NS], BF16)
            for fb in range(4):
                tr(pA[:, fb * 64:fb * 64 + NS], yT[:, fb * 128:(fb + 1) * 128], ident[0:NS, 0:NS], reads=[yT, cst], writes=[pA])
            vcopy(oTs[:, 0:4, :], pA[:, 0:4 * NS].rearrange("p (f t) -> p f t", t=NS), reads=[pA], writes=[oTs])

            uext = sb(e3, "uext_s", [128, 4, DB, 19])
            sp0 = sb(e3, "sp0", [120, 512])
            sp1 = sb(e3, "sp1", [120, 512])
            S.dma("sync", sp0[:], spool[0:120, :], writes=[sp0])
            S.dma("sync", sp1[:], spool[120:240, :], writes=[sp1])
            for g in range(4):
                tr(pB[:, 0:120], sp0[:, g * 128:(g + 1) * 128], ident[0:120, 0:120], reads=[sp0, cst], writes=[pB])
                tr(pB[:, 128:248], sp1[:, g * 128:(g + 1) * 128], ident[0:120, 0:120], reads=[sp1, cst], writes=[pB])
                vcopy(uext[:, g, 0:8, 0:15], pB[:, 0:120].rearrange("p (b j) -> p b j", j=15), reads=[pB], writes=[uext])
                vcopy(uext[:, g, 8:16, 0:15], pB[:, 128:248].rearrange("p (b j) -> p b j", j=15), reads=[pB], writes=[uext])
                tr(pM[:, 0:NS], u_s[:, g * 128:(g + 1) * 128], ident[0:NS, 0:NS], reads=[u_s, cst], writes=[pM])
                vcopy(uext[:, g, :, 15:19], pM[:, 0:NS].rearrange("p (t b) -> p b t", b=DB), reads=[pM], writes=[uext])
            s2 = sb(e3, "s2_s", [128, 4, DB, 19])
            s4 = sb(e3, "s4_s", [128, 3, DB, 19])
            s8 = sb(e3, "s8_s", [128, 2, DB, 19])
            s16 = sb(e3, "s16_s", [128, 1, DB, 19])
            d_s = sb(e3, "d_s", [128, 4, DT, DB])
            vtt(s2[:, :, :, 1:19], uext[:, :, :, 1:19], uext[:, :, :, 0:18], ALU.add, reads=[uext], writes=[s2])
            vtt(s4[:, :, :, 3:19], s2[:, 1:4, :, 3:19], s2[:, 1:4, :, 1:17], ALU.add, reads=[s2], writes=[s4])
            vtt(s8[:, :, :, 7:19], s4[:, 1:3, :, 7:19], s4[:, 1:3, :, 3:15], ALU.add, reads=[s4], writes=[s8])
            vtt(s16[:, :, :, 15:19], s8[:, 1:2, :, 15:19], s8[:, 1:2, :, 7:11], ALU.add, reads=[s8], writes=[s16])
            tots = [(s2, 0), (s4, 1), (s8, 2), (s16, 3)]
            for g in range(4):
                tt, off = tots[g]
                vstt(d_s[:, g, :, :].rearrange("p t b -> p b t"), tt[:, g - off, :, 15:19], 1.0 / WINS[g], uext[:, g, :, 15:19],
                     ALU.mult, ALU.subtract, reads=[tt, uext], writes=[d_s])
            gpT = sb(e3, "gpT", [128, 4, NS])
            for g in range(4):
                tr(pM[:, 64 + g * 64:64 + g * 64 + NS], gp_s[:, g * 128:(g + 1) * 128], ident[0:NS, 0:NS], reads=[gp_s, cst], writes=[pM])
            vcopy(gpT[:], pM[:, 64:64 + 4 * NS].rearrange("p (g t) -> p g t", t=NS), reads=[pM], writes=[gpT])
            for g in range(4):
                mm(pA[:, g * 64:g * 64 + NS], pw[:, g, :], d_s[:, g, :, :].rearrange("p t b -> p (t b)"), True, True, reads=[pw, d_s], writes=[pA])
            for g in range(4):
                vstt(oTs[:, 4 + g, :], pA[:, g * 64:g * 64 + NS], pvec[:, PV_PS + g:PV_PS + g + 1], gpT[:, g, :], ALU.mult, ALU.mult,
                     reads=[pA, pvec, gpT], writes=[oTs])

            sq = sb(e3, "sq2_s", [NS, D]); ssum = sb(e3, "ssum_s", [NS, 1])
            tmp1 = sb(e3, "tmp1_s", [NS, 1]); rstd = sb(e3, "rstd2_s", [NS, 1]); yo = sb(e3, "yo_s", [NS, D])
            oT_list_T = oTs
            final_tile((x_s, sq, ssum, tmp1, rstd, yo), NS, x_s, [oTs[:, fc, :] for fc in range(8)], ys[:], ys)
            S.finish([ys, nss, nps], engname="sync")
            S.barrier()
            chk("S3")

    with ExitStack() as es:
        def sbl(name, shape, dt=F32, n=2):
            return [sb(es, f"{name}_{i}", shape, dt) for i in range(n)]

        xt = sbl("xt", [128, D])
        sqx = sb(es, "sqx", [128, D], BF16)
        yo = sb(es, "yo", [128, D])
        ssx = sb(es, "ssx", [128, 1]); t1x = sb(es, "t1x", [128, 1]); rsx = sb(es, "rsx", [128, 1])
        xnb = sb(es, "xnb", [128, D], BF16)
        hT = sb(es, "hT", [128, 8, TB], BF16)
        praw = sbl("praw", [128, 4, TB + 1], n=1) * 2
        qsc = sbl("qsc", [128, 4, TB], n=1) * 2
        halo = sb(es, "halo", [128, 13])
        omu = sb(es, "omu2", [128, 13])
        psr = sb(es, "psr", [128, 4, TB]); psk = sb(es, "psk", [128, 4, TB]); psv = sb(es, "psv", [128, 4, TB])
        ps12 = sb(es, "ps12", [128, TB])
        psx = [T(g_[:, i, :], f"psx{gi_}_{i}", buf=g_.b) for gi_, g_ in enumerate([psr, psk, psv]) for i in range(4)] + [ps12]
        sg = sb(es, "sg", [128, 4, TB]); av = sb(es, "av", [128, 4, TB])
        gsil = sbl("gsil", [128, 4, TB], BF16)
        gpsil = sb(es, "gpsil", [128, 4, TB], BF16)
        uext = sb(es, "uext", [128, 4, 15 + TB])
        th = sb(es, "th", [64, TB])
        wbig = [sb(es, f"wbig{i}", [128, 4, TB]) for i in range(4)]
        w1, w2, w3, w4 = wbig
        srot = [T(wbig[i][:].rearrange("p f t -> p (f t)")[:, 0:15 + TB], f"srot{i}", buf=wbig[i].b) for i in range(4)]
        kkn = sb(es, "kkn", [128, 4, TB]); kmod = sb(es, "kmod", [128, 4, TB]); bv_ = sb(es, "bv_", [128, 4, TB])
        cum = sb(es, "cum", [128, 4, TB])
        dpl = T(kkn[:, 0, :], "dpl", buf=kkn.b)
        at = sbl("at", [64, 4, 2, TB], BF16)
        rt = sbl("rt", [64, 4, 2, TB], BF16)
        bt = sb(es, "bt", [64, 4, 2, TB], BF16)
        kt = sb(es, "kt", [64, 4, 2, TB], BF16)
        bh = sb(es, "bh", [128, 4, TB], BF16); kh = sb(es, "kh", [128, 4, TB], BF16); vb = sb(es, "vb", [128, 4, TB], BF16)
        bon = sbl("bon", [128, 4, TB])
        gC = sbl("gC", [64, 4, 2, NCH])
        VT = [[sb(es, f"VT{p}{c}", [64, 512], BF16) for c in range(NCH)] for p in range(2)]
        BKT = [[sb(es, f"BKT{p}{c}", [64, 1024], BF16) for c in range(NCH)] for p in range(2)]
        Aak = [[sb(es, f"Aak{p}{c}", [64, 512], BF16) for c in range(NCH)] for p in range(2)]
        Arb = [[sb(es, f"Arb{p}{c}", [64, 512], BF16) for c in range(NCH)] for p in range(2)]
        Ark = [[sb(es, f"Ark{p}{c}", [64, 512], BF16) for c in range(NCH)] for p in range(2)]
        Minv = [[sb(es, f"Minv{p}{c}", [64, 512], BF16) for c in range(NCH)] for p in range(2)]
        Nsb = [sb(es, f"Nsb{c}", [64, 512], BF16) for c in range(NCH)]
        NTsb = [sb(es, f"NTsb{c}", [64, 512], BF16) for c in range(NCH)]
        Xa0 = [sb(es, f"Xa0{c}", [64, 512], BF16) for c in range(NCH)]
        XTa0 = [sb(es, f"XTa0{c}", [64, 512], BF16) for c in range(NCH)]
        Qtmp = [sb(es, f"Qtmp{c}", [64, 512], BF16) for c in range(NCH)]
        ST = sb(es, "ST", [64, 8, 64]); STb = sb(es, "STb", [64, 8, 64], BF16)
        Wsb = sb(es, "Wsb", [64, 512], BF16); Usb = sb(es, "Usb", [64, 512], BF16)
        yc = sb(es, "yc", [64, 512]); ysq = sb(es, "ysq", [64, 512])
        STt = T(ysq[:].rearrange("p (h v) -> p h v", v=64), "STt", buf=ysq.b)
        m8 = sb(es, "m8", [64, 8]); v8 = sb(es, "v8", [64, 8]); r8 = sb(es, "r8", [64, 8]); t8 = sb(es, "t8", [64, 8])
        o1 = sb(es, "o1", [128, 4, 64])
        oT = sbl("oT", [128, 8, TB], BF16)
        ssum = sb(es, "ssum", [128, 1]); tmp1 = sb(es, "tmp1", [128, 1]); rstd = sb(es, "rstd", [128, 1])
        ppT = T(ysq[0:16, :], "ppT", buf=ysq.b); m13 = sb(es, "m13", [13, 128])
        SvT = T(yc[:].rearrange("p (h k) -> p h k", k=64), "SvT", buf=yc.b)

        memset(halo[:], 0.0, writes=[halo])
        memset(uext[:, :, 0:15], 0.0, writes=[uext])
        memset(ST[:], 0.0, writes=[ST])
        memset(STb[:], 0.0, writes=[STb])
        vts(omu[:], pvec[:, PV_MU:PV_MU + 13], -1.0, 1.0, ALU.mult, ALU.add, reads=[pvec], writes=[omu])

        def b8(ap):
            return ap.unsqueeze(1).to_broadcast([64, 8, 64])

        def h3(ap):
            return ap.rearrange("p (h v) -> p h v", v=64)

        def hc(h):
            return slice(h * 64, (h + 1) * 64)

        maskUs = b8(cst[0:64, C_MUS:C_MUS + 64])
        maskUi = b8(cst[0:64, C_MUI:C_MUI + 64])
        maskLs = b8(cst[0:64, C_MLS:C_MLS + 64])
        ident8 = b8(cst[0:64, C_ID:C_ID + 64])
        rstm = cst[:, C_RST:C_RST + 512]
        st = dict(gk=0, ak=0)
        pT32 = T(pT[:].bitcast(F32), "pT32", buf=pT.b)
        abanks = [pA, pB, pg[0], pg[1], pM, pT32]

        def nextbank():
            b = abanks[st["ak"] % len(abanks)]
            st["ak"] += 1
            return b

        def front(tb):
            pb = tb % 2
            t0 = tb * TB
            x_t = xt[pb]
            S.dma("sync", x_t[:], xp[t0:t0 + TB, :], writes=[x_t])
            act(sqx[:], x_t[:], AF.Square, reads=[x_t], writes=[sqx])
            vred(ssx[:], sqx[:], reads=[sqx], writes=[ssx])
            rsqrt_small(rsx[:], ssx[:], t1x[:], 1.0 / D, NORM_EPS, reads=[ssx], writes=[t1x, rsx])
            act(xnb[:], x_t[:], AF.Copy, reads=[x_t, rsx], writes=[xnb], scale=rsx[:, 0:1])
            yield
            for dc in range(8):
                tr(pT[:, dc * 128:(dc + 1) * 128], xnb[:, dc * 128:(dc + 1) * 128], identb[:], reads=[xnb, identb], writes=[pT])
            vcopy(hT[:].rearrange("p c t -> p (c t)"), pT[:], reads=[pT], writes=[hT])
            yield

            def gemm_group(ebs):
                bank = pg[st["gk"] % 2]
                st["gk"] += 1
                for i, eb in enumerate(ebs):
                    for dc in range(8):
                        mm(bank[:, i * TB:(i + 1) * TB], winb[:, dc, eb * 128:(eb + 1) * 128], hT[:, dc, :], dc == 0, dc == 7,
                           reads=[winb, hT], writes=[bank])
                return bank

            for gi, ebs in enumerate([[0, 1, 2, 3], [4, 5, 6, 7], [8, 9, 10, 11], [12]]):
                bank = gemm_group(ebs)
                yield
                n = len(ebs)
                pr = praw[gi % 2]
                qs = qsc[gi % 2]
                e0 = ebs[0]
                vcopy(pr[:, 0:n, 0:1], halo[:, e0:e0 + n].unsqueeze(2), reads=[halo], writes=[pr], eng="gpsimd")
                act(pr[:, 0:n, 1:TB + 1], bank[:, 0:n * TB].rearrange("p (e t) -> p e t", t=TB), AF.Copy, reads=[bank], writes=[pr])
                for i, eb in enumerate(ebs):
                    act(qs[:, i, :], bank[:, i * TB:(i + 1) * TB], AF.Copy, reads=[bank, omu], writes=[qs], scale=omu[:, eb:eb + 1])
                for i, eb in enumerate(ebs):
                    vstt(psx[eb][:], pr[:, i, 0:TB], pvec[:, PV_MU + eb:PV_MU + eb + 1], qs[:, i, :], ALU.mult, ALU.add,
                         reads=[pr, pvec, qs], writes=[psx[eb]])
                vcopy(halo[:, e0:e0 + n].unsqueeze(2), pr[:, 0:n, TB:TB + 1], reads=[pr], writes=[halo], eng="gpsimd")
                yield
            bank = gemm_group([13, 14, 15, 16])
            act(gsil[pb][:].rearrange("p f t -> p (f t)"), bank[:, :], AF.Silu, reads=[bank], writes=[gsil[pb]])
            yield
            bank = gemm_group([17, 18, 19, 20])
            act(uext[:, :, 15:15 + TB], bank[:, :].rearrange("p (g t) -> p g t", t=TB), AF.Copy, reads=[bank], writes=[uext])
            yield
            bank = gemm_group([21, 22, 23, 24])
            act(gpsil[:].rearrange("p g t -> p (g t)"), bank[:, :], AF.Silu, reads=[bank], writes=[gpsil])
            yield

            act(th[:], psx[12][0:64, :], AF.Tanh, reads=[psx[12]], writes=[th])
            for fb in range(4):
                mm(pA[:, fb * TB:(fb + 1) * TB], wd[:, fb * 128:(fb + 1) * 128], th[:], True, True, reads=[wd, th], writes=[pA])
            for fb in range(4):
                mm(pM[:, fb * TB:(fb + 1) * TB], wa[64:128, fb * 128:(fb + 1) * 128], psx[12][64:128, :], True, True, reads=[wa, psx[12]], writes=[pM])
            for fb in range(4):
                act(sg[:, fb, :], pA[:, fb * TB:(fb + 1) * TB], AF.Sigmoid, reads=[pA, pvec], writes=[sg], bias=pvec[:, PV_W0 + fb:PV_W0 + fb + 1])
                act(av[:, fb, :], pM[:, fb * TB:(fb + 1) * TB], AF.Sigmoid, reads=[pM, pvec], writes=[av], bias=pvec[:, PV_A0 + fb:PV_A0 + fb + 1])
            yield

            def pb4(col):
                return pvec[:, col:col + 4].unsqueeze(2).to_broadcast([128, 4, TB])

            def f2(t_):
                return t_[:].rearrange("p f t -> p (f t)")

            vcopy(vb[:], psv[:], reads=[psv], writes=[vb], eng="gpsimd")
            vtt(w1[:], psk[:], pb4(PV_KK), ALU.mult, reads=[psk, pvec], writes=[w1])
            vtt(w2[:], w1[:], w1[:], ALU.mult, reads=[w1], writes=[w2], eng="gpsimd")
            mm(pM[:, :], onesblk, f2(w2), True, True, reads=[cst, w2], writes=[pM])
            vts(f2(w2), pM[:, :], L2_EPS, None, ALU.add, None, reads=[pM], writes=[w2])
            act(w2[:], w2[:], AF.Ln, reads=[w2], writes=[w2])
            act(w2[:], w2[:], AF.Exp, reads=[w2], writes=[w2], scale=-0.5)
            vstt(kkn[:], w1[:], -1.0, w2[:], ALU.mult, ALU.mult, reads=[w1, w2], writes=[kkn])
            yield
            vtt(w1[:], av[:], pb4(PV_KA), ALU.mult, reads=[av, pvec], writes=[w1], eng="gpsimd")
            vtt(w1[:], w1[:], omka[:, 0:4].unsqueeze(2).to_broadcast([128, 4, TB]), ALU.add, reads=[w1, omka], writes=[w1], eng="gpsimd")
            vtt(kmod[:], psk[:], w1[:], ALU.mult, reads=[psk, w1], writes=[kmod])
            vstt(bv_[:], kkn[:], -1.0, av[:], ALU.mult, ALU.mult, reads=[kkn, av], writes=[bv_])
            vtt(w1[:], psr[:], pb4(PV_RK), ALU.mult, reads=[psr, pvec], writes=[w1], eng="gpsimd")
            vtt(w1[:], w1[:], kmod[:], ALU.mult, reads=[w1, kmod], writes=[w1])
            mm(pA[:, :], onesblk, f2(w1), True, True, reads=[cst, w1], writes=[pA])
            vtt(f2(bon[pb]), pA[:, :], f2(psv), ALU.mult, reads=[pA, psv], writes=[bon[pb]])
            yield
            S.op("vector", lambda e: e.tensor_tensor_scan(out=f2(cum), data0=rstm, data1=f2(sg), initial=0.0, op0=ALU.mult, op1=ALU.add),
                 reads=[cst, sg], writes=[cum], cost=1.2)
            c3 = cum[:].rearrange("p f (c t) -> p (f c) t", t=CH)
            vtt(w1[:], cum[:], sg[:], ALU.subtract, reads=[cum, sg], writes=[w1], eng="gpsimd")
            act(w2[:], cum[:], AF.Exp, reads=[cum], writes=[w2], scale=-C0)
            act(w3[:], cum[:], AF.Exp, reads=[cum], writes=[w3], scale=C0)
            act(w1[:], w1[:], AF.Exp, reads=[w1], writes=[w1], scale=-C0)
            vtt(w4[:].rearrange("p f (c t) -> p (f c) t", t=CH), c3[:, :, CH - 1:CH].to_broadcast([128, 4 * NCH, CH]), c3, ALU.subtract,
                reads=[cum], writes=[w4], eng="gpsimd")
            act(w4[:], w4[:], AF.Exp, reads=[w4], writes=[w4], scale=-C0)
            yield
            for j in range(2):
                pp = slice(64 * j, 64 * j + 64)
                e_ = "vector" if j == 0 else "gpsimd"
                vtt(rt[pb][:, :, j, :], psr[pp, :, :], w2[pp, :, :], ALU.mult, reads=[psr, w2], writes=[rt[pb]], eng=e_)
                vtt(bt[:, :, j, :], bv_[pp, :, :], w3[pp, :, :], ALU.mult, reads=[bv_, w3], writes=[bt], eng=e_)
                vtt(kt[:, :, j, :], kmod[pp, :, :], w3[pp, :, :], ALU.mult, reads=[kmod, w3], writes=[kt], eng=e_)
                vtt(at[pb][:, :, j, :], kkn[pp, :, :], w1[pp, :, :], ALU.mult, reads=[kkn, w1], writes=[at[pb]], eng=e_)
                act(gC[pb][:, :, j, :], cum[pp, :, :].rearrange("p f (c t) -> p f c t", t=CH)[:, :, :, CH - 1], AF.Exp,
                    reads=[cum], writes=[gC[pb]], scale=-C0)
            vtt(bh[:], bv_[:], w4[:], ALU.mult, reads=[bv_, w4], writes=[bh])
            vtt(kh[:], kmod[:], w4[:], ALU.mult, reads=[kmod, w4], writes=[kh], eng="gpsimd")
            yield

            L = 15 + TB
            for g in range(4):
                vtt(srot[0][:, 1:], uext[:, g, 1:], uext[:, g, 0:L - 1], ALU.add, reads=[uext], writes=[srot[0]], eng="gpsimd")
                tot = srot[0]
                if g >= 1:
                    vtt(srot[1][:, 3:], srot[0][:, 3:], srot[0][:, 1:L - 2], ALU.add, reads=[srot[0]], writes=[srot[1]], eng="gpsimd")
                    tot = srot[1]
                if g >= 2:
                    vtt(srot[2][:, 7:], srot[1][:, 7:], srot[1][:, 3:L - 4], ALU.add, reads=[srot[1]], writes=[srot[2]], eng="gpsimd")
                    tot = srot[2]
                if g >= 3:
                    vtt(srot[3][:, 15:], srot[2][:, 15:], srot[2][:, 7:L - 8], ALU.add, reads=[srot[2]], writes=[srot[3]], eng="gpsimd")
                    tot = srot[3]
                vstt(dpl[:], tot[:, 15:], 1.0 / WINS[g], uext[:, g, 15:], ALU.mult, ALU.subtract, reads=[tot, uext], writes=[dpl])
                if tb == 0:
                    vtt(dpl[:, 0:16], tot[:, 15:31], cst[:, C_ICNT + g * 16:C_ICNT + (g + 1) * 16], ALU.mult, reads=[tot, cst], writes=[dpl])
                    vtt(dpl[:, 0:16], dpl[:, 0:16], uext[:, g, 15:31], ALU.subtract, reads=[dpl, uext], writes=[dpl])
                mm(pM[:, 0:TB], pw[:, g, :], dpl[:], True, True, reads=[pw, dpl], writes=[pM])
                vstt(oT[pb][:, 4 + g, :], pM[:, 0:TB], pvec[:, PV_PS + g:PV_PS + g + 1], gpsil[:, g, :], ALU.mult, ALU.mult,
                     reads=[pM, pvec, gpsil], writes=[oT[pb]])
                yield
            if tb == NTB - 1:
                for g in range(4):
                    tr(pA[0:16, g * 128:(g + 1) * 128], uext[:, g, TB - 1:TB + 15], ident, reads=[uext, cst], writes=[pA])
                vcopy(ppT[:], pA[0:16, :], reads=[pA], writes=[ppT])
                S.dma("sync", npp[:], ppT[1:16, :], reads=[ppT], writes=[npp])
                tr(pB[0:13, 0:128], halo[:, 0:13], ident, reads=[halo, cst], writes=[pB])
                vcopy(m13[:], pB[0:13, 0:128], reads=[pB], writes=[m13])
                S.dma("sync", nsp[:], m13[:], reads=[m13], writes=[nsp])
            vcopy(uext[:, :, 0:15], uext[:, :, TB:TB + 15], reads=[uext], writes=[uext], eng="gpsimd")
            yield

            css = [slice(c * CH, (c + 1) * CH) for c in range(NCH)]
            for c in range(NCH):
                for qi, srcl in enumerate([bh, kh]):
                    for fb in range(4):
                        tr(pT[0:64, qi * 512 + fb * 128:qi * 512 + (fb + 1) * 128], srcl[:, fb, css[c]], identb[:], reads=[srcl, identb], writes=[pT])
                vcopy(BKT[pb][c][:], pT[0:64, :], reads=[pT], writes=[BKT[pb][c]])
                for fb in range(4):
                    tr(pT[0:64, fb * 128:(fb + 1) * 128], vb[:, fb, css[c]], identb[:], reads=[vb, identb], writes=[pT])
                act(VT[pb][c][:], pT[0:64, 0:512], AF.Copy, reads=[pT], writes=[VT[pb][c]])
                yield

            def hsl(tl, h, c):
                fb, j = divmod(h, 2)
                return tl[:, fb, j, css[c]]

            for (Lt, Rt, mask, dsts) in [(bt, at[pb], maskUs, Nsb), (at[pb], bt, maskLs, NTsb), (kt, at[pb], maskUs, Aak[pb]),
                                         (bt, rt[pb], maskUi, Arb[pb]), (kt, rt[pb], maskUi, Ark[pb])]:
                banks = []
                for c in range(NCH):
                    bank = nextbank()
                    banks.append(bank)
                    for h in range(8):
                        mm(bank[0:64, hc(h)], hsl(Lt, h, c), hsl(Rt, h, c), True, True, reads=[Lt, Rt], writes=[bank])
                for c in range(NCH):
                    vtt(h3(dsts[c][:]), h3(banks[c][0:64, :]), mask, ALU.mult, reads=[banks[c], cst], writes=[dsts[c]])
                yield
            X = list(Nsb); XT = list(NTsb)
            Q = [Qtmp[c] for c in range(NCH)]
            for c in range(NCH):
                vtt(h3(Q[c][:]), h3(Nsb[c][:]), ident8, ALU.add, reads=[Nsb[c], cst], writes=[Q[c]])
            for lvl in range(5):
                Xn = [(Xa0[c] if lvl % 2 == 0 else Nsb[c]) for c in range(NCH)]
                XTn = [(XTa0[c] if lvl % 2 == 0 else NTsb[c]) for c in range(NCH)]
                Qn = [(Minv[pb][c] if lvl % 2 == 0 else Qtmp[c]) for c in range(NCH)]
                banks = []
                for c in range(NCH):
                    bank = nextbank(); banks.append(bank)
                    for h in range(8):
                        mm(bank[0:64, hc(h)], X[c][:, hc(h)], XT[c][:, hc(h)], True, True, reads=[X[c], XT[c]], writes=[bank])
                for c in range(NCH):
                    act(XTn[c][:], banks[c][0:64, :], AF.Copy, reads=[banks[c]], writes=[XTn[c]])
                yield
                if lvl < 4:
                    banks = []
                    for c in range(NCH):
                        bank = nextbank(); banks.append(bank)
                        for h in range(8):
                            mm(bank[0:64, hc(h)], XT[c][:, hc(h)], X[c][:, hc(h)], True, True, reads=[X[c], XT[c]], writes=[bank])
                    for c in range(NCH):
                        act(Xn[c][:], banks[c][0:64, :], AF.Copy, reads=[banks[c]], writes=[Xn[c]])
                    yield
                banks = []
                for c in range(NCH):
                    bank = nextbank(); banks.append(bank)
                    for h in range(8):
                        mm(bank[0:64, hc(h)], XTn[c][:, hc(h)], Q[c][:, hc(h)], True, True, reads=[XTn[c], Q[c]], writes=[bank])
                for c in range(NCH):
                    vtt(Qn[c][:], banks[c][0:64, :], Q[c][:], ALU.add, reads=[banks[c], Q[c]], writes=[Qn[c]])
                X, XT, Q = Xn, XTn, Qn
                yield

        def chain(tb):
            pb = tb % 2
            t0 = tb * TB
            for c in range(NCH):
                cs = slice(c * CH, (c + 1) * CH)
                aT, rT = at[pb], rt[pb]
                VTc, BKTc, Aakc, Arbc, Arkc, Minvc = VT[pb][c], BKT[pb][c], Aak[pb][c], Arb[pb][c], Ark[pb][c], Minv[pb][c]
                for h in range(8):
                    fb, j = divmod(h, 2)
                    mm(pC[0:64, hc(h)], aT[:, fb, j, cs], STb[:, h, :], True, False, reads=[aT, STb], writes=[pC])
                    mm(pC[0:64, hc(h)], Aakc[:, hc(h)], VTc[:, hc(h)], False, True, reads=[Aakc, VTc], writes=[pC])
                act(Wsb[:], pC[0:64, :], AF.Copy, reads=[pC], writes=[Wsb])
                yield
                for h in range(8):
                    mm(pC[0:64, hc(h)], Minvc[:, hc(h)], Wsb[:, hc(h)], True, True, reads=[Minvc, Wsb], writes=[pC])
                act(Usb[:], pC[0:64, :], AF.Copy, reads=[pC], writes=[Usb])
                yield
                for h in range(8):
                    mm(pC[0:64, hc(h)], BKTc[:, hc(h)], Usb[:, hc(h)], True, False, reads=[BKTc, Usb], writes=[pC])
                    mm(pC[0:64, hc(h)], BKTc[:, 512 + h * 64:512 + (h + 1) * 64], VTc[:, hc(h)], False, True, reads=[BKTc, VTc], writes=[pC])
                for h in range(8):
                    fb, j = divmod(h, 2)
                    mm(pD[0:64, hc(h)], rT[:, fb, j, cs], STb[:, h, :], True, False, reads=[rT, STb], writes=[pD])
                    mm(pD[0:64, hc(h)], Arbc[:, hc(h)], Usb[:, hc(h)], False, False, reads=[Arbc, Usb], writes=[pD])
                    mm(pD[0:64, hc(h)], Arkc[:, hc(h)], VTc[:, hc(h)], False, True, reads=[Arkc, VTc], writes=[pD])
                vtt(STt[:], ST[:], gC[pb][:].rearrange("p f j c -> p (f j) c")[:, :, c:c + 1].to_broadcast([64, 8, 64]), ALU.mult,
                    reads=[ST, gC[pb]], writes=[STt])
                vtt(ST[:], STt[:], h3(pC[0:64, :]), ALU.add, reads=[STt, pC], writes=[ST])
                act(STb[:], ST[:], AF.Copy, reads=[ST], writes=[STb])
                yield
                y3 = h3(pD[0:64, :])
                vred(m8[:], y3, reads=[pD], writes=[m8])
                vts(m8[:], m8[:], 1.0 / 64, None, ALU.mult, None, reads=[m8], writes=[m8])
                vtt(h3(yc[:]), y3, m8[:].unsqueeze(2).to_broadcast([64, 8, 64]), ALU.subtract, reads=[pD, m8], writes=[yc])
                act(ysq[:], yc[:], AF.Square, reads=[yc], writes=[ysq])
                vred(v8[:], h3(ysq[:]), reads=[ysq], writes=[v8])
                rsqrt_small(r8[:], v8[:], t8[:], 1.0 / 64, GN_EPS, reads=[v8], writes=[t8, r8])
                vtt(h3(yc[:]), h3(yc[:]), r8[:].unsqueeze(2).to_broadcast([64, 8, 64]), ALU.mult, reads=[yc, r8], writes=[yc], eng="gpsimd")
                yield
                for fb in range(4):
                    tr(pD[:, fb * 64:(fb + 1) * 64], yc[:, fb * 128:(fb + 1) * 128], ident[0:64, 0:64], reads=[yc, cst], writes=[pD])
                for fb in range(4):
                    vts(o1[:, fb, :], pD[:, fb * 64:(fb + 1) * 64], pvec[:, PV_GW + fb:PV_GW + fb + 1], pvec[:, PV_GB + fb:PV_GB + fb + 1],
                        ALU.mult, ALU.add, reads=[pD, pvec], writes=[o1])
                vtt(o1[:], o1[:], bon[pb][:, :, cs], ALU.add, reads=[o1, bon[pb]], writes=[o1], eng="gpsimd")
                vtt(oT[pb][:, 0:4, cs], o1[:], gsil[pb][:, :, cs], ALU.mult, reads=[o1, gsil[pb]], writes=[oT[pb]])
                yield
            x_t = xt[pb]
            for half in range(2):
                bank = pD if half == 0 else pC
                for fc in range(8):
                    mm(bank[:, :], oT[pb][:, fc, :], woutb[:, fc, half * 512:(half + 1) * 512], fc == 0, fc == 7, reads=[oT[pb], woutb], writes=[bank])
                vtt(x_t[:, half * 512:(half + 1) * 512], bank[:, :], x_t[:, half * 512:(half + 1) * 512], ALU.add, reads=[bank, x_t], writes=[x_t])
                yield
            act(yo[:], x_t[:], AF.Square, reads=[x_t], writes=[yo])
            vred(ssum[:], yo[:], reads=[yo], writes=[ssum])
            rsqrt_small(rstd[:], ssum[:], tmp1[:], 1.0 / D, NORM_EPS, reads=[ssum], writes=[tmp1, rstd])
            vstt(yo[:], x_t[:], rstd[:, 0:1], normf[:], ALU.mult, ALU.mult, reads=[x_t, rstd, normf], writes=[yo])
            S.dma("sync", yp[t0:t0 + TB, :], yo[:], reads=[yo], writes=[yp])
            yield

        def run_all(g):
            n = 0
            for _ in g:
                n += 1
            return n

        def interleave(ga, na, gb, nb):
            ia = ib = 0
            da = db = False
            while not (da and db):
                pick_a = (not da) and (db or (ia * nb <= ib * na))
                if pick_a:
                    try:
                        next(ga); ia += 1
                    except StopIteration:
                        da = True
                else:
                    try:
                        next(gb); ib += 1
                    except StopIteration:
                        db = True
            return ia, ib

        run_all(front(0))

        def record_units(g):
            units = []
            S.rec = []
            for _ in g:
                if S.rec:
                    units.append(S.rec)
                S.rec = []
            if S.rec:
                units.append(S.rec)
            S.rec = None
            return units

        A, B = [], []
        for tb in range(NTB):
            A.append(record_units(chain(tb)))
            if tb + 1 < NTB:
                B.append(record_units(front(tb + 1)))
        S.merge_emit(A, B, a_ok=lambda ia, ib: ib >= ia, b_ok=lambda ib, ia: ia >= ib)
        for h in range(8):
            tr(pA[0:64, h * 64:(h + 1) * 64], ST[:, h, :], ident[0:64, 0:64], reads=[ST, cst], writes=[pA])
        vcopy(SvT[:].rearrange("p h k -> p (h k)"), pA[0:64, :], reads=[pA], writes=[SvT])
        S.dma("sync", nwp[:].rearrange("h v k -> v h k"), SvT[:], reads=[SvT], writes=[nwp])
        S.finish([yp, ys, nsp, nwp, npp, nss, nws, nps], engname="sync")
        S.barrier()
    es_top.close()
    return nc, S


_CACHE = {}


def _consts():
    cst = np.zeros((128, C_END), np.float32)
    cst[:, C_ID:C_ID + 128] = np.eye(128, dtype=np.float32)
    ob = np.zeros((128, 128), np.float32)
    ob[0:64, 0:64] = 1.0
    ob[64:128, 64:128] = 1.0
    cst[:, C_ONES:C_ONES + 128] = ob
    s = np.arange(64)[:, None]
    t = np.arange(64)[None, :]
    mus = (s < t).astype(np.float32)
    mui = (s <= t).astype(np.float32)
    mls = (s > t).astype(np.float32)
    i64 = np.eye(64, dtype=np.float32)
    cst[0:64, C_MUS:C_MUS + 64] = mus
    cst[0:64, C_MUI:C_MUI + 64] = mui
    cst[0:64, C_MLS:C_MLS + 64] = mls
    rst = np.ones((512,), np.float32)
    rst[::CH] = 0.0
    cst[:, C_RST:C_RST + 512] = rst[None, :]
    for g, w in enumerate(WINS):
        pos = np.arange(16)
        cst[:, C_ICNT + g * 16:C_ICNT + (g + 1) * 16] = (1.0 / np.minimum(pos + 1, w)).astype(np.float32)[None, :]
    return cst


def kernel(x_prompt, x_sample, state_shift, state_wkv, state_pool, norm_w, w_in, mu_shift,
           w_decay_b, w0, w_aaa_b, a0, k_k, k_a, r_k, gn_w, gn_b, pool_w, pool_scale, w_out, norm_f):
    f = lambda a: np.ascontiguousarray(np.asarray(a, dtype=np.float32))
    x_prompt, x_sample, state_shift, state_wkv, state_pool = map(f, (x_prompt, x_sample, state_shift, state_wkv, state_pool))
    if "nc" not in _CACHE:
        _CACHE["nc"] = build_program()
    nc, S = _CACHE["nc"]

    def colmajor(v, n):
        return f(v).reshape(n, 128).T

    pvec = np.concatenate([
        colmajor(norm_w[0], 8), colmajor(mu_shift[0], 13), colmajor(w0[0], 4), colmajor(a0[0], 4), colmajor(k_k[0], 4),
        colmajor(k_a[0], 4), colmajor(f(r_k[0]).reshape(-1), 4), colmajor(gn_w[0], 4), colmajor(gn_b[0], 4), colmajor(pool_scale[0], 4)], axis=1)
    pvec = f(pvec)
    browA = f(f(mu_shift[0])[None, :])
    browB = f(np.concatenate([f(w0[0]), f(a0[0]), f(k_k[0]), f(k_a[0]), f(r_k[0]).reshape(-1), f(gn_w[0]), f(gn_b[0])])[None, :])
    cst = _consts()
    shared = {
        "w_in": f(w_in[0]), "w_out": f(w_out[0]), "wdec": f(w_decay_b[0]), "waaa": f(w_aaa_b[0]), "poolw": f(pool_w[0]),
        "pvec": pvec, "browA": browA, "browB": browB, "normf": f(norm_f)[None, :], "cst": cst,
    }
    in_maps = []
    for c in range(NCORE):
        bs = slice(c * DB, (c + 1) * DB)
        m = dict(shared)
        m["xp"] = x_prompt[c]
        m["xs"] = f(x_sample[bs].transpose(1, 0, 2).reshape(NS, D))
        m["sshift"] = state_shift[0, bs]
        m["swkv"] = f(state_wkv[0, bs].reshape(128, 4096))
        m["spool"] = f(state_pool[0, bs].reshape(DB * 15, 512))
        in_maps.append(m)
    res = run_bass_kernel_spmd(nc, in_maps, core_ids=list(range(NCORE)))
    R = res.results
    y_prompt = np.stack([R[c]["yp"] for c in range(NCORE)], axis=0)
    y_sample = np.concatenate([R[c]["ys"].reshape(DT, DB, D).transpose(1, 0, 2) for c in range(NCORE)], axis=0)
    nsp = np.stack([R[c]["nsp"].reshape(D_SHIFT) for c in range(NCORE)], axis=0)[None]
    nwp = np.stack([R[c]["nwp"] for c in range(NCORE)], axis=0)[None]
    npp = np.stack([R[c]["npp"] for c in range(NCORE)], axis=0)[None]
    nss = np.concatenate([R[c]["nss"] for c in range(NCORE)], axis=0)[None]
    nws = np.concatenate([R[c]["nws"].reshape(DB, 8, 64, 64) for c in range(NCORE)], axis=0)[None]
    nps = np.concatenate([R[c]["nps"] for c in range(NCORE)], axis=0)[None]
    out = (y_prompt, y_sample, nsp, nwp, npp, nss, nws, nps)
    return tuple(np.ascontiguousarray(o.astype(np.float32)) for o in out)
```

```python
import numpy as np
from contextlib import ExitStack
import concourse.bass as bass
import concourse.mybir as mybir
from concourse.bass_utils import run_bass_kernel_spmd

F32 = mybir.dt.float32
BF16 = mybir.dt.bfloat16
AF = mybir.ActivationFunctionType
ALU = mybir.AluOpType
AX = mybir.AxisListType

D = 1024
SEQ = 2048
NCORE = 8
DB = 16
DT = 4
NS = DB * DT
D_SHIFT = 1664
D_IN = 3200
C0 = float(np.exp(-0.5))
NORM_EPS = 1e-6
GN_EPS = 64e-5
L2_EPS = 1e-12
TB = 128
NTB = SEQ // TB
CH = 64
FBIAS = 0.0
NCH = TB // CH
WINS = (2, 4, 8, 16)

C_ID, C_ONES, C_MUS, C_MUI, C_MLS, C_RST, C_ICNT, C_END = 0, 128, 256, 320, 384, 448, 960, 1024
PV_NW, PV_MU, PV_W0, PV_A0, PV_KK, PV_KA, PV_RK, PV_GW, PV_GB, PV_PS, PV_END = 0, 8, 21, 25, 29, 33, 37, 41, 45, 49, 53
BRB_W0, BRB_A0, BRB_KK, BRB_KA, BRB_RK, BRB_GW, BRB_GB = 0, 512, 1024, 1536, 2048, 2560, 3072


class Buf:
    __slots__ = ("name", "w", "r")

    def __init__(self, name):
        self.name = name
        self.w = None
        self.r = []


class T:
    def __init__(self, t, name, buf=None):
        self.t = t
        self.b = buf if buf is not None else Buf(name)

    def __getitem__(self, k):
        return self.t[k]


class Sched:
    def __init__(self, nc, n_dma_sems=32):
        self.nc = nc
        self.eng = {}
        for name in ["tensor", "vector", "scalar", "gpsimd", "sync"]:
            h = getattr(nc, name)
            sem = nc.alloc_semaphore(name="prog_" + name)
            self.eng[name] = dict(h=h, sem=sem, cnt=0, waited={})
        self.dma_sems = [dict(sem=nc.alloc_semaphore(name=f"dma{i}"), cnt=0) for i in range(n_dma_sems)]
        self.dma_rr = 0
        self.ninstr = 0
        self.rec = None

    def _wait(self, engname, tok):
        sem, val, src = tok
        e = self.eng[engname]
        key = id(sem)
        if e["waited"].get(key, 0) >= val:
            return
        e["h"].wait_ge(sem, val)
        e["waited"][key] = val
        self.ninstr += 1

    def _deps(self, engname, reads, writes):
        toks = []
        for b in reads:
            if b.w is not None:
                toks.append(b.w)
        for b in writes:
            if b.w is not None:
                toks.append(b.w)
            toks.extend(b.r)
        for tok in toks:
            if tok[2] == engname and engname == "tensor":
                continue
            self._wait(engname, tok)

    @staticmethod
    def _bufs(xs):
        return [x.b if isinstance(x, T) else x for x in xs]

    def _record(self, tok, reads, writes):
        for b in reads:
            b.r.append(tok)
            if len(b.r) > 64:
                b.r = b.r[-64:] if False else b.r
        for b in writes:
            b.w = tok
            b.r = []

    def op(self, engname, fn, reads=(), writes=(), cost=0.3):
        reads = self._bufs(reads)
        writes = self._bufs(writes)
        if self.rec is not None:
            self.rec.append(("op", engname, fn, reads, writes, cost, None))
            return None
        e = self.eng[engname]
        self._deps(engname, reads, writes)
        ins = fn(e["h"])
        e["cnt"] += 1
        ins.then_inc(e["sem"], 1)
        e["waited"][id(e["sem"])] = max(e["waited"].get(id(e["sem"]), 0), 0)
        tok = (e["sem"], e["cnt"], engname)
        self._record(tok, reads, writes)
        self.ninstr += 1
        return tok

    def dma(self, qname, out, in_, reads=(), writes=(), **kw):
        reads = self._bufs(reads)
        writes = self._bufs(writes)
        if self.rec is not None:
            self.rec.append(("dma", qname, (out, in_), reads, writes, 2.5, kw))
            return None
        e = self.eng[qname]
        self._deps(qname, reads, writes)
        d = self.dma_sems[self.dma_rr]
        self.dma_rr = (self.dma_rr + 1) % len(self.dma_sems)
        if d["cnt"] > 0:
            self._wait(qname, (d["sem"], 16 * d["cnt"], "dma"))
        ins = e["h"].dma_start(out=out, in_=in_, **kw)
        d["cnt"] += 1
        ins.then_inc(d["sem"], 16)
        tok = (d["sem"], 16 * d["cnt"], "dma")
        self._record(tok, reads, writes)
        self.ninstr += 1
        return tok

    def emit(self, r):
        kind, eng, fn, reads, writes, cost, kw = r
        if kind == "op":
            self.op(eng, fn, reads=reads, writes=writes)
        else:
            self.dma(eng, fn[0], fn[1], reads=reads, writes=writes, **kw)

    def merge_emit(self, A, B, a_ok, b_ok):
        eng_free = {}
        ready = {}
        acc = {}

        def est(r):
            kind, eng, fn, reads, writes, cost, kw = r
            t = eng_free.get(eng, 0.0)
            for b in reads:
                rt_, re_ = ready.get(id(b), (0.0, eng))
                t = max(t, rt_ + (0.15 if re_ != eng else 0.0))
            for b in writes:
                rt_, re_ = ready.get(id(b), (0.0, eng))
                t = max(t, rt_ + (0.15 if re_ != eng else 0.0), acc.get(id(b), 0.0) + 0.1)
            return t

        def commit(r, t):
            kind, eng, fn, reads, writes, cost, kw = r
            if kind == "dma":
                eng_free[eng] = t + 0.1
                end = t + cost
            else:
                end = t + cost
                eng_free[eng] = end
            for b in reads:
                acc[id(b)] = max(acc.get(id(b), 0.0), end)
            for b in writes:
                ready[id(b)] = (end, eng)
                acc[id(b)] = max(acc.get(id(b), 0.0), end)

        def run_unit(u):
            for r in u:
                commit(r, est(r))
                self.emit(r)

        ia = ib = 0
        ja = jb = 0
        while ia < len(A) or ib < len(B):
            ca = None
            cb = None
            if ia < len(A) and (ja > 0 or a_ok(ia, ib)):
                ca = A[ia][ja]
            if ib < len(B) and (jb > 0 or b_ok(ib, ia)):
                cb = B[ib][jb]
            assert ca is not None or cb is not None, (ia, ib, ja, jb)
            ta = est(ca[0]) if ca is not None else None
            tb_ = est(cb[0]) if cb is not None else None
            if cb is None or (ca is not None and ta + FBIAS < tb_):
                run_unit(ca); ja += 1
                if ja == len(A[ia]):
                    ia += 1; ja = 0
            else:
                run_unit(cb); jb += 1
                if jb == len(B[ib]):
                    ib += 1; jb = 0

    def barrier(self):
        toks = [(e["sem"], e["cnt"], n) for n, e in self.eng.items() if e["cnt"] > 0]
        toks += [(d["sem"], 16 * d["cnt"], "dma") for d in self.dma_sems if d["cnt"] > 0]
        for n in self.eng:
            for tok in toks:
                if tok[2] == n:
                    continue
                self._wait(n, tok)

    def finish(self, tiles, engname="sync"):
        for b in self._bufs(tiles):
            if b.w is not None:
                self._wait(engname, b.w)


class _Stop(Exception):
    pass


def build_program(stop=None):
    nc = bass.Bass("TRN2", target_bir_lowering=False)
    S = Sched(nc)
    try:
        _build_body(nc, S, stop)
    except _Stop:
        S.barrier()
    return nc, S


def _build_body(nc, S, stop):
    def chk(label):
        if stop == label:
            raise _Stop()


    def din(name, shape):
        return nc.dram_tensor(name, list(shape), F32, kind="ExternalInput").ap()

    def dout(name, shape):
        return T(nc.dram_tensor(name, list(shape), F32, kind="ExternalOutput").ap(), name)

    xp = din("xp", [SEQ, D])
    xs = din("xs", [NS, D])
    sshift = din("sshift", [DB, D_SHIFT])
    swkv = din("swkv", [128, 4096])
    spool = din("spool", [DB * 15, 512])
    w_in = din("w_in", [D, D_IN])
    w_out = din("w_out", [D, D])
    wdec = din("wdec", [64, 512])
    waaa = din("waaa", [64, 512])
    poolw = din("poolw", [4, 128, 128])
    pvec_d = din("pvec", [128, PV_END])
    browA_d = din("browA", [1, D_SHIFT])
    browB_d = din("browB", [1, 3584])
    normf_d = din("normf", [1, D])
    cst_d = din("cst", [128, C_END])

    yp = dout("yp", [SEQ, D])
    ys = dout("ys", [NS, D])
    nsp = dout("nsp", [13, 128])
    nwp = dout("nwp", [8, 64, 64])
    npp = dout("npp", [15, 512])
    nss = dout("nss", [DB, D_SHIFT])
    nws = dout("nws", [128, 4096])
    nps = dout("nps", [DB, 15, 512])
    scr1 = T(nc.dram_tensor("scr1", [6, DT, DB, 8, 64], F32, kind="Internal").ap(), "scr1")
    scr2 = T(nc.dram_tensor("scr2", [DB, 8, DT, 64], F32, kind="Internal").ap(), "scr2")

    es_top = ExitStack()

    def sb(es, name, shape, dt=F32):
        return T(es.enter_context(nc.sbuf_tensor("s_" + name, list(shape), dt)), name)

    def pst(name, shape, dt=F32):
        return T(nc.alloc_psum_tensor("p_" + name, list(shape), dt), name)

    def nel(ap):
        n = 1
        for s_ in ap.shape[1:]:
            n *= s_
        return n

    def mm(out, lhsT, rhs, start, stop, reads, writes):
        passes = 4 if lhsT.dtype == F32 else 1
        c_ = max(0.055, nel(rhs) * passes / 2000.0 + 0.03)
        S.op("tensor", lambda e: e.matmul(out, lhsT=lhsT, rhs=rhs, start=start, stop=stop), reads=reads, writes=writes, cost=c_)

    def tr(out, in_, ident, reads, writes):
        S.op("tensor", lambda e: e.transpose(out, in_, ident), reads=reads, writes=writes, cost=0.13)

    def act(out, in_, func, reads, writes, bias=None, scale=None, eng="scalar"):
        kw = {}
        if bias is not None:
            kw["bias"] = bias
        if scale is not None:
            kw["scale"] = scale
        S.op("scalar", lambda e: e.activation(out=out, in_=in_, func=func, **kw), reads=reads, writes=writes,
             cost=0.1 + 0.1 * len(kw) + nel(in_) * 0.00095)

    def ecost(eng, n):
        return 0.08 + n * (0.00105 if eng == "vector" else 0.0025)

    def vtt(out, in0, in1, op, reads, writes, eng="vector"):
        S.op(eng, lambda e: e.tensor_tensor(out=out, in0=in0, in1=in1, op=op), reads=reads, writes=writes, cost=ecost(eng, nel(out)))

    def vts(out, in0, s1, s2, op0, op1, reads, writes, eng="vector"):
        if op1 is None:
            S.op(eng, lambda e: e.tensor_scalar(out=out, in0=in0, scalar1=s1, scalar2=None, op0=op0), reads=reads, writes=writes,
                 cost=ecost(eng, nel(out)))
        else:
            S.op(eng, lambda e: e.tensor_scalar(out=out, in0=in0, scalar1=s1, scalar2=s2, op0=op0, op1=op1), reads=reads, writes=writes,
                 cost=ecost(eng, nel(out)))

    def vstt(out, in0, scalar, in1, op0, op1, reads, writes):
        S.op("vector", lambda e: e.scalar_tensor_tensor(out=out, in0=in0, scalar=scalar, in1=in1, op0=op0, op1=op1), reads=reads, writes=writes,
             cost=ecost("vector", nel(out)))

    def vcopy(out, in_, reads, writes, eng="vector"):
        S.op(eng, lambda e: e.tensor_copy(out=out, in_=in_), reads=reads, writes=writes, cost=ecost(eng, nel(out)))

    def vred(out, in_, reads, writes):
        S.op("vector", lambda e: e.tensor_reduce(out=out, in_=in_, axis=AX.X, op=ALU.add), reads=reads, writes=writes,
             cost=ecost("vector", nel(in_)))

    def vrecip(out, in_, reads, writes):
        S.op("vector", lambda e: e.reciprocal(out=out, in_=in_), reads=reads, writes=writes, cost=0.08 + nel(out) * 0.0084)

    def memset(ap, val, writes, eng="gpsimd"):
        S.op(eng, lambda e: e.memset(ap, val), writes=writes)

    def rsqrt_small(out, in_, tmp, scale, eps, reads, writes):
        act(tmp, in_, AF.Sqrt, reads=reads, writes=writes, bias=None, scale=None) if False else None
        vts(tmp, in_, scale, eps, ALU.mult, ALU.add, reads=reads, writes=writes)
        act(tmp, tmp, AF.Sqrt, reads=writes, writes=writes)
        vrecip(out, tmp, reads=writes, writes=writes)

    pg = [pst(f"pg{i}", [128, 512]) for i in range(2)]
    pT = pst("pT", [128, 1024], BF16)
    pM = pst("pM", [128, 512])
    pA = pst("pA", [128, 512])
    pB = pst("pB", [128, 512])
    pC = pst("pC", [128, 512])
    pD = pst("pD", [128, 512])

    cst = sb(es_top, "cst", [128, C_END])
    pvec = sb(es_top, "pvec", [128, PV_END])
    omu = sb(es_top, "omu", [128, 13])
    omka = sb(es_top, "omka", [128, 4])
    identb = sb(es_top, "identb", [128, 128], BF16)
    winb = sb(es_top, "winb", [128, 8, D_IN], BF16)
    woutb = sb(es_top, "woutb", [128, 8, D], BF16)
    wd = sb(es_top, "wd", [64, 512])
    wa = sb(es_top, "wa", [128, 512])
    pw = sb(es_top, "pw", [128, 4, 128])
    normf = sb(es_top, "normf", [128, D])

    ident = cst[:, C_ID:C_ID + 128]
    onesblk = cst[:, C_ONES:C_ONES + 128]

    S.dma("sync", cst[:], cst_d, writes=[cst])
    S.dma("sync", pvec[:], pvec_d, writes=[pvec])
    S.dma("sync", wd[:], wdec, writes=[wd])
    S.dma("sync", wa[64:128, :], waaa, writes=[wa])
    S.dma("sync", pw[:], poolw.rearrange("g c e -> c g e"), writes=[pw])
    S.dma("sync", normf[:], normf_d.partition_broadcast(128), writes=[normf])
    vcopy(identb[:], ident, reads=[cst], writes=[identb])
    vts(omka[:], pvec[:, PV_KA:PV_KA + 4], -1.0, 1.0, ALU.mult, ALU.add, reads=[pvec], writes=[omka])

    with ExitStack() as es:
        stg = [sb(es, f"stg{i}", [128, D_IN]) for i in range(3)]
        for dc in range(8):
            st = stg[dc % 3]
            S.dma("sync", st[:], w_in[dc * 128:(dc + 1) * 128, :], writes=[st])
            h = D_IN // 2
            vts(winb[:, dc, 0:h], st[:, 0:h], pvec[:, PV_NW + dc:PV_NW + dc + 1], None, ALU.mult, None, reads=[st, pvec], writes=[winb])
            act(winb[:, dc, h:], st[:, h:], AF.Copy, reads=[st, pvec], writes=[winb], scale=pvec[:, PV_NW + dc:PV_NW + dc + 1])
        S.barrier()
        chk("W")

    def final_tile(es_tiles, n, x_t, oT_list, out_dram_ap, out_T):
        res, sq, ssum, tmp1, rstd, yo = es_tiles
        for half in range(2):
            bank = pD if half == 0 else pC
            for fc in range(8):
                mm(bank[0:n, :], oT_list[fc], woutb[:, fc, half * 512:(half + 1) * 512], fc == 0, fc == 7,
                   reads=[oT_list_T, woutb], writes=[bank])
            vtt(res[0:n, half * 512:(half + 1) * 512], bank[0:n, :], x_t[0:n, half * 512:(half + 1) * 512], ALU.add,
                reads=[bank, x_t], writes=[res])
        act(sq[0:n, :], res[0:n, :], AF.Square, reads=[res], writes=[sq])
        vred(ssum[0:n, :], sq[0:n, :], reads=[sq], writes=[ssum])
        rsqrt_small(rstd[0:n, :], ssum[0:n, :], tmp1[0:n, :], 1.0 / D, NORM_EPS, reads=[ssum], writes=[tmp1, rstd])
        vstt(yo[0:n, :], res[0:n, :], rstd[0:n, 0:1], normf[0:n, :], ALU.mult, ALU.mult, reads=[res, rstd, normf], writes=[yo])
        S.dma("sync", out_dram_ap, yo[0:n, :], reads=[yo], writes=[out_T])

    oT_list_T = None

    with ExitStack() as es:
        browB = sb(es, "browB", [NS, 3584])
        S.dma("sync", browB[:], browB_d.partition_broadcast(NS), writes=[browB])
        x_s = sb(es, "x_s", [NS, D])
        S.dma("sync", x_s[:], xs, writes=[x_s])
        hTs = sb(es, "hTs", [128, 8, DB + NS], BF16)
        graw_s = sb(es, "graw_s", [NS, 512])
        u_s = sb(es, "u_s", [NS, 512])
        gp_s = sb(es, "gp_s", [NS, 512])
        bonus_s = sb(es, "bonus_s", [NS, 512])
        st8 = sb(es, "st8", [NS, 8])
        st8b = sb(es, "st8b", [NS, 8])
        st8c = sb(es, "st8c", [NS, 8])

        def v3(ap):
            return ap.rearrange("p (h k) -> p h k", k=64)

        def bc8(ap8):
            return ap8.unsqueeze(2).to_broadcast([NS, 8, 64])

        with ExitStack() as e1:
            browA = sb(e1, "browA", [NS, D_SHIFT])
            S.dma("sync", browA[:], browA_d.partition_broadcast(NS), writes=[browA])
            omka_b = sb(e1, "omka_b", [NS, 512])
            vts(omka_b[:], browB[:, BRB_KA:BRB_KA + 512], -1.0, 1.0, ALU.mult, ALU.add, reads=[browB], writes=[omka_b])
            sq_s = sb(e1, "sq_s", [NS, D])
            ss_s = sb(e1, "ss_s", [NS, 1])
            t1_s = sb(e1, "t1_s", [NS, 1])
            rstd_s = sb(e1, "rstd_s", [NS, 1])
            xn_s = sb(e1, "xn_s", [NS, D], BF16)
            act(sq_s[:], x_s[:], AF.Square, reads=[x_s], writes=[sq_s])
            vred(ss_s[:], sq_s[:], reads=[sq_s], writes=[ss_s])
            rsqrt_small(rstd_s[:], ss_s[:], t1_s[:], 1.0 / D, NORM_EPS, reads=[ss_s], writes=[t1_s, rstd_s])
            vts(xn_s[:], x_s[:], rstd_s[:, 0:1], None, ALU.mult, None, reads=[x_s, rstd_s], writes=[xn_s])
            memset(hTs[:, :, 0:DB], 0.0, writes=[hTs])
            for dc in range(8):
                tr(pT[:, dc * 128:dc * 128 + NS], xn_s[:, dc * 128:(dc + 1) * 128], identb[0:NS, 0:NS], reads=[xn_s, identb], writes=[pT])
            vcopy(hTs[:, :, DB:DB + NS], pT[:].rearrange("p (c t) -> p c t", t=128)[:, :, 0:NS], reads=[pT], writes=[hTs])

            p_s = sb(e1, "p_s", [NS, D_SHIFT])
            prev_s = sb(e1, "prev_s", [NS, D_SHIFT])
            col_chunks = [(0, 512), (512, 512), (1024, 512), (1536, 128)]
            kk_ = 0
            for (c0, n) in col_chunks:
                bank = pg[kk_ % 2]; kk_ += 1
                for dc in range(8):
                    mm(bank[0:NS, 0:n], hTs[:, dc, DB:DB + NS], winb[:, dc, c0:c0 + n], dc == 0, dc == 7, reads=[hTs, winb], writes=[bank])
                act(p_s[:, c0:c0 + n], bank[0:NS, 0:n], AF.Copy, reads=[bank], writes=[p_s])
                bank = pg[kk_ % 2]; kk_ += 1
                for dc in range(8):
                    mm(bank[0:NS, 0:n], hTs[:, dc, 0:NS], winb[:, dc, c0:c0 + n], dc == 0, dc == 7, reads=[hTs, winb], writes=[bank])
                vcopy(prev_s[:, c0:c0 + n], bank[0:NS, 0:n], reads=[bank], writes=[prev_s])
            for (c0, dst, fn) in [(1664, graw_s, AF.Silu), (2176, u_s, AF.Copy), (2688, gp_s, AF.Silu)]:
                bank = pg[kk_ % 2]; kk_ += 1
                for dc in range(8):
                    mm(bank[0:NS, :], hTs[:, dc, DB:DB + NS], winb[:, dc, c0:c0 + 512], dc == 0, dc == 7, reads=[hTs, winb], writes=[bank])
                act(dst[:], bank[0:NS, :], fn, reads=[bank], writes=[dst])
            S.dma("sync", prev_s[0:DB, :], sshift, writes=[prev_s])
            S.dma("sync", nss[:], p_s[NS - DB:NS, :], reads=[p_s], writes=[nss])
            S.dma("sync", nps[:, 0:11, :], spool.rearrange("(b j) c -> b j c", j=15)[:, 4:15, :], writes=[nps])
            for t in range(DT):
                S.dma("sync", nps[:, 11 + t, :], u_s[t * DB:(t + 1) * DB, :], reads=[u_s], writes=[nps])

            vtt(prev_s[:], prev_s[:], p_s[:], ALU.subtract, reads=[prev_s, p_s], writes=[prev_s])
            vtt(prev_s[:], prev_s[:], browA[:], ALU.mult, reads=[prev_s, browA], writes=[prev_s])
            vtt(prev_s[:], prev_s[:], p_s[:], ALU.add, reads=[prev_s, p_s], writes=[prev_s])
            ps_s = prev_s
            r_s = ps_s[:, 0:512]
            k_s = ps_s[:, 512:1024]
            v_s = ps_s[:, 1024:1536]

            lT = sb(e1, "lT", [128, NS])
            tr(pM[:, 0:NS], ps_s[:, 1536:1664], ident[0:NS, 0:NS], reads=[ps_s, cst], writes=[pM])
            act(lT[0:64, :], pM[0:64, 0:NS], AF.Tanh, reads=[pM], writes=[lT])
            act(lT[64:128, :], pM[64:128, 0:NS], AF.Copy, reads=[pM], writes=[lT])
            sg_s = sb(e1, "sg_s", [NS, 512])
            a_s = sb(e1, "a_s", [NS, 512])
            mm(pA[0:NS, :], lT[0:64, :], wd[:, :], True, True, reads=[lT, wd], writes=[pA])
            vtt(sg_s[:], pA[0:NS, :], browB[:, BRB_W0:BRB_W0 + 512], ALU.add, reads=[pA, browB], writes=[sg_s])
            act(sg_s[:], sg_s[:], AF.Sigmoid, reads=[sg_s], writes=[sg_s])
            mm(pB[0:NS, :], lT[64:128, :], wa[64:128, :], True, True, reads=[lT, wa], writes=[pB])
            vtt(a_s[:], pB[0:NS, :], browB[:, BRB_A0:BRB_A0 + 512], ALU.add, reads=[pB, browB], writes=[a_s])
            act(a_s[:], a_s[:], AF.Sigmoid, reads=[a_s], writes=[a_s])

            pk = sb(e1, "pk", [NS, 4, 512])
            PQ = {1: 0, 2: 1, 4: 2, 5: 3}
            tmpA = sb(e1, "tmpA", [NS, 512])
            tmpB = sb(e1, "tmpB", [NS, 512])
            act(pk[:, PQ[1], :], sg_s[:], AF.Exp, reads=[sg_s], writes=[pk], scale=-C0)
            vtt(tmpA[:], k_s, browB[:, BRB_KK:BRB_KK + 512], ALU.mult, reads=[ps_s, browB], writes=[tmpA])
            vtt(tmpB[:], tmpA[:], tmpA[:], ALU.mult, reads=[tmpA], writes=[tmpB])
            vred(st8[:], v3(tmpB[:]), reads=[tmpB], writes=[st8])
            rsqrt_small(st8b[:], st8[:], st8c[:], 1.0, L2_EPS, reads=[st8], writes=[st8c, st8b])
            vtt(v3(tmpA[:]), v3(tmpA[:]), bc8(st8b[:]), ALU.mult, reads=[tmpA, st8b], writes=[tmpA])
            vts(pk[:, PQ[4], :], tmpA[:], -1.0, None, ALU.mult, None, reads=[tmpA], writes=[pk])
            vtt(pk[:, PQ[5], :], tmpA[:], a_s[:], ALU.mult, reads=[tmpA, a_s], writes=[pk])
            vtt(tmpB[:], a_s[:], browB[:, BRB_KA:BRB_KA + 512], ALU.mult, reads=[a_s, browB], writes=[tmpB])
            vtt(tmpB[:], tmpB[:], omka_b[:], ALU.add, reads=[tmpB, omka_b], writes=[tmpB])
            vtt(pk[:, PQ[2], :], k_s, tmpB[:], ALU.mult, reads=[ps_s, tmpB], writes=[pk])
            vtt(tmpB[:], r_s, browB[:, BRB_RK:BRB_RK + 512], ALU.mult, reads=[ps_s, browB], writes=[tmpB])
            vtt(tmpB[:], tmpB[:], pk[:, PQ[2], :], ALU.mult, reads=[tmpB, pk], writes=[tmpB])
            vred(st8[:], v3(tmpB[:]), reads=[tmpB], writes=[st8])
            vtt(v3(bonus_s[:]), v3(v_s), bc8(st8[:]), ALU.mult, reads=[ps_s, st8], writes=[bonus_s])
            sview = scr1[:].rearrange("q t b h k -> q (t b) (h k)")
            S.dma("sync", sview[0], r_s, reads=[ps_s], writes=[scr1])
            S.dma("sync", sview[3], v_s, reads=[ps_s], writes=[scr1])
            for qq, slot in PQ.items():
                S.dma("sync", sview[qq], pk[:, slot, :], reads=[pk], writes=[scr1])
            S.finish([scr1], engname="sync")
            S.barrier()
            chk("S1")

        with ExitStack() as e2:
            sIn = sb(e2, "sIn", [128, 6, DT, 64])
            S.dma("sync", sIn[:], scr1[:].rearrange("q t b h k -> (b h) q t k"), reads=[scr1], writes=[sIn])
            St = sb(e2, "St", [128, 64, 64])
            S.dma("sync", St[:].rearrange("p v k -> p (v k)"), swkv, writes=[St])
            tmpS = sb(e2, "tmpS", [128, 64, 64])
            sa = sb(e2, "sa", [128, 64])
            yS = sb(e2, "yS", [128, DT, 64])
            stgo = [sb(e2, f"stgo{i}", [128, D]) for i in range(3)]
            for fc in range(8):
                so = stgo[fc % 3]
                S.dma("sync", so[:], w_out[fc * 128:(fc + 1) * 128, :], writes=[so])
                act(woutb[:, fc, :], so[:], AF.Copy, reads=[so], writes=[woutb])

            def bv(ap):
                return ap.unsqueeze(1).to_broadcast([128, 64, 64])

            def bk(ap):
                return ap.unsqueeze(2).to_broadcast([128, 64, 64])

            for t in range(DT):
                q = lambda i: sIn[:, i, t, :]
                vtt(tmpS[:], St[:], bv(q(4)), ALU.mult, reads=[St, sIn], writes=[tmpS])
                vred(sa[:], tmpS[:], reads=[tmpS], writes=[sa])
                vtt(St[:], St[:], bv(q(1)), ALU.mult, reads=[St, sIn], writes=[St])
                vtt(tmpS[:], bk(sa[:]), bv(q(5)), ALU.mult, reads=[sa, sIn], writes=[tmpS])
                vtt(St[:], St[:], tmpS[:], ALU.add, reads=[St, tmpS], writes=[St])
                vtt(tmpS[:], bk(q(3)), bv(q(2)), ALU.mult, reads=[sIn], writes=[tmpS])
                vtt(St[:], St[:], tmpS[:], ALU.add, reads=[St, tmpS], writes=[St])
                vtt(tmpS[:], St[:], bv(q(0)), ALU.mult, reads=[St, sIn], writes=[tmpS])
                vred(yS[:, t, :], tmpS[:], reads=[tmpS], writes=[yS])
            S.dma("sync", nws[:], St[:].rearrange("p v k -> p (v k)"), reads=[St], writes=[nws])
            S.dma("sync", scr2[:].rearrange("b h t v -> (b h) t v"), yS[:], reads=[yS], writes=[scr2])
            S.finish([scr2, nws], engname="sync")
            S.barrier()
            chk("S2")

        with ExitStack() as e3:
            yT = sb(e3, "yT", [NS, 512])
            tmpA = sb(e3, "tmpA3", [NS, 512])
            for t in range(DT):
                S.dma("sync", yT[t * DB:(t + 1) * DB, :].rearrange("b (h v) -> b h v", v=64), scr2[:][:, :, t, :], reads=[scr2], writes=[yT])
            vred(st8[:], v3(yT[:]), reads=[yT], writes=[st8])
            vts(st8[:], st8[:], 1.0 / 64, None, ALU.mult, None, reads=[st8], writes=[st8])
            vtt(v3(yT[:]), v3(yT[:]), bc8(st8[:]), ALU.subtract, reads=[yT, st8], writes=[yT])
            vtt(tmpA[:], yT[:], yT[:], ALU.mult, reads=[yT], writes=[tmpA])
            vred(st8[:], v3(tmpA[:]), reads=[tmpA], writes=[st8])
            rsqrt_small(st8b[:], st8[:], st8c[:], 1.0 / 64, GN_EPS, reads=[st8], writes=[st8c, st8b])
            vtt(v3(yT[:]), v3(yT[:]), bc8(st8b[:]), ALU.mult, reads=[yT, st8b], writes=[yT])
            vtt(yT[:], yT[:], browB[:, BRB_GW:BRB_GW + 512], ALU.mult, reads=[yT, browB], writes=[yT])
            vtt(yT[:], yT[:], browB[:, BRB_GB:BRB_GB + 512], ALU.add, reads=[yT, browB], writes=[yT])
            vtt(yT[:], yT[:], bonus_s[:], ALU.add, reads=[yT, bonus_s], writes=[yT])
            vtt(yT[:], yT[:], graw_s[:], ALU.mult, reads=[yT, graw_s], writes=[yT])
            oTs = sb(e3, "oTs", [128, 8, NS], BF16)
            for fb in range(4):
                tr(pA[:, fb * 64:fb * 64 + NS], yT[:, fb * 128:(fb + 1) * 128], ident[0:NS, 0:NS], reads=[yT, cst], writes=[pA])
            vcopy(oTs[:, 0:4, :], pA[:, 0:4 * NS].rearrange("p (f t) -> p f t", t=NS), reads=[pA], writes=[oTs])

            uext = sb(e3, "uext_s", [128, 4, DB, 19])
            sp0 = sb(e3, "sp0", [120, 512])
            sp1 = sb(e3, "sp1", [120, 512])
            S.dma("sync", sp0[:], spool[0:120, :], writes=[sp0])
            S.dma("sync", sp1[:], spool[120:240, :], writes=[sp1])
            for g in range(4):
                tr(pB[:, 0:120], sp0[:, g * 128:(g + 1) * 128], ident[0:120, 0:120], reads=[sp0, cst], writes=[pB])
                tr(pB[:, 128:248], sp1[:, g * 128:(g + 1) * 128], ident[0:120, 0:120], reads=[sp1, cst], writes=[pB])
                vcopy(uext[:, g, 0:8, 0:15], pB[:, 0:120].rearrange("p (b j) -> p b j", j=15), reads=[pB], writes=[uext])
                vcopy(uext[:, g, 8:16, 0:15], pB[:, 128:248].rearrange("p (b j) -> p b j", j=15), reads=[pB], writes=[uext])
                tr(pM[:, 0:NS], u_s[:, g * 128:(g + 1) * 128], ident[0:NS, 0:NS], reads=[u_s, cst], writes=[pM])
                vcopy(uext[:, g, :, 15:19], pM[:, 0:NS].rearrange("p (t b) -> p b t", b=DB), reads=[pM], writes=[uext])
            s2 = sb(e3, "s2_s", [128, 4, DB, 19])
            s4 = sb(e3, "s4_s", [128, 3, DB, 19])
            s8 = sb(e3, "s8_s", [128, 2, DB, 19])
            s16 = sb(e3, "s16_s", [128, 1, DB, 19])
            d_s = sb(e3, "d_s", [128, 4, DT, DB])
            vtt(s2[:, :, :, 1:19], uext[:, :, :, 1:19], uext[:, :, :, 0:18], ALU.add, reads=[uext], writes=[s2])
            vtt(s4[:, :, :, 3:19], s2[:, 1:4, :, 3:19], s2[:, 1:4, :, 1:17], ALU.add, reads=[s2], writes=[s4])
            vtt(s8[:, :, :, 7:19], s4[:, 1:3, :, 7:19], s4[:, 1:3, :, 3:15], ALU.add, reads=[s4], writes=[s8])
            vtt(s16[:, :, :, 15:19], s8[:, 1:2, :, 15:19], s8[:, 1:2, :, 7:11], ALU.add, reads=[s8], writes=[s16])
            tots = [(s2, 0), (s4, 1), (s8, 2), (s16, 3)]
            for g in range(4):
                tt, off = tots[g]
                vstt(d_s[:, g, :, :].rearrange("p t b -> p b t"), tt[:, g - off, :, 15:19], 1.0 / WINS[g], uext[:, g, :, 15:19],
                     ALU.mult, ALU.subtract, reads=[tt, uext], writes=[d_s])
            gpT = sb(e3, "gpT", [128, 4, NS])
            for g in range(4):
                tr(pM[:, 64 + g * 64:64 + g * 64 + NS], gp_s[:, g * 128:(g + 1) * 128], ident[0:NS, 0:NS], reads=[gp_s, cst], writes=[pM])
            vcopy(gpT[:], pM[:, 64:64 + 4 * NS].rearrange("p (g t) -> p g t", t=NS), reads=[pM], writes=[gpT])
            for g in range(4):
                mm(pA[:, g * 64:g * 64 + NS], pw[:, g, :], d_s[:, g, :, :].rearrange("p t b -> p (t b)"), True, True, reads=[pw, d_s], writes=[pA])
            for g in range(4):
                vstt(oTs[:, 4 + g, :], pA[:, g * 64:g * 64 + NS], pvec[:, PV_PS + g:PV_PS + g + 1], gpT[:, g, :], ALU.mult, ALU.mult,
                     reads=[pA, pvec, gpT], writes=[oTs])

            sq = sb(e3, "sq2_s", [NS, D]); ssum = sb(e3, "ssum_s", [NS, 1])
            tmp1 = sb(e3, "tmp1_s", [NS, 1]); rstd = sb(e3, "rstd2_s", [NS, 1]); yo = sb(e3, "yo_s", [NS, D])
            oT_list_T = oTs
            final_tile((x_s, sq, ssum, tmp1, rstd, yo), NS, x_s, [oTs[:, fc, :] for fc in range(8)], ys[:], ys)
            S.finish([ys, nss, nps], engname="sync")
            S.barrier()
            chk("S3")

    with ExitStack() as es:
        def sbl(name, shape, dt=F32, n=2):
            return [sb(es, f"{name}_{i}", shape, dt) for i in range(n)]

        xt = sbl("xt", [128, D])
        sqx = sb(es, "sqx", [128, D], BF16)
        yo = sb(es, "yo", [128, D])
        ssx = sb(es, "ssx", [128, 1]); t1x = sb(es, "t1x", [128, 1]); rsx = sb(es, "rsx", [128, 1])
        xnb = sb(es, "xnb", [128, D], BF16)
        hT = sb(es, "hT", [128, 8, TB], BF16)
        praw = sbl("praw", [128, 4, TB + 1], n=1) * 2
        qsc = sbl("qsc", [128, 4, TB], n=1) * 2
        halo = sb(es, "halo", [128, 13])
        omu = sb(es, "omu2", [128, 13])
        psr = sb(es, "psr", [128, 4, TB]); psk = sb(es, "psk", [128, 4, TB]); psv = sb(es, "psv", [128, 4, TB])
        ps12 = sb(es, "ps12", [128, TB])
        psx = [T(g_[:, i, :], f"psx{gi_}_{i}", buf=g_.b) for gi_, g_ in enumerate([psr, psk, psv]) for i in range(4)] + [ps12]
        sg = sb(es, "sg", [128, 4, TB]); av = sb(es, "av", [128, 4, TB])
        gsil = sbl("gsil", [128, 4, TB], BF16)
        gpsil = sb(es, "gpsil", [128, 4, TB], BF16)
        uext = sb(es, "uext", [128, 4, 15 + TB])
        th = sb(es, "th", [64, TB])
        wbig = [sb(es, f"wbig{i}", [128, 4, TB]) for i in range(4)]
        w1, w2, w3, w4 = wbig
        srot = [T(wbig[i][:].rearrange("p f t -> p (f t)")[:, 0:15 + TB], f"srot{i}", buf=wbig[i].b) for i in range(4)]
        kkn = sb(es, "kkn", [128, 4, TB]); kmod = sb(es, "kmod", [128, 4, TB]); bv_ = sb(es, "bv_", [128, 4, TB])
        cum = sb(es, "cum", [128, 4, TB])
        dpl = T(kkn[:, 0, :], "dpl", buf=kkn.b)
        at = sbl("at", [64, 4, 2, TB], BF16)
        rt = sbl("rt", [64, 4, 2, TB], BF16)
        bt = sb(es, "bt", [64, 4, 2, TB], BF16)
        kt = sb(es, "kt", [64, 4, 2, TB], BF16)
        bh = sb(es, "bh", [128, 4, TB], BF16); kh = sb(es, "kh", [128, 4, TB], BF16); vb = sb(es, "vb", [128, 4, TB], BF16)
        bon = sbl("bon", [128, 4, TB])
        gC = sbl("gC", [64, 4, 2, NCH])
        VT = [[sb(es, f"VT{p}{c}", [64, 512], BF16) for c in range(NCH)] for p in range(2)]
        BKT = [[sb(es, f"BKT{p}{c}", [64, 1024], BF16) for c in range(NCH)] for p in range(2)]
        Aak = [[sb(es, f"Aak{p}{c}", [64, 512], BF16) for c in range(NCH)] for p in range(2)]
        Arb = [[sb(es, f"Arb{p}{c}", [64, 512], BF16) for c in range(NCH)] for p in range(2)]
        Ark = [[sb(es, f"Ark{p}{c}", [64, 512], BF16) for c in range(NCH)] for p in range(2)]
        Minv = [[sb(es, f"Minv{p}{c}", [64, 512], BF16) for c in range(NCH)] for p in range(2)]
        Nsb = [sb(es, f"Nsb{c}", [64, 512], BF16) for c in range(NCH)]
        NTsb = [sb(es, f"NTsb{c}", [64, 512], BF16) for c in range(NCH)]
        Xa0 = [sb(es, f"Xa0{c}", [64, 512], BF16) for c in range(NCH)]
        XTa0 = [sb(es, f"XTa0{c}", [64, 512], BF16) for c in range(NCH)]
        Qtmp = [sb(es, f"Qtmp{c}", [64, 512], BF16) for c in range(NCH)]
        ST = sb(es, "ST", [64, 8, 64]); STb = sb(es, "STb", [64, 8, 64], BF16)
        Wsb = sb(es, "Wsb", [64, 512], BF16); Usb = sb(es, "Usb", [64, 512], BF16)
        yc = sb(es, "yc", [64, 512]); ysq = sb(es, "ysq", [64, 512])
        STt = T(ysq[:].rearrange("p (h v) -> p h v", v=64), "STt", buf=ysq.b)
        m8 = sb(es, "m8", [64, 8]); v8 = sb(es, "v8", [64, 8]); r8 = sb(es, "r8", [64, 8]); t8 = sb(es, "t8", [64, 8])
        o1 = sb(es, "o1", [128, 4, 64])
        oT = sbl("oT", [128, 8, TB], BF16)
        ssum = sb(es, "ssum", [128, 1]); tmp1 = sb(es, "tmp1", [128, 1]); rstd = sb(es, "rstd", [128, 1])
        ppT = T(ysq[0:16, :], "ppT", buf=ysq.b); m13 = sb(es, "m13", [13, 128])
        SvT = T(yc[:].rearrange("p (h k) -> p h k", k=64), "SvT", buf=yc.b)

        memset(halo[:], 0.0, writes=[halo])
        memset(uext[:, :, 0:15], 0.0, writes=[uext])
        memset(ST[:], 0.0, writes=[ST])
        memset(STb[:], 0.0, writes=[STb])
        vts(omu[:], pvec[:, PV_MU:PV_MU + 13], -1.0, 1.0, ALU.mult, ALU.add, reads=[pvec], writes=[omu])

        def b8(ap):
            return ap.unsqueeze(1).to_broadcast([64, 8, 64])

        def h3(ap):
            return ap.rearrange("p (h v) -> p h v", v=64)

        def hc(h):
            return slice(h * 64, (h + 1) * 64)

        maskUs = b8(cst[0:64, C_MUS:C_MUS + 64])
        maskUi = b8(cst[0:64, C_MUI:C_MUI + 64])
        maskLs = b8(cst[0:64, C_MLS:C_MLS + 64])
        ident8 = b8(cst[0:64, C_ID:C_ID + 64])
        rstm = cst[:, C_RST:C_RST + 512]
        st = dict(gk=0, ak=0)
        pT32 = T(pT[:].bitcast(F32), "pT32", buf=pT.b)
        abanks = [pA, pB, pg[0], pg[1], pM, pT32]

        def nextbank():
            b = abanks[st["ak"] % len(abanks)]
            st["ak"] += 1
            return b

        def front(tb):
            pb = tb % 2
            t0 = tb * TB
            x_t = xt[pb]
            S.dma("sync", x_t[:], xp[t0:t0 + TB, :], writes=[x_t])
            act(sqx[:], x_t[:], AF.Square, reads=[x_t], writes=[sqx])
            vred(ssx[:], sqx[:], reads=[sqx], writes=[ssx])
            rsqrt_small(rsx[:], ssx[:], t1x[:], 1.0 / D, NORM_EPS, reads=[ssx], writes=[t1x, rsx])
            act(xnb[:], x_t[:], AF.Copy, reads=[x_t, rsx], writes=[xnb], scale=rsx[:, 0:1])
            yield
            for dc in range(8):
                tr(pT[:, dc * 128:(dc + 1) * 128], xnb[:, dc * 128:(dc + 1) * 128], identb[:], reads=[xnb, identb], writes=[pT])
            vcopy(hT[:].rearrange("p c t -> p (c t)"), pT[:], reads=[pT], writes=[hT])
            yield

            def gemm_group(ebs):
                bank = pg[st["gk"] % 2]
                st["gk"] += 1
                for i, eb in enumerate(ebs):
                    for dc in range(8):
                        mm(bank[:, i * TB:(i + 1) * TB], winb[:, dc, eb * 128:(eb + 1) * 128], hT[:, dc, :], dc == 0, dc == 7,
                           reads=[winb, hT], writes=[bank])
                return bank

            for gi, ebs in enumerate([[0, 1, 2, 3], [4, 5, 6, 7], [8, 9, 10, 11], [12]]):
                bank = gemm_group(ebs)
                yield
                n = len(ebs)
                pr = praw[gi % 2]
                qs = qsc[gi % 2]
                e0 = ebs[0]
                vcopy(pr[:, 0:n, 0:1], halo[:, e0:e0 + n].unsqueeze(2), reads=[halo], writes=[pr], eng="gpsimd")
                act(pr[:, 0:n, 1:TB + 1], bank[:, 0:n * TB].rearrange("p (e t) -> p e t", t=TB), AF.Copy, reads=[bank], writes=[pr])
                for i, eb in enumerate(ebs):
                    act(qs[:, i, :], bank[:, i * TB:(i + 1) * TB], AF.Copy, reads=[bank, omu], writes=[qs], scale=omu[:, eb:eb + 1])
                for i, eb in enumerate(ebs):
                    vstt(psx[eb][:], pr[:, i, 0:TB], pvec[:, PV_MU + eb:PV_MU + eb + 1], qs[:, i, :], ALU.mult, ALU.add,
                         reads=[pr, pvec, qs], writes=[psx[eb]])
                vcopy(halo[:, e0:e0 + n].unsqueeze(2), pr[:, 0:n, TB:TB + 1], reads=[pr], writes=[halo], eng="gpsimd")
                yield
            bank = gemm_group([13, 14, 15, 16])
            act(gsil[pb][:].rearrange("p f t -> p (f t)"), bank[:, :], AF.Silu, reads=[bank], writes=[gsil[pb]])
            yield
            bank = gemm_group([17, 18, 19, 20])
            act(uext[:, :, 15:15 + TB], bank[:, :].rearrange("p (g t) -> p g t", t=TB), AF.Copy, reads=[bank], writes=[uext])
            yield
            bank = gemm_group([21, 22, 23, 24])
            act(gpsil[:].rearrange("p g t -> p (g t)"), bank[:, :], AF.Silu, reads=[bank], writes=[gpsil])
            yield

            act(th[:], psx[12][0:64, :], AF.Tanh, reads=[psx[12]], writes=[th])
            for fb in range(4):
                mm(pA[:, fb * TB:(fb + 1) * TB], wd[:, fb * 128:(fb + 1) * 128], th[:], True, True, reads=[wd, th], writes=[pA])
            for fb in range(4):
                mm(pM[:, fb * TB:(fb + 1) * TB], wa[64:128, fb * 128:(fb + 1) * 128], psx[12][64:128, :], True, True, reads=[wa, psx[12]], writes=[pM])
            for fb in range(4):
                act(sg[:, fb, :], pA[:, fb * TB:(fb + 1) * TB], AF.Sigmoid, reads=[pA, pvec], writes=[sg], bias=pvec[:, PV_W0 + fb:PV_W0 + fb + 1])
                act(av[:, fb, :], pM[:, fb * TB:(fb + 1) * TB], AF.Sigmoid, reads=[pM, pvec], writes=[av], bias=pvec[:, PV_A0 + fb:PV_A0 + fb + 1])
            yield

            def pb4(col):
                return pvec[:, col:col + 4].unsqueeze(2).to_broadcast([128, 4, TB])

            def f2(t_):
                return t_[:].rearrange("p f t -> p (f t)")

            vcopy(vb[:], psv[:], reads=[psv], writes=[vb], eng="gpsimd")
            vtt(w1[:], psk[:], pb4(PV_KK), ALU.mult, reads=[psk, pvec], writes=[w1])
            vtt(w2[:], w1[:], w1[:], ALU.mult, reads=[w1], writes=[w2])
            mm(pM[:, :], onesblk, f2(w2), True, True, reads=[cst, w2], writes=[pM])
            vts(f2(w2), pM[:, :], L2_EPS, None, ALU.add, None, reads=[pM], writes=[w2])
            act(w2[:], w2[:], AF.Ln, reads=[w2], writes=[w2])
            act(w2[:], w2[:], AF.Exp, reads=[w2], writes=[w2], scale=-0.5)
            vstt(kkn[:], w1[:], -1.0, w2[:], ALU.mult, ALU.mult, reads=[w1, w2], writes=[kkn])
            yield
            vtt(w1[:], av[:], pb4(PV_KA), ALU.mult, reads=[av, pvec], writes=[w1])
            vtt(w1[:], w1[:], omka[:, 0:4].unsqueeze(2).to_broadcast([128, 4, TB]), ALU.add, reads=[w1, omka], writes=[w1])
            vtt(kmod[:], psk[:], w1[:], ALU.mult, reads=[psk, w1], writes=[kmod])
            vstt(bv_[:], kkn[:], -1.0, av[:], ALU.mult, ALU.mult, reads=[kkn, av], writes=[bv_])
            vtt(w1[:], psr[:], pb4(PV_RK), ALU.mult, reads=[psr, pvec], writes=[w1])
            vtt(w1[:], w1[:], kmod[:], ALU.mult, reads=[w1, kmod], writes=[w1])
            mm(pA[:, :], onesblk, f2(w1), True, True, reads=[cst, w1], writes=[pA])
            vtt(f2(bon[pb]), pA[:, :], f2(psv), ALU.mult, reads=[pA, psv], writes=[bon[pb]])
            yield
            S.op("vector", lambda e: e.tensor_tensor_scan(out=f2(cum), data0=rstm, data1=f2(sg), initial=0.0, op0=ALU.mult, op1=ALU.add),
                 reads=[cst, sg], writes=[cum], cost=1.2)
            c3 = cum[:].rearrange("p f (c t) -> p (f c) t", t=CH)
            vtt(w1[:], cum[:], sg[:], ALU.subtract, reads=[cum, sg], writes=[w1])
            act(w2[:], cum[:], AF.Exp, reads=[cum], writes=[w2], scale=-C0)
            act(w3[:], cum[:], AF.Exp, reads=[cum], writes=[w3], scale=C0)
            act(w1[:], w1[:], AF.Exp, reads=[w1], writes=[w1], scale=-C0)
            vtt(w4[:].rearrange("p f (c t) -> p (f c) t", t=CH), c3[:, :, CH - 1:CH].to_broadcast([128, 4 * NCH, CH]), c3, ALU.subtract,
                reads=[cum], writes=[w4], eng="gpsimd")
            act(w4[:], w4[:], AF.Exp, reads=[w4], writes=[w4], scale=-C0)
            yield
            for j in range(2):
                pp = slice(64 * j, 64 * j + 64)
                e_ = "vector" if j == 0 else "gpsimd"
                vtt(rt[pb][:, :, j, :], psr[pp, :, :], w2[pp, :, :], ALU.mult, reads=[psr, w2], writes=[rt[pb]], eng=e_)
                vtt(bt[:, :, j, :], bv_[pp, :, :], w3[pp, :, :], ALU.mult, reads=[bv_, w3], writes=[bt], eng=e_)
                vtt(kt[:, :, j, :], kmod[pp, :, :], w3[pp, :, :], ALU.mult, reads=[kmod, w3], writes=[kt], eng=e_)
                vtt(at[pb][:, :, j, :], kkn[pp, :, :], w1[pp, :, :], ALU.mult, reads=[kkn, w1], writes=[at[pb]], eng=e_)
                act(gC[pb][:, :, j, :], cum[pp, :, :].rearrange("p f (c t) -> p f c t", t=CH)[:, :, :, CH - 1], AF.Exp,
                    reads=[cum], writes=[gC[pb]], scale=-C0)
            vtt(bh[:], bv_[:], w4[:], ALU.mult, reads=[bv_, w4], writes=[bh])
            vtt(kh[:], kmod[:], w4[:], ALU.mult, reads=[kmod, w4], writes=[kh], eng="gpsimd")
            yield

            L = 15 + TB
            for g in range(4):
                vtt(srot[0][:, 1:], uext[:, g, 1:], uext[:, g, 0:L - 1], ALU.add, reads=[uext], writes=[srot[0]], eng="gpsimd")
                tot = srot[0]
                if g >= 1:
                    vtt(srot[1][:, 3:], srot[0][:, 3:], srot[0][:, 1:L - 2], ALU.add, reads=[srot[0]], writes=[srot[1]], eng="gpsimd")
                    tot = srot[1]
                if g >= 2:
                    vtt(srot[2][:, 7:], srot[1][:, 7:], srot[1][:, 3:L - 4], ALU.add, reads=[srot[1]], writes=[srot[2]], eng="gpsimd")
                    tot = srot[2]
                if g >= 3:
                    vtt(srot[3][:, 15:], srot[2][:, 15:], srot[2][:, 7:L - 8], ALU.add, reads=[srot[2]], writes=[srot[3]], eng="gpsimd")
                    tot = srot[3]
                vstt(dpl[:], tot[:, 15:], 1.0 / WINS[g], uext[:, g, 15:], ALU.mult, ALU.subtract, reads=[tot, uext], writes=[dpl])
                if tb == 0:
                    vtt(dpl[:, 0:16], tot[:, 15:31], cst[:, C_ICNT + g * 16:C_ICNT + (g + 1) * 16], ALU.mult, reads=[tot, cst], writes=[dpl])
                    vtt(dpl[:, 0:16], dpl[:, 0:16], uext[:, g, 15:31], ALU.subtract, reads=[dpl, uext], writes=[dpl])
                mm(pM[:, 0:TB], pw[:, g, :], dpl[:], True, True, reads=[pw, dpl], writes=[pM])
                vstt(oT[pb][:, 4 + g, :], pM[:, 0:TB], pvec[:, PV_PS + g:PV_PS + g + 1], gpsil[:, g, :], ALU.mult, ALU.mult,
                     reads=[pM, pvec, gpsil], writes=[oT[pb]])
                yield
            if tb == NTB - 1:
                for g in range(4):
                    tr(pA[0:16, g * 128:(g + 1) * 128], uext[:, g, TB - 1:TB + 15], ident, reads=[uext, cst], writes=[pA])
                vcopy(ppT[:], pA[0:16, :], reads=[pA], writes=[ppT])
                S.dma("sync", npp[:], ppT[1:16, :], reads=[ppT], writes=[npp])
                tr(pB[0:13, 0:128], halo[:, 0:13], ident, reads=[halo, cst], writes=[pB])
                vcopy(m13[:], pB[0:13, 0:128], reads=[pB], writes=[m13])
                S.dma("sync", nsp[:], m13[:], reads=[m13], writes=[nsp])
            vcopy(uext[:, :, 0:15], uext[:, :, TB:TB + 15], reads=[uext], writes=[uext], eng="gpsimd")
            yield

            css = [slice(c * CH, (c + 1) * CH) for c in range(NCH)]
            for c in range(NCH):
                for qi, srcl in enumerate([bh, kh]):
                    for fb in range(4):
                        tr(pT[0:64, qi * 512 + fb * 128:qi * 512 + (fb + 1) * 128], srcl[:, fb, css[c]], identb[:], reads=[srcl, identb], writes=[pT])
                vcopy(BKT[pb][c][:], pT[0:64, :], reads=[pT], writes=[BKT[pb][c]])
                for fb in range(4):
                    tr(pT[0:64, fb * 128:(fb + 1) * 128], vb[:, fb, css[c]], identb[:], reads=[vb, identb], writes=[pT])
                act(VT[pb][c][:], pT[0:64, 0:512], AF.Copy, reads=[pT], writes=[VT[pb][c]])
                yield

            def hsl(tl, h, c):
                fb, j = divmod(h, 2)
                return tl[:, fb, j, css[c]]

            for (Lt, Rt, mask, dsts) in [(bt, at[pb], maskUs, Nsb), (at[pb], bt, maskLs, NTsb), (kt, at[pb], maskUs, Aak[pb]),
                                         (bt, rt[pb], maskUi, Arb[pb]), (kt, rt[pb], maskUi, Ark[pb])]:
                banks = []
                for c in range(NCH):
                    bank = nextbank()
                    banks.append(bank)
                    for h in range(8):
                        mm(bank[0:64, hc(h)], hsl(Lt, h, c), hsl(Rt, h, c), True, True, reads=[Lt, Rt], writes=[bank])
                for c in range(NCH):
                    vtt(h3(dsts[c][:]), h3(banks[c][0:64, :]), mask, ALU.mult, reads=[banks[c], cst], writes=[dsts[c]])
                yield
            X = list(Nsb); XT = list(NTsb)
            Q = [Qtmp[c] for c in range(NCH)]
            for c in range(NCH):
                vtt(h3(Q[c][:]), h3(Nsb[c][:]), ident8, ALU.add, reads=[Nsb[c], cst], writes=[Q[c]])
            for lvl in range(5):
                Xn = [(Xa0[c] if lvl % 2 == 0 else Nsb[c]) for c in range(NCH)]
                XTn = [(XTa0[c] if lvl % 2 == 0 else NTsb[c]) for c in range(NCH)]
                Qn = [(Minv[pb][c] if lvl % 2 == 0 else Qtmp[c]) for c in range(NCH)]
                banks = []
                for c in range(NCH):
                    bank = nextbank(); banks.append(bank)
                    for h in range(8):
                        mm(bank[0:64, hc(h)], X[c][:, hc(h)], XT[c][:, hc(h)], True, True, reads=[X[c], XT[c]], writes=[bank])
                for c in range(NCH):
                    act(XTn[c][:], banks[c][0:64, :], AF.Copy, reads=[banks[c]], writes=[XTn[c]])
                yield
                if lvl < 4:
                    banks = []
                    for c in range(NCH):
                        bank = nextbank(); banks.append(bank)
                        for h in range(8):
                            mm(bank[0:64, hc(h)], XT[c][:, hc(h)], X[c][:, hc(h)], True, True, reads=[X[c], XT[c]], writes=[bank])
                    for c in range(NCH):
                        act(Xn[c][:], banks[c][0:64, :], AF.Copy, reads=[banks[c]], writes=[Xn[c]])
                    yield
                banks = []
                for c in range(NCH):
                    bank = nextbank(); banks.append(bank)
                    for h in range(8):
                        mm(bank[0:64, hc(h)], XTn[c][:, hc(h)], Q[c][:, hc(h)], True, True, reads=[XTn[c], Q[c]], writes=[bank])
                for c in range(NCH):
                    vtt(Qn[c][:], banks[c][0:64, :], Q[c][:], ALU.add, reads=[banks[c], Q[c]], writes=[Qn[c]])
                X, XT, Q = Xn, XTn, Qn
                yield

        def chain(tb):
            pb = tb % 2
            t0 = tb * TB
            for c in range(NCH):
                cs = slice(c * CH, (c + 1) * CH)
                aT, rT = at[pb], rt[pb]
                VTc, BKTc, Aakc, Arbc, Arkc, Minvc = VT[pb][c], BKT[pb][c], Aak[pb][c], Arb[pb][c], Ark[pb][c], Minv[pb][c]
                for h in range(8):
                    fb, j = divmod(h, 2)
                    mm(pC[0:64, hc(h)], aT[:, fb, j, cs], STb[:, h, :], True, False, reads=[aT, STb], writes=[pC])
                    mm(pC[0:64, hc(h)], Aakc[:, hc(h)], VTc[:, hc(h)], False, True, reads=[Aakc, VTc], writes=[pC])
                act(Wsb[:], pC[0:64, :], AF.Copy, reads=[pC], writes=[Wsb])
                yield
                for h in range(8):
                    mm(pC[0:64, hc(h)], Minvc[:, hc(h)], Wsb[:, hc(h)], True, True, reads=[Minvc, Wsb], writes=[pC])
                act(Usb[:], pC[0:64, :], AF.Copy, reads=[pC], writes=[Usb])
                yield
                for h in range(8):
                    mm(pC[0:64, hc(h)], BKTc[:, hc(h)], Usb[:, hc(h)], True, False, reads=[BKTc, Usb], writes=[pC])
                    mm(pC[0:64, hc(h)], BKTc[:, 512 + h * 64:512 + (h + 1) * 64], VTc[:, hc(h)], False, True, reads=[BKTc, VTc], writes=[pC])
                for h in range(8):
                    fb, j = divmod(h, 2)
                    mm(pD[0:64, hc(h)], rT[:, fb, j, cs], STb[:, h, :], True, False, reads=[rT, STb], writes=[pD])
                    mm(pD[0:64, hc(h)], Arbc[:, hc(h)], Usb[:, hc(h)], False, False, reads=[Arbc, Usb], writes=[pD])
                    mm(pD[0:64, hc(h)], Arkc[:, hc(h)], VTc[:, hc(h)], False, True, reads=[Arkc, VTc], writes=[pD])
                vtt(STt[:], ST[:], gC[pb][:].rearrange("p f j c -> p (f j) c")[:, :, c:c + 1].to_broadcast([64, 8, 64]), ALU.mult,
                    reads=[ST, gC[pb]], writes=[STt])
                vtt(ST[:], STt[:], h3(pC[0:64, :]), ALU.add, reads=[STt, pC], writes=[ST])
                act(STb[:], ST[:], AF.Copy, reads=[ST], writes=[STb])
                yield
                y3 = h3(pD[0:64, :])
                vred(m8[:], y3, reads=[pD], writes=[m8])
                vts(m8[:], m8[:], 1.0 / 64, None, ALU.mult, None, reads=[m8], writes=[m8])
                vtt(h3(yc[:]), y3, m8[:].unsqueeze(2).to_broadcast([64, 8, 64]), ALU.subtract, reads=[pD, m8], writes=[yc])
                act(ysq[:], yc[:], AF.Square, reads=[yc], writes=[ysq])
                vred(v8[:], h3(ysq[:]), reads=[ysq], writes=[v8])
                rsqrt_small(r8[:], v8[:], t8[:], 1.0 / 64, GN_EPS, reads=[v8], writes=[t8, r8])
                vtt(h3(yc[:]), h3(yc[:]), r8[:].unsqueeze(2).to_broadcast([64, 8, 64]), ALU.mult, reads=[yc, r8], writes=[yc], eng="gpsimd")
                yield
                for fb in range(4):
                    tr(pD[:, fb * 64:(fb + 1) * 64], yc[:, fb * 128:(fb + 1) * 128], ident[0:64, 0:64], reads=[yc, cst], writes=[pD])
                for fb in range(4):
                    vts(o1[:, fb, :], pD[:, fb * 64:(fb + 1) * 64], pvec[:, PV_GW + fb:PV_GW + fb + 1], pvec[:, PV_GB + fb:PV_GB + fb + 1],
                        ALU.mult, ALU.add, reads=[pD, pvec], writes=[o1])
                vtt(o1[:], o1[:], bon[pb][:, :, cs], ALU.add, reads=[o1, bon[pb]], writes=[o1], eng="gpsimd")
                vtt(oT[pb][:, 0:4, cs], o1[:], gsil[pb][:, :, cs], ALU.mult, reads=[o1, gsil[pb]], writes=[oT[pb]])
                yield
            x_t = xt[pb]
            for half in range(2):
                bank = pD if half == 0 else pC
                for fc in range(8):
                    mm(bank[:, :], oT[pb][:, fc, :], woutb[:, fc, half * 512:(half + 1) * 512], fc == 0, fc == 7, reads=[oT[pb], woutb], writes=[bank])
                vtt(x_t[:, half * 512:(half + 1) * 512], bank[:, :], x_t[:, half * 512:(half + 1) * 512], ALU.add, reads=[bank, x_t], writes=[x_t])
                yield
            act(yo[:], x_t[:], AF.Square, reads=[x_t], writes=[yo])
            vred(ssum[:], yo[:], reads=[yo], writes=[ssum])
            rsqrt_small(rstd[:], ssum[:], tmp1[:], 1.0 / D, NORM_EPS, reads=[ssum], writes=[tmp1, rstd])
            vstt(yo[:], x_t[:], rstd[:, 0:1], normf[:], ALU.mult, ALU.mult, reads=[x_t, rstd, normf], writes=[yo])
            S.dma("sync", yp[t0:t0 + TB, :], yo[:], reads=[yo], writes=[yp])
            yield

        def run_all(g):
            n = 0
            for _ in g:
                n += 1
            return n

        def interleave(ga, na, gb, nb):
            ia = ib = 0
            da = db = False
            while not (da and db):
                pick_a = (not da) and (db or (ia * nb <= ib * na))
                if pick_a:
                    try:
                        next(ga); ia += 1
                    except StopIteration:
                        da = True
                else:
                    try:
                        next(gb); ib += 1
                    except StopIteration:
                        db = True
            return ia, ib

        run_all(front(0))

        def record_units(g):
            units = []
            S.rec = []
            for _ in g:
                if S.rec:
                    units.append(S.rec)
                S.rec = []
            if S.rec:
                units.append(S.rec)
            S.rec = None
            return units

        A, B = [], []
        for tb in range(NTB):
            A.append(record_units(chain(tb)))
            if tb + 1 < NTB:
                B.append(record_units(front(tb + 1)))
        S.merge_emit(A, B, a_ok=lambda ia, ib: ib >= ia, b_ok=lambda ib, ia: ia >= ib)
        for h in range(8):
            tr(pA[0:64, h * 64:(h + 1) * 64], ST[:, h, :], ident[0:64, 0:64], reads=[ST, cst], writes=[pA])
        vcopy(SvT[:].rearrange("p h k -> p (h k)"), pA[0:64, :], reads=[pA], writes=[SvT])
        S.dma("sync", nwp[:].rearrange("h v k -> v h k"), SvT[:], reads=[SvT], writes=[nwp])
        S.finish([yp, ys, nsp, nwp, npp, nss, nws, nps], engname="sync")
        S.barrier()
    es_top.close()
    return nc, S


_CACHE = {}


def _consts():
    cst = np.zeros((128, C_END), np.float32)
    cst[:, C_ID:C_ID + 128] = np.eye(128, dtype=np.float32)
    ob = np.zeros((128, 128), np.float32)
    ob[0:64, 0:64] = 1.0
    ob[64:128, 64:128] = 1.0
    cst[:, C_ONES:C_ONES + 128] = ob
    s = np.arange(64)[:, None]
    t = np.arange(64)[None, :]
    mus = (s < t).astype(np.float32)
    mui = (s <= t).astype(np.float32)
    mls = (s > t).astype(np.float32)
    i64 = np.eye(64, dtype=np.float32)
    cst[0:64, C_MUS:C_MUS + 64] = mus
    cst[0:64, C_MUI:C_MUI + 64] = mui
    cst[0:64, C_MLS:C_MLS + 64] = mls
    rst = np.ones((512,), np.float32)
    rst[::CH] = 0.0
    cst[:, C_RST:C_RST + 512] = rst[None, :]
    for g, w in enumerate(WINS):
        pos = np.arange(16)
        cst[:, C_ICNT + g * 16:C_ICNT + (g + 1) * 16] = (1.0 / np.minimum(pos + 1, w)).astype(np.float32)[None, :]
    return cst


def kernel(x_prompt, x_sample, state_shift, state_wkv, state_pool, norm_w, w_in, mu_shift,
           w_decay_b, w0, w_aaa_b, a0, k_k, k_a, r_k, gn_w, gn_b, pool_w, pool_scale, w_out, norm_f):
    f = lambda a: np.ascontiguousarray(np.asarray(a, dtype=np.float32))
    x_prompt, x_sample, state_shift, state_wkv, state_pool = map(f, (x_prompt, x_sample, state_shift, state_wkv, state_pool))
    if "nc" not in _CACHE:
        _CACHE["nc"] = build_program()
    nc, S = _CACHE["nc"]

    def colmajor(v, n):
        return f(v).reshape(n, 128).T

    pvec = np.concatenate([
        colmajor(norm_w[0], 8), colmajor(mu_shift[0], 13), colmajor(w0[0], 4), colmajor(a0[0], 4), colmajor(k_k[0], 4),
        colmajor(k_a[0], 4), colmajor(f(r_k[0]).reshape(-1), 4), colmajor(gn_w[0], 4), colmajor(gn_b[0], 4), colmajor(pool_scale[0], 4)], axis=1)
    pvec = f(pvec)
    browA = f(f(mu_shift[0])[None, :])
    browB = f(np.concatenate([f(w0[0]), f(a0[0]), f(k_k[0]), f(k_a[0]), f(r_k[0]).reshape(-1), f(gn_w[0]), f(gn_b[0])])[None, :])
    cst = _consts()
    shared = {
        "w_in": f(w_in[0]), "w_out": f(w_out[0]), "wdec": f(w_decay_b[0]), "waaa": f(w_aaa_b[0]), "poolw": f(pool_w[0]),
        "pvec": pvec, "browA": browA, "browB": browB, "normf": f(norm_f)[None, :], "cst": cst,
    }
    in_maps = []
    for c in range(NCORE):
        bs = slice(c * DB, (c + 1) * DB)
        m = dict(shared)
        m["xp"] = x_prompt[c]
        m["xs"] = f(x_sample[bs].transpose(1, 0, 2).reshape(NS, D))
        m["sshift"] = state_shift[0, bs]
        m["swkv"] = f(state_wkv[0, bs].reshape(128, 4096))
        m["spool"] = f(state_pool[0, bs].reshape(DB * 15, 512))
        in_maps.append(m)
    res = run_bass_kernel_spmd(nc, in_maps, core_ids=list(range(NCORE)))
    R = res.results
    y_prompt = np.stack([R[c]["yp"] for c in range(NCORE)], axis=0)
    y_sample = np.concatenate([R[c]["ys"].reshape(DT, DB, D).transpose(1, 0, 2) for c in range(NCORE)], axis=0)
    nsp = np.stack([R[c]["nsp"].reshape(D_SHIFT) for c in range(NCORE)], axis=0)[None]
    nwp = np.stack([R[c]["nwp"] for c in range(NCORE)], axis=0)[None]
    npp = np.stack([R[c]["npp"] for c in range(NCORE)], axis=0)[None]
    nss = np.concatenate([R[c]["nss"] for c in range(NCORE)], axis=0)[None]
    nws = np.concatenate([R[c]["nws"].reshape(DB, 8, 64, 64) for c in range(NCORE)], axis=0)[None]
    nps = np.concatenate([R[c]["nps"] for c in range(NCORE)], axis=0)[None]
    out = (y_prompt, y_sample, nsp, nwp, npp, nss, nws, nps)
    return tuple(np.ascontiguousarray(o.astype(np.float32)) for o in out)
```

```python
import numpy as np
from contextlib import ExitStack
import concourse.bass as bass
import concourse.mybir as mybir
from concourse.bass_utils import run_bass_kernel_spmd

F32 = mybir.dt.float32
BF16 = mybir.dt.bfloat16
AF = mybir.ActivationFunctionType
ALU = mybir.AluOpType
AX = mybir.AxisListType

D = 1024
SEQ = 2048
NCORE = 8
DB = 16
DT = 4
NS = DB * DT
D_SHIFT = 1664
D_IN = 3200
C0 = float(np.exp(-0.5))
NORM_EPS = 1e-6
GN_EPS = 64e-5
L2_EPS = 1e-12
TB = 128
NTB = SEQ // TB
CH = 64
FBIAS = 0.0
NCH = TB // CH
WINS = (2, 4, 8, 16)

C_ID, C_ONES, C_MUS, C_MUI, C_MLS, C_RST, C_ICNT, C_END = 0, 128, 256, 320, 384, 448, 960, 1024
PV_NW, PV_MU, PV_W0, PV_A0, PV_KK, PV_KA, PV_RK, PV_GW, PV_GB, PV_PS, PV_END = 0, 8, 21, 25, 29, 33, 37, 41, 45, 49, 53
BRB_W0, BRB_A0, BRB_KK, BRB_KA, BRB_RK, BRB_GW, BRB_GB = 0, 512, 1024, 1536, 2048, 2560, 3072


class Buf:
    __slots__ = ("name", "w", "r")

    def __init__(self, name):
        self.name = name
        self.w = None
        self.r = []


class T:
    def __init__(self, t, name, buf=None):
        self.t = t
        self.b = buf if buf is not None else Buf(name)

    def __getitem__(self, k):
        return self.t[k]


class Sched:
    def __init__(self, nc, n_dma_sems=32):
        self.nc = nc
        self.eng = {}
        for name in ["tensor", "vector", "scalar", "gpsimd", "sync"]:
            h = getattr(nc, name)
            sem = nc.alloc_semaphore(name="prog_" + name)
            self.eng[name] = dict(h=h, sem=sem, cnt=0, waited={})
        self.dma_sems = [dict(sem=nc.alloc_semaphore(name=f"dma{i}"), cnt=0) for i in range(n_dma_sems)]
        self.dma_rr = 0
        self.ninstr = 0
        self.rec = None

    def _wait(self, engname, tok):
        sem, val, src = tok
        e = self.eng[engname]
        key = id(sem)
        if e["waited"].get(key, 0) >= val:
            return
        e["h"].wait_ge(sem, val)
        e["waited"][key] = val
        self.ninstr += 1

    def _deps(self, engname, reads, writes):
        toks = []
        for b in reads:
            if b.w is not None:
                toks.append(b.w)
        for b in writes:
            if b.w is not None:
                toks.append(b.w)
            toks.extend(b.r)
        for tok in toks:
            if tok[2] == engname and engname == "tensor":
                continue
            self._wait(engname, tok)

    @staticmethod
    def _bufs(xs):
        return [x.b if isinstance(x, T) else x for x in xs]

    def _record(self, tok, reads, writes):
        for b in reads:
            b.r.append(tok)
            if len(b.r) > 64:
                b.r = b.r[-64:] if False else b.r
        for b in writes:
            b.w = tok
            b.r = []

    def op(self, engname, fn, reads=(), writes=(), cost=0.3):
        reads = self._bufs(reads)
        writes = self._bufs(writes)
        if self.rec is not None:
            self.rec.append(("op", engname, fn, reads, writes, cost, None))
            return None
        e = self.eng[engname]
        self._deps(engname, reads, writes)
        ins = fn(e["h"])
        e["cnt"] += 1
        ins.then_inc(e["sem"], 1)
        e["waited"][id(e["sem"])] = max(e["waited"].get(id(e["sem"]), 0), 0)
        tok = (e["sem"], e["cnt"], engname)
        self._record(tok, reads, writes)
        self.ninstr += 1
        return tok

    def dma(self, qname, out, in_, reads=(), writes=(), **kw):
        reads = self._bufs(reads)
        writes = self._bufs(writes)
        if self.rec is not None:
            self.rec.append(("dma", qname, (out, in_), reads, writes, 2.5, kw))
            return None
        e = self.eng[qname]
        self._deps(qname, reads, writes)
        d = self.dma_sems[self.dma_rr]
        self.dma_rr = (self.dma_rr + 1) % len(self.dma_sems)
        if d["cnt"] > 0:
            self._wait(qname, (d["sem"], 16 * d["cnt"], "dma"))
        ins = e["h"].dma_start(out=out, in_=in_, **kw)
        d["cnt"] += 1
        ins.then_inc(d["sem"], 16)
        tok = (d["sem"], 16 * d["cnt"], "dma")
        self._record(tok, reads, writes)
        self.ninstr += 1
        return tok

    def emit(self, r):
        kind, eng, fn, reads, writes, cost, kw = r
        if kind == "op":
            self.op(eng, fn, reads=reads, writes=writes)
        else:
            self.dma(eng, fn[0], fn[1], reads=reads, writes=writes, **kw)

    def merge_emit(self, A, B, a_ok, b_ok):
        eng_free = {}
        ready = {}
        acc = {}

        def est(r):
            kind, eng, fn, reads, writes, cost, kw = r
            t = eng_free.get(eng, 0.0)
            for b in reads:
                rt_, re_ = ready.get(id(b), (0.0, eng))
                t = max(t, rt_ + (0.15 if re_ != eng else 0.0))
            for b in writes:
                rt_, re_ = ready.get(id(b), (0.0, eng))
                t = max(t, rt_ + (0.15 if re_ != eng else 0.0), acc.get(id(b), 0.0) + 0.1)
            return t

        def commit(r, t):
            kind, eng, fn, reads, writes, cost, kw = r
            if kind == "dma":
                eng_free[eng] = t + 0.1
                end = t + cost
            else:
                end = t + cost
                eng_free[eng] = end
            for b in reads:
                acc[id(b)] = max(acc.get(id(b), 0.0), end)
            for b in writes:
                ready[id(b)] = (end, eng)
                acc[id(b)] = max(acc.get(id(b), 0.0), end)

        def run_unit(u):
            for r in u:
                commit(r, est(r))
                self.emit(r)

        ia = ib = 0
        ja = jb = 0
        while ia < len(A) or ib < len(B):
            ca = None
            cb = None
            if ia < len(A) and (ja > 0 or a_ok(ia, ib)):
                ca = A[ia][ja]
            if ib < len(B) and (jb > 0 or b_ok(ib, ia)):
                cb = B[ib][jb]
            assert ca is not None or cb is not None, (ia, ib, ja, jb)
            ta = est(ca[0]) if ca is not None else None
            tb_ = est(cb[0]) if cb is not None else None
            if cb is None or (ca is not None and ta + FBIAS < tb_):
                run_unit(ca); ja += 1
                if ja == len(A[ia]):
                    ia += 1; ja = 0
            else:
                run_unit(cb); jb += 1
                if jb == len(B[ib]):
                    ib += 1; jb = 0

    def barrier(self):
        toks = [(e["sem"], e["cnt"], n) for n, e in self.eng.items() if e["cnt"] > 0]
        toks += [(d["sem"], 16 * d["cnt"], "dma") for d in self.dma_sems if d["cnt"] > 0]
        for n in self.eng:
            for tok in toks:
                if tok[2] == n:
                    continue
                self._wait(n, tok)

    def finish(self, tiles, engname="sync"):
        for b in self._bufs(tiles):
            if b.w is not None:
                self._wait(engname, b.w)


class _Stop(Exception):
    pass


def build_program(stop=None):
    nc = bass.Bass("TRN2", target_bir_lowering=False)
    S = Sched(nc)
    try:
        _build_body(nc, S, stop)
    except _Stop:
        S.barrier()
    return nc, S


def _build_body(nc, S, stop):
    def chk(label):
        if stop == label:
            raise _Stop()


    def din(name, shape):
        return nc.dram_tensor(name, list(shape), F32, kind="ExternalInput").ap()

    def dout(name, shape):
        return T(nc.dram_tensor(name, list(shape), F32, kind="ExternalOutput").ap(), name)

    xp = din("xp", [SEQ, D])
    xs = din("xs", [NS, D])
    sshift = din("sshift", [DB, D_SHIFT])
    swkv = din("swkv", [128, 4096])
    spool = din("spool", [DB * 15, 512])
    w_in = din("w_in", [D, D_IN])
    w_out = din("w_out", [D, D])
    wdec = din("wdec", [64, 512])
    waaa = din("waaa", [64, 512])
    poolw = din("poolw", [4, 128, 128])
    pvec_d = din("pvec", [128, PV_END])
    browA_d = din("browA", [1, D_SHIFT])
    browB_d = din("browB", [1, 3584])
    normf_d = din("normf", [1, D])
    cst_d = din("cst", [128, C_END])

    yp = dout("yp", [SEQ, D])
    ys = dout("ys", [NS, D])
    nsp = dout("nsp", [13, 128])
    nwp = dout("nwp", [8, 64, 64])
    npp = dout("npp", [15, 512])
    nss = dout("nss", [DB, D_SHIFT])
    nws = dout("nws", [128, 4096])
    nps = dout("nps", [DB, 15, 512])
    scr1 = T(nc.dram_tensor("scr1", [6, DT, DB, 8, 64], F32, kind="Internal").ap(), "scr1")
    scr2 = T(nc.dram_tensor("scr2", [DB, 8, DT, 64], F32, kind="Internal").ap(), "scr2")

    es_top = ExitStack()

    def sb(es, name, shape, dt=F32):
        return T(es.enter_context(nc.sbuf_tensor("s_" + name, list(shape), dt)), name)

    def pst(name, shape, dt=F32):
        return T(nc.alloc_psum_tensor("p_" + name, list(shape), dt), name)

    def nel(ap):
        n = 1
        for s_ in ap.shape[1:]:
            n *= s_
        return n

    def mm(out, lhsT, rhs, start, stop, reads, writes):
        passes = 4 if lhsT.dtype == F32 else 1
        c_ = max(0.055, nel(rhs) * passes / 2000.0 + 0.03)
        S.op("tensor", lambda e: e.matmul(out, lhsT=lhsT, rhs=rhs, start=start, stop=stop), reads=reads, writes=writes, cost=c_)

    def tr(out, in_, ident, reads, writes):
        S.op("tensor", lambda e: e.transpose(out, in_, ident), reads=reads, writes=writes, cost=0.13)

    def act(out, in_, func, reads, writes, bias=None, scale=None, eng="scalar"):
        kw = {}
        if bias is not None:
            kw["bias"] = bias
        if scale is not None:
            kw["scale"] = scale
        S.op("scalar", lambda e: e.activation(out=out, in_=in_, func=func, **kw), reads=reads, writes=writes,
             cost=0.1 + 0.1 * len(kw) + nel(in_) * 0.00095)

    def ecost(eng, n):
        return 0.08 + n * (0.00105 if eng == "vector" else 0.0025)

    def vtt(out, in0, in1, op, reads, writes, eng="vector"):
        S.op(eng, lambda e: e.tensor_tensor(out=out, in0=in0, in1=in1, op=op), reads=reads, writes=writes, cost=ecost(eng, nel(out)))

    def vts(out, in0, s1, s2, op0, op1, reads, writes, eng="vector"):
        if op1 is None:
            S.op(eng, lambda e: e.tensor_scalar(out=out, in0=in0, scalar1=s1, scalar2=None, op0=op0), reads=reads, writes=writes,
                 cost=ecost(eng, nel(out)))
        else:
            S.op(eng, lambda e: e.tensor_scalar(out=out, in0=in0, scalar1=s1, scalar2=s2, op0=op0, op1=op1), reads=reads, writes=writes,
                 cost=ecost(eng, nel(out)))

    def vstt(out, in0, scalar, in1, op0, op1, reads, writes):
        S.op("vector", lambda e: e.scalar_tensor_tensor(out=out, in0=in0, scalar=scalar, in1=in1, op0=op0, op1=op1), reads=reads, writes=writes,
             cost=ecost("vector", nel(out)))

    def vcopy(out, in_, reads, writes, eng="vector"):
        S.op(eng, lambda e: e.tensor_copy(out=out, in_=in_), reads=reads, writes=writes, cost=ecost(eng, nel(out)))

    def vred(out, in_, reads, writes):
        S.op("vector", lambda e: e.tensor_reduce(out=out, in_=in_, axis=AX.X, op=ALU.add), reads=reads, writes=writes,
             cost=ecost("vector", nel(in_)))

    def vrecip(out, in_, reads, writes):
        S.op("vector", lambda e: e.reciprocal(out=out, in_=in_), reads=reads, writes=writes, cost=0.08 + nel(out) * 0.0084)

    def memset(ap, val, writes, eng="gpsimd"):
        S.op(eng, lambda e: e.memset(ap, val), writes=writes)

    def rsqrt_small(out, in_, tmp, scale, eps, reads, writes):
        act(tmp, in_, AF.Sqrt, reads=reads, writes=writes, bias=None, scale=None) if False else None
        vts(tmp, in_, scale, eps, ALU.mult, ALU.add, reads=reads, writes=writes)
        act(tmp, tmp, AF.Sqrt, reads=writes, writes=writes)
        vrecip(out, tmp, reads=writes, writes=writes)

    pg = [pst(f"pg{i}", [128, 512]) for i in range(2)]
    pT = pst("pT", [128, 1024], BF16)
    pM = pst("pM", [128, 512])
    pA = pst("pA", [128, 512])
    pB = pst("pB", [128, 512])
    pC = pst("pC", [128, 512])
    pD = pst("pD", [128, 512])

    cst = sb(es_top, "cst", [128, C_END])
    pvec = sb(es_top, "pvec", [128, PV_END])
    omu = sb(es_top, "omu", [128, 13])
    omka = sb(es_top, "omka", [128, 4])
    identb = sb(es_top, "identb", [128, 128], BF16)
    winb = sb(es_top, "winb", [128, 8, D_IN], BF16)
    woutb = sb(es_top, "woutb", [128, 8, D], BF16)
    wd = sb(es_top, "wd", [64, 512])
    wa = sb(es_top, "wa", [128, 512])
    pw = sb(es_top, "pw", [128, 4, 128])
    normf = sb(es_top, "normf", [128, D])

    ident = cst[:, C_ID:C_ID + 128]
    onesblk = cst[:, C_ONES:C_ONES + 128]

    S.dma("sync", cst[:], cst_d, writes=[cst])
    S.dma("sync", pvec[:], pvec_d, writes=[pvec])
    S.dma("sync", wd[:], wdec, writes=[wd])
    S.dma("sync", wa[64:128, :], waaa, writes=[wa])
    S.dma("sync", pw[:], poolw.rearrange("g c e -> c g e"), writes=[pw])
    S.dma("sync", normf[:], normf_d.partition_broadcast(128), writes=[normf])
    vcopy(identb[:], ident, reads=[cst], writes=[identb])
    vts(omka[:], pvec[:, PV_KA:PV_KA + 4], -1.0, 1.0, ALU.mult, ALU.add, reads=[pvec], writes=[omka])

    with ExitStack() as es:
        stg = [sb(es, f"stg{i}", [128, D_IN]) for i in range(3)]
        for dc in range(8):
            st = stg[dc % 3]
            S.dma("sync", st[:], w_in[dc * 128:(dc + 1) * 128, :], writes=[st])
            h = D_IN // 2
            vts(winb[:, dc, 0:h], st[:, 0:h], pvec[:, PV_NW + dc:PV_NW + dc + 1], None, ALU.mult, None, reads=[st, pvec], writes=[winb])
            act(winb[:, dc, h:], st[:, h:], AF.Copy, reads=[st, pvec], writes=[winb], scale=pvec[:, PV_NW + dc:PV_NW + dc + 1])
        S.barrier()
        chk("W")

    def final_tile(es_tiles, n, x_t, oT_list, out_dram_ap, out_T):
        res, sq, ssum, tmp1, rstd, yo = es_tiles
        for half in range(2):
            bank = pD if half == 0 else pC
            for fc in range(8):
                mm(bank[0:n, :], oT_list[fc], woutb[:, fc, half * 512:(half + 1) * 512], fc == 0, fc == 7,
                   reads=[oT_list_T, woutb], writes=[bank])
            vtt(res[0:n, half * 512:(half + 1) * 512], bank[0:n, :], x_t[0:n, half * 512:(half + 1) * 512], ALU.add,
                reads=[bank, x_t], writes=[res])
        act(sq[0:n, :], res[0:n, :], AF.Square, reads=[res], writes=[sq])
        vred(ssum[0:n, :], sq[0:n, :], reads=[sq], writes=[ssum])
        rsqrt_small(rstd[0:n, :], ssum[0:n, :], tmp1[0:n, :], 1.0 / D, NORM_EPS, reads=[ssum], writes=[tmp1, rstd])
        vstt(yo[0:n, :], res[0:n, :], rstd[0:n, 0:1], normf[0:n, :], ALU.mult, ALU.mult, reads=[res, rstd, normf], writes=[yo])
        S.dma("sync", out_dram_ap, yo[0:n, :], reads=[yo], writes=[out_T])

    oT_list_T = None

    with ExitStack() as es:
        browB = sb(es, "browB", [NS, 3584])
        S.dma("sync", browB[:], browB_d.partition_broadcast(NS), writes=[browB])
        x_s = sb(es, "x_s", [NS, D])
        S.dma("sync", x_s[:], xs, writes=[x_s])
        hTs = sb(es, "hTs", [128, 8, DB + NS], BF16)
        graw_s = sb(es, "graw_s", [NS, 512])
        u_s = sb(es, "u_s", [NS, 512])
        gp_s = sb(es, "gp_s", [NS, 512])
        bonus_s = sb(es, "bonus_s", [NS, 512])
        st8 = sb(es, "st8", [NS, 8])
        st8b = sb(es, "st8b", [NS, 8])
        st8c = sb(es, "st8c", [NS, 8])

        def v3(ap):
            return ap.rearrange("p (h k) -> p h k", k=64)

        def bc8(ap8):
            return ap8.unsqueeze(2).to_broadcast([NS, 8, 64])

        with ExitStack() as e1:
            browA = sb(e1, "browA", [NS, D_SHIFT])
            S.dma("sync", browA[:], browA_d.partition_broadcast(NS), writes=[browA])
            omka_b = sb(e1, "omka_b", [NS, 512])
            vts(omka_b[:], browB[:, BRB_KA:BRB_KA + 512], -1.0, 1.0, ALU.mult, ALU.add, reads=[browB], writes=[omka_b])
            sq_s = sb(e1, "sq_s", [NS, D])
            ss_s = sb(e1, "ss_s", [NS, 1])
            t1_s = sb(e1, "t1_s", [NS, 1])
            rstd_s = sb(e1, "rstd_s", [NS, 1])
            xn_s = sb(e1, "xn_s", [NS, D], BF16)
            act(sq_s[:], x_s[:], AF.Square, reads=[x_s], writes=[sq_s])
            vred(ss_s[:], sq_s[:], reads=[sq_s], writes=[ss_s])
            rsqrt_small(rstd_s[:], ss_s[:], t1_s[:], 1.0 / D, NORM_EPS, reads=[ss_s], writes=[t1_s, rstd_s])
            vts(xn_s[:], x_s[:], rstd_s[:, 0:1], None, ALU.mult, None, reads=[x_s, rstd_s], writes=[xn_s])
            memset(hTs[:, :, 0:DB], 0.0, writes=[hTs])
            for dc in range(8):
                tr(pT[:, dc * 128:dc * 128 + NS], xn_s[:, dc * 128:(dc + 1) * 128], identb[0:NS, 0:NS], reads=[xn_s, identb], writes=[pT])
            vcopy(hTs[:, :, DB:DB + NS], pT[:].rearrange("p (c t) -> p c t", t=128)[:, :, 0:NS], reads=[pT], writes=[hTs])

            p_s = sb(e1, "p_s", [NS, D_SHIFT])
            prev_s = sb(e1, "prev_s", [NS, D_SHIFT])
            col_chunks = [(0, 512), (512, 512), (1024, 512), (1536, 128)]
            kk_ = 0
            for (c0, n) in col_chunks:
                bank = pg[kk_ % 2]; kk_ += 1
                for dc in range(8):
                    mm(bank[0:NS, 0:n], hTs[:, dc, DB:DB + NS], winb[:, dc, c0:c0 + n], dc == 0, dc == 7, reads=[hTs, winb], writes=[bank])
                act(p_s[:, c0:c0 + n], bank[0:NS, 0:n], AF.Copy, reads=[bank], writes=[p_s])
                bank = pg[kk_ % 2]; kk_ += 1
                for dc in range(8):
                    mm(bank[0:NS, 0:n], hTs[:, dc, 0:NS], winb[:, dc, c0:c0 + n], dc == 0, dc == 7, reads=[hTs, winb], writes=[bank])
                vcopy(prev_s[:, c0:c0 + n], bank[0:NS, 0:n], reads=[bank], writes=[prev_s])
            for (c0, dst, fn) in [(1664, graw_s, AF.Silu), (2176, u_s, AF.Copy), (2688, gp_s, AF.Silu)]:
                bank = pg[kk_ % 2]; kk_ += 1
                for dc in range(8):
                    mm(bank[0:NS, :], hTs[:, dc, DB:DB + NS], winb[:, dc, c0:c0 + 512], dc == 0, dc == 7, reads=[hTs, winb], writes=[bank])
                act(dst[:], bank[0:NS, :], fn, reads=[bank], writes=[dst])
            S.dma("sync", prev_s[0:DB, :], sshift, writes=[prev_s])
            S.dma("sync", nss[:], p_s[NS - DB:NS, :], reads=[p_s], writes=[nss])
            S.dma("sync", nps[:, 0:11, :], spool.rearrange("(b j) c -> b j c", j=15)[:, 4:15, :], writes=[nps])
            for t in range(DT):
                S.dma("sync", nps[:, 11 + t, :], u_s[t * DB:(t + 1) * DB, :], reads=[u_s], writes=[nps])

            vtt(prev_s[:], prev_s[:], p_s[:], ALU.subtract, reads=[prev_s, p_s], writes=[prev_s])
            vtt(prev_s[:], prev_s[:], browA[:], ALU.mult, reads=[prev_s, browA], writes=[prev_s])
            vtt(prev_s[:], prev_s[:], p_s[:], ALU.add, reads=[prev_s, p_s], writes=[prev_s])
            ps_s = prev_s
            r_s = ps_s[:, 0:512]
            k_s = ps_s[:, 512:1024]
            v_s = ps_s[:, 1024:1536]

            lT = sb(e1, "lT", [128, NS])
            tr(pM[:, 0:NS], ps_s[:, 1536:1664], ident[0:NS, 0:NS], reads=[ps_s, cst], writes=[pM])
            act(lT[0:64, :], pM[0:64, 0:NS], AF.Tanh, reads=[pM], writes=[lT])
            act(lT[64:128, :], pM[64:128, 0:NS], AF.Copy, reads=[pM], writes=[lT])
            sg_s = sb(e1, "sg_s", [NS, 512])
            a_s = sb(e1, "a_s", [NS, 512])
            mm(pA[0:NS, :], lT[0:64, :], wd[:, :], True, True, reads=[lT, wd], writes=[pA])
            vtt(sg_s[:], pA[0:NS, :], browB[:, BRB_W0:BRB_W0 + 512], ALU.add, reads=[pA, browB], writes=[sg_s])
            act(sg_s[:], sg_s[:], AF.Sigmoid, reads=[sg_s], writes=[sg_s])
            mm(pB[0:NS, :], lT[64:128, :], wa[64:128, :], True, True, reads=[lT, wa], writes=[pB])
            vtt(a_s[:], pB[0:NS, :], browB[:, BRB_A0:BRB_A0 + 512], ALU.add, reads=[pB, browB], writes=[a_s])
            act(a_s[:], a_s[:], AF.Sigmoid, reads=[a_s], writes=[a_s])

            pk = sb(e1, "pk", [NS, 4, 512])
            PQ = {1: 0, 2: 1, 4: 2, 5: 3}
            tmpA = sb(e1, "tmpA", [NS, 512])
            tmpB = sb(e1, "tmpB", [NS, 512])
            act(pk[:, PQ[1], :], sg_s[:], AF.Exp, reads=[sg_s], writes=[pk], scale=-C0)
            vtt(tmpA[:], k_s, browB[:, BRB_KK:BRB_KK + 512], ALU.mult, reads=[ps_s, browB], writes=[tmpA])
            vtt(tmpB[:], tmpA[:], tmpA[:], ALU.mult, reads=[tmpA], writes=[tmpB])
            vred(st8[:], v3(tmpB[:]), reads=[tmpB], writes=[st8])
            rsqrt_small(st8b[:], st8[:], st8c[:], 1.0, L2_EPS, reads=[st8], writes=[st8c, st8b])
            vtt(v3(tmpA[:]), v3(tmpA[:]), bc8(st8b[:]), ALU.mult, reads=[tmpA, st8b], writes=[tmpA])
            vts(pk[:, PQ[4], :], tmpA[:], -1.0, None, ALU.mult, None, reads=[tmpA], writes=[pk])
            vtt(pk[:, PQ[5], :], tmpA[:], a_s[:], ALU.mult, reads=[tmpA, a_s], writes=[pk])
            vtt(tmpB[:], a_s[:], browB[:, BRB_KA:BRB_KA + 512], ALU.mult, reads=[a_s, browB], writes=[tmpB])
            vtt(tmpB[:], tmpB[:], omka_b[:], ALU.add, reads=[tmpB, omka_b], writes=[tmpB])
            vtt(pk[:, PQ[2], :], k_s, tmpB[:], ALU.mult, reads=[ps_s, tmpB], writes=[pk])
            vtt(tmpB[:], r_s, browB[:, BRB_RK:BRB_RK + 512], ALU.mult, reads=[ps_s, browB], writes=[tmpB])
            vtt(tmpB[:], tmpB[:], pk[:, PQ[2], :], ALU.mult, reads=[tmpB, pk], writes=[tmpB])
            vred(st8[:], v3(tmpB[:]), reads=[tmpB], writes=[st8])
            vtt(v3(bonus_s[:]), v3(v_s), bc8(st8[:]), ALU.mult, reads=[ps_s, st8], writes=[bonus_s])
            sview = scr1[:].rearrange("q t b h k -> q (t b) (h k)")
            S.dma("sync", sview[0], r_s, reads=[ps_s], writes=[scr1])
            S.dma("sync", sview[3], v_s, reads=[ps_s], writes=[scr1])
            for qq, slot in PQ.items():
                S.dma("sync", sview[qq], pk[:, slot, :], reads=[pk], writes=[scr1])
            S.finish([scr1], engname="sync")
            S.barrier()
            chk("S1")

        with ExitStack() as e2:
            sIn = sb(e2, "sIn", [128, 6, DT, 64])
            S.dma("sync", sIn[:], scr1[:].rearrange("q t b h k -> (b h) q t k"), reads=[scr1], writes=[sIn])
            St = sb(e2, "St", [128, 64, 64])
            S.dma("sync", St[:].rearrange("p v k -> p (v k)"), swkv, writes=[St])
            tmpS = sb(e2, "tmpS", [128, 64, 64])
            sa = sb(e2, "sa", [128, 64])
            yS = sb(e2, "yS", [128, DT, 64])
            stgo = [sb(e2, f"stgo{i}", [128, D]) for i in range(3)]
            for fc in range(8):
                so = stgo[fc % 3]
                S.dma("sync", so[:], w_out[fc * 128:(fc + 1) * 128, :], writes=[so])
                act(woutb[:, fc, :], so[:], AF.Copy, reads=[so], writes=[woutb])

            def bv(ap):
                return ap.unsqueeze(1).to_broadcast([128, 64, 64])

            def bk(ap):
                return ap.unsqueeze(2).to_broadcast([128, 64, 64])

            for t in range(DT):
                q = lambda i: sIn[:, i, t, :]
                vtt(tmpS[:], St[:], bv(q(4)), ALU.mult, reads=[St, sIn], writes=[tmpS])
                vred(sa[:], tmpS[:], reads=[tmpS], writes=[sa])
                vtt(St[:], St[:], bv(q(1)), ALU.mult, reads=[St, sIn], writes=[St])
                vtt(tmpS[:], bk(sa[:]), bv(q(5)), ALU.mult, reads=[sa, sIn], writes=[tmpS])
                vtt(St[:], St[:], tmpS[:], ALU.add, reads=[St, tmpS], writes=[St])
                vtt(tmpS[:], bk(q(3)), bv(q(2)), ALU.mult, reads=[sIn], writes=[tmpS])
                vtt(St[:], St[:], tmpS[:], ALU.add, reads=[St, tmpS], writes=[St])
                vtt(tmpS[:], St[:], bv(q(0)), ALU.mult, reads=[St, sIn], writes=[tmpS])
                vred(yS[:, t, :], tmpS[:], reads=[tmpS], writes=[yS])
            S.dma("sync", nws[:], St[:].rearrange("p v k -> p (v k)"), reads=[St], writes=[nws])
            S.dma("sync", scr2[:].rearrange("b h t v -> (b h) t v"), yS[:], reads=[yS], writes=[scr2])
            S.finish([scr2, nws], engname="sync")
            S.barrier()
            chk("S2")

        with ExitStack() as e3:
            yT = sb(e3, "yT", [NS, 512])
            tmpA = sb(e3, "tmpA3", [NS, 512])
            for t in range(DT):
                S.dma("sync", yT[t * DB:(t + 1) * DB, :].rearrange("b (h v) -> b h v", v=64), scr2[:][:, :, t, :], reads=[scr2], writes=[yT])
            vred(st8[:], v3(yT[:]), reads=[yT], writes=[st8])
            vts(st8[:], st8[:], 1.0 / 64, None, ALU.mult, None, reads=[st8], writes=[st8])
            vtt(v3(yT[:]), v3(yT[:]), bc8(st8[:]), ALU.subtract, reads=[yT, st8], writes=[yT])
            vtt(tmpA[:], yT[:], yT[:], ALU.mult, reads=[yT], writes=[tmpA])
            vred(st8[:], v3(tmpA[:]), reads=[tmpA], writes=[st8])
            rsqrt_small(st8b[:], st8[:], st8c[:], 1.0 / 64, GN_EPS, reads=[st8], writes=[st8c, st8b])
            vtt(v3(yT[:]), v3(yT[:]), bc8(st8b[:]), ALU.mult, reads=[yT, st8b], writes=[yT])
            vtt(yT[:], yT[:], browB[:, BRB_GW:BRB_GW + 512], ALU.mult, reads=[yT, browB], writes=[yT])
            vtt(yT[:], yT[:], browB[:, BRB_GB:BRB_GB + 512], ALU.add, reads=[yT, browB], writes=[yT])
            vtt(yT[:], yT[:], bonus_s[:], ALU.add, reads=[yT, bonus_s], writes=[yT])
            vtt(yT[:], yT[:], graw_s[:], ALU.mult, reads=[yT, graw_s], writes=[yT])
            oTs = sb(e3, "oTs", [128, 8, NS], BF16)
            for fb in range(4):
                tr(pA[:, fb * 64:fb * 64 + NS], yT[:, fb * 128:(fb + 1) * 128], ident[0:NS, 0:NS], reads=[yT, cst], writes=[pA])
            vcopy(oTs[:, 0:4, :], pA[:, 0:4 * NS].rearrange("p (f t) -> p f t", t=NS), reads=[pA], writes=[oTs])

            uext = sb(e3, "uext_s", [128, 4, DB, 19])
            sp0 = sb(e3, "sp0", [120, 512])
            sp1 = sb(e3, "sp1", [120, 512])
            S.dma("sync", sp0[:], spool[0:120, :], writes=[sp0])
            S.dma("sync", sp1[:], spool[120:240, :], writes=[sp1])
            for g in range(4):
                tr(pB[:, 0:120], sp0[:, g * 128:(g + 1) * 128], ident[0:120, 0:120], reads=[sp0, cst], writes=[pB])
                tr(pB[:, 128:248], sp1[:, g * 128:(g + 1) * 128], ident[0:120, 0:120], reads=[sp1, cst], writes=[pB])
                vcopy(uext[:, g, 0:8, 0:15], pB[:, 0:120].rearrange("p (b j) -> p b j", j=15), reads=[pB], writes=[uext])
                vcopy(uext[:, g, 8:16, 0:15], pB[:, 128:248].rearrange("p (b j) -> p b j", j=15), reads=[pB], writes=[uext])
                tr(pM[:, 0:NS], u_s[:, g * 128:(g + 1) * 128], ident[0:NS, 0:NS], reads=[u_s, cst], writes=[pM])
                vcopy(uext[:, g, :, 15:19], pM[:, 0:NS].rearrange("p (t b) -> p b t", b=DB), reads=[pM], writes=[uext])
            s2 = sb(e3, "s2_s", [128, 4, DB, 19])
            s4 = sb(e3, "s4_s", [128, 3, DB, 19])
            s8 = sb(e3, "s8_s", [128, 2, DB, 19])
            s16 = sb(e3, "s16_s", [128, 1, DB, 19])
            d_s = sb(e3, "d_s", [128, 4, DT, DB])
            vtt(s2[:, :, :, 1:19], uext[:, :, :, 1:19], uext[:, :, :, 0:18], ALU.add, reads=[uext], writes=[s2])
            vtt(s4[:, :, :, 3:19], s2[:, 1:4, :, 3:19], s2[:, 1:4, :, 1:17], ALU.add, reads=[s2], writes=[s4])
            vtt(s8[:, :, :, 7:19], s4[:, 1:3, :, 7:19], s4[:, 1:3, :, 3:15], ALU.add, reads=[s4], writes=[s8])
            vtt(s16[:, :, :, 15:19], s8[:, 1:2, :, 15:19], s8[:, 1:2, :, 7:11], ALU.add, reads=[s8], writes=[s16])
            tots = [(s2, 0), (s4, 1), (s8, 2), (s16, 3)]
            for g in range(4):
                tt, off = tots[g]
                vstt(d_s[:, g, :, :].rearrange("p t b -> p b t"), tt[:, g - off, :, 15:19], 1.0 / WINS[g], uext[:, g, :, 15:19],
                     ALU.mult, ALU.subtract, reads=[tt, uext], writes=[d_s])
            gpT = sb(e3, "gpT", [128, 4, NS])
            for g in range(4):
                tr(pM[:, 64 + g * 64:64 + g * 64 + NS], gp_s[:, g * 128:(g + 1) * 128], ident[0:NS, 0:NS], reads=[gp_s, cst], writes=[pM])
            vcopy(gpT[:], pM[:, 64:64 + 4 * NS].rearrange("p (g t) -> p g t", t=NS), reads=[pM], writes=[gpT])
            for g in range(4):
                mm(pA[:, g * 64:g * 64 + NS], pw[:, g, :], d_s[:, g, :, :].rearrange("p t b -> p (t b)"), True, True, reads=[pw, d_s], writes=[pA])
            for g in range(4):
                vstt(oTs[:, 4 + g, :], pA[:, g * 64:g * 64 + NS], pvec[:, PV_PS + g:PV_PS + g + 1], gpT[:, g, :], ALU.mult, ALU.mult,
                     reads=[pA, pvec, gpT], writes=[oTs])

            sq = sb(e3, "sq2_s", [NS, D]); ssum = sb(e3, "ssum_s", [NS, 1])
            tmp1 = sb(e3, "tmp1_s", [NS, 1]); rstd = sb(e3, "rstd2_s", [NS, 1]); yo = sb(e3, "yo_s", [NS, D])
            oT_list_T = oTs
            final_tile((x_s, sq, ssum, tmp1, rstd, yo), NS, x_s, [oTs[:, fc, :] for fc in range(8)], ys[:], ys)
            S.finish([ys, nss, nps], engname="sync")
            S.barrier()
            chk("S3")

    with ExitStack() as es:
        def sbl(name, shape, dt=F32, n=2):
            return [sb(es, f"{name}_{i}", shape, dt) for i in range(n)]

        xt = sbl("xt", [128, D])
        yo = sb(es, "yo", [128, D])
        ssx = sb(es, "ssx", [128, 1]); t1x = sb(es, "t1x", [128, 1]); rsx = sb(es, "rsx", [128, 1])
        xnb = sb(es, "xnb", [128, D], BF16)
        sqx = xnb
        hT = sb(es, "hT", [128, 8, TB], BF16)
        praw = sbl("praw", [128, 4, TB + 1])
        halo = sb(es, "halo", [128, 13])
        omu = sb(es, "omu2", [128, 13])
        psr = sb(es, "psr", [128, 4, TB]); psk = sb(es, "psk", [128, 4, TB]); psv = sb(es, "psv", [128, 4, TB])
        ps12 = sb(es, "ps12", [128, TB])
        psx = [T(g_[:, i, :], f"psx{gi_}_{i}", buf=g_.b) for gi_, g_ in enumerate([psr, psk, psv]) for i in range(4)] + [ps12]
        sg = sb(es, "sg", [128, 4, TB]); av = sb(es, "av", [128, 4, TB])
        gsil = sbl("gsil", [128, 4, TB], BF16)
        gpsil = sb(es, "gpsil", [128, 4, TB], BF16)
        uext = sb(es, "uext", [128, 4, 15 + TB])
        th = sb(es, "th", [64, TB])
        wbig = [sb(es, f"wbig{i}", [128, 4, TB]) for i in range(4)]
        w1, w2, w3, w4 = wbig
        srot = [T(wbig[i][:].rearrange("p f t -> p (f t)")[:, 0:15 + TB], f"srot{i}", buf=wbig[i].b) for i in range(4)]
        kkn = sb(es, "kkn", [128, 4, TB]); kmod = sb(es, "kmod", [128, 4, TB]); bv_ = sb(es, "bv_", [128, 4, TB])
        cum = sb(es, "cum", [128, 4, TB])
        dpl = T(kkn[:, 0, :], "dpl", buf=kkn.b)
        at = sbl("at", [64, 4, 2, TB], BF16)
        rt = sbl("rt", [64, 4, 2, TB], BF16)
        bt = sb(es, "bt", [64, 4, 2, TB], BF16)
        kt = sb(es, "kt", [64, 4, 2, TB], BF16)
        bh = sb(es, "bh", [128, 4, TB], BF16); kh = sb(es, "kh", [128, 4, TB], BF16); vb = sb(es, "vb", [128, 4, TB], BF16)
        bon = sbl("bon", [128, 4, TB])
        gC = sbl("gC", [64, 4, 2, NCH])
        VT = [[sb(es, f"VT{p}{c}", [64, 512], BF16) for c in range(NCH)] for p in range(2)]
        BKT = [[sb(es, f"BKT{p}{c}", [64, 1024], BF16) for c in range(NCH)] for p in range(2)]
        Aak = [[sb(es, f"Aak{p}{c}", [64, 512], BF16) for c in range(NCH)] for p in range(2)]
        Arb = [[sb(es, f"Arb{p}{c}", [64, 512], BF16) for c in range(NCH)] for p in range(2)]
        Ark = [[sb(es, f"Ark{p}{c}", [64, 512], BF16) for c in range(NCH)] for p in range(2)]
        Minv = [[sb(es, f"Minv{p}{c}", [64, 512], BF16) for c in range(NCH)] for p in range(2)]
        Nsb = [sb(es, f"Nsb{c}", [64, 512], BF16) for c in range(NCH)]
        NTsb = [sb(es, f"NTsb{c}", [64, 512], BF16) for c in range(NCH)]
        Xa0 = [sb(es, f"Xa0{c}", [64, 512], BF16) for c in range(NCH)]
        XTa0 = [sb(es, f"XTa0{c}", [64, 512], BF16) for c in range(NCH)]
        Qtmp = [sb(es, f"Qtmp{c}", [64, 512], BF16) for c in range(NCH)]
        ST = sb(es, "ST", [64, 8, 64]); STb = sb(es, "STb", [64, 8, 64], BF16)
        Wsb = sb(es, "Wsb", [64, 512], BF16); Usb = sb(es, "Usb", [64, 512], BF16)
        yc = sb(es, "yc", [64, 512]); ysq = sb(es, "ysq", [64, 512])
        STt = T(ysq[:].rearrange("p (h v) -> p h v", v=64), "STt", buf=ysq.b)
        m8 = sb(es, "m8", [64, 8]); v8 = sb(es, "v8", [64, 8]); r8 = sb(es, "r8", [64, 8]); t8 = sb(es, "t8", [64, 8])
        o1 = sb(es, "o1", [128, 4, 64])
        oT = sbl("oT", [128, 8, TB], BF16)
        ssum = sb(es, "ssum", [128, 1]); tmp1 = sb(es, "tmp1", [128, 1]); rstd = sb(es, "rstd", [128, 1])
        ppT = T(ysq[0:16, :], "ppT", buf=ysq.b); m13 = sb(es, "m13", [13, 128])
        SvT = T(yc[:].rearrange("p (h k) -> p h k", k=64), "SvT", buf=yc.b)

        memset(halo[:], 0.0, writes=[halo])
        memset(uext[:, :, 0:15], 0.0, writes=[uext])
        memset(ST[:], 0.0, writes=[ST])
        memset(STb[:], 0.0, writes=[STb])
        vts(omu[:], pvec[:, PV_MU:PV_MU + 13], -1.0, 1.0, ALU.mult, ALU.add, reads=[pvec], writes=[omu])

        def b8(ap):
            return ap.unsqueeze(1).to_broadcast([64, 8, 64])

        def h3(ap):
            return ap.rearrange("p (h v) -> p h v", v=64)

        def hc(h):
            return slice(h * 64, (h + 1) * 64)

        maskUs = b8(cst[0:64, C_MUS:C_MUS + 64])
        maskUi = b8(cst[0:64, C_MUI:C_MUI + 64])
        maskLs = b8(cst[0:64, C_MLS:C_MLS + 64])
        ident8 = b8(cst[0:64, C_ID:C_ID + 64])
        rstm = cst[:, C_RST:C_RST + 512]
        st = dict(gk=0, ak=0)
        pT32 = T(pT[:].bitcast(F32), "pT32", buf=pT.b)
        abanks = [pA, pB, pg[0], pg[1], pM, pT32]

        def nextbank():
            b = abanks[st["ak"] % len(abanks)]
            st["ak"] += 1
            return b

        def front(tb):
            pb = tb % 2
            t0 = tb * TB
            x_t = xt[pb]
            S.dma("sync", x_t[:], xp[t0:t0 + TB, :], writes=[x_t])
            act(sqx[:], x_t[:], AF.Square, reads=[x_t], writes=[sqx])
            vred(ssx[:], sqx[:], reads=[sqx], writes=[ssx])
            rsqrt_small(rsx[:], ssx[:], t1x[:], 1.0 / D, NORM_EPS, reads=[ssx], writes=[t1x, rsx])
            act(xnb[:], x_t[:], AF.Copy, reads=[x_t, rsx], writes=[xnb], scale=rsx[:, 0:1])
            yield
            for dc in range(8):
                tr(pT[:, dc * 128:(dc + 1) * 128], xnb[:, dc * 128:(dc + 1) * 128], identb[:], reads=[xnb, identb], writes=[pT])
            vcopy(hT[:].rearrange("p c t -> p (c t)"), pT[:], reads=[pT], writes=[hT])
            yield

            def gemm_group(ebs):
                bank = pg[st["gk"] % 2]
                st["gk"] += 1
                for i, eb in enumerate(ebs):
                    for dc in range(8):
                        mm(bank[:, i * TB:(i + 1) * TB], winb[:, dc, eb * 128:(eb + 1) * 128], hT[:, dc, :], dc == 0, dc == 7,
                           reads=[winb, hT], writes=[bank])
                return bank

            for gi, ebs in enumerate([[0, 1, 2, 3], [4, 5, 6, 7], [8, 9, 10, 11], [12]]):
                bank = gemm_group(ebs)
                yield
                n = len(ebs)
                pr = praw[gi % 2]
                e0 = ebs[0]
                gT = [psr, psk, psv, ps12][gi]
                dst = gT[:, 0:n, :] if gi < 3 else ps12[:].unsqueeze(1)
                mub = pvec[:, PV_MU + e0:PV_MU + e0 + n].unsqueeze(2).to_broadcast([128, n, TB])
                vcopy(pr[:, 0:n, 0:1], halo[:, e0:e0 + n].unsqueeze(2), reads=[halo], writes=[pr], eng="gpsimd")
                act(pr[:, 0:n, 1:TB + 1], bank[:, 0:n * TB].rearrange("p (e t) -> p e t", t=TB), AF.Copy, reads=[bank], writes=[pr])
                vtt(dst, pr[:, 0:n, 0:TB], pr[:, 0:n, 1:TB + 1], ALU.subtract, reads=[pr], writes=[gT])
                vtt(dst, dst, mub, ALU.mult, reads=[gT, pvec], writes=[gT])
                vtt(dst, dst, pr[:, 0:n, 1:TB + 1], ALU.add, reads=[gT, pr], writes=[gT])
                vcopy(halo[:, e0:e0 + n].unsqueeze(2), pr[:, 0:n, TB:TB + 1], reads=[pr], writes=[halo], eng="gpsimd")
                yield
            bank = gemm_group([13, 14, 15, 16])
            act(gsil[pb][:].rearrange("p f t -> p (f t)"), bank[:, :], AF.Silu, reads=[bank], writes=[gsil[pb]])
            yield
            bank = gemm_group([17, 18, 19, 20])
            act(uext[:, :, 15:15 + TB], bank[:, :].rearrange("p (g t) -> p g t", t=TB), AF.Copy, reads=[bank], writes=[uext])
            yield
            bank = gemm_group([21, 22, 23, 24])
            act(gpsil[:].rearrange("p g t -> p (g t)"), bank[:, :], AF.Silu, reads=[bank], writes=[gpsil])
            yield

            act(th[:], psx[12][0:64, :], AF.Tanh, reads=[psx[12]], writes=[th])
            for fb in range(4):
                mm(pA[:, fb * TB:(fb + 1) * TB], wd[:, fb * 128:(fb + 1) * 128], th[:], True, True, reads=[wd, th], writes=[pA])
            for fb in range(4):
                mm(pM[:, fb * TB:(fb + 1) * TB], wa[64:128, fb * 128:(fb + 1) * 128], psx[12][64:128, :], True, True, reads=[wa, psx[12]], writes=[pM])
            for fb in range(4):
                act(sg[:, fb, :], pA[:, fb * TB:(fb + 1) * TB], AF.Sigmoid, reads=[pA, pvec], writes=[sg], bias=pvec[:, PV_W0 + fb:PV_W0 + fb + 1])
                act(av[:, fb, :], pM[:, fb * TB:(fb + 1) * TB], AF.Sigmoid, reads=[pM, pvec], writes=[av], bias=pvec[:, PV_A0 + fb:PV_A0 + fb + 1])
            yield

            def pb4(col):
                return pvec[:, col:col + 4].unsqueeze(2).to_broadcast([128, 4, TB])

            def f2(t_):
                return t_[:].rearrange("p f t -> p (f t)")

            vcopy(vb[:], psv[:], reads=[psv], writes=[vb], eng="gpsimd")
            vtt(w1[:], psk[:], pb4(PV_KK), ALU.mult, reads=[psk, pvec], writes=[w1])
            vtt(w2[:], w1[:], w1[:], ALU.mult, reads=[w1], writes=[w2])
            mm(pM[:, :], onesblk, f2(w2), True, True, reads=[cst, w2], writes=[pM])
            vts(f2(w2), pM[:, :], L2_EPS, None, ALU.add, None, reads=[pM], writes=[w2])
            act(w2[:], w2[:], AF.Ln, reads=[w2], writes=[w2])
            act(w2[:], w2[:], AF.Exp, reads=[w2], writes=[w2], scale=-0.5)
            vstt(kkn[:], w1[:], -1.0, w2[:], ALU.mult, ALU.mult, reads=[w1, w2], writes=[kkn])
            yield
            vtt(w1[:], av[:], pb4(PV_KA), ALU.mult, reads=[av, pvec], writes=[w1])
            vtt(w1[:], w1[:], omka[:, 0:4].unsqueeze(2).to_broadcast([128, 4, TB]), ALU.add, reads=[w1, omka], writes=[w1])
            vtt(kmod[:], psk[:], w1[:], ALU.mult, reads=[psk, w1], writes=[kmod])
            vstt(bv_[:], kkn[:], -1.0, av[:], ALU.mult, ALU.mult, reads=[kkn, av], writes=[bv_])
            vtt(w1[:], psr[:], pb4(PV_RK), ALU.mult, reads=[psr, pvec], writes=[w1])
            vtt(w1[:], w1[:], kmod[:], ALU.mult, reads=[w1, kmod], writes=[w1])
            mm(pA[:, :], onesblk, f2(w1), True, True, reads=[cst, w1], writes=[pA])
            vtt(f2(bon[pb]), pA[:, :], f2(psv), ALU.mult, reads=[pA, psv], writes=[bon[pb]])
            yield
            S.op("vector", lambda e: e.tensor_tensor_scan(out=f2(cum), data0=rstm, data1=f2(sg), initial=0.0, op0=ALU.mult, op1=ALU.add),
                 reads=[cst, sg], writes=[cum], cost=1.2)
            c3 = cum[:].rearrange("p f (c t) -> p (f c) t", t=CH)
            vtt(w1[:], cum[:], sg[:], ALU.subtract, reads=[cum, sg], writes=[w1])
            act(w2[:], cum[:], AF.Exp, reads=[cum], writes=[w2], scale=-C0)
            act(w3[:], cum[:], AF.Exp, reads=[cum], writes=[w3], scale=C0)
            act(w1[:], w1[:], AF.Exp, reads=[w1], writes=[w1], scale=-C0)
            vtt(w4[:].rearrange("p f (c t) -> p (f c) t", t=CH), c3[:, :, CH - 1:CH].to_broadcast([128, 4 * NCH, CH]), c3, ALU.subtract,
                reads=[cum], writes=[w4], eng="gpsimd")
            act(w4[:], w4[:], AF.Exp, reads=[w4], writes=[w4], scale=-C0)
            yield
            for j in range(2):
                pp = slice(64 * j, 64 * j + 64)
                e_ = "vector" if j == 0 else "gpsimd"
                vtt(rt[pb][:, :, j, :], psr[pp, :, :], w2[pp, :, :], ALU.mult, reads=[psr, w2], writes=[rt[pb]], eng=e_)
                vtt(bt[:, :, j, :], bv_[pp, :, :], w3[pp, :, :], ALU.mult, reads=[bv_, w3], writes=[bt], eng=e_)
                vtt(kt[:, :, j, :], kmod[pp, :, :], w3[pp, :, :], ALU.mult, reads=[kmod, w3], writes=[kt], eng=e_)
                vtt(at[pb][:, :, j, :], kkn[pp, :, :], w1[pp, :, :], ALU.mult, reads=[kkn, w1], writes=[at[pb]], eng=e_)
                act(gC[pb][:, :, j, :], cum[pp, :, :].rearrange("p f (c t) -> p f c t", t=CH)[:, :, :, CH - 1], AF.Exp,
                    reads=[cum], writes=[gC[pb]], scale=-C0)
            vtt(bh[:], bv_[:], w4[:], ALU.mult, reads=[bv_, w4], writes=[bh])
            vtt(kh[:], kmod[:], w4[:], ALU.mult, reads=[kmod, w4], writes=[kh], eng="gpsimd")
            yield

            L = 15 + TB
            for g in range(4):
                vtt(srot[0][:, 1:], uext[:, g, 1:], uext[:, g, 0:L - 1], ALU.add, reads=[uext], writes=[srot[0]], eng="gpsimd")
                tot = srot[0]
                if g >= 1:
                    vtt(srot[1][:, 3:], srot[0][:, 3:], srot[0][:, 1:L - 2], ALU.add, reads=[srot[0]], writes=[srot[1]], eng="gpsimd")
                    tot = srot[1]
                if g >= 2:
                    vtt(srot[2][:, 7:], srot[1][:, 7:], srot[1][:, 3:L - 4], ALU.add, reads=[srot[1]], writes=[srot[2]], eng="gpsimd")
                    tot = srot[2]
                if g >= 3:
                    vtt(srot[3][:, 15:], srot[2][:, 15:], srot[2][:, 7:L - 8], ALU.add, reads=[srot[2]], writes=[srot[3]], eng="gpsimd")
                    tot = srot[3]
                vstt(dpl[:], tot[:, 15:], 1.0 / WINS[g], uext[:, g, 15:], ALU.mult, ALU.subtract, reads=[tot, uext], writes=[dpl])
                if tb == 0:
                    vtt(dpl[:, 0:16], tot[:, 15:31], cst[:, C_ICNT + g * 16:C_ICNT + (g + 1) * 16], ALU.mult, reads=[tot, cst], writes=[dpl])
                    vtt(dpl[:, 0:16], dpl[:, 0:16], uext[:, g, 15:31], ALU.subtract, reads=[dpl, uext], writes=[dpl])
                mm(pM[:, 0:TB], pw[:, g, :], dpl[:], True, True, reads=[pw, dpl], writes=[pM])
                vstt(oT[pb][:, 4 + g, :], pM[:, 0:TB], pvec[:, PV_PS + g:PV_PS + g + 1], gpsil[:, g, :], ALU.mult, ALU.mult,
                     reads=[pM, pvec, gpsil], writes=[oT[pb]])
                yield
            if tb == NTB - 1:
                for g in range(4):
                    tr(pA[0:16, g * 128:(g + 1) * 128], uext[:, g, TB - 1:TB + 15], ident, reads=[uext, cst], writes=[pA])
                vcopy(ppT[:], pA[0:16, :], reads=[pA], writes=[ppT])
                S.dma("sync", npp[:], ppT[1:16, :], reads=[ppT], writes=[npp])
                tr(pB[0:13, 0:128], halo[:, 0:13], ident, reads=[halo, cst], writes=[pB])
                vcopy(m13[:], pB[0:13, 0:128], reads=[pB], writes=[m13])
                S.dma("sync", nsp[:], m13[:], reads=[m13], writes=[nsp])
            vcopy(uext[:, :, 0:15], uext[:, :, TB:TB + 15], reads=[uext], writes=[uext], eng="gpsimd")
            yield

            css = [slice(c * CH, (c + 1) * CH) for c in range(NCH)]
            for c in range(NCH):
                for qi, srcl in enumerate([bh, kh]):
                    for fb in range(4):
                        tr(pT[0:64, qi * 512 + fb * 128:qi * 512 + (fb + 1) * 128], srcl[:, fb, css[c]], identb[:], reads=[srcl, identb], writes=[pT])
                vcopy(BKT[pb][c][:], pT[0:64, :], reads=[pT], writes=[BKT[pb][c]])
                for fb in range(4):
                    tr(pT[0:64, fb * 128:(fb + 1) * 128], vb[:, fb, css[c]], identb[:], reads=[vb, identb], writes=[pT])
                act(VT[pb][c][:], pT[0:64, 0:512], AF.Copy, reads=[pT], writes=[VT[pb][c]])
                yield

            def hsl(tl, h, c):
                fb, j = divmod(h, 2)
                return tl[:, fb, j, css[c]]

            for (Lt, Rt, mask, dsts) in [(bt, at[pb], maskUs, Nsb), (at[pb], bt, maskLs, NTsb), (kt, at[pb], maskUs, Aak[pb]),
                                         (bt, rt[pb], maskUi, Arb[pb]), (kt, rt[pb], maskUi, Ark[pb])]:
                banks = []
                for c in range(NCH):
                    bank = nextbank()
                    banks.append(bank)
                    for h in range(8):
                        mm(bank[0:64, hc(h)], hsl(Lt, h, c), hsl(Rt, h, c), True, True, reads=[Lt, Rt], writes=[bank])
                for c in range(NCH):
                    vtt(h3(dsts[c][:]), h3(banks[c][0:64, :]), mask, ALU.mult, reads=[banks[c], cst], writes=[dsts[c]])
                yield
            X = list(Nsb); XT = list(NTsb)
            Q = [Qtmp[c] for c in range(NCH)]
            for c in range(NCH):
                vtt(h3(Q[c][:]), h3(Nsb[c][:]), ident8, ALU.add, reads=[Nsb[c], cst], writes=[Q[c]])
            for lvl in range(5):
                Xn = [(Xa0[c] if lvl % 2 == 0 else Nsb[c]) for c in range(NCH)]
                XTn = [(XTa0[c] if lvl % 2 == 0 else NTsb[c]) for c in range(NCH)]
                Qn = [(Minv[pb][c] if lvl % 2 == 0 else Qtmp[c]) for c in range(NCH)]
                banks = []
                for c in range(NCH):
                    bank = nextbank(); banks.append(bank)
                    for h in range(8):
                        mm(bank[0:64, hc(h)], X[c][:, hc(h)], XT[c][:, hc(h)], True, True, reads=[X[c], XT[c]], writes=[bank])
                for c in range(NCH):
                    act(XTn[c][:], banks[c][0:64, :], AF.Copy, reads=[banks[c]], writes=[XTn[c]])
                yield
                if lvl < 4:
                    banks = []
                    for c in range(NCH):
                        bank = nextbank(); banks.append(bank)
                        for h in range(8):
                            mm(bank[0:64, hc(h)], XT[c][:, hc(h)], X[c][:, hc(h)], True, True, reads=[X[c], XT[c]], writes=[bank])
                    for c in range(NCH):
                        act(Xn[c][:], banks[c][0:64, :], AF.Copy, reads=[banks[c]], writes=[Xn[c]])
                    yield
                banks = []
                for c in range(NCH):
                    bank = nextbank(); banks.append(bank)
                    for h in range(8):
                        mm(bank[0:64, hc(h)], XTn[c][:, hc(h)], Q[c][:, hc(h)], True, True, reads=[XTn[c], Q[c]], writes=[bank])
                for c in range(NCH):
                    vtt(Qn[c][:], banks[c][0:64, :], Q[c][:], ALU.add, reads=[banks[c], Q[c]], writes=[Qn[c]])
                X, XT, Q = Xn, XTn, Qn
                yield

        def chain(tb):
            pb = tb % 2
            t0 = tb * TB
            for c in range(NCH):
                cs = slice(c * CH, (c + 1) * CH)
                aT, rT = at[pb], rt[pb]
                VTc, BKTc, Aakc, Arbc, Arkc, Minvc = VT[pb][c], BKT[pb][c], Aak[pb][c], Arb[pb][c], Ark[pb][c], Minv[pb][c]
                for h in range(8):
                    fb, j = divmod(h, 2)
                    mm(pC[0:64, hc(h)], aT[:, fb, j, cs], STb[:, h, :], True, False, reads=[aT, STb], writes=[pC])
                    mm(pC[0:64, hc(h)], Aakc[:, hc(h)], VTc[:, hc(h)], False, True, reads=[Aakc, VTc], writes=[pC])
                act(Wsb[:], pC[0:64, :], AF.Copy, reads=[pC], writes=[Wsb])
                yield
                for h in range(8):
                    mm(pC[0:64, hc(h)], Minvc[:, hc(h)], Wsb[:, hc(h)], True, True, reads=[Minvc, Wsb], writes=[pC])
                act(Usb[:], pC[0:64, :], AF.Copy, reads=[pC], writes=[Usb])
                yield
                for h in range(8):
                    mm(pC[0:64, hc(h)], BKTc[:, hc(h)], Usb[:, hc(h)], True, False, reads=[BKTc, Usb], writes=[pC])
                    mm(pC[0:64, hc(h)], BKTc[:, 512 + h * 64:512 + (h + 1) * 64], VTc[:, hc(h)], False, True, reads=[BKTc, VTc], writes=[pC])
                for h in range(8):
                    fb, j = divmod(h, 2)
                    mm(pD[0:64, hc(h)], rT[:, fb, j, cs], STb[:, h, :], True, False, reads=[rT, STb], writes=[pD])
                    mm(pD[0:64, hc(h)], Arbc[:, hc(h)], Usb[:, hc(h)], False, False, reads=[Arbc, Usb], writes=[pD])
                    mm(pD[0:64, hc(h)], Arkc[:, hc(h)], VTc[:, hc(h)], False, True, reads=[Arkc, VTc], writes=[pD])
                vtt(STt[:], ST[:], gC[pb][:].rearrange("p f j c -> p (f j) c")[:, :, c:c + 1].to_broadcast([64, 8, 64]), ALU.mult,
                    reads=[ST, gC[pb]], writes=[STt])
                vtt(ST[:], STt[:], h3(pC[0:64, :]), ALU.add, reads=[STt, pC], writes=[ST])
                act(STb[:], ST[:], AF.Copy, reads=[ST], writes=[STb])
                yield
                y3 = h3(pD[0:64, :])
                vred(m8[:], y3, reads=[pD], writes=[m8])
                vts(m8[:], m8[:], 1.0 / 64, None, ALU.mult, None, reads=[m8], writes=[m8])
                vtt(h3(yc[:]), y3, m8[:].unsqueeze(2).to_broadcast([64, 8, 64]), ALU.subtract, reads=[pD, m8], writes=[yc])
                act(ysq[:], yc[:], AF.Square, reads=[yc], writes=[ysq])
                vred(v8[:], h3(ysq[:]), reads=[ysq], writes=[v8])
                rsqrt_small(r8[:], v8[:], t8[:], 1.0 / 64, GN_EPS, reads=[v8], writes=[t8, r8])
                vtt(h3(yc[:]), h3(yc[:]), r8[:].unsqueeze(2).to_broadcast([64, 8, 64]), ALU.mult, reads=[yc, r8], writes=[yc], eng="gpsimd")
                yield
                for fb in range(4):
                    tr(pD[:, fb * 64:(fb + 1) * 64], yc[:, fb * 128:(fb + 1) * 128], ident[0:64, 0:64], reads=[yc, cst], writes=[pD])
                for fb in range(4):
                    vts(o1[:, fb, :], pD[:, fb * 64:(fb + 1) * 64], pvec[:, PV_GW + fb:PV_GW + fb + 1], pvec[:, PV_GB + fb:PV_GB + fb + 1],
                        ALU.mult, ALU.add, reads=[pD, pvec], writes=[o1])
                vtt(o1[:], o1[:], bon[pb][:, :, cs], ALU.add, reads=[o1, bon[pb]], writes=[o1], eng="gpsimd")
                vtt(oT[pb][:, 0:4, cs], o1[:], gsil[pb][:, :, cs], ALU.mult, reads=[o1, gsil[pb]], writes=[oT[pb]])
                yield
            x_t = xt[pb]
            for half in range(2):
                bank = pD if half == 0 else pC
                for fc in range(8):
                    mm(bank[:, :], oT[pb][:, fc, :], woutb[:, fc, half * 512:(half + 1) * 512], fc == 0, fc == 7, reads=[oT[pb], woutb], writes=[bank])
                vtt(x_t[:, half * 512:(half + 1) * 512], bank[:, :], x_t[:, half * 512:(half + 1) * 512], ALU.add, reads=[bank, x_t], writes=[x_t])
                yield
            act(yo[:], x_t[:], AF.Square, reads=[x_t], writes=[yo])
            vred(ssum[:], yo[:], reads=[yo], writes=[ssum])
            rsqrt_small(rstd[:], ssum[:], tmp1[:], 1.0 / D, NORM_EPS, reads=[ssum], writes=[tmp1, rstd])
            vstt(yo[:], x_t[:], rstd[:, 0:1], normf[:], ALU.mult, ALU.mult, reads=[x_t, rstd, normf], writes=[yo])
            S.dma("sync", yp[t0:t0 + TB, :], yo[:], reads=[yo], writes=[yp])
            yield

        def run_all(g):
            n = 0
            for _ in g:
                n += 1
            return n

        def interleave(ga, na, gb, nb):
            ia = ib = 0
            da = db = False
            while not (da and db):
                pick_a = (not da) and (db or (ia * nb <= ib * na))
                if pick_a:
                    try:
                        next(ga); ia += 1
                    except StopIteration:
                        da = True
                else:
                    try:
                        next(gb); ib += 1
                    except StopIteration:
                        db = True
            return ia, ib

        run_all(front(0))

        def record_units(g):
            units = []
            S.rec = []
            for _ in g:
                if S.rec:
                    units.append(S.rec)
                S.rec = []
            if S.rec:
                units.append(S.rec)
            S.rec = None
            return units

        A, B = [], []
        for tb in range(NTB):
            A.append(record_units(chain(tb)))
            if tb + 1 < NTB:
                B.append(record_units(front(tb + 1)))
        S.merge_emit(A, B, a_ok=lambda ia, ib: ib >= ia, b_ok=lambda ib, ia: ia >= ib)
        for h in range(8):
            tr(pA[0:64, h * 64:(h + 1) * 64], ST[:, h, :], ident[0:64, 0:64], reads=[ST, cst], writes=[pA])
        vcopy(SvT[:].rearrange("p h k -> p (h k)"), pA[0:64, :], reads=[pA], writes=[SvT])
        S.dma("sync", nwp[:].rearrange("h v k -> v h k"), SvT[:], reads=[SvT], writes=[nwp])
        S.finish([yp, ys, nsp, nwp, npp, nss, nws, nps], engname="sync")
        S.barrier()
    es_top.close()
    return nc, S


_CACHE = {}


def _consts():
    cst = np.zeros((128, C_END), np.float32)
    cst[:, C_ID:C_ID + 128] = np.eye(128, dtype=np.float32)
    ob = np.zeros((128, 128), np.float32)
    ob[0:64, 0:64] = 1.0
    ob[64:128, 64:128] = 1.0
    cst[:, C_ONES:C_ONES + 128] = ob
    s = np.arange(64)[:, None]
    t = np.arange(64)[None, :]
    mus = (s < t).astype(np.float32)
    mui = (s <= t).astype(np.float32)
    mls = (s > t).astype(np.float32)
    i64 = np.eye(64, dtype=np.float32)
    cst[0:64, C_MUS:C_MUS + 64] = mus
    cst[0:64, C_MUI:C_MUI + 64] = mui
    cst[0:64, C_MLS:C_MLS + 64] = mls
    rst = np.ones((512,), np.float32)
    rst[::CH] = 0.0
    cst[:, C_RST:C_RST + 512] = rst[None, :]
    for g, w in enumerate(WINS):
        pos = np.arange(16)
        cst[:, C_ICNT + g * 16:C_ICNT + (g + 1) * 16] = (1.0 / np.minimum(pos + 1, w)).astype(np.float32)[None, :]
    return cst


def kernel(x_prompt, x_sample, state_shift, state_wkv, state_pool, norm_w, w_in, mu_shift,
           w_decay_b, w0, w_aaa_b, a0, k_k, k_a, r_k, gn_w, gn_b, pool_w, pool_scale, w_out, norm_f):
    f = lambda a: np.ascontiguousarray(np.asarray(a, dtype=np.float32))
    x_prompt, x_sample, state_shift, state_wkv, state_pool = map(f, (x_prompt, x_sample, state_shift, state_wkv, state_pool))
    if "nc" not in _CACHE:
        _CACHE["nc"] = build_program()
    nc, S = _CACHE["nc"]

    def colmajor(v, n):
        return f(v).reshape(n, 128).T

    pvec = np.concatenate([
        colmajor(norm_w[0], 8), colmajor(mu_shift[0], 13), colmajor(w0[0], 4), colmajor(a0[0], 4), colmajor(k_k[0], 4),
        colmajor(k_a[0], 4), colmajor(f(r_k[0]).reshape(-1), 4), colmajor(gn_w[0], 4), colmajor(gn_b[0], 4), colmajor(pool_scale[0], 4)], axis=1)
    pvec = f(pvec)
    browA = f(f(mu_shift[0])[None, :])
    browB = f(np.concatenate([f(w0[0]), f(a0[0]), f(k_k[0]), f(k_a[0]), f(r_k[0]).reshape(-1), f(gn_w[0]), f(gn_b[0])])[None, :])
    cst = _consts()
    shared = {
        "w_in": f(w_in[0]), "w_out": f(w_out[0]), "wdec": f(w_decay_b[0]), "waaa": f(w_aaa_b[0]), "poolw": f(pool_w[0]),
        "pvec": pvec, "browA": browA, "browB": browB, "normf": f(norm_f)[None, :], "cst": cst,
    }
    in_maps = []
    for c in range(NCORE):
        bs = slice(c * DB, (c + 1) * DB)
        m = dict(shared)
        m["xp"] = x_prompt[c]
        m["xs"] = f(x_sample[bs].transpose(1, 0, 2).reshape(NS, D))
        m["sshift"] = state_shift[0, bs]
        m["swkv"] = f(state_wkv[0, bs].reshape(128, 4096))
        m["spool"] = f(state_pool[0, bs].reshape(DB * 15, 512))
        in_maps.append(m)
    res = run_bass_kernel_spmd(nc, in_maps, core_ids=list(range(NCORE)))
    R = res.results
    y_prompt = np.stack([R[c]["yp"] for c in range(NCORE)], axis=0)
    y_sample = np.concatenate([R[c]["ys"].reshape(DT, DB, D).transpose(1, 0, 2) for c in range(NCORE)], axis=0)
    nsp = np.stack([R[c]["nsp"].reshape(D_SHIFT) for c in range(NCORE)], axis=0)[None]
    nwp = np.stack([R[c]["nwp"] for c in range(NCORE)], axis=0)[None]
    npp = np.stack([R[c]["npp"] for c in range(NCORE)], axis=0)[None]
    nss = np.concatenate([R[c]["nss"] for c in range(NCORE)], axis=0)[None]
    nws = np.concatenate([R[c]["nws"].reshape(DB, 8, 64, 64) for c in range(NCORE)], axis=0)[None]
    nps = np.concatenate([R[c]["nps"] for c in range(NCORE)], axis=0)[None]
    out = (y_prompt, y_sample, nsp, nwp, npp, nss, nws, nps)
    return tuple(np.ascontiguousarray(o.astype(np.float32)) for o in out)
```

```python
import numpy as np
from contextlib import ExitStack
import concourse.bass as bass
import concourse.mybir as mybir
from concourse.bass_utils import run_bass_kernel_spmd

F32 = mybir.dt.float32
BF16 = mybir.dt.bfloat16
AF = mybir.ActivationFunctionType
ALU = mybir.AluOpType
AX = mybir.AxisListType

D = 1024
SEQ = 2048
NCORE = 8
DB = 16
DT = 4
NS = DB * DT
D_SHIFT = 1664
D_IN = 3200
C0 = float(np.exp(-0.5))
NORM_EPS = 1e-6
GN_EPS = 64e-5
L2_EPS = 1e-12
TB = 128
NTB = SEQ // TB
CH = 64
FBIAS = 0.0
NCH = TB // CH
WINS = (2, 4, 8, 16)

C_ID, C_ONES, C_MUS, C_MUI, C_MLS, C_RST, C_ICNT, C_END = 0, 128, 256, 320, 384, 448, 960, 1024
PV_NW, PV_MU, PV_W0, PV_A0, PV_KK, PV_KA, PV_RK, PV_GW, PV_GB, PV_PS, PV_END = 0, 8, 21, 25, 29, 33, 37, 41, 45, 49, 53
BRB_W0, BRB_A0, BRB_KK, BRB_KA, BRB_RK, BRB_GW, BRB_GB = 0, 512, 1024, 1536, 2048, 2560, 3072


class Buf:
    __slots__ = ("name", "w", "r")

    def __init__(self, name):
        self.name = name
        self.w = None
        self.r = []


class T:
    def __init__(self, t, name, buf=None):
        self.t = t
        self.b = buf if buf is not None else Buf(name)

    def __getitem__(self, k):
        return self.t[k]


class Sched:
    def __init__(self, nc, n_dma_sems=32):
        self.nc = nc
        self.eng = {}
        for name in ["tensor", "vector", "scalar", "gpsimd", "sync"]:
            h = getattr(nc, name)
            sem = nc.alloc_semaphore(name="prog_" + name)
            self.eng[name] = dict(h=h, sem=sem, cnt=0, waited={})
        self.dma_sems = [dict(sem=nc.alloc_semaphore(name=f"dma{i}"), cnt=0) for i in range(n_dma_sems)]
        self.dma_rr = 0
        self.ninstr = 0
        self.rec = None

    def _wait(self, engname, tok):
        sem, val, src = tok
        e = self.eng[engname]
        key = id(sem)
        if e["waited"].get(key, 0) >= val:
            return
        e["h"].wait_ge(sem, val)
        e["waited"][key] = val
        self.ninstr += 1

    def _deps(self, engname, reads, writes):
        toks = []
        for b in reads:
            if b.w is not None:
                toks.append(b.w)
        for b in writes:
            if b.w is not None:
                toks.append(b.w)
            toks.extend(b.r)
        for tok in toks:
            if tok[2] == engname and engname == "tensor":
                continue
            self._wait(engname, tok)

    @staticmethod
    def _bufs(xs):
        return [x.b if isinstance(x, T) else x for x in xs]

    def _record(self, tok, reads, writes):
        for b in reads:
            b.r.append(tok)
            if len(b.r) > 64:
                b.r = b.r[-64:] if False else b.r
        for b in writes:
            b.w = tok
            b.r = []

    def op(self, engname, fn, reads=(), writes=(), cost=0.3):
        reads = self._bufs(reads)
        writes = self._bufs(writes)
        if self.rec is not None:
            self.rec.append(("op", engname, fn, reads, writes, cost, None))
            return None
        e = self.eng[engname]
        self._deps(engname, reads, writes)
        ins = fn(e["h"])
        e["cnt"] += 1
        ins.then_inc(e["sem"], 1)
        e["waited"][id(e["sem"])] = max(e["waited"].get(id(e["sem"]), 0), 0)
        tok = (e["sem"], e["cnt"], engname)
        self._record(tok, reads, writes)
        self.ninstr += 1
        return tok

    def dma(self, qname, out, in_, reads=(), writes=(), **kw):
        reads = self._bufs(reads)
        writes = self._bufs(writes)
        if self.rec is not None:
            self.rec.append(("dma", qname, (out, in_), reads, writes, 2.5, kw))
            return None
        e = self.eng[qname]
        self._deps(qname, reads, writes)
        d = self.dma_sems[self.dma_rr]
        self.dma_rr = (self.dma_rr + 1) % len(self.dma_sems)
        if d["cnt"] > 0:
            self._wait(qname, (d["sem"], 16 * d["cnt"], "dma"))
        ins = e["h"].dma_start(out=out, in_=in_, **kw)
        d["cnt"] += 1
        ins.then_inc(d["sem"], 16)
        tok = (d["sem"], 16 * d["cnt"], "dma")
        self._record(tok, reads, writes)
        self.ninstr += 1
        return tok

    def emit(self, r):
        kind, eng, fn, reads, writes, cost, kw = r
        if kind == "op":
            self.op(eng, fn, reads=reads, writes=writes)
        else:
            self.dma(eng, fn[0], fn[1], reads=reads, writes=writes, **kw)

    def merge_emit(self, A, B, a_ok, b_ok):
        eng_free = {}
        ready = {}
        acc = {}

        def est(r):
            kind, eng, fn, reads, writes, cost, kw = r
            t = eng_free.get(eng, 0.0)
            for b in reads:
                rt_, re_ = ready.get(id(b), (0.0, eng))
                t = max(t, rt_ + (0.15 if re_ != eng else 0.0))
            for b in writes:
                rt_, re_ = ready.get(id(b), (0.0, eng))
                t = max(t, rt_ + (0.15 if re_ != eng else 0.0), acc.get(id(b), 0.0) + 0.1)
            return t

        def commit(r, t):
            kind, eng, fn, reads, writes, cost, kw = r
            if kind == "dma":
                eng_free[eng] = t + 0.1
                end = t + cost
            else:
                end = t + cost
                eng_free[eng] = end
            for b in reads:
                acc[id(b)] = max(acc.get(id(b), 0.0), end)
            for b in writes:
                ready[id(b)] = (end, eng)
                acc[id(b)] = max(acc.get(id(b), 0.0), end)

        def run_unit(u):
            for r in u:
                commit(r, est(r))
                self.emit(r)

        ia = ib = 0
        ja = jb = 0
        while ia < len(A) or ib < len(B):
            ca = None
            cb = None
            if ia < len(A) and (ja > 0 or a_ok(ia, ib)):
                ca = A[ia][ja]
            if ib < len(B) and (jb > 0 or b_ok(ib, ia)):
                cb = B[ib][jb]
            assert ca is not None or cb is not None, (ia, ib, ja, jb)
            ta = est(ca[0]) if ca is not None else None
            tb_ = est(cb[0]) if cb is not None else None
            if cb is None or (ca is not None and ta + FBIAS < tb_):
                run_unit(ca); ja += 1
                if ja == len(A[ia]):
                    ia += 1; ja = 0
            else:
                run_unit(cb); jb += 1
                if jb == len(B[ib]):
                    ib += 1; jb = 0

    def barrier(self):
        toks = [(e["sem"], e["cnt"], n) for n, e in self.eng.items() if e["cnt"] > 0]
        toks += [(d["sem"], 16 * d["cnt"], "dma") for d in self.dma_sems if d["cnt"] > 0]
        for n in self.eng:
            for tok in toks:
                if tok[2] == n:
                    continue
                self._wait(n, tok)

    def finish(self, tiles, engname="sync"):
        for b in self._bufs(tiles):
            if b.w is not None:
                self._wait(engname, b.w)


class _Stop(Exception):
    pass


def build_program(stop=None):
    nc = bass.Bass("TRN2", target_bir_lowering=False)
    S = Sched(nc)
    try:
        _build_body(nc, S, stop)
    except _Stop:
        S.barrier()
    return nc, S


def _build_body(nc, S, stop):
    def chk(label):
        if stop == label:
            raise _Stop()


    def din(name, shape):
        return nc.dram_tensor(name, list(shape), F32, kind="ExternalInput").ap()

    def dout(name, shape):
        return T(nc.dram_tensor(name, list(shape), F32, kind="ExternalOutput").ap(), name)

    xp = din("xp", [SEQ, D])
    xs = din("xs", [NS, D])
    sshift = din("sshift", [DB, D_SHIFT])
    swkv = din("swkv", [128, 4096])
    spool = din("spool", [DB * 15, 512])
    w_in = din("w_in", [D, D_IN])
    w_out = din("w_out", [D, D])
    wdec = din("wdec", [64, 512])
    waaa = din("waaa", [64, 512])
    poolw = din("poolw", [4, 128, 128])
    pvec_d = din("pvec", [128, PV_END])
    browA_d = din("browA", [1, D_SHIFT])
    browB_d = din("browB", [1, 3584])
    normf_d = din("normf", [1, D])
    cst_d = din("cst", [128, C_END])

    yp = dout("yp", [SEQ, D])
    ys = dout("ys", [NS, D])
    nsp = dout("nsp", [13, 128])
    nwp = dout("nwp", [8, 64, 64])
    npp = dout("npp", [15, 512])
    nss = dout("nss", [DB, D_SHIFT])
    nws = dout("nws", [128, 4096])
    nps = dout("nps", [DB, 15, 512])
    scr1 = T(nc.dram_tensor("scr1", [6, DT, DB, 8, 64], F32, kind="Internal").ap(), "scr1")
    scr2 = T(nc.dram_tensor("scr2", [DB, 8, DT, 64], F32, kind="Internal").ap(), "scr2")

    es_top = ExitStack()

    def sb(es, name, shape, dt=F32):
        return T(es.enter_context(nc.sbuf_tensor("s_" + name, list(shape), dt)), name)

    def pst(name, shape, dt=F32):
        return T(nc.alloc_psum_tensor("p_" + name, list(shape), dt), name)

    def nel(ap):
        n = 1
        for s_ in ap.shape[1:]:
            n *= s_
        return n

    def mm(out, lhsT, rhs, start, stop, reads, writes):
        passes = 4 if lhsT.dtype == F32 else 1
        c_ = max(0.055, nel(rhs) * passes / 2000.0 + 0.03)
        S.op("tensor", lambda e: e.matmul(out, lhsT=lhsT, rhs=rhs, start=start, stop=stop), reads=reads, writes=writes, cost=c_)

    def tr(out, in_, ident, reads, writes):
        S.op("tensor", lambda e: e.transpose(out, in_, ident), reads=reads, writes=writes, cost=0.13)

    def act(out, in_, func, reads, writes, bias=None, scale=None, eng="scalar", accum=None):
        kw = {}
        if accum is not None:
            kw["accum_out"] = accum
        if bias is not None:
            kw["bias"] = bias
        if scale is not None:
            kw["scale"] = scale
        S.op("scalar", lambda e: e.activation(out=out, in_=in_, func=func, **kw), reads=reads, writes=writes,
             cost=0.1 + 0.1 * len(kw) + nel(in_) * 0.00095)

    def ecost(eng, n):
        return 0.08 + n * (0.00105 if eng == "vector" else 0.0025)

    def vtt(out, in0, in1, op, reads, writes, eng="vector"):
        S.op(eng, lambda e: e.tensor_tensor(out=out, in0=in0, in1=in1, op=op), reads=reads, writes=writes, cost=ecost(eng, nel(out)))

    def vts(out, in0, s1, s2, op0, op1, reads, writes, eng="vector"):
        if op1 is None:
            S.op(eng, lambda e: e.tensor_scalar(out=out, in0=in0, scalar1=s1, scalar2=None, op0=op0), reads=reads, writes=writes,
                 cost=ecost(eng, nel(out)))
        else:
            S.op(eng, lambda e: e.tensor_scalar(out=out, in0=in0, scalar1=s1, scalar2=s2, op0=op0, op1=op1), reads=reads, writes=writes,
                 cost=ecost(eng, nel(out)))

    def vstt(out, in0, scalar, in1, op0, op1, reads, writes):
        S.op("vector", lambda e: e.scalar_tensor_tensor(out=out, in0=in0, scalar=scalar, in1=in1, op0=op0, op1=op1), reads=reads, writes=writes,
             cost=ecost("vector", nel(out)))

    def vcopy(out, in_, reads, writes, eng="vector"):
        S.op(eng, lambda e: e.tensor_copy(out=out, in_=in_), reads=reads, writes=writes, cost=ecost(eng, nel(out)))

    def vred(out, in_, reads, writes):
        S.op("vector", lambda e: e.tensor_reduce(out=out, in_=in_, axis=AX.X, op=ALU.add), reads=reads, writes=writes,
             cost=ecost("vector", nel(in_)))

    def vrecip(out, in_, reads, writes):
        S.op("vector", lambda e: e.reciprocal(out=out, in_=in_), reads=reads, writes=writes, cost=0.08 + nel(out) * 0.0084)

    def memset(ap, val, writes, eng="gpsimd"):
        S.op(eng, lambda e: e.memset(ap, val), writes=writes)

    def rsqrt_small(out, in_, tmp, scale, eps, reads, writes):
        act(tmp, in_, AF.Sqrt, reads=reads, writes=writes, bias=None, scale=None) if False else None
        vts(tmp, in_, scale, eps, ALU.mult, ALU.add, reads=reads, writes=writes)
        act(tmp, tmp, AF.Sqrt, reads=writes, writes=writes)
        vrecip(out, tmp, reads=writes, writes=writes)

    pg = [pst(f"pg{i}", [128, 512]) for i in range(2)]
    pT = pst("pT", [128, 1024], BF16)
    pM = pst("pM", [128, 512])
    pA = pst("pA", [128, 512])
    pB = pst("pB", [128, 512])
    pC = pst("pC", [128, 512])
    pD = pst("pD", [128, 512])

    cst = sb(es_top, "cst", [128, C_END])
    pvec = sb(es_top, "pvec", [128, PV_END])
    omu = sb(es_top, "omu", [128, 13])
    omka = sb(es_top, "omka", [128, 4])
    identb = sb(es_top, "identb", [128, 128], BF16)
    winb = sb(es_top, "winb", [128, 8, D_IN], BF16)
    woutb = sb(es_top, "woutb", [128, 8, D], BF16)
    wd = sb(es_top, "wd", [64, 512])
    wa = sb(es_top, "wa", [128, 512])
    pw = sb(es_top, "pw", [128, 4, 128])
    normf = sb(es_top, "normf", [128, D])

    ident = cst[:, C_ID:C_ID + 128]
    onesblk = cst[:, C_ONES:C_ONES + 128]

    S.dma("sync", cst[:], cst_d, writes=[cst])
    S.dma("sync", pvec[:], pvec_d, writes=[pvec])
    S.dma("sync", wd[:], wdec, writes=[wd])
    S.dma("sync", wa[64:128, :], waaa, writes=[wa])
    S.dma("sync", pw[:], poolw.rearrange("g c e -> c g e"), writes=[pw])
    S.dma("sync", normf[:], normf_d.partition_broadcast(128), writes=[normf])
    vcopy(identb[:], ident, reads=[cst], writes=[identb])
    vts(omka[:], pvec[:, PV_KA:PV_KA + 4], -1.0, 1.0, ALU.mult, ALU.add, reads=[pvec], writes=[omka])

    with ExitStack() as es:
        stg = [sb(es, f"stg{i}", [128, D_IN]) for i in range(3)]
        for dc in range(8):
            st = stg[dc % 3]
            S.dma("sync", st[:], w_in[dc * 128:(dc + 1) * 128, :], writes=[st])
            h = D_IN // 2
            vts(winb[:, dc, 0:h], st[:, 0:h], pvec[:, PV_NW + dc:PV_NW + dc + 1], None, ALU.mult, None, reads=[st, pvec], writes=[winb])
            act(winb[:, dc, h:], st[:, h:], AF.Copy, reads=[st, pvec], writes=[winb], scale=pvec[:, PV_NW + dc:PV_NW + dc + 1])
        S.barrier()
        chk("W")

    def final_tile(es_tiles, n, x_t, oT_list, out_dram_ap, out_T):
        res, sq, ssum, tmp1, rstd, yo = es_tiles
        for half in range(2):
            bank = pD if half == 0 else pC
            for fc in range(8):
                mm(bank[0:n, :], oT_list[fc], woutb[:, fc, half * 512:(half + 1) * 512], fc == 0, fc == 7,
                   reads=[oT_list_T, woutb], writes=[bank])
            vtt(res[0:n, half * 512:(half + 1) * 512], bank[0:n, :], x_t[0:n, half * 512:(half + 1) * 512], ALU.add,
                reads=[bank, x_t], writes=[res])
        act(sq[0:n, :], res[0:n, :], AF.Square, reads=[res], writes=[sq])
        vred(ssum[0:n, :], sq[0:n, :], reads=[sq], writes=[ssum])
        rsqrt_small(rstd[0:n, :], ssum[0:n, :], tmp1[0:n, :], 1.0 / D, NORM_EPS, reads=[ssum], writes=[tmp1, rstd])
        vstt(yo[0:n, :], res[0:n, :], rstd[0:n, 0:1], normf[0:n, :], ALU.mult, ALU.mult, reads=[res, rstd, normf], writes=[yo])
        S.dma("sync", out_dram_ap, yo[0:n, :], reads=[yo], writes=[out_T])

    oT_list_T = None

    with ExitStack() as es:
        browB = sb(es, "browB", [NS, 3584])
        S.dma("sync", browB[:], browB_d.partition_broadcast(NS), writes=[browB])
        x_s = sb(es, "x_s", [NS, D])
        S.dma("sync", x_s[:], xs, writes=[x_s])
        hTs = sb(es, "hTs", [128, 8, DB + NS], BF16)
        graw_s = sb(es, "graw_s", [NS, 512])
        u_s = sb(es, "u_s", [NS, 512])
        gp_s = sb(es, "gp_s", [NS, 512])
        bonus_s = sb(es, "bonus_s", [NS, 512])
        st8 = sb(es, "st8", [NS, 8])
        st8b = sb(es, "st8b", [NS, 8])
        st8c = sb(es, "st8c", [NS, 8])

        def v3(ap):
            return ap.rearrange("p (h k) -> p h k", k=64)

        def bc8(ap8):
            return ap8.unsqueeze(2).to_broadcast([NS, 8, 64])

        with ExitStack() as e1:
            browA = sb(e1, "browA", [NS, D_SHIFT])
            S.dma("sync", browA[:], browA_d.partition_broadcast(NS), writes=[browA])
            omka_b = sb(e1, "omka_b", [NS, 512])
            vts(omka_b[:], browB[:, BRB_KA:BRB_KA + 512], -1.0, 1.0, ALU.mult, ALU.add, reads=[browB], writes=[omka_b])
            sq_s = sb(e1, "sq_s", [NS, D])
            ss_s = sb(e1, "ss_s", [NS, 1])
            t1_s = sb(e1, "t1_s", [NS, 1])
            rstd_s = sb(e1, "rstd_s", [NS, 1])
            xn_s = sb(e1, "xn_s", [NS, D], BF16)
            act(sq_s[:], x_s[:], AF.Square, reads=[x_s], writes=[sq_s])
            vred(ss_s[:], sq_s[:], reads=[sq_s], writes=[ss_s])
            rsqrt_small(rstd_s[:], ss_s[:], t1_s[:], 1.0 / D, NORM_EPS, reads=[ss_s], writes=[t1_s, rstd_s])
            vts(xn_s[:], x_s[:], rstd_s[:, 0:1], None, ALU.mult, None, reads=[x_s, rstd_s], writes=[xn_s])
            memset(hTs[:, :, 0:DB], 0.0, writes=[hTs])
            for dc in range(8):
                tr(pT[:, dc * 128:dc * 128 + NS], xn_s[:, dc * 128:(dc + 1) * 128], identb[0:NS, 0:NS], reads=[xn_s, identb], writes=[pT])
            vcopy(hTs[:, :, DB:DB + NS], pT[:].rearrange("p (c t) -> p c t", t=128)[:, :, 0:NS], reads=[pT], writes=[hTs])

            p_s = sb(e1, "p_s", [NS, D_SHIFT])
            prev_s = sb(e1, "prev_s", [NS, D_SHIFT])
            col_chunks = [(0, 512), (512, 512), (1024, 512), (1536, 128)]
            kk_ = 0
            for (c0, n) in col_chunks:
                bank = pg[kk_ % 2]; kk_ += 1
                for dc in range(8):
                    mm(bank[0:NS, 0:n], hTs[:, dc, DB:DB + NS], winb[:, dc, c0:c0 + n], dc == 0, dc == 7, reads=[hTs, winb], writes=[bank])
                act(p_s[:, c0:c0 + n], bank[0:NS, 0:n], AF.Copy, reads=[bank], writes=[p_s])
                bank = pg[kk_ % 2]; kk_ += 1
                for dc in range(8):
                    mm(bank[0:NS, 0:n], hTs[:, dc, 0:NS], winb[:, dc, c0:c0 + n], dc == 0, dc == 7, reads=[hTs, winb], writes=[bank])
                vcopy(prev_s[:, c0:c0 + n], bank[0:NS, 0:n], reads=[bank], writes=[prev_s])
            for (c0, dst, fn) in [(1664, graw_s, AF.Silu), (2176, u_s, AF.Copy), (2688, gp_s, AF.Silu)]:
                bank = pg[kk_ % 2]; kk_ += 1
                for dc in range(8):
                    mm(bank[0:NS, :], hTs[:, dc, DB:DB + NS], winb[:, dc, c0:c0 + 512], dc == 0, dc == 7, reads=[hTs, winb], writes=[bank])
                act(dst[:], bank[0:NS, :], fn, reads=[bank], writes=[dst])
            S.dma("sync", prev_s[0:DB, :], sshift, writes=[prev_s])
            S.dma("sync", nss[:], p_s[NS - DB:NS, :], reads=[p_s], writes=[nss])
            S.dma("sync", nps[:, 0:11, :], spool.rearrange("(b j) c -> b j c", j=15)[:, 4:15, :], writes=[nps])
            for t in range(DT):
                S.dma("sync", nps[:, 11 + t, :], u_s[t * DB:(t + 1) * DB, :], reads=[u_s], writes=[nps])

            vtt(prev_s[:], prev_s[:], p_s[:], ALU.subtract, reads=[prev_s, p_s], writes=[prev_s])
            vtt(prev_s[:], prev_s[:], browA[:], ALU.mult, reads=[prev_s, browA], writes=[prev_s])
            vtt(prev_s[:], prev_s[:], p_s[:], ALU.add, reads=[prev_s, p_s], writes=[prev_s])
            ps_s = prev_s
            r_s = ps_s[:, 0:512]
            k_s = ps_s[:, 512:1024]
            v_s = ps_s[:, 1024:1536]

            lT = sb(e1, "lT", [128, NS])
            tr(pM[:, 0:NS], ps_s[:, 1536:1664], ident[0:NS, 0:NS], reads=[ps_s, cst], writes=[pM])
            act(lT[0:64, :], pM[0:64, 0:NS], AF.Tanh, reads=[pM], writes=[lT])
            act(lT[64:128, :], pM[64:128, 0:NS], AF.Copy, reads=[pM], writes=[lT])
            sg_s = sb(e1, "sg_s", [NS, 512])
            a_s = sb(e1, "a_s", [NS, 512])
            mm(pA[0:NS, :], lT[0:64, :], wd[:, :], True, True, reads=[lT, wd], writes=[pA])
            vtt(sg_s[:], pA[0:NS, :], browB[:, BRB_W0:BRB_W0 + 512], ALU.add, reads=[pA, browB], writes=[sg_s])
            act(sg_s[:], sg_s[:], AF.Sigmoid, reads=[sg_s], writes=[sg_s])
            mm(pB[0:NS, :], lT[64:128, :], wa[64:128, :], True, True, reads=[lT, wa], writes=[pB])
            vtt(a_s[:], pB[0:NS, :], browB[:, BRB_A0:BRB_A0 + 512], ALU.add, reads=[pB, browB], writes=[a_s])
            act(a_s[:], a_s[:], AF.Sigmoid, reads=[a_s], writes=[a_s])

            pk = sb(e1, "pk", [NS, 4, 512])
            PQ = {1: 0, 2: 1, 4: 2, 5: 3}
            tmpA = sb(e1, "tmpA", [NS, 512])
            tmpB = sb(e1, "tmpB", [NS, 512])
            act(pk[:, PQ[1], :], sg_s[:], AF.Exp, reads=[sg_s], writes=[pk], scale=-C0)
            vtt(tmpA[:], k_s, browB[:, BRB_KK:BRB_KK + 512], ALU.mult, reads=[ps_s, browB], writes=[tmpA])
            vtt(tmpB[:], tmpA[:], tmpA[:], ALU.mult, reads=[tmpA], writes=[tmpB])
            vred(st8[:], v3(tmpB[:]), reads=[tmpB], writes=[st8])
            rsqrt_small(st8b[:], st8[:], st8c[:], 1.0, L2_EPS, reads=[st8], writes=[st8c, st8b])
            vtt(v3(tmpA[:]), v3(tmpA[:]), bc8(st8b[:]), ALU.mult, reads=[tmpA, st8b], writes=[tmpA])
            vts(pk[:, PQ[4], :], tmpA[:], -1.0, None, ALU.mult, None, reads=[tmpA], writes=[pk])
            vtt(pk[:, PQ[5], :], tmpA[:], a_s[:], ALU.mult, reads=[tmpA, a_s], writes=[pk])
            vtt(tmpB[:], a_s[:], browB[:, BRB_KA:BRB_KA + 512], ALU.mult, reads=[a_s, browB], writes=[tmpB])
            vtt(tmpB[:], tmpB[:], omka_b[:], ALU.add, reads=[tmpB, omka_b], writes=[tmpB])
            vtt(pk[:, PQ[2], :], k_s, tmpB[:], ALU.mult, reads=[ps_s, tmpB], writes=[pk])
            vtt(tmpB[:], r_s, browB[:, BRB_RK:BRB_RK + 512], ALU.mult, reads=[ps_s, browB], writes=[tmpB])
            vtt(tmpB[:], tmpB[:], pk[:, PQ[2], :], ALU.mult, reads=[tmpB, pk], writes=[tmpB])
            vred(st8[:], v3(tmpB[:]), reads=[tmpB], writes=[st8])
            vtt(v3(bonus_s[:]), v3(v_s), bc8(st8[:]), ALU.mult, reads=[ps_s, st8], writes=[bonus_s])
            sview = scr1[:].rearrange("q t b h k -> q (t b) (h k)")
            S.dma("sync", sview[0], r_s, reads=[ps_s], writes=[scr1])
            S.dma("sync", sview[3], v_s, reads=[ps_s], writes=[scr1])
            for qq, slot in PQ.items():
                S.dma("sync", sview[qq], pk[:, slot, :], reads=[pk], writes=[scr1])
            S.finish([scr1], engname="sync")
            S.barrier()
            chk("S1")

        with ExitStack() as e2:
            sIn = sb(e2, "sIn", [128, 6, DT, 64])
            S.dma("sync", sIn[:], scr1[:].rearrange("q t b h k -> (b h) q t k"), reads=[scr1], writes=[sIn])
            St = sb(e2, "St", [128, 64, 64])
            S.dma("sync", St[:].rearrange("p v k -> p (v k)"), swkv, writes=[St])
            tmpS = sb(e2, "tmpS", [128, 64, 64])
            sa = sb(e2, "sa", [128, 64])
            yS = sb(e2, "yS", [128, DT, 64])
            stgo = [sb(e2, f"stgo{i}", [128, D]) for i in range(3)]
            for fc in range(8):
                so = stgo[fc % 3]
                S.dma("sync", so[:], w_out[fc * 128:(fc + 1) * 128, :], writes=[so])
                act(woutb[:, fc, :], so[:], AF.Copy, reads=[so], writes=[woutb])

            def bv(ap):
                return ap.unsqueeze(1).to_broadcast([128, 64, 64])

            def bk(ap):
                return ap.unsqueeze(2).to_broadcast([128, 64, 64])

            for t in range(DT):
                q = lambda i: sIn[:, i, t, :]
                vtt(tmpS[:], St[:], bv(q(4)), ALU.mult, reads=[St, sIn], writes=[tmpS])
                vred(sa[:], tmpS[:], reads=[tmpS], writes=[sa])
                vtt(St[:], St[:], bv(q(1)), ALU.mult, reads=[St, sIn], writes=[St])
                vtt(tmpS[:], bk(sa[:]), bv(q(5)), ALU.mult, reads=[sa, sIn], writes=[tmpS])
                vtt(St[:], St[:], tmpS[:], ALU.add, reads=[St, tmpS], writes=[St])
                vtt(tmpS[:], bk(q(3)), bv(q(2)), ALU.mult, reads=[sIn], writes=[tmpS])
                vtt(St[:], St[:], tmpS[:], ALU.add, reads=[St, tmpS], writes=[St])
                vtt(tmpS[:], St[:], bv(q(0)), ALU.mult, reads=[St, sIn], writes=[tmpS])
                vred(yS[:, t, :], tmpS[:], reads=[tmpS], writes=[yS])
            S.dma("sync", nws[:], St[:].rearrange("p v k -> p (v k)"), reads=[St], writes=[nws])
            S.dma("sync", scr2[:].rearrange("b h t v -> (b h) t v"), yS[:], reads=[yS], writes=[scr2])
            S.finish([scr2, nws], engname="sync")
            S.barrier()
            chk("S2")

        with ExitStack() as e3:
            yT = sb(e3, "yT", [NS, 512])
            tmpA = sb(e3, "tmpA3", [NS, 512])
            for t in range(DT):
                S.dma("sync", yT[t * DB:(t + 1) * DB, :].rearrange("b (h v) -> b h v", v=64), scr2[:][:, :, t, :], reads=[scr2], writes=[yT])
            vred(st8[:], v3(yT[:]), reads=[yT], writes=[st8])
            vts(st8[:], st8[:], 1.0 / 64, None, ALU.mult, None, reads=[st8], writes=[st8])
            vtt(v3(yT[:]), v3(yT[:]), bc8(st8[:]), ALU.subtract, reads=[yT, st8], writes=[yT])
            vtt(tmpA[:], yT[:], yT[:], ALU.mult, reads=[yT], writes=[tmpA])
            vred(st8[:], v3(tmpA[:]), reads=[tmpA], writes=[st8])
            rsqrt_small(st8b[:], st8[:], st8c[:], 1.0 / 64, GN_EPS, reads=[st8], writes=[st8c, st8b])
            vtt(v3(yT[:]), v3(yT[:]), bc8(st8b[:]), ALU.mult, reads=[yT, st8b], writes=[yT])
            vtt(yT[:], yT[:], browB[:, BRB_GW:BRB_GW + 512], ALU.mult, reads=[yT, browB], writes=[yT])
            vtt(yT[:], yT[:], browB[:, BRB_GB:BRB_GB + 512], ALU.add, reads=[yT, browB], writes=[yT])
            vtt(yT[:], yT[:], bonus_s[:], ALU.add, reads=[yT, bonus_s], writes=[yT])
            vtt(yT[:], yT[:], graw_s[:], ALU.mult, reads=[yT, graw_s], writes=[yT])
            oTs = sb(e3, "oTs", [128, 8, NS], BF16)
            for fb in range(4):
                tr(pA[:, fb * 64:fb * 64 + NS], yT[:, fb * 128:(fb + 1) * 128], ident[0:NS, 0:NS], reads=[yT, cst], writes=[pA])
            vcopy(oTs[:, 0:4, :], pA[:, 0:4 * NS].rearrange("p (f t) -> p f t", t=NS), reads=[pA], writes=[oTs])

            uext = sb(e3, "uext_s", [128, 4, DB, 19])
            sp0 = sb(e3, "sp0", [120, 512])
            sp1 = sb(e3, "sp1", [120, 512])
            S.dma("sync", sp0[:], spool[0:120, :], writes=[sp0])
            S.dma("sync", sp1[:], spool[120:240, :], writes=[sp1])
            for g in range(4):
                tr(pB[:, 0:120], sp0[:, g * 128:(g + 1) * 128], ident[0:120, 0:120], reads=[sp0, cst], writes=[pB])
                tr(pB[:, 128:248], sp1[:, g * 128:(g + 1) * 128], ident[0:120, 0:120], reads=[sp1, cst], writes=[pB])
                vcopy(uext[:, g, 0:8, 0:15], pB[:, 0:120].rearrange("p (b j) -> p b j", j=15), reads=[pB], writes=[uext])
                vcopy(uext[:, g, 8:16, 0:15], pB[:, 128:248].rearrange("p (b j) -> p b j", j=15), reads=[pB], writes=[uext])
                tr(pM[:, 0:NS], u_s[:, g * 128:(g + 1) * 128], ident[0:NS, 0:NS], reads=[u_s, cst], writes=[pM])
                vcopy(uext[:, g, :, 15:19], pM[:, 0:NS].rearrange("p (t b) -> p b t", b=DB), reads=[pM], writes=[uext])
            s2 = sb(e3, "s2_s", [128, 4, DB, 19])
            s4 = sb(e3, "s4_s", [128, 3, DB, 19])
            s8 = sb(e3, "s8_s", [128, 2, DB, 19])
            s16 = sb(e3, "s16_s", [128, 1, DB, 19])
            d_s = sb(e3, "d_s", [128, 4, DT, DB])
            vtt(s2[:, :, :, 1:19], uext[:, :, :, 1:19], uext[:, :, :, 0:18], ALU.add, reads=[uext], writes=[s2])
            vtt(s4[:, :, :, 3:19], s2[:, 1:4, :, 3:19], s2[:, 1:4, :, 1:17], ALU.add, reads=[s2], writes=[s4])
            vtt(s8[:, :, :, 7:19], s4[:, 1:3, :, 7:19], s4[:, 1:3, :, 3:15], ALU.add, reads=[s4], writes=[s8])
            vtt(s16[:, :, :, 15:19], s8[:, 1:2, :, 15:19], s8[:, 1:2, :, 7:11], ALU.add, reads=[s8], writes=[s16])
            tots = [(s2, 0), (s4, 1), (s8, 2), (s16, 3)]
            for g in range(4):
                tt, off = tots[g]
                vstt(d_s[:, g, :, :].rearrange("p t b -> p b t"), tt[:, g - off, :, 15:19], 1.0 / WINS[g], uext[:, g, :, 15:19],
                     ALU.mult, ALU.subtract, reads=[tt, uext], writes=[d_s])
            gpT = sb(e3, "gpT", [128, 4, NS])
            for g in range(4):
                tr(pM[:, 64 + g * 64:64 + g * 64 + NS], gp_s[:, g * 128:(g + 1) * 128], ident[0:NS, 0:NS], reads=[gp_s, cst], writes=[pM])
            vcopy(gpT[:], pM[:, 64:64 + 4 * NS].rearrange("p (g t) -> p g t", t=NS), reads=[pM], writes=[gpT])
            for g in range(4):
                mm(pA[:, g * 64:g * 64 + NS], pw[:, g, :], d_s[:, g, :, :].rearrange("p t b -> p (t b)"), True, True, reads=[pw, d_s], writes=[pA])
            for g in range(4):
                vstt(oTs[:, 4 + g, :], pA[:, g * 64:g * 64 + NS], pvec[:, PV_PS + g:PV_PS + g + 1], gpT[:, g, :], ALU.mult, ALU.mult,
                     reads=[pA, pvec, gpT], writes=[oTs])

            sq = sb(e3, "sq2_s", [NS, D]); ssum = sb(e3, "ssum_s", [NS, 1])
            tmp1 = sb(e3, "tmp1_s", [NS, 1]); rstd = sb(e3, "rstd2_s", [NS, 1]); yo = sb(e3, "yo_s", [NS, D])
            oT_list_T = oTs
            final_tile((x_s, sq, ssum, tmp1, rstd, yo), NS, x_s, [oTs[:, fc, :] for fc in range(8)], ys[:], ys)
            S.finish([ys, nss, nps], engname="sync")
            S.barrier()
            chk("S3")

    with ExitStack() as es:
        def sbl(name, shape, dt=F32, n=2):
            return [sb(es, f"{name}_{i}", shape, dt) for i in range(n)]

        xt = sbl("xt", [128, D])
        yo = sb(es, "yo", [128, D])
        ssx = sb(es, "ssx", [128, 1]); t1x = sb(es, "t1x", [128, 1]); rsx = sb(es, "rsx", [128, 1])
        xnb = sb(es, "xnb", [128, D], BF16)
        sqx = xnb
        hT = sb(es, "hT", [128, 8, TB], BF16)
        praw = sbl("praw", [128, 4, TB + 1])
        halo = sb(es, "halo", [128, 13])
        omu = sb(es, "omu2", [128, 13])
        psr = sb(es, "psr", [128, 4, TB]); psk = sb(es, "psk", [128, 4, TB]); psv = sb(es, "psv", [128, 4, TB])
        ps12 = sb(es, "ps12", [128, TB])
        psx = [T(g_[:, i, :], f"psx{gi_}_{i}", buf=g_.b) for gi_, g_ in enumerate([psr, psk, psv]) for i in range(4)] + [ps12]
        sg = sb(es, "sg", [128, 4, TB]); av = sb(es, "av", [128, 4, TB])
        gsil = sbl("gsil", [128, 4, TB], BF16)
        gpsil = sb(es, "gpsil", [128, 4, TB], BF16)
        uext = sb(es, "uext", [128, 4, 15 + TB])
        th = sb(es, "th", [64, TB])
        wbig = [sb(es, f"wbig{i}", [128, 4, TB]) for i in range(4)]
        w1, w2, w3, w4 = wbig
        srot = [T(wbig[i][:].rearrange("p f t -> p (f t)")[:, 0:15 + TB], f"srot{i}", buf=wbig[i].b) for i in range(4)]
        kkn = sb(es, "kkn", [128, 4, TB]); kmod = sb(es, "kmod", [128, 4, TB]); bv_ = sb(es, "bv_", [128, 4, TB])
        cum = sb(es, "cum", [128, 4, TB])
        dpl = T(kkn[:, 0, :], "dpl", buf=kkn.b)
        at = sbl("at", [64, 4, 2, TB], BF16)
        rt = sbl("rt", [64, 4, 2, TB], BF16)
        bt = sb(es, "bt", [64, 4, 2, TB], BF16)
        kt = sb(es, "kt", [64, 4, 2, TB], BF16)
        bh = sb(es, "bh", [128, 4, TB], BF16); kh = sb(es, "kh", [128, 4, TB], BF16); vb = sb(es, "vb", [128, 4, TB], BF16)
        bon = sbl("bon", [128, 4, TB])
        gC = sbl("gC", [64, 4, 2, NCH])
        VT = [[sb(es, f"VT{p}{c}", [64, 512], BF16) for c in range(NCH)] for p in range(2)]
        BKT = [[sb(es, f"BKT{p}{c}", [64, 1024], BF16) for c in range(NCH)] for p in range(2)]
        Aak = [[sb(es, f"Aak{p}{c}", [64, 512], BF16) for c in range(NCH)] for p in range(2)]
        Arb = [[sb(es, f"Arb{p}{c}", [64, 512], BF16) for c in range(NCH)] for p in range(2)]
        Ark = [[sb(es, f"Ark{p}{c}", [64, 512], BF16) for c in range(NCH)] for p in range(2)]
        Minv = [[sb(es, f"Minv{p}{c}", [64, 512], BF16) for c in range(NCH)] for p in range(2)]
        Nsb = [sb(es, f"Nsb{c}", [64, 512], BF16) for c in range(NCH)]
        NTsb = [sb(es, f"NTsb{c}", [64, 512], BF16) for c in range(NCH)]
        Xa0 = [sb(es, f"Xa0{c}", [64, 512], BF16) for c in range(NCH)]
        XTa0 = [sb(es, f"XTa0{c}", [64, 512], BF16) for c in range(NCH)]
        Qtmp = [sb(es, f"Qtmp{c}", [64, 512], BF16) for c in range(NCH)]
        ST = sb(es, "ST", [64, 8, 64]); STb = sb(es, "STb", [64, 8, 64], BF16)
        Wsb = sb(es, "Wsb", [64, 512], BF16); Usb = sb(es, "Usb", [64, 512], BF16)
        yc = sb(es, "yc", [64, 512]); ysq = sb(es, "ysq", [64, 512])
        STt = T(ysq[:].rearrange("p (h v) -> p h v", v=64), "STt", buf=ysq.b)
        m8 = sb(es, "m8", [64, 8]); v8 = sb(es, "v8", [64, 8]); r8 = sb(es, "r8", [64, 8]); t8 = sb(es, "t8", [64, 8])
        o1 = sb(es, "o1", [128, 4, 64])
        oT = sbl("oT", [128, 8, TB], BF16)
        ssum = sb(es, "ssum", [128, 1]); tmp1 = sb(es, "tmp1", [128, 1]); rstd = sb(es, "rstd", [128, 1])
        ppT = T(ysq[0:16, :], "ppT", buf=ysq.b); m13 = sb(es, "m13", [13, 128])
        SvT = T(yc[:].rearrange("p (h k) -> p h k", k=64), "SvT", buf=yc.b)

        memset(halo[:], 0.0, writes=[halo])
        memset(uext[:, :, 0:15], 0.0, writes=[uext])
        memset(ST[:], 0.0, writes=[ST])
        memset(STb[:], 0.0, writes=[STb])
        vts(omu[:], pvec[:, PV_MU:PV_MU + 13], -1.0, 1.0, ALU.mult, ALU.add, reads=[pvec], writes=[omu])

        def b8(ap):
            return ap.unsqueeze(1).to_broadcast([64, 8, 64])

        def h3(ap):
            return ap.rearrange("p (h v) -> p h v", v=64)

        def hc(h):
            return slice(h * 64, (h + 1) * 64)

        maskUs = b8(cst[0:64, C_MUS:C_MUS + 64])
        maskUi = b8(cst[0:64, C_MUI:C_MUI + 64])
        maskLs = b8(cst[0:64, C_MLS:C_MLS + 64])
        ident8 = b8(cst[0:64, C_ID:C_ID + 64])
        rstm = cst[:, C_RST:C_RST + 512]
        st = dict(gk=0, ak=0)
        pT32 = T(pT[:].bitcast(F32), "pT32", buf=pT.b)
        abanks = [pA, pB, pg[0], pg[1], pM, pT32]

        def nextbank():
            b = abanks[st["ak"] % len(abanks)]
            st["ak"] += 1
            return b

        def front(tb):
            pb = tb % 2
            t0 = tb * TB
            x_t = xt[pb]
            S.dma("sync", x_t[:], xp[t0:t0 + TB, :], writes=[x_t])
            act(sqx[:], x_t[:], AF.Square, reads=[x_t], writes=[sqx, ssx], accum=ssx[:])
            rsqrt_small(rsx[:], ssx[:], t1x[:], 1.0 / D, NORM_EPS, reads=[ssx], writes=[t1x, rsx])
            act(xnb[:], x_t[:], AF.Copy, reads=[x_t, rsx], writes=[xnb], scale=rsx[:, 0:1])
            yield
            for dc in range(8):
                tr(pT[:, dc * 128:(dc + 1) * 128], xnb[:, dc * 128:(dc + 1) * 128], identb[:], reads=[xnb, identb], writes=[pT])
            vcopy(hT[:].rearrange("p c t -> p (c t)"), pT[:], reads=[pT], writes=[hT])
            yield

            def gemm_group(ebs):
                bank = pg[st["gk"] % 2]
                st["gk"] += 1
                for i, eb in enumerate(ebs):
                    for dc in range(8):
                        mm(bank[:, i * TB:(i + 1) * TB], winb[:, dc, eb * 128:(eb + 1) * 128], hT[:, dc, :], dc == 0, dc == 7,
                           reads=[winb, hT], writes=[bank])
                return bank

            for gi, ebs in enumerate([[0, 1, 2, 3], [4, 5, 6, 7], [8, 9, 10, 11], [12]]):
                bank = gemm_group(ebs)
                yield
                n = len(ebs)
                pr = praw[gi % 2]
                e0 = ebs[0]
                gT = [psr, psk, psv, ps12][gi]
                dst = gT[:, 0:n, :] if gi < 3 else ps12[:].unsqueeze(1)
                mub = pvec[:, PV_MU + e0:PV_MU + e0 + n].unsqueeze(2).to_broadcast([128, n, TB])
                vcopy(pr[:, 0:n, 0:1], halo[:, e0:e0 + n].unsqueeze(2), reads=[halo], writes=[pr], eng="gpsimd")
                act(pr[:, 0:n, 1:TB + 1], bank[:, 0:n * TB].rearrange("p (e t) -> p e t", t=TB), AF.Copy, reads=[bank], writes=[pr])
                vtt(dst, pr[:, 0:n, 0:TB], pr[:, 0:n, 1:TB + 1], ALU.subtract, reads=[pr], writes=[gT])
                vtt(dst, dst, mub, ALU.mult, reads=[gT, pvec], writes=[gT])
                vtt(dst, dst, pr[:, 0:n, 1:TB + 1], ALU.add, reads=[gT, pr], writes=[gT])
                vcopy(halo[:, e0:e0 + n].unsqueeze(2), pr[:, 0:n, TB:TB + 1], reads=[pr], writes=[halo], eng="gpsimd")
                yield
            bank = gemm_group([13, 14, 15, 16])
            act(gsil[pb][:].rearrange("p f t -> p (f t)"), bank[:, :], AF.Silu, reads=[bank], writes=[gsil[pb]])
            yield
            bank = gemm_group([17, 18, 19, 20])
            act(uext[:, :, 15:15 + TB], bank[:, :].rearrange("p (g t) -> p g t", t=TB), AF.Copy, reads=[bank], writes=[uext])
            yield
            bank = gemm_group([21, 22, 23, 24])
            act(gpsil[:].rearrange("p g t -> p (g t)"), bank[:, :], AF.Silu, reads=[bank], writes=[gpsil])
            yield

            act(th[:], psx[12][0:64, :], AF.Tanh, reads=[psx[12]], writes=[th])
            for fb in range(4):
                mm(pA[:, fb * TB:(fb + 1) * TB], wd[:, fb * 128:(fb + 1) * 128], th[:], True, True, reads=[wd, th], writes=[pA])
            for fb in range(4):
                mm(pM[:, fb * TB:(fb + 1) * TB], wa[64:128, fb * 128:(fb + 1) * 128], psx[12][64:128, :], True, True, reads=[wa, psx[12]], writes=[pM])
            for fb in range(4):
                act(sg[:, fb, :], pA[:, fb * TB:(fb + 1) * TB], AF.Sigmoid, reads=[pA, pvec], writes=[sg], bias=pvec[:, PV_W0 + fb:PV_W0 + fb + 1])
                act(av[:, fb, :], pM[:, fb * TB:(fb + 1) * TB], AF.Sigmoid, reads=[pM, pvec], writes=[av], bias=pvec[:, PV_A0 + fb:PV_A0 + fb + 1])
            yield

            def pb4(col):
                return pvec[:, col:col + 4].unsqueeze(2).to_broadcast([128, 4, TB])

            def f2(t_):
                return t_[:].rearrange("p f t -> p (f t)")

            vcopy(vb[:], psv[:], reads=[psv], writes=[vb], eng="gpsimd")
            vtt(w1[:], psk[:], pb4(PV_KK), ALU.mult, reads=[psk, pvec], writes=[w1])
            vtt(w2[:], w1[:], w1[:], ALU.mult, reads=[w1], writes=[w2])
            mm(pM[:, :], onesblk, f2(w2), True, True, reads=[cst, w2], writes=[pM])
            vts(f2(w2), pM[:, :], L2_EPS, None, ALU.add, None, reads=[pM], writes=[w2])
            act(w2[:], w2[:], AF.Ln, reads=[w2], writes=[w2])
            act(w2[:], w2[:], AF.Exp, reads=[w2], writes=[w2], scale=-0.5)
            vstt(kkn[:], w1[:], -1.0, w2[:], ALU.mult, ALU.mult, reads=[w1, w2], writes=[kkn])
            yield
            vtt(w1[:], av[:], pb4(PV_KA), ALU.mult, reads=[av, pvec], writes=[w1])
            vtt(w1[:], w1[:], omka[:, 0:4].unsqueeze(2).to_broadcast([128, 4, TB]), ALU.add, reads=[w1, omka], writes=[w1])
            vtt(kmod[:], psk[:], w1[:], ALU.mult, reads=[psk, w1], writes=[kmod])
            vstt(bv_[:], kkn[:], -1.0, av[:], ALU.mult, ALU.mult, reads=[kkn, av], writes=[bv_])
            vtt(w1[:], psr[:], pb4(PV_RK), ALU.mult, reads=[psr, pvec], writes=[w1])
            vtt(w1[:], w1[:], kmod[:], ALU.mult, reads=[w1, kmod], writes=[w1])
            mm(pA[:, :], onesblk, f2(w1), True, True, reads=[cst, w1], writes=[pA])
            vtt(f2(bon[pb]), pA[:, :], f2(psv), ALU.mult, reads=[pA, psv], writes=[bon[pb]])
            yield
            S.op("vector", lambda e: e.tensor_tensor_scan(out=f2(cum), data0=rstm, data1=f2(sg), initial=0.0, op0=ALU.mult, op1=ALU.add),
                 reads=[cst, sg], writes=[cum], cost=1.2)
            c3 = cum[:].rearrange("p f (c t) -> p (f c) t", t=CH)
            vtt(w1[:], cum[:], sg[:], ALU.subtract, reads=[cum, sg], writes=[w1])
            act(w2[:], cum[:], AF.Exp, reads=[cum], writes=[w2], scale=-C0)
            act(w3[:], cum[:], AF.Exp, reads=[cum], writes=[w3], scale=C0)
            act(w1[:], w1[:], AF.Exp, reads=[w1], writes=[w1], scale=-C0)
            vtt(w4[:].rearrange("p f (c t) -> p (f c) t", t=CH), c3[:, :, CH - 1:CH].to_broadcast([128, 4 * NCH, CH]), c3, ALU.subtract,
                reads=[cum], writes=[w4], eng="gpsimd")
            act(w4[:], w4[:], AF.Exp, reads=[w4], writes=[w4], scale=-C0)
            yield
            for j in range(2):
                pp = slice(64 * j, 64 * j + 64)
                e_ = "vector" if j == 0 else "gpsimd"
                vtt(rt[pb][:, :, j, :], psr[pp, :, :], w2[pp, :, :], ALU.mult, reads=[psr, w2], writes=[rt[pb]], eng=e_)
                vtt(bt[:, :, j, :], bv_[pp, :, :], w3[pp, :, :], ALU.mult, reads=[bv_, w3], writes=[bt], eng=e_)
                vtt(kt[:, :, j, :], kmod[pp, :, :], w3[pp, :, :], ALU.mult, reads=[kmod, w3], writes=[kt], eng=e_)
                vtt(at[pb][:, :, j, :], kkn[pp, :, :], w1[pp, :, :], ALU.mult, reads=[kkn, w1], writes=[at[pb]], eng=e_)
                act(gC[pb][:, :, j, :], cum[pp, :, :].rearrange("p f (c t) -> p f c t", t=CH)[:, :, :, CH - 1], AF.Exp,
                    reads=[cum], writes=[gC[pb]], scale=-C0)
            vtt(bh[:], bv_[:], w4[:], ALU.mult, reads=[bv_, w4], writes=[bh])
            vtt(kh[:], kmod[:], w4[:], ALU.mult, reads=[kmod, w4], writes=[kh], eng="gpsimd")
            yield

            L = 15 + TB
            for g in range(4):
                vtt(srot[0][:, 1:], uext[:, g, 1:], uext[:, g, 0:L - 1], ALU.add, reads=[uext], writes=[srot[0]], eng="gpsimd")
                tot = srot[0]
                if g >= 1:
                    vtt(srot[1][:, 3:], srot[0][:, 3:], srot[0][:, 1:L - 2], ALU.add, reads=[srot[0]], writes=[srot[1]], eng="gpsimd")
                    tot = srot[1]
                if g >= 2:
                    vtt(srot[2][:, 7:], srot[1][:, 7:], srot[1][:, 3:L - 4], ALU.add, reads=[srot[1]], writes=[srot[2]], eng="gpsimd")
                    tot = srot[2]
                if g >= 3:
                    vtt(srot[3][:, 15:], srot[2][:, 15:], srot[2][:, 7:L - 8], ALU.add, reads=[srot[2]], writes=[srot[3]], eng="gpsimd")
                    tot = srot[3]
                vstt(dpl[:], tot[:, 15:], 1.0 / WINS[g], uext[:, g, 15:], ALU.mult, ALU.subtract, reads=[tot, uext], writes=[dpl])
                if tb == 0:
                    vtt(dpl[:, 0:16], tot[:, 15:31], cst[:, C_ICNT + g * 16:C_ICNT + (g + 1) * 16], ALU.mult, reads=[tot, cst], writes=[dpl])
                    vtt(dpl[:, 0:16], dpl[:, 0:16], uext[:, g, 15:31], ALU.subtract, reads=[dpl, uext], writes=[dpl])
                mm(pM[:, 0:TB], pw[:, g, :], dpl[:], True, True, reads=[pw, dpl], writes=[pM])
                vstt(oT[pb][:, 4 + g, :], pM[:, 0:TB], pvec[:, PV_PS + g:PV_PS + g + 1], gpsil[:, g, :], ALU.mult, ALU.mult,
                     reads=[pM, pvec, gpsil], writes=[oT[pb]])
                yield
            if tb == NTB - 1:
                for g in range(4):
                    tr(pA[0:16, g * 128:(g + 1) * 128], uext[:, g, TB - 1:TB + 15], ident, reads=[uext, cst], writes=[pA])
                vcopy(ppT[:], pA[0:16, :], reads=[pA], writes=[ppT])
                S.dma("sync", npp[:], ppT[1:16, :], reads=[ppT], writes=[npp])
                tr(pB[0:13, 0:128], halo[:, 0:13], ident, reads=[halo, cst], writes=[pB])
                vcopy(m13[:], pB[0:13, 0:128], reads=[pB], writes=[m13])
                S.dma("sync", nsp[:], m13[:], reads=[m13], writes=[nsp])
            vcopy(uext[:, :, 0:15], uext[:, :, TB:TB + 15], reads=[uext], writes=[uext], eng="gpsimd")
            yield

            css = [slice(c * CH, (c + 1) * CH) for c in range(NCH)]
            for c in range(NCH):
                for qi, srcl in enumerate([bh, kh]):
                    for fb in range(4):
                        tr(pT[0:64, qi * 512 + fb * 128:qi * 512 + (fb + 1) * 128], srcl[:, fb, css[c]], identb[:], reads=[srcl, identb], writes=[pT])
                vcopy(BKT[pb][c][:], pT[0:64, :], reads=[pT], writes=[BKT[pb][c]])
                for fb in range(4):
                    tr(pT[0:64, fb * 128:(fb + 1) * 128], vb[:, fb, css[c]], identb[:], reads=[vb, identb], writes=[pT])
                act(VT[pb][c][:], pT[0:64, 0:512], AF.Copy, reads=[pT], writes=[VT[pb][c]])
                yield

            def hsl(tl, h, c):
                fb, j = divmod(h, 2)
                return tl[:, fb, j, css[c]]

            for (Lt, Rt, mask, dsts) in [(bt, at[pb], maskUs, Nsb), (at[pb], bt, maskLs, NTsb), (kt, at[pb], maskUs, Aak[pb]),
                                         (bt, rt[pb], maskUi, Arb[pb]), (kt, rt[pb], maskUi, Ark[pb])]:
                banks = []
                for c in range(NCH):
                    bank = nextbank()
                    banks.append(bank)
                    for h in range(8):
                        mm(bank[0:64, hc(h)], hsl(Lt, h, c), hsl(Rt, h, c), True, True, reads=[Lt, Rt], writes=[bank])
                for c in range(NCH):
                    vtt(h3(dsts[c][:]), h3(banks[c][0:64, :]), mask, ALU.mult, reads=[banks[c], cst], writes=[dsts[c]])
                yield
            X = list(Nsb); XT = list(NTsb)
            Q = [Qtmp[c] for c in range(NCH)]
            for c in range(NCH):
                vtt(h3(Q[c][:]), h3(Nsb[c][:]), ident8, ALU.add, reads=[Nsb[c], cst], writes=[Q[c]])
            for lvl in range(5):
                Xn = [(Xa0[c] if lvl % 2 == 0 else Nsb[c]) for c in range(NCH)]
                XTn = [(XTa0[c] if lvl % 2 == 0 else NTsb[c]) for c in range(NCH)]
                Qn = [(Minv[pb][c] if lvl % 2 == 0 else Qtmp[c]) for c in range(NCH)]
                banks = []
                for c in range(NCH):
                    bank = nextbank(); banks.append(bank)
                    for h in range(8):
                        mm(bank[0:64, hc(h)], X[c][:, hc(h)], XT[c][:, hc(h)], True, True, reads=[X[c], XT[c]], writes=[bank])
                for c in range(NCH):
                    act(XTn[c][:], banks[c][0:64, :], AF.Copy, reads=[banks[c]], writes=[XTn[c]])
                yield
                if lvl < 4:
                    banks = []
                    for c in range(NCH):
                        bank = nextbank(); banks.append(bank)
                        for h in range(8):
                            mm(bank[0:64, hc(h)], XT[c][:, hc(h)], X[c][:, hc(h)], True, True, reads=[X[c], XT[c]], writes=[bank])
                    for c in range(NCH):
                        act(Xn[c][:], banks[c][0:64, :], AF.Copy, reads=[banks[c]], writes=[Xn[c]])
                    yield
                banks = []
                for c in range(NCH):
                    bank = nextbank(); banks.append(bank)
                    for h in range(8):
                        mm(bank[0:64, hc(h)], XTn[c][:, hc(h)], Q[c][:, hc(h)], True, True, reads=[XTn[c], Q[c]], writes=[bank])
                for c in range(NCH):
                    vtt(Qn[c][:], banks[c][0:64, :], Q[c][:], ALU.add, reads=[banks[c], Q[c]], writes=[Qn[c]])
                X, XT, Q = Xn, XTn, Qn
                yield

        def chain(tb):
            pb = tb % 2
            t0 = tb * TB
            for c in range(NCH):
                cs = slice(c * CH, (c + 1) * CH)
                aT, rT = at[pb], rt[pb]
                VTc, BKTc, Aakc, Arbc, Arkc, Minvc = VT[pb][c], BKT[pb][c], Aak[pb][c], Arb[pb][c], Ark[pb][c], Minv[pb][c]
                for h in range(8):
                    fb, j = divmod(h, 2)
                    mm(pC[0:64, hc(h)], aT[:, fb, j, cs], STb[:, h, :], True, False, reads=[aT, STb], writes=[pC])
                    mm(pC[0:64, hc(h)], Aakc[:, hc(h)], VTc[:, hc(h)], False, True, reads=[Aakc, VTc], writes=[pC])
                act(Wsb[:], pC[0:64, :], AF.Copy, reads=[pC], writes=[Wsb])
                yield
                for h in range(8):
                    mm(pC[0:64, hc(h)], Minvc[:, hc(h)], Wsb[:, hc(h)], True, True, reads=[Minvc, Wsb], writes=[pC])
                act(Usb[:], pC[0:64, :], AF.Copy, reads=[pC], writes=[Usb])
                yield
                for h in range(8):
                    mm(pC[0:64, hc(h)], BKTc[:, hc(h)], Usb[:, hc(h)], True, False, reads=[BKTc, Usb], writes=[pC])
                    mm(pC[0:64, hc(h)], BKTc[:, 512 + h * 64:512 + (h + 1) * 64], VTc[:, hc(h)], False, True, reads=[BKTc, VTc], writes=[pC])
                for h in range(8):
                    fb, j = divmod(h, 2)
                    mm(pD[0:64, hc(h)], rT[:, fb, j, cs], STb[:, h, :], True, False, reads=[rT, STb], writes=[pD])
                    mm(pD[0:64, hc(h)], Arbc[:, hc(h)], Usb[:, hc(h)], False, False, reads=[Arbc, Usb], writes=[pD])
                    mm(pD[0:64, hc(h)], Arkc[:, hc(h)], VTc[:, hc(h)], False, True, reads=[Arkc, VTc], writes=[pD])
                vtt(STt[:], ST[:], gC[pb][:].rearrange("p f j c -> p (f j) c")[:, :, c:c + 1].to_broadcast([64, 8, 64]), ALU.mult,
                    reads=[ST, gC[pb]], writes=[STt])
                vtt(ST[:], STt[:], h3(pC[0:64, :]), ALU.add, reads=[STt, pC], writes=[ST])
                act(STb[:], ST[:], AF.Copy, reads=[ST], writes=[STb])
                yield
                y3 = h3(pD[0:64, :])
                vred(m8[:], y3, reads=[pD], writes=[m8])
                vts(m8[:], m8[:], 1.0 / 64, None, ALU.mult, None, reads=[m8], writes=[m8])
                vtt(h3(yc[:]), y3, m8[:].unsqueeze(2).to_broadcast([64, 8, 64]), ALU.subtract, reads=[pD, m8], writes=[yc])
                act(ysq[:], yc[:], AF.Square, reads=[yc], writes=[ysq])
                vred(v8[:], h3(ysq[:]), reads=[ysq], writes=[v8])
                rsqrt_small(r8[:], v8[:], t8[:], 1.0 / 64, GN_EPS, reads=[v8], writes=[t8, r8])
                vtt(h3(yc[:]), h3(yc[:]), r8[:].unsqueeze(2).to_broadcast([64, 8, 64]), ALU.mult, reads=[yc, r8], writes=[yc], eng="gpsimd")
                yield
                for fb in range(4):
                    tr(pD[:, fb * 64:(fb + 1) * 64], yc[:, fb * 128:(fb + 1) * 128], ident[0:64, 0:64], reads=[yc, cst], writes=[pD])
                for fb in range(4):
                    vts(o1[:, fb, :], pD[:, fb * 64:(fb + 1) * 64], pvec[:, PV_GW + fb:PV_GW + fb + 1], pvec[:, PV_GB + fb:PV_GB + fb + 1],
                        ALU.mult, ALU.add, reads=[pD, pvec], writes=[o1])
                vtt(o1[:], o1[:], bon[pb][:, :, cs], ALU.add, reads=[o1, bon[pb]], writes=[o1], eng="gpsimd")
                vtt(oT[pb][:, 0:4, cs], o1[:], gsil[pb][:, :, cs], ALU.mult, reads=[o1, gsil[pb]], writes=[oT[pb]])
                yield
            x_t = xt[pb]
            for half in range(2):
                bank = pD if half == 0 else pC
                for fc in range(8):
                    mm(bank[:, :], oT[pb][:, fc, :], woutb[:, fc, half * 512:(half + 1) * 512], fc == 0, fc == 7, reads=[oT[pb], woutb], writes=[bank])
                vtt(x_t[:, half * 512:(half + 1) * 512], bank[:, :], x_t[:, half * 512:(half + 1) * 512], ALU.add, reads=[bank, x_t], writes=[x_t])
                yield
            act(yo[:], x_t[:], AF.Square, reads=[x_t], writes=[yo, ssum], accum=ssum[:])
            rsqrt_small(rstd[:], ssum[:], tmp1[:], 1.0 / D, NORM_EPS, reads=[ssum], writes=[tmp1, rstd])
            vstt(yo[:], x_t[:], rstd[:, 0:1], normf[:], ALU.mult, ALU.mult, reads=[x_t, rstd, normf], writes=[yo])
            S.dma("sync", yp[t0:t0 + TB, :], yo[:], reads=[yo], writes=[yp])
            yield

        def run_all(g):
            n = 0
            for _ in g:
                n += 1
            return n

        def interleave(ga, na, gb, nb):
            ia = ib = 0
            da = db = False
            while not (da and db):
                pick_a = (not da) and (db or (ia * nb <= ib * na))
                if pick_a:
                    try:
                        next(ga); ia += 1
                    except StopIteration:
                        da = True
                else:
                    try:
                        next(gb); ib += 1
                    except StopIteration:
                        db = True
            return ia, ib

        run_all(front(0))

        def record_units(g):
            units = []
            S.rec = []
            for _ in g:
                if S.rec:
                    units.append(S.rec)
                S.rec = []
            if S.rec:
                units.append(S.rec)
            S.rec = None
            return units

        A, B = [], []
        for tb in range(NTB):
            A.append(record_units(chain(tb)))
            if tb + 1 < NTB:
                B.append(record_units(front(tb + 1)))
        S.merge_emit(A, B, a_ok=lambda ia, ib: ib >= ia, b_ok=lambda ib, ia: ia >= ib)
        for h in range(8):
            tr(pA[0:64, h * 64:(h + 1) * 64], ST[:, h, :], ident[0:64, 0:64], reads=[ST, cst], writes=[pA])
        vcopy(SvT[:].rearrange("p h k -> p (h k)"), pA[0:64, :], reads=[pA], writes=[SvT])
        S.dma("sync", nwp[:].rearrange("h v k -> v h k"), SvT[:], reads=[SvT], writes=[nwp])
        S.finish([yp, ys, nsp, nwp, npp, nss, nws, nps], engname="sync")
        S.barrier()
    es_top.close()
    return nc, S


_CACHE = {}


def _consts():
    cst = np.zeros((128, C_END), np.float32)
    cst[:, C_ID:C_ID + 128] = np.eye(128, dtype=np.float32)
    ob = np.zeros((128, 128), np.float32)
    ob[0:64, 0:64] = 1.0
    ob[64:128, 64:128] = 1.0
    cst[:, C_ONES:C_ONES + 128] = ob
    s = np.arange(64)[:, None]
    t = np.arange(64)[None, :]
    mus = (s < t).astype(np.float32)
    mui = (s <= t).astype(np.float32)
    mls = (s > t).astype(np.float32)
    i64 = np.eye(64, dtype=np.float32)
    cst[0:64, C_MUS:C_MUS + 64] = mus
    cst[0:64, C_MUI:C_MUI + 64] = mui
    cst[0:64, C_MLS:C_MLS + 64] = mls
    rst = np.ones((512,), np.float32)
    rst[::CH] = 0.0
    cst[:, C_RST:C_RST + 512] = rst[None, :]
    for g, w in enumerate(WINS):
        pos = np.arange(16)
        cst[:, C_ICNT + g * 16:C_ICNT + (g + 1) * 16] = (1.0 / np.minimum(pos + 1, w)).astype(np.float32)[None, :]
    return cst


def kernel(x_prompt, x_sample, state_shift, state_wkv, state_pool, norm_w, w_in, mu_shift,
           w_decay_b, w0, w_aaa_b, a0, k_k, k_a, r_k, gn_w, gn_b, pool_w, pool_scale, w_out, norm_f):
    f = lambda a: np.ascontiguousarray(np.asarray(a, dtype=np.float32))
    x_prompt, x_sample, state_shift, state_wkv, state_pool = map(f, (x_prompt, x_sample, state_shift, state_wkv, state_pool))
    if "nc" not in _CACHE:
        _CACHE["nc"] = build_program()
    nc, S = _CACHE["nc"]

    def colmajor(v, n):
        return f(v).reshape(n, 128).T

    pvec = np.concatenate([
        colmajor(norm_w[0], 8), colmajor(mu_shift[0], 13), colmajor(w0[0], 4), colmajor(a0[0], 4), colmajor(k_k[0], 4),
        colmajor(k_a[0], 4), colmajor(f(r_k[0]).reshape(-1), 4), colmajor(gn_w[0], 4), colmajor(gn_b[0], 4), colmajor(pool_scale[0], 4)], axis=1)
    pvec = f(pvec)
    browA = f(f(mu_shift[0])[None, :])
    browB = f(np.concatenate([f(w0[0]), f(a0[0]), f(k_k[0]), f(k_a[0]), f(r_k[0]).reshape(-1), f(gn_w[0]), f(gn_b[0])])[None, :])
    cst = _consts()
    shared = {
        "w_in": f(w_in[0]), "w_out": f(w_out[0]), "wdec": f(w_decay_b[0]), "waaa": f(w_aaa_b[0]), "poolw": f(pool_w[0]),
        "pvec": pvec, "browA": browA, "browB": browB, "normf": f(norm_f)[None, :], "cst": cst,
    }
    in_maps = []
    for c in range(NCORE):
        bs = slice(c * DB, (c + 1) * DB)
        m = dict(shared)
        m["xp"] = x_prompt[c]
        m["xs"] = f(x_sample[bs].transpose(1, 0, 2).reshape(NS, D))
        m["sshift"] = state_shift[0, bs]
        m["swkv"] = f(state_wkv[0, bs].reshape(128, 4096))
        m["spool"] = f(state_pool[0, bs].reshape(DB * 15, 512))
        in_maps.append(m)
    res = run_bass_kernel_spmd(nc, in_maps, core_ids=list(range(NCORE)))
    R = res.results
    y_prompt = np.stack([R[c]["yp"] for c in range(NCORE)], axis=0)
    y_sample = np.concatenate([R[c]["ys"].reshape(DT, DB, D).transpose(1, 0, 2) for c in range(NCORE)], axis=0)
    nsp = np.stack([R[c]["nsp"].reshape(D_SHIFT) for c in range(NCORE)], axis=0)[None]
    nwp = np.stack([R[c]["nwp"] for c in range(NCORE)], axis=0)[None]
    npp = np.stack([R[c]["npp"] for c in range(NCORE)], axis=0)[None]
    nss = np.concatenate([R[c]["nss"] for c in range(NCORE)], axis=0)[None]
    nws = np.concatenate([R[c]["nws"].reshape(DB, 8, 64, 64) for c in range(NCORE)], axis=0)[None]
    nps = np.concatenate([R[c]["nps"] for c in range(NCORE)], axis=0)[None]
    out = (y_prompt, y_sample, nsp, nwp, npp, nss, nws, nps)
    return tuple(np.ascontiguousarray(o.astype(np.float32)) for o in out)
```

```python
import numpy as np
from contextlib import ExitStack
import concourse.bass as bass
import concourse.mybir as mybir
from concourse.bass_utils import run_bass_kernel_spmd

F32 = mybir.dt.float32
BF16 = mybir.dt.bfloat16
AF = mybir.ActivationFunctionType
ALU = mybir.AluOpType
AX = mybir.AxisListType

D = 1024
SEQ = 2048
NCORE = 8
DB = 16
DT = 4
NS = DB * DT
D_SHIFT = 1664
D_IN = 3200
C0 = float(np.exp(-0.5))
NORM_EPS = 1e-6
GN_EPS = 64e-5
L2_EPS = 1e-12
TB = 128
NTB = SEQ // TB
CH = 64
FBIAS = 0.0
NCH = TB // CH
WINS = (2, 4, 8, 16)

C_ID, C_ONES, C_MUS, C_MUI, C_MLS, C_RST, C_ICNT, C_END = 0, 128, 256, 320, 384, 448, 960, 1024
PV_NW, PV_MU, PV_W0, PV_A0, PV_KK, PV_KA, PV_RK, PV_GW, PV_GB, PV_PS, PV_END = 0, 8, 21, 25, 29, 33, 37, 41, 45, 49, 53
BRB_W0, BRB_A0, BRB_KK, BRB_KA, BRB_RK, BRB_GW, BRB_GB = 0, 512, 1024, 1536, 2048, 2560, 3072


class Buf:
    __slots__ = ("name", "w", "r")

    def __init__(self, name):
        self.name = name
        self.w = None
        self.r = []


class T:
    def __init__(self, t, name, buf=None):
        self.t = t
        self.b = buf if buf is not None else Buf(name)

    def __getitem__(self, k):
        return self.t[k]


class Sched:
    def __init__(self, nc, n_dma_sems=32):
        self.nc = nc
        self.eng = {}
        for name in ["tensor", "vector", "scalar", "gpsimd", "sync"]:
            h = getattr(nc, name)
            sem = nc.alloc_semaphore(name="prog_" + name)
            self.eng[name] = dict(h=h, sem=sem, cnt=0, waited={})
        self.dma_sems = [dict(sem=nc.alloc_semaphore(name=f"dma{i}"), cnt=0) for i in range(n_dma_sems)]
        self.dma_rr = 0
        self.ninstr = 0
        self.rec = None

    def _wait(self, engname, tok):
        sem, val, src = tok
        e = self.eng[engname]
        key = id(sem)
        if e["waited"].get(key, 0) >= val:
            return
        e["h"].wait_ge(sem, val)
        e["waited"][key] = val
        self.ninstr += 1

    def _deps(self, engname, reads, writes):
        toks = []
        for b in reads:
            if b.w is not None:
                toks.append(b.w)
        for b in writes:
            if b.w is not None:
                toks.append(b.w)
            toks.extend(b.r)
        for tok in toks:
            if tok[2] == engname and engname == "tensor":
                continue
            self._wait(engname, tok)

    @staticmethod
    def _bufs(xs):
        return [x.b if isinstance(x, T) else x for x in xs]

    def _record(self, tok, reads, writes):
        for b in reads:
            b.r.append(tok)
            if len(b.r) > 64:
                b.r = b.r[-64:] if False else b.r
        for b in writes:
            b.w = tok
            b.r = []

    def op(self, engname, fn, reads=(), writes=(), cost=0.3):
        reads = self._bufs(reads)
        writes = self._bufs(writes)
        if self.rec is not None:
            self.rec.append(("op", engname, fn, reads, writes, cost, None))
            return None
        e = self.eng[engname]
        self._deps(engname, reads, writes)
        ins = fn(e["h"])
        e["cnt"] += 1
        ins.then_inc(e["sem"], 1)
        e["waited"][id(e["sem"])] = max(e["waited"].get(id(e["sem"]), 0), 0)
        tok = (e["sem"], e["cnt"], engname)
        self._record(tok, reads, writes)
        self.ninstr += 1
        return tok

    def dma(self, qname, out, in_, reads=(), writes=(), **kw):
        reads = self._bufs(reads)
        writes = self._bufs(writes)
        if self.rec is not None:
            self.rec.append(("dma", qname, (out, in_), reads, writes, 2.5, kw))
            return None
        e = self.eng[qname]
        self._deps(qname, reads, writes)
        d = self.dma_sems[self.dma_rr]
        self.dma_rr = (self.dma_rr + 1) % len(self.dma_sems)
        if d["cnt"] > 0:
            self._wait(qname, (d["sem"], 16 * d["cnt"], "dma"))
        ins = e["h"].dma_start(out=out, in_=in_, **kw)
        d["cnt"] += 1
        ins.then_inc(d["sem"], 16)
        tok = (d["sem"], 16 * d["cnt"], "dma")
        self._record(tok, reads, writes)
        self.ninstr += 1
        return tok

    def emit(self, r):
        kind, eng, fn, reads, writes, cost, kw = r
        if kind == "op":
            self.op(eng, fn, reads=reads, writes=writes)
        else:
            self.dma(eng, fn[0], fn[1], reads=reads, writes=writes, **kw)

    def merge_emit(self, A, B, a_ok, b_ok):
        eng_free = {}
        ready = {}
        acc = {}

        def est(r):
            kind, eng, fn, reads, writes, cost, kw = r
            t = eng_free.get(eng, 0.0)
            for b in reads:
                rt_, re_ = ready.get(id(b), (0.0, eng))
                t = max(t, rt_ + (0.15 if re_ != eng else 0.0))
            for b in writes:
                rt_, re_ = ready.get(id(b), (0.0, eng))
                t = max(t, rt_ + (0.15 if re_ != eng else 0.0), acc.get(id(b), 0.0) + 0.1)
            return t

        def commit(r, t):
            kind, eng, fn, reads, writes, cost, kw = r
            if kind == "dma":
                eng_free[eng] = t + 0.1
                end = t + cost
            else:
                end = t + cost
                eng_free[eng] = end
            for b in reads:
                acc[id(b)] = max(acc.get(id(b), 0.0), end)
            for b in writes:
                ready[id(b)] = (end, eng)
                acc[id(b)] = max(acc.get(id(b), 0.0), end)

        def run_unit(u):
            for r in u:
                commit(r, est(r))
                self.emit(r)

        ia = ib = 0
        ja = jb = 0
        while ia < len(A) or ib < len(B):
            ca = None
            cb = None
            if ia < len(A) and (ja > 0 or a_ok(ia, ib)):
                ca = A[ia][ja]
            if ib < len(B) and (jb > 0 or b_ok(ib, ia)):
                cb = B[ib][jb]
            assert ca is not None or cb is not None, (ia, ib, ja, jb)
            ta = est(ca[0]) if ca is not None else None
            tb_ = est(cb[0]) if cb is not None else None
            if cb is None or (ca is not None and ta + FBIAS < tb_):
                run_unit(ca); ja += 1
                if ja == len(A[ia]):
                    ia += 1; ja = 0
            else:
                run_unit(cb); jb += 1
                if jb == len(B[ib]):
                    ib += 1; jb = 0

    def barrier(self):
        toks = [(e["sem"], e["cnt"], n) for n, e in self.eng.items() if e["cnt"] > 0]
        toks += [(d["sem"], 16 * d["cnt"], "dma") for d in self.dma_sems if d["cnt"] > 0]
        for n in self.eng:
            for tok in toks:
                if tok[2] == n:
                    continue
                self._wait(n, tok)

    def finish(self, tiles, engname="sync"):
        for b in self._bufs(tiles):
            if b.w is not None:
                self._wait(engname, b.w)


class _Stop(Exception):
    pass


def build_program(stop=None):
    nc = bass.Bass("TRN2", target_bir_lowering=False)
    S = Sched(nc)
    try:
        _build_body(nc, S, stop)
    except _Stop:
        S.barrier()
    return nc, S


def _build_body(nc, S, stop):
    def chk(label):
        if stop == label:
            raise _Stop()


    def din(name, shape):
        return nc.dram_tensor(name, list(shape), F32, kind="ExternalInput").ap()

    def dout(name, shape):
        return T(nc.dram_tensor(name, list(shape), F32, kind="ExternalOutput").ap(), name)

    xp = din("xp", [SEQ, D])
    xs = din("xs", [NS, D])
    sshift = din("sshift", [DB, D_SHIFT])
    swkv = din("swkv", [128, 4096])
    spool = din("spool", [DB * 15, 512])
    w_in = din("w_in", [D, D_IN])
    w_out = din("w_out", [D, D])
    wdec = din("wdec", [64, 512])
    waaa = din("waaa", [64, 512])
    poolw = din("poolw", [4, 128, 128])
    pvec_d = din("pvec", [128, PV_END])
    browA_d = din("browA", [1, D_SHIFT])
    browB_d = din("browB", [1, 3584])
    normf_d = din("normf", [1, D])
    cst_d = din("cst", [128, C_END])

    yp = dout("yp", [SEQ, D])
    ys = dout("ys", [NS, D])
    nsp = dout("nsp", [13, 128])
    nwp = dout("nwp", [8, 64, 64])
    npp = dout("npp", [15, 512])
    nss = dout("nss", [DB, D_SHIFT])
    nws = dout("nws", [128, 4096])
    nps = dout("nps", [DB, 15, 512])
    scr1 = T(nc.dram_tensor("scr1", [6, DT, DB, 8, 64], F32, kind="Internal").ap(), "scr1")
    scr2 = T(nc.dram_tensor("scr2", [DB, 8, DT, 64], F32, kind="Internal").ap(), "scr2")

    es_top = ExitStack()

    def sb(es, name, shape, dt=F32):
        return T(es.enter_context(nc.sbuf_tensor("s_" + name, list(shape), dt)), name)

    def pst(name, shape, dt=F32):
        return T(nc.alloc_psum_tensor("p_" + name, list(shape), dt), name)

    def nel(ap):
        n = 1
        for s_ in ap.shape[1:]:
            n *= s_
        return n

    def mm(out, lhsT, rhs, start, stop, reads, writes):
        passes = 4 if lhsT.dtype == F32 else 1
        c_ = max(0.055, nel(rhs) * passes / 2000.0 + 0.03)
        S.op("tensor", lambda e: e.matmul(out, lhsT=lhsT, rhs=rhs, start=start, stop=stop), reads=reads, writes=writes, cost=c_)

    def tr(out, in_, ident, reads, writes):
        S.op("tensor", lambda e: e.transpose(out, in_, ident), reads=reads, writes=writes, cost=0.13)

    def act(out, in_, func, reads, writes, bias=None, scale=None, eng="scalar", accum=None):
        kw = {}
        if accum is not None:
            kw["accum_out"] = accum
        if bias is not None:
            kw["bias"] = bias
        if scale is not None:
            kw["scale"] = scale
        S.op("scalar", lambda e: e.activation(out=out, in_=in_, func=func, **kw), reads=reads, writes=writes,
             cost=0.1 + 0.1 * len(kw) + nel(in_) * 0.00095)

    def ecost(eng, n):
        return 0.08 + n * (0.00105 if eng == "vector" else 0.0025)

    def vtt(out, in0, in1, op, reads, writes, eng="vector"):
        S.op(eng, lambda e: e.tensor_tensor(out=out, in0=in0, in1=in1, op=op), reads=reads, writes=writes, cost=ecost(eng, nel(out)))

    def vts(out, in0, s1, s2, op0, op1, reads, writes, eng="vector"):
        if op1 is None:
            S.op(eng, lambda e: e.tensor_scalar(out=out, in0=in0, scalar1=s1, scalar2=None, op0=op0), reads=reads, writes=writes,
                 cost=ecost(eng, nel(out)))
        else:
            S.op(eng, lambda e: e.tensor_scalar(out=out, in0=in0, scalar1=s1, scalar2=s2, op0=op0, op1=op1), reads=reads, writes=writes,
                 cost=ecost(eng, nel(out)))

    def vstt(out, in0, scalar, in1, op0, op1, reads, writes):
        S.op("vector", lambda e: e.scalar_tensor_tensor(out=out, in0=in0, scalar=scalar, in1=in1, op0=op0, op1=op1), reads=reads, writes=writes,
             cost=ecost("vector", nel(out)))

    def vcopy(out, in_, reads, writes, eng="vector"):
        S.op(eng, lambda e: e.tensor_copy(out=out, in_=in_), reads=reads, writes=writes, cost=ecost(eng, nel(out)))

    def vred(out, in_, reads, writes):
        S.op("vector", lambda e: e.tensor_reduce(out=out, in_=in_, axis=AX.X, op=ALU.add), reads=reads, writes=writes,
             cost=ecost("vector", nel(in_)))

    def vrecip(out, in_, reads, writes):
        S.op("vector", lambda e: e.reciprocal(out=out, in_=in_), reads=reads, writes=writes, cost=0.08 + nel(out) * 0.0084)

    def memset(ap, val, writes, eng="gpsimd"):
        S.op(eng, lambda e: e.memset(ap, val), writes=writes)

    def rsqrt_small(out, in_, tmp, scale, eps, reads, writes):
        act(tmp, in_, AF.Sqrt, reads=reads, writes=writes, bias=None, scale=None) if False else None
        vts(tmp, in_, scale, eps, ALU.mult, ALU.add, reads=reads, writes=writes)
        act(tmp, tmp, AF.Sqrt, reads=writes, writes=writes)
        vrecip(out, tmp, reads=writes, writes=writes)

    def rsqrt_act(out, in_, scale, eps_col, n, reads, writes):
        act(out, in_, AF.Ln, reads=list(reads) + [epsT], writes=writes, scale=scale, bias=epsT[0:n, eps_col:eps_col + 1])
        act(out, out, AF.Exp, reads=writes, writes=writes, scale=-0.5)

    pg = [pst(f"pg{i}", [128, 512]) for i in range(2)]
    pT = pst("pT", [128, 1024], BF16)
    pM = pst("pM", [128, 512])
    pA = pst("pA", [128, 512])
    pB = pst("pB", [128, 512])
    pC = pst("pC", [128, 512])
    pD = pst("pD", [128, 512])

    cst = sb(es_top, "cst", [128, C_END])
    pvec = sb(es_top, "pvec", [128, PV_END])
    omu = sb(es_top, "omu", [128, 13])
    omka = sb(es_top, "omka", [128, 4])
    identb = sb(es_top, "identb", [128, 128], BF16)
    winb = sb(es_top, "winb", [128, 8, D_IN], BF16)
    woutb = sb(es_top, "woutb", [128, 8, D], BF16)
    wd = sb(es_top, "wd", [64, 512])
    wa = sb(es_top, "wa", [128, 512])
    pw = sb(es_top, "pw", [128, 4, 128])
    normf = sb(es_top, "normf", [128, D])
    epsT = sb(es_top, "epsT", [128, 4])

    ident = cst[:, C_ID:C_ID + 128]
    onesblk = cst[:, C_ONES:C_ONES + 128]

    S.dma("sync", cst[:], cst_d, writes=[cst])
    S.dma("sync", pvec[:], pvec_d, writes=[pvec])
    S.dma("sync", wd[:], wdec, writes=[wd])
    S.dma("sync", wa[64:128, :], waaa, writes=[wa])
    S.dma("sync", pw[:], poolw.rearrange("g c e -> c g e"), writes=[pw])
    S.dma("sync", normf[:], normf_d.partition_broadcast(128), writes=[normf])
    vcopy(identb[:], ident, reads=[cst], writes=[identb])
    memset(epsT[:, 0:1], NORM_EPS, writes=[epsT])
    memset(epsT[:, 1:2], GN_EPS, writes=[epsT])
    memset(epsT[:, 2:3], L2_EPS, writes=[epsT])
    vts(omka[:], pvec[:, PV_KA:PV_KA + 4], -1.0, 1.0, ALU.mult, ALU.add, reads=[pvec], writes=[omka])

    with ExitStack() as es:
        stg = [sb(es, f"stg{i}", [128, D_IN]) for i in range(3)]
        for dc in range(8):
            st = stg[dc % 3]
            S.dma("sync", st[:], w_in[dc * 128:(dc + 1) * 128, :], writes=[st])
            h = D_IN // 2
            vts(winb[:, dc, 0:h], st[:, 0:h], pvec[:, PV_NW + dc:PV_NW + dc + 1], None, ALU.mult, None, reads=[st, pvec], writes=[winb])
            act(winb[:, dc, h:], st[:, h:], AF.Copy, reads=[st, pvec], writes=[winb], scale=pvec[:, PV_NW + dc:PV_NW + dc + 1])
        S.barrier()
        chk("W")

    def final_tile(es_tiles, n, x_t, oT_list, out_dram_ap, out_T):
        res, sq, ssum, tmp1, rstd, yo = es_tiles
        for half in range(2):
            bank = pD if half == 0 else pC
            for fc in range(8):
                mm(bank[0:n, :], oT_list[fc], woutb[:, fc, half * 512:(half + 1) * 512], fc == 0, fc == 7,
                   reads=[oT_list_T, woutb], writes=[bank])
            vtt(res[0:n, half * 512:(half + 1) * 512], bank[0:n, :], x_t[0:n, half * 512:(half + 1) * 512], ALU.add,
                reads=[bank, x_t], writes=[res])
        act(sq[0:n, :], res[0:n, :], AF.Square, reads=[res], writes=[sq])
        vred(ssum[0:n, :], sq[0:n, :], reads=[sq], writes=[ssum])
        rsqrt_small(rstd[0:n, :], ssum[0:n, :], tmp1[0:n, :], 1.0 / D, NORM_EPS, reads=[ssum], writes=[tmp1, rstd])
        vstt(yo[0:n, :], res[0:n, :], rstd[0:n, 0:1], normf[0:n, :], ALU.mult, ALU.mult, reads=[res, rstd, normf], writes=[yo])
        S.dma("sync", out_dram_ap, yo[0:n, :], reads=[yo], writes=[out_T])

    oT_list_T = None

    with ExitStack() as es:
        browB = sb(es, "browB", [NS, 3584])
        S.dma("sync", browB[:], browB_d.partition_broadcast(NS), writes=[browB])
        x_s = sb(es, "x_s", [NS, D])
        S.dma("sync", x_s[:], xs, writes=[x_s])
        hTs = sb(es, "hTs", [128, 8, DB + NS], BF16)
        graw_s = sb(es, "graw_s", [NS, 512])
        u_s = sb(es, "u_s", [NS, 512])
        gp_s = sb(es, "gp_s", [NS, 512])
        bonus_s = sb(es, "bonus_s", [NS, 512])
        st8 = sb(es, "st8", [NS, 8])
        st8b = sb(es, "st8b", [NS, 8])
        st8c = sb(es, "st8c", [NS, 8])

        def v3(ap):
            return ap.rearrange("p (h k) -> p h k", k=64)

        def bc8(ap8):
            return ap8.unsqueeze(2).to_broadcast([NS, 8, 64])

        with ExitStack() as e1:
            browA = sb(e1, "browA", [NS, D_SHIFT])
            S.dma("sync", browA[:], browA_d.partition_broadcast(NS), writes=[browA])
            omka_b = sb(e1, "omka_b", [NS, 512])
            vts(omka_b[:], browB[:, BRB_KA:BRB_KA + 512], -1.0, 1.0, ALU.mult, ALU.add, reads=[browB], writes=[omka_b])
            sq_s = sb(e1, "sq_s", [NS, D])
            ss_s = sb(e1, "ss_s", [NS, 1])
            t1_s = sb(e1, "t1_s", [NS, 1])
            rstd_s = sb(e1, "rstd_s", [NS, 1])
            xn_s = sb(e1, "xn_s", [NS, D], BF16)
            act(sq_s[:], x_s[:], AF.Square, reads=[x_s], writes=[sq_s])
            vred(ss_s[:], sq_s[:], reads=[sq_s], writes=[ss_s])
            rsqrt_small(rstd_s[:], ss_s[:], t1_s[:], 1.0 / D, NORM_EPS, reads=[ss_s], writes=[t1_s, rstd_s])
            vts(xn_s[:], x_s[:], rstd_s[:, 0:1], None, ALU.mult, None, reads=[x_s, rstd_s], writes=[xn_s])
            memset(hTs[:, :, 0:DB], 0.0, writes=[hTs])
            for dc in range(8):
                tr(pT[:, dc * 128:dc * 128 + NS], xn_s[:, dc * 128:(dc + 1) * 128], identb[0:NS, 0:NS], reads=[xn_s, identb], writes=[pT])
            vcopy(hTs[:, :, DB:DB + NS], pT[:].rearrange("p (c t) -> p c t", t=128)[:, :, 0:NS], reads=[pT], writes=[hTs])

            p_s = sb(e1, "p_s", [NS, D_SHIFT])
            prev_s = sb(e1, "prev_s", [NS, D_SHIFT])
            col_chunks = [(0, 512), (512, 512), (1024, 512), (1536, 128)]
            kk_ = 0
            for (c0, n) in col_chunks:
                bank = pg[kk_ % 2]; kk_ += 1
                for dc in range(8):
                    mm(bank[0:NS, 0:n], hTs[:, dc, DB:DB + NS], winb[:, dc, c0:c0 + n], dc == 0, dc == 7, reads=[hTs, winb], writes=[bank])
                act(p_s[:, c0:c0 + n], bank[0:NS, 0:n], AF.Copy, reads=[bank], writes=[p_s])
                bank = pg[kk_ % 2]; kk_ += 1
                for dc in range(8):
                    mm(bank[0:NS, 0:n], hTs[:, dc, 0:NS], winb[:, dc, c0:c0 + n], dc == 0, dc == 7, reads=[hTs, winb], writes=[bank])
                vcopy(prev_s[:, c0:c0 + n], bank[0:NS, 0:n], reads=[bank], writes=[prev_s])
            for (c0, dst, fn) in [(1664, graw_s, AF.Silu), (2176, u_s, AF.Copy), (2688, gp_s, AF.Silu)]:
                bank = pg[kk_ % 2]; kk_ += 1
                for dc in range(8):
                    mm(bank[0:NS, :], hTs[:, dc, DB:DB + NS], winb[:, dc, c0:c0 + 512], dc == 0, dc == 7, reads=[hTs, winb], writes=[bank])
                act(dst[:], bank[0:NS, :], fn, reads=[bank], writes=[dst])
            S.dma("sync", prev_s[0:DB, :], sshift, writes=[prev_s])
            S.dma("sync", nss[:], p_s[NS - DB:NS, :], reads=[p_s], writes=[nss])
            S.dma("sync", nps[:, 0:11, :], spool.rearrange("(b j) c -> b j c", j=15)[:, 4:15, :], writes=[nps])
            for t in range(DT):
                S.dma("sync", nps[:, 11 + t, :], u_s[t * DB:(t + 1) * DB, :], reads=[u_s], writes=[nps])

            vtt(prev_s[:], prev_s[:], p_s[:], ALU.subtract, reads=[prev_s, p_s], writes=[prev_s])
            vtt(prev_s[:], prev_s[:], browA[:], ALU.mult, reads=[prev_s, browA], writes=[prev_s])
            vtt(prev_s[:], prev_s[:], p_s[:], ALU.add, reads=[prev_s, p_s], writes=[prev_s])
            ps_s = prev_s
            r_s = ps_s[:, 0:512]
            k_s = ps_s[:, 512:1024]
            v_s = ps_s[:, 1024:1536]

            lT = sb(e1, "lT", [128, NS])
            tr(pM[:, 0:NS], ps_s[:, 1536:1664], ident[0:NS, 0:NS], reads=[ps_s, cst], writes=[pM])
            act(lT[0:64, :], pM[0:64, 0:NS], AF.Tanh, reads=[pM], writes=[lT])
            act(lT[64:128, :], pM[64:128, 0:NS], AF.Copy, reads=[pM], writes=[lT])
            sg_s = sb(e1, "sg_s", [NS, 512])
            a_s = sb(e1, "a_s", [NS, 512])
            mm(pA[0:NS, :], lT[0:64, :], wd[:, :], True, True, reads=[lT, wd], writes=[pA])
            vtt(sg_s[:], pA[0:NS, :], browB[:, BRB_W0:BRB_W0 + 512], ALU.add, reads=[pA, browB], writes=[sg_s])
            act(sg_s[:], sg_s[:], AF.Sigmoid, reads=[sg_s], writes=[sg_s])
            mm(pB[0:NS, :], lT[64:128, :], wa[64:128, :], True, True, reads=[lT, wa], writes=[pB])
            vtt(a_s[:], pB[0:NS, :], browB[:, BRB_A0:BRB_A0 + 512], ALU.add, reads=[pB, browB], writes=[a_s])
            act(a_s[:], a_s[:], AF.Sigmoid, reads=[a_s], writes=[a_s])

            pk = sb(e1, "pk", [NS, 4, 512])
            PQ = {1: 0, 2: 1, 4: 2, 5: 3}
            tmpA = sb(e1, "tmpA", [NS, 512])
            tmpB = sb(e1, "tmpB", [NS, 512])
            act(pk[:, PQ[1], :], sg_s[:], AF.Exp, reads=[sg_s], writes=[pk], scale=-C0)
            vtt(tmpA[:], k_s, browB[:, BRB_KK:BRB_KK + 512], ALU.mult, reads=[ps_s, browB], writes=[tmpA])
            vtt(tmpB[:], tmpA[:], tmpA[:], ALU.mult, reads=[tmpA], writes=[tmpB])
            vred(st8[:], v3(tmpB[:]), reads=[tmpB], writes=[st8])
            rsqrt_small(st8b[:], st8[:], st8c[:], 1.0, L2_EPS, reads=[st8], writes=[st8c, st8b])
            vtt(v3(tmpA[:]), v3(tmpA[:]), bc8(st8b[:]), ALU.mult, reads=[tmpA, st8b], writes=[tmpA])
            vts(pk[:, PQ[4], :], tmpA[:], -1.0, None, ALU.mult, None, reads=[tmpA], writes=[pk])
            vtt(pk[:, PQ[5], :], tmpA[:], a_s[:], ALU.mult, reads=[tmpA, a_s], writes=[pk])
            vtt(tmpB[:], a_s[:], browB[:, BRB_KA:BRB_KA + 512], ALU.mult, reads=[a_s, browB], writes=[tmpB])
            vtt(tmpB[:], tmpB[:], omka_b[:], ALU.add, reads=[tmpB, omka_b], writes=[tmpB])
            vtt(pk[:, PQ[2], :], k_s, tmpB[:], ALU.mult, reads=[ps_s, tmpB], writes=[pk])
            vtt(tmpB[:], r_s, browB[:, BRB_RK:BRB_RK + 512], ALU.mult, reads=[ps_s, browB], writes=[tmpB])
            vtt(tmpB[:], tmpB[:], pk[:, PQ[2], :], ALU.mult, reads=[tmpB, pk], writes=[tmpB])
            vred(st8[:], v3(tmpB[:]), reads=[tmpB], writes=[st8])
            vtt(v3(bonus_s[:]), v3(v_s), bc8(st8[:]), ALU.mult, reads=[ps_s, st8], writes=[bonus_s])
            sview = scr1[:].rearrange("q t b h k -> q (t b) (h k)")
            S.dma("sync", sview[0], r_s, reads=[ps_s], writes=[scr1])
            S.dma("sync", sview[3], v_s, reads=[ps_s], writes=[scr1])
            for qq, slot in PQ.items():
                S.dma("sync", sview[qq], pk[:, slot, :], reads=[pk], writes=[scr1])
            S.finish([scr1], engname="sync")
            S.barrier()
            chk("S1")

        with ExitStack() as e2:
            sIn = sb(e2, "sIn", [128, 6, DT, 64])
            S.dma("sync", sIn[:], scr1[:].rearrange("q t b h k -> (b h) q t k"), reads=[scr1], writes=[sIn])
            St = sb(e2, "St", [128, 64, 64])
            S.dma("sync", St[:].rearrange("p v k -> p (v k)"), swkv, writes=[St])
            tmpS = sb(e2, "tmpS", [128, 64, 64])
            sa = sb(e2, "sa", [128, 64])
            yS = sb(e2, "yS", [128, DT, 64])
            stgo = [sb(e2, f"stgo{i}", [128, D]) for i in range(3)]
            for fc in range(8):
                so = stgo[fc % 3]
                S.dma("sync", so[:], w_out[fc * 128:(fc + 1) * 128, :], writes=[so])
                act(woutb[:, fc, :], so[:], AF.Copy, reads=[so], writes=[woutb])

            def bv(ap):
                return ap.unsqueeze(1).to_broadcast([128, 64, 64])

            def bk(ap):
                return ap.unsqueeze(2).to_broadcast([128, 64, 64])

            for t in range(DT):
                q = lambda i: sIn[:, i, t, :]
                vtt(tmpS[:], St[:], bv(q(4)), ALU.mult, reads=[St, sIn], writes=[tmpS])
                vred(sa[:], tmpS[:], reads=[tmpS], writes=[sa])
                vtt(St[:], St[:], bv(q(1)), ALU.mult, reads=[St, sIn], writes=[St])
                vtt(tmpS[:], bk(sa[:]), bv(q(5)), ALU.mult, reads=[sa, sIn], writes=[tmpS])
                vtt(St[:], St[:], tmpS[:], ALU.add, reads=[St, tmpS], writes=[St])
                vtt(tmpS[:], bk(q(3)), bv(q(2)), ALU.mult, reads=[sIn], writes=[tmpS])
                vtt(St[:], St[:], tmpS[:], ALU.add, reads=[St, tmpS], writes=[St])
                vtt(tmpS[:], St[:], bv(q(0)), ALU.mult, reads=[St, sIn], writes=[tmpS])
                vred(yS[:, t, :], tmpS[:], reads=[tmpS], writes=[yS])
            S.dma("sync", nws[:], St[:].rearrange("p v k -> p (v k)"), reads=[St], writes=[nws])
            S.dma("sync", scr2[:].rearrange("b h t v -> (b h) t v"), yS[:], reads=[yS], writes=[scr2])
            S.finish([scr2, nws], engname="sync")
            S.barrier()
            chk("S2")

        with ExitStack() as e3:
            yT = sb(e3, "yT", [NS, 512])
            tmpA = sb(e3, "tmpA3", [NS, 512])
            for t in range(DT):
                S.dma("sync", yT[t * DB:(t + 1) * DB, :].rearrange("b (h v) -> b h v", v=64), scr2[:][:, :, t, :], reads=[scr2], writes=[yT])
            vred(st8[:], v3(yT[:]), reads=[yT], writes=[st8])
            vts(st8[:], st8[:], 1.0 / 64, None, ALU.mult, None, reads=[st8], writes=[st8])
            vtt(v3(yT[:]), v3(yT[:]), bc8(st8[:]), ALU.subtract, reads=[yT, st8], writes=[yT])
            vtt(tmpA[:], yT[:], yT[:], ALU.mult, reads=[yT], writes=[tmpA])
            vred(st8[:], v3(tmpA[:]), reads=[tmpA], writes=[st8])
            rsqrt_small(st8b[:], st8[:], st8c[:], 1.0 / 64, GN_EPS, reads=[st8], writes=[st8c, st8b])
            vtt(v3(yT[:]), v3(yT[:]), bc8(st8b[:]), ALU.mult, reads=[yT, st8b], writes=[yT])
            vtt(yT[:], yT[:], browB[:, BRB_GW:BRB_GW + 512], ALU.mult, reads=[yT, browB], writes=[yT])
            vtt(yT[:], yT[:], browB[:, BRB_GB:BRB_GB + 512], ALU.add, reads=[yT, browB], writes=[yT])
            vtt(yT[:], yT[:], bonus_s[:], ALU.add, reads=[yT, bonus_s], writes=[yT])
            vtt(yT[:], yT[:], graw_s[:], ALU.mult, reads=[yT, graw_s], writes=[yT])
            oTs = sb(e3, "oTs", [128, 8, NS], BF16)
            for fb in range(4):
                tr(pA[:, fb * 64:fb * 64 + NS], yT[:, fb * 128:(fb + 1) * 128], ident[0:NS, 0:NS], reads=[yT, cst], writes=[pA])
            vcopy(oTs[:, 0:4, :], pA[:, 0:4 * NS].rearrange("p (f t) -> p f t", t=NS), reads=[pA], writes=[oTs])

            uext = sb(e3, "uext_s", [128, 4, DB, 19])
            sp0 = sb(e3, "sp0", [120, 512])
            sp1 = sb(e3, "sp1", [120, 512])
            S.dma("sync", sp0[:], spool[0:120, :], writes=[sp0])
            S.dma("sync", sp1[:], spool[120:240, :], writes=[sp1])
            for g in range(4):
                tr(pB[:, 0:120], sp0[:, g * 128:(g + 1) * 128], ident[0:120, 0:120], reads=[sp0, cst], writes=[pB])
                tr(pB[:, 128:248], sp1[:, g * 128:(g + 1) * 128], ident[0:120, 0:120], reads=[sp1, cst], writes=[pB])
                vcopy(uext[:, g, 0:8, 0:15], pB[:, 0:120].rearrange("p (b j) -> p b j", j=15), reads=[pB], writes=[uext])
                vcopy(uext[:, g, 8:16, 0:15], pB[:, 128:248].rearrange("p (b j) -> p b j", j=15), reads=[pB], writes=[uext])
                tr(pM[:, 0:NS], u_s[:, g * 128:(g + 1) * 128], ident[0:NS, 0:NS], reads=[u_s, cst], writes=[pM])
                vcopy(uext[:, g, :, 15:19], pM[:, 0:NS].rearrange("p (t b) -> p b t", b=DB), reads=[pM], writes=[uext])
            s2 = sb(e3, "s2_s", [128, 4, DB, 19])
            s4 = sb(e3, "s4_s", [128, 3, DB, 19])
            s8 = sb(e3, "s8_s", [128, 2, DB, 19])
            s16 = sb(e3, "s16_s", [128, 1, DB, 19])
            d_s = sb(e3, "d_s", [128, 4, DT, DB])
            vtt(s2[:, :, :, 1:19], uext[:, :, :, 1:19], uext[:, :, :, 0:18], ALU.add, reads=[uext], writes=[s2])
            vtt(s4[:, :, :, 3:19], s2[:, 1:4, :, 3:19], s2[:, 1:4, :, 1:17], ALU.add, reads=[s2], writes=[s4])
            vtt(s8[:, :, :, 7:19], s4[:, 1:3, :, 7:19], s4[:, 1:3, :, 3:15], ALU.add, reads=[s4], writes=[s8])
            vtt(s16[:, :, :, 15:19], s8[:, 1:2, :, 15:19], s8[:, 1:2, :, 7:11], ALU.add, reads=[s8], writes=[s16])
            tots = [(s2, 0), (s4, 1), (s8, 2), (s16, 3)]
            for g in range(4):
                tt, off = tots[g]
                vstt(d_s[:, g, :, :].rearrange("p t b -> p b t"), tt[:, g - off, :, 15:19], 1.0 / WINS[g], uext[:, g, :, 15:19],
                     ALU.mult, ALU.subtract, reads=[tt, uext], writes=[d_s])
            gpT = sb(e3, "gpT", [128, 4, NS])
            for g in range(4):
                tr(pM[:, 64 + g * 64:64 + g * 64 + NS], gp_s[:, g * 128:(g + 1) * 128], ident[0:NS, 0:NS], reads=[gp_s, cst], writes=[pM])
            vcopy(gpT[:], pM[:, 64:64 + 4 * NS].rearrange("p (g t) -> p g t", t=NS), reads=[pM], writes=[gpT])
            for g in range(4):
                mm(pA[:, g * 64:g * 64 + NS], pw[:, g, :], d_s[:, g, :, :].rearrange("p t b -> p (t b)"), True, True, reads=[pw, d_s], writes=[pA])
            for g in range(4):
                vstt(oTs[:, 4 + g, :], pA[:, g * 64:g * 64 + NS], pvec[:, PV_PS + g:PV_PS + g + 1], gpT[:, g, :], ALU.mult, ALU.mult,
                     reads=[pA, pvec, gpT], writes=[oTs])

            sq = sb(e3, "sq2_s", [NS, D]); ssum = sb(e3, "ssum_s", [NS, 1])
            tmp1 = sb(e3, "tmp1_s", [NS, 1]); rstd = sb(e3, "rstd2_s", [NS, 1]); yo = sb(e3, "yo_s", [NS, D])
            oT_list_T = oTs
            final_tile((x_s, sq, ssum, tmp1, rstd, yo), NS, x_s, [oTs[:, fc, :] for fc in range(8)], ys[:], ys)
            S.finish([ys, nss, nps], engname="sync")
            S.barrier()
            chk("S3")

    with ExitStack() as es:
        def sbl(name, shape, dt=F32, n=2):
            return [sb(es, f"{name}_{i}", shape, dt) for i in range(n)]

        xt = sbl("xt", [128, D])
        yo = sb(es, "yo", [128, D])
        ssx = sb(es, "ssx", [128, 1]); t1x = sb(es, "t1x", [128, 1]); rsx = sb(es, "rsx", [128, 1])
        xnb = sb(es, "xnb", [128, D], BF16)
        sqx = xnb
        hT = sb(es, "hT", [128, 8, TB], BF16)
        praw = sbl("praw", [128, 4, TB + 1])
        halo = sb(es, "halo", [128, 13])
        omu = sb(es, "omu2", [128, 13])
        psr = sb(es, "psr", [128, 4, TB]); psk = sb(es, "psk", [128, 4, TB]); psv = sb(es, "psv", [128, 4, TB])
        ps12 = sb(es, "ps12", [128, TB])
        psx = [T(g_[:, i, :], f"psx{gi_}_{i}", buf=g_.b) for gi_, g_ in enumerate([psr, psk, psv]) for i in range(4)] + [ps12]
        sg = sb(es, "sg", [128, 4, TB]); av = sb(es, "av", [128, 4, TB])
        gsil = sbl("gsil", [128, 4, TB], BF16)
        gpsil = sb(es, "gpsil", [128, 4, TB], BF16)
        uext = sb(es, "uext", [128, 4, 15 + TB])
        th = sb(es, "th", [64, TB])
        wbig = [sb(es, f"wbig{i}", [128, 4, TB]) for i in range(4)]
        w1, w2, w3, w4 = wbig
        srot = [T(wbig[i][:].rearrange("p f t -> p (f t)")[:, 0:15 + TB], f"srot{i}", buf=wbig[i].b) for i in range(4)]
        kkn = sb(es, "kkn", [128, 4, TB]); kmod = sb(es, "kmod", [128, 4, TB]); bv_ = sb(es, "bv_", [128, 4, TB])
        cum = sb(es, "cum", [128, 4, TB])
        dpl = T(kkn[:, 0, :], "dpl", buf=kkn.b)
        at = sbl("at", [64, 4, 2, TB], BF16)
        rt = sbl("rt", [64, 4, 2, TB], BF16)
        bt = sb(es, "bt", [64, 4, 2, TB], BF16)
        kt = sb(es, "kt", [64, 4, 2, TB], BF16)
        bh = sb(es, "bh", [128, 4, TB], BF16); kh = sb(es, "kh", [128, 4, TB], BF16); vb = sb(es, "vb", [128, 4, TB], BF16)
        bon = sbl("bon", [128, 4, TB])
        gC = sbl("gC", [64, 4, 2, NCH])
        VT = [[sb(es, f"VT{p}{c}", [64, 512], BF16) for c in range(NCH)] for p in range(2)]
        BKT = [[sb(es, f"BKT{p}{c}", [64, 1024], BF16) for c in range(NCH)] for p in range(2)]
        Aak = [[sb(es, f"Aak{p}{c}", [64, 512], BF16) for c in range(NCH)] for p in range(2)]
        Arb = [[sb(es, f"Arb{p}{c}", [64, 512], BF16) for c in range(NCH)] for p in range(2)]
        Ark = [[sb(es, f"Ark{p}{c}", [64, 512], BF16) for c in range(NCH)] for p in range(2)]
        Minv = [[sb(es, f"Minv{p}{c}", [64, 512], BF16) for c in range(NCH)] for p in range(2)]
        Nsb = [sb(es, f"Nsb{c}", [64, 512], BF16) for c in range(NCH)]
        NTsb = [sb(es, f"NTsb{c}", [64, 512], BF16) for c in range(NCH)]
        Xa0 = [sb(es, f"Xa0{c}", [64, 512], BF16) for c in range(NCH)]
        XTa0 = [sb(es, f"XTa0{c}", [64, 512], BF16) for c in range(NCH)]
        Qtmp = [sb(es, f"Qtmp{c}", [64, 512], BF16) for c in range(NCH)]
        ST = sb(es, "ST", [64, 8, 64]); STb = sb(es, "STb", [64, 8, 64], BF16)
        Wsb = sb(es, "Wsb", [64, 512], BF16); Usb = sb(es, "Usb", [64, 512], BF16)
        yc = sb(es, "yc", [64, 512]); ysq = sb(es, "ysq", [64, 512])
        STt = T(ysq[:].rearrange("p (h v) -> p h v", v=64), "STt", buf=ysq.b)
        m8 = sb(es, "m8", [64, 8]); v8 = sb(es, "v8", [64, 8]); r8 = sb(es, "r8", [64, 8]); t8 = sb(es, "t8", [64, 8])
        o1 = sb(es, "o1", [128, 4, 64])
        oT = sbl("oT", [128, 8, TB], BF16)
        ssum = sb(es, "ssum", [128, 1]); tmp1 = sb(es, "tmp1", [128, 1]); rstd = sb(es, "rstd", [128, 1])
        ppT = T(ysq[0:16, :], "ppT", buf=ysq.b); m13 = sb(es, "m13", [13, 128])
        SvT = T(yc[:].rearrange("p (h k) -> p h k", k=64), "SvT", buf=yc.b)

        memset(halo[:], 0.0, writes=[halo])
        memset(uext[:, :, 0:15], 0.0, writes=[uext])
        memset(ST[:], 0.0, writes=[ST])
        memset(STb[:], 0.0, writes=[STb])
        vts(omu[:], pvec[:, PV_MU:PV_MU + 13], -1.0, 1.0, ALU.mult, ALU.add, reads=[pvec], writes=[omu])

        def b8(ap):
            return ap.unsqueeze(1).to_broadcast([64, 8, 64])

        def h3(ap):
            return ap.rearrange("p (h v) -> p h v", v=64)

        def hc(h):
            return slice(h * 64, (h + 1) * 64)

        maskUs = b8(cst[0:64, C_MUS:C_MUS + 64])
        maskUi = b8(cst[0:64, C_MUI:C_MUI + 64])
        maskLs = b8(cst[0:64, C_MLS:C_MLS + 64])
        ident8 = b8(cst[0:64, C_ID:C_ID + 64])
        rstm = cst[:, C_RST:C_RST + 512]
        st = dict(gk=0, ak=0)
        pT32 = T(pT[:].bitcast(F32), "pT32", buf=pT.b)
        abanks = [pA, pB, pg[0], pg[1], pM, pT32]

        def nextbank():
            b = abanks[st["ak"] % len(abanks)]
            st["ak"] += 1
            return b

        def front(tb):
            pb = tb % 2
            t0 = tb * TB
            x_t = xt[pb]
            S.dma("sync", x_t[:], xp[t0:t0 + TB, :], writes=[x_t])
            act(sqx[:], x_t[:], AF.Square, reads=[x_t], writes=[sqx, ssx], accum=ssx[:])
            rsqrt_act(rsx[:], ssx[:], 1.0 / D, 0, 128, reads=[ssx], writes=[rsx])
            act(xnb[:], x_t[:], AF.Copy, reads=[x_t, rsx], writes=[xnb], scale=rsx[:, 0:1])
            yield
            for dc in range(8):
                tr(pT[:, dc * 128:(dc + 1) * 128], xnb[:, dc * 128:(dc + 1) * 128], identb[:], reads=[xnb, identb], writes=[pT])
            vcopy(hT[:].rearrange("p c t -> p (c t)"), pT[:], reads=[pT], writes=[hT])
            yield

            def gemm_group(ebs):
                bank = pg[st["gk"] % 2]
                st["gk"] += 1
                for i, eb in enumerate(ebs):
                    for dc in range(8):
                        mm(bank[:, i * TB:(i + 1) * TB], winb[:, dc, eb * 128:(eb + 1) * 128], hT[:, dc, :], dc == 0, dc == 7,
                           reads=[winb, hT], writes=[bank])
                return bank

            for gi, ebs in enumerate([[0, 1, 2, 3], [4, 5, 6, 7], [8, 9, 10, 11], [12]]):
                bank = gemm_group(ebs)
                yield
                n = len(ebs)
                pr = praw[gi % 2]
                e0 = ebs[0]
                gT = [psr, psk, psv, ps12][gi]
                dst = gT[:, 0:n, :] if gi < 3 else ps12[:].unsqueeze(1)
                mub = pvec[:, PV_MU + e0:PV_MU + e0 + n].unsqueeze(2).to_broadcast([128, n, TB])
                vcopy(pr[:, 0:n, 0:1], halo[:, e0:e0 + n].unsqueeze(2), reads=[halo], writes=[pr], eng="gpsimd")
                act(pr[:, 0:n, 1:TB + 1], bank[:, 0:n * TB].rearrange("p (e t) -> p e t", t=TB), AF.Copy, reads=[bank], writes=[pr])
                vtt(dst, pr[:, 0:n, 0:TB], pr[:, 0:n, 1:TB + 1], ALU.subtract, reads=[pr], writes=[gT])
                vtt(dst, dst, mub, ALU.mult, reads=[gT, pvec], writes=[gT])
                vtt(dst, dst, pr[:, 0:n, 1:TB + 1], ALU.add, reads=[gT, pr], writes=[gT])
                vcopy(halo[:, e0:e0 + n].unsqueeze(2), pr[:, 0:n, TB:TB + 1], reads=[pr], writes=[halo], eng="gpsimd")
                yield
            bank = gemm_group([13, 14, 15, 16])
            act(gsil[pb][:].rearrange("p f t -> p (f t)"), bank[:, :], AF.Silu, reads=[bank], writes=[gsil[pb]])
            yield
            bank = gemm_group([17, 18, 19, 20])
            act(uext[:, :, 15:15 + TB], bank[:, :].rearrange("p (g t) -> p g t", t=TB), AF.Copy, reads=[bank], writes=[uext])
            yield
            bank = gemm_group([21, 22, 23, 24])
            act(gpsil[:].rearrange("p g t -> p (g t)"), bank[:, :], AF.Silu, reads=[bank], writes=[gpsil])
            yield

            act(th[:], psx[12][0:64, :], AF.Tanh, reads=[psx[12]], writes=[th])
            for fb in range(4):
                mm(pA[:, fb * TB:(fb + 1) * TB], wd[:, fb * 128:(fb + 1) * 128], th[:], True, True, reads=[wd, th], writes=[pA])
            for fb in range(4):
                mm(pM[:, fb * TB:(fb + 1) * TB], wa[64:128, fb * 128:(fb + 1) * 128], psx[12][64:128, :], True, True, reads=[wa, psx[12]], writes=[pM])
            for fb in range(4):
                act(sg[:, fb, :], pA[:, fb * TB:(fb + 1) * TB], AF.Sigmoid, reads=[pA, pvec], writes=[sg], bias=pvec[:, PV_W0 + fb:PV_W0 + fb + 1])
                act(av[:, fb, :], pM[:, fb * TB:(fb + 1) * TB], AF.Sigmoid, reads=[pM, pvec], writes=[av], bias=pvec[:, PV_A0 + fb:PV_A0 + fb + 1])
            yield

            def pb4(col):
                return pvec[:, col:col + 4].unsqueeze(2).to_broadcast([128, 4, TB])

            def f2(t_):
                return t_[:].rearrange("p f t -> p (f t)")

            vcopy(vb[:], psv[:], reads=[psv], writes=[vb], eng="gpsimd")
            vtt(w1[:], psk[:], pb4(PV_KK), ALU.mult, reads=[psk, pvec], writes=[w1])
            vtt(w2[:], w1[:], w1[:], ALU.mult, reads=[w1], writes=[w2])
            mm(pM[:, :], onesblk, f2(w2), True, True, reads=[cst, w2], writes=[pM])
            act(f2(w2), pM[:, :], AF.Ln, reads=[pM, epsT], writes=[w2], bias=epsT[:, 2:3])
            act(w2[:], w2[:], AF.Exp, reads=[w2], writes=[w2], scale=-0.5)
            vstt(kkn[:], w1[:], -1.0, w2[:], ALU.mult, ALU.mult, reads=[w1, w2], writes=[kkn])
            yield
            vtt(w1[:], av[:], pb4(PV_KA), ALU.mult, reads=[av, pvec], writes=[w1])
            vtt(w1[:], w1[:], omka[:, 0:4].unsqueeze(2).to_broadcast([128, 4, TB]), ALU.add, reads=[w1, omka], writes=[w1])
            vtt(kmod[:], psk[:], w1[:], ALU.mult, reads=[psk, w1], writes=[kmod])
            vstt(bv_[:], kkn[:], -1.0, av[:], ALU.mult, ALU.mult, reads=[kkn, av], writes=[bv_])
            vtt(w1[:], psr[:], pb4(PV_RK), ALU.mult, reads=[psr, pvec], writes=[w1])
            vtt(w1[:], w1[:], kmod[:], ALU.mult, reads=[w1, kmod], writes=[w1])
            mm(pA[:, :], onesblk, f2(w1), True, True, reads=[cst, w1], writes=[pA])
            vtt(f2(bon[pb]), pA[:, :], f2(psv), ALU.mult, reads=[pA, psv], writes=[bon[pb]])
            yield
            S.op("vector", lambda e: e.tensor_tensor_scan(out=f2(cum), data0=rstm, data1=f2(sg), initial=0.0, op0=ALU.mult, op1=ALU.add),
                 reads=[cst, sg], writes=[cum], cost=1.2)
            c3 = cum[:].rearrange("p f (c t) -> p (f c) t", t=CH)
            vtt(w1[:], cum[:], sg[:], ALU.subtract, reads=[cum, sg], writes=[w1])
            act(w2[:], cum[:], AF.Exp, reads=[cum], writes=[w2], scale=-C0)
            act(w3[:], cum[:], AF.Exp, reads=[cum], writes=[w3], scale=C0)
            act(w1[:], w1[:], AF.Exp, reads=[w1], writes=[w1], scale=-C0)
            vtt(w4[:].rearrange("p f (c t) -> p (f c) t", t=CH), c3[:, :, CH - 1:CH].to_broadcast([128, 4 * NCH, CH]), c3, ALU.subtract,
                reads=[cum], writes=[w4], eng="gpsimd")
            act(w4[:], w4[:], AF.Exp, reads=[w4], writes=[w4], scale=-C0)
            yield
            for j in range(2):
                pp = slice(64 * j, 64 * j + 64)
                e_ = "vector" if j == 0 else "gpsimd"
                vtt(rt[pb][:, :, j, :], psr[pp, :, :], w2[pp, :, :], ALU.mult, reads=[psr, w2], writes=[rt[pb]], eng=e_)
                vtt(bt[:, :, j, :], bv_[pp, :, :], w3[pp, :, :], ALU.mult, reads=[bv_, w3], writes=[bt], eng=e_)
                vtt(kt[:, :, j, :], kmod[pp, :, :], w3[pp, :, :], ALU.mult, reads=[kmod, w3], writes=[kt], eng=e_)
                vtt(at[pb][:, :, j, :], kkn[pp, :, :], w1[pp, :, :], ALU.mult, reads=[kkn, w1], writes=[at[pb]], eng=e_)
                act(gC[pb][:, :, j, :], cum[pp, :, :].rearrange("p f (c t) -> p f c t", t=CH)[:, :, :, CH - 1], AF.Exp,
                    reads=[cum], writes=[gC[pb]], scale=-C0)
            vtt(bh[:], bv_[:], w4[:], ALU.mult, reads=[bv_, w4], writes=[bh])
            vtt(kh[:], kmod[:], w4[:], ALU.mult, reads=[kmod, w4], writes=[kh], eng="gpsimd")
            yield

            L = 15 + TB
            for g in range(4):
                vtt(srot[0][:, 1:], uext[:, g, 1:], uext[:, g, 0:L - 1], ALU.add, reads=[uext], writes=[srot[0]], eng="gpsimd")
                tot = srot[0]
                if g >= 1:
                    vtt(srot[1][:, 3:], srot[0][:, 3:], srot[0][:, 1:L - 2], ALU.add, reads=[srot[0]], writes=[srot[1]], eng="gpsimd")
                    tot = srot[1]
                if g >= 2:
                    vtt(srot[2][:, 7:], srot[1][:, 7:], srot[1][:, 3:L - 4], ALU.add, reads=[srot[1]], writes=[srot[2]], eng="gpsimd")
                    tot = srot[2]
                if g >= 3:
                    vtt(srot[3][:, 15:], srot[2][:, 15:], srot[2][:, 7:L - 8], ALU.add, reads=[srot[2]], writes=[srot[3]], eng="gpsimd")
                    tot = srot[3]
                vstt(dpl[:], tot[:, 15:], 1.0 / WINS[g], uext[:, g, 15:], ALU.mult, ALU.subtract, reads=[tot, uext], writes=[dpl])
                if tb == 0:
                    vtt(dpl[:, 0:16], tot[:, 15:31], cst[:, C_ICNT + g * 16:C_ICNT + (g + 1) * 16], ALU.mult, reads=[tot, cst], writes=[dpl])
                    vtt(dpl[:, 0:16], dpl[:, 0:16], uext[:, g, 15:31], ALU.subtract, reads=[dpl, uext], writes=[dpl])
                mm(pM[:, 0:TB], pw[:, g, :], dpl[:], True, True, reads=[pw, dpl], writes=[pM])
                vstt(oT[pb][:, 4 + g, :], pM[:, 0:TB], pvec[:, PV_PS + g:PV_PS + g + 1], gpsil[:, g, :], ALU.mult, ALU.mult,
                     reads=[pM, pvec, gpsil], writes=[oT[pb]])
                yield
            if tb == NTB - 1:
                for g in range(4):
                    tr(pA[0:16, g * 128:(g + 1) * 128], uext[:, g, TB - 1:TB + 15], ident, reads=[uext, cst], writes=[pA])
                vcopy(ppT[:], pA[0:16, :], reads=[pA], writes=[ppT])
                S.dma("sync", npp[:], ppT[1:16, :], reads=[ppT], writes=[npp])
                tr(pB[0:13, 0:128], halo[:, 0:13], ident, reads=[halo, cst], writes=[pB])
                vcopy(m13[:], pB[0:13, 0:128], reads=[pB], writes=[m13])
                S.dma("sync", nsp[:], m13[:], reads=[m13], writes=[nsp])
            vcopy(uext[:, :, 0:15], uext[:, :, TB:TB + 15], reads=[uext], writes=[uext], eng="gpsimd")
            yield

            css = [slice(c * CH, (c + 1) * CH) for c in range(NCH)]
            for c in range(NCH):
                for qi, srcl in enumerate([bh, kh]):
                    for fb in range(4):
                        tr(pT[0:64, qi * 512 + fb * 128:qi * 512 + (fb + 1) * 128], srcl[:, fb, css[c]], identb[:], reads=[srcl, identb], writes=[pT])
                vcopy(BKT[pb][c][:], pT[0:64, :], reads=[pT], writes=[BKT[pb][c]])
                for fb in range(4):
                    tr(pT[0:64, fb * 128:(fb + 1) * 128], vb[:, fb, css[c]], identb[:], reads=[vb, identb], writes=[pT])
                act(VT[pb][c][:], pT[0:64, 0:512], AF.Copy, reads=[pT], writes=[VT[pb][c]])
                yield

            def hsl(tl, h, c):
                fb, j = divmod(h, 2)
                return tl[:, fb, j, css[c]]

            for (Lt, Rt, mask, dsts) in [(bt, at[pb], maskUs, Nsb), (at[pb], bt, maskLs, NTsb), (kt, at[pb], maskUs, Aak[pb]),
                                         (bt, rt[pb], maskUi, Arb[pb]), (kt, rt[pb], maskUi, Ark[pb])]:
                banks = []
                for c in range(NCH):
                    bank = nextbank()
                    banks.append(bank)
                    for h in range(8):
                        mm(bank[0:64, hc(h)], hsl(Lt, h, c), hsl(Rt, h, c), True, True, reads=[Lt, Rt], writes=[bank])
                for c in range(NCH):
                    vtt(h3(dsts[c][:]), h3(banks[c][0:64, :]), mask, ALU.mult, reads=[banks[c], cst], writes=[dsts[c]])
                yield
            X = list(Nsb); XT = list(NTsb)
            Q = [Qtmp[c] for c in range(NCH)]
            for c in range(NCH):
                vtt(h3(Q[c][:]), h3(Nsb[c][:]), ident8, ALU.add, reads=[Nsb[c], cst], writes=[Q[c]])
            for lvl in range(5):
                Xn = [(Xa0[c] if lvl % 2 == 0 else Nsb[c]) for c in range(NCH)]
                XTn = [(XTa0[c] if lvl % 2 == 0 else NTsb[c]) for c in range(NCH)]
                Qn = [(Minv[pb][c] if lvl % 2 == 0 else Qtmp[c]) for c in range(NCH)]
                banks = []
                for c in range(NCH):
                    bank = nextbank(); banks.append(bank)
                    for h in range(8):
                        mm(bank[0:64, hc(h)], X[c][:, hc(h)], XT[c][:, hc(h)], True, True, reads=[X[c], XT[c]], writes=[bank])
                for c in range(NCH):
                    act(XTn[c][:], banks[c][0:64, :], AF.Copy, reads=[banks[c]], writes=[XTn[c]])
                yield
                if lvl < 4:
                    banks = []
                    for c in range(NCH):
                        bank = nextbank(); banks.append(bank)
                        for h in range(8):
                            mm(bank[0:64, hc(h)], XT[c][:, hc(h)], X[c][:, hc(h)], True, True, reads=[X[c], XT[c]], writes=[bank])
                    for c in range(NCH):
                        act(Xn[c][:], banks[c][0:64, :], AF.Copy, reads=[banks[c]], writes=[Xn[c]])
                    yield
                banks = []
                for c in range(NCH):
                    bank = nextbank(); banks.append(bank)
                    for h in range(8):
                        mm(bank[0:64, hc(h)], XTn[c][:, hc(h)], Q[c][:, hc(h)], True, True, reads=[XTn[c], Q[c]], writes=[bank])
                for c in range(NCH):
                    vtt(Qn[c][:], banks[c][0:64, :], Q[c][:], ALU.add, reads=[banks[c], Q[c]], writes=[Qn[c]])
                X, XT, Q = Xn, XTn, Qn
                yield

        def chain(tb):
            pb = tb % 2
            t0 = tb * TB
            for c in range(NCH):
                cs = slice(c * CH, (c + 1) * CH)
                aT, rT = at[pb], rt[pb]
                VTc, BKTc, Aakc, Arbc, Arkc, Minvc = VT[pb][c], BKT[pb][c], Aak[pb][c], Arb[pb][c], Ark[pb][c], Minv[pb][c]
                for h in range(8):
                    fb, j = divmod(h, 2)
                    mm(pC[0:64, hc(h)], aT[:, fb, j, cs], STb[:, h, :], True, False, reads=[aT, STb], writes=[pC])
                    mm(pC[0:64, hc(h)], Aakc[:, hc(h)], VTc[:, hc(h)], False, True, reads=[Aakc, VTc], writes=[pC])
                act(Wsb[:], pC[0:64, :], AF.Copy, reads=[pC], writes=[Wsb])
                yield
                for h in range(8):
                    mm(pC[0:64, hc(h)], Minvc[:, hc(h)], Wsb[:, hc(h)], True, True, reads=[Minvc, Wsb], writes=[pC])
                act(Usb[:], pC[0:64, :], AF.Copy, reads=[pC], writes=[Usb])
                yield
                for h in range(8):
                    mm(pC[0:64, hc(h)], BKTc[:, hc(h)], Usb[:, hc(h)], True, False, reads=[BKTc, Usb], writes=[pC])
                    mm(pC[0:64, hc(h)], BKTc[:, 512 + h * 64:512 + (h + 1) * 64], VTc[:, hc(h)], False, True, reads=[BKTc, VTc], writes=[pC])
                for h in range(8):
                    fb, j = divmod(h, 2)
                    mm(pD[0:64, hc(h)], rT[:, fb, j, cs], STb[:, h, :], True, False, reads=[rT, STb], writes=[pD])
                    mm(pD[0:64, hc(h)], Arbc[:, hc(h)], Usb[:, hc(h)], False, False, reads=[Arbc, Usb], writes=[pD])
                    mm(pD[0:64, hc(h)], Arkc[:, hc(h)], VTc[:, hc(h)], False, True, reads=[Arkc, VTc], writes=[pD])
                vtt(STt[:], ST[:], gC[pb][:].rearrange("p f j c -> p (f j) c")[:, :, c:c + 1].to_broadcast([64, 8, 64]), ALU.mult,
                    reads=[ST, gC[pb]], writes=[STt])
                vtt(ST[:], STt[:], h3(pC[0:64, :]), ALU.add, reads=[STt, pC], writes=[ST])
                act(STb[:], ST[:], AF.Copy, reads=[ST], writes=[STb])
                yield
                y3 = h3(pD[0:64, :])
                vred(m8[:], y3, reads=[pD], writes=[m8])
                vts(m8[:], m8[:], 1.0 / 64, None, ALU.mult, None, reads=[m8], writes=[m8])
                vtt(h3(yc[:]), y3, m8[:].unsqueeze(2).to_broadcast([64, 8, 64]), ALU.subtract, reads=[pD, m8], writes=[yc])
                act(ysq[:], yc[:], AF.Square, reads=[yc], writes=[ysq])
                vred(v8[:], h3(ysq[:]), reads=[ysq], writes=[v8])
                rsqrt_act(r8[:], v8[:], 1.0 / 64, 1, 64, reads=[v8], writes=[r8])
                vtt(h3(yc[:]), h3(yc[:]), r8[:].unsqueeze(2).to_broadcast([64, 8, 64]), ALU.mult, reads=[yc, r8], writes=[yc], eng="gpsimd")
                yield
                for fb in range(4):
                    tr(pD[:, fb * 64:(fb + 1) * 64], yc[:, fb * 128:(fb + 1) * 128], ident[0:64, 0:64], reads=[yc, cst], writes=[pD])
                for fb in range(4):
                    vts(o1[:, fb, :], pD[:, fb * 64:(fb + 1) * 64], pvec[:, PV_GW + fb:PV_GW + fb + 1], pvec[:, PV_GB + fb:PV_GB + fb + 1],
                        ALU.mult, ALU.add, reads=[pD, pvec], writes=[o1])
                vtt(o1[:], o1[:], bon[pb][:, :, cs], ALU.add, reads=[o1, bon[pb]], writes=[o1], eng="gpsimd")
                vtt(oT[pb][:, 0:4, cs], o1[:], gsil[pb][:, :, cs], ALU.mult, reads=[o1, gsil[pb]], writes=[oT[pb]])
                yield
            x_t = xt[pb]
            for half in range(2):
                bank = pD if half == 0 else pC
                for fc in range(8):
                    mm(bank[:, :], oT[pb][:, fc, :], woutb[:, fc, half * 512:(half + 1) * 512], fc == 0, fc == 7, reads=[oT[pb], woutb], writes=[bank])
                vtt(x_t[:, half * 512:(half + 1) * 512], bank[:, :], x_t[:, half * 512:(half + 1) * 512], ALU.add, reads=[bank, x_t], writes=[x_t])
                yield
            act(yo[:], x_t[:], AF.Square, reads=[x_t], writes=[yo, ssum], accum=ssum[:])
            rsqrt_act(rstd[:], ssum[:], 1.0 / D, 0, 128, reads=[ssum], writes=[rstd])
            vstt(yo[:], x_t[:], rstd[:, 0:1], normf[:], ALU.mult, ALU.mult, reads=[x_t, rstd, normf], writes=[yo])
            S.dma("sync", yp[t0:t0 + TB, :], yo[:], reads=[yo], writes=[yp])
            yield

        def run_all(g):
            n = 0
            for _ in g:
                n += 1
            return n

        def interleave(ga, na, gb, nb):
            ia = ib = 0
            da = db = False
            while not (da and db):
                pick_a = (not da) and (db or (ia * nb <= ib * na))
                if pick_a:
                    try:
                        next(ga); ia += 1
                    except StopIteration:
                        da = True
                else:
                    try:
                        next(gb); ib += 1
                    except StopIteration:
                        db = True
            return ia, ib

        run_all(front(0))

        def record_units(g):
            units = []
            S.rec = []
            for _ in g:
                if S.rec:
                    units.append(S.rec)
                S.rec = []
            if S.rec:
                units.append(S.rec)
            S.rec = None
            return units

        A, B = [], []
        for tb in range(NTB):
            A.append(record_units(chain(tb)))
            if tb + 1 < NTB:
                B.append(record_units(front(tb + 1)))
        S.merge_emit(A, B, a_ok=lambda ia, ib: ib >= ia, b_ok=lambda ib, ia: ia >= ib)
        for h in range(8):
            tr(pA[0:64, h * 64:(h + 1) * 64], ST[:, h, :], ident[0:64, 0:64], reads=[ST, cst], writes=[pA])
        vcopy(SvT[:].rearrange("p h k -> p (h k)"), pA[0:64, :], reads=[pA], writes=[SvT])
        S.dma("sync", nwp[:].rearrange("h v k -> v h k"), SvT[:], reads=[SvT], writes=[nwp])
        S.finish([yp, ys, nsp, nwp, npp, nss, nws, nps], engname="sync")
        S.barrier()
    es_top.close()
    return nc, S


_CACHE = {}


def _consts():
    cst = np.zeros((128, C_END), np.float32)
    cst[:, C_ID:C_ID + 128] = np.eye(128, dtype=np.float32)
    ob = np.zeros((128, 128), np.float32)
    ob[0:64, 0:64] = 1.0
    ob[64:128, 64:128] = 1.0
    cst[:, C_ONES:C_ONES + 128] = ob
    s = np.arange(64)[:, None]
    t = np.arange(64)[None, :]
    mus = (s < t).astype(np.float32)
    mui = (s <= t).astype(np.float32)
    mls = (s > t).astype(np.float32)
    i64 = np.eye(64, dtype=np.float32)
    cst[0:64, C_MUS:C_MUS + 64] = mus
    cst[0:64, C_MUI:C_MUI + 64] = mui
    cst[0:64, C_MLS:C_MLS + 64] = mls
    rst = np.ones((512,), np.float32)
    rst[::CH] = 0.0
    cst[:, C_RST:C_RST + 512] = rst[None, :]
    for g, w in enumerate(WINS):
        pos = np.arange(16)
        cst[:, C_ICNT + g * 16:C_ICNT + (g + 1) * 16] = (1.0 / np.minimum(pos + 1, w)).astype(np.float32)[None, :]
    return cst


def kernel(x_prompt, x_sample, state_shift, state_wkv, state_pool, norm_w, w_in, mu_shift,
           w_decay_b, w0, w_aaa_b, a0, k_k, k_a, r_k, gn_w, gn_b, pool_w, pool_scale, w_out, norm_f):
    f = lambda a: np.ascontiguousarray(np.asarray(a, dtype=np.float32))
    x_prompt, x_sample, state_shift, state_wkv, state_pool = map(f, (x_prompt, x_sample, state_shift, state_wkv, state_pool))
    if "nc" not in _CACHE:
        _CACHE["nc"] = build_program()
    nc, S = _CACHE["nc"]

    def colmajor(v, n):
        return f(v).reshape(n, 128).T

    pvec = np.concatenate([
        colmajor(norm_w[0], 8), colmajor(mu_shift[0], 13), colmajor(w0[0], 4), colmajor(a0[0], 4), colmajor(k_k[0], 4),
        colmajor(k_a[0], 4), colmajor(f(r_k[0]).reshape(-1), 4), colmajor(gn_w[0], 4), colmajor(gn_b[0], 4), colmajor(pool_scale[0], 4)], axis=1)
    pvec = f(pvec)
    browA = f(f(mu_shift[0])[None, :])
    browB = f(np.concatenate([f(w0[0]), f(a0[0]), f(k_k[0]), f(k_a[0]), f(r_k[0]).reshape(-1), f(gn_w[0]), f(gn_b[0])])[None, :])
    cst = _consts()
    shared = {
        "w_in": f(w_in[0]), "w_out": f(w_out[0]), "wdec": f(w_decay_b[0]), "waaa": f(w_aaa_b[0]), "poolw": f(pool_w[0]),
        "pvec": pvec, "browA": browA, "browB": browB, "normf": f(norm_f)[None, :], "cst": cst,
    }
    in_maps = []
    for c in range(NCORE):
        bs = slice(c * DB, (c + 1) * DB)
        m = dict(shared)
        m["xp"] = x_prompt[c]
        m["xs"] = f(x_sample[bs].transpose(1, 0, 2).reshape(NS, D))
        m["sshift"] = state_shift[0, bs]
        m["swkv"] = f(state_wkv[0, bs].reshape(128, 4096))
        m["spool"] = f(state_pool[0, bs].reshape(DB * 15, 512))
        in_maps.append(m)
    res = run_bass_kernel_spmd(nc, in_maps, core_ids=list(range(NCORE)))
    R = res.results
    y_prompt = np.stack([R[c]["yp"] for c in range(NCORE)], axis=0)
    y_sample = np.concatenate([R[c]["ys"].reshape(DT, DB, D).transpose(1, 0, 2) for c in range(NCORE)], axis=0)
    nsp = np.stack([R[c]["nsp"].reshape(D_SHIFT) for c in range(NCORE)], axis=0)[None]
    nwp = np.stack([R[c]["nwp"] for c in range(NCORE)], axis=0)[None]
    npp = np.stack([R[c]["npp"] for c in range(NCORE)], axis=0)[None]
    nss = np.concatenate([R[c]["nss"] for c in range(NCORE)], axis=0)[None]
    nws = np.concatenate([R[c]["nws"].reshape(DB, 8, 64, 64) for c in range(NCORE)], axis=0)[None]
    nps = np.concatenate([R[c]["nps"] for c in range(NCORE)], axis=0)[None]
    out = (y_prompt, y_sample, nsp, nwp, npp, nss, nws, nps)
    return tuple(np.ascontiguousarray(o.astype(np.float32)) for o in out)
```

```python
import numpy as np
from contextlib import ExitStack
import concourse.bass as bass
import concourse.mybir as mybir
from concourse.bass_utils import run_bass_kernel_spmd

F32 = mybir.dt.float32
BF16 = mybir.dt.bfloat16
AF = mybir.ActivationFunctionType
ALU = mybir.AluOpType
AX = mybir.AxisListType

D = 1024
SEQ = 2048
NCORE = 8
DB = 16
DT = 4
NS = DB * DT
D_SHIFT = 1664
D_IN = 3200
C0 = float(np.exp(-0.5))
NORM_EPS = 1e-6
GN_EPS = 64e-5
L2_EPS = 1e-12
TB = 128
NTB = SEQ // TB
CH = 64
FBIAS = 0.0
NCH = TB // CH
WINS = (2, 4, 8, 16)

C_ID, C_ONES, C_MUS, C_MUI, C_MLS, C_RST, C_ICNT, C_END = 0, 128, 256, 320, 384, 448, 960, 1024
PV_NW, PV_MU, PV_W0, PV_A0, PV_KK, PV_KA, PV_RK, PV_GW, PV_GB, PV_PS, PV_END = 0, 8, 21, 25, 29, 33, 37, 41, 45, 49, 53
BRB_W0, BRB_A0, BRB_KK, BRB_KA, BRB_RK, BRB_GW, BRB_GB = 0, 512, 1024, 1536, 2048, 2560, 3072


class Buf:
    __slots__ = ("name", "w", "r")

    def __init__(self, name):
        self.name = name
        self.w = None
        self.r = []


class T:
    def __init__(self, t, name, buf=None):
        self.t = t
        self.b = buf if buf is not None else Buf(name)

    def __getitem__(self, k):
        return self.t[k]


class Sched:
    def __init__(self, nc, n_dma_sems=32):
        self.nc = nc
        self.eng = {}
        for name in ["tensor", "vector", "scalar", "gpsimd", "sync"]:
            h = getattr(nc, name)
            sem = nc.alloc_semaphore(name="prog_" + name)
            self.eng[name] = dict(h=h, sem=sem, cnt=0, waited={})
        self.dma_sems = [dict(sem=nc.alloc_semaphore(name=f"dma{i}"), cnt=0) for i in range(n_dma_sems)]
        self.dma_rr = 0
        self.ninstr = 0
        self.rec = None

    def _wait(self, engname, tok):
        sem, val, src = tok
        e = self.eng[engname]
        key = id(sem)
        if e["waited"].get(key, 0) >= val:
            return
        e["h"].wait_ge(sem, val)
        e["waited"][key] = val
        self.ninstr += 1

    def _deps(self, engname, reads, writes):
        toks = []
        for b in reads:
            if b.w is not None:
                toks.append(b.w)
        for b in writes:
            if b.w is not None:
                toks.append(b.w)
            toks.extend(b.r)
        for tok in toks:
            if tok[2] == engname and engname == "tensor":
                continue
            self._wait(engname, tok)

    @staticmethod
    def _bufs(xs):
        return [x.b if isinstance(x, T) else x for x in xs]

    def _record(self, tok, reads, writes):
        for b in reads:
            b.r.append(tok)
            if len(b.r) > 64:
                b.r = b.r[-64:] if False else b.r
        for b in writes:
            b.w = tok
            b.r = []

    def op(self, engname, fn, reads=(), writes=(), cost=0.3):
        reads = self._bufs(reads)
        writes = self._bufs(writes)
        if self.rec is not None:
            self.rec.append(("op", engname, fn, reads, writes, cost, None))
            return None
        e = self.eng[engname]
        self._deps(engname, reads, writes)
        ins = fn(e["h"])
        e["cnt"] += 1
        ins.then_inc(e["sem"], 1)
        e["waited"][id(e["sem"])] = max(e["waited"].get(id(e["sem"]), 0), 0)
        tok = (e["sem"], e["cnt"], engname)
        self._record(tok, reads, writes)
        self.ninstr += 1
        return tok

    def dma(self, qname, out, in_, reads=(), writes=(), **kw):
        reads = self._bufs(reads)
        writes = self._bufs(writes)
        if self.rec is not None:
            self.rec.append(("dma", qname, (out, in_), reads, writes, 2.5, kw))
            return None
        e = self.eng[qname]
        self._deps(qname, reads, writes)
        d = self.dma_sems[self.dma_rr]
        self.dma_rr = (self.dma_rr + 1) % len(self.dma_sems)
        if d["cnt"] > 0:
            self._wait(qname, (d["sem"], 16 * d["cnt"], "dma"))
        ins = e["h"].dma_start(out=out, in_=in_, **kw)
        d["cnt"] += 1
        ins.then_inc(d["sem"], 16)
        tok = (d["sem"], 16 * d["cnt"], "dma")
        self._record(tok, reads, writes)
        self.ninstr += 1
        return tok

    def emit(self, r):
        kind, eng, fn, reads, writes, cost, kw = r
        if kind == "op":
            self.op(eng, fn, reads=reads, writes=writes)
        else:
            self.dma(eng, fn[0], fn[1], reads=reads, writes=writes, **kw)

    def merge_emit(self, A, B, a_ok, b_ok):
        eng_free = {}
        ready = {}
        acc = {}

        def est(r):
            kind, eng, fn, reads, writes, cost, kw = r
            t = eng_free.get(eng, 0.0)
            for b in reads:
                rt_, re_ = ready.get(id(b), (0.0, eng))
                t = max(t, rt_ + (0.15 if re_ != eng else 0.0))
            for b in writes:
                rt_, re_ = ready.get(id(b), (0.0, eng))
                t = max(t, rt_ + (0.15 if re_ != eng else 0.0), acc.get(id(b), 0.0) + 0.1)
            return t

        def commit(r, t):
            kind, eng, fn, reads, writes, cost, kw = r
            if kind == "dma":
                eng_free[eng] = t + 0.1
                end = t + cost
            else:
                end = t + cost
                eng_free[eng] = end
            for b in reads:
                acc[id(b)] = max(acc.get(id(b), 0.0), end)
            for b in writes:
                ready[id(b)] = (end, eng)
                acc[id(b)] = max(acc.get(id(b), 0.0), end)

        def run_unit(u):
            for r in u:
                commit(r, est(r))
                self.emit(r)

        ia = ib = 0
        ja = jb = 0
        while ia < len(A) or ib < len(B):
            ca = None
            cb = None
            if ia < len(A) and (ja > 0 or a_ok(ia, ib)):
                ca = A[ia][ja]
            if ib < len(B) and (jb > 0 or b_ok(ib, ia)):
                cb = B[ib][jb]
            assert ca is not None or cb is not None, (ia, ib, ja, jb)
            ta = est(ca[0]) if ca is not None else None
            tb_ = est(cb[0]) if cb is not None else None
            if cb is None or (ca is not None and ta + FBIAS < tb_):
                run_unit(ca); ja += 1
                if ja == len(A[ia]):
                    ia += 1; ja = 0
            else:
                run_unit(cb); jb += 1
                if jb == len(B[ib]):
                    ib += 1; jb = 0

    def barrier(self):
        toks = [(e["sem"], e["cnt"], n) for n, e in self.eng.items() if e["cnt"] > 0]
        toks += [(d["sem"], 16 * d["cnt"], "dma") for d in self.dma_sems if d["cnt"] > 0]
        for n in self.eng:
            for tok in toks:
                if tok[2] == n:
                    continue
                self._wait(n, tok)

    def finish(self, tiles, engname="sync"):
        for b in self._bufs(tiles):
            if b.w is not None:
                self._wait(engname, b.w)


class _Stop(Exception):
    pass


def build_program(stop=None):
    nc = bass.Bass("TRN2", target_bir_lowering=False)
    S = Sched(nc)
    try:
        _build_body(nc, S, stop)
    except _Stop:
        S.barrier()
    return nc, S


def _build_body(nc, S, stop):
    def chk(label):
        if stop == label:
            raise _Stop()


    def din(name, shape):
        return nc.dram_tensor(name, list(shape), F32, kind="ExternalInput").ap()

    def dout(name, shape):
        return T(nc.dram_tensor(name, list(shape), F32, kind="ExternalOutput").ap(), name)

    xp = din("xp", [SEQ, D])
    xs = din("xs", [NS, D])
    sshift = din("sshift", [DB, D_SHIFT])
    swkv = din("swkv", [128, 4096])
    spool = din("spool", [DB * 15, 512])
    w_in = din("w_in", [D, D_IN])
    w_out = din("w_out", [D, D])
    wdec = din("wdec", [64, 512])
    waaa = din("waaa", [64, 512])
    poolw = din("poolw", [4, 128, 128])
    pvec_d = din("pvec", [128, PV_END])
    browA_d = din("browA", [1, D_SHIFT])
    browB_d = din("browB", [1, 3584])
    normf_d = din("normf", [1, D])
    cst_d = din("cst", [128, C_END])

    yp = dout("yp", [SEQ, D])
    ys = dout("ys", [NS, D])
    nsp = dout("nsp", [13, 128])
    nwp = dout("nwp", [8, 64, 64])
    npp = dout("npp", [15, 512])
    nss = dout("nss", [DB, D_SHIFT])
    nws = dout("nws", [128, 4096])
    nps = dout("nps", [DB, 15, 512])
    scr1 = T(nc.dram_tensor("scr1", [6, DT, DB, 8, 64], F32, kind="Internal").ap(), "scr1")
    scr2 = T(nc.dram_tensor("scr2", [DB, 8, DT, 64], F32, kind="Internal").ap(), "scr2")

    es_top = ExitStack()

    def sb(es, name, shape, dt=F32):
        return T(es.enter_context(nc.sbuf_tensor("s_" + name, list(shape), dt)), name)

    def pst(name, shape, dt=F32):
        return T(nc.alloc_psum_tensor("p_" + name, list(shape), dt), name)

    def nel(ap):
        n = 1
        for s_ in ap.shape[1:]:
            n *= s_
        return n

    def mm(out, lhsT, rhs, start, stop, reads, writes):
        passes = 4 if lhsT.dtype == F32 else 1
        c_ = max(0.055, nel(rhs) * passes / 2000.0 + 0.03)
        S.op("tensor", lambda e: e.matmul(out, lhsT=lhsT, rhs=rhs, start=start, stop=stop), reads=reads, writes=writes, cost=c_)

    def tr(out, in_, ident, reads, writes):
        S.op("tensor", lambda e: e.transpose(out, in_, ident), reads=reads, writes=writes, cost=0.13)

    def act(out, in_, func, reads, writes, bias=None, scale=None, eng="scalar", accum=None):
        kw = {}
        if accum is not None:
            kw["accum_out"] = accum
        if bias is not None:
            kw["bias"] = bias
        if scale is not None:
            kw["scale"] = scale
        S.op("scalar", lambda e: e.activation(out=out, in_=in_, func=func, **kw), reads=reads, writes=writes,
             cost=0.1 + 0.1 * len(kw) + nel(in_) * 0.00095)

    def ecost(eng, n):
        return 0.08 + n * (0.00105 if eng == "vector" else 0.0025)

    def vtt(out, in0, in1, op, reads, writes, eng="vector"):
        S.op(eng, lambda e: e.tensor_tensor(out=out, in0=in0, in1=in1, op=op), reads=reads, writes=writes, cost=ecost(eng, nel(out)))

    def vts(out, in0, s1, s2, op0, op1, reads, writes, eng="vector"):
        if op1 is None:
            S.op(eng, lambda e: e.tensor_scalar(out=out, in0=in0, scalar1=s1, scalar2=None, op0=op0), reads=reads, writes=writes,
                 cost=ecost(eng, nel(out)))
        else:
            S.op(eng, lambda e: e.tensor_scalar(out=out, in0=in0, scalar1=s1, scalar2=s2, op0=op0, op1=op1), reads=reads, writes=writes,
                 cost=ecost(eng, nel(out)))

    def vstt(out, in0, scalar, in1, op0, op1, reads, writes):
        S.op("vector", lambda e: e.scalar_tensor_tensor(out=out, in0=in0, scalar=scalar, in1=in1, op0=op0, op1=op1), reads=reads, writes=writes,
             cost=ecost("vector", nel(out)))

    def vcopy(out, in_, reads, writes, eng="vector"):
        S.op(eng, lambda e: e.tensor_copy(out=out, in_=in_), reads=reads, writes=writes, cost=ecost(eng, nel(out)))

    def vred(out, in_, reads, writes):
        S.op("vector", lambda e: e.tensor_reduce(out=out, in_=in_, axis=AX.X, op=ALU.add), reads=reads, writes=writes,
             cost=ecost("vector", nel(in_)))

    def vrecip(out, in_, reads, writes):
        S.op("vector", lambda e: e.reciprocal(out=out, in_=in_), reads=reads, writes=writes, cost=0.08 + nel(out) * 0.0084)

    def memset(ap, val, writes, eng="gpsimd"):
        S.op(eng, lambda e: e.memset(ap, val), writes=writes)

    def rsqrt_small(out, in_, tmp, scale, eps, reads, writes):
        act(tmp, in_, AF.Sqrt, reads=reads, writes=writes, bias=None, scale=None) if False else None
        vts(tmp, in_, scale, eps, ALU.mult, ALU.add, reads=reads, writes=writes)
        act(tmp, tmp, AF.Sqrt, reads=writes, writes=writes)
        vrecip(out, tmp, reads=writes, writes=writes)

    def rsqrt_act(out, in_, scale, eps_col, n, reads, writes):
        act(out, in_, AF.Ln, reads=list(reads) + [epsT], writes=writes, scale=scale, bias=epsT[0:n, eps_col:eps_col + 1])
        act(out, out, AF.Exp, reads=writes, writes=writes, scale=-0.5)

    pg = [pst(f"pg{i}", [128, 512]) for i in range(2)]
    pT = pst("pT", [128, 1024], BF16)
    pM = pst("pM", [128, 512])
    pA = pst("pA", [128, 512])
    pB = pst("pB", [128, 512])
    pC = pst("pC", [128, 512])
    pD = pst("pD", [128, 512])

    cst = sb(es_top, "cst", [128, C_END])
    pvec = sb(es_top, "pvec", [128, PV_END])
    omu = sb(es_top, "omu", [128, 13])
    omka = sb(es_top, "omka", [128, 4])
    identb = sb(es_top, "identb", [128, 128], BF16)
    winb = sb(es_top, "winb", [128, 8, D_IN], BF16)
    woutb = sb(es_top, "woutb", [128, 8, D], BF16)
    wd = sb(es_top, "wd", [64, 512])
    wa = sb(es_top, "wa", [128, 512])
    pw = sb(es_top, "pw", [128, 4, 128])
    normf = sb(es_top, "normf", [128, D])
    epsT = sb(es_top, "epsT", [128, 4])

    ident = cst[:, C_ID:C_ID + 128]
    onesblk = cst[:, C_ONES:C_ONES + 128]

    S.dma("sync", cst[:], cst_d, writes=[cst])
    S.dma("sync", pvec[:], pvec_d, writes=[pvec])
    S.dma("sync", wd[:], wdec, writes=[wd])
    S.dma("sync", wa[64:128, :], waaa, writes=[wa])
    S.dma("sync", pw[:], poolw.rearrange("g c e -> c g e"), writes=[pw])
    S.dma("sync", normf[:], normf_d.partition_broadcast(128), writes=[normf])
    vcopy(identb[:], ident, reads=[cst], writes=[identb])
    memset(epsT[:, 0:1], NORM_EPS, writes=[epsT])
    memset(epsT[:, 1:2], GN_EPS, writes=[epsT])
    memset(epsT[:, 2:3], L2_EPS, writes=[epsT])
    vts(omka[:], pvec[:, PV_KA:PV_KA + 4], -1.0, 1.0, ALU.mult, ALU.add, reads=[pvec], writes=[omka])

    with ExitStack() as es:
        stg = [sb(es, f"stg{i}", [128, D_IN]) for i in range(3)]
        for dc in range(8):
            st = stg[dc % 3]
            S.dma("sync", st[:], w_in[dc * 128:(dc + 1) * 128, :], writes=[st])
            h = D_IN // 2
            vts(winb[:, dc, 0:h], st[:, 0:h], pvec[:, PV_NW + dc:PV_NW + dc + 1], None, ALU.mult, None, reads=[st, pvec], writes=[winb])
            act(winb[:, dc, h:], st[:, h:], AF.Copy, reads=[st, pvec], writes=[winb], scale=pvec[:, PV_NW + dc:PV_NW + dc + 1])
        S.barrier()
        chk("W")

    def final_tile(es_tiles, n, x_t, oT_list, out_dram_ap, out_T):
        res, sq, ssum, tmp1, rstd, yo = es_tiles
        for half in range(2):
            bank = pD if half == 0 else pC
            for fc in range(8):
                mm(bank[0:n, :], oT_list[fc], woutb[:, fc, half * 512:(half + 1) * 512], fc == 0, fc == 7,
                   reads=[oT_list_T, woutb], writes=[bank])
            vtt(res[0:n, half * 512:(half + 1) * 512], bank[0:n, :], x_t[0:n, half * 512:(half + 1) * 512], ALU.add,
                reads=[bank, x_t], writes=[res])
        act(sq[0:n, :], res[0:n, :], AF.Square, reads=[res], writes=[sq])
        vred(ssum[0:n, :], sq[0:n, :], reads=[sq], writes=[ssum])
        rsqrt_small(rstd[0:n, :], ssum[0:n, :], tmp1[0:n, :], 1.0 / D, NORM_EPS, reads=[ssum], writes=[tmp1, rstd])
        vstt(yo[0:n, :], res[0:n, :], rstd[0:n, 0:1], normf[0:n, :], ALU.mult, ALU.mult, reads=[res, rstd, normf], writes=[yo])
        S.dma("sync", out_dram_ap, yo[0:n, :], reads=[yo], writes=[out_T])

    oT_list_T = None

    with ExitStack() as es:
        browB = sb(es, "browB", [NS, 3584])
        S.dma("sync", browB[:], browB_d.partition_broadcast(NS), writes=[browB])
        x_s = sb(es, "x_s", [NS, D])
        S.dma("sync", x_s[:], xs, writes=[x_s])
        hTs = sb(es, "hTs", [128, 8, DB + NS], BF16)
        graw_s = sb(es, "graw_s", [NS, 512])
        u_s = sb(es, "u_s", [NS, 512])
        gp_s = sb(es, "gp_s", [NS, 512])
        bonus_s = sb(es, "bonus_s", [NS, 512])
        st8 = sb(es, "st8", [NS, 8])
        st8b = sb(es, "st8b", [NS, 8])
        st8c = sb(es, "st8c", [NS, 8])

        def v3(ap):
            return ap.rearrange("p (h k) -> p h k", k=64)

        def bc8(ap8):
            return ap8.unsqueeze(2).to_broadcast([NS, 8, 64])

        with ExitStack() as e1:
            browA = sb(e1, "browA", [NS, D_SHIFT])
            S.dma("sync", browA[:], browA_d.partition_broadcast(NS), writes=[browA])
            omka_b = sb(e1, "omka_b", [NS, 512])
            vts(omka_b[:], browB[:, BRB_KA:BRB_KA + 512], -1.0, 1.0, ALU.mult, ALU.add, reads=[browB], writes=[omka_b])
            sq_s = sb(e1, "sq_s", [NS, D])
            ss_s = sb(e1, "ss_s", [NS, 1])
            t1_s = sb(e1, "t1_s", [NS, 1])
            rstd_s = sb(e1, "rstd_s", [NS, 1])
            xn_s = sb(e1, "xn_s", [NS, D], BF16)
            act(sq_s[:], x_s[:], AF.Square, reads=[x_s], writes=[sq_s])
            vred(ss_s[:], sq_s[:], reads=[sq_s], writes=[ss_s])
            rsqrt_small(rstd_s[:], ss_s[:], t1_s[:], 1.0 / D, NORM_EPS, reads=[ss_s], writes=[t1_s, rstd_s])
            vts(xn_s[:], x_s[:], rstd_s[:, 0:1], None, ALU.mult, None, reads=[x_s, rstd_s], writes=[xn_s])
            memset(hTs[:, :, 0:DB], 0.0, writes=[hTs])
            for dc in range(8):
                tr(pT[:, dc * 128:dc * 128 + NS], xn_s[:, dc * 128:(dc + 1) * 128], identb[0:NS, 0:NS], reads=[xn_s, identb], writes=[pT])
            vcopy(hTs[:, :, DB:DB + NS], pT[:].rearrange("p (c t) -> p c t", t=128)[:, :, 0:NS], reads=[pT], writes=[hTs])

            p_s = sb(e1, "p_s", [NS, D_SHIFT])
            prev_s = sb(e1, "prev_s", [NS, D_SHIFT])
            col_chunks = [(0, 512), (512, 512), (1024, 512), (1536, 128)]
            kk_ = 0
            for (c0, n) in col_chunks:
                bank = pg[kk_ % 2]; kk_ += 1
                for dc in range(8):
                    mm(bank[0:NS, 0:n], hTs[:, dc, DB:DB + NS], winb[:, dc, c0:c0 + n], dc == 0, dc == 7, reads=[hTs, winb], writes=[bank])
                act(p_s[:, c0:c0 + n], bank[0:NS, 0:n], AF.Copy, reads=[bank], writes=[p_s])
                bank = pg[kk_ % 2]; kk_ += 1
                for dc in range(8):
                    mm(bank[0:NS, 0:n], hTs[:, dc, 0:NS], winb[:, dc, c0:c0 + n], dc == 0, dc == 7, reads=[hTs, winb], writes=[bank])
                vcopy(prev_s[:, c0:c0 + n], bank[0:NS, 0:n], reads=[bank], writes=[prev_s])
            for (c0, dst, fn) in [(1664, graw_s, AF.Silu), (2176, u_s, AF.Copy), (2688, gp_s, AF.Silu)]:
                bank = pg[kk_ % 2]; kk_ += 1
                for dc in range(8):
                    mm(bank[0:NS, :], hTs[:, dc, DB:DB + NS], winb[:, dc, c0:c0 + 512], dc == 0, dc == 7, reads=[hTs, winb], writes=[bank])
                act(dst[:], bank[0:NS, :], fn, reads=[bank], writes=[dst])
            S.dma("sync", prev_s[0:DB, :], sshift, writes=[prev_s])
            S.dma("sync", nss[:], p_s[NS - DB:NS, :], reads=[p_s], writes=[nss])
            S.dma("sync", nps[:, 0:11, :], spool.rearrange("(b j) c -> b j c", j=15)[:, 4:15, :], writes=[nps])
            for t in range(DT):
                S.dma("sync", nps[:, 11 + t, :], u_s[t * DB:(t + 1) * DB, :], reads=[u_s], writes=[nps])

            vtt(prev_s[:], prev_s[:], p_s[:], ALU.subtract, reads=[prev_s, p_s], writes=[prev_s])
            vtt(prev_s[:], prev_s[:], browA[:], ALU.mult, reads=[prev_s, browA], writes=[prev_s])
            vtt(prev_s[:], prev_s[:], p_s[:], ALU.add, reads=[prev_s, p_s], writes=[prev_s])
            ps_s = prev_s
            r_s = ps_s[:, 0:512]
            k_s = ps_s[:, 512:1024]
            v_s = ps_s[:, 1024:1536]

            lT = sb(e1, "lT", [128, NS])
            tr(pM[:, 0:NS], ps_s[:, 1536:1664], ident[0:NS, 0:NS], reads=[ps_s, cst], writes=[pM])
            act(lT[0:64, :], pM[0:64, 0:NS], AF.Tanh, reads=[pM], writes=[lT])
            act(lT[64:128, :], pM[64:128, 0:NS], AF.Copy, reads=[pM], writes=[lT])
            sg_s = sb(e1, "sg_s", [NS, 512])
            a_s = sb(e1, "a_s", [NS, 512])
            mm(pA[0:NS, :], lT[0:64, :], wd[:, :], True, True, reads=[lT, wd], writes=[pA])
            vtt(sg_s[:], pA[0:NS, :], browB[:, BRB_W0:BRB_W0 + 512], ALU.add, reads=[pA, browB], writes=[sg_s])
            act(sg_s[:], sg_s[:], AF.Sigmoid, reads=[sg_s], writes=[sg_s])
            mm(pB[0:NS, :], lT[64:128, :], wa[64:128, :], True, True, reads=[lT, wa], writes=[pB])
            vtt(a_s[:], pB[0:NS, :], browB[:, BRB_A0:BRB_A0 + 512], ALU.add, reads=[pB, browB], writes=[a_s])
            act(a_s[:], a_s[:], AF.Sigmoid, reads=[a_s], writes=[a_s])

            pk = sb(e1, "pk", [NS, 4, 512])
            PQ = {1: 0, 2: 1, 4: 2, 5: 3}
            tmpA = sb(e1, "tmpA", [NS, 512])
            tmpB = sb(e1, "tmpB", [NS, 512])
            act(pk[:, PQ[1], :], sg_s[:], AF.Exp, reads=[sg_s], writes=[pk], scale=-C0)
            vtt(tmpA[:], k_s, browB[:, BRB_KK:BRB_KK + 512], ALU.mult, reads=[ps_s, browB], writes=[tmpA])
            vtt(tmpB[:], tmpA[:], tmpA[:], ALU.mult, reads=[tmpA], writes=[tmpB])
            vred(st8[:], v3(tmpB[:]), reads=[tmpB], writes=[st8])
            rsqrt_small(st8b[:], st8[:], st8c[:], 1.0, L2_EPS, reads=[st8], writes=[st8c, st8b])
            vtt(v3(tmpA[:]), v3(tmpA[:]), bc8(st8b[:]), ALU.mult, reads=[tmpA, st8b], writes=[tmpA])
            vts(pk[:, PQ[4], :], tmpA[:], -1.0, None, ALU.mult, None, reads=[tmpA], writes=[pk])
            vtt(pk[:, PQ[5], :], tmpA[:], a_s[:], ALU.mult, reads=[tmpA, a_s], writes=[pk])
            vtt(tmpB[:], a_s[:], browB[:, BRB_KA:BRB_KA + 512], ALU.mult, reads=[a_s, browB], writes=[tmpB])
            vtt(tmpB[:], tmpB[:], omka_b[:], ALU.add, reads=[tmpB, omka_b], writes=[tmpB])
            vtt(pk[:, PQ[2], :], k_s, tmpB[:], ALU.mult, reads=[ps_s, tmpB], writes=[pk])
            vtt(tmpB[:], r_s, browB[:, BRB_RK:BRB_RK + 512], ALU.mult, reads=[ps_s, browB], writes=[tmpB])
            vtt(tmpB[:], tmpB[:], pk[:, PQ[2], :], ALU.mult, reads=[tmpB, pk], writes=[tmpB])
            vred(st8[:], v3(tmpB[:]), reads=[tmpB], writes=[st8])
            vtt(v3(bonus_s[:]), v3(v_s), bc8(st8[:]), ALU.mult, reads=[ps_s, st8], writes=[bonus_s])
            sview = scr1[:].rearrange("q t b h k -> q (t b) (h k)")
            S.dma("sync", sview[0], r_s, reads=[ps_s], writes=[scr1])
            S.dma("sync", sview[3], v_s, reads=[ps_s], writes=[scr1])
            for qq, slot in PQ.items():
                S.dma("sync", sview[qq], pk[:, slot, :], reads=[pk], writes=[scr1])
            S.finish([scr1], engname="sync")
            S.barrier()
            chk("S1")

        with ExitStack() as e2:
            sIn = sb(e2, "sIn", [128, 6, DT, 64])
            S.dma("sync", sIn[:], scr1[:].rearrange("q t b h k -> (b h) q t k"), reads=[scr1], writes=[sIn])
            St = sb(e2, "St", [128, 64, 64])
            S.dma("sync", St[:].rearrange("p v k -> p (v k)"), swkv, writes=[St])
            tmpS = sb(e2, "tmpS", [128, 64, 64])
            sa = sb(e2, "sa", [128, 64])
            yS = sb(e2, "yS", [128, DT, 64])
            stgo = [sb(e2, f"stgo{i}", [128, D]) for i in range(3)]
            for fc in range(8):
                so = stgo[fc % 3]
                S.dma("sync", so[:], w_out[fc * 128:(fc + 1) * 128, :], writes=[so])
                act(woutb[:, fc, :], so[:], AF.Copy, reads=[so], writes=[woutb])

            def bv(ap):
                return ap.unsqueeze(1).to_broadcast([128, 64, 64])

            def bk(ap):
                return ap.unsqueeze(2).to_broadcast([128, 64, 64])

            for t in range(DT):
                q = lambda i: sIn[:, i, t, :]
                vtt(tmpS[:], St[:], bv(q(4)), ALU.mult, reads=[St, sIn], writes=[tmpS])
                vred(sa[:], tmpS[:], reads=[tmpS], writes=[sa])
                vtt(St[:], St[:], bv(q(1)), ALU.mult, reads=[St, sIn], writes=[St])
                vtt(tmpS[:], bk(sa[:]), bv(q(5)), ALU.mult, reads=[sa, sIn], writes=[tmpS])
                vtt(St[:], St[:], tmpS[:], ALU.add, reads=[St, tmpS], writes=[St])
                vtt(tmpS[:], bk(q(3)), bv(q(2)), ALU.mult, reads=[sIn], writes=[tmpS])
                vtt(St[:], St[:], tmpS[:], ALU.add, reads=[St, tmpS], writes=[St])
                vtt(tmpS[:], St[:], bv(q(0)), ALU.mult, reads=[St, sIn], writes=[tmpS])
                vred(yS[:, t, :], tmpS[:], reads=[tmpS], writes=[yS])
            S.dma("sync", nws[:], St[:].rearrange("p v k -> p (v k)"), reads=[St], writes=[nws])
            S.dma("sync", scr2[:].rearrange("b h t v -> (b h) t v"), yS[:], reads=[yS], writes=[scr2])
            S.finish([scr2, nws], engname="sync")
            S.barrier()
            chk("S2")

        with ExitStack() as e3:
            yT = sb(e3, "yT", [NS, 512])
            tmpA = sb(e3, "tmpA3", [NS, 512])
            for t in range(DT):
                S.dma("sync", yT[t * DB:(t + 1) * DB, :].rearrange("b (h v) -> b h v", v=64), scr2[:][:, :, t, :], reads=[scr2], writes=[yT])
            vred(st8[:], v3(yT[:]), reads=[yT], writes=[st8])
            vts(st8[:], st8[:], 1.0 / 64, None, ALU.mult, None, reads=[st8], writes=[st8])
            vtt(v3(yT[:]), v3(yT[:]), bc8(st8[:]), ALU.subtract, reads=[yT, st8], writes=[yT])
            vtt(tmpA[:], yT[:], yT[:], ALU.mult, reads=[yT], writes=[tmpA])
            vred(st8[:], v3(tmpA[:]), reads=[tmpA], writes=[st8])
            rsqrt_small(st8b[:], st8[:], st8c[:], 1.0 / 64, GN_EPS, reads=[st8], writes=[st8c, st8b])
            vtt(v3(yT[:]), v3(yT[:]), bc8(st8b[:]), ALU.mult, reads=[yT, st8b], writes=[yT])
            vtt(yT[:], yT[:], browB[:, BRB_GW:BRB_GW + 512], ALU.mult, reads=[yT, browB], writes=[yT])
            vtt(yT[:], yT[:], browB[:, BRB_GB:BRB_GB + 512], ALU.add, reads=[yT, browB], writes=[yT])
            vtt(yT[:], yT[:], bonus_s[:], ALU.add, reads=[yT, bonus_s], writes=[yT])
            vtt(yT[:], yT[:], graw_s[:], ALU.mult, reads=[yT, graw_s], writes=[yT])
            oTs = sb(e3, "oTs", [128, 8, NS], BF16)
            for fb in range(4):
                tr(pA[:, fb * 64:fb * 64 + NS], yT[:, fb * 128:(fb + 1) * 128], ident[0:NS, 0:NS], reads=[yT, cst], writes=[pA])
            vcopy(oTs[:, 0:4, :], pA[:, 0:4 * NS].rearrange("p (f t) -> p f t", t=NS), reads=[pA], writes=[oTs])

            uext = sb(e3, "uext_s", [128, 4, DB, 19])
            sp0 = sb(e3, "sp0", [120, 512])
            sp1 = sb(e3, "sp1", [120, 512])
            S.dma("sync", sp0[:], spool[0:120, :], writes=[sp0])
            S.dma("sync", sp1[:], spool[120:240, :], writes=[sp1])
            for g in range(4):
                tr(pB[:, 0:120], sp0[:, g * 128:(g + 1) * 128], ident[0:120, 0:120], reads=[sp0, cst], writes=[pB])
                tr(pB[:, 128:248], sp1[:, g * 128:(g + 1) * 128], ident[0:120, 0:120], reads=[sp1, cst], writes=[pB])
                vcopy(uext[:, g, 0:8, 0:15], pB[:, 0:120].rearrange("p (b j) -> p b j", j=15), reads=[pB], writes=[uext])
                vcopy(uext[:, g, 8:16, 0:15], pB[:, 128:248].rearrange("p (b j) -> p b j", j=15), reads=[pB], writes=[uext])
                tr(pM[:, 0:NS], u_s[:, g * 128:(g + 1) * 128], ident[0:NS, 0:NS], reads=[u_s, cst], writes=[pM])
                vcopy(uext[:, g, :, 15:19], pM[:, 0:NS].rearrange("p (t b) -> p b t", b=DB), reads=[pM], writes=[uext])
            s2 = sb(e3, "s2_s", [128, 4, DB, 19])
            s4 = sb(e3, "s4_s", [128, 3, DB, 19])
            s8 = sb(e3, "s8_s", [128, 2, DB, 19])
            s16 = sb(e3, "s16_s", [128, 1, DB, 19])
            d_s = sb(e3, "d_s", [128, 4, DT, DB])
            vtt(s2[:, :, :, 1:19], uext[:, :, :, 1:19], uext[:, :, :, 0:18], ALU.add, reads=[uext], writes=[s2])
            vtt(s4[:, :, :, 3:19], s2[:, 1:4, :, 3:19], s2[:, 1:4, :, 1:17], ALU.add, reads=[s2], writes=[s4])
            vtt(s8[:, :, :, 7:19], s4[:, 1:3, :, 7:19], s4[:, 1:3, :, 3:15], ALU.add, reads=[s4], writes=[s8])
            vtt(s16[:, :, :, 15:19], s8[:, 1:2, :, 15:19], s8[:, 1:2, :, 7:11], ALU.add, reads=[s8], writes=[s16])
            tots = [(s2, 0), (s4, 1), (s8, 2), (s16, 3)]
            for g in range(4):
                tt, off = tots[g]
                vstt(d_s[:, g, :, :].rearrange("p t b -> p b t"), tt[:, g - off, :, 15:19], 1.0 / WINS[g], uext[:, g, :, 15:19],
                     ALU.mult, ALU.subtract, reads=[tt, uext], writes=[d_s])
            gpT = sb(e3, "gpT", [128, 4, NS])
            for g in range(4):
                tr(pM[:, 64 + g * 64:64 + g * 64 + NS], gp_s[:, g * 128:(g + 1) * 128], ident[0:NS, 0:NS], reads=[gp_s, cst], writes=[pM])
            vcopy(gpT[:], pM[:, 64:64 + 4 * NS].rearrange("p (g t) -> p g t", t=NS), reads=[pM], writes=[gpT])
            for g in range(4):
                mm(pA[:, g * 64:g * 64 + NS], pw[:, g, :], d_s[:, g, :, :].rearrange("p t b -> p (t b)"), True, True, reads=[pw, d_s], writes=[pA])
            for g in range(4):
                vstt(oTs[:, 4 + g, :], pA[:, g * 64:g * 64 + NS], pvec[:, PV_PS + g:PV_PS + g + 1], gpT[:, g, :], ALU.mult, ALU.mult,
                     reads=[pA, pvec, gpT], writes=[oTs])

            sq = sb(e3, "sq2_s", [NS, D]); ssum = sb(e3, "ssum_s", [NS, 1])
            tmp1 = sb(e3, "tmp1_s", [NS, 1]); rstd = sb(e3, "rstd2_s", [NS, 1]); yo = sb(e3, "yo_s", [NS, D])
            oT_list_T = oTs
            final_tile((x_s, sq, ssum, tmp1, rstd, yo), NS, x_s, [oTs[:, fc, :] for fc in range(8)], ys[:], ys)
            S.finish([ys, nss, nps], engname="sync")
            S.barrier()
            chk("S3")

    with ExitStack() as es:
        def sbl(name, shape, dt=F32, n=2):
            return [sb(es, f"{name}_{i}", shape, dt) for i in range(n)]

        xt = sbl("xt", [128, D])
        yo = sb(es, "yo", [128, D])
        ssx = sb(es, "ssx", [128, 1]); t1x = sb(es, "t1x", [128, 1]); rsx = sb(es, "rsx", [128, 1])
        xnb = sb(es, "xnb", [128, D], BF16)
        sqx = xnb
        hT = sb(es, "hT", [128, 8, TB], BF16)
        praw = sbl("praw", [128, 4, TB + 1])
        halo = sb(es, "halo", [128, 13])
        omu = sb(es, "omu2", [128, 13])
        psr = sb(es, "psr", [128, 4, TB]); psk = sb(es, "psk", [128, 4, TB]); psv = sb(es, "psv", [128, 4, TB])
        ps12 = sb(es, "ps12", [128, TB])
        psx = [T(g_[:, i, :], f"psx{gi_}_{i}", buf=g_.b) for gi_, g_ in enumerate([psr, psk, psv]) for i in range(4)] + [ps12]
        sg = sb(es, "sg", [128, 4, TB]); av = sb(es, "av", [128, 4, TB])
        gsil = sbl("gsil", [128, 4, TB], BF16)
        gpsil = sb(es, "gpsil", [128, 4, TB], BF16)
        uext = sb(es, "uext", [128, 4, 15 + TB])
        th = sb(es, "th", [64, TB])
        wbig = [sb(es, f"wbig{i}", [128, 4, TB]) for i in range(4)]
        w1, w2, w3, w4 = wbig
        srot = [T(wbig[i][:].rearrange("p f t -> p (f t)")[:, 0:15 + TB], f"srot{i}", buf=wbig[i].b) for i in range(4)]
        kkn = sb(es, "kkn", [128, 4, TB]); kmod = sb(es, "kmod", [128, 4, TB]); bv_ = sb(es, "bv_", [128, 4, TB])
        cum = sb(es, "cum", [128, 4, TB])
        dpl = T(kkn[:, 0, :], "dpl", buf=kkn.b)
        at = sbl("at", [64, 4, 2, TB], BF16)
        rt = sbl("rt", [64, 4, 2, TB], BF16)
        bt = sb(es, "bt", [64, 4, 2, TB], BF16)
        kt = sb(es, "kt", [64, 4, 2, TB], BF16)
        bh = sb(es, "bh", [128, 4, TB], BF16); kh = sb(es, "kh", [128, 4, TB], BF16); vb = sb(es, "vb", [128, 4, TB], BF16)
        bon = sbl("bon", [128, 4, TB])
        gC = sbl("gC", [64, 4, 2, NCH])
        VT = [[sb(es, f"VT{p}{c}", [64, 512], BF16) for c in range(NCH)] for p in range(2)]
        BKT = [[sb(es, f"BKT{p}{c}", [64, 1024], BF16) for c in range(NCH)] for p in range(2)]
        Aak = [[sb(es, f"Aak{p}{c}", [64, 512], BF16) for c in range(NCH)] for p in range(2)]
        Arb = [[sb(es, f"Arb{p}{c}", [64, 512], BF16) for c in range(NCH)] for p in range(2)]
        Ark = [[sb(es, f"Ark{p}{c}", [64, 512], BF16) for c in range(NCH)] for p in range(2)]
        Minv = [[sb(es, f"Minv{p}{c}", [64, 512], BF16) for c in range(NCH)] for p in range(2)]
        Nsb = [sb(es, f"Nsb{c}", [64, 512], BF16) for c in range(NCH)]
        NTsb = [sb(es, f"NTsb{c}", [64, 512], BF16) for c in range(NCH)]
        Xa0 = [sb(es, f"Xa0{c}", [64, 512], BF16) for c in range(NCH)]
        XTa0 = [sb(es, f"XTa0{c}", [64, 512], BF16) for c in range(NCH)]
        Qtmp = [sb(es, f"Qtmp{c}", [64, 512], BF16) for c in range(NCH)]
        ST = sb(es, "ST", [64, 8, 64]); STb = sb(es, "STb", [64, 8, 64], BF16)
        Wsb = sb(es, "Wsb", [64, 512], BF16); Usb = sb(es, "Usb", [64, 512], BF16)
        yc = sb(es, "yc", [64, 512]); ysq = sb(es, "ysq", [64, 512])
        STt = T(ysq[:].rearrange("p (h v) -> p h v", v=64), "STt", buf=ysq.b)
        m8 = sb(es, "m8", [64, 8]); v8 = sb(es, "v8", [64, 8]); r8 = sb(es, "r8", [64, 8]); t8 = sb(es, "t8", [64, 8])
        o1 = sb(es, "o1", [128, 4, 64])
        oT = sbl("oT", [128, 8, TB], BF16)
        ssum = sb(es, "ssum", [128, 1]); tmp1 = sb(es, "tmp1", [128, 1]); rstd = sb(es, "rstd", [128, 1])
        ppT = T(ysq[0:16, :], "ppT", buf=ysq.b); m13 = sb(es, "m13", [13, 128])
        SvT = T(yc[:].rearrange("p (h k) -> p h k", k=64), "SvT", buf=yc.b)

        memset(halo[:], 0.0, writes=[halo])
        memset(uext[:, :, 0:15], 0.0, writes=[uext])
        memset(ST[:], 0.0, writes=[ST])
        memset(STb[:], 0.0, writes=[STb])
        vts(omu[:], pvec[:, PV_MU:PV_MU + 13], -1.0, 1.0, ALU.mult, ALU.add, reads=[pvec], writes=[omu])

        def b8(ap):
            return ap.unsqueeze(1).to_broadcast([64, 8, 64])

        def h3(ap):
            return ap.rearrange("p (h v) -> p h v", v=64)

        def hc(h):
            return slice(h * 64, (h + 1) * 64)

        maskUs = b8(cst[0:64, C_MUS:C_MUS + 64])
        maskUi = b8(cst[0:64, C_MUI:C_MUI + 64])
        maskLs = b8(cst[0:64, C_MLS:C_MLS + 64])
        ident8 = b8(cst[0:64, C_ID:C_ID + 64])
        rstm = cst[:, C_RST:C_RST + 512]
        st = dict(gk=0, ak=0)
        pT32 = T(pT[:].bitcast(F32), "pT32", buf=pT.b)
        abanks = [pA, pB, pg[0], pg[1], pM, pT32]

        def nextbank():
            b = abanks[st["ak"] % len(abanks)]
            st["ak"] += 1
            return b

        def front(tb):
            pb = tb % 2
            t0 = tb * TB
            x_t = xt[pb]
            S.dma("sync", x_t[:], xp[t0:t0 + TB, :], writes=[x_t])
            act(sqx[:], x_t[:], AF.Square, reads=[x_t], writes=[sqx, ssx], accum=ssx[:])
            rsqrt_act(rsx[:], ssx[:], 1.0 / D, 0, 128, reads=[ssx], writes=[rsx])
            act(xnb[:], x_t[:], AF.Copy, reads=[x_t, rsx], writes=[xnb], scale=rsx[:, 0:1])
            yield
            for dc in range(8):
                tr(pT[:, dc * 128:(dc + 1) * 128], xnb[:, dc * 128:(dc + 1) * 128], identb[:], reads=[xnb, identb], writes=[pT])
            vcopy(hT[:].rearrange("p c t -> p (c t)"), pT[:], reads=[pT], writes=[hT])
            yield

            def gemm_group(ebs):
                bank = pg[st["gk"] % 2]
                st["gk"] += 1
                for i, eb in enumerate(ebs):
                    for dc in range(8):
                        mm(bank[:, i * TB:(i + 1) * TB], winb[:, dc, eb * 128:(eb + 1) * 128], hT[:, dc, :], dc == 0, dc == 7,
                           reads=[winb, hT], writes=[bank])
                return bank

            for gi, ebs in enumerate([[0, 1, 2, 3], [4, 5, 6, 7], [8, 9, 10, 11], [12]]):
                bank = gemm_group(ebs)
                yield
                n = len(ebs)
                pr = praw[gi % 2]
                e0 = ebs[0]
                gT = [psr, psk, psv, ps12][gi]
                dst = gT[:, 0:n, :] if gi < 3 else ps12[:].unsqueeze(1)
                mub = pvec[:, PV_MU + e0:PV_MU + e0 + n].unsqueeze(2).to_broadcast([128, n, TB])
                vcopy(pr[:, 0:n, 0:1], halo[:, e0:e0 + n].unsqueeze(2), reads=[halo], writes=[pr], eng="gpsimd")
                act(pr[:, 0:n, 1:TB + 1], bank[:, 0:n * TB].rearrange("p (e t) -> p e t", t=TB), AF.Copy, reads=[bank], writes=[pr])
                vtt(dst, pr[:, 0:n, 0:TB], pr[:, 0:n, 1:TB + 1], ALU.subtract, reads=[pr], writes=[gT])
                vtt(dst, dst, mub, ALU.mult, reads=[gT, pvec], writes=[gT])
                vtt(dst, dst, pr[:, 0:n, 1:TB + 1], ALU.add, reads=[gT, pr], writes=[gT])
                vcopy(halo[:, e0:e0 + n].unsqueeze(2), pr[:, 0:n, TB:TB + 1], reads=[pr], writes=[halo], eng="gpsimd")
                yield
            bank = gemm_group([13, 14, 15, 16])
            act(gsil[pb][:].rearrange("p f t -> p (f t)"), bank[:, :], AF.Silu, reads=[bank], writes=[gsil[pb]])
            yield
            bank = gemm_group([17, 18, 19, 20])
            act(uext[:, :, 15:15 + TB], bank[:, :].rearrange("p (g t) -> p g t", t=TB), AF.Copy, reads=[bank], writes=[uext])
            yield
            bank = gemm_group([21, 22, 23, 24])
            act(gpsil[:].rearrange("p g t -> p (g t)"), bank[:, :], AF.Silu, reads=[bank], writes=[gpsil])
            yield

            act(th[:], psx[12][0:64, :], AF.Tanh, reads=[psx[12]], writes=[th])
            for fb in range(4):
                mm(pA[:, fb * TB:(fb + 1) * TB], wd[:, fb * 128:(fb + 1) * 128], th[:], True, True, reads=[wd, th], writes=[pA])
            for fb in range(4):
                mm(pM[:, fb * TB:(fb + 1) * TB], wa[64:128, fb * 128:(fb + 1) * 128], psx[12][64:128, :], True, True, reads=[wa, psx[12]], writes=[pM])
            for fb in range(4):
                act(sg[:, fb, :], pA[:, fb * TB:(fb + 1) * TB], AF.Sigmoid, reads=[pA, pvec], writes=[sg], bias=pvec[:, PV_W0 + fb:PV_W0 + fb + 1])
                act(av[:, fb, :], pM[:, fb * TB:(fb + 1) * TB], AF.Sigmoid, reads=[pM, pvec], writes=[av], bias=pvec[:, PV_A0 + fb:PV_A0 + fb + 1])
            yield

            def pb4(col):
                return pvec[:, col:col + 4].unsqueeze(2).to_broadcast([128, 4, TB])

            def f2(t_):
                return t_[:].rearrange("p f t -> p (f t)")

            bomk = omka[:, 0:4].unsqueeze(2).to_broadcast([128, 4, TB])
            c3 = cum[:].rearrange("p f (c t) -> p (f c) t", t=CH)
            p0, p1 = slice(0, 64), slice(64, 128)
            vcopy(vb[:], psv[:], reads=[psv], writes=[vb], eng="gpsimd")
            vtt(w1[:], psk[:], pb4(PV_KK), ALU.mult, reads=[psk, pvec], writes=[w1])
            vtt(w2[:], w1[:], w1[:], ALU.mult, reads=[w1], writes=[w2])
            mm(pM[:, :], onesblk, f2(w2), True, True, reads=[cst, w2], writes=[pM])
            S.op("vector", lambda e: e.tensor_tensor_scan(out=f2(cum), data0=rstm, data1=f2(sg), initial=0.0, op0=ALU.mult, op1=ALU.add),
                 reads=[cst, sg], writes=[cum], cost=1.2)
            act(f2(w2), pM[:, :], AF.Ln, reads=[pM, epsT], writes=[w2], bias=epsT[:, 2:3])
            act(w2[:], w2[:], AF.Exp, reads=[w2], writes=[w2], scale=-0.5)
            vtt(w3[:], av[:], pb4(PV_KA), ALU.mult, reads=[av, pvec], writes=[w3])
            vtt(w3[:], w3[:], bomk, ALU.add, reads=[w3, omka], writes=[w3])
            vtt(kmod[:], psk[:], w3[:], ALU.mult, reads=[psk, w3], writes=[kmod])
            vtt(w4[:], cum[:], sg[:], ALU.subtract, reads=[cum, sg], writes=[w4])
            act(w4[:], w4[:], AF.Exp, reads=[w4], writes=[w4], scale=-C0)
            vtt(w3[:], psr[:], pb4(PV_RK), ALU.mult, reads=[psr, pvec], writes=[w3])
            vtt(w3[:], w3[:], kmod[:], ALU.mult, reads=[w3, kmod], writes=[w3])
            mm(pA[:, :], onesblk, f2(w3), True, True, reads=[cst, w3], writes=[pA])
            yield
            vstt(kkn[:], w1[:], -1.0, w2[:], ALU.mult, ALU.mult, reads=[w1, w2], writes=[kkn])
            vstt(bv_[:], kkn[:], -1.0, av[:], ALU.mult, ALU.mult, reads=[kkn, av], writes=[bv_])
            act(w1[:], cum[:], AF.Exp, reads=[cum], writes=[w1], scale=-C0)
            act(w2[:], cum[:], AF.Exp, reads=[cum], writes=[w2], scale=C0)
            vtt(at[pb][:, :, 1, :], kkn[p1, :, :], w4[p1, :, :], ALU.mult, reads=[kkn, w4], writes=[at[pb]], eng="gpsimd")
            vtt(at[pb][:, :, 0, :], kkn[p0, :, :], w4[p0, :, :], ALU.mult, reads=[kkn, w4], writes=[at[pb]])
            vtt(f2(bon[pb]), pA[:, :], f2(psv), ALU.mult, reads=[pA, psv], writes=[bon[pb]])
            vtt(w3[:].rearrange("p f (c t) -> p (f c) t", t=CH), c3[:, :, CH - 1:CH].to_broadcast([128, 4 * NCH, CH]), c3, ALU.subtract,
                reads=[cum], writes=[w3], eng="gpsimd")
            act(w3[:], w3[:], AF.Exp, reads=[w3], writes=[w3], scale=-C0)
            for j in range(2):
                pp = slice(64 * j, 64 * j + 64)
                act(gC[pb][:, :, j, :], cum[pp, :, :].rearrange("p f (c t) -> p f c t", t=CH)[:, :, :, CH - 1], AF.Exp,
                    reads=[cum], writes=[gC[pb]], scale=-C0)
            yield
            for j in range(2):
                pp = slice(64 * j, 64 * j + 64)
                e_ = "vector" if j == 0 else "gpsimd"
                vtt(rt[pb][:, :, j, :], psr[pp, :, :], w1[pp, :, :], ALU.mult, reads=[psr, w1], writes=[rt[pb]], eng=e_)
                vtt(kt[:, :, j, :], kmod[pp, :, :], w2[pp, :, :], ALU.mult, reads=[kmod, w2], writes=[kt], eng=e_)
                vtt(bt[:, :, j, :], bv_[pp, :, :], w2[pp, :, :], ALU.mult, reads=[bv_, w2], writes=[bt], eng=e_)
            vtt(bh[:], bv_[:], w3[:], ALU.mult, reads=[bv_, w3], writes=[bh])
            vtt(kh[:], kmod[:], w3[:], ALU.mult, reads=[kmod, w3], writes=[kh], eng="gpsimd")
            yield

            L = 15 + TB
            for g in range(4):
                vtt(srot[0][:, 1:], uext[:, g, 1:], uext[:, g, 0:L - 1], ALU.add, reads=[uext], writes=[srot[0]], eng="gpsimd")
                tot = srot[0]
                if g >= 1:
                    vtt(srot[1][:, 3:], srot[0][:, 3:], srot[0][:, 1:L - 2], ALU.add, reads=[srot[0]], writes=[srot[1]], eng="gpsimd")
                    tot = srot[1]
                if g >= 2:
                    vtt(srot[2][:, 7:], srot[1][:, 7:], srot[1][:, 3:L - 4], ALU.add, reads=[srot[1]], writes=[srot[2]], eng="gpsimd")
                    tot = srot[2]
                if g >= 3:
                    vtt(srot[3][:, 15:], srot[2][:, 15:], srot[2][:, 7:L - 8], ALU.add, reads=[srot[2]], writes=[srot[3]], eng="gpsimd")
                    tot = srot[3]
                vstt(dpl[:], tot[:, 15:], 1.0 / WINS[g], uext[:, g, 15:], ALU.mult, ALU.subtract, reads=[tot, uext], writes=[dpl])
                if tb == 0:
                    vtt(dpl[:, 0:16], tot[:, 15:31], cst[:, C_ICNT + g * 16:C_ICNT + (g + 1) * 16], ALU.mult, reads=[tot, cst], writes=[dpl])
                    vtt(dpl[:, 0:16], dpl[:, 0:16], uext[:, g, 15:31], ALU.subtract, reads=[dpl, uext], writes=[dpl])
                mm(pM[:, 0:TB], pw[:, g, :], dpl[:], True, True, reads=[pw, dpl], writes=[pM])
                vstt(oT[pb][:, 4 + g, :], pM[:, 0:TB], pvec[:, PV_PS + g:PV_PS + g + 1], gpsil[:, g, :], ALU.mult, ALU.mult,
                     reads=[pM, pvec, gpsil], writes=[oT[pb]])
                yield
            if tb == NTB - 1:
                for g in range(4):
                    tr(pA[0:16, g * 128:(g + 1) * 128], uext[:, g, TB - 1:TB + 15], ident, reads=[uext, cst], writes=[pA])
                vcopy(ppT[:], pA[0:16, :], reads=[pA], writes=[ppT])
                S.dma("sync", npp[:], ppT[1:16, :], reads=[ppT], writes=[npp])
                tr(pB[0:13, 0:128], halo[:, 0:13], ident, reads=[halo, cst], writes=[pB])
                vcopy(m13[:], pB[0:13, 0:128], reads=[pB], writes=[m13])
                S.dma("sync", nsp[:], m13[:], reads=[m13], writes=[nsp])
            vcopy(uext[:, :, 0:15], uext[:, :, TB:TB + 15], reads=[uext], writes=[uext], eng="gpsimd")
            yield

            css = [slice(c * CH, (c + 1) * CH) for c in range(NCH)]
            for c in range(NCH):
                for qi, srcl in enumerate([bh, kh]):
                    for fb in range(4):
                        tr(pT[0:64, qi * 512 + fb * 128:qi * 512 + (fb + 1) * 128], srcl[:, fb, css[c]], identb[:], reads=[srcl, identb], writes=[pT])
                vcopy(BKT[pb][c][:], pT[0:64, :], reads=[pT], writes=[BKT[pb][c]])
                for fb in range(4):
                    tr(pT[0:64, fb * 128:(fb + 1) * 128], vb[:, fb, css[c]], identb[:], reads=[vb, identb], writes=[pT])
                act(VT[pb][c][:], pT[0:64, 0:512], AF.Copy, reads=[pT], writes=[VT[pb][c]])
                yield

            def hsl(tl, h, c):
                fb, j = divmod(h, 2)
                return tl[:, fb, j, css[c]]

            for (Lt, Rt, mask, dsts) in [(bt, at[pb], maskUs, Nsb), (at[pb], bt, maskLs, NTsb), (kt, at[pb], maskUs, Aak[pb]),
                                         (bt, rt[pb], maskUi, Arb[pb]), (kt, rt[pb], maskUi, Ark[pb])]:
                banks = []
                for c in range(NCH):
                    bank = nextbank()
                    banks.append(bank)
                    for h in range(8):
                        mm(bank[0:64, hc(h)], hsl(Lt, h, c), hsl(Rt, h, c), True, True, reads=[Lt, Rt], writes=[bank])
                for c in range(NCH):
                    vtt(h3(dsts[c][:]), h3(banks[c][0:64, :]), mask, ALU.mult, reads=[banks[c], cst], writes=[dsts[c]])
                yield
            X = list(Nsb); XT = list(NTsb)
            Q = [Qtmp[c] for c in range(NCH)]
            for c in range(NCH):
                vtt(h3(Q[c][:]), h3(Nsb[c][:]), ident8, ALU.add, reads=[Nsb[c], cst], writes=[Q[c]])
            for lvl in range(5):
                Xn = [(Xa0[c] if lvl % 2 == 0 else Nsb[c]) for c in range(NCH)]
                XTn = [(XTa0[c] if lvl % 2 == 0 else NTsb[c]) for c in range(NCH)]
                Qn = [(Minv[pb][c] if lvl % 2 == 0 else Qtmp[c]) for c in range(NCH)]
                banks = []
                for c in range(NCH):
                    bank = nextbank(); banks.append(bank)
                    for h in range(8):
                        mm(bank[0:64, hc(h)], X[c][:, hc(h)], XT[c][:, hc(h)], True, True, reads=[X[c], XT[c]], writes=[bank])
                for c in range(NCH):
                    act(XTn[c][:], banks[c][0:64, :], AF.Copy, reads=[banks[c]], writes=[XTn[c]])
                yield
                if lvl < 4:
                    banks = []
                    for c in range(NCH):
                        bank = nextbank(); banks.append(bank)
                        for h in range(8):
                            mm(bank[0:64, hc(h)], XT[c][:, hc(h)], X[c][:, hc(h)], True, True, reads=[X[c], XT[c]], writes=[bank])
                    for c in range(NCH):
                        act(Xn[c][:], banks[c][0:64, :], AF.Copy, reads=[banks[c]], writes=[Xn[c]])
                    yield
                banks = []
                for c in range(NCH):
                    bank = nextbank(); banks.append(bank)
                    for h in range(8):
                        mm(bank[0:64, hc(h)], XTn[c][:, hc(h)], Q[c][:, hc(h)], True, True, reads=[XTn[c], Q[c]], writes=[bank])
                for c in range(NCH):
                    vtt(Qn[c][:], banks[c][0:64, :], Q[c][:], ALU.add, reads=[banks[c], Q[c]], writes=[Qn[c]])
                X, XT, Q = Xn, XTn, Qn
                yield

        def chain(tb):
            pb = tb % 2
            t0 = tb * TB
            for c in range(NCH):
                cs = slice(c * CH, (c + 1) * CH)
                aT, rT = at[pb], rt[pb]
                VTc, BKTc, Aakc, Arbc, Arkc, Minvc = VT[pb][c], BKT[pb][c], Aak[pb][c], Arb[pb][c], Ark[pb][c], Minv[pb][c]
                for h in range(8):
                    fb, j = divmod(h, 2)
                    mm(pC[0:64, hc(h)], aT[:, fb, j, cs], STb[:, h, :], True, False, reads=[aT, STb], writes=[pC])
                    mm(pC[0:64, hc(h)], Aakc[:, hc(h)], VTc[:, hc(h)], False, True, reads=[Aakc, VTc], writes=[pC])
                act(Wsb[:], pC[0:64, :], AF.Copy, reads=[pC], writes=[Wsb])
                yield
                for h in range(8):
                    mm(pC[0:64, hc(h)], Minvc[:, hc(h)], Wsb[:, hc(h)], True, True, reads=[Minvc, Wsb], writes=[pC])
                act(Usb[:], pC[0:64, :], AF.Copy, reads=[pC], writes=[Usb])
                yield
                for h in range(8):
                    mm(pC[0:64, hc(h)], BKTc[:, hc(h)], Usb[:, hc(h)], True, False, reads=[BKTc, Usb], writes=[pC])
                    mm(pC[0:64, hc(h)], BKTc[:, 512 + h * 64:512 + (h + 1) * 64], VTc[:, hc(h)], False, True, reads=[BKTc, VTc], writes=[pC])
                for h in range(8):
                    fb, j = divmod(h, 2)
                    mm(pD[0:64, hc(h)], rT[:, fb, j, cs], STb[:, h, :], True, False, reads=[rT, STb], writes=[pD])
                    mm(pD[0:64, hc(h)], Arbc[:, hc(h)], Usb[:, hc(h)], False, False, reads=[Arbc, Usb], writes=[pD])
                    mm(pD[0:64, hc(h)], Arkc[:, hc(h)], VTc[:, hc(h)], False, True, reads=[Arkc, VTc], writes=[pD])
                vtt(STt[:], ST[:], gC[pb][:].rearrange("p f j c -> p (f j) c")[:, :, c:c + 1].to_broadcast([64, 8, 64]), ALU.mult,
                    reads=[ST, gC[pb]], writes=[STt])
                vtt(ST[:], STt[:], h3(pC[0:64, :]), ALU.add, reads=[STt, pC], writes=[ST])
                act(STb[:], ST[:], AF.Copy, reads=[ST], writes=[STb])
                yield
                y3 = h3(pD[0:64, :])
                vred(m8[:], y3, reads=[pD], writes=[m8])
                vts(m8[:], m8[:], 1.0 / 64, None, ALU.mult, None, reads=[m8], writes=[m8])
                vtt(h3(yc[:]), y3, m8[:].unsqueeze(2).to_broadcast([64, 8, 64]), ALU.subtract, reads=[pD, m8], writes=[yc])
                act(ysq[:], yc[:], AF.Square, reads=[yc], writes=[ysq])
                vred(v8[:], h3(ysq[:]), reads=[ysq], writes=[v8])
                rsqrt_act(r8[:], v8[:], 1.0 / 64, 1, 64, reads=[v8], writes=[r8])
                vtt(h3(yc[:]), h3(yc[:]), r8[:].unsqueeze(2).to_broadcast([64, 8, 64]), ALU.mult, reads=[yc, r8], writes=[yc], eng="gpsimd")
                yield
                for fb in range(4):
                    tr(pD[:, fb * 64:(fb + 1) * 64], yc[:, fb * 128:(fb + 1) * 128], ident[0:64, 0:64], reads=[yc, cst], writes=[pD])
                for fb in range(4):
                    vts(o1[:, fb, :], pD[:, fb * 64:(fb + 1) * 64], pvec[:, PV_GW + fb:PV_GW + fb + 1], pvec[:, PV_GB + fb:PV_GB + fb + 1],
                        ALU.mult, ALU.add, reads=[pD, pvec], writes=[o1])
                vtt(o1[:], o1[:], bon[pb][:, :, cs], ALU.add, reads=[o1, bon[pb]], writes=[o1], eng="gpsimd")
                vtt(oT[pb][:, 0:4, cs], o1[:], gsil[pb][:, :, cs], ALU.mult, reads=[o1, gsil[pb]], writes=[oT[pb]])
                yield
            x_t = xt[pb]
            for half in range(2):
                bank = pD if half == 0 else pC
                for fc in range(8):
                    mm(bank[:, :], oT[pb][:, fc, :], woutb[:, fc, half * 512:(half + 1) * 512], fc == 0, fc == 7, reads=[oT[pb], woutb], writes=[bank])
                vtt(x_t[:, half * 512:(half + 1) * 512], bank[:, :], x_t[:, half * 512:(half + 1) * 512], ALU.add, reads=[bank, x_t], writes=[x_t])
                yield
            act(yo[:], x_t[:], AF.Square, reads=[x_t], writes=[yo, ssum], accum=ssum[:])
            rsqrt_act(rstd[:], ssum[:], 1.0 / D, 0, 128, reads=[ssum], writes=[rstd])
            vstt(yo[:], x_t[:], rstd[:, 0:1], normf[:], ALU.mult, ALU.mult, reads=[x_t, rstd, normf], writes=[yo])
            S.dma("sync", yp[t0:t0 + TB, :], yo[:], reads=[yo], writes=[yp])
            yield

        def run_all(g):
            n = 0
            for _ in g:
                n += 1
            return n

        def interleave(ga, na, gb, nb):
            ia = ib = 0
            da = db = False
            while not (da and db):
                pick_a = (not da) and (db or (ia * nb <= ib * na))
                if pick_a:
                    try:
                        next(ga); ia += 1
                    except StopIteration:
                        da = True
                else:
                    try:
                        next(gb); ib += 1
                    except StopIteration:
                        db = True
            return ia, ib

        run_all(front(0))

        def record_units(g):
            units = []
            S.rec = []
            for _ in g:
                if S.rec:
                    units.append(S.rec)
                S.rec = []
            if S.rec:
                units.append(S.rec)
            S.rec = None
            return units

        A, B = [], []
        for tb in range(NTB):
            A.append(record_units(chain(tb)))
            if tb + 1 < NTB:
                B.append(record_units(front(tb + 1)))
        S.merge_emit(A, B, a_ok=lambda ia, ib: ib >= ia, b_ok=lambda ib, ia: ia >= ib)
        for h in range(8):
            tr(pA[0:64, h * 64:(h + 1) * 64], ST[:, h, :], ident[0:64, 0:64], reads=[ST, cst], writes=[pA])
        vcopy(SvT[:].rearrange("p h k -> p (h k)"), pA[0:64, :], reads=[pA], writes=[SvT])
        S.dma("sync", nwp[:].rearrange("h v k -> v h k"), SvT[:], reads=[SvT], writes=[nwp])
        S.finish([yp, ys, nsp, nwp, npp, nss, nws, nps], engname="sync")
        S.barrier()
    es_top.close()
    return nc, S


_CACHE = {}


def _consts():
    cst = np.zeros((128, C_END), np.float32)
    cst[:, C_ID:C_ID + 128] = np.eye(128, dtype=np.float32)
    ob = np.zeros((128, 128), np.float32)
    ob[0:64, 0:64] = 1.0
    ob[64:128, 64:128] = 1.0
    cst[:, C_ONES:C_ONES + 128] = ob
    s = np.arange(64)[:, None]
    t = np.arange(64)[None, :]
    mus = (s < t).astype(np.float32)
    mui = (s <= t).astype(np.float32)
    mls = (s > t).astype(np.float32)
    i64 = np.eye(64, dtype=np.float32)
    cst[0:64, C_MUS:C_MUS + 64] = mus
    cst[0:64, C_MUI:C_MUI + 64] = mui
    cst[0:64, C_MLS:C_MLS + 64] = mls
    rst = np.ones((512,), np.float32)
    rst[::CH] = 0.0
    cst[:, C_RST:C_RST + 512] = rst[None, :]
    for g, w in enumerate(WINS):
        pos = np.arange(16)
        cst[:, C_ICNT + g * 16:C_ICNT + (g + 1) * 16] = (1.0 / np.minimum(pos + 1, w)).astype(np.float32)[None, :]
    return cst


def kernel(x_prompt, x_sample, state_shift, state_wkv, state_pool, norm_w, w_in, mu_shift,
           w_decay_b, w0, w_aaa_b, a0, k_k, k_a, r_k, gn_w, gn_b, pool_w, pool_scale, w_out, norm_f):
    f = lambda a: np.ascontiguousarray(np.asarray(a, dtype=np.float32))
    x_prompt, x_sample, state_shift, state_wkv, state_pool = map(f, (x_prompt, x_sample, state_shift, state_wkv, state_pool))
    if "nc" not in _CACHE:
        _CACHE["nc"] = build_program()
    nc, S = _CACHE["nc"]

    def colmajor(v, n):
        return f(v).reshape(n, 128).T

    pvec = np.concatenate([
        colmajor(norm_w[0], 8), colmajor(mu_shift[0], 13), colmajor(w0[0], 4), colmajor(a0[0], 4), colmajor(k_k[0], 4),
        colmajor(k_a[0], 4), colmajor(f(r_k[0]).reshape(-1), 4), colmajor(gn_w[0], 4), colmajor(gn_b[0], 4), colmajor(pool_scale[0], 4)], axis=1)
    pvec = f(pvec)
    browA = f(f(mu_shift[0])[None, :])
    browB = f(np.concatenate([f(w0[0]), f(a0[0]), f(k_k[0]), f(k_a[0]), f(r_k[0]).reshape(-1), f(gn_w[0]), f(gn_b[0])])[None, :])
    cst = _consts()
    shared = {
        "w_in": f(w_in[0]), "w_out": f(w_out[0]), "wdec": f(w_decay_b[0]), "waaa": f(w_aaa_b[0]), "poolw": f(pool_w[0]),
        "pvec": pvec, "browA": browA, "browB": browB, "normf": f(norm_f)[None, :], "cst": cst,
    }
    in_maps = []
    for c in range(NCORE):
        bs = slice(c * DB, (c + 1) * DB)
        m = dict(shared)
        m["xp"] = x_prompt[c]
        m["xs"] = f(x_sample[bs].transpose(1, 0, 2).reshape(NS, D))
        m["sshift"] = state_shift[0, bs]
        m["swkv"] = f(state_wkv[0, bs].reshape(128, 4096))
        m["spool"] = f(state_pool[0, bs].reshape(DB * 15, 512))
        in_maps.append(m)
    res = run_bass_kernel_spmd(nc, in_maps, core_ids=list(range(NCORE)))
    R = res.results
    y_prompt = np.stack([R[c]["yp"] for c in range(NCORE)], axis=0)
    y_sample = np.concatenate([R[c]["ys"].reshape(DT, DB, D).transpose(1, 0, 2) for c in range(NCORE)], axis=0)
    nsp = np.stack([R[c]["nsp"].reshape(D_SHIFT) for c in range(NCORE)], axis=0)[None]
    nwp = np.stack([R[c]["nwp"] for c in range(NCORE)], axis=0)[None]
    npp = np.stack([R[c]["npp"] for c in range(NCORE)], axis=0)[None]
    nss = np.concatenate([R[c]["nss"] for c in range(NCORE)], axis=0)[None]
    nws = np.concatenate([R[c]["nws"].reshape(DB, 8, 64, 64) for c in range(NCORE)], axis=0)[None]
    nps = np.concatenate([R[c]["nps"] for c in range(NCORE)], axis=0)[None]
    out = (y_prompt, y_sample, nsp, nwp, npp, nss, nws, nps)
    return tuple(np.ascontiguousarray(o.astype(np.float32)) for o in out)
```

```python
import numpy as np
from contextlib import ExitStack
import concourse.bass as bass
import concourse.mybir as mybir
from concourse.bass_utils import run_bass_kernel_spmd

F32 = mybir.dt.float32
BF16 = mybir.dt.bfloat16
AF = mybir.ActivationFunctionType
ALU = mybir.AluOpType
AX = mybir.AxisListType

D = 1024
SEQ = 2048
NCORE = 8
DB = 16
DT = 4
NS = DB * DT
D_SHIFT = 1664
D_IN = 3200
C0 = float(np.exp(-0.5))
NORM_EPS = 1e-6
GN_EPS = 64e-5
L2_EPS = 1e-12
TB = 128
NTB = SEQ // TB
CH = 64
FBIAS = 0.0
NCH = TB // CH
WINS = (2, 4, 8, 16)

C_ID, C_ONES, C_MUS, C_MUI, C_MLS, C_RST, C_ICNT, C_END = 0, 128, 256, 320, 384, 448, 960, 1024
PV_NW, PV_MU, PV_W0, PV_A0, PV_KK, PV_KA, PV_RK, PV_GW, PV_GB, PV_PS, PV_END = 0, 8, 21, 25, 29, 33, 37, 41, 45, 49, 53
BRB_W0, BRB_A0, BRB_KK, BRB_KA, BRB_RK, BRB_GW, BRB_GB = 0, 512, 1024, 1536, 2048, 2560, 3072


class Buf:
    __slots__ = ("name", "w", "r")

    def __init__(self, name):
        self.name = name
        self.w = None
        self.r = []


class T:
    def __init__(self, t, name, buf=None):
        self.t = t
        self.b = buf if buf is not None else Buf(name)

    def __getitem__(self, k):
        return self.t[k]


class Sched:
    def __init__(self, nc, n_dma_sems=32):
        self.nc = nc
        self.eng = {}
        for name in ["tensor", "vector", "scalar", "gpsimd", "sync"]:
            h = getattr(nc, name)
            sem = nc.alloc_semaphore(name="prog_" + name)
            self.eng[name] = dict(h=h, sem=sem, cnt=0, waited={})
        self.dma_sems = [dict(sem=nc.alloc_semaphore(name=f"dma{i}"), cnt=0) for i in range(n_dma_sems)]
        self.dma_rr = 0
        self.ninstr = 0
        self.rec = None

    def _wait(self, engname, tok):
        sem, val, src = tok
        e = self.eng[engname]
        key = id(sem)
        if e["waited"].get(key, 0) >= val:
            return
        e["h"].wait_ge(sem, val)
        e["waited"][key] = val
        self.ninstr += 1

    def _deps(self, engname, reads, writes):
        toks = []
        for b in reads:
            if b.w is not None:
                toks.append(b.w)
        for b in writes:
            if b.w is not None:
                toks.append(b.w)
            toks.extend(b.r)
        for tok in toks:
            if tok[2] == engname and engname == "tensor":
                continue
            self._wait(engname, tok)

    @staticmethod
    def _bufs(xs):
        return [x.b if isinstance(x, T) else x for x in xs]

    def _record(self, tok, reads, writes):
        for b in reads:
            b.r.append(tok)
            if len(b.r) > 64:
                b.r = b.r[-64:] if False else b.r
        for b in writes:
            b.w = tok
            b.r = []

    def op(self, engname, fn, reads=(), writes=(), cost=0.3):
        reads = self._bufs(reads)
        writes = self._bufs(writes)
        if self.rec is not None:
            self.rec.append(("op", engname, fn, reads, writes, cost, None))
            return None
        e = self.eng[engname]
        self._deps(engname, reads, writes)
        ins = fn(e["h"])
        e["cnt"] += 1
        ins.then_inc(e["sem"], 1)
        e["waited"][id(e["sem"])] = max(e["waited"].get(id(e["sem"]), 0), 0)
        tok = (e["sem"], e["cnt"], engname)
        self._record(tok, reads, writes)
        self.ninstr += 1
        return tok

    def dma(self, qname, out, in_, reads=(), writes=(), **kw):
        reads = self._bufs(reads)
        writes = self._bufs(writes)
        if self.rec is not None:
            self.rec.append(("dma", qname, (out, in_), reads, writes, 2.5, kw))
            return None
        e = self.eng[qname]
        self._deps(qname, reads, writes)
        d = self.dma_sems[self.dma_rr]
        self.dma_rr = (self.dma_rr + 1) % len(self.dma_sems)
        if d["cnt"] > 0:
            self._wait(qname, (d["sem"], 16 * d["cnt"], "dma"))
        ins = e["h"].dma_start(out=out, in_=in_, **kw)
        d["cnt"] += 1
        ins.then_inc(d["sem"], 16)
        tok = (d["sem"], 16 * d["cnt"], "dma")
        self._record(tok, reads, writes)
        self.ninstr += 1
        return tok

    def emit(self, r):
        kind, eng, fn, reads, writes, cost, kw = r
        if kind == "op":
            self.op(eng, fn, reads=reads, writes=writes)
        else:
            self.dma(eng, fn[0], fn[1], reads=reads, writes=writes, **kw)

    def merge_emit(self, A, B, a_ok, b_ok):
        eng_free = {}
        ready = {}
        acc = {}

        def est(r):
            kind, eng, fn, reads, writes, cost, kw = r
            t = eng_free.get(eng, 0.0)
            for b in reads:
                rt_, re_ = ready.get(id(b), (0.0, eng))
                t = max(t, rt_ + (0.15 if re_ != eng else 0.0))
            for b in writes:
                rt_, re_ = ready.get(id(b), (0.0, eng))
                t = max(t, rt_ + (0.15 if re_ != eng else 0.0), acc.get(id(b), 0.0) + 0.1)
            return t

        def commit(r, t):
            kind, eng, fn, reads, writes, cost, kw = r
            if kind == "dma":
                eng_free[eng] = t + 0.1
                end = t + cost
            else:
                end = t + cost
                eng_free[eng] = end
            for b in reads:
                acc[id(b)] = max(acc.get(id(b), 0.0), end)
            for b in writes:
                ready[id(b)] = (end, eng)
                acc[id(b)] = max(acc.get(id(b), 0.0), end)

        def run_unit(u):
            for r in u:
                commit(r, est(r))
                self.emit(r)

        ia = ib = 0
        ja = jb = 0
        while ia < len(A) or ib < len(B):
            ca = None
            cb = None
            if ia < len(A) and (ja > 0 or a_ok(ia, ib)):
                ca = A[ia][ja]
            if ib < len(B) and (jb > 0 or b_ok(ib, ia)):
                cb = B[ib][jb]
            assert ca is not None or cb is not None, (ia, ib, ja, jb)
            ta = est(ca[0]) if ca is not None else None
            tb_ = est(cb[0]) if cb is not None else None
            if cb is None or (ca is not None and ta + FBIAS < tb_):
                run_unit(ca); ja += 1
                if ja == len(A[ia]):
                    ia += 1; ja = 0
            else:
                run_unit(cb); jb += 1
                if jb == len(B[ib]):
                    ib += 1; jb = 0

    def barrier(self):
        toks = [(e["sem"], e["cnt"], n) for n, e in self.eng.items() if e["cnt"] > 0]
        toks += [(d["sem"], 16 * d["cnt"], "dma") for d in self.dma_sems if d["cnt"] > 0]
        for n in self.eng:
            for tok in toks:
                if tok[2] == n:
                    continue
                self._wait(n, tok)

    def finish(self, tiles, engname="sync"):
        for b in self._bufs(tiles):
            if b.w is not None:
                self._wait(engname, b.w)


class _Stop(Exception):
    pass


def build_program(stop=None):
    nc = bass.Bass("TRN2", target_bir_lowering=False)
    S = Sched(nc)
    try:
        _build_body(nc, S, stop)
    except _Stop:
        S.barrier()
    return nc, S


def _build_body(nc, S, stop):
    def chk(label):
        if stop == label:
            raise _Stop()


    def din(name, shape):
        return nc.dram_tensor(name, list(shape), F32, kind="ExternalInput").ap()

    def dout(name, shape):
        return T(nc.dram_tensor(name, list(shape), F32, kind="ExternalOutput").ap(), name)

    xp = din("xp", [SEQ, D])
    xs = din("xs", [NS, D])
    sshift = din("sshift", [DB, D_SHIFT])
    swkv = din("swkv", [128, 4096])
    spool = din("spool", [DB * 15, 512])
    w_in = din("w_in", [D, D_IN])
    w_out = din("w_out", [D, D])
    wdec = din("wdec", [64, 512])
    waaa = din("waaa", [64, 512])
    poolw = din("poolw", [4, 128, 128])
    pvec_d = din("pvec", [128, PV_END])
    browA_d = din("browA", [1, D_SHIFT])
    browB_d = din("browB", [1, 3584])
    normf_d = din("normf", [1, D])
    cst_d = din("cst", [128, C_END])

    yp = dout("yp", [SEQ, D])
    ys = dout("ys", [NS, D])
    nsp = dout("nsp", [13, 128])
    nwp = dout("nwp", [8, 64, 64])
    npp = dout("npp", [15, 512])
    nss = dout("nss", [DB, D_SHIFT])
    nws = dout("nws", [128, 4096])
    nps = dout("nps", [DB, 15, 512])
    scr1 = T(nc.dram_tensor("scr1", [6, DT, DB, 8, 64], F32, kind="Internal").ap(), "scr1")
    scr2 = T(nc.dram_tensor("scr2", [DB, 8, DT, 64], F32, kind="Internal").ap(), "scr2")

    es_top = ExitStack()

    def sb(es, name, shape, dt=F32):
        return T(es.enter_context(nc.sbuf_tensor("s_" + name, list(shape), dt)), name)

    def pst(name, shape, dt=F32):
        return T(nc.alloc_psum_tensor("p_" + name, list(shape), dt), name)

    def nel(ap):
        n = 1
        for s_ in ap.shape[1:]:
            n *= s_
        return n

    def mm(out, lhsT, rhs, start, stop, reads, writes):
        passes = 4 if lhsT.dtype == F32 else 1
        c_ = max(0.055, nel(rhs) * passes / 2000.0 + 0.03)
        S.op("tensor", lambda e: e.matmul(out, lhsT=lhsT, rhs=rhs, start=start, stop=stop), reads=reads, writes=writes, cost=c_)

    def tr(out, in_, ident, reads, writes):
        S.op("tensor", lambda e: e.transpose(out, in_, ident), reads=reads, writes=writes, cost=0.13)

    def act(out, in_, func, reads, writes, bias=None, scale=None, eng="scalar", accum=None):
        kw = {}
        if accum is not None:
            kw["accum_out"] = accum
        if bias is not None:
            kw["bias"] = bias
        if scale is not None:
            kw["scale"] = scale
        S.op("scalar", lambda e: e.activation(out=out, in_=in_, func=func, **kw), reads=reads, writes=writes,
             cost=0.1 + 0.1 * len(kw) + nel(in_) * 0.00095)

    def ecost(eng, n):
        return 0.08 + n * (0.00105 if eng == "vector" else 0.0025)

    def vtt(out, in0, in1, op, reads, writes, eng="vector"):
        S.op(eng, lambda e: e.tensor_tensor(out=out, in0=in0, in1=in1, op=op), reads=reads, writes=writes, cost=ecost(eng, nel(out)))

    def vts(out, in0, s1, s2, op0, op1, reads, writes, eng="vector"):
        if op1 is None:
            S.op(eng, lambda e: e.tensor_scalar(out=out, in0=in0, scalar1=s1, scalar2=None, op0=op0), reads=reads, writes=writes,
                 cost=ecost(eng, nel(out)))
        else:
            S.op(eng, lambda e: e.tensor_scalar(out=out, in0=in0, scalar1=s1, scalar2=s2, op0=op0, op1=op1), reads=reads, writes=writes,
                 cost=ecost(eng, nel(out)))

    def vstt(out, in0, scalar, in1, op0, op1, reads, writes):
        S.op("vector", lambda e: e.scalar_tensor_tensor(out=out, in0=in0, scalar=scalar, in1=in1, op0=op0, op1=op1), reads=reads, writes=writes,
             cost=ecost("vector", nel(out)))

    def vcopy(out, in_, reads, writes, eng="vector"):
        S.op(eng, lambda e: e.tensor_copy(out=out, in_=in_), reads=reads, writes=writes, cost=ecost(eng, nel(out)))

    def vred(out, in_, reads, writes):
        S.op("vector", lambda e: e.tensor_reduce(out=out, in_=in_, axis=AX.X, op=ALU.add), reads=reads, writes=writes,
             cost=ecost("vector", nel(in_)))

    def vrecip(out, in_, reads, writes):
        S.op("vector", lambda e: e.reciprocal(out=out, in_=in_), reads=reads, writes=writes, cost=0.08 + nel(out) * 0.0084)

    def memset(ap, val, writes, eng="gpsimd"):
        S.op(eng, lambda e: e.memset(ap, val), writes=writes)

    def rsqrt_small(out, in_, tmp, scale, eps, reads, writes):
        act(tmp, in_, AF.Sqrt, reads=reads, writes=writes, bias=None, scale=None) if False else None
        vts(tmp, in_, scale, eps, ALU.mult, ALU.add, reads=reads, writes=writes)
        act(tmp, tmp, AF.Sqrt, reads=writes, writes=writes)
        vrecip(out, tmp, reads=writes, writes=writes)

    def rsqrt_act(out, in_, scale, eps_col, n, reads, writes):
        act(out, in_, AF.Ln, reads=list(reads) + [epsT], writes=writes, scale=scale, bias=epsT[0:n, eps_col:eps_col + 1])
        act(out, out, AF.Exp, reads=writes, writes=writes, scale=-0.5)

    pg = [pst(f"pg{i}", [128, 512]) for i in range(2)]
    pT = pst("pT", [128, 1024], BF16)
    pM = pst("pM", [128, 512])
    pA = pst("pA", [128, 512])
    pB = pst("pB", [128, 512])
    pC = pst("pC", [128, 512])
    pD = pst("pD", [128, 512])

    cst = sb(es_top, "cst", [128, C_END])
    pvec = sb(es_top, "pvec", [128, PV_END])
    omu = sb(es_top, "omu", [128, 13])
    omka = sb(es_top, "omka", [128, 4])
    identb = sb(es_top, "identb", [128, 128], BF16)
    winb = sb(es_top, "winb", [128, 8, D_IN], BF16)
    woutb = sb(es_top, "woutb", [128, 8, D], BF16)
    wd = sb(es_top, "wd", [64, 512])
    wa = sb(es_top, "wa", [128, 512])
    pw = sb(es_top, "pw", [128, 4, 128])
    normf = sb(es_top, "normf", [128, D])
    epsT = sb(es_top, "epsT", [128, 4])

    ident = cst[:, C_ID:C_ID + 128]
    onesblk = cst[:, C_ONES:C_ONES + 128]

    S.dma("sync", cst[:], cst_d, writes=[cst])
    S.dma("sync", pvec[:], pvec_d, writes=[pvec])
    S.dma("sync", wd[:], wdec, writes=[wd])
    S.dma("sync", wa[64:128, :], waaa, writes=[wa])
    S.dma("sync", pw[:], poolw.rearrange("g c e -> c g e"), writes=[pw])
    S.dma("sync", normf[:], normf_d.partition_broadcast(128), writes=[normf])
    vcopy(identb[:], ident, reads=[cst], writes=[identb])
    memset(epsT[:, 0:1], NORM_EPS, writes=[epsT])
    memset(epsT[:, 1:2], GN_EPS, writes=[epsT])
    memset(epsT[:, 2:3], L2_EPS, writes=[epsT])
    vts(omka[:], pvec[:, PV_KA:PV_KA + 4], -1.0, 1.0, ALU.mult, ALU.add, reads=[pvec], writes=[omka])

    with ExitStack() as es:
        stg = [sb(es, f"stg{i}", [128, D_IN]) for i in range(3)]
        for dc in range(8):
            st = stg[dc % 3]
            S.dma("sync", st[:], w_in[dc * 128:(dc + 1) * 128, :], writes=[st])
            h = D_IN // 2
            vts(winb[:, dc, 0:h], st[:, 0:h], pvec[:, PV_NW + dc:PV_NW + dc + 1], None, ALU.mult, None, reads=[st, pvec], writes=[winb])
            act(winb[:, dc, h:], st[:, h:], AF.Copy, reads=[st, pvec], writes=[winb], scale=pvec[:, PV_NW + dc:PV_NW + dc + 1])
        S.barrier()
        chk("W")

    def final_tile(es_tiles, n, x_t, oT_list, out_dram_ap, out_T):
        res, sq, ssum, tmp1, rstd, yo = es_tiles
        for half in range(2):
            bank = pD if half == 0 else pC
            for fc in range(8):
                mm(bank[0:n, :], oT_list[fc], woutb[:, fc, half * 512:(half + 1) * 512], fc == 0, fc == 7,
                   reads=[oT_list_T, woutb], writes=[bank])
            vtt(res[0:n, half * 512:(half + 1) * 512], bank[0:n, :], x_t[0:n, half * 512:(half + 1) * 512], ALU.add,
                reads=[bank, x_t], writes=[res])
        act(sq[0:n, :], res[0:n, :], AF.Square, reads=[res], writes=[sq])
        vred(ssum[0:n, :], sq[0:n, :], reads=[sq], writes=[ssum])
        rsqrt_small(rstd[0:n, :], ssum[0:n, :], tmp1[0:n, :], 1.0 / D, NORM_EPS, reads=[ssum], writes=[tmp1, rstd])
        vstt(yo[0:n, :], res[0:n, :], rstd[0:n, 0:1], normf[0:n, :], ALU.mult, ALU.mult, reads=[res, rstd, normf], writes=[yo])
        S.dma("sync", out_dram_ap, yo[0:n, :], reads=[yo], writes=[out_T])

    oT_list_T = None

    with ExitStack() as es:
        browB = sb(es, "browB", [NS, 3584])
        S.dma("sync", browB[:], browB_d.partition_broadcast(NS), writes=[browB])
        x_s = sb(es, "x_s", [NS, D])
        S.dma("sync", x_s[:], xs, writes=[x_s])
        hTs = sb(es, "hTs", [128, 8, DB + NS], BF16)
        graw_s = sb(es, "graw_s", [NS, 512])
        u_s = sb(es, "u_s", [NS, 512])
        gp_s = sb(es, "gp_s", [NS, 512])
        bonus_s = sb(es, "bonus_s", [NS, 512])
        st8 = sb(es, "st8", [NS, 8])
        st8b = sb(es, "st8b", [NS, 8])
        st8c = sb(es, "st8c", [NS, 8])

        def v3(ap):
            return ap.rearrange("p (h k) -> p h k", k=64)

        def bc8(ap8):
            return ap8.unsqueeze(2).to_broadcast([NS, 8, 64])

        with ExitStack() as e1:
            browA = sb(e1, "browA", [NS, D_SHIFT])
            S.dma("sync", browA[:], browA_d.partition_broadcast(NS), writes=[browA])
            omka_b = sb(e1, "omka_b", [NS, 512])
            vts(omka_b[:], browB[:, BRB_KA:BRB_KA + 512], -1.0, 1.0, ALU.mult, ALU.add, reads=[browB], writes=[omka_b])
            sq_s = sb(e1, "sq_s", [NS, D])
            ss_s = sb(e1, "ss_s", [NS, 1])
            t1_s = sb(e1, "t1_s", [NS, 1])
            rstd_s = sb(e1, "rstd_s", [NS, 1])
            xn_s = sb(e1, "xn_s", [NS, D], BF16)
            act(sq_s[:], x_s[:], AF.Square, reads=[x_s], writes=[sq_s])
            vred(ss_s[:], sq_s[:], reads=[sq_s], writes=[ss_s])
            rsqrt_small(rstd_s[:], ss_s[:], t1_s[:], 1.0 / D, NORM_EPS, reads=[ss_s], writes=[t1_s, rstd_s])
            vts(xn_s[:], x_s[:], rstd_s[:, 0:1], None, ALU.mult, None, reads=[x_s, rstd_s], writes=[xn_s])
            memset(hTs[:, :, 0:DB], 0.0, writes=[hTs])
            for dc in range(8):
                tr(pT[:, dc * 128:dc * 128 + NS], xn_s[:, dc * 128:(dc + 1) * 128], identb[0:NS, 0:NS], reads=[xn_s, identb], writes=[pT])
            vcopy(hTs[:, :, DB:DB + NS], pT[:].rearrange("p (c t) -> p c t", t=128)[:, :, 0:NS], reads=[pT], writes=[hTs])

            p_s = sb(e1, "p_s", [NS, D_SHIFT])
            prev_s = sb(e1, "prev_s", [NS, D_SHIFT])
            col_chunks = [(0, 512), (512, 512), (1024, 512), (1536, 128)]
            kk_ = 0
            for (c0, n) in col_chunks:
                bank = pg[kk_ % 2]; kk_ += 1
                for dc in range(8):
                    mm(bank[0:NS, 0:n], hTs[:, dc, DB:DB + NS], winb[:, dc, c0:c0 + n], dc == 0, dc == 7, reads=[hTs, winb], writes=[bank])
                act(p_s[:, c0:c0 + n], bank[0:NS, 0:n], AF.Copy, reads=[bank], writes=[p_s])
                bank = pg[kk_ % 2]; kk_ += 1
                for dc in range(8):
                    mm(bank[0:NS, 0:n], hTs[:, dc, 0:NS], winb[:, dc, c0:c0 + n], dc == 0, dc == 7, reads=[hTs, winb], writes=[bank])
                vcopy(prev_s[:, c0:c0 + n], bank[0:NS, 0:n], reads=[bank], writes=[prev_s])
            for (c0, dst, fn) in [(1664, graw_s, AF.Silu), (2176, u_s, AF.Copy), (2688, gp_s, AF.Silu)]:
                bank = pg[kk_ % 2]; kk_ += 1
                for dc in range(8):
                    mm(bank[0:NS, :], hTs[:, dc, DB:DB + NS], winb[:, dc, c0:c0 + 512], dc == 0, dc == 7, reads=[hTs, winb], writes=[bank])
                act(dst[:], bank[0:NS, :], fn, reads=[bank], writes=[dst])
            S.dma("sync", prev_s[0:DB, :], sshift, writes=[prev_s])
            S.dma("sync", nss[:], p_s[NS - DB:NS, :], reads=[p_s], writes=[nss])
            S.dma("sync", nps[:, 0:11, :], spool.rearrange("(b j) c -> b j c", j=15)[:, 4:15, :], writes=[nps])
            for t in range(DT):
                S.dma("sync", nps[:, 11 + t, :], u_s[t * DB:(t + 1) * DB, :], reads=[u_s], writes=[nps])

            vtt(prev_s[:], prev_s[:], p_s[:], ALU.subtract, reads=[prev_s, p_s], writes=[prev_s])
            vtt(prev_s[:], prev_s[:], browA[:], ALU.mult, reads=[prev_s, browA], writes=[prev_s])
            vtt(prev_s[:], prev_s[:], p_s[:], ALU.add, reads=[prev_s, p_s], writes=[prev_s])
            ps_s = prev_s
            r_s = ps_s[:, 0:512]
            k_s = ps_s[:, 512:1024]
            v_s = ps_s[:, 1024:1536]

            lT = sb(e1, "lT", [128, NS])
            tr(pM[:, 0:NS], ps_s[:, 1536:1664], ident[0:NS, 0:NS], reads=[ps_s, cst], writes=[pM])
            act(lT[0:64, :], pM[0:64, 0:NS], AF.Tanh, reads=[pM], writes=[lT])
            act(lT[64:128, :], pM[64:128, 0:NS], AF.Copy, reads=[pM], writes=[lT])
            sg_s = sb(e1, "sg_s", [NS, 512])
            a_s = sb(e1, "a_s", [NS, 512])
            mm(pA[0:NS, :], lT[0:64, :], wd[:, :], True, True, reads=[lT, wd], writes=[pA])
            vtt(sg_s[:], pA[0:NS, :], browB[:, BRB_W0:BRB_W0 + 512], ALU.add, reads=[pA, browB], writes=[sg_s])
            act(sg_s[:], sg_s[:], AF.Sigmoid, reads=[sg_s], writes=[sg_s])
            mm(pB[0:NS, :], lT[64:128, :], wa[64:128, :], True, True, reads=[lT, wa], writes=[pB])
            vtt(a_s[:], pB[0:NS, :], browB[:, BRB_A0:BRB_A0 + 512], ALU.add, reads=[pB, browB], writes=[a_s])
            act(a_s[:], a_s[:], AF.Sigmoid, reads=[a_s], writes=[a_s])

            pk = sb(e1, "pk", [NS, 4, 512])
            PQ = {1: 0, 2: 1, 4: 2, 5: 3}
            tmpA = sb(e1, "tmpA", [NS, 512])
            tmpB = sb(e1, "tmpB", [NS, 512])
            act(pk[:, PQ[1], :], sg_s[:], AF.Exp, reads=[sg_s], writes=[pk], scale=-C0)
            vtt(tmpA[:], k_s, browB[:, BRB_KK:BRB_KK + 512], ALU.mult, reads=[ps_s, browB], writes=[tmpA])
            vtt(tmpB[:], tmpA[:], tmpA[:], ALU.mult, reads=[tmpA], writes=[tmpB])
            vred(st8[:], v3(tmpB[:]), reads=[tmpB], writes=[st8])
            rsqrt_small(st8b[:], st8[:], st8c[:], 1.0, L2_EPS, reads=[st8], writes=[st8c, st8b])
            vtt(v3(tmpA[:]), v3(tmpA[:]), bc8(st8b[:]), ALU.mult, reads=[tmpA, st8b], writes=[tmpA])
            vts(pk[:, PQ[4], :], tmpA[:], -1.0, None, ALU.mult, None, reads=[tmpA], writes=[pk])
            vtt(pk[:, PQ[5], :], tmpA[:], a_s[:], ALU.mult, reads=[tmpA, a_s], writes=[pk])
            vtt(tmpB[:], a_s[:], browB[:, BRB_KA:BRB_KA + 512], ALU.mult, reads=[a_s, browB], writes=[tmpB])
            vtt(tmpB[:], tmpB[:], omka_b[:], ALU.add, reads=[tmpB, omka_b], writes=[tmpB])
            vtt(pk[:, PQ[2], :], k_s, tmpB[:], ALU.mult, reads=[ps_s, tmpB], writes=[pk])
            vtt(tmpB[:], r_s, browB[:, BRB_RK:BRB_RK + 512], ALU.mult, reads=[ps_s, browB], writes=[tmpB])
            vtt(tmpB[:], tmpB[:], pk[:, PQ[2], :], ALU.mult, reads=[tmpB, pk], writes=[tmpB])
            vred(st8[:], v3(tmpB[:]), reads=[tmpB], writes=[st8])
            vtt(v3(bonus_s[:]), v3(v_s), bc8(st8[:]), ALU.mult, reads=[ps_s, st8], writes=[bonus_s])
            sview = scr1[:].rearrange("q t b h k -> q (t b) (h k)")
            S.dma("sync", sview[0], r_s, reads=[ps_s], writes=[scr1])
            S.dma("sync", sview[3], v_s, reads=[ps_s], writes=[scr1])
            for qq, slot in PQ.items():
                S.dma("sync", sview[qq], pk[:, slot, :], reads=[pk], writes=[scr1])
            S.finish([scr1], engname="sync")
            S.barrier()
            chk("S1")

        with ExitStack() as e2:
            sIn = sb(e2, "sIn", [128, 6, DT, 64])
            S.dma("sync", sIn[:], scr1[:].rearrange("q t b h k -> (b h) q t k"), reads=[scr1], writes=[sIn])
            St = sb(e2, "St", [128, 64, 64])
            S.dma("sync", St[:].rearrange("p v k -> p (v k)"), swkv, writes=[St])
            tmpS = sb(e2, "tmpS", [128, 64, 64])
            sa = sb(e2, "sa", [128, 64])
            yS = sb(e2, "yS", [128, DT, 64])
            stgo = [sb(e2, f"stgo{i}", [128, D]) for i in range(3)]
            for fc in range(8):
                so = stgo[fc % 3]
                S.dma("sync", so[:], w_out[fc * 128:(fc + 1) * 128, :], writes=[so])
                act(woutb[:, fc, :], so[:], AF.Copy, reads=[so], writes=[woutb])

            def bv(ap):
                return ap.unsqueeze(1).to_broadcast([128, 64, 64])

            def bk(ap):
                return ap.unsqueeze(2).to_broadcast([128, 64, 64])

            for t in range(DT):
                q = lambda i: sIn[:, i, t, :]
                vtt(tmpS[:], St[:], bv(q(4)), ALU.mult, reads=[St, sIn], writes=[tmpS])
                vred(sa[:], tmpS[:], reads=[tmpS], writes=[sa])
                vtt(St[:], St[:], bv(q(1)), ALU.mult, reads=[St, sIn], writes=[St])
                vtt(tmpS[:], bk(sa[:]), bv(q(5)), ALU.mult, reads=[sa, sIn], writes=[tmpS])
                vtt(St[:], St[:], tmpS[:], ALU.add, reads=[St, tmpS], writes=[St])
                vtt(tmpS[:], bk(q(3)), bv(q(2)), ALU.mult, reads=[sIn], writes=[tmpS])
                vtt(St[:], St[:], tmpS[:], ALU.add, reads=[St, tmpS], writes=[St])
                vtt(tmpS[:], St[:], bv(q(0)), ALU.mult, reads=[St, sIn], writes=[tmpS])
                vred(yS[:, t, :], tmpS[:], reads=[tmpS], writes=[yS])
            S.dma("sync", nws[:], St[:].rearrange("p v k -> p (v k)"), reads=[St], writes=[nws])
            S.dma("sync", scr2[:].rearrange("b h t v -> (b h) t v"), yS[:], reads=[yS], writes=[scr2])
            S.finish([scr2, nws], engname="sync")
            S.barrier()
            chk("S2")

        with ExitStack() as e3:
            yT = sb(e3, "yT", [NS, 512])
            tmpA = sb(e3, "tmpA3", [NS, 512])
            for t in range(DT):
                S.dma("sync", yT[t * DB:(t + 1) * DB, :].rearrange("b (h v) -> b h v", v=64), scr2[:][:, :, t, :], reads=[scr2], writes=[yT])
            vred(st8[:], v3(yT[:]), reads=[yT], writes=[st8])
            vts(st8[:], st8[:], 1.0 / 64, None, ALU.mult, None, reads=[st8], writes=[st8])
            vtt(v3(yT[:]), v3(yT[:]), bc8(st8[:]), ALU.subtract, reads=[yT, st8], writes=[yT])
            vtt(tmpA[:], yT[:], yT[:], ALU.mult, reads=[yT], writes=[tmpA])
            vred(st8[:], v3(tmpA[:]), reads=[tmpA], writes=[st8])
            rsqrt_small(st8b[:], st8[:], st8c[:], 1.0 / 64, GN_EPS, reads=[st8], writes=[st8c, st8b])
            vtt(v3(yT[:]), v3(yT[:]), bc8(st8b[:]), ALU.mult, reads=[yT, st8b], writes=[yT])
            vtt(yT[:], yT[:], browB[:, BRB_GW:BRB_GW + 512], ALU.mult, reads=[yT, browB], writes=[yT])
            vtt(yT[:], yT[:], browB[:, BRB_GB:BRB_GB + 512], ALU.add, reads=[yT, browB], writes=[yT])
            vtt(yT[:], yT[:], bonus_s[:], ALU.add, reads=[yT, bonus_s], writes=[yT])
            vtt(yT[:], yT[:], graw_s[:], ALU.mult, reads=[yT, graw_s], writes=[yT])
            oTs = sb(e3, "oTs", [128, 8, NS], BF16)
            for fb in range(4):
                tr(pA[:, fb * 64:fb * 64 + NS], yT[:, fb * 128:(fb + 1) * 128], ident[0:NS, 0:NS], reads=[yT, cst], writes=[pA])
            vcopy(oTs[:, 0:4, :], pA[:, 0:4 * NS].rearrange("p (f t) -> p f t", t=NS), reads=[pA], writes=[oTs])

            uext = sb(e3, "uext_s", [128, 4, DB, 19])
            sp0 = sb(e3, "sp0", [120, 512])
            sp1 = sb(e3, "sp1", [120, 512])
            S.dma("sync", sp0[:], spool[0:120, :], writes=[sp0])
            S.dma("sync", sp1[:], spool[120:240, :], writes=[sp1])
            for g in range(4):
                tr(pB[:, 0:120], sp0[:, g * 128:(g + 1) * 128], ident[0:120, 0:120], reads=[sp0, cst], writes=[pB])
                tr(pB[:, 128:248], sp1[:, g * 128:(g + 1) * 128], ident[0:120, 0:120], reads=[sp1, cst], writes=[pB])
                vcopy(uext[:, g, 0:8, 0:15], pB[:, 0:120].rearrange("p (b j) -> p b j", j=15), reads=[pB], writes=[uext])
                vcopy(uext[:, g, 8:16, 0:15], pB[:, 128:248].rearrange("p (b j) -> p b j", j=15), reads=[pB], writes=[uext])
                tr(pM[:, 0:NS], u_s[:, g * 128:(g + 1) * 128], ident[0:NS, 0:NS], reads=[u_s, cst], writes=[pM])
                vcopy(uext[:, g, :, 15:19], pM[:, 0:NS].rearrange("p (t b) -> p b t", b=DB), reads=[pM], writes=[uext])
            s2 = sb(e3, "s2_s", [128, 4, DB, 19])
            s4 = sb(e3, "s4_s", [128, 3, DB, 19])
            s8 = sb(e3, "s8_s", [128, 2, DB, 19])
            s16 = sb(e3, "s16_s", [128, 1, DB, 19])
            d_s = sb(e3, "d_s", [128, 4, DT, DB])
            vtt(s2[:, :, :, 1:19], uext[:, :, :, 1:19], uext[:, :, :, 0:18], ALU.add, reads=[uext], writes=[s2])
            vtt(s4[:, :, :, 3:19], s2[:, 1:4, :, 3:19], s2[:, 1:4, :, 1:17], ALU.add, reads=[s2], writes=[s4])
            vtt(s8[:, :, :, 7:19], s4[:, 1:3, :, 7:19], s4[:, 1:3, :, 3:15], ALU.add, reads=[s4], writes=[s8])
            vtt(s16[:, :, :, 15:19], s8[:, 1:2, :, 15:19], s8[:, 1:2, :, 7:11], ALU.add, reads=[s8], writes=[s16])
            tots = [(s2, 0), (s4, 1), (s8, 2), (s16, 3)]
            for g in range(4):
                tt, off = tots[g]
                vstt(d_s[:, g, :, :].rearrange("p t b -> p b t"), tt[:, g - off, :, 15:19], 1.0 / WINS[g], uext[:, g, :, 15:19],
                     ALU.mult, ALU.subtract, reads=[tt, uext], writes=[d_s])
            gpT = sb(e3, "gpT", [128, 4, NS])
            for g in range(4):
                tr(pM[:, 64 + g * 64:64 + g * 64 + NS], gp_s[:, g * 128:(g + 1) * 128], ident[0:NS, 0:NS], reads=[gp_s, cst], writes=[pM])
            vcopy(gpT[:], pM[:, 64:64 + 4 * NS].rearrange("p (g t) -> p g t", t=NS), reads=[pM], writes=[gpT])
            for g in range(4):
                mm(pA[:, g * 64:g * 64 + NS], pw[:, g, :], d_s[:, g, :, :].rearrange("p t b -> p (t b)"), True, True, reads=[pw, d_s], writes=[pA])
            for g in range(4):
                vstt(oTs[:, 4 + g, :], pA[:, g * 64:g * 64 + NS], pvec[:, PV_PS + g:PV_PS + g + 1], gpT[:, g, :], ALU.mult, ALU.mult,
                     reads=[pA, pvec, gpT], writes=[oTs])

            sq = sb(e3, "sq2_s", [NS, D]); ssum = sb(e3, "ssum_s", [NS, 1])
            tmp1 = sb(e3, "tmp1_s", [NS, 1]); rstd = sb(e3, "rstd2_s", [NS, 1]); yo = sb(e3, "yo_s", [NS, D])
            oT_list_T = oTs
            final_tile((x_s, sq, ssum, tmp1, rstd, yo), NS, x_s, [oTs[:, fc, :] for fc in range(8)], ys[:], ys)
            S.finish([ys, nss, nps], engname="sync")
            S.barrier()
            chk("S3")

    with ExitStack() as es:
        def sbl(name, shape, dt=F32, n=2):
            return [sb(es, f"{name}_{i}", shape, dt) for i in range(n)]

        xt = sbl("xt", [128, D])
        yo = sb(es, "yo", [128, D])
        ssx = sb(es, "ssx", [128, 1]); t1x = sb(es, "t1x", [128, 1]); rsx = sb(es, "rsx", [128, 1])
        xnb = sb(es, "xnb", [128, D], BF16)
        sqx = xnb
        hT = sb(es, "hT", [128, 8, TB], BF16)
        praw = sbl("praw", [128, 4, TB + 1])
        halo = sb(es, "halo", [128, 13])
        omu = sb(es, "omu2", [128, 13])
        psr = sb(es, "psr", [128, 4, TB]); psk = sb(es, "psk", [128, 4, TB]); psv = sb(es, "psv", [128, 4, TB])
        ps12 = sb(es, "ps12", [128, TB])
        psx = [T(g_[:, i, :], f"psx{gi_}_{i}", buf=g_.b) for gi_, g_ in enumerate([psr, psk, psv]) for i in range(4)] + [ps12]
        sg = sb(es, "sg", [128, 4, TB]); av = sb(es, "av", [128, 4, TB])
        gsil = sbl("gsil", [128, 4, TB], BF16)
        gpsil = sb(es, "gpsil", [128, 4, TB], BF16)
        uext = sb(es, "uext", [128, 4, 15 + TB])
        th = sb(es, "th", [64, TB])
        wbig = [sb(es, f"wbig{i}", [128, 4, TB]) for i in range(4)]
        w1, w2, w3, w4 = wbig
        srot = [T(wbig[i][:].rearrange("p f t -> p (f t)")[:, 0:15 + TB], f"srot{i}", buf=wbig[i].b) for i in range(4)]
        kkn = sb(es, "kkn", [128, 4, TB]); kmod = sb(es, "kmod", [128, 4, TB]); bv_ = sb(es, "bv_", [128, 4, TB])
        cum = sb(es, "cum", [128, 4, TB])
        dpl = T(kkn[:, 0, :], "dpl", buf=kkn.b)
        at = sbl("at", [64, 4, 2, TB], BF16)
        rt = sbl("rt", [64, 4, 2, TB], BF16)
        bt = sb(es, "bt", [64, 4, 2, TB], BF16)
        kt = sb(es, "kt", [64, 4, 2, TB], BF16)
        bh = sb(es, "bh", [128, 4, TB], BF16); kh = sb(es, "kh", [128, 4, TB], BF16); vb = sb(es, "vb", [128, 4, TB], BF16)
        bon = sbl("bon", [128, 4, TB])
        gC = sbl("gC", [64, 4, 2, NCH])
        VT = [[sb(es, f"VT{p}{c}", [64, 512], BF16) for c in range(NCH)] for p in range(2)]
        BKT = [[sb(es, f"BKT{p}{c}", [64, 1024], BF16) for c in range(NCH)] for p in range(2)]
        Aak = [[sb(es, f"Aak{p}{c}", [64, 512], BF16) for c in range(NCH)] for p in range(2)]
        Arb = [[sb(es, f"Arb{p}{c}", [64, 512], BF16) for c in range(NCH)] for p in range(2)]
        Ark = [[sb(es, f"Ark{p}{c}", [64, 512], BF16) for c in range(NCH)] for p in range(2)]
        Minv = [[sb(es, f"Minv{p}{c}", [64, 512], BF16) for c in range(NCH)] for p in range(2)]
        Nsb = [sb(es, f"Nsb{c}", [64, 512], BF16) for c in range(NCH)]
        NTsb = [sb(es, f"NTsb{c}", [64, 512], BF16) for c in range(NCH)]
        Xa0 = [sb(es, f"Xa0{c}", [64, 512], BF16) for c in range(NCH)]
        XTa0 = [sb(es, f"XTa0{c}", [64, 512], BF16) for c in range(NCH)]
        Qtmp = [sb(es, f"Qtmp{c}", [64, 512], BF16) for c in range(NCH)]
        ST = sb(es, "ST", [64, 8, 64]); STb = sb(es, "STb", [64, 8, 64], BF16)
        Wsb = sb(es, "Wsb", [64, 512], BF16); Usb = sb(es, "Usb", [64, 512], BF16)
        yc = sb(es, "yc", [64, 512]); ysq = sb(es, "ysq", [64, 512])
        STt = T(ysq[:].rearrange("p (h v) -> p h v", v=64), "STt", buf=ysq.b)
        m8 = sb(es, "m8", [64, 8]); v8 = sb(es, "v8", [64, 8]); r8 = sb(es, "r8", [64, 8]); t8 = sb(es, "t8", [64, 8])
        o1 = sb(es, "o1", [128, 4, 64])
        oT = sbl("oT", [128, 8, TB], BF16)
        ssum = sb(es, "ssum", [128, 1]); tmp1 = sb(es, "tmp1", [128, 1]); rstd = sb(es, "rstd", [128, 1])
        ppT = T(ysq[0:16, :], "ppT", buf=ysq.b); m13 = sb(es, "m13", [13, 128])
        SvT = T(yc[:].rearrange("p (h k) -> p h k", k=64), "SvT", buf=yc.b)

        memset(halo[:], 0.0, writes=[halo])
        memset(uext[:, :, 0:15], 0.0, writes=[uext])
        memset(ST[:], 0.0, writes=[ST])
        memset(STb[:], 0.0, writes=[STb])
        vts(omu[:], pvec[:, PV_MU:PV_MU + 13], -1.0, 1.0, ALU.mult, ALU.add, reads=[pvec], writes=[omu])

        def b8(ap):
            return ap.unsqueeze(1).to_broadcast([64, 8, 64])

        def h3(ap):
            return ap.rearrange("p (h v) -> p h v", v=64)

        def hc(h):
            return slice(h * 64, (h + 1) * 64)

        maskUs = b8(cst[0:64, C_MUS:C_MUS + 64])
        maskUi = b8(cst[0:64, C_MUI:C_MUI + 64])
        maskLs = b8(cst[0:64, C_MLS:C_MLS + 64])
        ident8 = b8(cst[0:64, C_ID:C_ID + 64])
        rstm = cst[:, C_RST:C_RST + 512]
        st = dict(gk=0, ak=0)
        pT32 = T(pT[:].bitcast(F32), "pT32", buf=pT.b)
        abanks = [pA, pB, pg[0], pg[1], pM, pT32]

        def nextbank():
            b = abanks[st["ak"] % len(abanks)]
            st["ak"] += 1
            return b

        def front(tb):
            pb = tb % 2
            t0 = tb * TB
            x_t = xt[pb]
            S.dma("sync", x_t[:], xp[t0:t0 + TB, :], writes=[x_t])
            act(sqx[:], x_t[:], AF.Square, reads=[x_t], writes=[sqx, ssx], accum=ssx[:])
            rsqrt_act(rsx[:], ssx[:], 1.0 / D, 0, 128, reads=[ssx], writes=[rsx])
            act(xnb[:], x_t[:], AF.Copy, reads=[x_t, rsx], writes=[xnb], scale=rsx[:, 0:1])
            yield
            for dc in range(8):
                tr(pT[:, dc * 128:(dc + 1) * 128], xnb[:, dc * 128:(dc + 1) * 128], identb[:], reads=[xnb, identb], writes=[pT])
            vcopy(hT[:].rearrange("p c t -> p (c t)"), pT[:], reads=[pT], writes=[hT])
            yield

            def gemm_group(ebs):
                bank = pg[st["gk"] % 2]
                st["gk"] += 1
                for i, eb in enumerate(ebs):
                    for dc in range(8):
                        mm(bank[:, i * TB:(i + 1) * TB], winb[:, dc, eb * 128:(eb + 1) * 128], hT[:, dc, :], dc == 0, dc == 7,
                           reads=[winb, hT], writes=[bank])
                return bank

            for gi, ebs in enumerate([[0, 1, 2, 3], [4, 5, 6, 7], [8, 9, 10, 11], [12]]):
                bank = gemm_group(ebs)
                yield
                n = len(ebs)
                pr = praw[gi % 2]
                e0 = ebs[0]
                gT = [psr, psk, psv, ps12][gi]
                dst = gT[:, 0:n, :] if gi < 3 else ps12[:].unsqueeze(1)
                mub = pvec[:, PV_MU + e0:PV_MU + e0 + n].unsqueeze(2).to_broadcast([128, n, TB])
                vcopy(pr[:, 0:n, 0:1], halo[:, e0:e0 + n].unsqueeze(2), reads=[halo], writes=[pr], eng="gpsimd")
                act(pr[:, 0:n, 1:TB + 1], bank[:, 0:n * TB].rearrange("p (e t) -> p e t", t=TB), AF.Copy, reads=[bank], writes=[pr])
                vtt(dst, pr[:, 0:n, 0:TB], pr[:, 0:n, 1:TB + 1], ALU.subtract, reads=[pr], writes=[gT])
                vtt(dst, dst, mub, ALU.mult, reads=[gT, pvec], writes=[gT])
                vtt(dst, dst, pr[:, 0:n, 1:TB + 1], ALU.add, reads=[gT, pr], writes=[gT])
                vcopy(halo[:, e0:e0 + n].unsqueeze(2), pr[:, 0:n, TB:TB + 1], reads=[pr], writes=[halo], eng="gpsimd")
                yield
            bank = gemm_group([13, 14, 15, 16])
            act(gsil[pb][:].rearrange("p f t -> p (f t)"), bank[:, :], AF.Silu, reads=[bank], writes=[gsil[pb]])
            yield
            bank = gemm_group([17, 18, 19, 20])
            act(uext[:, :, 15:15 + TB], bank[:, :].rearrange("p (g t) -> p g t", t=TB), AF.Copy, reads=[bank], writes=[uext])
            yield
            bank = gemm_group([21, 22, 23, 24])
            act(gpsil[:].rearrange("p g t -> p (g t)"), bank[:, :], AF.Silu, reads=[bank], writes=[gpsil])
            yield

            act(th[:], psx[12][0:64, :], AF.Tanh, reads=[psx[12]], writes=[th])
            for fb in range(4):
                mm(pA[:, fb * TB:(fb + 1) * TB], wd[:, fb * 128:(fb + 1) * 128], th[:], True, True, reads=[wd, th], writes=[pA])
            for fb in range(4):
                mm(pM[:, fb * TB:(fb + 1) * TB], wa[64:128, fb * 128:(fb + 1) * 128], psx[12][64:128, :], True, True, reads=[wa, psx[12]], writes=[pM])
            for fb in range(4):
                act(sg[:, fb, :], pA[:, fb * TB:(fb + 1) * TB], AF.Sigmoid, reads=[pA, pvec], writes=[sg], bias=pvec[:, PV_W0 + fb:PV_W0 + fb + 1])
                act(av[:, fb, :], pM[:, fb * TB:(fb + 1) * TB], AF.Sigmoid, reads=[pM, pvec], writes=[av], bias=pvec[:, PV_A0 + fb:PV_A0 + fb + 1])
            yield

            def pb4(col):
                return pvec[:, col:col + 4].unsqueeze(2).to_broadcast([128, 4, TB])

            def f2(t_):
                return t_[:].rearrange("p f t -> p (f t)")

            bomk = omka[:, 0:4].unsqueeze(2).to_broadcast([128, 4, TB])
            c3 = cum[:].rearrange("p f (c t) -> p (f c) t", t=CH)
            p0, p1 = slice(0, 64), slice(64, 128)
            vcopy(vb[:], psv[:], reads=[psv], writes=[vb], eng="gpsimd")
            vtt(w1[:], psk[:], pb4(PV_KK), ALU.mult, reads=[psk, pvec], writes=[w1])
            vtt(w2[:], w1[:], w1[:], ALU.mult, reads=[w1], writes=[w2])
            mm(pM[:, :], onesblk, f2(w2), True, True, reads=[cst, w2], writes=[pM])
            S.op("vector", lambda e: e.tensor_tensor_scan(out=f2(cum), data0=rstm, data1=f2(sg), initial=0.0, op0=ALU.mult, op1=ALU.add),
                 reads=[cst, sg], writes=[cum], cost=1.2)
            act(f2(w2), pM[:, :], AF.Ln, reads=[pM, epsT], writes=[w2], bias=epsT[:, 2:3])
            act(w2[:], w2[:], AF.Exp, reads=[w2], writes=[w2], scale=-0.5)
            vtt(w3[:], av[:], pb4(PV_KA), ALU.mult, reads=[av, pvec], writes=[w3])
            vtt(w3[:], w3[:], bomk, ALU.add, reads=[w3, omka], writes=[w3])
            vtt(kmod[:], psk[:], w3[:], ALU.mult, reads=[psk, w3], writes=[kmod])
            vtt(w4[:], cum[:], sg[:], ALU.subtract, reads=[cum, sg], writes=[w4])
            act(w4[:], w4[:], AF.Exp, reads=[w4], writes=[w4], scale=-C0)
            vtt(w3[:], psr[:], pb4(PV_RK), ALU.mult, reads=[psr, pvec], writes=[w3])
            vtt(w3[:], w3[:], kmod[:], ALU.mult, reads=[w3, kmod], writes=[w3])
            mm(pA[:, :], onesblk, f2(w3), True, True, reads=[cst, w3], writes=[pA])
            yield
            vstt(kkn[:], w1[:], -1.0, w2[:], ALU.mult, ALU.mult, reads=[w1, w2], writes=[kkn])
            vstt(bv_[:], kkn[:], -1.0, av[:], ALU.mult, ALU.mult, reads=[kkn, av], writes=[bv_])
            act(w1[:], cum[:], AF.Exp, reads=[cum], writes=[w1], scale=-C0)
            act(w2[:], cum[:], AF.Exp, reads=[cum], writes=[w2], scale=C0)
            vtt(at[pb][:, :, 1, :], kkn[p1, :, :], w4[p1, :, :], ALU.mult, reads=[kkn, w4], writes=[at[pb]])
            vtt(at[pb][:, :, 0, :], kkn[p0, :, :], w4[p0, :, :], ALU.mult, reads=[kkn, w4], writes=[at[pb]])
            vtt(f2(bon[pb]), pA[:, :], f2(psv), ALU.mult, reads=[pA, psv], writes=[bon[pb]])
            vtt(w3[:].rearrange("p f (c t) -> p (f c) t", t=CH), c3[:, :, CH - 1:CH].to_broadcast([128, 4 * NCH, CH]), c3, ALU.subtract,
                reads=[cum], writes=[w3], eng="gpsimd")
            act(w3[:], w3[:], AF.Exp, reads=[w3], writes=[w3], scale=-C0)
            for j in range(2):
                pp = slice(64 * j, 64 * j + 64)
                act(gC[pb][:, :, j, :], cum[pp, :, :].rearrange("p f (c t) -> p f c t", t=CH)[:, :, :, CH - 1], AF.Exp,
                    reads=[cum], writes=[gC[pb]], scale=-C0)
            yield
            for j in range(2):
                pp = slice(64 * j, 64 * j + 64)
                e_ = "vector"
                vtt(rt[pb][:, :, j, :], psr[pp, :, :], w1[pp, :, :], ALU.mult, reads=[psr, w1], writes=[rt[pb]], eng=e_)
                vtt(kt[:, :, j, :], kmod[pp, :, :], w2[pp, :, :], ALU.mult, reads=[kmod, w2], writes=[kt], eng=e_)
                vtt(bt[:, :, j, :], bv_[pp, :, :], w2[pp, :, :], ALU.mult, reads=[bv_, w2], writes=[bt], eng=e_)
            vtt(bh[:], bv_[:], w3[:], ALU.mult, reads=[bv_, w3], writes=[bh])
            vtt(kh[:], kmod[:], w3[:], ALU.mult, reads=[kmod, w3], writes=[kh], eng="gpsimd")
            yield

            L = 15 + TB
            for g in range(4):
                vtt(srot[0][:, 1:], uext[:, g, 1:], uext[:, g, 0:L - 1], ALU.add, reads=[uext], writes=[srot[0]], eng="gpsimd")
                tot = srot[0]
                if g >= 1:
                    vtt(srot[1][:, 3:], srot[0][:, 3:], srot[0][:, 1:L - 2], ALU.add, reads=[srot[0]], writes=[srot[1]], eng="gpsimd")
                    tot = srot[1]
                if g >= 2:
                    vtt(srot[2][:, 7:], srot[1][:, 7:], srot[1][:, 3:L - 4], ALU.add, reads=[srot[1]], writes=[srot[2]], eng="gpsimd")
                    tot = srot[2]
                if g >= 3:
                    vtt(srot[3][:, 15:], srot[2][:, 15:], srot[2][:, 7:L - 8], ALU.add, reads=[srot[2]], writes=[srot[3]], eng="gpsimd")
                    tot = srot[3]
                vstt(dpl[:], tot[:, 15:], 1.0 / WINS[g], uext[:, g, 15:], ALU.mult, ALU.subtract, reads=[tot, uext], writes=[dpl])
                if tb == 0:
                    vtt(dpl[:, 0:16], tot[:, 15:31], cst[:, C_ICNT + g * 16:C_ICNT + (g + 1) * 16], ALU.mult, reads=[tot, cst], writes=[dpl])
                    vtt(dpl[:, 0:16], dpl[:, 0:16], uext[:, g, 15:31], ALU.subtract, reads=[dpl, uext], writes=[dpl])
                mm(pM[:, 0:TB], pw[:, g, :], dpl[:], True, True, reads=[pw, dpl], writes=[pM])
                vstt(oT[pb][:, 4 + g, :], pM[:, 0:TB], pvec[:, PV_PS + g:PV_PS + g + 1], gpsil[:, g, :], ALU.mult, ALU.mult,
                     reads=[pM, pvec, gpsil], writes=[oT[pb]])
                yield
            if tb == NTB - 1:
                for g in range(4):
                    tr(pA[0:16, g * 128:(g + 1) * 128], uext[:, g, TB - 1:TB + 15], ident, reads=[uext, cst], writes=[pA])
                vcopy(ppT[:], pA[0:16, :], reads=[pA], writes=[ppT])
                S.dma("sync", npp[:], ppT[1:16, :], reads=[ppT], writes=[npp])
                tr(pB[0:13, 0:128], halo[:, 0:13], ident, reads=[halo, cst], writes=[pB])
                vcopy(m13[:], pB[0:13, 0:128], reads=[pB], writes=[m13])
                S.dma("sync", nsp[:], m13[:], reads=[m13], writes=[nsp])
            vcopy(uext[:, :, 0:15], uext[:, :, TB:TB + 15], reads=[uext], writes=[uext], eng="gpsimd")
            yield

            css = [slice(c * CH, (c + 1) * CH) for c in range(NCH)]
            for c in range(NCH):
                for qi, srcl in enumerate([bh, kh]):
                    for fb in range(4):
                        tr(pT[0:64, qi * 512 + fb * 128:qi * 512 + (fb + 1) * 128], srcl[:, fb, css[c]], identb[:], reads=[srcl, identb], writes=[pT])
                vcopy(BKT[pb][c][:], pT[0:64, :], reads=[pT], writes=[BKT[pb][c]])
                for fb in range(4):
                    tr(pT[0:64, fb * 128:(fb + 1) * 128], vb[:, fb, css[c]], identb[:], reads=[vb, identb], writes=[pT])
                act(VT[pb][c][:], pT[0:64, 0:512], AF.Copy, reads=[pT], writes=[VT[pb][c]])
                yield

            def hsl(tl, h, c):
                fb, j = divmod(h, 2)
                return tl[:, fb, j, css[c]]

            for (Lt, Rt, mask, dsts) in [(bt, at[pb], maskUs, Nsb), (at[pb], bt, maskLs, NTsb), (kt, at[pb], maskUs, Aak[pb]),
                                         (bt, rt[pb], maskUi, Arb[pb]), (kt, rt[pb], maskUi, Ark[pb])]:
                banks = []
                for c in range(NCH):
                    bank = nextbank()
                    banks.append(bank)
                    for h in range(8):
                        mm(bank[0:64, hc(h)], hsl(Lt, h, c), hsl(Rt, h, c), True, True, reads=[Lt, Rt], writes=[bank])
                for c in range(NCH):
                    vtt(h3(dsts[c][:]), h3(banks[c][0:64, :]), mask, ALU.mult, reads=[banks[c], cst], writes=[dsts[c]])
                yield
            X = list(Nsb); XT = list(NTsb)
            Q = [Qtmp[c] for c in range(NCH)]
            for c in range(NCH):
                vtt(h3(Q[c][:]), h3(Nsb[c][:]), ident8, ALU.add, reads=[Nsb[c], cst], writes=[Q[c]])
            for lvl in range(5):
                Xn = [(Xa0[c] if lvl % 2 == 0 else Nsb[c]) for c in range(NCH)]
                XTn = [(XTa0[c] if lvl % 2 == 0 else NTsb[c]) for c in range(NCH)]
                Qn = [(Minv[pb][c] if lvl % 2 == 0 else Qtmp[c]) for c in range(NCH)]
                banks = []
                for c in range(NCH):
                    bank = nextbank(); banks.append(bank)
                    for h in range(8):
                        mm(bank[0:64, hc(h)], X[c][:, hc(h)], XT[c][:, hc(h)], True, True, reads=[X[c], XT[c]], writes=[bank])
                for c in range(NCH):
                    act(XTn[c][:], banks[c][0:64, :], AF.Copy, reads=[banks[c]], writes=[XTn[c]])
                yield
                if lvl < 4:
                    banks = []
                    for c in range(NCH):
                        bank = nextbank(); banks.append(bank)
                        for h in range(8):
                            mm(bank[0:64, hc(h)], XT[c][:, hc(h)], X[c][:, hc(h)], True, True, reads=[X[c], XT[c]], writes=[bank])
                    for c in range(NCH):
                        act(Xn[c][:], banks[c][0:64, :], AF.Copy, reads=[banks[c]], writes=[Xn[c]])
                    yield
                banks = []
                for c in range(NCH):
                    bank = nextbank(); banks.append(bank)
                    for h in range(8):
                        mm(bank[0:64, hc(h)], XTn[c][:, hc(h)], Q[c][:, hc(h)], True, True, reads=[XTn[c], Q[c]], writes=[bank])
                for c in range(NCH):
                    vtt(Qn[c][:], banks[c][0:64, :], Q[c][:], ALU.add, reads=[banks[c], Q[c]], writes=[Qn[c]])
                X, XT, Q = Xn, XTn, Qn
                yield

        def chain(tb):
            pb = tb % 2
            t0 = tb * TB
            for c in range(NCH):
                cs = slice(c * CH, (c + 1) * CH)
                aT, rT = at[pb], rt[pb]
                VTc, BKTc, Aakc, Arbc, Arkc, Minvc = VT[pb][c], BKT[pb][c], Aak[pb][c], Arb[pb][c], Ark[pb][c], Minv[pb][c]
                for h in range(8):
                    fb, j = divmod(h, 2)
                    mm(pC[0:64, hc(h)], aT[:, fb, j, cs], STb[:, h, :], True, False, reads=[aT, STb], writes=[pC])
                    mm(pC[0:64, hc(h)], Aakc[:, hc(h)], VTc[:, hc(h)], False, True, reads=[Aakc, VTc], writes=[pC])
                act(Wsb[:], pC[0:64, :], AF.Copy, reads=[pC], writes=[Wsb])
                yield
                for h in range(8):
                    mm(pC[0:64, hc(h)], Minvc[:, hc(h)], Wsb[:, hc(h)], True, True, reads=[Minvc, Wsb], writes=[pC])
                act(Usb[:], pC[0:64, :], AF.Copy, reads=[pC], writes=[Usb])
                yield
                for h in range(8):
                    mm(pC[0:64, hc(h)], BKTc[:, hc(h)], Usb[:, hc(h)], True, False, reads=[BKTc, Usb], writes=[pC])
                    mm(pC[0:64, hc(h)], BKTc[:, 512 + h * 64:512 + (h + 1) * 64], VTc[:, hc(h)], False, True, reads=[BKTc, VTc], writes=[pC])
                for h in range(8):
                    fb, j = divmod(h, 2)
                    mm(pD[0:64, hc(h)], rT[:, fb, j, cs], STb[:, h, :], True, False, reads=[rT, STb], writes=[pD])
                    mm(pD[0:64, hc(h)], Arbc[:, hc(h)], Usb[:, hc(h)], False, False, reads=[Arbc, Usb], writes=[pD])
                    mm(pD[0:64, hc(h)], Arkc[:, hc(h)], VTc[:, hc(h)], False, True, reads=[Arkc, VTc], writes=[pD])
                vtt(STt[:], ST[:], gC[pb][:].rearrange("p f j c -> p (f j) c")[:, :, c:c + 1].to_broadcast([64, 8, 64]), ALU.mult,
                    reads=[ST, gC[pb]], writes=[STt])
                vtt(ST[:], STt[:], h3(pC[0:64, :]), ALU.add, reads=[STt, pC], writes=[ST])
                act(STb[:], ST[:], AF.Copy, reads=[ST], writes=[STb])
                yield
                y3 = h3(pD[0:64, :])
                vred(m8[:], y3, reads=[pD], writes=[m8])
                vts(m8[:], m8[:], 1.0 / 64, None, ALU.mult, None, reads=[m8], writes=[m8])
                vtt(h3(yc[:]), y3, m8[:].unsqueeze(2).to_broadcast([64, 8, 64]), ALU.subtract, reads=[pD, m8], writes=[yc])
                act(ysq[:], yc[:], AF.Square, reads=[yc], writes=[ysq])
                vred(v8[:], h3(ysq[:]), reads=[ysq], writes=[v8])
                rsqrt_act(r8[:], v8[:], 1.0 / 64, 1, 64, reads=[v8], writes=[r8])
                vtt(h3(yc[:]), h3(yc[:]), r8[:].unsqueeze(2).to_broadcast([64, 8, 64]), ALU.mult, reads=[yc, r8], writes=[yc], eng="gpsimd")
                yield
                for fb in range(4):
                    tr(pD[:, fb * 64:(fb + 1) * 64], yc[:, fb * 128:(fb + 1) * 128], ident[0:64, 0:64], reads=[yc, cst], writes=[pD])
                for fb in range(4):
                    vts(o1[:, fb, :], pD[:, fb * 64:(fb + 1) * 64], pvec[:, PV_GW + fb:PV_GW + fb + 1], pvec[:, PV_GB + fb:PV_GB + fb + 1],
                        ALU.mult, ALU.add, reads=[pD, pvec], writes=[o1])
                vtt(o1[:], o1[:], bon[pb][:, :, cs], ALU.add, reads=[o1, bon[pb]], writes=[o1], eng="gpsimd")
                vtt(oT[pb][:, 0:4, cs], o1[:], gsil[pb][:, :, cs], ALU.mult, reads=[o1, gsil[pb]], writes=[oT[pb]])
                yield
            x_t = xt[pb]
            for half in range(2):
                bank = pD if half == 0 else pC
                for fc in range(8):
                    mm(bank[:, :], oT[pb][:, fc, :], woutb[:, fc, half * 512:(half + 1) * 512], fc == 0, fc == 7, reads=[oT[pb], woutb], writes=[bank])
                vtt(x_t[:, half * 512:(half + 1) * 512], bank[:, :], x_t[:, half * 512:(half + 1) * 512], ALU.add, reads=[bank, x_t], writes=[x_t])
                yield
            act(yo[:], x_t[:], AF.Square, reads=[x_t], writes=[yo, ssum], accum=ssum[:])
            rsqrt_act(rstd[:], ssum[:], 1.0 / D, 0, 128, reads=[ssum], writes=[rstd])
            vstt(yo[:], x_t[:], rstd[:, 0:1], normf[:], ALU.mult, ALU.mult, reads=[x_t, rstd, normf], writes=[yo])
            S.dma("sync", yp[t0:t0 + TB, :], yo[:], reads=[yo], writes=[yp])
            yield

        def run_all(g):
            n = 0
            for _ in g:
                n += 1
            return n

        def interleave(ga, na, gb, nb):
            ia = ib = 0
            da = db = False
            while not (da and db):
                pick_a = (not da) and (db or (ia * nb <= ib * na))
                if pick_a:
                    try:
                        next(ga); ia += 1
                    except StopIteration:
                        da = True
                else:
                    try:
                        next(gb); ib += 1
                    except StopIteration:
                        db = True
            return ia, ib

        run_all(front(0))

        def record_units(g):
            units = []
            S.rec = []
            for _ in g:
                if S.rec:
                    units.append(S.rec)
                S.rec = []
            if S.rec:
                units.append(S.rec)
            S.rec = None
            return units

        A, B = [], []
        for tb in range(NTB):
            A.append(record_units(chain(tb)))
            if tb + 1 < NTB:
                B.append(record_units(front(tb + 1)))
        S.merge_emit(A, B, a_ok=lambda ia, ib: ib >= ia, b_ok=lambda ib, ia: ia >= ib)
        for h in range(8):
            tr(pA[0:64, h * 64:(h + 1) * 64], ST[:, h, :], ident[0:64, 0:64], reads=[ST, cst], writes=[pA])
        vcopy(SvT[:].rearrange("p h k -> p (h k)"), pA[0:64, :], reads=[pA], writes=[SvT])
        S.dma("sync", nwp[:].rearrange("h v k -> v h k"), SvT[:], reads=[SvT], writes=[nwp])
        S.finish([yp, ys, nsp, nwp, npp, nss, nws, nps], engname="sync")
        S.barrier()
    es_top.close()
    return nc, S


_CACHE = {}


def _consts():
    cst = np.zeros((128, C_END), np.float32)
    cst[:, C_ID:C_ID + 128] = np.eye(128, dtype=np.float32)
    ob = np.zeros((128, 128), np.float32)
    ob[0:64, 0:64] = 1.0
    ob[64:128, 64:128] = 1.0
    cst[:, C_ONES:C_ONES + 128] = ob
    s = np.arange(64)[:, None]
    t = np.arange(64)[None, :]
    mus = (s < t).astype(np.float32)
    mui = (s <= t).astype(np.float32)
    mls = (s > t).astype(np.float32)
    i64 = np.eye(64, dtype=np.float32)
    cst[0:64, C_MUS:C_MUS + 64] = mus
    cst[0:64, C_MUI:C_MUI + 64] = mui
    cst[0:64, C_MLS:C_MLS + 64] = mls
    rst = np.ones((512,), np.float32)
    rst[::CH] = 0.0
    cst[:, C_RST:C_RST + 512] = rst[None, :]
    for g, w in enumerate(WINS):
        pos = np.arange(16)
        cst[:, C_ICNT + g * 16:C_ICNT + (g + 1) * 16] = (1.0 / np.minimum(pos + 1, w)).astype(np.float32)[None, :]
    return cst


def kernel(x_prompt, x_sample, state_shift, state_wkv, state_pool, norm_w, w_in, mu_shift,
           w_decay_b, w0, w_aaa_b, a0, k_k, k_a, r_k, gn_w, gn_b, pool_w, pool_scale, w_out, norm_f):
    f = lambda a: np.ascontiguousarray(np.asarray(a, dtype=np.float32))
    x_prompt, x_sample, state_shift, state_wkv, state_pool = map(f, (x_prompt, x_sample, state_shift, state_wkv, state_pool))
    if "nc" not in _CACHE:
        _CACHE["nc"] = build_program()
    nc, S = _CACHE["nc"]

    def colmajor(v, n):
        return f(v).reshape(n, 128).T

    pvec = np.concatenate([
        colmajor(norm_w[0], 8), colmajor(mu_shift[0], 13), colmajor(w0[0], 4), colmajor(a0[0], 4), colmajor(k_k[0], 4),
        colmajor(k_a[0], 4), colmajor(f(r_k[0]).reshape(-1), 4), colmajor(gn_w[0], 4), colmajor(gn_b[0], 4), colmajor(pool_scale[0], 4)], axis=1)
    pvec = f(pvec)
    browA = f(f(mu_shift[0])[None, :])
    browB = f(np.concatenate([f(w0[0]), f(a0[0]), f(k_k[0]), f(k_a[0]), f(r_k[0]).reshape(-1), f(gn_w[0]), f(gn_b[0])])[None, :])
    cst = _consts()
    shared = {
        "w_in": f(w_in[0]), "w_out": f(w_out[0]), "wdec": f(w_decay_b[0]), "waaa": f(w_aaa_b[0]), "poolw": f(pool_w[0]),
        "pvec": pvec, "browA": browA, "browB": browB, "normf": f(norm_f)[None, :], "cst": cst,
    }
    in_maps = []
    for c in range(NCORE):
        bs = slice(c * DB, (c + 1) * DB)
        m = dict(shared)
        m["xp"] = x_prompt[c]
        m["xs"] = f(x_sample[bs].transpose(1, 0, 2).reshape(NS, D))
        m["sshift"] = state_shift[0, bs]
        m["swkv"] = f(state_wkv[0, bs].reshape(128, 4096))
        m["spool"] = f(state_pool[0, bs].reshape(DB * 15, 512))
        in_maps.append(m)
    res = run_bass_kernel_spmd(nc, in_maps, core_ids=list(range(NCORE)))
    R = res.results
    y_prompt = np.stack([R[c]["yp"] for c in range(NCORE)], axis=0)
    y_sample = np.concatenate([R[c]["ys"].reshape(DT, DB, D).transpose(1, 0, 2) for c in range(NCORE)], axis=0)
    nsp = np.stack([R[c]["nsp"].reshape(D_SHIFT) for c in range(NCORE)], axis=0)[None]
    nwp = np.stack([R[c]["nwp"] for c in range(NCORE)], axis=0)[None]
    npp = np.stack([R[c]["npp"] for c in range(NCORE)], axis=0)[None]
    nss = np.concatenate([R[c]["nss"] for c in range(NCORE)], axis=0)[None]
    nws = np.concatenate([R[c]["nws"].reshape(DB, 8, 64, 64) for c in range(NCORE)], axis=0)[None]
    nps = np.concatenate([R[c]["nps"] for c in range(NCORE)], axis=0)[None]
    out = (y_prompt, y_sample, nsp, nwp, npp, nss, nws, nps)
    return tuple(np.ascontiguousarray(o.astype(np.float32)) for o in out)
```

```python
import numpy as np
from contextlib import ExitStack
import concourse.bass as bass
import concourse.mybir as mybir
from concourse.bass_utils import run_bass_kernel_spmd

F32 = mybir.dt.float32
BF16 = mybir.dt.bfloat16
AF = mybir.ActivationFunctionType
ALU = mybir.AluOpType
AX = mybir.AxisListType

D = 1024
SEQ = 2048
NCORE = 8
DB = 16
DT = 4
NS = DB * DT
D_SHIFT = 1664
D_IN = 3200
C0 = float(np.exp(-0.5))
NORM_EPS = 1e-6
GN_EPS = 64e-5
L2_EPS = 1e-12
TB = 128
NTB = SEQ // TB
CH = 64
FBIAS = 0.0
NCH = TB // CH
WINS = (2, 4, 8, 16)

C_ID, C_ONES, C_MUS, C_MUI, C_MLS, C_RST, C_ICNT, C_END = 0, 128, 256, 320, 384, 448, 960, 1024
PV_NW, PV_MU, PV_W0, PV_A0, PV_KK, PV_KA, PV_RK, PV_GW, PV_GB, PV_PS, PV_END = 0, 8, 21, 25, 29, 33, 37, 41, 45, 49, 53
BRB_W0, BRB_A0, BRB_KK, BRB_KA, BRB_RK, BRB_GW, BRB_GB = 0, 512, 1024, 1536, 2048, 2560, 3072


class Buf:
    __slots__ = ("name", "w", "r")

    def __init__(self, name):
        self.name = name
        self.w = None
        self.r = []


class T:
    def __init__(self, t, name, buf=None):
        self.t = t
        self.b = buf if buf is not None else Buf(name)

    def __getitem__(self, k):
        return self.t[k]


class Sched:
    def __init__(self, nc, n_dma_sems=32):
        self.nc = nc
        self.eng = {}
        for name in ["tensor", "vector", "scalar", "gpsimd", "sync"]:
            h = getattr(nc, name)
            sem = nc.alloc_semaphore(name="prog_" + name)
            self.eng[name] = dict(h=h, sem=sem, cnt=0, waited={})
        self.dma_sems = [dict(sem=nc.alloc_semaphore(name=f"dma{i}"), cnt=0) for i in range(n_dma_sems)]
        self.dma_rr = 0
        self.ninstr = 0
        self.rec = None

    def _wait(self, engname, tok):
        sem, val, src = tok
        e = self.eng[engname]
        key = id(sem)
        if e["waited"].get(key, 0) >= val:
            return
        e["h"].wait_ge(sem, val)
        e["waited"][key] = val
        self.ninstr += 1

    def _deps(self, engname, reads, writes):
        toks = []
        for b in reads:
            if b.w is not None:
                toks.append(b.w)
        for b in writes:
            if b.w is not None:
                toks.append(b.w)
            toks.extend(b.r)
        for tok in toks:
            if tok[2] == engname and engname == "tensor":
                continue
            self._wait(engname, tok)

    @staticmethod
    def _bufs(xs):
        return [x.b if isinstance(x, T) else x for x in xs]

    def _record(self, tok, reads, writes):
        for b in reads:
            b.r.append(tok)
            if len(b.r) > 64:
                b.r = b.r[-64:] if False else b.r
        for b in writes:
            b.w = tok
            b.r = []

    def op(self, engname, fn, reads=(), writes=(), cost=0.3):
        reads = self._bufs(reads)
        writes = self._bufs(writes)
        if self.rec is not None:
            self.rec.append(("op", engname, fn, reads, writes, cost, None))
            return None
        e = self.eng[engname]
        self._deps(engname, reads, writes)
        ins = fn(e["h"])
        e["cnt"] += 1
        ins.then_inc(e["sem"], 1)
        e["waited"][id(e["sem"])] = max(e["waited"].get(id(e["sem"]), 0), 0)
        tok = (e["sem"], e["cnt"], engname)
        self._record(tok, reads, writes)
        self.ninstr += 1
        return tok

    def dma(self, qname, out, in_, reads=(), writes=(), **kw):
        reads = self._bufs(reads)
        writes = self._bufs(writes)
        if self.rec is not None:
            self.rec.append(("dma", qname, (out, in_), reads, writes, 2.5, kw))
            return None
        e = self.eng[qname]
        self._deps(qname, reads, writes)
        d = self.dma_sems[self.dma_rr]
        self.dma_rr = (self.dma_rr + 1) % len(self.dma_sems)
        if d["cnt"] > 0:
            self._wait(qname, (d["sem"], 16 * d["cnt"], "dma"))
        ins = e["h"].dma_start(out=out, in_=in_, **kw)
        d["cnt"] += 1
        ins.then_inc(d["sem"], 16)
        tok = (d["sem"], 16 * d["cnt"], "dma")
        self._record(tok, reads, writes)
        self.ninstr += 1
        return tok

    def emit(self, r):
        kind, eng, fn, reads, writes, cost, kw = r
        if kind == "op":
            self.op(eng, fn, reads=reads, writes=writes)
        else:
            self.dma(eng, fn[0], fn[1], reads=reads, writes=writes, **kw)

    def merge_emit(self, A, B, a_ok, b_ok):
        eng_free = {}
        ready = {}
        acc = {}

        def est(r):
            kind, eng, fn, reads, writes, cost, kw = r
            t = eng_free.get(eng, 0.0)
            for b in reads:
                rt_, re_ = ready.get(id(b), (0.0, eng))
                t = max(t, rt_ + (0.15 if re_ != eng else 0.0))
            for b in writes:
                rt_, re_ = ready.get(id(b), (0.0, eng))
                t = max(t, rt_ + (0.15 if re_ != eng else 0.0), acc.get(id(b), 0.0) + 0.1)
            return t

        def commit(r, t):
            kind, eng, fn, reads, writes, cost, kw = r
            if kind == "dma":
                eng_free[eng] = t + 0.1
                end = t + cost
            else:
                end = t + cost
                eng_free[eng] = end
            for b in reads:
                acc[id(b)] = max(acc.get(id(b), 0.0), end)
            for b in writes:
                ready[id(b)] = (end, eng)
                acc[id(b)] = max(acc.get(id(b), 0.0), end)

        def run_unit(u):
            for r in u:
                commit(r, est(r))
                self.emit(r)

        ia = ib = 0
        ja = jb = 0
        while ia < len(A) or ib < len(B):
            ca = None
            cb = None
            if ia < len(A) and (ja > 0 or a_ok(ia, ib)):
                ca = A[ia][ja]
            if ib < len(B) and (jb > 0 or b_ok(ib, ia)):
                cb = B[ib][jb]
            assert ca is not None or cb is not None, (ia, ib, ja, jb)
            ta = est(ca[0]) if ca is not None else None
            tb_ = est(cb[0]) if cb is not None else None
            if cb is None or (ca is not None and ta + FBIAS < tb_):
                run_unit(ca); ja += 1
                if ja == len(A[ia]):
                    ia += 1; ja = 0
            else:
                run_unit(cb); jb += 1
                if jb == len(B[ib]):
                    ib += 1; jb = 0

    def barrier(self):
        toks = [(e["sem"], e["cnt"], n) for n, e in self.eng.items() if e["cnt"] > 0]
        toks += [(d["sem"], 16 * d["cnt"], "dma") for d in self.dma_sems if d["cnt"] > 0]
        for n in self.eng:
            for tok in toks:
                if tok[2] == n:
                    continue
                self._wait(n, tok)

    def finish(self, tiles, engname="sync"):
        for b in self._bufs(tiles):
            if b.w is not None:
                self._wait(engname, b.w)


class _Stop(Exception):
    pass


def build_program(stop=None):
    nc = bass.Bass("TRN2", target_bir_lowering=False)
    S = Sched(nc)
    try:
        _build_body(nc, S, stop)
    except _Stop:
        S.barrier()
    return nc, S


def _build_body(nc, S, stop):
    def chk(label):
        if stop == label:
            raise _Stop()


    def din(name, shape):
        return nc.dram_tensor(name, list(shape), F32, kind="ExternalInput").ap()

    def dout(name, shape):
        return T(nc.dram_tensor(name, list(shape), F32, kind="ExternalOutput").ap(), name)

    xp = din("xp", [SEQ, D])
    xs = din("xs", [NS, D])
    sshift = din("sshift", [DB, D_SHIFT])
    swkv = din("swkv", [128, 4096])
    spool = din("spool", [DB * 15, 512])
    w_in = din("w_in", [D, D_IN])
    w_out = din("w_out", [D, D])
    wdec = din("wdec", [64, 512])
    waaa = din("waaa", [64, 512])
    poolw = din("poolw", [4, 128, 128])
    pvec_d = din("pvec", [128, PV_END])
    browA_d = din("browA", [1, D_SHIFT])
    browB_d = din("browB", [1, 3584])
    normf_d = din("normf", [1, D])
    cst_d = din("cst", [128, C_END])

    yp = dout("yp", [SEQ, D])
    ys = dout("ys", [NS, D])
    nsp = dout("nsp", [13, 128])
    nwp = dout("nwp", [8, 64, 64])
    npp = dout("npp", [15, 512])
    nss = dout("nss", [DB, D_SHIFT])
    nws = dout("nws", [128, 4096])
    nps = dout("nps", [DB, 15, 512])
    scr1 = T(nc.dram_tensor("scr1", [6, DT, DB, 8, 64], F32, kind="Internal").ap(), "scr1")
    scr2 = T(nc.dram_tensor("scr2", [DB, 8, DT, 64], F32, kind="Internal").ap(), "scr2")

    es_top = ExitStack()

    def sb(es, name, shape, dt=F32):
        return T(es.enter_context(nc.sbuf_tensor("s_" + name, list(shape), dt)), name)

    def pst(name, shape, dt=F32):
        return T(nc.alloc_psum_tensor("p_" + name, list(shape), dt), name)

    def nel(ap):
        n = 1
        for s_ in ap.shape[1:]:
            n *= s_
        return n

    def mm(out, lhsT, rhs, start, stop, reads, writes):
        passes = 4 if lhsT.dtype == F32 else 1
        c_ = max(0.055, nel(rhs) * passes / 2000.0 + 0.03)
        S.op("tensor", lambda e: e.matmul(out, lhsT=lhsT, rhs=rhs, start=start, stop=stop), reads=reads, writes=writes, cost=c_)

    def tr(out, in_, ident, reads, writes):
        S.op("tensor", lambda e: e.transpose(out, in_, ident), reads=reads, writes=writes, cost=0.13)

    def act(out, in_, func, reads, writes, bias=None, scale=None, eng="scalar", accum=None):
        kw = {}
        if accum is not None:
            kw["accum_out"] = accum
        if bias is not None:
            kw["bias"] = bias
        if scale is not None:
            kw["scale"] = scale
        S.op("scalar", lambda e: e.activation(out=out, in_=in_, func=func, **kw), reads=reads, writes=writes,
             cost=0.1 + 0.1 * len(kw) + nel(in_) * 0.00095)

    def ecost(eng, n):
        return 0.08 + n * (0.00105 if eng == "vector" else 0.0025)

    def vtt(out, in0, in1, op, reads, writes, eng="vector"):
        S.op(eng, lambda e: e.tensor_tensor(out=out, in0=in0, in1=in1, op=op), reads=reads, writes=writes, cost=ecost(eng, nel(out)))

    def vts(out, in0, s1, s2, op0, op1, reads, writes, eng="vector"):
        if op1 is None:
            S.op(eng, lambda e: e.tensor_scalar(out=out, in0=in0, scalar1=s1, scalar2=None, op0=op0), reads=reads, writes=writes,
                 cost=ecost(eng, nel(out)))
        else:
            S.op(eng, lambda e: e.tensor_scalar(out=out, in0=in0, scalar1=s1, scalar2=s2, op0=op0, op1=op1), reads=reads, writes=writes,
                 cost=ecost(eng, nel(out)))

    def vstt(out, in0, scalar, in1, op0, op1, reads, writes):
        S.op("vector", lambda e: e.scalar_tensor_tensor(out=out, in0=in0, scalar=scalar, in1=in1, op0=op0, op1=op1), reads=reads, writes=writes,
             cost=ecost("vector", nel(out)))

    def vcopy(out, in_, reads, writes, eng="vector"):
        S.op(eng, lambda e: e.tensor_copy(out=out, in_=in_), reads=reads, writes=writes, cost=ecost(eng, nel(out)))

    def vred(out, in_, reads, writes):
        S.op("vector", lambda e: e.tensor_reduce(out=out, in_=in_, axis=AX.X, op=ALU.add), reads=reads, writes=writes,
             cost=ecost("vector", nel(in_)))

    def vrecip(out, in_, reads, writes):
        S.op("vector", lambda e: e.reciprocal(out=out, in_=in_), reads=reads, writes=writes, cost=0.08 + nel(out) * 0.0084)

    def memset(ap, val, writes, eng="gpsimd"):
        S.op(eng, lambda e: e.memset(ap, val), writes=writes)

    def rsqrt_small(out, in_, tmp, scale, eps, reads, writes):
        act(tmp, in_, AF.Sqrt, reads=reads, writes=writes, bias=None, scale=None) if False else None
        vts(tmp, in_, scale, eps, ALU.mult, ALU.add, reads=reads, writes=writes)
        act(tmp, tmp, AF.Sqrt, reads=writes, writes=writes)
        vrecip(out, tmp, reads=writes, writes=writes)

    def rsqrt_act(out, in_, scale, eps_col, n, reads, writes):
        act(out, in_, AF.Ln, reads=list(reads) + [epsT], writes=writes, scale=scale, bias=epsT[0:n, eps_col:eps_col + 1])
        act(out, out, AF.Exp, reads=writes, writes=writes, scale=-0.5)

    pg = [pst(f"pg{i}", [128, 512]) for i in range(2)]
    pT = pst("pT", [128, 1024], BF16)
    pM = pst("pM", [128, 512])
    pA = pst("pA", [128, 512])
    pB = pst("pB", [128, 512])
    pC = pst("pC", [128, 512])
    pD = pst("pD", [128, 512])

    cst = sb(es_top, "cst", [128, C_END])
    pvec = sb(es_top, "pvec", [128, PV_END])
    omu = sb(es_top, "omu", [128, 13])
    omka = sb(es_top, "omka", [128, 4])
    identb = sb(es_top, "identb", [128, 128], BF16)
    winb = sb(es_top, "winb", [128, 8, D_IN], BF16)
    woutb = sb(es_top, "woutb", [128, 8, D], BF16)
    wd = sb(es_top, "wd", [64, 512])
    wa = sb(es_top, "wa", [128, 512])
    pw = sb(es_top, "pw", [128, 4, 128])
    normf = sb(es_top, "normf", [128, D])
    epsT = sb(es_top, "epsT", [128, 4])

    ident = cst[:, C_ID:C_ID + 128]
    onesblk = cst[:, C_ONES:C_ONES + 128]

    S.dma("sync", cst[:], cst_d, writes=[cst])
    S.dma("sync", pvec[:], pvec_d, writes=[pvec])
    S.dma("sync", wd[:], wdec, writes=[wd])
    S.dma("sync", wa[64:128, :], waaa, writes=[wa])
    S.dma("sync", pw[:], poolw.rearrange("g c e -> c g e"), writes=[pw])
    S.dma("sync", normf[:], normf_d.partition_broadcast(128), writes=[normf])
    vcopy(identb[:], ident, reads=[cst], writes=[identb])
    memset(epsT[:, 0:1], NORM_EPS, writes=[epsT])
    memset(epsT[:, 1:2], GN_EPS, writes=[epsT])
    memset(epsT[:, 2:3], L2_EPS, writes=[epsT])
    vts(omka[:], pvec[:, PV_KA:PV_KA + 4], -1.0, 1.0, ALU.mult, ALU.add, reads=[pvec], writes=[omka])

    with ExitStack() as es:
        stg = [sb(es, f"stg{i}", [128, D_IN]) for i in range(3)]
        for dc in range(8):
            st = stg[dc % 3]
            S.dma("sync", st[:], w_in[dc * 128:(dc + 1) * 128, :], writes=[st])
            h = D_IN // 2
            vts(winb[:, dc, 0:h], st[:, 0:h], pvec[:, PV_NW + dc:PV_NW + dc + 1], None, ALU.mult, None, reads=[st, pvec], writes=[winb])
            act(winb[:, dc, h:], st[:, h:], AF.Copy, reads=[st, pvec], writes=[winb], scale=pvec[:, PV_NW + dc:PV_NW + dc + 1])
        S.barrier()
        chk("W")

    def final_tile(es_tiles, n, x_t, oT_list, out_dram_ap, out_T):
        res, sq, ssum, tmp1, rstd, yo = es_tiles
        for half in range(2):
            bank = pD if half == 0 else pC
            for fc in range(8):
                mm(bank[0:n, :], oT_list[fc], woutb[:, fc, half * 512:(half + 1) * 512], fc == 0, fc == 7,
                   reads=[oT_list_T, woutb], writes=[bank])
            vtt(res[0:n, half * 512:(half + 1) * 512], bank[0:n, :], x_t[0:n, half * 512:(half + 1) * 512], ALU.add,
                reads=[bank, x_t], writes=[res])
        act(sq[0:n, :], res[0:n, :], AF.Square, reads=[res], writes=[sq])
        vred(ssum[0:n, :], sq[0:n, :], reads=[sq], writes=[ssum])
        rsqrt_small(rstd[0:n, :], ssum[0:n, :], tmp1[0:n, :], 1.0 / D, NORM_EPS, reads=[ssum], writes=[tmp1, rstd])
        vstt(yo[0:n, :], res[0:n, :], rstd[0:n, 0:1], normf[0:n, :], ALU.mult, ALU.mult, reads=[res, rstd, normf], writes=[yo])
        S.dma("sync", out_dram_ap, yo[0:n, :], reads=[yo], writes=[out_T])

    oT_list_T = None

    with ExitStack() as es:
        browB = sb(es, "browB", [NS, 3584])
        S.dma("sync", browB[:], browB_d.partition_broadcast(NS), writes=[browB])
        x_s = sb(es, "x_s", [NS, D])
        S.dma("sync", x_s[:], xs, writes=[x_s])
        hTs = sb(es, "hTs", [128, 8, DB + NS], BF16)
        graw_s = sb(es, "graw_s", [NS, 512])
        u_s = sb(es, "u_s", [NS, 512])
        gp_s = sb(es, "gp_s", [NS, 512])
        bonus_s = sb(es, "bonus_s", [NS, 512])
        st8 = sb(es, "st8", [NS, 8])
        st8b = sb(es, "st8b", [NS, 8])
        st8c = sb(es, "st8c", [NS, 8])

        def v3(ap):
            return ap.rearrange("p (h k) -> p h k", k=64)

        def bc8(ap8):
            return ap8.unsqueeze(2).to_broadcast([NS, 8, 64])

        with ExitStack() as e1:
            browA = sb(e1, "browA", [NS, D_SHIFT])
            S.dma("sync", browA[:], browA_d.partition_broadcast(NS), writes=[browA])
            omka_b = sb(e1, "omka_b", [NS, 512])
            vts(omka_b[:], browB[:, BRB_KA:BRB_KA + 512], -1.0, 1.0, ALU.mult, ALU.add, reads=[browB], writes=[omka_b])
            sq_s = sb(e1, "sq_s", [NS, D])
            ss_s = sb(e1, "ss_s", [NS, 1])
            t1_s = sb(e1, "t1_s", [NS, 1])
            rstd_s = sb(e1, "rstd_s", [NS, 1])
            xn_s = sb(e1, "xn_s", [NS, D], BF16)
            act(sq_s[:], x_s[:], AF.Square, reads=[x_s], writes=[sq_s])
            vred(ss_s[:], sq_s[:], reads=[sq_s], writes=[ss_s])
            rsqrt_small(rstd_s[:], ss_s[:], t1_s[:], 1.0 / D, NORM_EPS, reads=[ss_s], writes=[t1_s, rstd_s])
            vts(xn_s[:], x_s[:], rstd_s[:, 0:1], None, ALU.mult, None, reads=[x_s, rstd_s], writes=[xn_s])
            memset(hTs[:, :, 0:DB], 0.0, writes=[hTs])
            for dc in range(8):
                tr(pT[:, dc * 128:dc * 128 + NS], xn_s[:, dc * 128:(dc + 1) * 128], identb[0:NS, 0:NS], reads=[xn_s, identb], writes=[pT])
            vcopy(hTs[:, :, DB:DB + NS], pT[:].rearrange("p (c t) -> p c t", t=128)[:, :, 0:NS], reads=[pT], writes=[hTs])

            p_s = sb(e1, "p_s", [NS, D_SHIFT])
            prev_s = sb(e1, "prev_s", [NS, D_SHIFT])
            col_chunks = [(0, 512), (512, 512), (1024, 512), (1536, 128)]
            kk_ = 0
            for (c0, n) in col_chunks:
                bank = pg[kk_ % 2]; kk_ += 1
                for dc in range(8):
                    mm(bank[0:NS, 0:n], hTs[:, dc, DB:DB + NS], winb[:, dc, c0:c0 + n], dc == 0, dc == 7, reads=[hTs, winb], writes=[bank])
                act(p_s[:, c0:c0 + n], bank[0:NS, 0:n], AF.Copy, reads=[bank], writes=[p_s])
                bank = pg[kk_ % 2]; kk_ += 1
                for dc in range(8):
                    mm(bank[0:NS, 0:n], hTs[:, dc, 0:NS], winb[:, dc, c0:c0 + n], dc == 0, dc == 7, reads=[hTs, winb], writes=[bank])
                vcopy(prev_s[:, c0:c0 + n], bank[0:NS, 0:n], reads=[bank], writes=[prev_s])
            for (c0, dst, fn) in [(1664, graw_s, AF.Silu), (2176, u_s, AF.Copy), (2688, gp_s, AF.Silu)]:
                bank = pg[kk_ % 2]; kk_ += 1
                for dc in range(8):
                    mm(bank[0:NS, :], hTs[:, dc, DB:DB + NS], winb[:, dc, c0:c0 + 512], dc == 0, dc == 7, reads=[hTs, winb], writes=[bank])
                act(dst[:], bank[0:NS, :], fn, reads=[bank], writes=[dst])
            S.dma("sync", prev_s[0:DB, :], sshift, writes=[prev_s])
            S.dma("sync", nss[:], p_s[NS - DB:NS, :], reads=[p_s], writes=[nss])
            S.dma("sync", nps[:, 0:11, :], spool.rearrange("(b j) c -> b j c", j=15)[:, 4:15, :], writes=[nps])
            for t in range(DT):
                S.dma("sync", nps[:, 11 + t, :], u_s[t * DB:(t + 1) * DB, :], reads=[u_s], writes=[nps])

            vtt(prev_s[:], prev_s[:], p_s[:], ALU.subtract, reads=[prev_s, p_s], writes=[prev_s])
            vtt(prev_s[:], prev_s[:], browA[:], ALU.mult, reads=[prev_s, browA], writes=[prev_s])
            vtt(prev_s[:], prev_s[:], p_s[:], ALU.add, reads=[prev_s, p_s], writes=[prev_s])
            ps_s = prev_s
            r_s = ps_s[:, 0:512]
            k_s = ps_s[:, 512:1024]
            v_s = ps_s[:, 1024:1536]

            lT = sb(e1, "lT", [128, NS])
            tr(pM[:, 0:NS], ps_s[:, 1536:1664], ident[0:NS, 0:NS], reads=[ps_s, cst], writes=[pM])
            act(lT[0:64, :], pM[0:64, 0:NS], AF.Tanh, reads=[pM], writes=[lT])
            act(lT[64:128, :], pM[64:128, 0:NS], AF.Copy, reads=[pM], writes=[lT])
            sg_s = sb(e1, "sg_s", [NS, 512])
            a_s = sb(e1, "a_s", [NS, 512])
            mm(pA[0:NS, :], lT[0:64, :], wd[:, :], True, True, reads=[lT, wd], writes=[pA])
            vtt(sg_s[:], pA[0:NS, :], browB[:, BRB_W0:BRB_W0 + 512], ALU.add, reads=[pA, browB], writes=[sg_s])
            act(sg_s[:], sg_s[:], AF.Sigmoid, reads=[sg_s], writes=[sg_s])
            mm(pB[0:NS, :], lT[64:128, :], wa[64:128, :], True, True, reads=[lT, wa], writes=[pB])
            vtt(a_s[:], pB[0:NS, :], browB[:, BRB_A0:BRB_A0 + 512], ALU.add, reads=[pB, browB], writes=[a_s])
            act(a_s[:], a_s[:], AF.Sigmoid, reads=[a_s], writes=[a_s])

            pk = sb(e1, "pk", [NS, 4, 512])
            PQ = {1: 0, 2: 1, 4: 2, 5: 3}
            tmpA = sb(e1, "tmpA", [NS, 512])
            tmpB = sb(e1, "tmpB", [NS, 512])
            act(pk[:, PQ[1], :], sg_s[:], AF.Exp, reads=[sg_s], writes=[pk], scale=-C0)
            vtt(tmpA[:], k_s, browB[:, BRB_KK:BRB_KK + 512], ALU.mult, reads=[ps_s, browB], writes=[tmpA])
            vtt(tmpB[:], tmpA[:], tmpA[:], ALU.mult, reads=[tmpA], writes=[tmpB])
            vred(st8[:], v3(tmpB[:]), reads=[tmpB], writes=[st8])
            rsqrt_small(st8b[:], st8[:], st8c[:], 1.0, L2_EPS, reads=[st8], writes=[st8c, st8b])
            vtt(v3(tmpA[:]), v3(tmpA[:]), bc8(st8b[:]), ALU.mult, reads=[tmpA, st8b], writes=[tmpA])
            vts(pk[:, PQ[4], :], tmpA[:], -1.0, None, ALU.mult, None, reads=[tmpA], writes=[pk])
            vtt(pk[:, PQ[5], :], tmpA[:], a_s[:], ALU.mult, reads=[tmpA, a_s], writes=[pk])
            vtt(tmpB[:], a_s[:], browB[:, BRB_KA:BRB_KA + 512], ALU.mult, reads=[a_s, browB], writes=[tmpB])
            vtt(tmpB[:], tmpB[:], omka_b[:], ALU.add, reads=[tmpB, omka_b], writes=[tmpB])
            vtt(pk[:, PQ[2], :], k_s, tmpB[:], ALU.mult, reads=[ps_s, tmpB], writes=[pk])
            vtt(tmpB[:], r_s, browB[:, BRB_RK:BRB_RK + 512], ALU.mult, reads=[ps_s, browB], writes=[tmpB])
            vtt(tmpB[:], tmpB[:], pk[:, PQ[2], :], ALU.mult, reads=[tmpB, pk], writes=[tmpB])
            vred(st8[:], v3(tmpB[:]), reads=[tmpB], writes=[st8])
            vtt(v3(bonus_s[:]), v3(v_s), bc8(st8[:]), ALU.mult, reads=[ps_s, st8], writes=[bonus_s])
            sview = scr1[:].rearrange("q t b h k -> q (t b) (h k)")
            S.dma("sync", sview[0], r_s, reads=[ps_s], writes=[scr1])
            S.dma("sync", sview[3], v_s, reads=[ps_s], writes=[scr1])
            for qq, slot in PQ.items():
                S.dma("sync", sview[qq], pk[:, slot, :], reads=[pk], writes=[scr1])
            S.finish([scr1], engname="sync")
            S.barrier()
            chk("S1")

        with ExitStack() as e2:
            sIn = sb(e2, "sIn", [128, 6, DT, 64])
            S.dma("sync", sIn[:], scr1[:].rearrange("q t b h k -> (b h) q t k"), reads=[scr1], writes=[sIn])
            St = sb(e2, "St", [128, 64, 64])
            S.dma("sync", St[:].rearrange("p v k -> p (v k)"), swkv, writes=[St])
            tmpS = sb(e2, "tmpS", [128, 64, 64])
            sa = sb(e2, "sa", [128, 64])
            yS = sb(e2, "yS", [128, DT, 64])
            stgo = [sb(e2, f"stgo{i}", [128, D]) for i in range(3)]
            for fc in range(8):
                so = stgo[fc % 3]
                S.dma("sync", so[:], w_out[fc * 128:(fc + 1) * 128, :], writes=[so])
                act(woutb[:, fc, :], so[:], AF.Copy, reads=[so], writes=[woutb])

            def bv(ap):
                return ap.unsqueeze(1).to_broadcast([128, 64, 64])

            def bk(ap):
                return ap.unsqueeze(2).to_broadcast([128, 64, 64])

            for t in range(DT):
                q = lambda i: sIn[:, i, t, :]
                vtt(tmpS[:], St[:], bv(q(4)), ALU.mult, reads=[St, sIn], writes=[tmpS])
                vred(sa[:], tmpS[:], reads=[tmpS], writes=[sa])
                vtt(St[:], St[:], bv(q(1)), ALU.mult, reads=[St, sIn], writes=[St])
                vtt(tmpS[:], bk(sa[:]), bv(q(5)), ALU.mult, reads=[sa, sIn], writes=[tmpS])
                vtt(St[:], St[:], tmpS[:], ALU.add, reads=[St, tmpS], writes=[St])
                vtt(tmpS[:], bk(q(3)), bv(q(2)), ALU.mult, reads=[sIn], writes=[tmpS])
                vtt(St[:], St[:], tmpS[:], ALU.add, reads=[St, tmpS], writes=[St])
                vtt(tmpS[:], St[:], bv(q(0)), ALU.mult, reads=[St, sIn], writes=[tmpS])
                vred(yS[:, t, :], tmpS[:], reads=[tmpS], writes=[yS])
            S.dma("sync", nws[:], St[:].rearrange("p v k -> p (v k)"), reads=[St], writes=[nws])
            S.dma("sync", scr2[:].rearrange("b h t v -> (b h) t v"), yS[:], reads=[yS], writes=[scr2])
            S.finish([scr2, nws], engname="sync")
            S.barrier()
            chk("S2")

        with ExitStack() as e3:
            yT = sb(e3, "yT", [NS, 512])
            tmpA = sb(e3, "tmpA3", [NS, 512])
            for t in range(DT):
                S.dma("sync", yT[t * DB:(t + 1) * DB, :].rearrange("b (h v) -> b h v", v=64), scr2[:][:, :, t, :], reads=[scr2], writes=[yT])
            vred(st8[:], v3(yT[:]), reads=[yT], writes=[st8])
            vts(st8[:], st8[:], 1.0 / 64, None, ALU.mult, None, reads=[st8], writes=[st8])
            vtt(v3(yT[:]), v3(yT[:]), bc8(st8[:]), ALU.subtract, reads=[yT, st8], writes=[yT])
            vtt(tmpA[:], yT[:], yT[:], ALU.mult, reads=[yT], writes=[tmpA])
            vred(st8[:], v3(tmpA[:]), reads=[tmpA], writes=[st8])
            rsqrt_small(st8b[:], st8[:], st8c[:], 1.0 / 64, GN_EPS, reads=[st8], writes=[st8c, st8b])
            vtt(v3(yT[:]), v3(yT[:]), bc8(st8b[:]), ALU.mult, reads=[yT, st8b], writes=[yT])
            vtt(yT[:], yT[:], browB[:, BRB_GW:BRB_GW + 512], ALU.mult, reads=[yT, browB], writes=[yT])
            vtt(yT[:], yT[:], browB[:, BRB_GB:BRB_GB + 512], ALU.add, reads=[yT, browB], writes=[yT])
            vtt(yT[:], yT[:], bonus_s[:], ALU.add, reads=[yT, bonus_s], writes=[yT])
            vtt(yT[:], yT[:], graw_s[:], ALU.mult, reads=[yT, graw_s], writes=[yT])
            oTs = sb(e3, "oTs", [128, 8, NS], BF16)
            for fb in range(4):
                tr(pA[:, fb * 64:fb * 64 + NS], yT[:, fb * 128:(fb + 1) * 128], ident[0:NS, 0:NS], reads=[yT, cst], writes=[pA])
            vcopy(oTs[:, 0:4, :], pA[:, 0:4 * NS].rearrange("p (f t) -> p f t", t=NS), reads=[pA], writes=[oTs])

            uext = sb(e3, "uext_s", [128, 4, DB, 19])
            sp0 = sb(e3, "sp0", [120, 512])
            sp1 = sb(e3, "sp1", [120, 512])
            S.dma("sync", sp0[:], spool[0:120, :], writes=[sp0])
            S.dma("sync", sp1[:], spool[120:240, :], writes=[sp1])
            for g in range(4):
                tr(pB[:, 0:120], sp0[:, g * 128:(g + 1) * 128], ident[0:120, 0:120], reads=[sp0, cst], writes=[pB])
                tr(pB[:, 128:248], sp1[:, g * 128:(g + 1) * 128], ident[0:120, 0:120], reads=[sp1, cst], writes=[pB])
                vcopy(uext[:, g, 0:8, 0:15], pB[:, 0:120].rearrange("p (b j) -> p b j", j=15), reads=[pB], writes=[uext])
                vcopy(uext[:, g, 8:16, 0:15], pB[:, 128:248].rearrange("p (b j) -> p b j", j=15), reads=[pB], writes=[uext])
                tr(pM[:, 0:NS], u_s[:, g * 128:(g + 1) * 128], ident[0:NS, 0:NS], reads=[u_s, cst], writes=[pM])
                vcopy(uext[:, g, :, 15:19], pM[:, 0:NS].rearrange("p (t b) -> p b t", b=DB), reads=[pM], writes=[uext])
            s2 = sb(e3, "s2_s", [128, 4, DB, 19])
            s4 = sb(e3, "s4_s", [128, 3, DB, 19])
            s8 = sb(e3, "s8_s", [128, 2, DB, 19])
            s16 = sb(e3, "s16_s", [128, 1, DB, 19])
            d_s = sb(e3, "d_s", [128, 4, DT, DB])
            vtt(s2[:, :, :, 1:19], uext[:, :, :, 1:19], uext[:, :, :, 0:18], ALU.add, reads=[uext], writes=[s2])
            vtt(s4[:, :, :, 3:19], s2[:, 1:4, :, 3:19], s2[:, 1:4, :, 1:17], ALU.add, reads=[s2], writes=[s4])
            vtt(s8[:, :, :, 7:19], s4[:, 1:3, :, 7:19], s4[:, 1:3, :, 3:15], ALU.add, reads=[s4], writes=[s8])
            vtt(s16[:, :, :, 15:19], s8[:, 1:2, :, 15:19], s8[:, 1:2, :, 7:11], ALU.add, reads=[s8], writes=[s16])
            tots = [(s2, 0), (s4, 1), (s8, 2), (s16, 3)]
            for g in range(4):
                tt, off = tots[g]
                vstt(d_s[:, g, :, :].rearrange("p t b -> p b t"), tt[:, g - off, :, 15:19], 1.0 / WINS[g], uext[:, g, :, 15:19],
                     ALU.mult, ALU.subtract, reads=[tt, uext], writes=[d_s])
            gpT = sb(e3, "gpT", [128, 4, NS])
            for g in range(4):
                tr(pM[:, 64 + g * 64:64 + g * 64 + NS], gp_s[:, g * 128:(g + 1) * 128], ident[0:NS, 0:NS], reads=[gp_s, cst], writes=[pM])
            vcopy(gpT[:], pM[:, 64:64 + 4 * NS].rearrange("p (g t) -> p g t", t=NS), reads=[pM], writes=[gpT])
            for g in range(4):
                mm(pA[:, g * 64:g * 64 + NS], pw[:, g, :], d_s[:, g, :, :].rearrange("p t b -> p (t b)"), True, True, reads=[pw, d_s], writes=[pA])
            for g in range(4):
                vstt(oTs[:, 4 + g, :], pA[:, g * 64:g * 64 + NS], pvec[:, PV_PS + g:PV_PS + g + 1], gpT[:, g, :], ALU.mult, ALU.mult,
                     reads=[pA, pvec, gpT], writes=[oTs])

            sq = sb(e3, "sq2_s", [NS, D]); ssum = sb(e3, "ssum_s", [NS, 1])
            tmp1 = sb(e3, "tmp1_s", [NS, 1]); rstd = sb(e3, "rstd2_s", [NS, 1]); yo = sb(e3, "yo_s", [NS, D])
            oT_list_T = oTs
            final_tile((x_s, sq, ssum, tmp1, rstd, yo), NS, x_s, [oTs[:, fc, :] for fc in range(8)], ys[:], ys)
            S.finish([ys, nss, nps], engname="sync")
            S.barrier()
            chk("S3")

    with ExitStack() as es:
        def sbl(name, shape, dt=F32, n=2):
            return [sb(es, f"{name}_{i}", shape, dt) for i in range(n)]

        xt = sbl("xt", [128, D])
        yo = sb(es, "yo", [128, D])
        ssx = sb(es, "ssx", [128, 1]); t1x = sb(es, "t1x", [128, 1]); rsx = sb(es, "rsx", [128, 1])
        xnb = sb(es, "xnb", [128, D], BF16)
        sqx = xnb
        hT = sb(es, "hT", [128, 8, TB], BF16)
        praw = sbl("praw", [128, 4, TB + 1])
        halo = sb(es, "halo", [128, 13])
        omu = sb(es, "omu2", [128, 13])
        psr = sb(es, "psr", [128, 4, TB]); psk = sb(es, "psk", [128, 4, TB]); psv = sb(es, "psv", [128, 4, TB])
        ps12 = sb(es, "ps12", [128, TB])
        psx = [T(g_[:, i, :], f"psx{gi_}_{i}", buf=g_.b) for gi_, g_ in enumerate([psr, psk, psv]) for i in range(4)] + [ps12]
        sg = sb(es, "sg", [128, 4, TB]); av = sb(es, "av", [128, 4, TB])
        gsil = sbl("gsil", [128, 4, TB], BF16)
        gpsil = sb(es, "gpsil", [128, 4, TB], BF16)
        uext = sb(es, "uext", [128, 4, 15 + TB])
        th = sb(es, "th", [64, TB])
        wbig = [sb(es, f"wbig{i}", [128, 4, TB]) for i in range(4)]
        w1, w2, w3, w4 = wbig
        srot = [T(wbig[i][:].rearrange("p f t -> p (f t)")[:, 0:15 + TB], f"srot{i}", buf=wbig[i].b) for i in range(4)]
        kkn = sb(es, "kkn", [128, 4, TB]); kmod = sb(es, "kmod", [128, 4, TB]); bv_ = sb(es, "bv_", [128, 4, TB])
        cum = sb(es, "cum", [128, 4, TB])
        dpl = T(kkn[:, 0, :], "dpl", buf=kkn.b)
        at = sbl("at", [64, 4, 2, TB], BF16)
        rt = sbl("rt", [64, 4, 2, TB], BF16)
        bt = sb(es, "bt", [64, 4, 2, TB], BF16)
        kt = sb(es, "kt", [64, 4, 2, TB], BF16)
        bh = sb(es, "bh", [128, 4, TB], BF16); kh = sb(es, "kh", [128, 4, TB], BF16); vb = sb(es, "vb", [128, 4, TB], BF16)
        bon = sbl("bon", [128, 4, TB])
        gC = sbl("gC", [64, 4, 2, NCH])
        VT = [[sb(es, f"VT{p}{c}", [64, 512], BF16) for c in range(NCH)] for p in range(2)]
        BKT = [[sb(es, f"BKT{p}{c}", [64, 1024], BF16) for c in range(NCH)] for p in range(2)]
        Aak = [[sb(es, f"Aak{p}{c}", [64, 512], BF16) for c in range(NCH)] for p in range(2)]
        Arb = [[sb(es, f"Arb{p}{c}", [64, 512], BF16) for c in range(NCH)] for p in range(2)]
        Ark = [[sb(es, f"Ark{p}{c}", [64, 512], BF16) for c in range(NCH)] for p in range(2)]
        Minv = [[sb(es, f"Minv{p}{c}", [64, 512], BF16) for c in range(NCH)] for p in range(2)]
        Nsb = [sb(es, f"Nsb{c}", [64, 512], BF16) for c in range(NCH)]
        NTsb = [sb(es, f"NTsb{c}", [64, 512], BF16) for c in range(NCH)]
        Xa0 = [sb(es, f"Xa0{c}", [64, 512], BF16) for c in range(NCH)]
        XTa0 = [sb(es, f"XTa0{c}", [64, 512], BF16) for c in range(NCH)]
        Qtmp = [sb(es, f"Qtmp{c}", [64, 512], BF16) for c in range(NCH)]
        ST = sb(es, "ST", [64, 8, 64]); STb = sb(es, "STb", [64, 8, 64], BF16)
        Wsb = sb(es, "Wsb", [64, 512], BF16); Usb = sb(es, "Usb", [64, 512], BF16)
        yc = sb(es, "yc", [64, 512]); ysq = sb(es, "ysq", [64, 512])
        STt = T(ysq[:].rearrange("p (h v) -> p h v", v=64), "STt", buf=ysq.b)
        m8 = sb(es, "m8", [64, 8]); v8 = sb(es, "v8", [64, 8]); r8 = sb(es, "r8", [64, 8]); t8 = sb(es, "t8", [64, 8])
        o1 = sb(es, "o1", [128, 4, 64])
        oT = sbl("oT", [128, 8, TB], BF16)
        ssum = sb(es, "ssum", [128, 1]); tmp1 = sb(es, "tmp1", [128, 1]); rstd = sb(es, "rstd", [128, 1])
        ppT = T(ysq[0:16, :], "ppT", buf=ysq.b); m13 = sb(es, "m13", [13, 128])
        SvT = T(yc[:].rearrange("p (h k) -> p h k", k=64), "SvT", buf=yc.b)

        memset(halo[:], 0.0, writes=[halo])
        memset(uext[:, :, 0:15], 0.0, writes=[uext])
        memset(ST[:], 0.0, writes=[ST])
        memset(STb[:], 0.0, writes=[STb])
        vts(omu[:], pvec[:, PV_MU:PV_MU + 13], -1.0, 1.0, ALU.mult, ALU.add, reads=[pvec], writes=[omu])

        def b8(ap):
            return ap.unsqueeze(1).to_broadcast([64, 8, 64])

        def h3(ap):
            return ap.rearrange("p (h v) -> p h v", v=64)

        def hc(h):
            return slice(h * 64, (h + 1) * 64)

        maskUs = b8(cst[0:64, C_MUS:C_MUS + 64])
        maskUi = b8(cst[0:64, C_MUI:C_MUI + 64])
        maskLs = b8(cst[0:64, C_MLS:C_MLS + 64])
        ident8 = b8(cst[0:64, C_ID:C_ID + 64])
        rstm = cst[:, C_RST:C_RST + 512]
        st = dict(gk=0, ak=0)
        pT32 = T(pT[:].bitcast(F32), "pT32", buf=pT.b)
        abanks = [pA, pB, pg[0], pg[1], pM, pT32]

        def nextbank():
            b = abanks[st["ak"] % len(abanks)]
            st["ak"] += 1
            return b

        def front(tb):
            pb = tb % 2
            t0 = tb * TB
            x_t = xt[pb]
            S.dma("sync", x_t[:], xp[t0:t0 + TB, :], writes=[x_t])
            act(sqx[:], x_t[:], AF.Square, reads=[x_t], writes=[sqx, ssx], accum=ssx[:])
            rsqrt_act(rsx[:], ssx[:], 1.0 / D, 0, 128, reads=[ssx], writes=[rsx])
            act(xnb[:], x_t[:], AF.Copy, reads=[x_t, rsx], writes=[xnb], scale=rsx[:, 0:1])
            yield
            for dc in range(8):
                tr(pT[:, dc * 128:(dc + 1) * 128], xnb[:, dc * 128:(dc + 1) * 128], identb[:], reads=[xnb, identb], writes=[pT])
            vcopy(hT[:].rearrange("p c t -> p (c t)"), pT[:], reads=[pT], writes=[hT])
            yield

            def gemm_group(ebs):
                bank = pg[st["gk"] % 2]
                st["gk"] += 1
                for i, eb in enumerate(ebs):
                    for dc in range(8):
                        mm(bank[:, i * TB:(i + 1) * TB], winb[:, dc, eb * 128:(eb + 1) * 128], hT[:, dc, :], dc == 0, dc == 7,
                           reads=[winb, hT], writes=[bank])
                return bank

            for gi, ebs in enumerate([[0, 1, 2, 3], [4, 5, 6, 7], [8, 9, 10, 11], [12]]):
                bank = gemm_group(ebs)
                yield
                n = len(ebs)
                pr = praw[gi % 2]
                e0 = ebs[0]
                gT = [psr, psk, psv, ps12][gi]
                dst = gT[:, 0:n, :] if gi < 3 else ps12[:].unsqueeze(1)
                mub = pvec[:, PV_MU + e0:PV_MU + e0 + n].unsqueeze(2).to_broadcast([128, n, TB])
                act(pr[:, 0:n, 0:1], halo[:, e0:e0 + n].unsqueeze(2), AF.Copy, reads=[halo], writes=[pr])
                act(pr[:, 0:n, 1:TB + 1], bank[:, 0:n * TB].rearrange("p (e t) -> p e t", t=TB), AF.Copy, reads=[bank], writes=[pr])
                vtt(dst, pr[:, 0:n, 0:TB], pr[:, 0:n, 1:TB + 1], ALU.subtract, reads=[pr], writes=[gT])
                vtt(dst, dst, mub, ALU.mult, reads=[gT, pvec], writes=[gT])
                vtt(dst, dst, pr[:, 0:n, 1:TB + 1], ALU.add, reads=[gT, pr], writes=[gT])
                act(halo[:, e0:e0 + n].unsqueeze(2), pr[:, 0:n, TB:TB + 1], AF.Copy, reads=[pr], writes=[halo])
                yield
            bank = gemm_group([13, 14, 15, 16])
            act(gsil[pb][:].rearrange("p f t -> p (f t)"), bank[:, :], AF.Silu, reads=[bank], writes=[gsil[pb]])
            yield
            bank = gemm_group([17, 18, 19, 20])
            act(uext[:, :, 15:15 + TB], bank[:, :].rearrange("p (g t) -> p g t", t=TB), AF.Copy, reads=[bank], writes=[uext])
            yield
            bank = gemm_group([21, 22, 23, 24])
            act(gpsil[:].rearrange("p g t -> p (g t)"), bank[:, :], AF.Silu, reads=[bank], writes=[gpsil])
            yield

            act(th[:], psx[12][0:64, :], AF.Tanh, reads=[psx[12]], writes=[th])
            for fb in range(4):
                mm(pA[:, fb * TB:(fb + 1) * TB], wd[:, fb * 128:(fb + 1) * 128], th[:], True, True, reads=[wd, th], writes=[pA])
            for fb in range(4):
                mm(pM[:, fb * TB:(fb + 1) * TB], wa[64:128, fb * 128:(fb + 1) * 128], psx[12][64:128, :], True, True, reads=[wa, psx[12]], writes=[pM])
            for fb in range(4):
                act(sg[:, fb, :], pA[:, fb * TB:(fb + 1) * TB], AF.Sigmoid, reads=[pA, pvec], writes=[sg], bias=pvec[:, PV_W0 + fb:PV_W0 + fb + 1])
                act(av[:, fb, :], pM[:, fb * TB:(fb + 1) * TB], AF.Sigmoid, reads=[pM, pvec], writes=[av], bias=pvec[:, PV_A0 + fb:PV_A0 + fb + 1])
            yield

            def pb4(col):
                return pvec[:, col:col + 4].unsqueeze(2).to_broadcast([128, 4, TB])

            def f2(t_):
                return t_[:].rearrange("p f t -> p (f t)")

            bomk = omka[:, 0:4].unsqueeze(2).to_broadcast([128, 4, TB])
            c3 = cum[:].rearrange("p f (c t) -> p (f c) t", t=CH)
            p0, p1 = slice(0, 64), slice(64, 128)
            act(vb[:], psv[:], AF.Copy, reads=[psv], writes=[vb])
            vtt(w1[:], psk[:], pb4(PV_KK), ALU.mult, reads=[psk, pvec], writes=[w1])
            vtt(w2[:], w1[:], w1[:], ALU.mult, reads=[w1], writes=[w2])
            mm(pM[:, :], onesblk, f2(w2), True, True, reads=[cst, w2], writes=[pM])
            S.op("vector", lambda e: e.tensor_tensor_scan(out=f2(cum), data0=rstm, data1=f2(sg), initial=0.0, op0=ALU.mult, op1=ALU.add),
                 reads=[cst, sg], writes=[cum], cost=1.2)
            act(f2(w2), pM[:, :], AF.Ln, reads=[pM, epsT], writes=[w2], bias=epsT[:, 2:3])
            act(w2[:], w2[:], AF.Exp, reads=[w2], writes=[w2], scale=-0.5)
            vtt(w3[:], av[:], pb4(PV_KA), ALU.mult, reads=[av, pvec], writes=[w3])
            vtt(w3[:], w3[:], bomk, ALU.add, reads=[w3, omka], writes=[w3])
            vtt(kmod[:], psk[:], w3[:], ALU.mult, reads=[psk, w3], writes=[kmod])
            vtt(w4[:], cum[:], sg[:], ALU.subtract, reads=[cum, sg], writes=[w4])
            act(w4[:], w4[:], AF.Exp, reads=[w4], writes=[w4], scale=-C0)
            vtt(w3[:], psr[:], pb4(PV_RK), ALU.mult, reads=[psr, pvec], writes=[w3])
            vtt(w3[:], w3[:], kmod[:], ALU.mult, reads=[w3, kmod], writes=[w3])
            mm(pA[:, :], onesblk, f2(w3), True, True, reads=[cst, w3], writes=[pA])
            yield
            vstt(kkn[:], w1[:], -1.0, w2[:], ALU.mult, ALU.mult, reads=[w1, w2], writes=[kkn])
            vstt(bv_[:], kkn[:], -1.0, av[:], ALU.mult, ALU.mult, reads=[kkn, av], writes=[bv_])
            act(w1[:], cum[:], AF.Exp, reads=[cum], writes=[w1], scale=-C0)
            act(w2[:], cum[:], AF.Exp, reads=[cum], writes=[w2], scale=C0)
            vtt(at[pb][:, :, 1, :], kkn[p1, :, :], w4[p1, :, :], ALU.mult, reads=[kkn, w4], writes=[at[pb]])
            vtt(at[pb][:, :, 0, :], kkn[p0, :, :], w4[p0, :, :], ALU.mult, reads=[kkn, w4], writes=[at[pb]])
            vtt(f2(bon[pb]), pA[:, :], f2(psv), ALU.mult, reads=[pA, psv], writes=[bon[pb]])
            vtt(w3[:].rearrange("p f (c t) -> p (f c) t", t=CH), c3[:, :, CH - 1:CH].to_broadcast([128, 4 * NCH, CH]), c3, ALU.subtract,
                reads=[cum], writes=[w3], eng="gpsimd")
            act(w3[:], w3[:], AF.Exp, reads=[w3], writes=[w3], scale=-C0)
            for j in range(2):
                pp = slice(64 * j, 64 * j + 64)
                act(gC[pb][:, :, j, :], cum[pp, :, :].rearrange("p f (c t) -> p f c t", t=CH)[:, :, :, CH - 1], AF.Exp,
                    reads=[cum], writes=[gC[pb]], scale=-C0)
            yield
            for j in range(2):
                pp = slice(64 * j, 64 * j + 64)
                e_ = "vector"
                vtt(rt[pb][:, :, j, :], psr[pp, :, :], w1[pp, :, :], ALU.mult, reads=[psr, w1], writes=[rt[pb]], eng=e_)
                vtt(kt[:, :, j, :], kmod[pp, :, :], w2[pp, :, :], ALU.mult, reads=[kmod, w2], writes=[kt], eng=e_)
                vtt(bt[:, :, j, :], bv_[pp, :, :], w2[pp, :, :], ALU.mult, reads=[bv_, w2], writes=[bt], eng=e_)
            vtt(bh[:], bv_[:], w3[:], ALU.mult, reads=[bv_, w3], writes=[bh])
            vtt(kh[:], kmod[:], w3[:], ALU.mult, reads=[kmod, w3], writes=[kh], eng="gpsimd")
            yield

            L = 15 + TB
            for g in range(4):
                vtt(srot[0][:, 1:], uext[:, g, 1:], uext[:, g, 0:L - 1], ALU.add, reads=[uext], writes=[srot[0]], eng="gpsimd")
                tot = srot[0]
                if g >= 1:
                    vtt(srot[1][:, 3:], srot[0][:, 3:], srot[0][:, 1:L - 2], ALU.add, reads=[srot[0]], writes=[srot[1]], eng="gpsimd")
                    tot = srot[1]
                if g >= 2:
                    vtt(srot[2][:, 7:], srot[1][:, 7:], srot[1][:, 3:L - 4], ALU.add, reads=[srot[1]], writes=[srot[2]], eng="gpsimd")
                    tot = srot[2]
                if g >= 3:
                    vtt(srot[3][:, 15:], srot[2][:, 15:], srot[2][:, 7:L - 8], ALU.add, reads=[srot[2]], writes=[srot[3]], eng="gpsimd")
                    tot = srot[3]
                vstt(dpl[:], tot[:, 15:], 1.0 / WINS[g], uext[:, g, 15:], ALU.mult, ALU.subtract, reads=[tot, uext], writes=[dpl])
                if tb == 0:
                    vtt(dpl[:, 0:16], tot[:, 15:31], cst[:, C_ICNT + g * 16:C_ICNT + (g + 1) * 16], ALU.mult, reads=[tot, cst], writes=[dpl])
                    vtt(dpl[:, 0:16], dpl[:, 0:16], uext[:, g, 15:31], ALU.subtract, reads=[dpl, uext], writes=[dpl])
                mm(pM[:, 0:TB], pw[:, g, :], dpl[:], True, True, reads=[pw, dpl], writes=[pM])
                vstt(oT[pb][:, 4 + g, :], pM[:, 0:TB], pvec[:, PV_PS + g:PV_PS + g + 1], gpsil[:, g, :], ALU.mult, ALU.mult,
                     reads=[pM, pvec, gpsil], writes=[oT[pb]])
                yield
            if tb == NTB - 1:
                for g in range(4):
                    tr(pA[0:16, g * 128:(g + 1) * 128], uext[:, g, TB - 1:TB + 15], ident, reads=[uext, cst], writes=[pA])
                vcopy(ppT[:], pA[0:16, :], reads=[pA], writes=[ppT])
                S.dma("sync", npp[:], ppT[1:16, :], reads=[ppT], writes=[npp])
                tr(pB[0:13, 0:128], halo[:, 0:13], ident, reads=[halo, cst], writes=[pB])
                vcopy(m13[:], pB[0:13, 0:128], reads=[pB], writes=[m13])
                S.dma("sync", nsp[:], m13[:], reads=[m13], writes=[nsp])
            vcopy(uext[:, :, 0:15], uext[:, :, TB:TB + 15], reads=[uext], writes=[uext], eng="gpsimd")
            yield

            css = [slice(c * CH, (c + 1) * CH) for c in range(NCH)]
            for c in range(NCH):
                for qi, srcl in enumerate([bh, kh]):
                    for fb in range(4):
                        tr(pT[0:64, qi * 512 + fb * 128:qi * 512 + (fb + 1) * 128], srcl[:, fb, css[c]], identb[:], reads=[srcl, identb], writes=[pT])
                vcopy(BKT[pb][c][:], pT[0:64, :], reads=[pT], writes=[BKT[pb][c]])
                for fb in range(4):
                    tr(pT[0:64, fb * 128:(fb + 1) * 128], vb[:, fb, css[c]], identb[:], reads=[vb, identb], writes=[pT])
                act(VT[pb][c][:], pT[0:64, 0:512], AF.Copy, reads=[pT], writes=[VT[pb][c]])
                yield

            def hsl(tl, h, c):
                fb, j = divmod(h, 2)
                return tl[:, fb, j, css[c]]

            for (Lt, Rt, mask, dsts) in [(bt, at[pb], maskUs, Nsb), (at[pb], bt, maskLs, NTsb), (kt, at[pb], maskUs, Aak[pb]),
                                         (bt, rt[pb], maskUi, Arb[pb]), (kt, rt[pb], maskUi, Ark[pb])]:
                banks = []
                for c in range(NCH):
                    bank = nextbank()
                    banks.append(bank)
                    for h in range(8):
                        mm(bank[0:64, hc(h)], hsl(Lt, h, c), hsl(Rt, h, c), True, True, reads=[Lt, Rt], writes=[bank])
                for c in range(NCH):
                    vtt(h3(dsts[c][:]), h3(banks[c][0:64, :]), mask, ALU.mult, reads=[banks[c], cst], writes=[dsts[c]])
                yield
            X = list(Nsb); XT = list(NTsb)
            Q = [Qtmp[c] for c in range(NCH)]
            for c in range(NCH):
                vtt(h3(Q[c][:]), h3(Nsb[c][:]), ident8, ALU.add, reads=[Nsb[c], cst], writes=[Q[c]])
            for lvl in range(5):
                Xn = [(Xa0[c] if lvl % 2 == 0 else Nsb[c]) for c in range(NCH)]
                XTn = [(XTa0[c] if lvl % 2 == 0 else NTsb[c]) for c in range(NCH)]
                Qn = [(Minv[pb][c] if lvl % 2 == 0 else Qtmp[c]) for c in range(NCH)]
                banks = []
                for c in range(NCH):
                    bank = nextbank(); banks.append(bank)
                    for h in range(8):
                        mm(bank[0:64, hc(h)], X[c][:, hc(h)], XT[c][:, hc(h)], True, True, reads=[X[c], XT[c]], writes=[bank])
                for c in range(NCH):
                    act(XTn[c][:], banks[c][0:64, :], AF.Copy, reads=[banks[c]], writes=[XTn[c]])
                yield
                if lvl < 4:
                    banks = []
                    for c in range(NCH):
                        bank = nextbank(); banks.append(bank)
                        for h in range(8):
                            mm(bank[0:64, hc(h)], XT[c][:, hc(h)], X[c][:, hc(h)], True, True, reads=[X[c], XT[c]], writes=[bank])
                    for c in range(NCH):
                        act(Xn[c][:], banks[c][0:64, :], AF.Copy, reads=[banks[c]], writes=[Xn[c]])
                    yield
                banks = []
                for c in range(NCH):
                    bank = nextbank(); banks.append(bank)
                    for h in range(8):
                        mm(bank[0:64, hc(h)], XTn[c][:, hc(h)], Q[c][:, hc(h)], True, True, reads=[XTn[c], Q[c]], writes=[bank])
                for c in range(NCH):
                    vtt(Qn[c][:], banks[c][0:64, :], Q[c][:], ALU.add, reads=[banks[c], Q[c]], writes=[Qn[c]])
                X, XT, Q = Xn, XTn, Qn
                yield

        def chain(tb):
            pb = tb % 2
            t0 = tb * TB
            for c in range(NCH):
                cs = slice(c * CH, (c + 1) * CH)
                aT, rT = at[pb], rt[pb]
                VTc, BKTc, Aakc, Arbc, Arkc, Minvc = VT[pb][c], BKT[pb][c], Aak[pb][c], Arb[pb][c], Ark[pb][c], Minv[pb][c]
                for h in range(8):
                    fb, j = divmod(h, 2)
                    mm(pC[0:64, hc(h)], aT[:, fb, j, cs], STb[:, h, :], True, False, reads=[aT, STb], writes=[pC])
                    mm(pC[0:64, hc(h)], Aakc[:, hc(h)], VTc[:, hc(h)], False, True, reads=[Aakc, VTc], writes=[pC])
                act(Wsb[:], pC[0:64, :], AF.Copy, reads=[pC], writes=[Wsb])
                yield
                for h in range(8):
                    mm(pC[0:64, hc(h)], Minvc[:, hc(h)], Wsb[:, hc(h)], True, True, reads=[Minvc, Wsb], writes=[pC])
                act(Usb[:], pC[0:64, :], AF.Copy, reads=[pC], writes=[Usb])
                yield
                for h in range(8):
                    mm(pC[0:64, hc(h)], BKTc[:, hc(h)], Usb[:, hc(h)], True, False, reads=[BKTc, Usb], writes=[pC])
                    mm(pC[0:64, hc(h)], BKTc[:, 512 + h * 64:512 + (h + 1) * 64], VTc[:, hc(h)], False, True, reads=[BKTc, VTc], writes=[pC])
                for h in range(8):
                    fb, j = divmod(h, 2)
                    mm(pD[0:64, hc(h)], rT[:, fb, j, cs], STb[:, h, :], True, False, reads=[rT, STb], writes=[pD])
                    mm(pD[0:64, hc(h)], Arbc[:, hc(h)], Usb[:, hc(h)], False, False, reads=[Arbc, Usb], writes=[pD])
                    mm(pD[0:64, hc(h)], Arkc[:, hc(h)], VTc[:, hc(h)], False, True, reads=[Arkc, VTc], writes=[pD])
                vtt(STt[:], ST[:], gC[pb][:].rearrange("p f j c -> p (f j) c")[:, :, c:c + 1].to_broadcast([64, 8, 64]), ALU.mult,
                    reads=[ST, gC[pb]], writes=[STt])
                vtt(ST[:], STt[:], h3(pC[0:64, :]), ALU.add, reads=[STt, pC], writes=[ST])
                act(STb[:], ST[:], AF.Copy, reads=[ST], writes=[STb])
                yield
                y3 = h3(pD[0:64, :])
                vred(m8[:], y3, reads=[pD], writes=[m8])
                vts(m8[:], m8[:], 1.0 / 64, None, ALU.mult, None, reads=[m8], writes=[m8])
                vtt(h3(yc[:]), y3, m8[:].unsqueeze(2).to_broadcast([64, 8, 64]), ALU.subtract, reads=[pD, m8], writes=[yc])
                act(ysq[:], yc[:], AF.Square, reads=[yc], writes=[ysq])
                vred(v8[:], h3(ysq[:]), reads=[ysq], writes=[v8])
                rsqrt_act(r8[:], v8[:], 1.0 / 64, 1, 64, reads=[v8], writes=[r8])
                vtt(h3(yc[:]), h3(yc[:]), r8[:].unsqueeze(2).to_broadcast([64, 8, 64]), ALU.mult, reads=[yc, r8], writes=[yc], eng="gpsimd")
                yield
                for fb in range(4):
                    tr(pD[:, fb * 64:(fb + 1) * 64], yc[:, fb * 128:(fb + 1) * 128], ident[0:64, 0:64], reads=[yc, cst], writes=[pD])
                for fb in range(4):
                    vts(o1[:, fb, :], pD[:, fb * 64:(fb + 1) * 64], pvec[:, PV_GW + fb:PV_GW + fb + 1], pvec[:, PV_GB + fb:PV_GB + fb + 1],
                        ALU.mult, ALU.add, reads=[pD, pvec], writes=[o1])
                vtt(o1[:], o1[:], bon[pb][:, :, cs], ALU.add, reads=[o1, bon[pb]], writes=[o1], eng="gpsimd")
                vtt(oT[pb][:, 0:4, cs], o1[:], gsil[pb][:, :, cs], ALU.mult, reads=[o1, gsil[pb]], writes=[oT[pb]])
                yield
            x_t = xt[pb]
            for half in range(2):
                bank = pD if half == 0 else pC
                for fc in range(8):
                    mm(bank[:, :], oT[pb][:, fc, :], woutb[:, fc, half * 512:(half + 1) * 512], fc == 0, fc == 7, reads=[oT[pb], woutb], writes=[bank])
                vtt(x_t[:, half * 512:(half + 1) * 512], bank[:, :], x_t[:, half * 512:(half + 1) * 512], ALU.add, reads=[bank, x_t], writes=[x_t])
                yield
            act(yo[:], x_t[:], AF.Square, reads=[x_t], writes=[yo, ssum], accum=ssum[:])
            rsqrt_act(rstd[:], ssum[:], 1.0 / D, 0, 128, reads=[ssum], writes=[rstd])
            vstt(yo[:], x_t[:], rstd[:, 0:1], normf[:], ALU.mult, ALU.mult, reads=[x_t, rstd, normf], writes=[yo])
            S.dma("sync", yp[t0:t0 + TB, :], yo[:], reads=[yo], writes=[yp])
            yield

        def run_all(g):
            n = 0
            for _ in g:
                n += 1
            return n

        def interleave(ga, na, gb, nb):
            ia = ib = 0
            da = db = False
            while not (da and db):
                pick_a = (not da) and (db or (ia * nb <= ib * na))
                if pick_a:
                    try:
                        next(ga); ia += 1
                    except StopIteration:
                        da = True
                else:
                    try:
                        next(gb); ib += 1
                    except StopIteration:
                        db = True
            return ia, ib

        run_all(front(0))

        def record_units(g):
            units = []
            S.rec = []
            for _ in g:
                if S.rec:
                    units.append(S.rec)
                S.rec = []
            if S.rec:
                units.append(S.rec)
            S.rec = None
            return units

        A, B = [], []
        for tb in range(NTB):
            A.append(record_units(chain(tb)))
            if tb + 1 < NTB:
                B.append(record_units(front(tb + 1)))
        S.merge_emit(A, B, a_ok=lambda ia, ib: ib >= ia, b_ok=lambda ib, ia: ia >= ib)
        for h in range(8):
            tr(pA[0:64, h * 64:(h + 1) * 64], ST[:, h, :], ident[0:64, 0:64], reads=[ST, cst], writes=[pA])
        vcopy(SvT[:].rearrange("p h k -> p (h k)"), pA[0:64, :], reads=[pA], writes=[SvT])
        S.dma("sync", nwp[:].rearrange("h v k -> v h k"), SvT[:], reads=[SvT], writes=[nwp])
        S.finish([yp, ys, nsp, nwp, npp, nss, nws, nps], engname="sync")
        S.barrier()
    es_top.close()
    return nc, S


_CACHE = {}


def _consts():
    cst = np.zeros((128, C_END), np.float32)
    cst[:, C_ID:C_ID + 128] = np.eye(128, dtype=np.float32)
    ob = np.zeros((128, 128), np.float32)
    ob[0:64, 0:64] = 1.0
    ob[64:128, 64:128] = 1.0
    cst[:, C_ONES:C_ONES + 128] = ob
    s = np.arange(64)[:, None]
    t = np.arange(64)[None, :]
    mus = (s < t).astype(np.float32)
    mui = (s <= t).astype(np.float32)
    mls = (s > t).astype(np.float32)
    i64 = np.eye(64, dtype=np.float32)
    cst[0:64, C_MUS:C_MUS + 64] = mus
    cst[0:64, C_MUI:C_MUI + 64] = mui
    cst[0:64, C_MLS:C_MLS + 64] = mls
    rst = np.ones((512,), np.float32)
    rst[::CH] = 0.0
    cst[:, C_RST:C_RST + 512] = rst[None, :]
    for g, w in enumerate(WINS):
        pos = np.arange(16)
        cst[:, C_ICNT + g * 16:C_ICNT + (g + 1) * 16] = (1.0 / np.minimum(pos + 1, w)).astype(np.float32)[None, :]
    return cst


def kernel(x_prompt, x_sample, state_shift, state_wkv, state_pool, norm_w, w_in, mu_shift,
           w_decay_b, w0, w_aaa_b, a0, k_k, k_a, r_k, gn_w, gn_b, pool_w, pool_scale, w_out, norm_f):
    f = lambda a: np.ascontiguousarray(np.asarray(a, dtype=np.float32))
    x_prompt, x_sample, state_shift, state_wkv, state_pool = map(f, (x_prompt, x_sample, state_shift, state_wkv, state_pool))
    if "nc" not in _CACHE:
        _CACHE["nc"] = build_program()
    nc, S = _CACHE["nc"]

    def colmajor(v, n):
        return f(v).reshape(n, 128).T

    pvec = np.concatenate([
        colmajor(norm_w[0], 8), colmajor(mu_shift[0], 13), colmajor(w0[0], 4), colmajor(a0[0], 4), colmajor(k_k[0], 4),
        colmajor(k_a[0], 4), colmajor(f(r_k[0]).reshape(-1), 4), colmajor(gn_w[0], 4), colmajor(gn_b[0], 4), colmajor(pool_scale[0], 4)], axis=1)
    pvec = f(pvec)
    browA = f(f(mu_shift[0])[None, :])
    browB = f(np.concatenate([f(w0[0]), f(a0[0]), f(k_k[0]), f(k_a[0]), f(r_k[0]).reshape(-1), f(gn_w[0]), f(gn_b[0])])[None, :])
    cst = _consts()
    shared = {
        "w_in": f(w_in[0]), "w_out": f(w_out[0]), "wdec": f(w_decay_b[0]), "waaa": f(w_aaa_b[0]), "poolw": f(pool_w[0]),
        "pvec": pvec, "browA": browA, "browB": browB, "normf": f(norm_f)[None, :], "cst": cst,
    }
    in_maps = []
    for c in range(NCORE):
        bs = slice(c * DB, (c + 1) * DB)
        m = dict(shared)
        m["xp"] = x_prompt[c]
        m["xs"] = f(x_sample[bs].transpose(1, 0, 2).reshape(NS, D))
        m["sshift"] = state_shift[0, bs]
        m["swkv"] = f(state_wkv[0, bs].reshape(128, 4096))
        m["spool"] = f(state_pool[0, bs].reshape(DB * 15, 512))
        in_maps.append(m)
    res = run_bass_kernel_spmd(nc, in_maps, core_ids=list(range(NCORE)))
    R = res.results
    y_prompt = np.stack([R[c]["yp"] for c in range(NCORE)], axis=0)
    y_sample = np.concatenate([R[c]["ys"].reshape(DT, DB, D).transpose(1, 0, 2) for c in range(NCORE)], axis=0)
    nsp = np.stack([R[c]["nsp"].reshape(D_SHIFT) for c in range(NCORE)], axis=0)[None]
    nwp = np.stack([R[c]["nwp"] for c in range(NCORE)], axis=0)[None]
    npp = np.stack([R[c]["npp"] for c in range(NCORE)], axis=0)[None]
    nss = np.concatenate([R[c]["nss"] for c in range(NCORE)], axis=0)[None]
    nws = np.concatenate([R[c]["nws"].reshape(DB, 8, 64, 64) for c in range(NCORE)], axis=0)[None]
    nps = np.concatenate([R[c]["nps"] for c in range(NCORE)], axis=0)[None]
    out = (y_prompt, y_sample, nsp, nwp, npp, nss, nws, nps)
    return tuple(np.ascontiguousarray(o.astype(np.float32)) for o in out)
```

```python
import numpy as np
from contextlib import ExitStack
import concourse.bass as bass
import concourse.mybir as mybir
from concourse.bass_utils import run_bass_kernel_spmd

F32 = mybir.dt.float32
BF16 = mybir.dt.bfloat16
AF = mybir.ActivationFunctionType
ALU = mybir.AluOpType
AX = mybir.AxisListType

D = 1024
SEQ = 2048
NCORE = 8
DB = 16
DT = 4
NS = DB * DT
D_SHIFT = 1664
D_IN = 3200
C0 = float(np.exp(-0.5))
NORM_EPS = 1e-6
GN_EPS = 64e-5
L2_EPS = 1e-12
TB = 128
NTB = SEQ // TB
CH = 64
FBIAS = 0.0
NCH = TB // CH
WINS = (2, 4, 8, 16)

C_ID, C_ONES, C_MUS, C_MUI, C_MLS, C_RST, C_ICNT, C_END = 0, 128, 256, 320, 384, 448, 960, 1024
PV_NW, PV_MU, PV_W0, PV_A0, PV_KK, PV_KA, PV_RK, PV_GW, PV_GB, PV_PS, PV_END = 0, 8, 21, 25, 29, 33, 37, 41, 45, 49, 53
BRB_W0, BRB_A0, BRB_KK, BRB_KA, BRB_RK, BRB_GW, BRB_GB = 0, 512, 1024, 1536, 2048, 2560, 3072


class Buf:
    __slots__ = ("name", "w", "r")

    def __init__(self, name):
        self.name = name
        self.w = None
        self.r = []


class T:
    def __init__(self, t, name, buf=None):
        self.t = t
        self.b = buf if buf is not None else Buf(name)

    def __getitem__(self, k):
        return self.t[k]


class Sched:
    def __init__(self, nc, n_dma_sems=32):
        self.nc = nc
        self.eng = {}
        for name in ["tensor", "vector", "scalar", "gpsimd", "sync"]:
            h = getattr(nc, name)
            sem = nc.alloc_semaphore(name="prog_" + name)
            self.eng[name] = dict(h=h, sem=sem, cnt=0, waited={})
        self.dma_sems = [dict(sem=nc.alloc_semaphore(name=f"dma{i}"), cnt=0) for i in range(n_dma_sems)]
        self.dma_rr = 0
        self.ninstr = 0
        self.rec = None

    def _wait(self, engname, tok):
        sem, val, src = tok
        e = self.eng[engname]
        key = id(sem)
        if e["waited"].get(key, 0) >= val:
            return
        e["h"].wait_ge(sem, val)
        e["waited"][key] = val
        self.ninstr += 1

    def _deps(self, engname, reads, writes):
        toks = []
        for b in reads:
            if b.w is not None:
                toks.append(b.w)
        for b in writes:
            if b.w is not None:
                toks.append(b.w)
            toks.extend(b.r)
        for tok in toks:
            if tok[2] == engname and engname == "tensor":
                continue
            self._wait(engname, tok)

    @staticmethod
    def _bufs(xs):
        return [x.b if isinstance(x, T) else x for x in xs]

    def _record(self, tok, reads, writes):
        for b in reads:
            b.r.append(tok)
            if len(b.r) > 64:
                b.r = b.r[-64:] if False else b.r
        for b in writes:
            b.w = tok
            b.r = []

    def op(self, engname, fn, reads=(), writes=(), cost=0.3):
        reads = self._bufs(reads)
        writes = self._bufs(writes)
        if self.rec is not None:
            self.rec.append(("op", engname, fn, reads, writes, cost, None))
            return None
        e = self.eng[engname]
        self._deps(engname, reads, writes)
        ins = fn(e["h"])
        e["cnt"] += 1
        ins.then_inc(e["sem"], 1)
        e["waited"][id(e["sem"])] = max(e["waited"].get(id(e["sem"]), 0), 0)
        tok = (e["sem"], e["cnt"], engname)
        self._record(tok, reads, writes)
        self.ninstr += 1
        return tok

    def dma(self, qname, out, in_, reads=(), writes=(), **kw):
        reads = self._bufs(reads)
        writes = self._bufs(writes)
        if self.rec is not None:
            self.rec.append(("dma", qname, (out, in_), reads, writes, 2.5, kw))
            return None
        e = self.eng[qname]
        self._deps(qname, reads, writes)
        d = self.dma_sems[self.dma_rr]
        self.dma_rr = (self.dma_rr + 1) % len(self.dma_sems)
        if d["cnt"] > 0:
            self._wait(qname, (d["sem"], 16 * d["cnt"], "dma"))
        ins = e["h"].dma_start(out=out, in_=in_, **kw)
        d["cnt"] += 1
        ins.then_inc(d["sem"], 16)
        tok = (d["sem"], 16 * d["cnt"], "dma")
        self._record(tok, reads, writes)
        self.ninstr += 1
        return tok

    def emit(self, r):
        kind, eng, fn, reads, writes, cost, kw = r
        if kind == "op":
            self.op(eng, fn, reads=reads, writes=writes)
        else:
            self.dma(eng, fn[0], fn[1], reads=reads, writes=writes, **kw)

    def merge_emit(self, A, B, a_ok, b_ok):
        eng_free = {}
        ready = {}
        acc = {}

        def est(r):
            kind, eng, fn, reads, writes, cost, kw = r
            t = eng_free.get(eng, 0.0)
            for b in reads:
                rt_, re_ = ready.get(id(b), (0.0, eng))
                t = max(t, rt_ + (0.35 if re_ != eng else 0.0))
            for b in writes:
                rt_, re_ = ready.get(id(b), (0.0, eng))
                t = max(t, rt_ + (0.35 if re_ != eng else 0.0), acc.get(id(b), 0.0) + 0.1)
            return t

        def commit(r, t):
            kind, eng, fn, reads, writes, cost, kw = r
            if kind == "dma":
                eng_free[eng] = t + 0.1
                end = t + cost
            else:
                end = t + cost
                eng_free[eng] = end
            for b in reads:
                acc[id(b)] = max(acc.get(id(b), 0.0), end)
            for b in writes:
                ready[id(b)] = (end, eng)
                acc[id(b)] = max(acc.get(id(b), 0.0), end)

        def run_unit(u):
            for r in u:
                commit(r, est(r))
                self.emit(r)

        ia = ib = 0
        ja = jb = 0
        while ia < len(A) or ib < len(B):
            ca = None
            cb = None
            if ia < len(A) and (ja > 0 or a_ok(ia, ib)):
                ca = A[ia][ja]
            if ib < len(B) and (jb > 0 or b_ok(ib, ia)):
                cb = B[ib][jb]
            assert ca is not None or cb is not None, (ia, ib, ja, jb)
            ta = est(ca[0]) if ca is not None else None
            tb_ = est(cb[0]) if cb is not None else None
            if cb is None or (ca is not None and ta + FBIAS < tb_):
                run_unit(ca); ja += 1
                if ja == len(A[ia]):
                    ia += 1; ja = 0
            else:
                run_unit(cb); jb += 1
                if jb == len(B[ib]):
                    ib += 1; jb = 0

    def barrier(self):
        toks = [(e["sem"], e["cnt"], n) for n, e in self.eng.items() if e["cnt"] > 0]
        toks += [(d["sem"], 16 * d["cnt"], "dma") for d in self.dma_sems if d["cnt"] > 0]
        for n in self.eng:
            for tok in toks:
                if tok[2] == n:
                    continue
                self._wait(n, tok)

    def finish(self, tiles, engname="sync"):
        for b in self._bufs(tiles):
            if b.w is not None:
                self._wait(engname, b.w)


class _Stop(Exception):
    pass


def build_program(stop=None):
    nc = bass.Bass("TRN2", target_bir_lowering=False)
    S = Sched(nc)
    try:
        _build_body(nc, S, stop)
    except _Stop:
        S.barrier()
    return nc, S


def _build_body(nc, S, stop):
    def chk(label):
        if stop == label:
            raise _Stop()


    def din(name, shape):
        return nc.dram_tensor(name, list(shape), F32, kind="ExternalInput").ap()

    def dout(name, shape):
        return T(nc.dram_tensor(name, list(shape), F32, kind="ExternalOutput").ap(), name)

    xp = din("xp", [SEQ, D])
    xs = din("xs", [NS, D])
    sshift = din("sshift", [DB, D_SHIFT])
    swkv = din("swkv", [128, 4096])
    spool = din("spool", [DB * 15, 512])
    w_in = din("w_in", [D, D_IN])
    w_out = din("w_out", [D, D])
    wdec = din("wdec", [64, 512])
    waaa = din("waaa", [64, 512])
    poolw = din("poolw", [4, 128, 128])
    pvec_d = din("pvec", [128, PV_END])
    browA_d = din("browA", [1, D_SHIFT])
    browB_d = din("browB", [1, 3584])
    normf_d = din("normf", [1, D])
    cst_d = din("cst", [128, C_END])

    yp = dout("yp", [SEQ, D])
    ys = dout("ys", [NS, D])
    nsp = dout("nsp", [13, 128])
    nwp = dout("nwp", [8, 64, 64])
    npp = dout("npp", [15, 512])
    nss = dout("nss", [DB, D_SHIFT])
    nws = dout("nws", [128, 4096])
    nps = dout("nps", [DB, 15, 512])
    scr1 = T(nc.dram_tensor("scr1", [6, DT, DB, 8, 64], F32, kind="Internal").ap(), "scr1")
    scr2 = T(nc.dram_tensor("scr2", [DB, 8, DT, 64], F32, kind="Internal").ap(), "scr2")

    es_top = ExitStack()

    def sb(es, name, shape, dt=F32):
        return T(es.enter_context(nc.sbuf_tensor("s_" + name, list(shape), dt)), name)

    def pst(name, shape, dt=F32):
        return T(nc.alloc_psum_tensor("p_" + name, list(shape), dt), name)

    def nel(ap):
        n = 1
        for s_ in ap.shape[1:]:
            n *= s_
        return n

    def mm(out, lhsT, rhs, start, stop, reads, writes):
        passes = 4 if lhsT.dtype == F32 else 1
        c_ = max(0.055, nel(rhs) * passes / 2000.0 + 0.03)
        S.op("tensor", lambda e: e.matmul(out, lhsT=lhsT, rhs=rhs, start=start, stop=stop), reads=reads, writes=writes, cost=c_)

    def tr(out, in_, ident, reads, writes):
        S.op("tensor", lambda e: e.transpose(out, in_, ident), reads=reads, writes=writes, cost=0.13)

    def act(out, in_, func, reads, writes, bias=None, scale=None, eng="scalar", accum=None):
        kw = {}
        if accum is not None:
            kw["accum_out"] = accum
        if bias is not None:
            kw["bias"] = bias
        if scale is not None:
            kw["scale"] = scale
        S.op("scalar", lambda e: e.activation(out=out, in_=in_, func=func, **kw), reads=reads, writes=writes,
             cost=0.1 + 0.1 * len(kw) + nel(in_) * 0.00095)

    def ecost(eng, n):
        return 0.08 + n * (0.00105 if eng == "vector" else 0.0025)

    def vtt(out, in0, in1, op, reads, writes, eng="vector"):
        S.op(eng, lambda e: e.tensor_tensor(out=out, in0=in0, in1=in1, op=op), reads=reads, writes=writes, cost=ecost(eng, nel(out)))

    def vts(out, in0, s1, s2, op0, op1, reads, writes, eng="vector"):
        if op1 is None:
            S.op(eng, lambda e: e.tensor_scalar(out=out, in0=in0, scalar1=s1, scalar2=None, op0=op0), reads=reads, writes=writes,
                 cost=ecost(eng, nel(out)))
        else:
            S.op(eng, lambda e: e.tensor_scalar(out=out, in0=in0, scalar1=s1, scalar2=s2, op0=op0, op1=op1), reads=reads, writes=writes,
                 cost=ecost(eng, nel(out)))

    def vstt(out, in0, scalar, in1, op0, op1, reads, writes):
        S.op("vector", lambda e: e.scalar_tensor_tensor(out=out, in0=in0, scalar=scalar, in1=in1, op0=op0, op1=op1), reads=reads, writes=writes,
             cost=ecost("vector", nel(out)))

    def vcopy(out, in_, reads, writes, eng="vector"):
        S.op(eng, lambda e: e.tensor_copy(out=out, in_=in_), reads=reads, writes=writes, cost=ecost(eng, nel(out)))

    def vred(out, in_, reads, writes):
        S.op("vector", lambda e: e.tensor_reduce(out=out, in_=in_, axis=AX.X, op=ALU.add), reads=reads, writes=writes,
             cost=ecost("vector", nel(in_)))

    def vrecip(out, in_, reads, writes):
        S.op("vector", lambda e: e.reciprocal(out=out, in_=in_), reads=reads, writes=writes, cost=0.08 + nel(out) * 0.0084)

    def memset(ap, val, writes, eng="gpsimd"):
        S.op(eng, lambda e: e.memset(ap, val), writes=writes)

    def rsqrt_small(out, in_, tmp, scale, eps, reads, writes):
        act(tmp, in_, AF.Sqrt, reads=reads, writes=writes, bias=None, scale=None) if False else None
        vts(tmp, in_, scale, eps, ALU.mult, ALU.add, reads=reads, writes=writes)
        act(tmp, tmp, AF.Sqrt, reads=writes, writes=writes)
        vrecip(out, tmp, reads=writes, writes=writes)

    def rsqrt_act(out, in_, scale, eps_col, n, reads, writes):
        act(out, in_, AF.Ln, reads=list(reads) + [epsT], writes=writes, scale=scale, bias=epsT[0:n, eps_col:eps_col + 1])
        act(out, out, AF.Exp, reads=writes, writes=writes, scale=-0.5)

    pg = [pst(f"pg{i}", [128, 512]) for i in range(2)]
    pT = pst("pT", [128, 1024], BF16)
    pM = pst("pM", [128, 512])
    pA = pst("pA", [128, 512])
    pB = pst("pB", [128, 512])
    pC = pst("pC", [128, 512])
    pD = pst("pD", [128, 512])

    cst = sb(es_top, "cst", [128, C_END])
    pvec = sb(es_top, "pvec", [128, PV_END])
    omu = sb(es_top, "omu", [128, 13])
    omka = sb(es_top, "omka", [128, 4])
    identb = sb(es_top, "identb", [128, 128], BF16)
    winb = sb(es_top, "winb", [128, 8, D_IN], BF16)
    woutb = sb(es_top, "woutb", [128, 8, D], BF16)
    wd = sb(es_top, "wd", [64, 512])
    wa = sb(es_top, "wa", [128, 512])
    pw = sb(es_top, "pw", [128, 4, 128])
    normf = sb(es_top, "normf", [128, D])
    epsT = sb(es_top, "epsT", [128, 4])

    ident = cst[:, C_ID:C_ID + 128]
    onesblk = cst[:, C_ONES:C_ONES + 128]

    S.dma("sync", cst[:], cst_d, writes=[cst])
    S.dma("sync", pvec[:], pvec_d, writes=[pvec])
    S.dma("sync", wd[:], wdec, writes=[wd])
    S.dma("sync", wa[64:128, :], waaa, writes=[wa])
    S.dma("sync", pw[:], poolw.rearrange("g c e -> c g e"), writes=[pw])
    S.dma("sync", normf[:], normf_d.partition_broadcast(128), writes=[normf])
    vcopy(identb[:], ident, reads=[cst], writes=[identb])
    memset(epsT[:, 0:1], NORM_EPS, writes=[epsT])
    memset(epsT[:, 1:2], GN_EPS, writes=[epsT])
    memset(epsT[:, 2:3], L2_EPS, writes=[epsT])
    vts(omka[:], pvec[:, PV_KA:PV_KA + 4], -1.0, 1.0, ALU.mult, ALU.add, reads=[pvec], writes=[omka])

    with ExitStack() as es:
        stg = [sb(es, f"stg{i}", [128, D_IN]) for i in range(3)]
        for dc in range(8):
            st = stg[dc % 3]
            S.dma("sync", st[:], w_in[dc * 128:(dc + 1) * 128, :], writes=[st])
            h = D_IN // 2
            vts(winb[:, dc, 0:h], st[:, 0:h], pvec[:, PV_NW + dc:PV_NW + dc + 1], None, ALU.mult, None, reads=[st, pvec], writes=[winb])
            act(winb[:, dc, h:], st[:, h:], AF.Copy, reads=[st, pvec], writes=[winb], scale=pvec[:, PV_NW + dc:PV_NW + dc + 1])
        S.barrier()
        chk("W")

    def final_tile(es_tiles, n, x_t, oT_list, out_dram_ap, out_T):
        res, sq, ssum, tmp1, rstd, yo = es_tiles
        for half in range(2):
            bank = pD if half == 0 else pC
            for fc in range(8):
                mm(bank[0:n, :], oT_list[fc], woutb[:, fc, half * 512:(half + 1) * 512], fc == 0, fc == 7,
                   reads=[oT_list_T, woutb], writes=[bank])
            vtt(res[0:n, half * 512:(half + 1) * 512], bank[0:n, :], x_t[0:n, half * 512:(half + 1) * 512], ALU.add,
                reads=[bank, x_t], writes=[res])
        act(sq[0:n, :], res[0:n, :], AF.Square, reads=[res], writes=[sq])
        vred(ssum[0:n, :], sq[0:n, :], reads=[sq], writes=[ssum])
        rsqrt_small(rstd[0:n, :], ssum[0:n, :], tmp1[0:n, :], 1.0 / D, NORM_EPS, reads=[ssum], writes=[tmp1, rstd])
        vstt(yo[0:n, :], res[0:n, :], rstd[0:n, 0:1], normf[0:n, :], ALU.mult, ALU.mult, reads=[res, rstd, normf], writes=[yo])
        S.dma("sync", out_dram_ap, yo[0:n, :], reads=[yo], writes=[out_T])

    oT_list_T = None

    with ExitStack() as es:
        browB = sb(es, "browB", [NS, 3584])
        S.dma("sync", browB[:], browB_d.partition_broadcast(NS), writes=[browB])
        x_s = sb(es, "x_s", [NS, D])
        S.dma("sync", x_s[:], xs, writes=[x_s])
        hTs = sb(es, "hTs", [128, 8, DB + NS], BF16)
        graw_s = sb(es, "graw_s", [NS, 512])
        u_s = sb(es, "u_s", [NS, 512])
        gp_s = sb(es, "gp_s", [NS, 512])
        bonus_s = sb(es, "bonus_s", [NS, 512])
        st8 = sb(es, "st8", [NS, 8])
        st8b = sb(es, "st8b", [NS, 8])
        st8c = sb(es, "st8c", [NS, 8])

        def v3(ap):
            return ap.rearrange("p (h k) -> p h k", k=64)

        def bc8(ap8):
            return ap8.unsqueeze(2).to_broadcast([NS, 8, 64])

        with ExitStack() as e1:
            browA = sb(e1, "browA", [NS, D_SHIFT])
            S.dma("sync", browA[:], browA_d.partition_broadcast(NS), writes=[browA])
            omka_b = sb(e1, "omka_b", [NS, 512])
            vts(omka_b[:], browB[:, BRB_KA:BRB_KA + 512], -1.0, 1.0, ALU.mult, ALU.add, reads=[browB], writes=[omka_b])
            sq_s = sb(e1, "sq_s", [NS, D])
            ss_s = sb(e1, "ss_s", [NS, 1])
            t1_s = sb(e1, "t1_s", [NS, 1])
            rstd_s = sb(e1, "rstd_s", [NS, 1])
            xn_s = sb(e1, "xn_s", [NS, D], BF16)
            act(sq_s[:], x_s[:], AF.Square, reads=[x_s], writes=[sq_s])
            vred(ss_s[:], sq_s[:], reads=[sq_s], writes=[ss_s])
            rsqrt_small(rstd_s[:], ss_s[:], t1_s[:], 1.0 / D, NORM_EPS, reads=[ss_s], writes=[t1_s, rstd_s])
            vts(xn_s[:], x_s[:], rstd_s[:, 0:1], None, ALU.mult, None, reads=[x_s, rstd_s], writes=[xn_s])
            memset(hTs[:, :, 0:DB], 0.0, writes=[hTs])
            for dc in range(8):
                tr(pT[:, dc * 128:dc * 128 + NS], xn_s[:, dc * 128:(dc + 1) * 128], identb[0:NS, 0:NS], reads=[xn_s, identb], writes=[pT])
            vcopy(hTs[:, :, DB:DB + NS], pT[:].rearrange("p (c t) -> p c t", t=128)[:, :, 0:NS], reads=[pT], writes=[hTs])

            p_s = sb(e1, "p_s", [NS, D_SHIFT])
            prev_s = sb(e1, "prev_s", [NS, D_SHIFT])
            col_chunks = [(0, 512), (512, 512), (1024, 512), (1536, 128)]
            kk_ = 0
            for (c0, n) in col_chunks:
                bank = pg[kk_ % 2]; kk_ += 1
                for dc in range(8):
                    mm(bank[0:NS, 0:n], hTs[:, dc, DB:DB + NS], winb[:, dc, c0:c0 + n], dc == 0, dc == 7, reads=[hTs, winb], writes=[bank])
                act(p_s[:, c0:c0 + n], bank[0:NS, 0:n], AF.Copy, reads=[bank], writes=[p_s])
                bank = pg[kk_ % 2]; kk_ += 1
                for dc in range(8):
                    mm(bank[0:NS, 0:n], hTs[:, dc, 0:NS], winb[:, dc, c0:c0 + n], dc == 0, dc == 7, reads=[hTs, winb], writes=[bank])
                vcopy(prev_s[:, c0:c0 + n], bank[0:NS, 0:n], reads=[bank], writes=[prev_s])
            for (c0, dst, fn) in [(1664, graw_s, AF.Silu), (2176, u_s, AF.Copy), (2688, gp_s, AF.Silu)]:
                bank = pg[kk_ % 2]; kk_ += 1
                for dc in range(8):
                    mm(bank[0:NS, :], hTs[:, dc, DB:DB + NS], winb[:, dc, c0:c0 + 512], dc == 0, dc == 7, reads=[hTs, winb], writes=[bank])
                act(dst[:], bank[0:NS, :], fn, reads=[bank], writes=[dst])
            S.dma("sync", prev_s[0:DB, :], sshift, writes=[prev_s])
            S.dma("sync", nss[:], p_s[NS - DB:NS, :], reads=[p_s], writes=[nss])
            S.dma("sync", nps[:, 0:11, :], spool.rearrange("(b j) c -> b j c", j=15)[:, 4:15, :], writes=[nps])
            for t in range(DT):
                S.dma("sync", nps[:, 11 + t, :], u_s[t * DB:(t + 1) * DB, :], reads=[u_s], writes=[nps])

            vtt(prev_s[:], prev_s[:], p_s[:], ALU.subtract, reads=[prev_s, p_s], writes=[prev_s])
            vtt(prev_s[:], prev_s[:], browA[:], ALU.mult, reads=[prev_s, browA], writes=[prev_s])
            vtt(prev_s[:], prev_s[:], p_s[:], ALU.add, reads=[prev_s, p_s], writes=[prev_s])
            ps_s = prev_s
            r_s = ps_s[:, 0:512]
            k_s = ps_s[:, 512:1024]
            v_s = ps_s[:, 1024:1536]

            lT = sb(e1, "lT", [128, NS])
            tr(pM[:, 0:NS], ps_s[:, 1536:1664], ident[0:NS, 0:NS], reads=[ps_s, cst], writes=[pM])
            act(lT[0:64, :], pM[0:64, 0:NS], AF.Tanh, reads=[pM], writes=[lT])
            act(lT[64:128, :], pM[64:128, 0:NS], AF.Copy, reads=[pM], writes=[lT])
            sg_s = sb(e1, "sg_s", [NS, 512])
            a_s = sb(e1, "a_s", [NS, 512])
            mm(pA[0:NS, :], lT[0:64, :], wd[:, :], True, True, reads=[lT, wd], writes=[pA])
            vtt(sg_s[:], pA[0:NS, :], browB[:, BRB_W0:BRB_W0 + 512], ALU.add, reads=[pA, browB], writes=[sg_s])
            act(sg_s[:], sg_s[:], AF.Sigmoid, reads=[sg_s], writes=[sg_s])
            mm(pB[0:NS, :], lT[64:128, :], wa[64:128, :], True, True, reads=[lT, wa], writes=[pB])
            vtt(a_s[:], pB[0:NS, :], browB[:, BRB_A0:BRB_A0 + 512], ALU.add, reads=[pB, browB], writes=[a_s])
            act(a_s[:], a_s[:], AF.Sigmoid, reads=[a_s], writes=[a_s])

            pk = sb(e1, "pk", [NS, 4, 512])
            PQ = {1: 0, 2: 1, 4: 2, 5: 3}
            tmpA = sb(e1, "tmpA", [NS, 512])
            tmpB = sb(e1, "tmpB", [NS, 512])
            act(pk[:, PQ[1], :], sg_s[:], AF.Exp, reads=[sg_s], writes=[pk], scale=-C0)
            vtt(tmpA[:], k_s, browB[:, BRB_KK:BRB_KK + 512], ALU.mult, reads=[ps_s, browB], writes=[tmpA])
            vtt(tmpB[:], tmpA[:], tmpA[:], ALU.mult, reads=[tmpA], writes=[tmpB])
            vred(st8[:], v3(tmpB[:]), reads=[tmpB], writes=[st8])
            rsqrt_small(st8b[:], st8[:], st8c[:], 1.0, L2_EPS, reads=[st8], writes=[st8c, st8b])
            vtt(v3(tmpA[:]), v3(tmpA[:]), bc8(st8b[:]), ALU.mult, reads=[tmpA, st8b], writes=[tmpA])
            vts(pk[:, PQ[4], :], tmpA[:], -1.0, None, ALU.mult, None, reads=[tmpA], writes=[pk])
            vtt(pk[:, PQ[5], :], tmpA[:], a_s[:], ALU.mult, reads=[tmpA, a_s], writes=[pk])
            vtt(tmpB[:], a_s[:], browB[:, BRB_KA:BRB_KA + 512], ALU.mult, reads=[a_s, browB], writes=[tmpB])
            vtt(tmpB[:], tmpB[:], omka_b[:], ALU.add, reads=[tmpB, omka_b], writes=[tmpB])
            vtt(pk[:, PQ[2], :], k_s, tmpB[:], ALU.mult, reads=[ps_s, tmpB], writes=[pk])
            vtt(tmpB[:], r_s, browB[:, BRB_RK:BRB_RK + 512], ALU.mult, reads=[ps_s, browB], writes=[tmpB])
            vtt(tmpB[:], tmpB[:], pk[:, PQ[2], :], ALU.mult, reads=[tmpB, pk], writes=[tmpB])
            vred(st8[:], v3(tmpB[:]), reads=[tmpB], writes=[st8])
            vtt(v3(bonus_s[:]), v3(v_s), bc8(st8[:]), ALU.mult, reads=[ps_s, st8], writes=[bonus_s])
            sview = scr1[:].rearrange("q t b h k -> q (t b) (h k)")
            S.dma("sync", sview[0], r_s, reads=[ps_s], writes=[scr1])
            S.dma("sync", sview[3], v_s, reads=[ps_s], writes=[scr1])
            for qq, slot in PQ.items():
                S.dma("sync", sview[qq], pk[:, slot, :], reads=[pk], writes=[scr1])
            S.finish([scr1], engname="sync")
            S.barrier()
            chk("S1")

        with ExitStack() as e2:
            sIn = sb(e2, "sIn", [128, 6, DT, 64])
            S.dma("sync", sIn[:], scr1[:].rearrange("q t b h k -> (b h) q t k"), reads=[scr1], writes=[sIn])
            St = sb(e2, "St", [128, 64, 64])
            S.dma("sync", St[:].rearrange("p v k -> p (v k)"), swkv, writes=[St])
            tmpS = sb(e2, "tmpS", [128, 64, 64])
            sa = sb(e2, "sa", [128, 64])
            yS = sb(e2, "yS", [128, DT, 64])
            stgo = [sb(e2, f"stgo{i}", [128, D]) for i in range(3)]
            for fc in range(8):
                so = stgo[fc % 3]
                S.dma("sync", so[:], w_out[fc * 128:(fc + 1) * 128, :], writes=[so])
                act(woutb[:, fc, :], so[:], AF.Copy, reads=[so], writes=[woutb])

            def bv(ap):
                return ap.unsqueeze(1).to_broadcast([128, 64, 64])

            def bk(ap):
                return ap.unsqueeze(2).to_broadcast([128, 64, 64])

            for t in range(DT):
                q = lambda i: sIn[:, i, t, :]
                vtt(tmpS[:], St[:], bv(q(4)), ALU.mult, reads=[St, sIn], writes=[tmpS])
                vred(sa[:], tmpS[:], reads=[tmpS], writes=[sa])
                vtt(St[:], St[:], bv(q(1)), ALU.mult, reads=[St, sIn], writes=[St])
                vtt(tmpS[:], bk(sa[:]), bv(q(5)), ALU.mult, reads=[sa, sIn], writes=[tmpS])
                vtt(St[:], St[:], tmpS[:], ALU.add, reads=[St, tmpS], writes=[St])
                vtt(tmpS[:], bk(q(3)), bv(q(2)), ALU.mult, reads=[sIn], writes=[tmpS])
                vtt(St[:], St[:], tmpS[:], ALU.add, reads=[St, tmpS], writes=[St])
                vtt(tmpS[:], St[:], bv(q(0)), ALU.mult, reads=[St, sIn], writes=[tmpS])
                vred(yS[:, t, :], tmpS[:], reads=[tmpS], writes=[yS])
            S.dma("sync", nws[:], St[:].rearrange("p v k -> p (v k)"), reads=[St], writes=[nws])
            S.dma("sync", scr2[:].rearrange("b h t v -> (b h) t v"), yS[:], reads=[yS], writes=[scr2])
            S.finish([scr2, nws], engname="sync")
            S.barrier()
            chk("S2")

        with ExitStack() as e3:
            yT = sb(e3, "yT", [NS, 512])
            tmpA = sb(e3, "tmpA3", [NS, 512])
            for t in range(DT):
                S.dma("sync", yT[t * DB:(t + 1) * DB, :].rearrange("b (h v) -> b h v", v=64), scr2[:][:, :, t, :], reads=[scr2], writes=[yT])
            vred(st8[:], v3(yT[:]), reads=[yT], writes=[st8])
            vts(st8[:], st8[:], 1.0 / 64, None, ALU.mult, None, reads=[st8], writes=[st8])
            vtt(v3(yT[:]), v3(yT[:]), bc8(st8[:]), ALU.subtract, reads=[yT, st8], writes=[yT])
            vtt(tmpA[:], yT[:], yT[:], ALU.mult, reads=[yT], writes=[tmpA])
            vred(st8[:], v3(tmpA[:]), reads=[tmpA], writes=[st8])
            rsqrt_small(st8b[:], st8[:], st8c[:], 1.0 / 64, GN_EPS, reads=[st8], writes=[st8c, st8b])
            vtt(v3(yT[:]), v3(yT[:]), bc8(st8b[:]), ALU.mult, reads=[yT, st8b], writes=[yT])
            vtt(yT[:], yT[:], browB[:, BRB_GW:BRB_GW + 512], ALU.mult, reads=[yT, browB], writes=[yT])
            vtt(yT[:], yT[:], browB[:, BRB_GB:BRB_GB + 512], ALU.add, reads=[yT, browB], writes=[yT])
            vtt(yT[:], yT[:], bonus_s[:], ALU.add, reads=[yT, bonus_s], writes=[yT])
            vtt(yT[:], yT[:], graw_s[:], ALU.mult, reads=[yT, graw_s], writes=[yT])
            oTs = sb(e3, "oTs", [128, 8, NS], BF16)
            for fb in range(4):
                tr(pA[:, fb * 64:fb * 64 + NS], yT[:, fb * 128:(fb + 1) * 128], ident[0:NS, 0:NS], reads=[yT, cst], writes=[pA])
            vcopy(oTs[:, 0:4, :], pA[:, 0:4 * NS].rearrange("p (f t) -> p f t", t=NS), reads=[pA], writes=[oTs])

            uext = sb(e3, "uext_s", [128, 4, DB, 19])
            sp0 = sb(e3, "sp0", [120, 512])
            sp1 = sb(e3, "sp1", [120, 512])
            S.dma("sync", sp0[:], spool[0:120, :], writes=[sp0])
            S.dma("sync", sp1[:], spool[120:240, :], writes=[sp1])
            for g in range(4):
                tr(pB[:, 0:120], sp0[:, g * 128:(g + 1) * 128], ident[0:120, 0:120], reads=[sp0, cst], writes=[pB])
                tr(pB[:, 128:248], sp1[:, g * 128:(g + 1) * 128], ident[0:120, 0:120], reads=[sp1, cst], writes=[pB])
                vcopy(uext[:, g, 0:8, 0:15], pB[:, 0:120].rearrange("p (b j) -> p b j", j=15), reads=[pB], writes=[uext])
                vcopy(uext[:, g, 8:16, 0:15], pB[:, 128:248].rearrange("p (b j) -> p b j", j=15), reads=[pB], writes=[uext])
                tr(pM[:, 0:NS], u_s[:, g * 128:(g + 1) * 128], ident[0:NS, 0:NS], reads=[u_s, cst], writes=[pM])
                vcopy(uext[:, g, :, 15:19], pM[:, 0:NS].rearrange("p (t b) -> p b t", b=DB), reads=[pM], writes=[uext])
            s2 = sb(e3, "s2_s", [128, 4, DB, 19])
            s4 = sb(e3, "s4_s", [128, 3, DB, 19])
            s8 = sb(e3, "s8_s", [128, 2, DB, 19])
            s16 = sb(e3, "s16_s", [128, 1, DB, 19])
            d_s = sb(e3, "d_s", [128, 4, DT, DB])
            vtt(s2[:, :, :, 1:19], uext[:, :, :, 1:19], uext[:, :, :, 0:18], ALU.add, reads=[uext], writes=[s2])
            vtt(s4[:, :, :, 3:19], s2[:, 1:4, :, 3:19], s2[:, 1:4, :, 1:17], ALU.add, reads=[s2], writes=[s4])
            vtt(s8[:, :, :, 7:19], s4[:, 1:3, :, 7:19], s4[:, 1:3, :, 3:15], ALU.add, reads=[s4], writes=[s8])
            vtt(s16[:, :, :, 15:19], s8[:, 1:2, :, 15:19], s8[:, 1:2, :, 7:11], ALU.add, reads=[s8], writes=[s16])
            tots = [(s2, 0), (s4, 1), (s8, 2), (s16, 3)]
            for g in range(4):
                tt, off = tots[g]
                vstt(d_s[:, g, :, :].rearrange("p t b -> p b t"), tt[:, g - off, :, 15:19], 1.0 / WINS[g], uext[:, g, :, 15:19],
                     ALU.mult, ALU.subtract, reads=[tt, uext], writes=[d_s])
            gpT = sb(e3, "gpT", [128, 4, NS])
            for g in range(4):
                tr(pM[:, 64 + g * 64:64 + g * 64 + NS], gp_s[:, g * 128:(g + 1) * 128], ident[0:NS, 0:NS], reads=[gp_s, cst], writes=[pM])
            vcopy(gpT[:], pM[:, 64:64 + 4 * NS].rearrange("p (g t) -> p g t", t=NS), reads=[pM], writes=[gpT])
            for g in range(4):
                mm(pA[:, g * 64:g * 64 + NS], pw[:, g, :], d_s[:, g, :, :].rearrange("p t b -> p (t b)"), True, True, reads=[pw, d_s], writes=[pA])
            for g in range(4):
                vstt(oTs[:, 4 + g, :], pA[:, g * 64:g * 64 + NS], pvec[:, PV_PS + g:PV_PS + g + 1], gpT[:, g, :], ALU.mult, ALU.mult,
                     reads=[pA, pvec, gpT], writes=[oTs])

            sq = sb(e3, "sq2_s", [NS, D]); ssum = sb(e3, "ssum_s", [NS, 1])
            tmp1 = sb(e3, "tmp1_s", [NS, 1]); rstd = sb(e3, "rstd2_s", [NS, 1]); yo = sb(e3, "yo_s", [NS, D])
            oT_list_T = oTs
            final_tile((x_s, sq, ssum, tmp1, rstd, yo), NS, x_s, [oTs[:, fc, :] for fc in range(8)], ys[:], ys)
            S.finish([ys, nss, nps], engname="sync")
            S.barrier()
            chk("S3")

    with ExitStack() as es:
        def sbl(name, shape, dt=F32, n=2):
            return [sb(es, f"{name}_{i}", shape, dt) for i in range(n)]

        xt = sbl("xt", [128, D])
        yo = sb(es, "yo", [128, D])
        ssx = sb(es, "ssx", [128, 1]); t1x = sb(es, "t1x", [128, 1]); rsx = sb(es, "rsx", [128, 1])
        xnb = sb(es, "xnb", [128, D], BF16)
        sqx = xnb
        hT = sb(es, "hT", [128, 8, TB], BF16)
        praw = sbl("praw", [128, 4, TB + 1])
        halo = sb(es, "halo", [128, 13])
        omu = sb(es, "omu2", [128, 13])
        psr = sb(es, "psr", [128, 4, TB]); psk = sb(es, "psk", [128, 4, TB]); psv = sb(es, "psv", [128, 4, TB])
        ps12 = sb(es, "ps12", [128, TB])
        psx = [T(g_[:, i, :], f"psx{gi_}_{i}", buf=g_.b) for gi_, g_ in enumerate([psr, psk, psv]) for i in range(4)] + [ps12]
        sg = sb(es, "sg", [128, 4, TB]); av = sb(es, "av", [128, 4, TB])
        gsil = sbl("gsil", [128, 4, TB], BF16)
        gpsil = sb(es, "gpsil", [128, 4, TB], BF16)
        uext = sb(es, "uext", [128, 4, 15 + TB])
        th = sb(es, "th", [64, TB])
        wbig = [sb(es, f"wbig{i}", [128, 4, TB]) for i in range(4)]
        w1, w2, w3, w4 = wbig
        srot = [T(wbig[i][:].rearrange("p f t -> p (f t)")[:, 0:15 + TB], f"srot{i}", buf=wbig[i].b) for i in range(4)]
        kkn = sb(es, "kkn", [128, 4, TB]); kmod = sb(es, "kmod", [128, 4, TB]); bv_ = sb(es, "bv_", [128, 4, TB])
        cum = sb(es, "cum", [128, 4, TB])
        dpl = T(kkn[:, 0, :], "dpl", buf=kkn.b)
        at = sbl("at", [64, 4, 2, TB], BF16)
        rt = sbl("rt", [64, 4, 2, TB], BF16)
        bt = sb(es, "bt", [64, 4, 2, TB], BF16)
        kt = sb(es, "kt", [64, 4, 2, TB], BF16)
        bh = sb(es, "bh", [128, 4, TB], BF16); kh = sb(es, "kh", [128, 4, TB], BF16); vb = sb(es, "vb", [128, 4, TB], BF16)
        bon = sbl("bon", [128, 4, TB])
        gC = sbl("gC", [64, 4, 2, NCH])
        VT = [[sb(es, f"VT{p}{c}", [64, 512], BF16) for c in range(NCH)] for p in range(2)]
        BKT = [[sb(es, f"BKT{p}{c}", [64, 1024], BF16) for c in range(NCH)] for p in range(2)]
        Aak = [[sb(es, f"Aak{p}{c}", [64, 512], BF16) for c in range(NCH)] for p in range(2)]
        Arb = [[sb(es, f"Arb{p}{c}", [64, 512], BF16) for c in range(NCH)] for p in range(2)]
        Ark = [[sb(es, f"Ark{p}{c}", [64, 512], BF16) for c in range(NCH)] for p in range(2)]
        Minv = [[sb(es, f"Minv{p}{c}", [64, 512], BF16) for c in range(NCH)] for p in range(2)]
        Nsb = [sb(es, f"Nsb{c}", [64, 512], BF16) for c in range(NCH)]
        NTsb = [sb(es, f"NTsb{c}", [64, 512], BF16) for c in range(NCH)]
        Xa0 = [sb(es, f"Xa0{c}", [64, 512], BF16) for c in range(NCH)]
        XTa0 = [sb(es, f"XTa0{c}", [64, 512], BF16) for c in range(NCH)]
        Qtmp = [sb(es, f"Qtmp{c}", [64, 512], BF16) for c in range(NCH)]
        ST = sb(es, "ST", [64, 8, 64]); STb = sb(es, "STb", [64, 8, 64], BF16)
        Wsb = sb(es, "Wsb", [64, 512], BF16); Usb = sb(es, "Usb", [64, 512], BF16)
        yc = sb(es, "yc", [64, 512]); ysq = sb(es, "ysq", [64, 512])
        STt = T(ysq[:].rearrange("p (h v) -> p h v", v=64), "STt", buf=ysq.b)
        m8 = sb(es, "m8", [64, 8]); v8 = sb(es, "v8", [64, 8]); r8 = sb(es, "r8", [64, 8]); t8 = sb(es, "t8", [64, 8])
        o1 = sb(es, "o1", [128, 4, 64])
        oT = sbl("oT", [128, 8, TB], BF16)
        ssum = sb(es, "ssum", [128, 1]); tmp1 = sb(es, "tmp1", [128, 1]); rstd = sb(es, "rstd", [128, 1])
        ppT = T(ysq[0:16, :], "ppT", buf=ysq.b); m13 = sb(es, "m13", [13, 128])
        SvT = T(yc[:].rearrange("p (h k) -> p h k", k=64), "SvT", buf=yc.b)

        memset(halo[:], 0.0, writes=[halo])
        memset(uext[:, :, 0:15], 0.0, writes=[uext])
        memset(ST[:], 0.0, writes=[ST])
        memset(STb[:], 0.0, writes=[STb])
        vts(omu[:], pvec[:, PV_MU:PV_MU + 13], -1.0, 1.0, ALU.mult, ALU.add, reads=[pvec], writes=[omu])

        def b8(ap):
            return ap.unsqueeze(1).to_broadcast([64, 8, 64])

        def h3(ap):
            return ap.rearrange("p (h v) -> p h v", v=64)

        def hc(h):
            return slice(h * 64, (h + 1) * 64)

        maskUs = b8(cst[0:64, C_MUS:C_MUS + 64])
        maskUi = b8(cst[0:64, C_MUI:C_MUI + 64])
        maskLs = b8(cst[0:64, C_MLS:C_MLS + 64])
        ident8 = b8(cst[0:64, C_ID:C_ID + 64])
        rstm = cst[:, C_RST:C_RST + 512]
        st = dict(gk=0, ak=0)
        pT32 = T(pT[:].bitcast(F32), "pT32", buf=pT.b)
        abanks = [pA, pB, pg[0], pg[1], pM, pT32]

        def nextbank():
            b = abanks[st["ak"] % len(abanks)]
            st["ak"] += 1
            return b

        def front(tb):
            pb = tb % 2
            t0 = tb * TB
            x_t = xt[pb]
            S.dma("sync", x_t[:], xp[t0:t0 + TB, :], writes=[x_t])
            act(sqx[:], x_t[:], AF.Square, reads=[x_t], writes=[sqx, ssx], accum=ssx[:])
            rsqrt_act(rsx[:], ssx[:], 1.0 / D, 0, 128, reads=[ssx], writes=[rsx])
            act(xnb[:], x_t[:], AF.Copy, reads=[x_t, rsx], writes=[xnb], scale=rsx[:, 0:1])
            yield
            for dc in range(8):
                tr(pT[:, dc * 128:(dc + 1) * 128], xnb[:, dc * 128:(dc + 1) * 128], identb[:], reads=[xnb, identb], writes=[pT])
            vcopy(hT[:].rearrange("p c t -> p (c t)"), pT[:], reads=[pT], writes=[hT])
            yield

            def gemm_group(ebs):
                bank = pg[st["gk"] % 2]
                st["gk"] += 1
                for i, eb in enumerate(ebs):
                    for dc in range(8):
                        mm(bank[:, i * TB:(i + 1) * TB], winb[:, dc, eb * 128:(eb + 1) * 128], hT[:, dc, :], dc == 0, dc == 7,
                           reads=[winb, hT], writes=[bank])
                return bank

            for gi, ebs in enumerate([[0, 1, 2, 3], [4, 5, 6, 7], [8, 9, 10, 11], [12]]):
                bank = gemm_group(ebs)
                yield
                n = len(ebs)
                pr = praw[gi % 2]
                e0 = ebs[0]
                gT = [psr, psk, psv, ps12][gi]
                dst = gT[:, 0:n, :] if gi < 3 else ps12[:].unsqueeze(1)
                mub = pvec[:, PV_MU + e0:PV_MU + e0 + n].unsqueeze(2).to_broadcast([128, n, TB])
                act(pr[:, 0:n, 0:1], halo[:, e0:e0 + n].unsqueeze(2), AF.Copy, reads=[halo], writes=[pr])
                act(pr[:, 0:n, 1:TB + 1], bank[:, 0:n * TB].rearrange("p (e t) -> p e t", t=TB), AF.Copy, reads=[bank], writes=[pr])
                vtt(dst, pr[:, 0:n, 0:TB], pr[:, 0:n, 1:TB + 1], ALU.subtract, reads=[pr], writes=[gT])
                vtt(dst, dst, mub, ALU.mult, reads=[gT, pvec], writes=[gT])
                vtt(dst, dst, pr[:, 0:n, 1:TB + 1], ALU.add, reads=[gT, pr], writes=[gT])
                act(halo[:, e0:e0 + n].unsqueeze(2), pr[:, 0:n, TB:TB + 1], AF.Copy, reads=[pr], writes=[halo])
                yield
            bank = gemm_group([13, 14, 15, 16])
            act(gsil[pb][:].rearrange("p f t -> p (f t)"), bank[:, :], AF.Silu, reads=[bank], writes=[gsil[pb]])
            yield
            bank = gemm_group([17, 18, 19, 20])
            act(uext[:, :, 15:15 + TB], bank[:, :].rearrange("p (g t) -> p g t", t=TB), AF.Copy, reads=[bank], writes=[uext])
            yield
            bank = gemm_group([21, 22, 23, 24])
            act(gpsil[:].rearrange("p g t -> p (g t)"), bank[:, :], AF.Silu, reads=[bank], writes=[gpsil])
            yield

            act(th[:], psx[12][0:64, :], AF.Tanh, reads=[psx[12]], writes=[th])
            for fb in range(4):
                mm(pA[:, fb * TB:(fb + 1) * TB], wd[:, fb * 128:(fb + 1) * 128], th[:], True, True, reads=[wd, th], writes=[pA])
            for fb in range(4):
                mm(pM[:, fb * TB:(fb + 1) * TB], wa[64:128, fb * 128:(fb + 1) * 128], psx[12][64:128, :], True, True, reads=[wa, psx[12]], writes=[pM])
            for fb in range(4):
                act(sg[:, fb, :], pA[:, fb * TB:(fb + 1) * TB], AF.Sigmoid, reads=[pA, pvec], writes=[sg], bias=pvec[:, PV_W0 + fb:PV_W0 + fb + 1])
                act(av[:, fb, :], pM[:, fb * TB:(fb + 1) * TB], AF.Sigmoid, reads=[pM, pvec], writes=[av], bias=pvec[:, PV_A0 + fb:PV_A0 + fb + 1])
            yield

            def pb4(col):
                return pvec[:, col:col + 4].unsqueeze(2).to_broadcast([128, 4, TB])

            def f2(t_):
                return t_[:].rearrange("p f t -> p (f t)")

            bomk = omka[:, 0:4].unsqueeze(2).to_broadcast([128, 4, TB])
            c3 = cum[:].rearrange("p f (c t) -> p (f c) t", t=CH)
            p0, p1 = slice(0, 64), slice(64, 128)
            act(vb[:], psv[:], AF.Copy, reads=[psv], writes=[vb])
            vtt(w1[:], psk[:], pb4(PV_KK), ALU.mult, reads=[psk, pvec], writes=[w1])
            vtt(w2[:], w1[:], w1[:], ALU.mult, reads=[w1], writes=[w2])
            mm(pM[:, :], onesblk, f2(w2), True, True, reads=[cst, w2], writes=[pM])
            S.op("vector", lambda e: e.tensor_tensor_scan(out=f2(cum), data0=rstm, data1=f2(sg), initial=0.0, op0=ALU.mult, op1=ALU.add),
                 reads=[cst, sg], writes=[cum], cost=1.2)
            act(f2(w2), pM[:, :], AF.Ln, reads=[pM, epsT], writes=[w2], bias=epsT[:, 2:3])
            act(w2[:], w2[:], AF.Exp, reads=[w2], writes=[w2], scale=-0.5)
            vtt(w3[:], av[:], pb4(PV_KA), ALU.mult, reads=[av, pvec], writes=[w3])
            vtt(w3[:], w3[:], bomk, ALU.add, reads=[w3, omka], writes=[w3])
            vtt(kmod[:], psk[:], w3[:], ALU.mult, reads=[psk, w3], writes=[kmod])
            vtt(w4[:], cum[:], sg[:], ALU.subtract, reads=[cum, sg], writes=[w4])
            act(w4[:], w4[:], AF.Exp, reads=[w4], writes=[w4], scale=-C0)
            vtt(w3[:], psr[:], pb4(PV_RK), ALU.mult, reads=[psr, pvec], writes=[w3])
            vtt(w3[:], w3[:], kmod[:], ALU.mult, reads=[w3, kmod], writes=[w3])
            mm(pA[:, :], onesblk, f2(w3), True, True, reads=[cst, w3], writes=[pA])
            yield
            vstt(kkn[:], w1[:], -1.0, w2[:], ALU.mult, ALU.mult, reads=[w1, w2], writes=[kkn])
            vstt(bv_[:], kkn[:], -1.0, av[:], ALU.mult, ALU.mult, reads=[kkn, av], writes=[bv_])
            act(w1[:], cum[:], AF.Exp, reads=[cum], writes=[w1], scale=-C0)
            act(w2[:], cum[:], AF.Exp, reads=[cum], writes=[w2], scale=C0)
            vtt(at[pb][:, :, 1, :], kkn[p1, :, :], w4[p1, :, :], ALU.mult, reads=[kkn, w4], writes=[at[pb]])
            vtt(at[pb][:, :, 0, :], kkn[p0, :, :], w4[p0, :, :], ALU.mult, reads=[kkn, w4], writes=[at[pb]])
            vtt(f2(bon[pb]), pA[:, :], f2(psv), ALU.mult, reads=[pA, psv], writes=[bon[pb]])
            vtt(w3[:].rearrange("p f (c t) -> p (f c) t", t=CH), c3[:, :, CH - 1:CH].to_broadcast([128, 4 * NCH, CH]), c3, ALU.subtract,
                reads=[cum], writes=[w3], eng="gpsimd")
            act(w3[:], w3[:], AF.Exp, reads=[w3], writes=[w3], scale=-C0)
            for j in range(2):
                pp = slice(64 * j, 64 * j + 64)
                act(gC[pb][:, :, j, :], cum[pp, :, :].rearrange("p f (c t) -> p f c t", t=CH)[:, :, :, CH - 1], AF.Exp,
                    reads=[cum], writes=[gC[pb]], scale=-C0)
            yield
            for j in range(2):
                pp = slice(64 * j, 64 * j + 64)
                e_ = "vector"
                vtt(rt[pb][:, :, j, :], psr[pp, :, :], w1[pp, :, :], ALU.mult, reads=[psr, w1], writes=[rt[pb]], eng=e_)
                vtt(kt[:, :, j, :], kmod[pp, :, :], w2[pp, :, :], ALU.mult, reads=[kmod, w2], writes=[kt], eng=e_)
                vtt(bt[:, :, j, :], bv_[pp, :, :], w2[pp, :, :], ALU.mult, reads=[bv_, w2], writes=[bt], eng=e_)
            vtt(bh[:], bv_[:], w3[:], ALU.mult, reads=[bv_, w3], writes=[bh])
            vtt(kh[:], kmod[:], w3[:], ALU.mult, reads=[kmod, w3], writes=[kh], eng="gpsimd")
            yield

            L = 15 + TB
            for g in range(4):
                vtt(srot[0][:, 1:], uext[:, g, 1:], uext[:, g, 0:L - 1], ALU.add, reads=[uext], writes=[srot[0]], eng="gpsimd")
                tot = srot[0]
                if g >= 1:
                    vtt(srot[1][:, 3:], srot[0][:, 3:], srot[0][:, 1:L - 2], ALU.add, reads=[srot[0]], writes=[srot[1]], eng="gpsimd")
                    tot = srot[1]
                if g >= 2:
                    vtt(srot[2][:, 7:], srot[1][:, 7:], srot[1][:, 3:L - 4], ALU.add, reads=[srot[1]], writes=[srot[2]], eng="gpsimd")
                    tot = srot[2]
                if g >= 3:
                    vtt(srot[3][:, 15:], srot[2][:, 15:], srot[2][:, 7:L - 8], ALU.add, reads=[srot[2]], writes=[srot[3]], eng="gpsimd")
                    tot = srot[3]
                vstt(dpl[:], tot[:, 15:], 1.0 / WINS[g], uext[:, g, 15:], ALU.mult, ALU.subtract, reads=[tot, uext], writes=[dpl])
                if tb == 0:
                    vtt(dpl[:, 0:16], tot[:, 15:31], cst[:, C_ICNT + g * 16:C_ICNT + (g + 1) * 16], ALU.mult, reads=[tot, cst], writes=[dpl])
                    vtt(dpl[:, 0:16], dpl[:, 0:16], uext[:, g, 15:31], ALU.subtract, reads=[dpl, uext], writes=[dpl])
                mm(pM[:, 0:TB], pw[:, g, :], dpl[:], True, True, reads=[pw, dpl], writes=[pM])
                vstt(oT[pb][:, 4 + g, :], pM[:, 0:TB], pvec[:, PV_PS + g:PV_PS + g + 1], gpsil[:, g, :], ALU.mult, ALU.mult,
                     reads=[pM, pvec, gpsil], writes=[oT[pb]])
                yield
            if tb == NTB - 1:
                for g in range(4):
                    tr(pA[0:16, g * 128:(g + 1) * 128], uext[:, g, TB - 1:TB + 15], ident, reads=[uext, cst], writes=[pA])
                vcopy(ppT[:], pA[0:16, :], reads=[pA], writes=[ppT])
                S.dma("sync", npp[:], ppT[1:16, :], reads=[ppT], writes=[npp])
                tr(pB[0:13, 0:128], halo[:, 0:13], ident, reads=[halo, cst], writes=[pB])
                vcopy(m13[:], pB[0:13, 0:128], reads=[pB], writes=[m13])
                S.dma("sync", nsp[:], m13[:], reads=[m13], writes=[nsp])
            vcopy(uext[:, :, 0:15], uext[:, :, TB:TB + 15], reads=[uext], writes=[uext], eng="gpsimd")
            yield

            css = [slice(c * CH, (c + 1) * CH) for c in range(NCH)]
            for c in range(NCH):
                for qi, srcl in enumerate([bh, kh]):
                    for fb in range(4):
                        tr(pT[0:64, qi * 512 + fb * 128:qi * 512 + (fb + 1) * 128], srcl[:, fb, css[c]], identb[:], reads=[srcl, identb], writes=[pT])
                vcopy(BKT[pb][c][:], pT[0:64, :], reads=[pT], writes=[BKT[pb][c]])
                for fb in range(4):
                    tr(pT[0:64, fb * 128:(fb + 1) * 128], vb[:, fb, css[c]], identb[:], reads=[vb, identb], writes=[pT])
                act(VT[pb][c][:], pT[0:64, 0:512], AF.Copy, reads=[pT], writes=[VT[pb][c]])
                yield

            def hsl(tl, h, c):
                fb, j = divmod(h, 2)
                return tl[:, fb, j, css[c]]

            for (Lt, Rt, mask, dsts) in [(bt, at[pb], maskUs, Nsb), (at[pb], bt, maskLs, NTsb), (kt, at[pb], maskUs, Aak[pb]),
                                         (bt, rt[pb], maskUi, Arb[pb]), (kt, rt[pb], maskUi, Ark[pb])]:
                banks = []
                for c in range(NCH):
                    bank = nextbank()
                    banks.append(bank)
                    for h in range(8):
                        mm(bank[0:64, hc(h)], hsl(Lt, h, c), hsl(Rt, h, c), True, True, reads=[Lt, Rt], writes=[bank])
                for c in range(NCH):
                    vtt(h3(dsts[c][:]), h3(banks[c][0:64, :]), mask, ALU.mult, reads=[banks[c], cst], writes=[dsts[c]])
                yield
            X = list(Nsb); XT = list(NTsb)
            Q = [Qtmp[c] for c in range(NCH)]
            for c in range(NCH):
                vtt(h3(Q[c][:]), h3(Nsb[c][:]), ident8, ALU.add, reads=[Nsb[c], cst], writes=[Q[c]])
            for lvl in range(5):
                Xn = [(Xa0[c] if lvl % 2 == 0 else Nsb[c]) for c in range(NCH)]
                XTn = [(XTa0[c] if lvl % 2 == 0 else NTsb[c]) for c in range(NCH)]
                Qn = [(Minv[pb][c] if lvl % 2 == 0 else Qtmp[c]) for c in range(NCH)]
                banks = []
                for c in range(NCH):
                    bank = nextbank(); banks.append(bank)
                    for h in range(8):
                        mm(bank[0:64, hc(h)], X[c][:, hc(h)], XT[c][:, hc(h)], True, True, reads=[X[c], XT[c]], writes=[bank])
                for c in range(NCH):
                    act(XTn[c][:], banks[c][0:64, :], AF.Copy, reads=[banks[c]], writes=[XTn[c]])
                yield
                if lvl < 4:
                    banks = []
                    for c in range(NCH):
                        bank = nextbank(); banks.append(bank)
                        for h in range(8):
                            mm(bank[0:64, hc(h)], XT[c][:, hc(h)], X[c][:, hc(h)], True, True, reads=[X[c], XT[c]], writes=[bank])
                    for c in range(NCH):
                        act(Xn[c][:], banks[c][0:64, :], AF.Copy, reads=[banks[c]], writes=[Xn[c]])
                    yield
                banks = []
                for c in range(NCH):
                    bank = nextbank(); banks.append(bank)
                    for h in range(8):
                        mm(bank[0:64, hc(h)], XTn[c][:, hc(h)], Q[c][:, hc(h)], True, True, reads=[XTn[c], Q[c]], writes=[bank])
                for c in range(NCH):
                    vtt(Qn[c][:], banks[c][0:64, :], Q[c][:], ALU.add, reads=[banks[c], Q[c]], writes=[Qn[c]])
                X, XT, Q = Xn, XTn, Qn
                yield

        def chain(tb):
            pb = tb % 2
            t0 = tb * TB
            for c in range(NCH):
                cs = slice(c * CH, (c + 1) * CH)
                aT, rT = at[pb], rt[pb]
                VTc, BKTc, Aakc, Arbc, Arkc, Minvc = VT[pb][c], BKT[pb][c], Aak[pb][c], Arb[pb][c], Ark[pb][c], Minv[pb][c]
                for h in range(8):
                    fb, j = divmod(h, 2)
                    mm(pC[0:64, hc(h)], aT[:, fb, j, cs], STb[:, h, :], True, False, reads=[aT, STb], writes=[pC])
                    mm(pC[0:64, hc(h)], Aakc[:, hc(h)], VTc[:, hc(h)], False, True, reads=[Aakc, VTc], writes=[pC])
                act(Wsb[:], pC[0:64, :], AF.Copy, reads=[pC], writes=[Wsb])
                yield
                for h in range(8):
                    mm(pC[0:64, hc(h)], Minvc[:, hc(h)], Wsb[:, hc(h)], True, True, reads=[Minvc, Wsb], writes=[pC])
                act(Usb[:], pC[0:64, :], AF.Copy, reads=[pC], writes=[Usb])
                yield
                for h in range(8):
                    mm(pC[0:64, hc(h)], BKTc[:, hc(h)], Usb[:, hc(h)], True, False, reads=[BKTc, Usb], writes=[pC])
                    mm(pC[0:64, hc(h)], BKTc[:, 512 + h * 64:512 + (h + 1) * 64], VTc[:, hc(h)], False, True, reads=[BKTc, VTc], writes=[pC])
                for h in range(8):
                    fb, j = divmod(h, 2)
                    mm(pD[0:64, hc(h)], rT[:, fb, j, cs], STb[:, h, :], True, False, reads=[rT, STb], writes=[pD])
                    mm(pD[0:64, hc(h)], Arbc[:, hc(h)], Usb[:, hc(h)], False, False, reads=[Arbc, Usb], writes=[pD])
                    mm(pD[0:64, hc(h)], Arkc[:, hc(h)], VTc[:, hc(h)], False, True, reads=[Arkc, VTc], writes=[pD])
                vtt(STt[:], ST[:], gC[pb][:].rearrange("p f j c -> p (f j) c")[:, :, c:c + 1].to_broadcast([64, 8, 64]), ALU.mult,
                    reads=[ST, gC[pb]], writes=[STt])
                vtt(ST[:], STt[:], h3(pC[0:64, :]), ALU.add, reads=[STt, pC], writes=[ST])
                act(STb[:], ST[:], AF.Copy, reads=[ST], writes=[STb])
                yield
                y3 = h3(pD[0:64, :])
                vred(m8[:], y3, reads=[pD], writes=[m8])
                vts(m8[:], m8[:], 1.0 / 64, None, ALU.mult, None, reads=[m8], writes=[m8])
                vtt(h3(yc[:]), y3, m8[:].unsqueeze(2).to_broadcast([64, 8, 64]), ALU.subtract, reads=[pD, m8], writes=[yc])
                act(ysq[:], yc[:], AF.Square, reads=[yc], writes=[ysq])
                vred(v8[:], h3(ysq[:]), reads=[ysq], writes=[v8])
                rsqrt_act(r8[:], v8[:], 1.0 / 64, 1, 64, reads=[v8], writes=[r8])
                vtt(h3(yc[:]), h3(yc[:]), r8[:].unsqueeze(2).to_broadcast([64, 8, 64]), ALU.mult, reads=[yc, r8], writes=[yc], eng="gpsimd")
                yield
                for fb in range(4):
                    tr(pD[:, fb * 64:(fb + 1) * 64], yc[:, fb * 128:(fb + 1) * 128], ident[0:64, 0:64], reads=[yc, cst], writes=[pD])
                for fb in range(4):
                    vts(o1[:, fb, :], pD[:, fb * 64:(fb + 1) * 64], pvec[:, PV_GW + fb:PV_GW + fb + 1], pvec[:, PV_GB + fb:PV_GB + fb + 1],
                        ALU.mult, ALU.add, reads=[pD, pvec], writes=[o1])
                vtt(o1[:], o1[:], bon[pb][:, :, cs], ALU.add, reads=[o1, bon[pb]], writes=[o1], eng="gpsimd")
                vtt(oT[pb][:, 0:4, cs], o1[:], gsil[pb][:, :, cs], ALU.mult, reads=[o1, gsil[pb]], writes=[oT[pb]])
                yield
            x_t = xt[pb]
            for half in range(2):
                bank = pD if half == 0 else pC
                for fc in range(8):
                    mm(bank[:, :], oT[pb][:, fc, :], woutb[:, fc, half * 512:(half + 1) * 512], fc == 0, fc == 7, reads=[oT[pb], woutb], writes=[bank])
                vtt(x_t[:, half * 512:(half + 1) * 512], bank[:, :], x_t[:, half * 512:(half + 1) * 512], ALU.add, reads=[bank, x_t], writes=[x_t])
                yield
            act(yo[:], x_t[:], AF.Square, reads=[x_t], writes=[yo, ssum], accum=ssum[:])
            rsqrt_act(rstd[:], ssum[:], 1.0 / D, 0, 128, reads=[ssum], writes=[rstd])
            vstt(yo[:], x_t[:], rstd[:, 0:1], normf[:], ALU.mult, ALU.mult, reads=[x_t, rstd, normf], writes=[yo])
            S.dma("sync", yp[t0:t0 + TB, :], yo[:], reads=[yo], writes=[yp])
            yield

        def run_all(g):
            n = 0
            for _ in g:
                n += 1
            return n

        def interleave(ga, na, gb, nb):
            ia = ib = 0
            da = db = False
            while not (da and db):
                pick_a = (not da) and (db or (ia * nb <= ib * na))
                if pick_a:
                    try:
                        next(ga); ia += 1
                    except StopIteration:
                        da = True
                else:
                    try:
                        next(gb); ib += 1
                    except StopIteration:
                        db = True
            return ia, ib

        run_all(front(0))

        def record_units(g):
            units = []
            S.rec = []
            for _ in g:
                if S.rec:
                    units.append(S.rec)
                S.rec = []
            if S.rec:
                units.append(S.rec)
            S.rec = None
            return units

        A, B = [], []
        for tb in range(NTB):
            A.append(record_units(chain(tb)))
            if tb + 1 < NTB:
                B.append(record_units(front(tb + 1)))
        S.merge_emit(A, B, a_ok=lambda ia, ib: ib >= ia, b_ok=lambda ib, ia: ia >= ib)
        for h in range(8):
            tr(pA[0:64, h * 64:(h + 1) * 64], ST[:, h, :], ident[0:64, 0:64], reads=[ST, cst], writes=[pA])
        vcopy(SvT[:].rearrange("p h k -> p (h k)"), pA[0:64, :], reads=[pA], writes=[SvT])
        S.dma("sync", nwp[:].rearrange("h v k -> v h k"), SvT[:], reads=[SvT], writes=[nwp])
        S.finish([yp, ys, nsp, nwp, npp, nss, nws, nps], engname="sync")
        S.barrier()
    es_top.close()
    return nc, S


_CACHE = {}


def _consts():
    cst = np.zeros((128, C_END), np.float32)
    cst[:, C_ID:C_ID + 128] = np.eye(128, dtype=np.float32)
    ob = np.zeros((128, 128), np.float32)
    ob[0:64, 0:64] = 1.0
    ob[64:128, 64:128] = 1.0
    cst[:, C_ONES:C_ONES + 128] = ob
    s = np.arange(64)[:, None]
    t = np.arange(64)[None, :]
    mus = (s < t).astype(np.float32)
    mui = (s <= t).astype(np.float32)
    mls = (s > t).astype(np.float32)
    i64 = np.eye(64, dtype=np.float32)
    cst[0:64, C_MUS:C_MUS + 64] = mus
    cst[0:64, C_MUI:C_MUI + 64] = mui
    cst[0:64, C_MLS:C_MLS + 64] = mls
    rst = np.ones((512,), np.float32)
    rst[::CH] = 0.0
    cst[:, C_RST:C_RST + 512] = rst[None, :]
    for g, w in enumerate(WINS):
        pos = np.arange(16)
        cst[:, C_ICNT + g * 16:C_ICNT + (g + 1) * 16] = (1.0 / np.minimum(pos + 1, w)).astype(np.float32)[None, :]
    return cst


def kernel(x_prompt, x_sample, state_shift, state_wkv, state_pool, norm_w, w_in, mu_shift,
           w_decay_b, w0, w_aaa_b, a0, k_k, k_a, r_k, gn_w, gn_b, pool_w, pool_scale, w_out, norm_f):
    f = lambda a: np.ascontiguousarray(np.asarray(a, dtype=np.float32))
    x_prompt, x_sample, state_shift, state_wkv, state_pool = map(f, (x_prompt, x_sample, state_shift, state_wkv, state_pool))
    if "nc" not in _CACHE:
        _CACHE["nc"] = build_program()
    nc, S = _CACHE["nc"]

    def colmajor(v, n):
        return f(v).reshape(n, 128).T

    pvec = np.concatenate([
        colmajor(norm_w[0], 8), colmajor(mu_shift[0], 13), colmajor(w0[0], 4), colmajor(a0[0], 4), colmajor(k_k[0], 4),
        colmajor(k_a[0], 4), colmajor(f(r_k[0]).reshape(-1), 4), colmajor(gn_w[0], 4), colmajor(gn_b[0], 4), colmajor(pool_scale[0], 4)], axis=1)
    pvec = f(pvec)
    browA = f(f(mu_shift[0])[None, :])
    browB = f(np.concatenate([f(w0[0]), f(a0[0]), f(k_k[0]), f(k_a[0]), f(r_k[0]).reshape(-1), f(gn_w[0]), f(gn_b[0])])[None, :])
    cst = _consts()
    shared = {
        "w_in": f(w_in[0]), "w_out": f(w_out[0]), "wdec": f(w_decay_b[0]), "waaa": f(w_aaa_b[0]), "poolw": f(pool_w[0]),
        "pvec": pvec, "browA": browA, "browB": browB, "normf": f(norm_f)[None, :], "cst": cst,
    }
    in_maps = []
    for c in range(NCORE):
        bs = slice(c * DB, (c + 1) * DB)
        m = dict(shared)
        m["xp"] = x_prompt[c]
        m["xs"] = f(x_sample[bs].transpose(1, 0, 2).reshape(NS, D))
        m["sshift"] = state_shift[0, bs]
        m["swkv"] = f(state_wkv[0, bs].reshape(128, 4096))
        m["spool"] = f(state_pool[0, bs].reshape(DB * 15, 512))
        in_maps.append(m)
    res = run_bass_kernel_spmd(nc, in_maps, core_ids=list(range(NCORE)))
    R = res.results
    y_prompt = np.stack([R[c]["yp"] for c in range(NCORE)], axis=0)
    y_sample = np.concatenate([R[c]["ys"].reshape(DT, DB, D).transpose(1, 0, 2) for c in range(NCORE)], axis=0)
    nsp = np.stack([R[c]["nsp"].reshape(D_SHIFT) for c in range(NCORE)], axis=0)[None]
    nwp = np.stack([R[c]["nwp"] for c in range(NCORE)], axis=0)[None]
    npp = np.stack([R[c]["npp"] for c in range(NCORE)], axis=0)[None]
    nss = np.concatenate([R[c]["nss"] for c in range(NCORE)], axis=0)[None]
    nws = np.concatenate([R[c]["nws"].reshape(DB, 8, 64, 64) for c in range(NCORE)], axis=0)[None]
    nps = np.concatenate([R[c]["nps"] for c in range(NCORE)], axis=0)[None]
    out = (y_prompt, y_sample, nsp, nwp, npp, nss, nws, nps)
    return tuple(np.ascontiguousarray(o.astype(np.float32)) for o in out)
```
